# Optimizing a Trainium2 kernel written in Bass

```python
import jax
import jax.numpy as jnp
from jax import lax
import numpy as np

D_MODEL = 1024
BATCH = 32
SEQ = 256
DEPTH = 2
DEC_BATCH = 2
DEC_SEQ = 2048
PAST_LEN = 512

GRID_W = 64
N_BRANCH = 4
BRANCH_W = 512
H_A = 8
N_A = 64
LORA_W = 64
LORA_A = 64
RWKV_GN_EPS = 64e-5
H_B = 8
NOPE = 64
ROPE = 32
VDIM = 64
Q_LORA = 256
KV_LORA = 128
ROPE_BASE = 10000.0
Q_BLOCK = 128
G_C = 4
CHUNK = 128
G_D = 4
NORM_EPS = 1e-6
IN_W = 5728
SHIFT_W = 1728
IN_SPLITS = (512, 1024, 1536, 1600, 1664, 1728, 2240, 2496, 2624, 2656, 3168, 3680, 4192, 4704, 5216)

kernel_name = 'hybrid_rwkv_mla_gmlp_fnet_diffusion_step'


def rms_norm(x, g):
    xf = x.astype(jnp.float32)
    y = xf * lax.rsqrt(jnp.mean(xf * xf, axis=-1, keepdims=True) + NORM_EPS)
    return (y * g.astype(jnp.float32)).astype(x.dtype)


def layer_norm(x, g, b):
    xf = x.astype(jnp.float32)
    mu = jnp.mean(xf, axis=-1, keepdims=True)
    var = jnp.mean(jnp.square(xf - mu), axis=-1, keepdims=True)
    y = (xf - mu) * lax.rsqrt(var + 1e-5) * g.astype(jnp.float32) + b.astype(jnp.float32)
    return y.astype(x.dtype)


def head_group_norm(o, g, b):
    B, T, H, N = o.shape
    mu = jnp.mean(o, axis=-1, keepdims=True)
    var = jnp.mean(jnp.square(o - mu), axis=-1, keepdims=True)
    y = ((o - mu) * lax.rsqrt(var + RWKV_GN_EPS)).reshape(B, T, H * N)
    return y * g.astype(jnp.float32) + b.astype(jnp.float32)


def centred_shift(z):
    prev = jnp.pad(z[:, :-1], ((0, 0), (1, 0), (0, 0)))
    nxt = jnp.pad(z[:, 1:], ((0, 0), (0, 1), (0, 0)))
    return 0.5 * (prev + nxt)


def axial_rope(n_tokens):
    rows = n_tokens // GRID_W
    row = jnp.repeat(jnp.arange(rows, dtype=jnp.float32), GRID_W)
    col = jnp.tile(jnp.arange(GRID_W, dtype=jnp.float32), rows)
    n_freq = ROPE // 4
    inv = ROPE_BASE ** (-jnp.arange(n_freq, dtype=jnp.float32) / n_freq)
    ang = jnp.concatenate([row[:, None] * inv, col[:, None] * inv], axis=-1)
    return jnp.cos(ang)[:, None, :], jnp.sin(ang)[:, None, :]


def apply_rope(x, cos, sin):
    xf = x.astype(jnp.float32).reshape(*x.shape[:-1], ROPE // 2, 2)
    x0, x1 = xf[..., 0], xf[..., 1]
    out = jnp.stack([x0 * cos - x1 * sin, x0 * sin + x1 * cos], axis=-1)
    return out.reshape(x.shape).astype(x.dtype)


def block_attention(q, k, v):
    B, S, H, Dq = q.shape
    scale = (NOPE + ROPE) ** -0.5
    qb = jnp.moveaxis(q.reshape(B, S // Q_BLOCK, Q_BLOCK, H, Dq), 1, 0)

    def one_block(qi):
        s = jnp.einsum('bqhd,bkhd->bhqk', qi, k).astype(jnp.float32) * scale
        pr = jax.nn.softmax(s, axis=-1).astype(v.dtype)
        return jnp.einsum('bhqk,bkhd->bqhd', pr, v)

    o = lax.map(one_block, qb)
    return jnp.moveaxis(o, 0, 1).reshape(B, S, H, v.shape[-1])


def wkv_scan(r, w, k, v, kk, a, s0, reverse):
    xs = tuple(jnp.moveaxis(t, 1, 0) for t in (r, w, k, v, kk, a))

    def step(S, inp):
        r_t, w_t, k_t, v_t, kk_t, a_t = inp
        sa = jnp.einsum('bhvk,bhk->bhv', S, -kk_t)
        S = (S * w_t[:, :, None, :] + sa[..., None] * (kk_t * a_t)[:, :, None, :]
             + v_t[..., None] * k_t[:, :, None, :])
        return S, jnp.einsum('bhvk,bhk->bhv', S, r_t)

    s_fin, ys = lax.scan(step, s0, xs, reverse=reverse)
    return jnp.moveaxis(ys, 0, 1), s_fin


def rwkv_branch(r, k, v, wdf, wdb, ad, g, p, s0_f, s0_b):
    B, T, _ = r.shape
    f32 = jnp.float32

    def heads(t):
        return t.astype(f32).reshape(B, T, H_A, N_A)

    rh, kh, vh = heads(r), heads(k), heads(v)
    kk = heads(k * p['rwkv_k_k'])
    kk = kk / jnp.maximum(jnp.linalg.norm(kk, axis=-1, keepdims=True), 1e-12)
    k_a = p['rwkv_k_a'].astype(f32).reshape(H_A, N_A)
    r_k = p['rwkv_r_k'].astype(f32)
    outs, bonuses, finals = [], [], []
    for d, (wd, s0) in enumerate(((wdf, s0_f), (wdb, s0_b))):
        pre = (p['rwkv_w0'][d] + jnp.tanh(wd) @ p['rwkv_w_up'][d]).astype(f32)
        decay = jnp.exp(-jnp.exp(-jax.nn.softplus(-pre) - 0.5))
        ah = heads(jax.nn.sigmoid(p['rwkv_a0'][d] + ad @ p['rwkv_a_up'][d]))
        kt = kh * (1.0 + (ah - 1.0) * k_a)
        od, sf = wkv_scan(rh, heads(decay), kt, vh, kk, ah, s0.astype(f32), reverse=(d == 1))
        outs.append(od)
        bonuses.append(jnp.sum(rh * kt * r_k, axis=-1, keepdims=True) * vh)
        finals.append(sf)
    y = head_group_norm(outs[0] + outs[1], p['rwkv_ln_g'], p['rwkv_ln_b'])
    y = y + (bonuses[0] + bonuses[1]).reshape(B, T, BRANCH_W)
    return y.astype(r.dtype) * jax.nn.silu(g), finals[0], finals[1]


def mla_queries(qd, kvd, kr, p, rope):
    B, T, _ = qd.shape
    q = (rms_norm(qd, p['mla_q_norm']) @ p['mla_w_q_up']).reshape(B, T, H_B, NOPE + ROPE)
    ckv = rms_norm(kvd, p['mla_kv_norm'])
    kr = kr.reshape(B, T, 1, ROPE)
    q_nope, q_rope = q[..., :NOPE], q[..., NOPE:]
    if rope is not None:
        q_rope = apply_rope(q_rope, rope[0], rope[1])
        kr = apply_rope(kr, rope[0], rope[1])
    return jnp.concatenate([q_nope, q_rope], axis=-1), ckv, kr.reshape(B, T, ROPE)


def mla_expand(ckv, kr, p):
    B, T, _ = ckv.shape
    kv = (ckv @ p['mla_w_kv_up']).reshape(B, T, H_B, NOPE + VDIM)
    k_rope = jnp.broadcast_to(kr.reshape(B, T, 1, ROPE), (B, T, H_B, ROPE)).astype(kv.dtype)
    return jnp.concatenate([kv[..., :NOPE], k_rope], axis=-1), kv[..., NOPE:]


def gmlp_branch(u, vc, g, p):
    B, T, _ = vc.shape
    u = jax.nn.gelu(u)
    vc = layer_norm(jax.nn.gelu(vc), p['gmlp_ln_g'], p['gmlp_ln_b'])
    vg = vc.reshape(B, T // CHUNK, CHUNK, G_C, BRANCH_W // G_C)
    mixed = jnp.einsum('gpq,bcqgd->bcpgd', p['gmlp_w_s'], vg) + p['gmlp_b_s'].T[:, :, None]
    return u * mixed.reshape(B, T, BRANCH_W) * jax.nn.silu(g)


def fourier_branch(f, g):
    B, T, _ = f.shape
    fg = f.astype(jnp.float32).reshape(B, T, G_D, BRANCH_W // G_D)
    y = jnp.fft.fft2(fg, axes=(1, 3), norm='ortho').real
    return y.reshape(B, T, BRANCH_W).astype(f.dtype) * jax.nn.silu(g)


def trunk_layer(x, cond, p, rope, ctx):
    B, T, _ = x.shape
    mod = jax.nn.silu(cond) @ p['w_ada'] + p['b_ada']
    shift, scale, gate = jnp.split(mod[..., None, :], 3, axis=-1)
    h = rms_norm(x, p['norm_g']) * (1.0 + scale) + shift
    z = h @ p['w_in']
    zs = z[..., :SHIFT_W]
    zs = zs + (centred_shift(zs) - zs) * p['shift_mu']
    z = jnp.concatenate([zs, z[..., SHIFT_W:]], axis=-1)
    (r, k, v, wdf, wdb, ad, g_a, qd, kvd, kr, g_b, u, vc, g_c, f, g_d) = jnp.split(z, IN_SPLITS, axis=-1)
    if ctx is None:
        s0_f = jnp.zeros((B, H_A, N_A, N_A), jnp.float32)
        s0_b = s0_f
    else:
        s0_f, s0_b, ckv_c, kr_c = ctx
    o_a, s_f, s_b = rwkv_branch(r, k, v, wdf, wdb, ad, g_a, p, s0_f, s0_b)
    q, ckv, kr_t = mla_queries(qd, kvd, kr, p, rope)
    k_all, v_all = mla_expand(ckv, kr_t, p)
    if ctx is not None:
        k_c, v_c = mla_expand(ckv_c.astype(ckv.dtype), kr_c, p)
        k_all = jnp.concatenate([k_all, k_c], axis=1)
        v_all = jnp.concatenate([v_all, v_c], axis=1)
    o_b = block_attention(q, k_all, v_all).reshape(B, T, H_B * VDIM) * jax.nn.silu(g_b)
    o_c = gmlp_branch(u, vc, g_c, p)
    o_d = fourier_branch(f, g_d)
    proj = jnp.einsum('nbtw,nwd->nbtd', jnp.stack([o_a, o_b, o_c, o_d]), p['w_branch'])
    gates = jax.nn.sigmoid(h @ p['w_merge'] + p['b_merge']).reshape(B, T, N_BRANCH, D_MODEL)
    merged = jnp.einsum('nbtd,btnd->btd', proj, gates)
    x = x + gate * (merged @ p['w_out'])
    return x, s_f, s_b, ckv, kr_t


def setup_inputs(seed: int = 0) -> dict:
    key = jax.random.key(seed)
    keys = list(jax.random.split(key, 40))
    f32 = jnp.float32

    def nrm(i, shape, s):
        return jax.random.normal(keys[i], shape, f32) * s

    def unif(i, shape, lo, hi):
        return jax.random.uniform(keys[i], shape, f32, lo, hi)

    D = D_MODEL
    return {
        'x_prompt': nrm(0, (BATCH, SEQ, D), 1.0),
        'x_sample': nrm(1, (DEC_BATCH, DEC_SEQ, D), 1.0),
        'state_rwkv_fwd': nrm(2, (DEC_BATCH, DEPTH, H_A, N_A, N_A), 0.3),
        'state_rwkv_bwd': nrm(3, (DEC_BATCH, DEPTH, H_A, N_A, N_A), 0.3),
        'cache_mla_ckv': nrm(4, (DEC_BATCH, DEPTH, PAST_LEN, KV_LORA), 1.0),
        'cache_mla_krope': nrm(5, (DEC_BATCH, DEPTH, PAST_LEN, ROPE), 1.0),
        'c': nrm(6, (DEC_BATCH, D), 1.0),
        'c_ctx': nrm(7, (D,), 1.0),
        'norm_g': 1.0 + nrm(8, (DEPTH, D), 0.02),
        'w_ada': nrm(9, (DEPTH, D, 3 * D), 0.5 * D ** -0.5),
        'b_ada': nrm(10, (DEPTH, 3 * D), 0.02),
        'w_in': nrm(11, (DEPTH, D, IN_W), D ** -0.5),
        'shift_mu': unif(12, (DEPTH, SHIFT_W), 0.0, 1.0),
        'rwkv_w0': unif(13, (DEPTH, 2, BRANCH_W), -6.0, 1.0),
        'rwkv_w_up': nrm(14, (DEPTH, 2, LORA_W, BRANCH_W), 0.5 * LORA_W ** -0.5),
        'rwkv_a0': nrm(15, (DEPTH, 2, BRANCH_W), 0.5),
        'rwkv_a_up': nrm(16, (DEPTH, 2, LORA_A, BRANCH_W), 0.3 * LORA_A ** -0.5),
        'rwkv_k_k': 0.85 + nrm(17, (DEPTH, BRANCH_W), 0.02),
        'rwkv_k_a': 1.0 + nrm(18, (DEPTH, BRANCH_W), 0.02),
        'rwkv_r_k': nrm(19, (DEPTH, H_A, N_A), 0.1),
        'rwkv_ln_g': 1.0 + nrm(20, (DEPTH, BRANCH_W), 0.02),
        'rwkv_ln_b': nrm(21, (DEPTH, BRANCH_W), 0.02),
        'mla_q_norm': 1.0 + nrm(22, (DEPTH, Q_LORA), 0.02),
        'mla_w_q_up': nrm(23, (DEPTH, Q_LORA, H_B * (NOPE + ROPE)), Q_LORA ** -0.5),
        'mla_kv_norm': 1.0 + nrm(24, (DEPTH, KV_LORA), 0.02),
        'mla_w_kv_up': nrm(25, (DEPTH, KV_LORA, H_B * (NOPE + VDIM)), KV_LORA ** -0.5),
        'gmlp_ln_g': 1.0 + nrm(26, (DEPTH, BRANCH_W), 0.02),
        'gmlp_ln_b': nrm(27, (DEPTH, BRANCH_W), 0.02),
        'gmlp_w_s': nrm(28, (DEPTH, G_C, CHUNK, CHUNK), CHUNK ** -0.5),
        'gmlp_b_s': 1.0 + nrm(29, (DEPTH, G_C, CHUNK), 0.02),
        'w_branch': nrm(30, (DEPTH, N_BRANCH, BRANCH_W, D), BRANCH_W ** -0.5),
        'w_merge': nrm(31, (DEPTH, D, N_BRANCH * D), D ** -0.5),
        'b_merge': nrm(32, (DEPTH, N_BRANCH * D), 0.02),
        'w_out': nrm(33, (DEPTH, D, D), D ** -0.5),
        'final_norm_g': 1.0 + nrm(34, (D,), 0.02),
    }


def reference(x_prompt, x_sample, state_rwkv_fwd, state_rwkv_bwd, cache_mla_ckv, cache_mla_krope,
              c, c_ctx, norm_g, w_ada, b_ada, w_in, shift_mu, rwkv_w0, rwkv_w_up, rwkv_a0, rwkv_a_up,
              rwkv_k_k, rwkv_k_a, rwkv_r_k, rwkv_ln_g, rwkv_ln_b, mla_q_norm, mla_w_q_up, mla_kv_norm,
              mla_w_kv_up, gmlp_ln_g, gmlp_ln_b, gmlp_w_s, gmlp_b_s, w_branch, w_merge, b_merge, w_out,
              final_norm_g):
    rope = axial_rope(x_sample.shape[1])
    xc = x_prompt
    xl = x_sample
    sf_list, sb_list, ckv_list, kr_list = [], [], [], []
    for l in range(DEPTH):
        p = {
            'norm_g': norm_g[l], 'w_ada': w_ada[l], 'b_ada': b_ada[l], 'w_in': w_in[l],
            'shift_mu': shift_mu[l], 'rwkv_w0': rwkv_w0[l], 'rwkv_w_up': rwkv_w_up[l],
            'rwkv_a0': rwkv_a0[l], 'rwkv_a_up': rwkv_a_up[l], 'rwkv_k_k': rwkv_k_k[l],
            'rwkv_k_a': rwkv_k_a[l], 'rwkv_r_k': rwkv_r_k[l], 'rwkv_ln_g': rwkv_ln_g[l],
            'rwkv_ln_b': rwkv_ln_b[l], 'mla_q_norm': mla_q_norm[l], 'mla_w_q_up': mla_w_q_up[l],
            'mla_kv_norm': mla_kv_norm[l], 'mla_w_kv_up': mla_w_kv_up[l], 'gmlp_ln_g': gmlp_ln_g[l],
            'gmlp_ln_b': gmlp_ln_b[l], 'gmlp_w_s': gmlp_w_s[l], 'gmlp_b_s': gmlp_b_s[l],
            'w_branch': w_branch[l], 'w_merge': w_merge[l], 'b_merge': b_merge[l], 'w_out': w_out[l],
        }
        xc, s_f, s_b, ckv, kr = trunk_layer(xc, c_ctx, p, None, None)
        sf_list.append(s_f)
        sb_list.append(s_b)
        ckv_list.append(ckv)
        kr_list.append(kr)
        ctx = (state_rwkv_fwd[:, l], state_rwkv_bwd[:, l], cache_mla_ckv[:, l], cache_mla_krope[:, l])
        xl = trunk_layer(xl, c, p, rope, ctx)[0]
    y_prompt = rms_norm(xc, final_norm_g)
    y_sample = rms_norm(xl, final_norm_g)
    new_state_rwkv_fwd = jnp.stack(sf_list, axis=1).astype(x_prompt.dtype)
    new_state_rwkv_bwd = jnp.stack(sb_list, axis=1).astype(x_prompt.dtype)
    new_cache_mla_ckv = jnp.stack(ckv_list, axis=1)
    new_cache_mla_krope = jnp.stack(kr_list, axis=1)
    return (y_prompt, y_sample, new_state_rwkv_fwd, new_state_rwkv_bwd, new_cache_mla_ckv, new_cache_mla_krope)
```

```python
import numpy as np
import ml_dtypes
import concourse.bass as bass
import concourse.mybir as mybir
from concourse.bass_utils import run_bass_kernel_spmd

F32 = mybir.dt.float32
BF16 = mybir.dt.bfloat16
I32 = mybir.dt.int32
ALU = mybir.AluOpType
AF = mybir.ActivationFunctionType
AX = mybir.AxisListType

D = 1024
DEPTH = 2
SEQ = 256
DSEQ = 2048
PAST = 512
NCORE = 8
NPT = 1024
NST = 512
NTOK = NPT + NST
EPS = 1e-6
GN_EPS = 64e-5
EP = 24000
AGP = {"rk": 1024, "vx": 864, "f": 512}
VX_OFF = dict(v=0, lora=512, ckv=704, kr=832)


class Buf:
    def __init__(self, h, name, space):
        self.h = h
        self.name = name
        self.space = space
        self.w = {}
        self.r = {}
        self.dsem = None
        self.dcnt = 0
        self.dsid = None

    def __getitem__(self, idx):
        return self.h[idx]

    def ap(self):
        return self.h.ap() if self.space == "dram" else self.h[:]


class FW:
    def __init__(self, nc):
        self.nc = nc
        self.E = {"pe": nc.tensor, "act": nc.scalar, "dve": nc.vector, "pool": nc.gpsimd, "sp": nc.sync}
        self.cnt = {e: 0 for e in self.E}
        self.esem = {e: [] for e in self.E}
        self.waited = {e: {} for e in self.E}
        self.pend = {e: [] for e in self.E}
        self.nbuf = 0
        self.ninst = 0
        self.dma_out = {}
        self.sem_pool = []
        self.stack = []
        self.rots = {}
        self.free_sems = []

    def sb(self, shape, dt, name=None):
        self.nbuf += 1
        name = name or "t"
        g = self.nc.sbuf_tensor(f"{name}_{self.nbuf}", list(shape), dt)
        h = g.__enter__()
        b = Buf(h, name, "sbuf")
        if self.stack:
            self.stack[-1].append((g, b))
        return b

    def ps(self, shape, dt=F32, name=None):
        self.nbuf += 1
        name = name or "p"
        h = self.nc.alloc_psum_tensor(f"{name}_{self.nbuf}", list(shape), dt)
        return Buf(h, name, "psum")

    def dram(self, name, shape, dt, kind=None):
        if kind is None:
            h = self.nc.dram_tensor(name, list(shape), dt)
        else:
            h = self.nc.dram_tensor(name, list(shape), dt, kind=kind)
        return Buf(h, name, "dram")

    def rot(self, shape, dt, name, n=2):
        key = (len(self.stack), name)
        if key not in self.rots:
            self.rots[key] = [[self.sb(shape, dt, name) for _ in range(n)], 0]
        ent = self.rots[key]
        t = ent[0][ent[1] % n]
        ent[1] += 1
        return t

    def push(self):
        self.stack.append([])

    def pop(self):
        self.barrier()
        depth = len(self.stack)
        for k in [k for k in self.rots if k[0] == depth]:
            del self.rots[k]
        for g, b in reversed(self.stack.pop()):
            if b.dsem is not None:
                self.free_sems.append((b.dsem, b.dcnt))
                b.dsem = None
            g.__exit__(None, None, None)

    def _sem_for(self, eng, k):
        i = (k - 1) // EP
        while len(self.esem[eng]) <= i:
            self.esem[eng].append(self.nc.alloc_semaphore(f"s_{eng}_{len(self.esem[eng])}"))
        return self.esem[eng][i], (k - 1) % EP + 1

    def _wait(self, eng, ev):
        if ev is None:
            return
        if ev[0] == "eng":
            _, e2, k = ev
            if e2 == eng and eng == "pe":
                return
            key = ("eng", e2, (k - 1) // EP)
            sem, val = self._sem_for(e2, k)
        else:
            _, sem, val, sid = ev
            key = ("sem", sid)
        if self.waited[eng].get(key, 0) >= val:
            return
        self.waited[eng][key] = val
        self.E[eng].wait_ge(sem, val)

    def _check_pend(self, eng, b):
        for e2, lst in self.pend.items():
            if e2 == eng:
                continue
            for (pb, _) in lst:
                if pb is b:
                    raise RuntimeError(f"buffer {b.name} has pending unsignalled access on {e2}, touched by {eng}")

    def _deps(self, eng, reads, writes):
        evs = []
        for b in reads:
            self._check_pend(eng, b)
            evs.extend(b.w.values())
        for b in writes:
            self._check_pend(eng, b)
            for wv in b.w.values():
                if not (wv[0] == "eng" and wv[1] == eng):
                    evs.append(wv)
            for ev in b.r.values():
                if ev[0] == "eng" and ev[1] == eng and eng == "pe":
                    continue
                evs.append(ev)
        for ev in evs:
            self._wait(eng, ev)

    def op(self, eng, fn, reads=(), writes=(), signal=True):
        self._deps(eng, reads, writes)
        ins = fn()
        self.ninst += 1
        if signal:
            self.cnt[eng] += 1
            k = self.cnt[eng]
            sem, val = self._sem_for(eng, k)
            ins.then_inc(sem, 1)
            ev = ("eng", eng, k)
            for (pb, kind) in self.pend[eng]:
                if kind == "r":
                    pb.r[eng] = ev
                else:
                    pb.w = {eng: ev}
                    pb.r = {}
            self.pend[eng] = []
            for b in reads:
                b.r[eng] = ev
            for b in writes:
                b.w = {eng: ev}
                b.r = {}
        else:
            for b in reads:
                self.pend[eng].append((b, "r"))
            for b in writes:
                self.pend[eng].append((b, "w"))
        return ins

    def _dma_sem(self, b):
        if b.dsem is None:
            self.nbuf += 1
            if self.free_sems:
                b.dsem, b.dcnt = self.free_sems.pop()
            else:
                b.dsem = self.nc.alloc_semaphore(f"d_{b.name}_{self.nbuf}")
            b.dsid = self.nbuf
        return b.dsem

    def dma(self, q, out_b, out_ap, in_b, in_ap, sem_owner=None, inc=16, fn=None, extra_reads=(), **kw):
        eng = q
        evs = []
        for b in (in_b, out_b) + tuple(extra_reads):
            self._check_pend(eng, b)
        evs.extend(in_b.w.values())
        for b in extra_reads:
            evs.extend(b.w.values())
        owner = sem_owner or (out_b if out_b.space != "dram" else in_b)
        sem = self._dma_sem(owner)
        for wv in out_b.w.values():
            if wv[0] != "sem":
                evs.append(wv)
        for ev in out_b.r.values():
            evs.append(ev)
        for ev in evs:
            self._wait(eng, ev)
        if fn is None:
            ins = self.E[eng].dma_start(out=out_ap, in_=in_ap, **kw)
        else:
            ins = fn()
        self.ninst += 1
        owner.dcnt += inc
        ins.then_inc(sem, inc)
        ev = ("sem", sem, owner.dcnt, owner.dsid)
        in_b.r[("dma", owner.dsid)] = ev
        for b in extra_reads:
            b.r[("dma", owner.dsid)] = ev
        out_b.w = {k: v for k, v in out_b.w.items() if v[0] == "sem"}
        out_b.w[("dma", owner.dsid)] = ev
        out_b.r = {}
        self.dma_out[owner.dsid] = ev
        return ev

    def wait_buf(self, eng, b):
        self._check_pend(eng, b)
        for ev in b.w.values():
            self._wait(eng, ev)
        for ev in b.r.values():
            self._wait(eng, ev)

    def barrier(self):
        for e in self.E:
            if self.pend[e]:
                raise RuntimeError(f"barrier with pending unsignalled ops on {e}")
        last = []
        for e in ("pe", "act", "dve", "pool"):
            if self.cnt[e] > 0:
                last.append(("eng", e, self.cnt[e]))
        for e in self.E:
            for ev in last:
                if ev[1] == e and e == "pe":
                    continue
                self._wait(e, ev)
            for ev in self.dma_out.values():
                self._wait(e, ev)
        self.dma_out = {}


COLS = dict(r=(0, 512), k=(512, 1024), v=(1024, 1536), wdf=(1536, 1600), wdb=(1600, 1664), ad=(1664, 1728),
            ga=(1728, 2240), qd=(2240, 2496), kvd=(2496, 2624), kr=(2624, 2656), gb=(2656, 3168),
            u=(3168, 3680), vc=(3680, 4192), gc=(4192, 4704), f=(4704, 5216), gd=(5216, 5728))


def _rng(name):
    a, b = COLS[name]
    return np.arange(a, b)


_SWAP32 = np.arange(32).reshape(16, 2)[:, ::-1].reshape(32)

WIN_BLOCKS = [
    ("A0", [_rng("r")]), ("A1", [_rng("k")]), ("A2", [_rng("v")]),
    ("A3", [_rng("wdf"), _rng("wdb"), _rng("ad")]), ("A4", [_rng("ga")]),
    ("B0", [_rng("qd"), _rng("kvd"), _rng("kr"), _rng("kr")[_SWAP32]]), ("B1", [_rng("gb")]),
    ("C0", [_rng("u")]), ("C1", [_rng("vc")]), ("C2", [_rng("gc")]),
    ("D0", [_rng("f")]), ("D1", [_rng("gd")]),
]
WIN_W = {n: int(sum(len(c) for c in cols)) for n, cols in WIN_BLOCKS}
WIN_IDX = {n: 6 + i for i, (n, _) in enumerate(WIN_BLOCKS)}
BLK_PER_LAYER = 36
FBLK = 4096


def _kcp(w, W):
    return w.reshape(8, 128, W).transpose(1, 0, 2).reshape(128, 8 * W)


def build_stream(w_ada, w_in, w_branch, w_merge, w_out):
    st = np.zeros((DEPTH * BLK_PER_LAYER, 128, FBLK), np.float32)
    for l in range(DEPTH):
        base = l * BLK_PER_LAYER
        for b in range(6):
            st[base + b, :, :] = _kcp(w_ada[l][:, 512 * b:512 * b + 512], 512)
        for i, (n, cols) in enumerate(WIN_BLOCKS):
            cc = np.concatenate(cols)
            W = len(cc)
            st[base + 6 + i, :, :8 * W] = _kcp(w_in[l][:, cc], W)
        for d in range(8):
            cc = np.concatenate([n * 1024 + d * 128 + np.arange(128) for n in range(4)])
            st[base + 18 + 2 * d, :, :] = _kcp(w_merge[l][:, cc], 512)
            wb = w_branch[l][:, :, d * 128:(d + 1) * 128]
            wb = wb.reshape(4, 4, 128, 128).transpose(2, 0, 1, 3)
            st[base + 19 + 2 * d, :, :2048] = wb.reshape(128, 2048)
        for b in range(2):
            st[base + 34 + b, :, :] = _kcp(w_out[l][:, 512 * b:512 * b + 512], 512)
    return st


SP_OFF = {}
_o = 0
for _n, _w in [("norm_g", 8), ("b_ada", 24), ("mu_rkv", 12), ("mu_lora", 3), ("b_merge", 32), ("rw", 36),
               ("qn", 2), ("kvn", 1), ("gln_g", 4), ("gln_b", 4), ("fin_g", 8)]:
    SP_OFF[_n] = (_o, _w)
    _o += _w
NSP = _o
RW_NAMES = ["w0_f", "w0_b", "a0_f", "a0_b", "k_k", "k_a", "r_k", "ln_g", "ln_b"]


def _pc(v, n):
    return np.asarray(v, np.float32).reshape(n, 128).T


def build_small(inp):
    sp = np.zeros((DEPTH, 128, NSP), np.float32)
    for l in range(DEPTH):
        def put(name, arr):
            o, w = SP_OFF[name]
            sp[l, :, o:o + w] = arr
        put("norm_g", _pc(inp["norm_g"][l], 8))
        put("b_ada", _pc(inp["b_ada"][l], 24))
        put("mu_rkv", _pc(inp["shift_mu"][l][:1536], 12))
        ml = np.zeros((128, 3), np.float32)
        ml[:64, :] = inp["shift_mu"][l][1536:1728].reshape(3, 64).T
        put("mu_lora", ml)
        bm = inp["b_merge"][l].reshape(4, 8, 128)
        put("b_merge", bm.transpose(2, 1, 0).reshape(128, 32))
        rwv = [inp["rwkv_w0"][l][0], inp["rwkv_w0"][l][1], inp["rwkv_a0"][l][0], inp["rwkv_a0"][l][1],
               inp["rwkv_k_k"][l], inp["rwkv_k_a"][l], inp["rwkv_r_k"][l].reshape(512), inp["rwkv_ln_g"][l],
               inp["rwkv_ln_b"][l]]
        rw = np.stack([_pc(v, 4) for v in rwv], axis=1)
        put("rw", rw.reshape(128, 36))
        put("qn", _pc(inp["mla_q_norm"][l], 2))
        put("kvn", _pc(inp["mla_kv_norm"][l], 1))
        put("gln_g", _pc(inp["gmlp_ln_g"][l], 4))
        put("gln_b", _pc(inp["gmlp_ln_b"][l], 4))
        put("fin_g", _pc(inp["final_norm_g"], 8))
    return sp


def rw_col(name, pair):
    o, _ = SP_OFF["rw"]
    return o + RW_NAMES.index(name) * 4 + pair


def build_consts():
    c = {}
    c["ident"] = np.eye(128, dtype=np.float32)
    hb = np.arange(128) // 64
    c["bones"] = (hb[:, None] == hb[None, :]).astype(np.float32)
    p = np.arange(128)[:, None]
    f = np.arange(128)[None, :]
    mk = {}
    bd32 = ((p // 32) == (f // 32)).astype(np.float32)
    od64 = (((p // 64) == (f // 64)) & ((p // 32) != (f // 32))).astype(np.float32)
    od128 = ((p // 64) != (f // 64)).astype(np.float32)
    for dname, ms, mi, mt in (("f", p < f, p <= f, f < p), ("b", p > f, p >= f, f > p)):
        ms = ms.astype(np.float32)
        mi = mi.astype(np.float32)
        mtf = -(mt.astype(np.float32))
        mk[dname] = np.concatenate([ms, mi, -ms * bd32, mi, mtf * bd32, mtf * bd32, mtf * od64, mtf * od64,
                                    mtf * od128, mtf * od128], axis=1)
    c["mask"] = np.stack([mk["f"], mk["b"]], axis=1).reshape(128, 2 * 1280)
    dd = np.arange(128)
    ang = 2 * np.pi * np.outer(dd, dd) / 128.0
    for nm, T in (("dftd_p", SEQ), ("dftd_s", DSEQ)):
        sc = 1.0 / np.sqrt(T * 128.0)
        c[nm] = np.concatenate([np.cos(ang) * sc, -np.sin(ang) * sc], axis=1).astype(np.float32)
    tt = np.arange(SEQ)
    angp = 2 * np.pi * np.outer(tt, tt) / SEQ
    cp = np.stack([np.cos(angp), np.sin(angp)], axis=1)
    c["dftT_p"] = cp.reshape(2, 128, 2, SEQ).transpose(1, 0, 2, 3).reshape(128, 2 * 2 * SEQ).astype(np.float32)
    return c


def build_core_consts(j):
    t = np.arange(DSEQ)
    k1 = 512 * j + np.arange(512)
    ang = 2 * np.pi * ((np.outer(t, k1)) % DSEQ) / DSEQ
    cs = np.stack([np.cos(ang), np.sin(ang)], axis=1)
    dftT = cs.reshape(16, 128, 2, 512).astype(np.float32)
    pos = 512 * j + np.arange(512)
    row = (pos // 64).astype(np.float32)
    col = (pos % 64).astype(np.float32)
    inv = (10000.0 ** (-np.arange(8, dtype=np.float32) / 8)).astype(np.float32)
    ang = np.concatenate([row[:, None] * inv, col[:, None] * inv], axis=-1).astype(np.float32)
    cos = np.cos(ang).astype(np.float32)
    sin = np.sin(ang).astype(np.float32)
    COS = np.repeat(cos, 2, axis=1).T
    SIN = np.stack([-sin, sin], axis=2).reshape(512, 32).T
    rope = np.stack([COS, SIN], axis=1).astype(np.float32)
    return dftT, rope


class Prog:
    def __init__(self, cfg):
        self.cfg = cfg
        self.branches = cfg.get("branches", "ABCD")
        self.depth = cfg.get("depth", DEPTH)
        nc = bass.Bass("TRN2", target_bir_lowering=False)
        self.nc = nc
        fw = FW(nc)
        self.fw = fw
        self.V, self.A, self.G, self.T = nc.vector, nc.scalar, nc.gpsimd, nc.tensor
        di = lambda n, s, dt=F32: fw.dram(n, s, dt, kind="ExternalInput")
        do = lambda n, s, dt=F32: fw.dram(n, s, dt, kind="ExternalOutput")
        self.d = dict(
            xp=di("xp", [D, NPT]), xs=di("xs", [D, NST]),
            wst=di("wst", [DEPTH * BLK_PER_LAYER, 128, FBLK]),
            sp=di("sp", [DEPTH, 128, NSP]), cond=di("cond", [128, 16]),
            ident=di("ident", [128, 128]), bones=di("bones", [128, 128]), mask=di("mask", [128, 2560]),
            dftd_p=di("dftd_p", [128, 256]), dftd_s=di("dftd_s", [128, 256]), dftT_p=di("dftT_p", [128, 1024]),
            dftT_s=di("dftT_s", [16, 128, 1024]), rope=di("rope", [32, 1024]),
            wup=di("wup", [DEPTH, 64, 1024]), aup=di("aup", [DEPTH, 64, 1024]),
            wupo=di("wupo", [DEPTH, 64, 256]), aupo=di("aupo", [DEPTH, 64, 256]),
            spo=di("spo", [DEPTH, 128, 12]),
            wq=di("wq", [DEPTH, 128, 2 * 8 * 96]), wqs=di("wqs", [DEPTH, 128, 2 * 8 * 32]),
            wkk=di("wkk", [DEPTH, 128, 512]), wkv=di("wkv", [DEPTH, 128, 512]),
            wsT=di("wsT", [DEPTH, 128, 512]), bsb=di("bsb", [DEPTH, 128, 512]),
            st0=di("st0", [DEPTH, 2, 128, 64]), cckv=di("cckv", [DEPTH, 128, PAST]),
            ckr=di("ckr", [DEPTH, 32, PAST]),
            idx1=di("idx1", [128, 12], I32), idx2=di("idx2", [128, 4], I32),
            yp=do("yp", [D, NPT]), ys=do("ys", [D, NST]),
            stout=do("stout", [DEPTH * 2 * 4 * 4 * 128, 64]),
            ckvout=do("ckvout", [DEPTH, 128, NPT]), krout=do("krout", [DEPTH, 32, NPT]),
        )
        self.ag1_in = [{k: fw.dram(f"ag1i{k}{l}", [n, 512], BF16) for k, n in AGP.items()} for l in range(DEPTH)]
        self.ag1_out = [{k: fw.dram(f"ag1o{k}{l}", [4 * n, 512], BF16) for k, n in AGP.items()} for l in range(DEPTH)]
        self.ag2_in = [fw.dram(f"ag2i{l}", [512, 512], BF16) for l in range(DEPTH)]
        self.ag2_out = [fw.dram(f"ag2o{l}", [2048, 512], BF16) for l in range(DEPTH)]
        self.psb = [fw.ps([128, 512], F32, f"bank{i}") for i in range(6)]
        self.pst = [fw.ps([128, 1024], BF16, f"pst{i}") for i in range(2)]
        self.pst_rr = 0
        self.ps_rr = 0
        self.xT = fw.sb([128, 8, NTOK], F32, "xT")
        self.hTs = [fw.sb([128, 8, 512], BF16, "hTa"), None]
        self.oTs = [[fw.sb([128, 4, 512], BF16, f"oTa{n}") for n in range(4)], None]
        self.slots = [fw.sb([128, FBLK], BF16, f"slot{i}") for i in range(2)]
        self.slot_rr = 0
        self.plan = []
        self.plan_pos = 0
        self.stg = None
        self.blocks_left = 0
        self.dma_issued = {}
        self.load_consts()

    ps_range = (0, 4)

    def nps(self, lo=None, hi=None):
        lo = self.ps_range[0] if lo is None else lo
        hi = self.ps_range[1] if hi is None else hi
        n = hi - lo
        b = self.psb[lo + (self.ps_rr % n)]
        self.ps_rr += 1
        return b

    def dve(self, fn, r, w):
        return self.fw.op("dve", fn, r, w)

    def rsqrt(self, out_buf, out_ap, in_buf, in_ap):
        self.act(lambda: self.A.activation(in_ap, in_ap, AF.Sqrt), [in_buf], [in_buf])
        self.dve(lambda: self.V.reciprocal(out_ap, in_ap), [in_buf], [out_buf])

    def act(self, fn, r, w):
        return self.fw.op("act", fn, r, w)

    def pool(self, fn, r, w):
        return self.fw.op("pool", fn, r, w)

    def pe(self, fn, r, w, signal=True):
        return self.fw.op("pe", fn, r, w, signal=signal)

    def load(self, dst, dst_ap, src, src_ap, q="sp"):
        if dst_ap.dtype != src_ap.dtype:
            q = "pool"
        return self.fw.dma(q, dst, dst_ap, src, src_ap)

    def load_consts(self):
        fw, d = self.fw, self.d
        self.ident = fw.sb([128, 128], BF16, "ident")
        self.bones = fw.sb([128, 128], BF16, "bones")
        self.mask = fw.sb([128, 2560], BF16, "mask")
        self.ones = fw.sb([128, 128], BF16, "ones")
        self.onesf = fw.sb([128, 128], F32, "onesf")
        self.rope = None
        self.cond = fw.sb([128, 16], F32, "cond")
        self.idx1 = fw.sb([128, 12], I32, "idx1")
        self.idx2 = fw.sb([128, 4], I32, "idx2")
        for nm in ("ident", "bones", "mask", "cond", "idx1", "idx2"):
            t = getattr(self, nm)
            self.load(t, t[:], d[nm], d[nm].ap())
        self.pool(lambda: self.G.memset(self.ones[:], 1.0), [], [self.ones])
        self.pool(lambda: self.G.memset(self.onesf[:], 1.0), [], [self.onesf])
        self.hm = fw.sb([128, 2], F32, "hm")
        self.dve(lambda: self.V.tensor_copy(self.hm[:, 0:2], self.bones[:, 0:128:64]), [self.bones], [self.hm])
        xv = self.xT
        self.load(xv, xv[:, :, 0:NPT], d["xp"], d["xp"].ap().rearrange("(k p) t -> p k t", p=128))
        self.load(xv, xv[:, :, NPT:NTOK], d["xs"], d["xs"].ap().rearrange("(k p) t -> p k t", p=128))

    def stream_plan(self, ids):
        self.plan.extend(ids)

    def stream_begin(self, nblocks):
        fw = self.fw
        self.stg = [fw.sb([128, FBLK // 2], F32, f"wstg{i}") for i in range(2)]
        self.blocks_left = nblocks
        self.dma_issued = {}

    def _issue_dma(self, pos):
        blk = self.plan[pos]
        src = self.d["wst"]
        for hf in range(2):
            st = self.stg[hf]
            self.fw.dma("sp", st, st[:, :], src, src[blk, :, hf * (FBLK // 2):(hf + 1) * (FBLK // 2)])
        self.dma_issued[pos] = True

    def next_block(self, blk):
        pos = self.plan_pos
        assert self.plan[pos] == blk, (pos, self.plan[pos], blk)
        assert self.blocks_left > 0
        if pos not in self.dma_issued:
            self._issue_dma(pos)
        slot = self.slots[pos % len(self.slots)]
        for hf in range(2):
            st = self.stg[hf]
            if hf == 0:
                self.dve(lambda hf=hf, st=st: self.V.tensor_copy(slot[:, hf * (FBLK // 2):(hf + 1) * (FBLK // 2)], st[:, :]), [st], [slot])
            else:
                self.act(lambda hf=hf, st=st: self.A.copy(slot[:, hf * (FBLK // 2):(hf + 1) * (FBLK // 2)], st[:, :]), [st], [slot])
        self.plan_pos += 1
        self.blocks_left -= 1
        return slot

    def prefetch_next(self):
        pos = self.plan_pos
        if self.blocks_left > 0 and pos < len(self.plan) and pos not in self.dma_issued:
            self._issue_dma(pos)

    def load_layer_small(self, l):
        fw, d = self.fw, self.d
        self.sp = fw.sb([128, NSP], F32, "sp")
        self.load(self.sp, self.sp[:], d["sp"], d["sp"][l, :, :])
        sp = self.sp
        o, _ = SP_OFF["mu_rkv"]
        self.mu1 = fw.sb([128, 15], F32, "mu1")
        self.muh = fw.sb([128, 15], F32, "muh")
        self.dve(lambda: self.V.tensor_scalar(self.mu1[:], sp[:, o:o + 15], -1.0, 1.0, ALU.mult, ALU.add), [sp], [self.mu1])
        self.dve(lambda: self.V.tensor_scalar(self.muh[:], sp[:, o:o + 15], 0.5, None, ALU.mult), [sp], [self.muh])
        o2, _ = SP_OFF["rw"]
        self.rwh = fw.sb([128, 16], F32, "rwh")
        self.dve(lambda: self.V.tensor_scalar(self.rwh[:], sp[:, o2:o2 + 16], 0.5, None, ALU.mult), [sp], [self.rwh])
        ob, _ = SP_OFF["b_merge"]
        self.bmh = fw.sb([128, 32], F32, "bmh")
        self.dve(lambda: self.V.tensor_scalar(self.bmh[:], sp[:, ob:ob + 32], 0.5, None, ALU.mult), [sp], [self.bmh])

    def spc(self, name, i=0, n=1):
        o, _ = SP_OFF[name]
        return self.sp[:, o + i:o + i + n]

    def ada(self, l):
        fw = self.fw
        sc = fw.sb([128, 16], BF16, "scond")
        th = fw.sb([128, 16], F32, "cth")
        c = self.cond
        self.act(lambda: self.A.activation(th[:], c[:], AF.Tanh, scale=0.5), [c], [th])
        t2 = fw.sb([128, 16], F32, "ct2")
        self.dve(lambda: self.V.scalar_tensor_tensor(t2[:], th[:], 1.0, c[:], ALU.add, ALU.mult), [th, c], [t2])
        self.dve(lambda: self.V.tensor_scalar(sc[:], t2[:], 0.5, None, ALU.mult), [t2], [sc])
        ps = self.psb[5]
        self.mod = fw.sb([128, 24, 2], F32, "mod")
        self.gmod = fw.sb([128, 8, 2], F32, "gmod")
        fw.push()
        self.stream_begin(6)
        for b in range(6):
            slot = self.next_block(l * BLK_PER_LAYER + b)
            for n in range(4):
                m = b * 4 + n
                for kc in range(8):
                    self.pe(lambda kc=kc, n=n, m=m, slot=slot: self.T.matmul(
                        ps[:, 2 * m:2 * m + 2], slot[:, kc * 512 + n * 128:kc * 512 + n * 128 + 128],
                        sc[:, 2 * kc:2 * kc + 2], start=(kc == 0), stop=(kc == 7)),
                        [slot, sc], [ps], signal=(kc == 7 and n == 3))
            self.prefetch_next()
        fw.pop()
        ob, _ = SP_OFF["b_ada"]
        for cc in range(2):
            self.dve(lambda cc=cc: self.V.tensor_tensor(self.mod[:, :, cc], ps[:, cc:48:2], self.sp[:, ob:ob + 24], ALU.add),
                     [ps, self.sp], [self.mod])
        og, _ = SP_OFF["norm_g"]
        for cc in range(2):
            self.dve(lambda cc=cc: self.V.scalar_tensor_tensor(self.gmod[:, :, cc], self.mod[:, 8:16, cc], 1.0,
                                                               self.sp[:, og:og + 8], ALU.add, ALU.mult),
                     [self.mod, self.sp], [self.gmod])

    def rstd_tile(self, src, views, nfeat, out, N):
        fw = self.fw
        ps = self.nps()
        nk = len(views)
        for i, v in enumerate(views):
            sq = fw.rot([128, 512], BF16, "sq")
            self.act(lambda v=v, sq=sq: self.A.activation(sq[:, :N], v, AF.Square), [src], [sq])
            self.pe(lambda i=i, sq=sq: self.T.matmul(ps[:, :N], self.ones[:], sq[:, :N], start=(i == 0), stop=(i == nk - 1)),
                    [self.ones, sq], [ps], signal=True)
        t = fw.sb([128, 512], F32, "rs_t")
        self.dve(lambda: self.V.tensor_scalar(t[:, :N], ps[:, :N], 1.0 / nfeat, EPS, ALU.mult, ALU.add), [ps], [t])
        self.rsqrt(out, out[:, :N], t, t[:, :N])

    def make_h(self, cc, x0, h0, N):
        fw = self.fw
        fw.push()
        rstd = fw.sb([128, 512], F32, "rstd")
        self.rstd_tile(self.xT, [self.xT[:, kc, x0:x0 + N] for kc in range(8)], float(D), rstd, N)
        for kc in range(8):
            tmp = fw.rot([128, 512], F32, "htmp")
            self.dve(lambda kc=kc, tmp=tmp: self.V.scalar_tensor_tensor(
                tmp[:, :N], self.xT[:, kc, x0:x0 + N], self.gmod[:, kc, cc:cc + 1], rstd[:, :N], ALU.mult, ALU.mult),
                [self.xT, self.gmod, rstd], [tmp])
            self.act(lambda kc=kc, tmp=tmp: self.A.activation(
                self.hTs[h0 // 512][:, kc, 0:N], tmp[:, :N], AF.Identity, bias=self.mod[:, kc, cc:cc + 1], scale=1.0),
                [tmp, self.mod], [self.hTs[h0 // 512]])
        fw.pop()

    def zmm(self, slot, W, c0, w, h0, N, ps=None, prow=0):
        ps = ps or self.nps()
        for kc in range(8):
            self.pe(lambda kc=kc: self.T.matmul(ps[prow:prow + w, :N], slot[:, kc * W + c0:kc * W + c0 + w],
                                                self.hTs[h0 // 512][:, kc, 0:N], start=(kc == 0), stop=(kc == 7)),
                    [slot, self.hTs[h0 // 512]], [ps], signal=(kc == 7))
        return ps

    def silu2(self, ps, rows, N, out_ap, out_buf):
        fw = self.fw
        th = fw.rot([128, 512], F32, "s2th")
        self.act(lambda: self.A.activation(th[:rows, :N], ps[:rows, :N], AF.Tanh, scale=0.5), [ps], [th])
        self.dve(lambda: self.V.scalar_tensor_tensor(out_ap, th[:rows, :N], 1.0, ps[:rows, :N], ALU.add, ALU.mult),
                 [th, ps], [out_buf])

    def gelu2(self, ps, rows, N, out_ap, out_buf):
        fw = self.fw
        u = fw.rot([128, 512], F32, "g2u")
        self.act(lambda: self.A.activation(u[:rows, :N], ps[:rows, :N], AF.Square), [ps], [u])
        self.dve(lambda: self.V.tensor_scalar(u[:rows, :N], u[:rows, :N], 0.044715, 1.0, ALU.mult, ALU.add), [u], [u])
        self.dve(lambda: self.V.tensor_tensor(u[:rows, :N], u[:rows, :N], ps[:rows, :N], ALU.mult), [u, ps], [u])
        self.act(lambda: self.A.activation(u[:rows, :N], u[:rows, :N], AF.Tanh, scale=0.7978845608028654), [u], [u])
        self.dve(lambda: self.V.scalar_tensor_tensor(out_ap, u[:rows, :N], 1.0, ps[:rows, :N], ALU.add, ALU.mult),
                 [u, ps], [out_buf])

    def transpose_to(self, src_buf, src_ap, dst_buf, dst_ap, rows=128, cols=128, eng="act"):
        pt = self.pst[self.pst_rr % 2]
        self.pst_rr += 1
        self.pe(lambda: self.T.transpose(pt[:cols, :rows], src_ap, self.ident[:rows, :rows]), [src_buf, self.ident], [pt])
        if eng == "act":
            self.act(lambda: self.A.copy(dst_ap, pt[:cols, :rows]), [pt], [dst_buf])
        else:
            self.dve(lambda: self.V.tensor_copy(dst_ap, pt[:cols, :rows]), [pt], [dst_buf])

    def phaseC(self, l, h0, N, o0):
        fw = self.fw
        base = l * BLK_PER_LAYER
        fw.push()
        self.stream_begin(3)
        U2 = fw.sb([128, 4, 512], BF16, "U2")
        GV = fw.sb([128, 4, 512], BF16, "GV")
        GC2 = fw.sb([128, 4, 512], BF16, "GC2")
        wsT = fw.sb([128, 512], BF16, "wsT")
        bsb = fw.sb([128, 512], F32, "bsb")
        self.load(wsT, wsT[:], self.d["wsT"], self.d["wsT"][l, :, :])
        self.load(bsb, bsb[:], self.d["bsb"], self.d["bsb"][l, :, :])
        slot = self.next_block(base + WIN_IDX["C0"])
        for c in range(4):
            ps = self.zmm(slot, 512, c * 128, 128, h0, N)
            self.gelu2(ps, 128, N, U2[:, c, :N], U2)
        self.prefetch_next()
        slot = self.next_block(base + WIN_IDX["C1"])
        for c in range(4):
            ps = self.zmm(slot, 512, c * 128, 128, h0, N)
            self.gelu2(ps, 128, N, GV[:, c, :N], GV)
        self.prefetch_next()
        slot = self.next_block(base + WIN_IDX["C2"])
        for c in range(4):
            ps = self.zmm(slot, 512, c * 128, 128, h0, N)
            self.silu2(ps, 128, N, GC2[:, c, :N], GC2)
        self.prefetch_next()
        psm = self.nps()
        psq = self.nps()
        for c in range(4):
            self.pe(lambda c=c: self.T.matmul(psm[:, :N], self.ones[:], GV[:, c, :N], start=(c == 0), stop=(c == 3)),
                    [self.ones, GV], [psm], signal=(c == 3))
        for c in range(4):
            sq = fw.rot([128, 512], BF16, "gsq")
            self.act(lambda c=c, sq=sq: self.A.activation(sq[:, :N], GV[:, c, :N], AF.Square), [GV], [sq])
            self.pe(lambda c=c, sq=sq: self.T.matmul(psq[:, :N], self.ones[:], sq[:, :N], start=(c == 0), stop=(c == 3)),
                    [self.ones, sq], [psq], signal=True)
        mu = fw.sb([128, 512], F32, "gmu")
        msq = fw.sb([128, 512], F32, "gmsq")
        var = fw.sb([128, 512], F32, "gvar")
        rstd = fw.sb([128, 512], F32, "grstd")
        self.dve(lambda: self.V.tensor_scalar(mu[:, :N], psm[:, :N], 1.0 / 512, None, ALU.mult), [psm], [mu])
        self.dve(lambda: self.V.tensor_tensor(msq[:, :N], mu[:, :N], mu[:, :N], ALU.mult), [mu], [msq])
        self.dve(lambda: self.V.scalar_tensor_tensor(var[:, :N], psq[:, :N], 1.0 / 512, msq[:, :N], ALU.mult, ALU.subtract),
                 [psq, msq], [var])
        self.dve(lambda: self.V.tensor_scalar(var[:, :N], var[:, :N], 4e-5, None, ALU.add), [var], [var])
        self.rsqrt(rstd, rstd[:, :N], var, var[:, :N])
        VN = fw.sb([128, 4, 512], BF16, "VN")
        for c in range(4):
            t = fw.rot([128, 512], F32, "lnt")
            self.dve(lambda c=c, t=t: self.V.tensor_tensor(t[:, :N], GV[:, c, :N], mu[:, :N], ALU.subtract), [GV, mu], [t])
            self.dve(lambda t=t: self.V.tensor_tensor(t[:, :N], t[:, :N], rstd[:, :N], ALU.mult), [t, rstd], [t])
            self.act(lambda c=c, t=t: self.A.activation(VN[:, c, :N], t[:, :N], AF.Identity, bias=self.spc("gln_b", c),
                                                        scale=self.spc("gln_g", c)), [t, self.sp], [VN])
        nsub = N // 128
        for g in range(4):
            pmix = self.nps()
            for s in range(nsub):
                vtm = fw.rot([128, 128], BF16, "vtm")
                self.transpose_to(VN, VN[:, g, s * 128:(s + 1) * 128], vtm, vtm[:], eng=("act" if s % 2 else "dve"))
                self.pe(lambda g=g, s=s, vtm=vtm: self.T.matmul(pmix[:, s * 128:(s + 1) * 128], vtm[:],
                                                                 wsT[:, g * 128:(g + 1) * 128], start=True, stop=True),
                        [vtm, wsT], [pmix], signal=(s == nsub - 1))
            t = fw.rot([128, 512], F32, "mixt")
            for s in range(nsub):
                self.dve(lambda g=g, s=s, t=t: self.V.tensor_tensor(t[:, s * 128:(s + 1) * 128], pmix[:, s * 128:(s + 1) * 128],
                                                                     bsb[:, g * 128:(g + 1) * 128], ALU.add), [pmix, bsb], [t])
            self.dve(lambda g=g, t=t: self.V.scalar_tensor_tensor(t[:, :N], t[:, :N], 0.25, U2[:, g, :N], ALU.mult, ALU.mult),
                     [t, U2], [t])
            self.dve(lambda g=g, t=t: self.V.tensor_tensor(self.oTs[o0 // 512][2][:, g, 0:N], t[:, :N], GC2[:, g, :N], ALU.mult),
                     [t, GC2], [self.oTs[o0 // 512][2]])
        fw.pop()

    def merge_out(self, l, cc, x0, NT):
        fw = self.fw
        base = l * BLK_PER_LAYER
        ntile = NT // 512
        fw.push()
        self.stream_begin(18)
        merged = fw.sb([128, 8, NT], BF16, "merged")
        for d in range(8):
            slotM = self.next_block(base + 18 + 2 * d)
            self.prefetch_next()
            slotB = self.next_block(base + 19 + 2 * d)
            for tt in range(ntile):
                t0 = tt * 512
                acc = fw.rot([128, 512], F32, "macc")
                for n in range(4):
                    psg = self.nps()
                    for kc in range(8):
                        self.pe(lambda kc=kc, n=n, tt=tt: self.T.matmul(psg[:, :], slotM[:, kc * 512 + n * 128:kc * 512 + n * 128 + 128],
                                                                 self.hTs[tt][:, kc, :], start=(kc == 0), stop=(kc == 7)),
                                [slotM, self.hTs[tt]], [psg], signal=(kc == 7))
                    psp = self.nps()
                    for k4 in range(4):
                        self.pe(lambda k4=k4, n=n, tt=tt: self.T.matmul(psp[:, :], slotB[:, (n * 4 + k4) * 128:(n * 4 + k4) * 128 + 128],
                                                                 self.oTs[tt][n][:, k4, :], start=(k4 == 0), stop=(k4 == 3)),
                                [slotB, self.oTs[tt][n]], [psp], signal=(k4 == 3))
                    th = fw.rot([128, 512], F32, "mth")
                    self.act(lambda n=n, th=th, psg=psg: self.A.activation(th[:], psg[:], AF.Tanh, bias=self.bmh[:, d * 4 + n:d * 4 + n + 1],
                                                                           scale=0.5), [psg, self.bmh], [th])
                    if n == 0:
                        self.dve(lambda th=th, psp=psp: self.V.scalar_tensor_tensor(acc[:], th[:], 1.0, psp[:], ALU.add, ALU.mult),
                                 [th, psp], [acc])
                    else:
                        self.dve(lambda th=th, psp=psp: self.V.scalar_tensor_tensor(th[:], th[:], 1.0, psp[:], ALU.add, ALU.mult),
                                 [th, psp], [th])
                        self.dve(lambda th=th: self.V.tensor_tensor(acc[:], acc[:], th[:], ALU.add), [acc, th], [acc])
                self.act(lambda acc=acc, t0=t0: self.A.mul(merged[:, d, t0:t0 + 512], acc[:], 0.5), [acc], [merged])
            self.prefetch_next()
        for b in range(2):
            slotO = self.next_block(base + 34 + b)
            self.prefetch_next()
            for dd in range(4):
                dch = b * 4 + dd
                for tt in range(ntile):
                    t0 = tt * 512
                    ps = self.nps()
                    for kc in range(8):
                        self.pe(lambda kc=kc, dd=dd: self.T.matmul(ps[:, :], slotO[:, kc * 512 + dd * 128:kc * 512 + dd * 128 + 128],
                                                                   merged[:, kc, t0:t0 + 512], start=(kc == 0), stop=(kc == 7)),
                                [slotO, merged], [ps], signal=(kc == 7))
                    self.dve(lambda dch=dch, t0=t0, ps=ps: self.V.scalar_tensor_tensor(
                        self.xT[:, dch, x0 + t0:x0 + t0 + 512], ps[:, :], self.mod[:, 16 + dch, cc:cc + 1],
                        self.xT[:, dch, x0 + t0:x0 + t0 + 512], ALU.mult, ALU.add), [ps, self.mod, self.xT], [self.xT])
        fw.pop()

    def final_out(self):
        fw = self.fw
        for (x0, N, dst) in ((0, 512, ("yp", 0)), (512, 512, ("yp", 512)), (NPT, 512, ("ys", 0))):
            fw.push()
            rstd = fw.sb([128, 512], F32, "frstd")
            self.rstd_tile(self.xT, [self.xT[:, kc, x0:x0 + N] for kc in range(8)], float(D), rstd, N)
            stg = fw.sb([128, 8, 512], F32, "fstg")
            for kc in range(8):
                self.dve(lambda kc=kc: self.V.scalar_tensor_tensor(stg[:, kc, :], self.xT[:, kc, x0:x0 + N], self.spc("fin_g", kc),
                                                                   rstd[:, :], ALU.mult, ALU.mult), [self.xT, self.sp, rstd], [stg])
            dt = self.d[dst[0]]
            ncol = NPT if dst[0] == "yp" else NST
            dview = dt.ap().rearrange("(k p) t -> p k t", p=128)[:, :, dst[1]:dst[1] + N]
            fw.dma("sp", dt, dview, stg, stg[:], sem_owner=self.outsem)
            fw.pop()

    def shift_evac(self, ps, rows, N, nseq, mu1, muh, out_ap, out_buf, tanh=False):
        fw = self.fw
        zt = fw.rot([128, 512], F32, "shz")
        o32 = fw.rot([128, 512], F32, "sho")
        self.act(lambda: self.A.copy(zt[:rows, :N], ps[:rows, :N]), [ps], [zt])
        self.dve(lambda: self.V.tensor_scalar(o32[:rows, :N], zt[:rows, :N], mu1, None, ALU.mult), [zt, self.mu1], [o32])
        z3 = zt[:rows, :N].rearrange("p (s t) -> p s t", s=nseq)
        o3 = o32[:rows, :N].rearrange("p (s t) -> p s t", s=nseq)
        Tq = N // nseq
        self.dve(lambda: self.V.scalar_tensor_tensor(o3[:, :, 1:Tq], z3[:, :, 0:Tq - 1], muh, o3[:, :, 1:Tq], ALU.mult, ALU.add),
                 [zt, self.muh, o32], [o32])
        self.dve(lambda: self.V.scalar_tensor_tensor(o3[:, :, 0:Tq - 1], z3[:, :, 1:Tq], muh, o3[:, :, 0:Tq - 1], ALU.mult, ALU.add),
                 [zt, self.muh, o32], [o32])
        if tanh:
            self.act(lambda: self.A.activation(out_ap, o32[:rows, :N], AF.Tanh), [o32], [out_buf])
        else:
            self.act(lambda: self.A.copy(out_ap, o32[:rows, :N]), [o32], [out_buf])

    def load_rwkv_w(self, l, own):
        fw, d = self.fw, self.d
        ncol = 256 if own else 1024
        self.wup = fw.sb([64, ncol], BF16, "wup")
        self.aup = fw.sb([64, ncol], BF16, "aup")
        sw, sa = (d["wupo"], d["aupo"]) if own else (d["wup"], d["aup"])
        self.load(self.wup, self.wup[:, :], sw, sw[l, :, :])
        self.load(self.aup, self.aup[:, :], sa, sa[l, :, :])

    def phaseA_prompt(self, l, half):
        fw = self.fw
        base = l * BLK_PER_LAYER
        h0 = half * 512
        fw.push()
        self.load_rwkv_w(l, False)
        zz = [fw.sb([128, 4, 512], BF16, nm) for nm in ("zr", "zk", "zv")]
        lo = [fw.sb([64, 512], BF16, nm) for nm in ("twdf", "twdb", "adT")]
        GA2 = fw.sb([128, 4, 512], BF16, "GA2")
        fw.push()
        self.stream_begin(5)
        for which in range(3):
            slot = self.next_block(base + WIN_IDX[f"A{which}"])
            for c in range(4):
                ps = self.zmm(slot, 512, c * 128, 128, h0, 512)
                i = which * 4 + c
                self.shift_evac(ps, 128, 512, 2, self.mu1[:, i:i + 1], self.muh[:, i:i + 1], zz[which][:, c, :], zz[which])
            self.prefetch_next()
        slot = self.next_block(base + WIN_IDX["A3"])
        for i in range(3):
            ps = self.zmm(slot, 192, i * 64, 64, h0, 512)
            self.shift_evac(ps, 64, 512, 2, self.mu1[0:64, 12 + i:13 + i], self.muh[0:64, 12 + i:13 + i], lo[i][:, :], lo[i], tanh=(i < 2))
        self.prefetch_next()
        slot = self.next_block(base + WIN_IDX["A4"])
        for c in range(4):
            ps = self.zmm(slot, 512, c * 128, 128, h0, 512)
            self.silu2(ps, 128, 512, GA2[:, c, :], GA2)
        self.prefetch_next()
        fw.pop()
        for sq in range(2):
            for pair in range(4):
                t0 = sq * 256
                seqi = half * 2 + sq

                def yout(c, yfin, pair=pair, t0=t0):
                    cs = t0 + c * 128
                    self.dve(lambda: self.V.scalar_tensor_tensor(self.oTs[half][0][:, pair, cs:cs + 128], yfin[:, :], 0.5,
                                                                 GA2[:, pair, cs:cs + 128], ALU.mult, ALU.mult), [yfin, GA2], [self.oTs[half][0]])

                def stout(dd, ST, pair=pair, seqi=seqi):
                    so = self.d["stout"]
                    row = (((l * 2 + dd) * 4 + seqi) * 4 + pair) * 128
                    fw.dma("sp", so, so[row:row + 128, :], ST, ST[:, :], sem_owner=self.outsem)

                J = dict(T=256, r=(zz[0], lambda a, b, pair=pair, t0=t0: zz[0][:, pair, t0 + a:t0 + b]),
                         k=(zz[1], lambda a, b, pair=pair, t0=t0: zz[1][:, pair, t0 + a:t0 + b]),
                         v=(zz[2], lambda a, b, pair=pair, t0=t0: zz[2][:, pair, t0 + a:t0 + b]),
                         twd=[(lo[0], lambda a, b, t0=t0: lo[0][:, t0 + a:t0 + b]), (lo[1], lambda a, b, t0=t0: lo[1][:, t0 + a:t0 + b])],
                         ad=(lo[2], lambda a, b, t0=t0: lo[2][:, t0 + a:t0 + b]),
                         par=lambda nm, pair=pair: self.sp[:, rw_col(nm, pair):rw_col(nm, pair) + 1],
                         parh=lambda nm, pair=pair: self.rwh[:, RW_NAMES.index(nm) * 4 + pair:RW_NAMES.index(nm) * 4 + pair + 1],
                         parbufs=[self.sp, self.rwh],
                         wup=lambda dd, pair=pair: self.wup[:, dd * 512 + pair * 128:dd * 512 + pair * 128 + 128],
                         aup=lambda dd, pair=pair: self.aup[:, dd * 512 + pair * 128:dd * 512 + pair * 128 + 128],
                         st0=None, yout=yout, stout=stout, scoped=False)
                self.rwkv_job(J)
        fw.pop()

    def phaseA_contrib(self, l):
        fw = self.fw
        base = l * BLK_PER_LAYER
        self.GA2 = fw.sb([128, 4, 512], BF16, "GA2s")
        fw.push()
        self.stream_begin(5)
        for which, (part, row0) in enumerate((("rk", 0), ("rk", 512), ("vx", VX_OFF["v"]))):
            slot = self.next_block(base + WIN_IDX[f"A{which}"])
            for c in range(4):
                self.contrib_rows(l, slot, 512, c * 128, 128, part, row0 + c * 128)
            self.prefetch_next()
        slot = self.next_block(base + WIN_IDX["A3"])
        for i in range(3):
            self.contrib_rows(l, slot, 192, i * 64, 64, "vx", VX_OFF["lora"] + i * 64)
        self.prefetch_next()
        slot = self.next_block(base + WIN_IDX["A4"])
        for c in range(4):
            ps = self.zmm(slot, 512, c * 128, 128, 0, 512)
            self.silu2(ps, 128, 512, self.GA2[:, c, :], self.GA2)
        self.prefetch_next()
        fw.pop()

    def gather_rows(self, dst, dst_ap, src, idx_col):
        fw = self.fw
        idx = self.idx1 if idx_col < 12 else self.idx2
        col = idx_col if idx_col < 12 else idx_col - 12
        fw.dma("pool", dst, None, src, None, extra_reads=[idx],
               fn=lambda: self.G.indirect_dma_start(out=dst_ap, out_offset=None, in_=src.h.ap(),
                                                    in_offset=bass.IndirectOffsetOnAxis(ap=idx[:, col:col + 1], axis=0)))

    def phaseA_consume(self, l):
        fw = self.fw
        V, A, G = self.V, self.A, self.G
        fw.push()
        self.load_rwkv_w(l, True)
        spo = fw.sb([128, 12], F32, "spo")
        self.load(spo, spo[:, :], self.d["spo"], self.d["spo"][l, :, :])
        spoh = fw.sb([128, 4], F32, "spoh")
        self.dve(lambda: V.tensor_scalar(spoh[:, :], spo[:, 0:4], 0.5, None, ALU.mult), [spo], [spoh])
        mu1o = fw.sb([128, 3], F32, "mu1o")
        muho = fw.sb([128, 3], F32, "muho")
        self.dve(lambda: V.tensor_scalar(mu1o[:, :], spo[:, 9:12], -1.0, 1.0, ALU.mult, ALU.add), [spo], [mu1o])
        self.dve(lambda: V.tensor_scalar(muho[:, :], spo[:, 9:12], 0.5, None, ALU.mult), [spo], [muho])
        T = DSEQ
        zz = [fw.sb([128, T], BF16, nm) for nm in ("sr", "sk", "sv")]
        lo = [fw.sb([64, T], BF16, nm) for nm in ("stwf", "stwb", "sad")]
        fw.push()
        raw = fw.sb([128, T], BF16, "sraw")
        o32 = fw.sb([128, T], F32, "so32")

        def shift_full(rows, src, m1, mh, dst, tanh=False):
            self.dve(lambda: V.tensor_scalar(o32[:rows, :], src[:rows, :], m1, None, ALU.mult), [src, mu1o, self.mu1], [o32])
            self.dve(lambda: V.scalar_tensor_tensor(o32[:rows, 1:T], src[:rows, 0:T - 1], mh, o32[:rows, 1:T], ALU.mult, ALU.add),
                     [src, muho, self.muh, o32], [o32])
            self.dve(lambda: V.scalar_tensor_tensor(o32[:rows, 0:T - 1], src[:rows, 1:T], mh, o32[:rows, 0:T - 1], ALU.mult, ALU.add),
                     [src, muho, self.muh, o32], [o32])
            if tanh:
                self.act(lambda: A.activation(dst[:rows, :], o32[:rows, :], AF.Tanh), [o32], [dst])
            else:
                self.act(lambda: A.copy(dst[:rows, :], o32[:rows, :]), [o32], [dst])

        for which in range(3):
            src = self.ag1_out[l]["rk" if which < 2 else "vx"]
            for q in range(4):
                self.gather_rows(raw, raw[:, q * 512:(q + 1) * 512], src, which * 4 + q)
            shift_full(128, raw, mu1o[:, which:which + 1], muho[:, which:which + 1], zz[which])
        agv = self.ag1_out[l]["vx"]
        for i in range(3):
            for q in range(4):
                r0 = q * 864 + VX_OFF["lora"] + i * 64
                fw.dma("sp", raw, raw[0:64, q * 512:(q + 1) * 512], agv, agv[r0:r0 + 64, :])
            shift_full(64, raw, self.mu1[0:64, 12 + i:13 + i], self.muh[0:64, 12 + i:13 + i], lo[i], tanh=(i < 2))
        fw.pop()
        stg = [None]

        def yout(c, yfin):
            q, cc = c // 4, c % 4
            if cc == 0:
                stg[0] = fw.rot([128, 512], BF16, "ystg", n=2)
            st = stg[0]
            self.act(lambda: A.copy(st[:, cc * 128:(cc + 1) * 128], yfin[:, :]), [yfin], [st])
            if cc == 3:
                ag = self.ag2_in[l]
                fw.dma("sp", ag, ag[q * 128:(q + 1) * 128, :], st, st[:, :])

        pidx = {nm: i for i, nm in enumerate(RW_NAMES)}
        J = dict(T=T, r=(zz[0], lambda a, b: zz[0][:, a:b]), k=(zz[1], lambda a, b: zz[1][:, a:b]), v=(zz[2], lambda a, b: zz[2][:, a:b]),
                 twd=[(lo[0], lambda a, b: lo[0][:, a:b]), (lo[1], lambda a, b: lo[1][:, a:b])],
                 ad=(lo[2], lambda a, b: lo[2][:, a:b]),
                 par=lambda nm: spo[:, pidx[nm]:pidx[nm] + 1],
                 parh=lambda nm: spoh[:, pidx[nm]:pidx[nm] + 1],
                 parbufs=[spo, spoh],
                 wup=lambda dd: self.wup[:, dd * 128:(dd + 1) * 128],
                 aup=lambda dd: self.aup[:, dd * 128:(dd + 1) * 128],
                 st0=lambda dd: (self.d["st0"], self.d["st0"][l, dd, :, :]), yout=yout, stout=None, seg=128, segpar=True)
        self.rwkv_job(J)
        self.allgather(self.ag2_out[l], self.ag2_in[l])
        fw.pop()

    def phaseA_final(self, l):
        fw = self.fw
        fw.push()
        for r in range(4):
            ya = fw.rot([128, 512], BF16, "ya", n=2)
            self.gather_rows(ya, ya[:, :], self.ag2_out[l], 12 + r)
            self.dve(lambda r=r, ya=ya: self.V.scalar_tensor_tensor(self.oTs[0][0][:, r, 0:512], ya[:, :], 0.5, self.GA2[:, r, :], ALU.mult, ALU.mult),
                     [ya, self.GA2], [self.oTs[0][0]])
        fw.pop()

    def rwkv_job(self, J):
        fw = self.fw
        V, A, G, T_ = self.V, self.A, self.G, self.T
        T = J["T"]
        nch = T // 128
        SEG = J.get("seg", 256)
        nseg = T // SEG
        ncs = SEG // 128
        rB, rf = J["r"]
        kB, kf = J["k"]
        vB, vf = J["v"]
        adB, adf = J["ad"]
        par, parh, pbufs = J["par"], J["parh"], J["parbufs"]
        self.ps_range = (0, 6)
        scoped = J.get("scoped", True)
        jpush = (lambda: fw.push()) if scoped else (lambda: None)
        jpop = (lambda: fw.pop()) if scoped else (lambda: None)
        jt = (lambda shp, dt, nm: fw.sb(shp, dt, nm)) if scoped else (lambda shp, dt, nm: fw.rot(shp, dt, "J" + nm, n=1))
        jpush()
        kap = jt([128, T], BF16, "kap")
        Vtm = jt([128, nch, 128], BF16, "Vtm")
        Yacc = jt([128, nch, 128], F32, "Yacc")
        Bacc = jt([128, T], F32, "Bacc")
        ST = [jt([128, 64], F32, f"ST{dd}") for dd in range(2)]
        STb = [jt([128, 64], BF16, f"STb{dd}") for dd in range(2)]
        KW = min(T, 512)
        self.pool(lambda: G.memset(Yacc[:, :, :], 0.0), [], [Yacc])
        self.pool(lambda: G.memset(Bacc[:, :], 0.0), [], [Bacc])
        jpush()
        for p0 in range(0, T, 512):
            N = min(512, T - p0)
            kk = fw.rot([128, KW], F32, "kk", n=(2 if scoped else 1))
            sq = fw.rot([128, KW], BF16, "kksq", n=(2 if scoped else 1))
            self.dve(lambda: V.tensor_scalar(kk[:, :N], kf(p0, p0 + N), par("k_k"), None, ALU.mult), [kB] + pbufs, [kk])
            self.act(lambda: A.activation(sq[:, :N], kk[:, :N], AF.Square), [kk], [sq])
            ps = self.nps()
            self.pe(lambda: T_.matmul(ps[:, :N], self.bones[:, :], sq[:, :N], start=True, stop=True), [self.bones, sq], [ps])
            t = fw.rot([128, KW], F32, "kkt", n=(2 if scoped else 1))
            self.dve(lambda: V.tensor_scalar(t[:, :N], ps[:, :N], 1e-24, None, ALU.max), [ps], [t])
            self.rsqrt(t, t[:, :N], t, t[:, :N])
            self.dve(lambda: V.tensor_tensor(kap[:, p0:p0 + N], kk[:, :N], t[:, :N], ALU.mult), [kk, t], [kap])
        import os
        STOP = int(os.environ.get("RWKV_STOP", "99"))
        if STOP <= 1:
            jpop(); jpop(); return
        for c in range(nch):
            self.transpose_to(vB, vf(c * 128, c * 128 + 128), Vtm, Vtm[:, c, :], eng=("act" if c % 2 else "dve"))
        jpop()
        if STOP <= 2:
            jpop(); return
        jpush()
        for dd in range(2):
            if J["st0"] is None:
                self.pool(lambda dd=dd: G.memset(ST[dd][:, :], 0.0), [], [ST[dd]])
            else:
                src, sap = J["st0"](dd)
                fw.dma("sp", ST[dd], ST[dd][:, :], src, sap)
            self.act(lambda dd=dd: A.copy(STb[dd][:, :], ST[dd][:, :]), [ST[dd]], [STb[dd]])
        MK = self.mask
        def rw_segment(dd, sg, res, segpar):
            sfx = "fb"[dd]
            twB, twf = J["twd"][dd]
            s0 = sg * SEG
            N = SEG
            f32t = lambda nm: fw.rot([128, SEG], F32, nm + (str(dd) if segpar else ""), n=1)
            a = f32t("ra")
            ps = self.nps()
            self.pe(lambda: T_.matmul(ps[:, :N], J["aup"](dd), adf(s0, s0 + N), start=True, stop=True), [self.aup, adB], [ps])
            self.act(lambda: A.activation(a[:, :], ps[:, :N], AF.Tanh, bias=parh("a0_" + sfx), scale=0.5), [ps] + pbufs, [a])
            yield
            self.dve(lambda: V.tensor_scalar(a[:, :], a[:, :], 0.5, 0.5, ALU.mult, ALU.add), [a], [a])
            kt = f32t("rkt")
            self.dve(lambda: V.tensor_scalar(kt[:, :], a[:, :], 1.0, par("k_a"), ALU.subtract, ALU.mult), [a] + pbufs, [kt])
            self.dve(lambda: V.scalar_tensor_tensor(kt[:, :], kt[:, :], 1.0, kf(s0, s0 + N), ALU.add, ALU.mult), [kt, kB], [kt])
            b = f32t("rb")
            self.dve(lambda: V.tensor_tensor(b[:, :], a[:, :], kap[:, s0:s0 + N], ALU.mult), [a, kap], [b])
            lw = f32t("rlw")
            ps = self.nps()
            self.pe(lambda: T_.matmul(ps[:, :N], J["wup"](dd), twf(s0, s0 + N), start=True, stop=True), [self.wup, twB], [ps])
            self.act(lambda: A.activation(lw[:, :], ps[:, :N], AF.Tanh, bias=parh("w0_" + sfx), scale=0.5), [ps] + pbufs, [lw])
            yield
            self.dve(lambda: V.tensor_scalar(lw[:, :], lw[:, :], -0.3032653298563167, -0.3032653298563167, ALU.mult, ALU.add), [lw], [lw])
            rkr = fw.rot([128, SEG], BF16, "rkr" + str(dd), n=1)
            self.dve(lambda: V.scalar_tensor_tensor(rkr[:, :], kt[:, :], par("r_k"), rf(s0, s0 + N), ALU.mult, ALU.mult),
                     [kt, rB] + pbufs, [rkr])
            ps = self.nps()
            self.pe(lambda: T_.matmul(ps[:, :N], self.bones[:, :], rkr[:, :], start=True, stop=True), [self.bones, rkr], [ps])
            self.dve(lambda: V.tensor_tensor(Bacc[:, s0:s0 + N], Bacc[:, s0:s0 + N], ps[:, :N], ALU.add), [Bacc, ps], [Bacc])
            P = f32t("rP")
            for c in range(ncs):
                self.dve(lambda c=c: V.tensor_tensor_scan(P[:, c * 128:(c + 1) * 128], self.onesf[:, :], lw[:, c * 128:(c + 1) * 128],
                                                          0.0, ALU.mult, ALU.add), [self.onesf, lw], [P])
            Q = f32t("rQ")
            R = f32t("rR")
            self.dve(lambda: V.tensor_tensor(Q[:, :], P[:, :], lw[:, :], ALU.subtract), [P, lw], [Q])
            for c in range(ncs):
                self.dve(lambda c=c: V.tensor_scalar(R[:, c * 128:(c + 1) * 128], P[:, c * 128:(c + 1) * 128], -1.0,
                                                     P[:, c * 128 + 127:c * 128 + 128], ALU.mult, ALU.add), [P], [R])
            gL = fw.rot([128, 2], F32, "gL" + str(dd), n=2)
            self.act(lambda: A.activation(gL[:, 0:ncs], P[:, 127:SEG:128], AF.Exp), [P], [gL])
            yield
            if dd == 0:
                srcs = [(P, -1.0, a), (Q, 1.0, Q), (P, 1.0, P), (R, 1.0, R)]
            else:
                RL = f32t("rRL")
                self.dve(lambda: V.tensor_tensor(RL[:, :], R[:, :], lw[:, :], ALU.add), [R, lw], [RL])
                srcs = [(RL, -1.0, a), (R, 1.0, R), (RL, 1.0, RL), (Q, 1.0, Q)]
            E = [None] * 4
            for i, (sb_, sc_, dst_) in enumerate(srcs):
                self.act(lambda i=i, sb_=sb_, sc_=sc_, dst_=dst_: A.activation(dst_[:, :], sb_[:, :], AF.Exp, scale=sc_), [sb_], [dst_])
                E[i] = dst_
            Kd2 = fw.rot([128, 2, SEG], BF16, "Kd2" + str(dd), n=1)
            Bd2 = fw.rot([128, 2, SEG], BF16, "Bd2" + str(dd), n=1)
            KL = fw.rot([128, SEG], BF16, "KL" + str(dd), n=1)
            BL = fw.rot([128, SEG], BF16, "BL" + str(dd), n=1)
            KqRq2 = fw.rot([128, 2, ncs, 2, 128], BF16, "KqRq2" + str(dd), n=1)
            hm = self.hm
            kap3 = kap[:, s0:s0 + N].rearrange("p (c t) -> p c t", c=ncs)
            r3 = rf(s0, s0 + N).rearrange("p (c t) -> p c t", c=ncs)
            for h in range(2):
                self.dve(lambda h=h: V.scalar_tensor_tensor(Kd2[:, h, :], kt[:, :], hm[:, h:h + 1], E[0][:, :], ALU.mult, ALU.mult), [kt, hm, E[0]], [Kd2])
                self.dve(lambda h=h: V.scalar_tensor_tensor(Bd2[:, h, :], b[:, :], hm[:, h:h + 1], E[0][:, :], ALU.mult, ALU.mult), [b, hm, E[0]], [Bd2])
                self.dve(lambda h=h: V.scalar_tensor_tensor(KqRq2[:, h, :, 0, :], kap3, hm[:, h:h + 1], E[1][:, :].rearrange("p (c t) -> p c t", c=ncs),
                                                            ALU.mult, ALU.mult), [kap, hm, E[1]], [KqRq2])
                self.dve(lambda h=h: V.scalar_tensor_tensor(KqRq2[:, h, :, 1, :], r3, hm[:, h:h + 1], E[2][:, :].rearrange("p (c t) -> p c t", c=ncs),
                                                            ALU.mult, ALU.mult), [rB, hm, E[2]], [KqRq2])
            self.dve(lambda: V.tensor_tensor(KL[:, :], kt[:, :], E[3][:, :], ALU.mult), [kt, E[3]], [KL])
            self.dve(lambda: V.tensor_tensor(BL[:, :], b[:, :], E[3][:, :], ALU.mult), [b, E[3]], [BL])

            res.update(dict(Kd2=Kd2, Bd2=Bd2, KL=KL, BL=BL, KqRq2=KqRq2, gL=gL, sg=sg))
            yield

        def rw_pre(dd, c, Pd, res):
            sfx2 = f"{dd}{c}"
            mk0 = dd * 1280
            Kd2, Bd2, KqRq2 = Pd["Kd2"], Pd["Bd2"], Pd["KqRq2"]
            cs = slice(c * 128, (c + 1) * 128)
            Am = [fw.rot([128, 512], BF16, f"Am{h}_{sfx2}", n=1) for h in range(2)]
            psB = self.nps()
            for h in range(2):
                psA = self.nps()
                rhsA = KqRq2[:, h, c, :, :].rearrange("p a t -> p (a t)")
                self.pe(lambda h=h, psA=psA, rhsA=rhsA: T_.matmul(psA[:, 0:256], Kd2[:, h, cs], rhsA, start=True, stop=True),
                        [Kd2, KqRq2], [psA], signal=False)
                self.pe(lambda h=h, psA=psA, rhsA=rhsA: T_.matmul(psA[:, 256:512], Bd2[:, h, cs], rhsA, start=True, stop=True),
                        [Bd2, KqRq2], [psA])
                self.dve(lambda h=h, psA=psA: V.tensor_tensor(Am[h][:, :], psA[:, :], MK[:, mk0:mk0 + 512], ALU.mult), [psA, MK], [Am[h]])
                self.pe(lambda h=h: T_.matmul(psB[:, h * 128:(h + 1) * 128], KqRq2[:, h, c, 0, :], Bd2[:, h, cs], start=True, stop=True),
                        [KqRq2, Bd2], [psB], signal=(h == 1))
            PT = [fw.rot([128, 2, 128], BF16, f"PT{i}_{sfx2}", n=1) for i in range(2)]
            PX = [fw.rot([128, 2, 256], BF16, f"PX{i}_{sfx2}", n=1) for i in range(2)]
            C64T = fw.rot([128, 2, 128], BF16, "C64T" + sfx2, n=1)
            C128T = fw.rot([128, 2, 128], BF16, "C128T" + sfx2, n=1)
            f2 = lambda t: t[:, :, :].rearrange("p a t -> p (a t)")
            self.dve(lambda: V.tensor_tensor(f2(PT[0]), psB[:, 0:256], MK[:, mk0 + 512:mk0 + 768], ALU.mult), [psB, MK], [PT[0]])
            self.dve(lambda: V.tensor_tensor(f2(C64T), psB[:, 0:256], MK[:, mk0 + 768:mk0 + 1024], ALU.mult), [psB, MK], [C64T])
            self.dve(lambda: V.tensor_tensor(f2(C128T), psB[:, 0:256], MK[:, mk0 + 1024:mk0 + 1280], ALU.mult), [psB, MK], [C128T])
            for h in range(2):
                self.act(lambda h=h: A.copy(PX[0][:, h, 0:128], Am[h][:, 256:384]), [Am[h]], [PX[0]])
            yield
            cur = 0
            Xb = fw.rot([128, 2, 128], BF16, "Xb32" + sfx2, n=1)
            for j in range(1, 6):
                nxt = 1 - cur
                if j == 1:
                    ps = self.nps()
                    pst_ = self.nps()
                    for h in range(2):
                        self.pe(lambda h=h, ps=ps, cur=cur: T_.matmul(ps[:, h * 256:h * 256 + 128], PT[cur][:, h, :], PX[cur][:, h, 0:128],
                                                                      start=True, stop=True), [PT[cur], PX[cur]], [ps], signal=(h == 1))
                        self.pe(lambda h=h, pst_=pst_, cur=cur: T_.matmul(pst_[:, h * 128:(h + 1) * 128], PX[cur][:, h, 0:128], PT[cur][:, h, :],
                                                                          start=True, stop=True), [PT[cur], PX[cur]], [pst_], signal=(h == 1))
                    for h in range(2):
                        self.dve(lambda h=h, cur=cur, nxt=nxt: V.tensor_tensor(PX[nxt][:, h, 128:256], PX[cur][:, h, 0:128], self.ident[:, :], ALU.add),
                                 [PX[cur], self.ident], [PX[nxt]])
                    self.act(lambda ps=ps, nxt=nxt: A.copy(PX[nxt][:, :, 0:128], ps[:, :].rearrange("p (a t) -> p a t", a=2)[:, :, 0:128]),
                             [ps], [PX[nxt]])
                    self.act(lambda pst_=pst_, nxt=nxt: A.copy(f2(PT[nxt]), pst_[:, 0:256]), [pst_], [PT[nxt]])
                elif j < 5:
                    ps = self.nps()
                    pst_ = self.nps()
                    for h in range(2):
                        self.pe(lambda h=h, ps=ps, cur=cur: T_.matmul(ps[:, h * 256:(h + 1) * 256], PT[cur][:, h, :], PX[cur][:, h, :],
                                                                      start=True, stop=True), [PT[cur], PX[cur]], [ps], signal=(h == 1))
                        self.pe(lambda h=h, pst_=pst_, cur=cur: T_.matmul(pst_[:, h * 128:(h + 1) * 128], PX[cur][:, h, 0:128], PT[cur][:, h, :],
                                                                          start=True, stop=True), [PT[cur], PX[cur]], [pst_], signal=(h == 1))
                    ps3 = ps[:, :].rearrange("p (a t) -> p a t", a=2)
                    self.act(lambda ps3=ps3, ps=ps, nxt=nxt: A.copy(PX[nxt][:, :, 0:128], ps3[:, :, 0:128]), [ps], [PX[nxt]])
                    self.dve(lambda ps3=ps3, ps=ps, cur=cur, nxt=nxt: V.tensor_tensor(PX[nxt][:, :, 128:256], ps3[:, :, 128:256], PX[cur][:, :, 128:256], ALU.add),
                             [ps, PX[cur]], [PX[nxt]])
                    self.act(lambda pst_=pst_, nxt=nxt: A.copy(f2(PT[nxt]), pst_[:, 0:256]), [pst_], [PT[nxt]])
                else:
                    ps = self.nps()
                    for h in range(2):
                        self.pe(lambda h=h, ps=ps, cur=cur: T_.matmul(ps[:, h * 128:(h + 1) * 128], PT[cur][:, h, :], PX[cur][:, h, 128:256],
                                                                      start=True, stop=True), [PT[cur], PX[cur]], [ps], signal=(h == 1))
                    self.dve(lambda ps=ps, cur=cur: V.tensor_tensor(Xb[:, :, :], ps[:, 0:256].rearrange("p (a t) -> p a t", a=2), PX[cur][:, :, 128:256], ALU.add),
                             [ps, PX[cur]], [Xb])
                cur = nxt
                yield
            TT = None
            for lvl, CT in enumerate((C64T, C128T)):
                XT = fw.rot([128, 2, 128], BF16, "XTm" + sfx2, n=1)
                Zt = fw.rot([128, 2, 128], BF16, "Ztm" + sfx2, n=1)
                ptt = self.pst[self.pst_rr % 2]
                self.pst_rr += 1
                for h in range(2):
                    self.pe(lambda h=h, ptt=ptt, Xb=Xb: T_.transpose(ptt[:, h * 128:(h + 1) * 128], Xb[:, h, :], self.ident[:, :]),
                            [Xb, self.ident], [ptt], signal=(h == 1))
                self.act(lambda ptt=ptt, XT=XT: A.copy(f2(XT), ptt[:, 0:256]), [ptt], [XT])
                psz = self.nps()
                for h in range(2):
                    self.pe(lambda h=h, psz=psz, CT=CT, Xb=Xb: T_.matmul(psz[:, h * 128:(h + 1) * 128], CT[:, h, :], Xb[:, h, :], start=True, stop=True),
                            [CT, Xb], [psz], signal=(h == 1))
                self.dve(lambda psz=psz, Zt=Zt: V.tensor_copy(f2(Zt), psz[:, 0:256]), [psz], [Zt])
                psw = self.nps()
                for h in range(2):
                    self.pe(lambda h=h, psw=psw, XT=XT, Zt=Zt: T_.matmul(psw[:, h * 128:(h + 1) * 128], XT[:, h, :], Zt[:, h, :], start=True, stop=True),
                            [XT, Zt], [psw], signal=(h == 1))
                Xn = fw.rot([128, 2, 128], BF16, ("Xb64" if lvl == 0 else "TT") + sfx2, n=1)
                self.dve(lambda psw=psw, Xn=Xn, Xb=Xb: V.tensor_tensor(f2(Xn), psw[:, 0:256], f2(Xb), ALU.add), [psw, Xb], [Xn])
                Xb = Xn
                yield
            TT = Xb

            res["Am"] = Am
            res["TT"] = TT
            yield

        def rw_seq(dd, Pd, pres):
            KL, BL, KqRq2, gL, sg = Pd["KL"], Pd["BL"], Pd["KqRq2"], Pd["gL"], Pd["sg"]
            chunks = list(range(ncs)) if dd == 0 else list(reversed(range(ncs)))
            for c in chunks:
                cg = sg * ncs + c
                cs = slice(c * 128, (c + 1) * 128)
                Am, TT = pres[(dd, c)]["Am"], pres[(dd, c)]["TT"]
                Sb = STb[dd]
                psG = self.nps()
                for h in range(2):
                    hs = slice(64 * h, 64 * h + 64)
                    vs = slice(64 * h, 64 * h + 64)
                    self.pe(lambda h=h, vs=vs: T_.matmul(psG[:, vs], KqRq2[:, h, c, 0, :], Sb[:, :], start=(h == 0), stop=False, skip_group_check=True),
                            [KqRq2, Sb], [psG], signal=False)
                    self.pe(lambda h=h, vs=vs: T_.matmul(psG[:, vs], Am[h][:, 0:128], Vtm[:, cg, vs], start=False, stop=(h == 1), skip_group_check=True),
                            [Am[h], Vtm], [psG], signal=(h == 1))
                Gn = fw.rot([128, 128], BF16, "Gn" + str(dd), n=1)
                self.act(lambda: A.mul(Gn[:, :], psG[:, 0:128], -1.0), [psG], [Gn])
                yield
                psU = self.nps()
                for h in range(2):
                    vs = slice(64 * h, 64 * h + 64)
                    self.pe(lambda h=h, vs=vs: T_.matmul(psU[:, vs], TT[:, h, :], Gn[:, vs], start=(h == 0), stop=(h == 1), skip_group_check=True),
                            [TT, Gn], [psU], signal=(h == 1))
                U = fw.rot([128, 128], BF16, "U" + str(dd), n=1)
                self.dve(lambda: V.tensor_copy(U[:, :], psU[:, 0:128]), [psU], [U])
                yield
                yield
                psY = self.nps()
                for h in range(2):
                    hs = slice(64 * h, 64 * h + 64)
                    vs = slice(64 * h, 64 * h + 64)
                    self.pe(lambda h=h, vs=vs: T_.matmul(psY[:, vs], KqRq2[:, h, c, 1, :], Sb[:, :], start=(h == 0), stop=False, skip_group_check=True),
                            [KqRq2, Sb], [psY], signal=False)
                    self.pe(lambda h=h, vs=vs: T_.matmul(psY[:, vs], Am[h][:, 128:256], Vtm[:, cg, vs], start=False, stop=False, skip_group_check=True),
                            [Am[h], Vtm], [psY], signal=False)
                    self.pe(lambda h=h, vs=vs: T_.matmul(psY[:, vs], Am[h][:, 384:512], U[:, vs], start=False, stop=(h == 1), skip_group_check=True),
                            [Am[h], U], [psY], signal=(h == 1))
                self.dve(lambda: V.tensor_tensor(Yacc[:, cg, :], Yacc[:, cg, :], psY[:, 0:128], ALU.add), [Yacc, psY], [Yacc])
                yield
                yield
                KLt = fw.rot([128, 128], BF16, "KLt" + str(dd), n=1)
                BLt = fw.rot([128, 128], BF16, "BLt" + str(dd), n=1)
                self.transpose_to(KL, KL[:, cs], KLt, KLt[:, :], eng="act")
                self.transpose_to(BL, BL[:, cs], BLt, BLt[:, :], eng="dve")
                psS = self.nps()
                for h in range(2):
                    hs = slice(64 * h, 64 * h + 64)
                    vs = slice(64 * h, 64 * h + 64)
                    self.pe(lambda h=h, hs=hs, vs=vs: T_.matmul(psS[hs, 0:64], KLt[:, hs], Vtm[:, cg, vs], start=True, stop=False),
                            [KLt, Vtm], [psS], signal=False)
                    self.pe(lambda h=h, hs=hs, vs=vs: T_.matmul(psS[hs, 0:64], BLt[:, hs], U[:, vs], start=False, stop=True),
                            [BLt, U], [psS], signal=(h == 1))
                self.dve(lambda: V.scalar_tensor_tensor(ST[dd][:, :], ST[dd][:, :], gL[:, c:c + 1], psS[:, 0:64], ALU.mult, ALU.add),
                         [ST[dd], gL, psS], [ST[dd]])
                self.act(lambda: A.copy(STb[dd][:, :], ST[dd][:, :]), [ST[dd]], [STb[dd]])

        def run_rr(gens):
            gens = list(gens)
            while gens:
                for g in list(gens):
                    try:
                        next(g)
                    except StopIteration:
                        gens.remove(g)

        for step in range(nseg):
            sgs = (step, nseg - 1 - step)
            Pd = [{}, {}]
            segpar = J.get("segpar", False)
            if segpar:
                run_rr([rw_segment(dd, sgs[dd], Pd[dd], True) for dd in range(2)])
            else:
                for dd in range(2):
                    for _ in rw_segment(dd, sgs[dd], Pd[dd], False):
                        pass
            if STOP <= 3:
                continue
            pres = {(dd, c): {} for dd in range(2) for c in range(ncs)}
            run_rr([rw_pre(dd, c, Pd[dd], pres[(dd, c)]) for dd in range(2) for c in range(ncs)])
            if STOP <= 5:
                continue
            run_rr([rw_seq(dd, Pd[dd], pres) for dd in range(2)])
        if J["stout"] is not None:
            for dd in range(2):
                J["stout"](dd, ST[dd])
        jpop()
        if STOP <= 6:
            jpop(); return
        n2 = nch * 2
        sums = jt([128, n2], F32, "gsum")
        ssq = jt([128, n2], F32, "gssq")
        Ysq = jt([128, nch * 128], F32, "Ysq") if scoped else fw.rot([128, KW], F32, "kk", n=1)
        Yf = Yacc[:, :, :].rearrange("p c x -> p (c x)")
        self.dve(lambda: V.tensor_reduce(sums[:, :], Yf.rearrange("p (g x) -> p g x", x=64), AX.X, ALU.add), [Yacc], [sums])
        self.act(lambda: A.activation(Ysq[:, :], Yf, AF.Square), [Yacc], [Ysq])
        self.dve(lambda: V.tensor_reduce(ssq[:, :], Ysq[:, :].rearrange("p (g x) -> p g x", x=64), AX.X, ALU.add), [Ysq], [ssq])
        mean = jt([128, n2], F32, "gmean")
        var = jt([128, n2], F32, "gvar2")
        self.dve(lambda: V.tensor_scalar(mean[:, :], sums[:, :], 1.0 / 64, None, ALU.mult), [sums], [mean])
        self.dve(lambda: V.tensor_tensor(var[:, :], mean[:, :], mean[:, :], ALU.mult), [mean], [var])
        self.dve(lambda: V.scalar_tensor_tensor(var[:, :], ssq[:, :], 1.0 / 64, var[:, :], ALU.mult, ALU.subtract), [ssq, var], [var])
        self.dve(lambda: V.tensor_scalar(var[:, :], var[:, :], GN_EPS, None, ALU.add), [var], [var])
        self.rsqrt(var, var[:, :], var, var[:, :])
        yn = jt([128, nch, 128], BF16, "yn")
        for c in range(nch):
            for h in range(2):
                g = c * 2 + h
                self.dve(lambda c=c, h=h, g=g: V.tensor_scalar(yn[:, c, h * 64:(h + 1) * 64], Yacc[:, c, h * 64:(h + 1) * 64], mean[:, g:g + 1], var[:, g:g + 1],
                                                               ALU.subtract, ALU.mult), [Yacc, mean, var], [yn])
        for c in range(nch):
            pt = self.pst[self.pst_rr % 2]
            self.pst_rr += 1
            self.pe(lambda c=c, pt=pt: T_.transpose(pt[:, :128], yn[:, c, :], self.ident[:, :]), [yn, self.ident], [pt])
            yT = fw.rot([128, 128], F32, "yT", n=(2 if scoped else 1))
            self.act(lambda pt=pt, yT=yT: A.activation(yT[:, :], pt[:, :128], AF.Identity, bias=par("ln_b"), scale=par("ln_g")), [pt] + pbufs, [yT])
            bo = fw.rot([128, 128], F32, "bo", n=(2 if scoped else 1))
            self.dve(lambda c=c, bo=bo: V.tensor_tensor(bo[:, :], Bacc[:, c * 128:(c + 1) * 128], vf(c * 128, (c + 1) * 128), ALU.mult), [Bacc, vB], [bo])
            yfin = fw.rot([128, 128], F32, "yfin", n=(2 if scoped else 1))
            self.dve(lambda yT=yT, bo=bo, yfin=yfin: V.tensor_tensor(yfin[:, :], yT[:, :], bo[:, :], ALU.add), [yT, bo], [yfin])
            J["yout"](c, yfin)
        jpop()
        self.ps_range = (0, 4)

    def load_mla_w(self, l):
        fw, d = self.fw, self.d
        self.wq = fw.sb([128, 2 * 8 * 96], BF16, "wq")
        self.wqs = fw.sb([128, 2 * 8 * 32], BF16, "wqs")
        self.wkk = fw.sb([128, 512], BF16, "wkk")
        self.wkv = fw.sb([128, 512], BF16, "wkv")
        for nm in ("wq", "wqs", "wkk", "wkv"):
            t = getattr(self, nm)
            self.load(t, t[:], d[nm], d[nm][l, :, :])

    def mla_front(self, l, h0, rope, GB2, Qh, ckv_f, ckv_b, kr_f, kr_b):
        fw = self.fw
        base = l * BLK_PER_LAYER
        W = WIN_W["B0"]
        slot = self.next_block(base + WIN_IDX["B0"])
        qd = fw.sb([128, 2, 512], F32, "qd")
        kvd = fw.sb([128, 512], F32, "kvd")
        for c in range(2):
            ps = self.zmm(slot, W, c * 128, 128, h0, 512)
            self.act(lambda c=c, ps=ps: self.A.copy(qd[:, c, :], ps[:, :]), [ps], [qd])
        ps = self.zmm(slot, W, 256, 128, h0, 512)
        self.dve(lambda ps=ps: self.V.tensor_copy(kvd[:, :], ps[:, :]), [ps], [kvd])
        pk = self.zmm(slot, W, 384, 32, h0, 512, prow=64)
        R = self.rope
        if rope:
            pks = self.zmm(slot, W, 416, 32, h0, 512, prow=64)
            t1 = fw.sb([96, 512], F32, "krt1")
            self.dve(lambda: self.V.tensor_tensor(t1[64:96, :], pk[64:96, :], R[64:96, 0:512], ALU.mult), [pk, R], [t1])
            self.dve(lambda: self.V.tensor_tensor(kr_f[64:96, :], pks[64:96, :], R[64:96, 512:1024], ALU.mult), [pks, R], [kr_f])
            self.dve(lambda: self.V.tensor_tensor(kr_f[64:96, :], kr_f[64:96, :], t1[64:96, :], ALU.add), [kr_f, t1], [kr_f])
        else:
            self.act(lambda: self.A.copy(kr_f[64:96, :], pk[64:96, :]), [pk], [kr_f])
        self.act(lambda: self.A.copy(kr_b[64:96, :], kr_f[64:96, :]), [kr_f], [kr_b])
        self.prefetch_next()
        slot = self.next_block(base + WIN_IDX["B1"])
        for c in range(4):
            ps = self.zmm(slot, 512, c * 128, 128, h0, 512)
            self.silu2(ps, 128, 512, GB2[:, c, :], GB2)
        self.prefetch_next()
        rq = fw.sb([128, 512], F32, "rq")
        self.rstd_tile(qd, [qd[:, c, :] for c in range(2)], 256.0, rq, 512)
        qn = fw.sb([128, 2, 512], BF16, "qn")
        for c in range(2):
            self.dve(lambda c=c: self.V.scalar_tensor_tensor(qn[:, c, :], qd[:, c, :], self.spc("qn", c), rq[:, :], ALU.mult, ALU.mult),
                     [qd, self.sp, rq], [qn])
        rk = fw.sb([128, 512], F32, "rkv")
        self.rstd_tile(kvd, [kvd[:, :]], 128.0, rk, 512)
        self.dve(lambda: self.V.scalar_tensor_tensor(ckv_f[:, :], kvd[:, :], self.spc("kvn", 0), rk[:, :], ALU.mult, ALU.mult),
                 [kvd, self.sp, rk], [ckv_f])
        self.act(lambda: self.A.copy(ckv_b[:, :], ckv_f[:, :]), [ckv_f], [ckv_b])
        for h in range(8):
            ps = self.nps()
            for c in range(2):
                self.pe(lambda c=c, h=h, ps=ps: self.T.matmul(ps[:96, :], self.wq[:, (c * 8 + h) * 96:(c * 8 + h) * 96 + 96], qn[:, c, :],
                                                             start=(c == 0), stop=(c == 1)), [self.wq, qn], [ps], signal=(c == 1))
            if rope:
                ps2 = self.nps()
                for c in range(2):
                    self.pe(lambda c=c, h=h, ps2=ps2: self.T.matmul(ps2[64:96, :], self.wqs[:, (c * 8 + h) * 32:(c * 8 + h) * 32 + 32], qn[:, c, :],
                                                                   start=(c == 0), stop=(c == 1)), [self.wqs, qn], [ps2], signal=(c == 1))
                t1 = fw.rot([96, 512], F32, "qrt1")
                t2 = fw.rot([96, 512], F32, "qrt2")
                self.dve(lambda ps=ps, t1=t1: self.V.tensor_tensor(t1[64:96, :], ps[64:96, :], R[64:96, 0:512], ALU.mult), [ps, R], [t1])
                self.dve(lambda ps2=ps2, t2=t2: self.V.tensor_tensor(t2[64:96, :], ps2[64:96, :], R[64:96, 512:1024], ALU.mult), [ps2, R], [t2])
                self.dve(lambda h=h, t1=t1, t2=t2: self.V.tensor_tensor(Qh[h][64:96, :], t1[64:96, :], t2[64:96, :], ALU.add), [t1, t2], [Qh[h]])
                self.act(lambda h=h, ps=ps: self.A.copy(Qh[h][0:64, :], ps[0:64, :]), [ps], [Qh[h]])
            else:
                self.act(lambda h=h, ps=ps: self.A.copy(Qh[h][:, :], ps[:96, :]), [ps], [Qh[h]])

    def mla_kv_chunk(self, ckv_b, kr_b, heads, Kh, Vaug, nk):
        for i, h in enumerate(heads):
            ps = self.nps()
            self.pe(lambda h=h, ps=ps: self.T.matmul(ps[:64, :nk], self.wkk[:, h * 64:(h + 1) * 64], ckv_b[:, :nk], start=True, stop=True),
                    [self.wkk, ckv_b], [ps])
            self.act(lambda i=i, ps=ps: self.A.copy(Kh[i][0:64, :nk], ps[0:64, :nk]), [ps], [Kh[i]])
            self.dve(lambda i=i: self.V.tensor_copy(Kh[i][64:96, :nk], kr_b[64:96, :nk]), [kr_b], [Kh[i]])
        for kb in range(nk // 128):
            ps = self.nps()
            self.pe(lambda kb=kb, ps=ps: self.T.matmul(ps[:, :], ckv_b[:, kb * 128:(kb + 1) * 128], self.wkv[:, :], start=True, stop=True),
                    [ckv_b, self.wkv], [ps])
            for h8 in range(8):
                pass
            self.dve(lambda kb=kb, ps=ps: self.V.tensor_copy(
                Vaug[:, kb * 520:(kb + 1) * 520].rearrange("p (h e) -> p h e", e=65)[:, :, 0:64],
                ps[:, :].rearrange("p (h e) -> p h e", e=64)), [ps], [Vaug])

    def attn_accum(self, Qh4, q0, nq, Kh4, Vaug, heads, k0, nkb, first, last, Oacc):
        fw = self.fw
        nqs = nq // 128
        its = [(kb, i, h) for kb in range(nkb) for i, h in enumerate(heads)]

        def score(kb, i, h):
            pss = self.psb[4 + (self.sc_rr % 2)]
            self.sc_rr += 1
            self.pe(lambda: self.T.matmul(pss[:, :nq], Kh4[i][:, k0 + kb * 128:k0 + kb * 128 + 128],
                                          Qh4[i][:, q0:q0 + nq], start=True, stop=True), [Kh4[i], Qh4[i]], [pss])
            PT = fw.rot([128, 512], BF16, "PT", n=3)
            self.act(lambda: self.A.activation(PT[:, :nq], pss[:, :nq], AF.Exp, scale=96.0 ** -0.5), [pss], [PT])
            return PT

        def pv(kb, i, h, PT):
            for qs in range(nqs):
                self.pe(lambda qs=qs: self.T.matmul(
                    Oacc[qs][:, i * 65:(i + 1) * 65], PT[:, qs * 128:(qs + 1) * 128],
                    Vaug[:, (k0 // 128 + kb) * 520 + h * 65:(k0 // 128 + kb) * 520 + h * 65 + 65],
                    start=(first and kb == 0 and i == 0), stop=(last and kb == nkb - 1 and i == 3), skip_group_check=True),
                    [PT, Vaug], [Oacc[qs]], signal=(qs == nqs - 1))

        pend = score(*its[0])
        for n in range(len(its)):
            nxt = score(*its[n + 1]) if n + 1 < len(its) else None
            pv(*its[n], pend)
            pend = nxt

    def attn_finish(self, Oacc, nqs, ob, hh):
        fw = self.fw
        for qs in range(nqs):
            rec = fw.rot([128, 4], F32, "rec", n=4)
            self.dve(lambda qs=qs, rec=rec: self.V.reciprocal(rec[:, :], Oacc[qs][:, 64:260:65]), [Oacc[qs]], [rec])
            for i in range(4):
                self.dve(lambda qs=qs, i=i, rec=rec: self.V.tensor_scalar(ob[qs][:, hh * 256 + i * 64:hh * 256 + i * 64 + 64],
                                                                          Oacc[qs][:, i * 65:i * 65 + 64], rec[:, i:i + 1], None, ALU.mult),
                         [Oacc[qs], rec], [ob[qs]])

    def attn_out(self, ob, nqs, GB2, g0, o0):
        for qs in range(nqs):
            for c in range(4):
                pt = self.pst[self.pst_rr % 2]
                self.pst_rr += 1
                self.pe(lambda qs=qs, c=c, pt=pt: self.T.transpose(pt[:, :128], ob[qs][:, c * 128:(c + 1) * 128], self.ident[:, :]),
                        [ob[qs], self.ident], [pt])
                self.dve(lambda qs=qs, c=c, pt=pt: self.V.scalar_tensor_tensor(
                    self.oTs[o0 // 512][1][:, c, o0 % 512 + qs * 128:o0 % 512 + qs * 128 + 128], pt[:, :128], 0.5, GB2[:, c, g0 + qs * 128:g0 + qs * 128 + 128],
                    ALU.mult, ALU.mult), [pt, GB2], [self.oTs[o0 // 512][1]])

    def phaseB_prompt(self, l, half):
        fw = self.fw
        h0 = half * 512
        fw.push()
        self.load_mla_w(l)
        GB2 = fw.sb([128, 4, 512], BF16, "GB2")
        Qh = [fw.sb([96, 512], BF16, f"Qh{h}") for h in range(8)]
        ckv_f = fw.sb([128, 512], F32, "ckvf")
        ckv_b = fw.sb([128, 512], BF16, "ckvb")
        kr_f = fw.sb([96, 512], F32, "krf")
        kr_b = fw.sb([96, 512], BF16, "krb")
        fw.push()
        self.stream_begin(2)
        self.mla_front(l, h0, False, GB2, Qh, ckv_f, ckv_b, kr_f, kr_b)
        fw.pop()
        fw.dma("sp", self.d["ckvout"], self.d["ckvout"][l, :, h0:h0 + 512], ckv_f, ckv_f[:, :], sem_owner=self.outsem)
        fw.dma("sp", self.d["krout"], self.d["krout"][l, :, h0:h0 + 512], kr_f, kr_f[64:96, :], sem_owner=self.outsem)
        Kh = [fw.sb([96, 512], BF16, f"Kh{h}") for h in range(8)]
        Vaug = fw.sb([128, 4 * 520], BF16, "Vaug")
        self.pool(lambda: self.G.memset(Vaug[:, :], 1.0), [], [Vaug])
        self.mla_kv_chunk(ckv_b, kr_b, list(range(8)), Kh, Vaug, 512)
        self.sc_rr = 0
        import os
        if "dumpB" in os.environ.get("KDBG", "") and l == 0 and half == 0:
            so = self.d["stout"]
            fw.dma("pool", so, so[0:768, :].rearrange("(p a) b -> p (a b)", p=96), Qh[0], Qh[0][:, :])
            fw.dma("pool", so, so[768:1536, :].rearrange("(p a) b -> p (a b)", p=96), Kh[0], Kh[0][:, :])
            fw.dma("pool", so, so[1536:5632, :].rearrange("(p a) b -> p (a b)", p=128), Vaug, Vaug[:, 0:2048])
        for sq in range(2):
            ob = [fw.rot([128, 512], BF16, "ob", n=4) for _ in range(2)]
            for hh in range(2):
                heads = list(range(hh * 4, hh * 4 + 4))
                Oacc = [self.psb[0 + 2 * (hh % 2)], self.psb[1 + 2 * (hh % 2)]]
                self.attn_accum([Qh[h] for h in heads], sq * 256, 256, [Kh[h] for h in heads], Vaug, heads, sq * 256, 2, True, True, Oacc)
                self.attn_finish(Oacc, 2, ob, hh)
            if "dumpB" in os.environ.get("KDBG", "") and l == 0 and half == 0 and sq == 0:
                so = self.d["stout"]
                fw.dma("pool", so, so[5696:6720, :].rearrange("(p a) b -> p (a b)", p=128), ob[0], ob[0][:, :])
            self.attn_out(ob, 2, GB2, sq * 256, h0 + sq * 256)
        fw.pop()

    def phaseB_contrib(self, l):
        fw = self.fw
        self.load_mla_w(l)
        self.GB2 = fw.sb([128, 4, 512], BF16, "GB2s")
        self.Qh = [fw.sb([96, 512], BF16, f"Qhs{h}") for h in range(8)]
        fw.push()
        self.rope = fw.sb([96, 1024], F32, "rope")
        self.load(self.rope, self.rope[64:96, :], self.d["rope"], self.d["rope"].ap())
        self.stream_begin(2)
        ckv_f = fw.sb([128, 512], F32, "ckvf")
        ckv_b = fw.sb([128, 512], BF16, "ckvb")
        kr_f = fw.sb([96, 512], F32, "krf")
        kr_b = fw.sb([96, 512], BF16, "krb")
        self.mla_front(l, 0, True, self.GB2, self.Qh, ckv_f, ckv_b, kr_f, kr_b)
        ag = self.ag1_in[l]["vx"]
        fw.dma("sp", ag, ag[VX_OFF["ckv"]:VX_OFF["ckv"] + 128, :], ckv_b, ckv_b[:, :])
        fw.dma("sp", ag, ag[VX_OFF["kr"]:VX_OFF["kr"] + 32, :], kr_b, kr_b[64:96, :])
        fw.pop()

    def phaseB_consume(self, l):
        fw = self.fw
        ago = self.ag1_out[l]["vx"]
        fw.push()
        self.sc_rr = 0
        self.ps_range = (4, 6)
        ob = [fw.sb([128, 512], BF16, f"obs{i}") for i in range(4)]
        for hh in range(2):
            heads = list(range(hh * 4, hh * 4 + 4))
            Oacc = self.psb[0:4]
            for ch in range(5):
                ckv_b = fw.rot([128, 512], BF16, "ckvg", n=2)
                kr_b = fw.rot([96, 512], BF16, "krg", n=2)
                if ch < 4:
                    fw.dma("sp", ckv_b, ckv_b[:, :], ago, ago[ch * 864 + VX_OFF["ckv"]:ch * 864 + VX_OFF["ckv"] + 128, :])
                    fw.dma("sp", kr_b, kr_b[64:96, :], ago, ago[ch * 864 + VX_OFF["kr"]:ch * 864 + VX_OFF["kr"] + 32, :])
                else:
                    fw.dma("pool", ckv_b, ckv_b[:, :], self.d["cckv"], self.d["cckv"][l, :, :])
                    fw.dma("pool", kr_b, kr_b[64:96, :], self.d["ckr"], self.d["ckr"][l, :, :])
                Kh = [fw.rot([96, 512], BF16, f"Khs{i}", n=2) for i in range(4)]
                Vaug = fw.rot([128, 4 * 520], BF16, "Vaugs", n=2)
                self.pool(lambda Vaug=Vaug: self.G.memset(Vaug[:, :], 1.0), [], [Vaug])
                self.mla_kv_chunk(ckv_b, kr_b, heads, Kh, Vaug, 512)
                self.attn_accum([self.Qh[h] for h in heads], 0, 512, Kh, Vaug, heads, 0, 4, ch == 0, ch == 4, Oacc)
            self.attn_finish(Oacc, 4, ob, hh)
        self.ps_range = (0, 4)
        self.attn_out(ob, 4, self.GB2, 0, 0)
        fw.pop()

    def fnet_stage1(self, fT_buf, fT_ap_fn, dftd, G1):
        for hb in range(2):
            ps = self.psb[4 + hb]
            for gg in range(2):
                g = hb * 2 + gg
                self.pe(lambda g=g, gg=gg, ps=ps: self.T.matmul(ps[:, gg * 256:(gg + 1) * 256], fT_ap_fn(g), dftd[:, :],
                                                                 start=True, stop=True), [fT_buf, dftd], [ps], signal=(gg == 1))
            if hb == 0:
                self.act(lambda ps=ps: self.A.copy(G1[:, 0:512], ps[:, :]), [ps], [G1])
            else:
                self.dve(lambda ps=ps: self.V.tensor_copy(G1[:, 512:1024], ps[:, :]), [ps], [G1])

    def phaseD_prompt(self, l, half):
        fw = self.fw
        base = l * BLK_PER_LAYER
        h0 = half * 512
        fw.push()
        self.dftd_p = fw.sb([128, 256], BF16, "dftd_p")
        self.dftT_p = fw.sb([128, 1024], BF16, "dftT_p")
        for nm in ("dftd_p", "dftT_p"):
            t = getattr(self, nm)
            self.load(t, t[:], self.d[nm], self.d[nm].ap())
        fT = fw.sb([128, 4, 512], BF16, "fT")
        GD2 = fw.sb([128, 4, 512], BF16, "GD2")
        fw.push()
        self.stream_begin(2)
        slot = self.next_block(base + WIN_IDX["D0"])
        for c in range(4):
            ps = self.zmm(slot, 512, c * 128, 128, h0, 512)
            if c % 2:
                self.act(lambda c=c, ps=ps: self.A.copy(fT[:, c, :], ps[:, :]), [ps], [fT])
            else:
                self.dve(lambda c=c, ps=ps: self.V.tensor_copy(fT[:, c, :], ps[:, :]), [ps], [fT])
        self.prefetch_next()
        slot = self.next_block(base + WIN_IDX["D1"])
        for c in range(4):
            ps = self.zmm(slot, 512, c * 128, 128, h0, 512)
            self.silu2(ps, 128, 512, GD2[:, c, :], GD2)
        self.prefetch_next()
        fw.pop()
        for sq in range(2):
            t0 = sq * 256
            G1 = [fw.rot([128, 1024], BF16, "G1", n=4) for _ in range(2)]
            for tt in range(2):
                self.fnet_stage1(fT, lambda g, tt=tt: fT[:, g, t0 + tt * 128:t0 + tt * 128 + 128], self.dftd_p, G1[tt])
            for g in range(4):
                ps = self.nps()
                i = 0
                for tt in range(2):
                    for cs in range(2):
                        self.pe(lambda g=g, tt=tt, cs=cs, ps=ps, i=i: self.T.matmul(
                            ps[:, :256], G1[tt][:, g * 256 + cs * 128:g * 256 + cs * 128 + 128],
                            self.dftT_p[:, (tt * 2 + cs) * 256:(tt * 2 + cs) * 256 + 256], start=(i == 0), stop=(i == 3)),
                            [G1[tt], self.dftT_p], [ps], signal=(i == 3))
                        i += 1
                self.dve(lambda g=g, ps=ps: self.V.scalar_tensor_tensor(
                    self.oTs[half][3][:, g, t0:t0 + 256], ps[:, :256], 0.5, GD2[:, g, t0:t0 + 256], ALU.mult, ALU.mult),
                    [ps, GD2], [self.oTs[half][3]])
        fw.pop()

    def contrib_rows(self, l, slot, W, c0, w, part, row0):
        fw = self.fw
        ps = self.zmm(slot, W, c0, w, 0, 512)
        stg = fw.rot([128, 512], BF16, "agstg", n=3)
        if self.cflip % 2:
            self.act(lambda: self.A.copy(stg[:w, :], ps[:w, :]), [ps], [stg])
        else:
            self.dve(lambda: self.V.tensor_copy(stg[:w, :], ps[:w, :]), [ps], [stg])
        self.cflip += 1
        ag = self.ag1_in[l][part]
        fw.dma("sp", ag, ag[row0:row0 + w, :], stg, stg[:w, :])

    def phaseD_contrib(self, l):
        fw = self.fw
        base = l * BLK_PER_LAYER
        self.GD2 = fw.sb([128, 4, 512], BF16, "GD2s")
        fw.push()
        self.stream_begin(2)
        slot = self.next_block(base + WIN_IDX["D0"])
        for c in range(4):
            self.contrib_rows(l, slot, 512, c * 128, 128, "f", c * 128)
        self.prefetch_next()
        slot = self.next_block(base + WIN_IDX["D1"])
        for c in range(4):
            ps = self.zmm(slot, 512, c * 128, 128, 0, 512)
            self.silu2(ps, 128, 512, self.GD2[:, c, :], self.GD2)
        self.prefetch_next()
        fw.pop()

    def phaseD_consume(self, l):
        fw = self.fw
        ago = self.ag1_out[l]["f"]
        fw.push()
        self.dftd_s = fw.sb([128, 256], BF16, "dftd_s")
        self.load(self.dftd_s, self.dftd_s[:], self.d["dftd_s"], self.d["dftd_s"].ap())
        acc = self.psb[0:4]
        nt = 0
        for q in range(4):
            fq = fw.rot([128, 4, 512], BF16, "fq", n=2)
            src = ago[q * 512:q * 512 + 512, :].rearrange("(g d) t -> d g t", d=128)
            fw.dma("sp", fq, fq[:], ago, src)
            for s4 in range(4):
                tt = q * 4 + s4
                ct = fw.rot([128, 1024], BF16, "ct", n=3)
                fw.dma("pool", ct, ct[:], self.d["dftT_s"], self.d["dftT_s"][tt, :, :])
                G1 = fw.rot([128, 1024], BF16, "G1s", n=3)
                self.fnet_stage1(fq, lambda g, s4=s4, fq=fq: fq[:, g, s4 * 128:(s4 + 1) * 128], self.dftd_s, G1)
                for g in range(4):
                    for cs in range(2):
                        self.pe(lambda g=g, cs=cs, G1=G1, ct=ct, tt=tt: self.T.matmul(
                            acc[g][:, :], G1[:, g * 256 + cs * 128:g * 256 + cs * 128 + 128], ct[:, cs * 512:(cs + 1) * 512],
                            start=(tt == 0 and cs == 0), stop=(tt == 15 and cs == 1)),
                            [G1, ct], [acc[g]], signal=(g == 3 and cs == 1))
        for g in range(4):
            self.dve(lambda g=g: self.V.scalar_tensor_tensor(self.oTs[0][3][:, g, 0:512], acc[g][:, :], 0.5, self.GD2[:, g, :],
                                                             ALU.mult, ALU.mult), [acc[g], self.GD2], [self.oTs[0][3]])
        fw.pop()

    def allgather(self, dst, src):
        fw = self.fw
        fw.dma("pool", dst, None, src, None, sem_owner=dst, inc=1,
               fn=lambda: self.G.collective_compute("AllGather", ALU.bypass, replica_groups=[[0, 1, 2, 3], [4, 5, 6, 7]],
                                                     ins=[src.h.ap()], outs=[dst.h.ap()]))

    def sample_pass(self, l):
        import os
        fw = self.fw
        dbg = os.environ.get("KDBG", "")
        self.cflip = 0
        br = self.branches
        fw.push()
        if "A" in br:
            self.phaseA_contrib(l)
        fw.push()
        if "B" in br:
            self.phaseB_contrib(l)
        fw.push()
        if "D" in br:
            self.phaseD_contrib(l)
        if "noag" not in dbg:
            if "A" in br:
                self.allgather(self.ag1_out[l]["rk"], self.ag1_in[l]["rk"])
            if "A" in br or "B" in br:
                self.allgather(self.ag1_out[l]["vx"], self.ag1_in[l]["vx"])
            if "D" in br:
                self.allgather(self.ag1_out[l]["f"], self.ag1_in[l]["f"])
        if "C" in br:
            self.phaseC(l, 0, 512, 0)
        if "D" in br and "nocons" not in dbg:
            self.phaseD_consume(l)
        fw.pop()
        if "B" in br:
            self.phaseB_consume(l)
        fw.pop()
        if "A" in br and "noAcons" not in dbg:
            self.phaseA_consume(l)
            self.phaseA_final(l)
        fw.pop()

    def zero_branch(self, n):
        for hf in range(2):
            if self.oTs[hf] is not None:
                t = self.oTs[hf][n]
                self.pool(lambda t=t: self.G.memset(t[:], 0.0), [], [t])

    def win_plan(self, l, grp="p"):
        base = l * BLK_PER_LAYER
        ids = []
        order = (("A", ["A0", "A1", "A2", "A3", "A4"]), ("B", ["B0", "B1"]), ("C", ["C0", "C1", "C2"]), ("D", ["D0", "D1"]))
        if grp == "s":
            order = (order[0], order[1], order[3], order[2])
        for br, names in order:
            if br in self.branches:
                ids += [base + WIN_IDX[n] for n in names]
        return ids

    def tail_plan(self, l):
        base = l * BLK_PER_LAYER
        return [base + 18 + i for i in range(16)] + [base + 34, base + 35]

    def build(self):
        fw = self.fw
        self.outsem = Buf(None, "outsem", "none")
        for l in range(self.depth):
            self.stream_plan([l * BLK_PER_LAYER + b for b in range(6)])
            self.stream_plan(self.win_plan(l) * 2 + self.tail_plan(l))
            self.stream_plan(self.win_plan(l, 's') + self.tail_plan(l))
        for l in range(self.depth):
            fw.push()
            self.load_layer_small(l)
            self.ada(l)
            fw.push()
            self.hTs[1] = fw.sb([128, 8, 512], BF16, "hTb")
            self.oTs[1] = [fw.sb([128, 4, 512], BF16, f"oTb{n}") for n in range(4)]
            for n, br in enumerate("ABCD"):
                if br not in self.branches:
                    self.zero_branch(n)
            for half in range(2):
                self.make_h(0, half * 512, half * 512, 512)
            for half in range(2):
                self.phases(l, "p", half)
            self.merge_out(l, 0, 0, NPT)
            fw.pop()
            self.hTs[1] = None
            self.oTs[1] = None
            self.make_h(1, NPT, 0, 512)
            self.sample_pass(l)
            self.merge_out(l, 1, NPT, NST)
            fw.pop()
        fw.push()
        self.sp = fw.sb([128, NSP], F32, "spf")
        self.load(self.sp, self.sp[:], self.d["sp"], self.d["sp"][0, :, :])
        self.final_out()
        fw.pop()
        for e in ("sp",):
            for ev in list(fw.dma_out.values()):
                fw._wait(e, ev)
        fw.barrier()
        return self.nc

    def phases(self, l, grp, half):
        h0 = half * 512
        if "A" in self.branches:
            self.phaseA_prompt(l, half)
        if "B" in self.branches:
            self.phaseB_prompt(l, half)
        if "C" in self.branches:
            self.phaseC(l, h0, 512, h0)
        if "D" in self.branches:
            self.phaseD_prompt(l, half)


_CFG = {"branches": "ABCD", "depth": DEPTH}


def make_in_maps(inp):
    f32 = lambda a: np.ascontiguousarray(np.asarray(a, dtype=np.float32))
    inp = {k: np.asarray(v) for k, v in inp.items()}
    wst = build_stream(f32(inp["w_ada"]), f32(inp["w_in"]), f32(inp["w_branch"]), f32(inp["w_merge"]), f32(inp["w_out"]))
    sp = build_small(inp)
    cst = build_consts()
    L = DEPTH
    wup = f32(inp["rwkv_w_up"]).transpose(0, 2, 1, 3).reshape(L, 64, 1024)
    aup = f32(inp["rwkv_a_up"]).transpose(0, 2, 1, 3).reshape(L, 64, 1024)
    wqu = f32(inp["mla_w_q_up"]).reshape(L, 2, 128, 8, 96)
    wq = wqu.transpose(0, 2, 1, 3, 4).reshape(L, 128, 2 * 8 * 96)
    wqs = wqu[..., 64 + _SWAP32].transpose(0, 2, 1, 3, 4).reshape(L, 128, 2 * 8 * 32)
    wkvu = f32(inp["mla_w_kv_up"]).reshape(L, 128, 8, 128)
    wkk = np.ascontiguousarray(wkvu[..., :64]).reshape(L, 128, 512)
    wkv = np.ascontiguousarray(wkvu[..., 64:]).reshape(L, 128, 512)
    wsT = f32(inp["gmlp_w_s"]).transpose(0, 3, 1, 2).reshape(L, 128, 512)
    bsb = np.ascontiguousarray(np.broadcast_to(f32(inp["gmlp_b_s"]).reshape(L, 1, 512), (L, 128, 512)))
    maps = []
    xp = f32(inp["x_prompt"])
    xs = f32(inp["x_sample"])
    for c in range(NCORE):
        s, j = c // 4, c % 4
        dftT, rope = build_core_consts(j)
        cond = np.stack([f32(inp["c_ctx"]).reshape(8, 128).T, f32(inp["c"])[s].reshape(8, 128).T], axis=2).reshape(128, 16)
        o, _ = SP_OFF["rw"]
        om, _ = SP_OFF["mu_rkv"]
        spo = np.concatenate([sp[:, :, o:o + 36].reshape(L, 128, 9, 4)[:, :, :, j],
                              sp[:, :, om:om + 12].reshape(L, 128, 3, 4)[:, :, :, j]], axis=2)
        spo = np.ascontiguousarray(spo)
        st0 = np.stack([f32(inp["state_rwkv_fwd"])[s, :, 2 * j:2 * j + 2], f32(inp["state_rwkv_bwd"])[s, :, 2 * j:2 * j + 2]],
                       axis=1)
        st0 = st0.transpose(0, 1, 2, 4, 3).reshape(L, 2, 128, 64)
        p = np.arange(128)
        idx1 = np.zeros((128, 12), np.int32)
        for q in range(4):
            idx1[:, 0 * 4 + q] = q * 1024 + 128 * j + p
            idx1[:, 1 * 4 + q] = q * 1024 + 512 + 128 * j + p
            idx1[:, 2 * 4 + q] = q * 864 + 128 * j + p
        idx2 = np.zeros((128, 4), np.int32)
        for r in range(4):
            idx2[:, r] = (r * 4 + j) * 128 + p
        m = dict(
            xp=np.ascontiguousarray(xp[4 * c:4 * c + 4].reshape(NPT, D).T),
            xs=np.ascontiguousarray(xs[s, 512 * j:512 * j + 512].T),
            wst=wst, sp=sp, cond=np.ascontiguousarray(cond),
            ident=cst["ident"], bones=cst["bones"], mask=cst["mask"],
            dftd_p=cst["dftd_p"], dftd_s=cst["dftd_s"], dftT_p=cst["dftT_p"],
            dftT_s=np.ascontiguousarray(dftT.reshape(16, 128, 1024)), rope=np.ascontiguousarray(rope.reshape(32, 1024)),
            wup=wup, aup=aup,
            wupo=np.ascontiguousarray(wup.reshape(L, 64, 2, 4, 128)[:, :, :, j]).reshape(L, 64, 256),
            aupo=np.ascontiguousarray(aup.reshape(L, 64, 2, 4, 128)[:, :, :, j]).reshape(L, 64, 256),
            spo=spo, wq=wq, wqs=wqs, wkk=wkk, wkv=wkv, wsT=wsT, bsb=bsb,
            st0=np.ascontiguousarray(st0),
            cckv=np.ascontiguousarray(f32(inp["cache_mla_ckv"])[s].transpose(0, 2, 1)),
            ckr=np.ascontiguousarray(f32(inp["cache_mla_krope"])[s].transpose(0, 2, 1)),
            idx1=idx1, idx2=idx2,
        )
        maps.append(m)
    return maps


def assemble(results):
    B = 32
    yp = np.zeros((B, SEQ, D), np.float32)
    ys = np.zeros((2, DSEQ, D), np.float32)
    sf = np.zeros((B, DEPTH, 8, 64, 64), np.float32)
    sbw = np.zeros((B, DEPTH, 8, 64, 64), np.float32)
    ckv = np.zeros((B, DEPTH, SEQ, 128), np.float32)
    kr = np.zeros((B, DEPTH, SEQ, 32), np.float32)
    for c in range(NCORE):
        r = results[c]
        s, j = c // 4, c % 4
        yp[4 * c:4 * c + 4] = np.asarray(r["yp"]).T.reshape(4, SEQ, D)
        ys[s, 512 * j:512 * j + 512] = np.asarray(r["ys"]).T
        st = np.asarray(r["stout"]).reshape(DEPTH, 2, 4, 4, 2, 64, 64)
        st = st.transpose(1, 2, 0, 3, 4, 6, 5).reshape(2, 4, DEPTH, 8, 64, 64)
        sf[4 * c:4 * c + 4] = st[0]
        sbw[4 * c:4 * c + 4] = st[1]
        ck = np.asarray(r["ckvout"]).reshape(DEPTH, 128, 4, SEQ)
        ckv[4 * c:4 * c + 4] = ck.transpose(2, 0, 3, 1)
        k2 = np.asarray(r["krout"]).reshape(DEPTH, 32, 4, SEQ)
        kr[4 * c:4 * c + 4] = k2.transpose(2, 0, 3, 1)
    return yp, ys, sf, sbw, ckv, kr


def kernel(**inputs):
    prog = Prog(dict(_CFG))
    nc = prog.build()
    maps = make_in_maps(inputs)
    res = run_bass_kernel_spmd(nc, maps, core_ids=list(range(NCORE)))
    return assemble(res.results)
```

```python
import numpy as np
import ml_dtypes
import concourse.bass as bass
import concourse.mybir as mybir
from concourse.bass_utils import run_bass_kernel_spmd

F32 = mybir.dt.float32
BF16 = mybir.dt.bfloat16
I32 = mybir.dt.int32
ALU = mybir.AluOpType
AF = mybir.ActivationFunctionType
AX = mybir.AxisListType

D = 1024
DEPTH = 2
SEQ = 256
DSEQ = 2048
PAST = 512
NCORE = 8
NPT = 1024
NST = 512
NTOK = NPT + NST
EPS = 1e-6
GN_EPS = 64e-5
EP = 24000
AGP = {"rk": 1024, "vx": 864, "f": 512}
VX_OFF = dict(v=0, lora=512, ckv=704, kr=832)


class Buf:
    def __init__(self, h, name, space):
        self.h = h
        self.name = name
        self.space = space
        self.w = {}
        self.r = {}
        self.dsem = None
        self.dcnt = 0
        self.dsid = None

    def __getitem__(self, idx):
        return self.h[idx]

    def ap(self):
        return self.h.ap() if self.space == "dram" else self.h[:]


class FW:
    def __init__(self, nc):
        self.nc = nc
        self.E = {"pe": nc.tensor, "act": nc.scalar, "dve": nc.vector, "pool": nc.gpsimd, "sp": nc.sync}
        self.cnt = {e: 0 for e in self.E}
        self.esem = {e: [] for e in self.E}
        self.waited = {e: {} for e in self.E}
        self.pend = {e: [] for e in self.E}
        self.nbuf = 0
        self.ninst = 0
        self.dma_out = {}
        self.sem_pool = []
        self.stack = []
        self.rots = {}
        self.free_sems = []

    def sb(self, shape, dt, name=None):
        self.nbuf += 1
        name = name or "t"
        g = self.nc.sbuf_tensor(f"{name}_{self.nbuf}", list(shape), dt)
        h = g.__enter__()
        b = Buf(h, name, "sbuf")
        if self.stack:
            self.stack[-1].append((g, b))
        return b

    def ps(self, shape, dt=F32, name=None):
        self.nbuf += 1
        name = name or "p"
        h = self.nc.alloc_psum_tensor(f"{name}_{self.nbuf}", list(shape), dt)
        return Buf(h, name, "psum")

    def dram(self, name, shape, dt, kind=None):
        if kind is None:
            h = self.nc.dram_tensor(name, list(shape), dt)
        else:
            h = self.nc.dram_tensor(name, list(shape), dt, kind=kind)
        return Buf(h, name, "dram")

    def rot(self, shape, dt, name, n=2):
        key = (len(self.stack), name)
        if key not in self.rots:
            self.rots[key] = [[self.sb(shape, dt, name) for _ in range(n)], 0]
        ent = self.rots[key]
        t = ent[0][ent[1] % n]
        ent[1] += 1
        return t

    def push(self):
        self.stack.append([])

    def pop(self):
        self.barrier()
        depth = len(self.stack)
        for k in [k for k in self.rots if k[0] == depth]:
            del self.rots[k]
        for g, b in reversed(self.stack.pop()):
            if b.dsem is not None:
                self.free_sems.append((b.dsem, b.dcnt))
                b.dsem = None
            g.__exit__(None, None, None)

    def _sem_for(self, eng, k):
        i = (k - 1) // EP
        while len(self.esem[eng]) <= i:
            self.esem[eng].append(self.nc.alloc_semaphore(f"s_{eng}_{len(self.esem[eng])}"))
        return self.esem[eng][i], (k - 1) % EP + 1

    def _wait(self, eng, ev):
        if ev is None:
            return
        if ev[0] == "eng":
            _, e2, k = ev
            if e2 == eng and eng == "pe":
                return
            key = ("eng", e2, (k - 1) // EP)
            sem, val = self._sem_for(e2, k)
        else:
            _, sem, val, sid = ev
            key = ("sem", sid)
        if self.waited[eng].get(key, 0) >= val:
            return
        self.waited[eng][key] = val
        self.E[eng].wait_ge(sem, val)

    def _check_pend(self, eng, b):
        for e2, lst in self.pend.items():
            if e2 == eng:
                continue
            for (pb, _) in lst:
                if pb is b:
                    raise RuntimeError(f"buffer {b.name} has pending unsignalled access on {e2}, touched by {eng}")

    def _deps(self, eng, reads, writes):
        evs = []
        for b in reads:
            self._check_pend(eng, b)
            evs.extend(b.w.values())
        for b in writes:
            self._check_pend(eng, b)
            for wv in b.w.values():
                if not (wv[0] == "eng" and wv[1] == eng):
                    evs.append(wv)
            for ev in b.r.values():
                if ev[0] == "eng" and ev[1] == eng and eng == "pe":
                    continue
                evs.append(ev)
        for ev in evs:
            self._wait(eng, ev)

    def op(self, eng, fn, reads=(), writes=(), signal=True):
        self._deps(eng, reads, writes)
        ins = fn()
        self.ninst += 1
        if signal:
            self.cnt[eng] += 1
            k = self.cnt[eng]
            sem, val = self._sem_for(eng, k)
            ins.then_inc(sem, 1)
            ev = ("eng", eng, k)
            for (pb, kind) in self.pend[eng]:
                if kind == "r":
                    pb.r[eng] = ev
                else:
                    pb.w = {eng: ev}
                    pb.r = {}
            self.pend[eng] = []
            for b in reads:
                b.r[eng] = ev
            for b in writes:
                b.w = {eng: ev}
                b.r = {}
        else:
            for b in reads:
                self.pend[eng].append((b, "r"))
            for b in writes:
                self.pend[eng].append((b, "w"))
        return ins

    def _dma_sem(self, b):
        if b.dsem is None:
            self.nbuf += 1
            if self.free_sems:
                b.dsem, b.dcnt = self.free_sems.pop()
            else:
                b.dsem = self.nc.alloc_semaphore(f"d_{b.name}_{self.nbuf}")
            b.dsid = self.nbuf
        return b.dsem

    def dma(self, q, out_b, out_ap, in_b, in_ap, sem_owner=None, inc=16, fn=None, extra_reads=(), **kw):
        eng = q
        evs = []
        for b in (in_b, out_b) + tuple(extra_reads):
            self._check_pend(eng, b)
        evs.extend(in_b.w.values())
        for b in extra_reads:
            evs.extend(b.w.values())
        owner = sem_owner or (out_b if out_b.space != "dram" else in_b)
        sem = self._dma_sem(owner)
        for wv in out_b.w.values():
            if wv[0] != "sem":
                evs.append(wv)
        for ev in out_b.r.values():
            evs.append(ev)
        for ev in evs:
            self._wait(eng, ev)
        if fn is None:
            ins = self.E[eng].dma_start(out=out_ap, in_=in_ap, **kw)
        else:
            ins = fn()
        self.ninst += 1
        owner.dcnt += inc
        ins.then_inc(sem, inc)
        ev = ("sem", sem, owner.dcnt, owner.dsid)
        in_b.r[("dma", owner.dsid)] = ev
        for b in extra_reads:
            b.r[("dma", owner.dsid)] = ev
        out_b.w = {k: v for k, v in out_b.w.items() if v[0] == "sem"}
        out_b.w[("dma", owner.dsid)] = ev
        out_b.r = {}
        self.dma_out[owner.dsid] = ev
        return ev

    def wait_buf(self, eng, b):
        self._check_pend(eng, b)
        for ev in b.w.values():
            self._wait(eng, ev)
        for ev in b.r.values():
            self._wait(eng, ev)

    def barrier(self):
        for e in self.E:
            if self.pend[e]:
                raise RuntimeError(f"barrier with pending unsignalled ops on {e}")
        last = []
        for e in ("pe", "act", "dve", "pool"):
            if self.cnt[e] > 0:
                last.append(("eng", e, self.cnt[e]))
        for e in self.E:
            for ev in last:
                if ev[1] == e and e == "pe":
                    continue
                self._wait(e, ev)
            for ev in self.dma_out.values():
                self._wait(e, ev)
        self.dma_out = {}


COLS = dict(r=(0, 512), k=(512, 1024), v=(1024, 1536), wdf=(1536, 1600), wdb=(1600, 1664), ad=(1664, 1728),
            ga=(1728, 2240), qd=(2240, 2496), kvd=(2496, 2624), kr=(2624, 2656), gb=(2656, 3168),
            u=(3168, 3680), vc=(3680, 4192), gc=(4192, 4704), f=(4704, 5216), gd=(5216, 5728))


def _rng(name):
    a, b = COLS[name]
    return np.arange(a, b)


_SWAP32 = np.arange(32).reshape(16, 2)[:, ::-1].reshape(32)

WIN_BLOCKS = [
    ("A0", [_rng("r")]), ("A1", [_rng("k")]), ("A2", [_rng("v")]),
    ("A3", [_rng("wdf"), _rng("wdb"), _rng("ad")]), ("A4", [_rng("ga")]),
    ("B0", [_rng("qd"), _rng("kvd"), _rng("kr"), _rng("kr")[_SWAP32]]), ("B1", [_rng("gb")]),
    ("C0", [_rng("u")]), ("C1", [_rng("vc")]), ("C2", [_rng("gc")]),
    ("D0", [_rng("f")]), ("D1", [_rng("gd")]),
]
WIN_W = {n: int(sum(len(c) for c in cols)) for n, cols in WIN_BLOCKS}
WIN_IDX = {n: 6 + i for i, (n, _) in enumerate(WIN_BLOCKS)}
BLK_PER_LAYER = 36
FBLK = 4096


def _kcp(w, W):
    return w.reshape(8, 128, W).transpose(1, 0, 2).reshape(128, 8 * W)


def build_stream(w_ada, w_in, w_branch, w_merge, w_out):
    st = np.zeros((DEPTH * BLK_PER_LAYER, 128, FBLK), np.float32)
    for l in range(DEPTH):
        base = l * BLK_PER_LAYER
        for b in range(6):
            st[base + b, :, :] = _kcp(w_ada[l][:, 512 * b:512 * b + 512], 512)
        for i, (n, cols) in enumerate(WIN_BLOCKS):
            cc = np.concatenate(cols)
            W = len(cc)
            st[base + 6 + i, :, :8 * W] = _kcp(w_in[l][:, cc], W)
        for d in range(8):
            cc = np.concatenate([n * 1024 + d * 128 + np.arange(128) for n in range(4)])
            st[base + 18 + 2 * d, :, :] = _kcp(w_merge[l][:, cc], 512)
            wb = w_branch[l][:, :, d * 128:(d + 1) * 128]
            wb = wb.reshape(4, 4, 128, 128).transpose(2, 0, 1, 3)
            st[base + 19 + 2 * d, :, :2048] = wb.reshape(128, 2048)
        for b in range(2):
            st[base + 34 + b, :, :] = _kcp(w_out[l][:, 512 * b:512 * b + 512], 512)
    return st


SP_OFF = {}
_o = 0
for _n, _w in [("norm_g", 8), ("b_ada", 24), ("mu_rkv", 12), ("mu_lora", 3), ("b_merge", 32), ("rw", 36),
               ("qn", 2), ("kvn", 1), ("gln_g", 4), ("gln_b", 4), ("fin_g", 8)]:
    SP_OFF[_n] = (_o, _w)
    _o += _w
NSP = _o
RW_NAMES = ["w0_f", "w0_b", "a0_f", "a0_b", "k_k", "k_a", "r_k", "ln_g", "ln_b"]


def _pc(v, n):
    return np.asarray(v, np.float32).reshape(n, 128).T


def build_small(inp):
    sp = np.zeros((DEPTH, 128, NSP), np.float32)
    for l in range(DEPTH):
        def put(name, arr):
            o, w = SP_OFF[name]
            sp[l, :, o:o + w] = arr
        put("norm_g", _pc(inp["norm_g"][l], 8))
        put("b_ada", _pc(inp["b_ada"][l], 24))
        put("mu_rkv", _pc(inp["shift_mu"][l][:1536], 12))
        ml = np.zeros((128, 3), np.float32)
        ml[:64, :] = inp["shift_mu"][l][1536:1728].reshape(3, 64).T
        put("mu_lora", ml)
        bm = inp["b_merge"][l].reshape(4, 8, 128)
        put("b_merge", bm.transpose(2, 1, 0).reshape(128, 32))
        rwv = [inp["rwkv_w0"][l][0], inp["rwkv_w0"][l][1], inp["rwkv_a0"][l][0], inp["rwkv_a0"][l][1],
               inp["rwkv_k_k"][l], inp["rwkv_k_a"][l], inp["rwkv_r_k"][l].reshape(512), inp["rwkv_ln_g"][l],
               inp["rwkv_ln_b"][l]]
        rw = np.stack([_pc(v, 4) for v in rwv], axis=1)
        put("rw", rw.reshape(128, 36))
        put("qn", _pc(inp["mla_q_norm"][l], 2))
        put("kvn", _pc(inp["mla_kv_norm"][l], 1))
        put("gln_g", _pc(inp["gmlp_ln_g"][l], 4))
        put("gln_b", _pc(inp["gmlp_ln_b"][l], 4))
        put("fin_g", _pc(inp["final_norm_g"], 8))
    return sp


def rw_col(name, pair):
    o, _ = SP_OFF["rw"]
    return o + RW_NAMES.index(name) * 4 + pair


def build_consts():
    c = {}
    c["ident"] = np.eye(128, dtype=np.float32)
    hb = np.arange(128) // 64
    c["bones"] = (hb[:, None] == hb[None, :]).astype(np.float32)
    p = np.arange(128)[:, None]
    f = np.arange(128)[None, :]
    mk = {}
    bd32 = ((p // 32) == (f // 32)).astype(np.float32)
    od64 = (((p // 64) == (f // 64)) & ((p // 32) != (f // 32))).astype(np.float32)
    od128 = ((p // 64) != (f // 64)).astype(np.float32)
    for dname, ms, mi, mt in (("f", p < f, p <= f, f < p), ("b", p > f, p >= f, f > p)):
        ms = ms.astype(np.float32)
        mi = mi.astype(np.float32)
        mtf = -(mt.astype(np.float32))
        mk[dname] = np.concatenate([ms, mi, -ms * bd32, mi, mtf * bd32, mtf * bd32, mtf * od64, mtf * od64,
                                    mtf * od128, mtf * od128], axis=1)
    c["mask"] = np.stack([mk["f"], mk["b"]], axis=1).reshape(128, 2 * 1280)
    dd = np.arange(128)
    ang = 2 * np.pi * np.outer(dd, dd) / 128.0
    for nm, T in (("dftd_p", SEQ), ("dftd_s", DSEQ)):
        sc = 1.0 / np.sqrt(T * 128.0)
        c[nm] = np.concatenate([np.cos(ang) * sc, -np.sin(ang) * sc], axis=1).astype(np.float32)
    tt = np.arange(SEQ)
    angp = 2 * np.pi * np.outer(tt, tt) / SEQ
    cp = np.stack([np.cos(angp), np.sin(angp)], axis=1)
    c["dftT_p"] = cp.reshape(2, 128, 2, SEQ).transpose(1, 0, 2, 3).reshape(128, 2 * 2 * SEQ).astype(np.float32)
    return c


def build_core_consts(j):
    t = np.arange(DSEQ)
    k1 = 512 * j + np.arange(512)
    ang = 2 * np.pi * ((np.outer(t, k1)) % DSEQ) / DSEQ
    cs = np.stack([np.cos(ang), np.sin(ang)], axis=1)
    dftT = cs.reshape(16, 128, 2, 512).astype(np.float32)
    pos = 512 * j + np.arange(512)
    row = (pos // 64).astype(np.float32)
    col = (pos % 64).astype(np.float32)
    inv = (10000.0 ** (-np.arange(8, dtype=np.float32) / 8)).astype(np.float32)
    ang = np.concatenate([row[:, None] * inv, col[:, None] * inv], axis=-1).astype(np.float32)
    cos = np.cos(ang).astype(np.float32)
    sin = np.sin(ang).astype(np.float32)
    COS = np.repeat(cos, 2, axis=1).T
    SIN = np.stack([-sin, sin], axis=2).reshape(512, 32).T
    rope = np.stack([COS, SIN], axis=1).astype(np.float32)
    return dftT, rope


class Prog:
    def __init__(self, cfg):
        self.cfg = cfg
        self.branches = cfg.get("branches", "ABCD")
        self.depth = cfg.get("depth", DEPTH)
        nc = bass.Bass("TRN2", target_bir_lowering=False)
        self.nc = nc
        fw = FW(nc)
        self.fw = fw
        self.V, self.A, self.G, self.T = nc.vector, nc.scalar, nc.gpsimd, nc.tensor
        di = lambda n, s, dt=F32: fw.dram(n, s, dt, kind="ExternalInput")
        do = lambda n, s, dt=F32: fw.dram(n, s, dt, kind="ExternalOutput")
        self.d = dict(
            xp=di("xp", [D, NPT]), xs=di("xs", [D, NST]),
            wst=di("wst", [DEPTH * BLK_PER_LAYER, 128, FBLK]),
            sp=di("sp", [DEPTH, 128, NSP]), cond=di("cond", [128, 16]),
            ident=di("ident", [128, 128]), bones=di("bones", [128, 128]), mask=di("mask", [128, 2560]),
            dftd_p=di("dftd_p", [128, 256]), dftd_s=di("dftd_s", [128, 256]), dftT_p=di("dftT_p", [128, 1024]),
            dftT_s=di("dftT_s", [16, 128, 1024]), rope=di("rope", [32, 1024]),
            wup=di("wup", [DEPTH, 64, 1024]), aup=di("aup", [DEPTH, 64, 1024]),
            wupo=di("wupo", [DEPTH, 64, 256]), aupo=di("aupo", [DEPTH, 64, 256]),
            spo=di("spo", [DEPTH, 128, 12]),
            wq=di("wq", [DEPTH, 128, 2 * 8 * 96]), wqs=di("wqs", [DEPTH, 128, 2 * 8 * 32]),
            wkk=di("wkk", [DEPTH, 128, 512]), wkv=di("wkv", [DEPTH, 128, 512]),
            wsT=di("wsT", [DEPTH, 128, 512]), bsb=di("bsb", [DEPTH, 128, 512]),
            st0=di("st0", [DEPTH, 2, 128, 64]), cckv=di("cckv", [DEPTH, 128, PAST]),
            ckr=di("ckr", [DEPTH, 32, PAST]),
            idx1=di("idx1", [128, 12], I32), idx2=di("idx2", [128, 4], I32),
            yp=do("yp", [D, NPT]), ys=do("ys", [D, NST]),
            stout=do("stout", [DEPTH * 2 * 4 * 4 * 128, 64]),
            ckvout=do("ckvout", [DEPTH, 128, NPT]), krout=do("krout", [DEPTH, 32, NPT]),
        )
        self.ag1_in = [{k: fw.dram(f"ag1i{k}{l}", [n, 512], BF16) for k, n in AGP.items()} for l in range(DEPTH)]
        self.ag1_out = [{k: fw.dram(f"ag1o{k}{l}", [4 * n, 512], BF16) for k, n in AGP.items()} for l in range(DEPTH)]
        self.ag2_in = [fw.dram(f"ag2i{l}", [512, 512], BF16) for l in range(DEPTH)]
        self.ag2_out = [fw.dram(f"ag2o{l}", [2048, 512], BF16) for l in range(DEPTH)]
        self.psb = [fw.ps([128, 512], F32, f"bank{i}") for i in range(6)]
        self.pst = [fw.ps([128, 1024], BF16, f"pst{i}") for i in range(2)]
        self.pst_rr = 0
        self.ps_rr = 0
        self.xT = fw.sb([128, 8, NTOK], F32, "xT")
        self.hTs = [fw.sb([128, 8, 512], BF16, "hTa"), None]
        self.oTs = [[fw.sb([128, 4, 512], BF16, f"oTa{n}") for n in range(4)], None]
        self.slots = None
        self.slot_rr = 0
        self.plan = []
        self.plan_pos = 0
        self.stg = None
        self.blocks_left = 0
        self.dma_issued = {}
        self.load_consts()

    ps_range = (0, 4)

    def nps(self, lo=None, hi=None):
        lo = self.ps_range[0] if lo is None else lo
        hi = self.ps_range[1] if hi is None else hi
        n = hi - lo
        b = self.psb[lo + (self.ps_rr % n)]
        self.ps_rr += 1
        return b

    def dve(self, fn, r, w):
        return self.fw.op("dve", fn, r, w)

    def rsqrt(self, out_buf, out_ap, in_buf, in_ap):
        self.act(lambda: self.A.activation(in_ap, in_ap, AF.Sqrt), [in_buf], [in_buf])
        self.dve(lambda: self.V.reciprocal(out_ap, in_ap), [in_buf], [out_buf])

    def act(self, fn, r, w):
        return self.fw.op("act", fn, r, w)

    def pool(self, fn, r, w):
        return self.fw.op("pool", fn, r, w)

    def pe(self, fn, r, w, signal=True):
        return self.fw.op("pe", fn, r, w, signal=signal)

    def load(self, dst, dst_ap, src, src_ap, q="sp"):
        if dst_ap.dtype != src_ap.dtype:
            q = "pool"
        return self.fw.dma(q, dst, dst_ap, src, src_ap)

    def load_consts(self):
        fw, d = self.fw, self.d
        self.ident = fw.sb([128, 128], BF16, "ident")
        self.bones = fw.sb([128, 128], BF16, "bones")
        self.mask = fw.sb([128, 2560], BF16, "mask")
        self.ones = fw.sb([128, 128], BF16, "ones")
        self.onesf = fw.sb([128, 128], F32, "onesf")
        self.rope = None
        self.cond = fw.sb([128, 16], F32, "cond")
        self.idx1 = fw.sb([128, 12], I32, "idx1")
        self.idx2 = fw.sb([128, 4], I32, "idx2")
        for nm in ("ident", "bones", "mask", "cond", "idx1", "idx2"):
            t = getattr(self, nm)
            self.load(t, t[:], d[nm], d[nm].ap())
        self.pool(lambda: self.G.memset(self.ones[:], 1.0), [], [self.ones])
        self.pool(lambda: self.G.memset(self.onesf[:], 1.0), [], [self.onesf])
        self.hm = fw.sb([128, 2], F32, "hm")
        self.dve(lambda: self.V.tensor_copy(self.hm[:, 0:2], self.bones[:, 0:128:64]), [self.bones], [self.hm])
        xv = self.xT
        self.load(xv, xv[:, :, 0:NPT], d["xp"], d["xp"].ap().rearrange("(k p) t -> p k t", p=128))
        self.load(xv, xv[:, :, NPT:NTOK], d["xs"], d["xs"].ap().rearrange("(k p) t -> p k t", p=128))

    def stream_plan(self, ids):
        self.plan.extend(ids)

    def stream_begin(self, nblocks):
        fw = self.fw
        self.slots = [fw.sb([128, FBLK], BF16, f"slot{i}") for i in range(2)]
        self.stg = [fw.sb([128, FBLK // 2], F32, f"wstg{i}") for i in range(2)]
        self.blocks_left = nblocks
        self.dma_issued = {}

    def _issue_dma(self, pos):
        blk = self.plan[pos]
        src = self.d["wst"]
        for hf in range(2):
            st = self.stg[hf]
            self.fw.dma("sp", st, st[:, :], src, src[blk, :, hf * (FBLK // 2):(hf + 1) * (FBLK // 2)])
        self.dma_issued[pos] = True

    def next_block(self, blk):
        pos = self.plan_pos
        assert self.plan[pos] == blk, (pos, self.plan[pos], blk)
        assert self.blocks_left > 0
        if pos not in self.dma_issued:
            self._issue_dma(pos)
        slot = self.slots[pos % len(self.slots)]
        for hf in range(2):
            st = self.stg[hf]
            if hf == 0:
                self.dve(lambda hf=hf, st=st: self.V.tensor_copy(slot[:, hf * (FBLK // 2):(hf + 1) * (FBLK // 2)], st[:, :]), [st], [slot])
            else:
                self.act(lambda hf=hf, st=st: self.A.copy(slot[:, hf * (FBLK // 2):(hf + 1) * (FBLK // 2)], st[:, :]), [st], [slot])
        self.plan_pos += 1
        self.blocks_left -= 1
        return slot

    def prefetch_next(self):
        pos = self.plan_pos
        if self.blocks_left > 0 and pos < len(self.plan) and pos not in self.dma_issued:
            self._issue_dma(pos)

    def load_layer_small(self, l):
        fw, d = self.fw, self.d
        self.sp = fw.sb([128, NSP], F32, "sp")
        self.load(self.sp, self.sp[:], d["sp"], d["sp"][l, :, :])
        sp = self.sp
        o, _ = SP_OFF["mu_rkv"]
        self.mu1 = fw.sb([128, 15], F32, "mu1")
        self.muh = fw.sb([128, 15], F32, "muh")
        self.dve(lambda: self.V.tensor_scalar(self.mu1[:], sp[:, o:o + 15], -1.0, 1.0, ALU.mult, ALU.add), [sp], [self.mu1])
        self.dve(lambda: self.V.tensor_scalar(self.muh[:], sp[:, o:o + 15], 0.5, None, ALU.mult), [sp], [self.muh])
        o2, _ = SP_OFF["rw"]
        self.rwh = fw.sb([128, 16], F32, "rwh")
        self.dve(lambda: self.V.tensor_scalar(self.rwh[:], sp[:, o2:o2 + 16], 0.5, None, ALU.mult), [sp], [self.rwh])
        ob, _ = SP_OFF["b_merge"]
        self.bmh = fw.sb([128, 32], F32, "bmh")
        self.dve(lambda: self.V.tensor_scalar(self.bmh[:], sp[:, ob:ob + 32], 0.5, None, ALU.mult), [sp], [self.bmh])

    def spc(self, name, i=0, n=1):
        o, _ = SP_OFF[name]
        return self.sp[:, o + i:o + i + n]

    def ada(self, l):
        fw = self.fw
        sc = fw.sb([128, 16], BF16, "scond")
        th = fw.sb([128, 16], F32, "cth")
        c = self.cond
        self.act(lambda: self.A.activation(th[:], c[:], AF.Tanh, scale=0.5), [c], [th])
        t2 = fw.sb([128, 16], F32, "ct2")
        self.dve(lambda: self.V.scalar_tensor_tensor(t2[:], th[:], 1.0, c[:], ALU.add, ALU.mult), [th, c], [t2])
        self.dve(lambda: self.V.tensor_scalar(sc[:], t2[:], 0.5, None, ALU.mult), [t2], [sc])
        ps = self.psb[5]
        self.mod = fw.sb([128, 24, 2], F32, "mod")
        self.gmod = fw.sb([128, 8, 2], F32, "gmod")
        fw.push()
        self.stream_begin(6)
        for b in range(6):
            slot = self.next_block(l * BLK_PER_LAYER + b)
            for n in range(4):
                m = b * 4 + n
                for kc in range(8):
                    self.pe(lambda kc=kc, n=n, m=m, slot=slot: self.T.matmul(
                        ps[:, 2 * m:2 * m + 2], slot[:, kc * 512 + n * 128:kc * 512 + n * 128 + 128],
                        sc[:, 2 * kc:2 * kc + 2], start=(kc == 0), stop=(kc == 7)),
                        [slot, sc], [ps], signal=(kc == 7 and n == 3))
            self.prefetch_next()
        fw.pop()
        ob, _ = SP_OFF["b_ada"]
        for cc in range(2):
            self.dve(lambda cc=cc: self.V.tensor_tensor(self.mod[:, :, cc], ps[:, cc:48:2], self.sp[:, ob:ob + 24], ALU.add),
                     [ps, self.sp], [self.mod])
        og, _ = SP_OFF["norm_g"]
        for cc in range(2):
            self.dve(lambda cc=cc: self.V.scalar_tensor_tensor(self.gmod[:, :, cc], self.mod[:, 8:16, cc], 1.0,
                                                               self.sp[:, og:og + 8], ALU.add, ALU.mult),
                     [self.mod, self.sp], [self.gmod])

    def rstd_tile(self, src, views, nfeat, out, N):
        fw = self.fw
        ps = self.nps()
        nk = len(views)
        for i, v in enumerate(views):
            sq = fw.rot([128, 512], BF16, "sq")
            self.act(lambda v=v, sq=sq: self.A.activation(sq[:, :N], v, AF.Square), [src], [sq])
            self.pe(lambda i=i, sq=sq: self.T.matmul(ps[:, :N], self.ones[:], sq[:, :N], start=(i == 0), stop=(i == nk - 1)),
                    [self.ones, sq], [ps], signal=True)
        t = fw.sb([128, 512], F32, "rs_t")
        self.dve(lambda: self.V.tensor_scalar(t[:, :N], ps[:, :N], 1.0 / nfeat, EPS, ALU.mult, ALU.add), [ps], [t])
        self.rsqrt(out, out[:, :N], t, t[:, :N])

    def make_h(self, cc, x0, h0, N):
        fw = self.fw
        fw.push()
        rstd = fw.sb([128, 512], F32, "rstd")
        self.rstd_tile(self.xT, [self.xT[:, kc, x0:x0 + N] for kc in range(8)], float(D), rstd, N)
        for kc in range(8):
            tmp = fw.rot([128, 512], F32, "htmp")
            self.dve(lambda kc=kc, tmp=tmp: self.V.scalar_tensor_tensor(
                tmp[:, :N], self.xT[:, kc, x0:x0 + N], self.gmod[:, kc, cc:cc + 1], rstd[:, :N], ALU.mult, ALU.mult),
                [self.xT, self.gmod, rstd], [tmp])
            self.act(lambda kc=kc, tmp=tmp: self.A.activation(
                self.hTs[h0 // 512][:, kc, 0:N], tmp[:, :N], AF.Identity, bias=self.mod[:, kc, cc:cc + 1], scale=1.0),
                [tmp, self.mod], [self.hTs[h0 // 512]])
        fw.pop()

    def zmm(self, slot, W, c0, w, h0, N, ps=None, prow=0):
        ps = ps or self.nps()
        for kc in range(8):
            self.pe(lambda kc=kc: self.T.matmul(ps[prow:prow + w, :N], slot[:, kc * W + c0:kc * W + c0 + w],
                                                self.hTs[h0 // 512][:, kc, 0:N], start=(kc == 0), stop=(kc == 7)),
                    [slot, self.hTs[h0 // 512]], [ps], signal=(kc == 7))
        return ps

    def silu2(self, ps, rows, N, out_ap, out_buf):
        fw = self.fw
        th = fw.rot([128, 512], F32, "s2th")
        self.act(lambda: self.A.activation(th[:rows, :N], ps[:rows, :N], AF.Tanh, scale=0.5), [ps], [th])
        self.dve(lambda: self.V.scalar_tensor_tensor(out_ap, th[:rows, :N], 1.0, ps[:rows, :N], ALU.add, ALU.mult),
                 [th, ps], [out_buf])

    def gelu2(self, ps, rows, N, out_ap, out_buf):
        fw = self.fw
        u = fw.rot([128, 512], F32, "g2u")
        self.act(lambda: self.A.activation(u[:rows, :N], ps[:rows, :N], AF.Square), [ps], [u])
        self.dve(lambda: self.V.tensor_scalar(u[:rows, :N], u[:rows, :N], 0.044715, 1.0, ALU.mult, ALU.add), [u], [u])
        self.dve(lambda: self.V.tensor_tensor(u[:rows, :N], u[:rows, :N], ps[:rows, :N], ALU.mult), [u, ps], [u])
        self.act(lambda: self.A.activation(u[:rows, :N], u[:rows, :N], AF.Tanh, scale=0.7978845608028654), [u], [u])
        self.dve(lambda: self.V.scalar_tensor_tensor(out_ap, u[:rows, :N], 1.0, ps[:rows, :N], ALU.add, ALU.mult),
                 [u, ps], [out_buf])

    def transpose_to(self, src_buf, src_ap, dst_buf, dst_ap, rows=128, cols=128, eng="act"):
        pt = self.pst[self.pst_rr % 2]
        self.pst_rr += 1
        self.pe(lambda: self.T.transpose(pt[:cols, :rows], src_ap, self.ident[:rows, :rows]), [src_buf, self.ident], [pt])
        if eng == "act":
            self.act(lambda: self.A.copy(dst_ap, pt[:cols, :rows]), [pt], [dst_buf])
        else:
            self.dve(lambda: self.V.tensor_copy(dst_ap, pt[:cols, :rows]), [pt], [dst_buf])

    def phaseC(self, l, h0, N, o0):
        fw = self.fw
        base = l * BLK_PER_LAYER
        fw.push()
        self.stream_begin(3)
        U2 = fw.sb([128, 4, 512], BF16, "U2")
        GV = fw.sb([128, 4, 512], BF16, "GV")
        GC2 = fw.sb([128, 4, 512], BF16, "GC2")
        wsT = fw.sb([128, 512], BF16, "wsT")
        bsb = fw.sb([128, 512], F32, "bsb")
        self.load(wsT, wsT[:], self.d["wsT"], self.d["wsT"][l, :, :])
        self.load(bsb, bsb[:], self.d["bsb"], self.d["bsb"][l, :, :])
        slot = self.next_block(base + WIN_IDX["C0"])
        for c in range(4):
            ps = self.zmm(slot, 512, c * 128, 128, h0, N)
            self.gelu2(ps, 128, N, U2[:, c, :N], U2)
        self.prefetch_next()
        slot = self.next_block(base + WIN_IDX["C1"])
        for c in range(4):
            ps = self.zmm(slot, 512, c * 128, 128, h0, N)
            self.gelu2(ps, 128, N, GV[:, c, :N], GV)
        self.prefetch_next()
        slot = self.next_block(base + WIN_IDX["C2"])
        for c in range(4):
            ps = self.zmm(slot, 512, c * 128, 128, h0, N)
            self.silu2(ps, 128, N, GC2[:, c, :N], GC2)
        self.prefetch_next()
        psm = self.nps()
        psq = self.nps()
        for c in range(4):
            self.pe(lambda c=c: self.T.matmul(psm[:, :N], self.ones[:], GV[:, c, :N], start=(c == 0), stop=(c == 3)),
                    [self.ones, GV], [psm], signal=(c == 3))
        for c in range(4):
            sq = fw.rot([128, 512], BF16, "gsq")
            self.act(lambda c=c, sq=sq: self.A.activation(sq[:, :N], GV[:, c, :N], AF.Square), [GV], [sq])
            self.pe(lambda c=c, sq=sq: self.T.matmul(psq[:, :N], self.ones[:], sq[:, :N], start=(c == 0), stop=(c == 3)),
                    [self.ones, sq], [psq], signal=True)
        mu = fw.sb([128, 512], F32, "gmu")
        msq = fw.sb([128, 512], F32, "gmsq")
        var = fw.sb([128, 512], F32, "gvar")
        rstd = fw.sb([128, 512], F32, "grstd")
        self.dve(lambda: self.V.tensor_scalar(mu[:, :N], psm[:, :N], 1.0 / 512, None, ALU.mult), [psm], [mu])
        self.dve(lambda: self.V.tensor_tensor(msq[:, :N], mu[:, :N], mu[:, :N], ALU.mult), [mu], [msq])
        self.dve(lambda: self.V.scalar_tensor_tensor(var[:, :N], psq[:, :N], 1.0 / 512, msq[:, :N], ALU.mult, ALU.subtract),
                 [psq, msq], [var])
        self.dve(lambda: self.V.tensor_scalar(var[:, :N], var[:, :N], 4e-5, None, ALU.add), [var], [var])
        self.rsqrt(rstd, rstd[:, :N], var, var[:, :N])
        VN = fw.sb([128, 4, 512], BF16, "VN")
        for c in range(4):
            t = fw.rot([128, 512], F32, "lnt")
            self.dve(lambda c=c, t=t: self.V.tensor_tensor(t[:, :N], GV[:, c, :N], mu[:, :N], ALU.subtract), [GV, mu], [t])
            self.dve(lambda t=t: self.V.tensor_tensor(t[:, :N], t[:, :N], rstd[:, :N], ALU.mult), [t, rstd], [t])
            self.act(lambda c=c, t=t: self.A.activation(VN[:, c, :N], t[:, :N], AF.Identity, bias=self.spc("gln_b", c),
                                                        scale=self.spc("gln_g", c)), [t, self.sp], [VN])
        nsub = N // 128
        for g in range(4):
            pmix = self.nps()
            for s in range(nsub):
                vtm = fw.rot([128, 128], BF16, "vtm")
                self.transpose_to(VN, VN[:, g, s * 128:(s + 1) * 128], vtm, vtm[:], eng=("act" if s % 2 else "dve"))
                self.pe(lambda g=g, s=s, vtm=vtm: self.T.matmul(pmix[:, s * 128:(s + 1) * 128], vtm[:],
                                                                 wsT[:, g * 128:(g + 1) * 128], start=True, stop=True),
                        [vtm, wsT], [pmix], signal=(s == nsub - 1))
            t = fw.rot([128, 512], F32, "mixt")
            for s in range(nsub):
                self.dve(lambda g=g, s=s, t=t: self.V.tensor_tensor(t[:, s * 128:(s + 1) * 128], pmix[:, s * 128:(s + 1) * 128],
                                                                     bsb[:, g * 128:(g + 1) * 128], ALU.add), [pmix, bsb], [t])
            self.dve(lambda g=g, t=t: self.V.scalar_tensor_tensor(t[:, :N], t[:, :N], 0.25, U2[:, g, :N], ALU.mult, ALU.mult),
                     [t, U2], [t])
            self.dve(lambda g=g, t=t: self.V.tensor_tensor(self.oTs[o0 // 512][2][:, g, 0:N], t[:, :N], GC2[:, g, :N], ALU.mult),
                     [t, GC2], [self.oTs[o0 // 512][2]])
        fw.pop()

    def merge_out(self, l, cc, x0, NT):
        fw = self.fw
        base = l * BLK_PER_LAYER
        ntile = NT // 512
        fw.push()
        self.stream_begin(18)
        merged = fw.sb([128, 8, NT], BF16, "merged")
        for d in range(8):
            slotM = self.next_block(base + 18 + 2 * d)
            self.prefetch_next()
            slotB = self.next_block(base + 19 + 2 * d)
            for tt in range(ntile):
                t0 = tt * 512
                acc = fw.rot([128, 512], F32, "macc")
                for n in range(4):
                    psg = self.nps()
                    for kc in range(8):
                        self.pe(lambda kc=kc, n=n, tt=tt: self.T.matmul(psg[:, :], slotM[:, kc * 512 + n * 128:kc * 512 + n * 128 + 128],
                                                                 self.hTs[tt][:, kc, :], start=(kc == 0), stop=(kc == 7)),
                                [slotM, self.hTs[tt]], [psg], signal=(kc == 7))
                    psp = self.nps()
                    for k4 in range(4):
                        self.pe(lambda k4=k4, n=n, tt=tt: self.T.matmul(psp[:, :], slotB[:, (n * 4 + k4) * 128:(n * 4 + k4) * 128 + 128],
                                                                 self.oTs[tt][n][:, k4, :], start=(k4 == 0), stop=(k4 == 3)),
                                [slotB, self.oTs[tt][n]], [psp], signal=(k4 == 3))
                    th = fw.rot([128, 512], F32, "mth")
                    self.act(lambda n=n, th=th, psg=psg: self.A.activation(th[:], psg[:], AF.Tanh, bias=self.bmh[:, d * 4 + n:d * 4 + n + 1],
                                                                           scale=0.5), [psg, self.bmh], [th])
                    if n == 0:
                        self.dve(lambda th=th, psp=psp: self.V.scalar_tensor_tensor(acc[:], th[:], 1.0, psp[:], ALU.add, ALU.mult),
                                 [th, psp], [acc])
                    else:
                        self.dve(lambda th=th, psp=psp: self.V.scalar_tensor_tensor(th[:], th[:], 1.0, psp[:], ALU.add, ALU.mult),
                                 [th, psp], [th])
                        self.dve(lambda th=th: self.V.tensor_tensor(acc[:], acc[:], th[:], ALU.add), [acc, th], [acc])
                self.act(lambda acc=acc, t0=t0: self.A.mul(merged[:, d, t0:t0 + 512], acc[:], 0.5), [acc], [merged])
            self.prefetch_next()
        for b in range(2):
            slotO = self.next_block(base + 34 + b)
            self.prefetch_next()
            for dd in range(4):
                dch = b * 4 + dd
                for tt in range(ntile):
                    t0 = tt * 512
                    ps = self.nps()
                    for kc in range(8):
                        self.pe(lambda kc=kc, dd=dd: self.T.matmul(ps[:, :], slotO[:, kc * 512 + dd * 128:kc * 512 + dd * 128 + 128],
                                                                   merged[:, kc, t0:t0 + 512], start=(kc == 0), stop=(kc == 7)),
                                [slotO, merged], [ps], signal=(kc == 7))
                    self.dve(lambda dch=dch, t0=t0, ps=ps: self.V.scalar_tensor_tensor(
                        self.xT[:, dch, x0 + t0:x0 + t0 + 512], ps[:, :], self.mod[:, 16 + dch, cc:cc + 1],
                        self.xT[:, dch, x0 + t0:x0 + t0 + 512], ALU.mult, ALU.add), [ps, self.mod, self.xT], [self.xT])
        fw.pop()

    def final_out(self):
        fw = self.fw
        for (x0, N, dst) in ((0, 512, ("yp", 0)), (512, 512, ("yp", 512)), (NPT, 512, ("ys", 0))):
            fw.push()
            rstd = fw.sb([128, 512], F32, "frstd")
            self.rstd_tile(self.xT, [self.xT[:, kc, x0:x0 + N] for kc in range(8)], float(D), rstd, N)
            stg = fw.sb([128, 8, 512], F32, "fstg")
            for kc in range(8):
                self.dve(lambda kc=kc: self.V.scalar_tensor_tensor(stg[:, kc, :], self.xT[:, kc, x0:x0 + N], self.spc("fin_g", kc),
                                                                   rstd[:, :], ALU.mult, ALU.mult), [self.xT, self.sp, rstd], [stg])
            dt = self.d[dst[0]]
            ncol = NPT if dst[0] == "yp" else NST
            dview = dt.ap().rearrange("(k p) t -> p k t", p=128)[:, :, dst[1]:dst[1] + N]
            fw.dma("sp", dt, dview, stg, stg[:], sem_owner=self.outsem)
            fw.pop()

    def shift_evac(self, ps, rows, N, nseq, mu1, muh, out_ap, out_buf, tanh=False):
        fw = self.fw
        zt = fw.rot([128, 512], F32, "shz")
        o32 = fw.rot([128, 512], F32, "sho")
        self.act(lambda: self.A.copy(zt[:rows, :N], ps[:rows, :N]), [ps], [zt])
        self.dve(lambda: self.V.tensor_scalar(o32[:rows, :N], zt[:rows, :N], mu1, None, ALU.mult), [zt, self.mu1], [o32])
        z3 = zt[:rows, :N].rearrange("p (s t) -> p s t", s=nseq)
        o3 = o32[:rows, :N].rearrange("p (s t) -> p s t", s=nseq)
        Tq = N // nseq
        self.dve(lambda: self.V.scalar_tensor_tensor(o3[:, :, 1:Tq], z3[:, :, 0:Tq - 1], muh, o3[:, :, 1:Tq], ALU.mult, ALU.add),
                 [zt, self.muh, o32], [o32])
        self.dve(lambda: self.V.scalar_tensor_tensor(o3[:, :, 0:Tq - 1], z3[:, :, 1:Tq], muh, o3[:, :, 0:Tq - 1], ALU.mult, ALU.add),
                 [zt, self.muh, o32], [o32])
        if tanh:
            self.act(lambda: self.A.activation(out_ap, o32[:rows, :N], AF.Tanh), [o32], [out_buf])
        else:
            self.act(lambda: self.A.copy(out_ap, o32[:rows, :N]), [o32], [out_buf])

    def load_rwkv_w(self, l, own):
        fw, d = self.fw, self.d
        ncol = 256 if own else 1024
        self.wup = fw.sb([64, ncol], BF16, "wup")
        self.aup = fw.sb([64, ncol], BF16, "aup")
        sw, sa = (d["wupo"], d["aupo"]) if own else (d["wup"], d["aup"])
        self.load(self.wup, self.wup[:, :], sw, sw[l, :, :])
        self.load(self.aup, self.aup[:, :], sa, sa[l, :, :])

    def phaseA_prompt(self, l, half):
        fw = self.fw
        base = l * BLK_PER_LAYER
        h0 = half * 512
        fw.push()
        self.load_rwkv_w(l, False)
        zz = [fw.sb([128, 4, 512], BF16, nm) for nm in ("zr", "zk", "zv")]
        lo = [fw.sb([64, 512], BF16, nm) for nm in ("twdf", "twdb", "adT")]
        GA2 = fw.sb([128, 4, 512], BF16, "GA2")
        fw.push()
        self.stream_begin(5)
        for which in range(3):
            slot = self.next_block(base + WIN_IDX[f"A{which}"])
            for c in range(4):
                ps = self.zmm(slot, 512, c * 128, 128, h0, 512)
                i = which * 4 + c
                self.shift_evac(ps, 128, 512, 2, self.mu1[:, i:i + 1], self.muh[:, i:i + 1], zz[which][:, c, :], zz[which])
            self.prefetch_next()
        slot = self.next_block(base + WIN_IDX["A3"])
        for i in range(3):
            ps = self.zmm(slot, 192, i * 64, 64, h0, 512)
            self.shift_evac(ps, 64, 512, 2, self.mu1[0:64, 12 + i:13 + i], self.muh[0:64, 12 + i:13 + i], lo[i][:, :], lo[i], tanh=(i < 2))
        self.prefetch_next()
        slot = self.next_block(base + WIN_IDX["A4"])
        for c in range(4):
            ps = self.zmm(slot, 512, c * 128, 128, h0, 512)
            self.silu2(ps, 128, 512, GA2[:, c, :], GA2)
        self.prefetch_next()
        fw.pop()
        for sq in range(2):
            for pair in range(4):
                t0 = sq * 256
                seqi = half * 2 + sq

                def yout(c, yfin, pair=pair, t0=t0):
                    cs = t0 + c * 128
                    self.dve(lambda: self.V.scalar_tensor_tensor(self.oTs[half][0][:, pair, cs:cs + 128], yfin[:, :], 0.5,
                                                                 GA2[:, pair, cs:cs + 128], ALU.mult, ALU.mult), [yfin, GA2], [self.oTs[half][0]])

                def stout(dd, ST, pair=pair, seqi=seqi):
                    so = self.d["stout"]
                    row = (((l * 2 + dd) * 4 + seqi) * 4 + pair) * 128
                    fw.dma("sp", so, so[row:row + 128, :], ST, ST[:, :], sem_owner=self.outsem)

                J = dict(T=256, r=(zz[0], lambda a, b, pair=pair, t0=t0: zz[0][:, pair, t0 + a:t0 + b]),
                         k=(zz[1], lambda a, b, pair=pair, t0=t0: zz[1][:, pair, t0 + a:t0 + b]),
                         v=(zz[2], lambda a, b, pair=pair, t0=t0: zz[2][:, pair, t0 + a:t0 + b]),
                         twd=[(lo[0], lambda a, b, t0=t0: lo[0][:, t0 + a:t0 + b]), (lo[1], lambda a, b, t0=t0: lo[1][:, t0 + a:t0 + b])],
                         ad=(lo[2], lambda a, b, t0=t0: lo[2][:, t0 + a:t0 + b]),
                         par=lambda nm, pair=pair: self.sp[:, rw_col(nm, pair):rw_col(nm, pair) + 1],
                         parh=lambda nm, pair=pair: self.rwh[:, RW_NAMES.index(nm) * 4 + pair:RW_NAMES.index(nm) * 4 + pair + 1],
                         parbufs=[self.sp, self.rwh],
                         wup=lambda dd, pair=pair: self.wup[:, dd * 512 + pair * 128:dd * 512 + pair * 128 + 128],
                         aup=lambda dd, pair=pair: self.aup[:, dd * 512 + pair * 128:dd * 512 + pair * 128 + 128],
                         st0=None, yout=yout, stout=stout, scoped=False)
                self.rwkv_job(J)
        fw.pop()

    def phaseA_contrib(self, l):
        fw = self.fw
        base = l * BLK_PER_LAYER
        self.GA2 = fw.sb([128, 4, 512], BF16, "GA2s")
        fw.push()
        self.stream_begin(5)
        for which, (part, row0) in enumerate((("rk", 0), ("rk", 512), ("vx", VX_OFF["v"]))):
            slot = self.next_block(base + WIN_IDX[f"A{which}"])
            for c in range(4):
                self.contrib_rows(l, slot, 512, c * 128, 128, part, row0 + c * 128)
            self.prefetch_next()
        slot = self.next_block(base + WIN_IDX["A3"])
        for i in range(3):
            self.contrib_rows(l, slot, 192, i * 64, 64, "vx", VX_OFF["lora"] + i * 64)
        self.prefetch_next()
        slot = self.next_block(base + WIN_IDX["A4"])
        for c in range(4):
            ps = self.zmm(slot, 512, c * 128, 128, 0, 512)
            self.silu2(ps, 128, 512, self.GA2[:, c, :], self.GA2)
        self.prefetch_next()
        fw.pop()

    def gather_rows(self, dst, dst_ap, src, idx_col):
        fw = self.fw
        idx = self.idx1 if idx_col < 12 else self.idx2
        col = idx_col if idx_col < 12 else idx_col - 12
        fw.dma("pool", dst, None, src, None, extra_reads=[idx],
               fn=lambda: self.G.indirect_dma_start(out=dst_ap, out_offset=None, in_=src.h.ap(),
                                                    in_offset=bass.IndirectOffsetOnAxis(ap=idx[:, col:col + 1], axis=0)))

    def phaseA_consume(self, l):
        fw = self.fw
        V, A, G = self.V, self.A, self.G
        fw.push()
        self.load_rwkv_w(l, True)
        spo = fw.sb([128, 12], F32, "spo")
        self.load(spo, spo[:, :], self.d["spo"], self.d["spo"][l, :, :])
        spoh = fw.sb([128, 4], F32, "spoh")
        self.dve(lambda: V.tensor_scalar(spoh[:, :], spo[:, 0:4], 0.5, None, ALU.mult), [spo], [spoh])
        mu1o = fw.sb([128, 3], F32, "mu1o")
        muho = fw.sb([128, 3], F32, "muho")
        self.dve(lambda: V.tensor_scalar(mu1o[:, :], spo[:, 9:12], -1.0, 1.0, ALU.mult, ALU.add), [spo], [mu1o])
        self.dve(lambda: V.tensor_scalar(muho[:, :], spo[:, 9:12], 0.5, None, ALU.mult), [spo], [muho])
        T = DSEQ
        zz = [fw.sb([128, T], BF16, nm) for nm in ("sr", "sk", "sv")]
        lo = [fw.sb([64, T], BF16, nm) for nm in ("stwf", "stwb", "sad")]
        fw.push()
        raw = fw.sb([128, T], BF16, "sraw")
        o32 = fw.sb([128, T], F32, "so32")

        def shift_full(rows, src, m1, mh, dst, tanh=False):
            self.dve(lambda: V.tensor_scalar(o32[:rows, :], src[:rows, :], m1, None, ALU.mult), [src, mu1o, self.mu1], [o32])
            self.dve(lambda: V.scalar_tensor_tensor(o32[:rows, 1:T], src[:rows, 0:T - 1], mh, o32[:rows, 1:T], ALU.mult, ALU.add),
                     [src, muho, self.muh, o32], [o32])
            self.dve(lambda: V.scalar_tensor_tensor(o32[:rows, 0:T - 1], src[:rows, 1:T], mh, o32[:rows, 0:T - 1], ALU.mult, ALU.add),
                     [src, muho, self.muh, o32], [o32])
            if tanh:
                self.act(lambda: A.activation(dst[:rows, :], o32[:rows, :], AF.Tanh), [o32], [dst])
            else:
                self.act(lambda: A.copy(dst[:rows, :], o32[:rows, :]), [o32], [dst])

        for which in range(3):
            src = self.ag1_out[l]["rk" if which < 2 else "vx"]
            for q in range(4):
                self.gather_rows(raw, raw[:, q * 512:(q + 1) * 512], src, which * 4 + q)
            shift_full(128, raw, mu1o[:, which:which + 1], muho[:, which:which + 1], zz[which])
        agv = self.ag1_out[l]["vx"]
        for i in range(3):
            for q in range(4):
                r0 = q * 864 + VX_OFF["lora"] + i * 64
                fw.dma("sp", raw, raw[0:64, q * 512:(q + 1) * 512], agv, agv[r0:r0 + 64, :])
            shift_full(64, raw, self.mu1[0:64, 12 + i:13 + i], self.muh[0:64, 12 + i:13 + i], lo[i], tanh=(i < 2))
        fw.pop()
        stg = [None]

        def yout(c, yfin):
            q, cc = c // 4, c % 4
            if cc == 0:
                stg[0] = fw.rot([128, 512], BF16, "ystg", n=2)
            st = stg[0]
            self.act(lambda: A.copy(st[:, cc * 128:(cc + 1) * 128], yfin[:, :]), [yfin], [st])
            if cc == 3:
                ag = self.ag2_in[l]
                fw.dma("sp", ag, ag[q * 128:(q + 1) * 128, :], st, st[:, :])

        pidx = {nm: i for i, nm in enumerate(RW_NAMES)}
        J = dict(T=T, r=(zz[0], lambda a, b: zz[0][:, a:b]), k=(zz[1], lambda a, b: zz[1][:, a:b]), v=(zz[2], lambda a, b: zz[2][:, a:b]),
                 twd=[(lo[0], lambda a, b: lo[0][:, a:b]), (lo[1], lambda a, b: lo[1][:, a:b])],
                 ad=(lo[2], lambda a, b: lo[2][:, a:b]),
                 par=lambda nm: spo[:, pidx[nm]:pidx[nm] + 1],
                 parh=lambda nm: spoh[:, pidx[nm]:pidx[nm] + 1],
                 parbufs=[spo, spoh],
                 wup=lambda dd: self.wup[:, dd * 128:(dd + 1) * 128],
                 aup=lambda dd: self.aup[:, dd * 128:(dd + 1) * 128],
                 st0=lambda dd: (self.d["st0"], self.d["st0"][l, dd, :, :]), yout=yout, stout=None, seg=256, segpar=False)
        self.rwkv_job(J)
        self.allgather(self.ag2_out[l], self.ag2_in[l])
        fw.pop()

    def phaseA_final(self, l):
        fw = self.fw
        fw.push()
        for r in range(4):
            ya = fw.rot([128, 512], BF16, "ya", n=2)
            self.gather_rows(ya, ya[:, :], self.ag2_out[l], 12 + r)
            self.dve(lambda r=r, ya=ya: self.V.scalar_tensor_tensor(self.oTs[0][0][:, r, 0:512], ya[:, :], 0.5, self.GA2[:, r, :], ALU.mult, ALU.mult),
                     [ya, self.GA2], [self.oTs[0][0]])
        fw.pop()

    def rwkv_job(self, J):
        fw = self.fw
        V, A, G, T_ = self.V, self.A, self.G, self.T
        T = J["T"]
        nch = T // 128
        SEG = J.get("seg", 256)
        nseg = T // SEG
        ncs = SEG // 128
        rB, rf = J["r"]
        kB, kf = J["k"]
        vB, vf = J["v"]
        adB, adf = J["ad"]
        par, parh, pbufs = J["par"], J["parh"], J["parbufs"]
        self.ps_range = (0, 6)
        scoped = J.get("scoped", True)
        jpush = (lambda: fw.push()) if scoped else (lambda: None)
        jpop = (lambda: fw.pop()) if scoped else (lambda: None)
        jt = (lambda shp, dt, nm: fw.sb(shp, dt, nm)) if scoped else (lambda shp, dt, nm: fw.rot(shp, dt, "J" + nm, n=1))
        jpush()
        kap = jt([128, T], BF16, "kap")
        Vtm = jt([128, nch, 128], BF16, "Vtm")
        Yacc = jt([128, nch, 128], F32, "Yacc")
        Bacc = jt([128, T], F32, "Bacc")
        ST = [jt([128, 64], F32, f"ST{dd}") for dd in range(2)]
        STb = [jt([128, 64], BF16, f"STb{dd}") for dd in range(2)]
        KW = min(T, 512)
        self.pool(lambda: G.memset(Yacc[:, :, :], 0.0), [], [Yacc])
        self.pool(lambda: G.memset(Bacc[:, :], 0.0), [], [Bacc])
        jpush()
        for p0 in range(0, T, 512):
            N = min(512, T - p0)
            kk = fw.rot([128, KW], F32, "kk", n=(2 if scoped else 1))
            sq = fw.rot([128, KW], BF16, "kksq", n=(2 if scoped else 1))
            self.dve(lambda: V.tensor_scalar(kk[:, :N], kf(p0, p0 + N), par("k_k"), None, ALU.mult), [kB] + pbufs, [kk])
            self.act(lambda: A.activation(sq[:, :N], kk[:, :N], AF.Square), [kk], [sq])
            ps = self.nps()
            self.pe(lambda: T_.matmul(ps[:, :N], self.bones[:, :], sq[:, :N], start=True, stop=True), [self.bones, sq], [ps])
            t = fw.rot([128, KW], F32, "kkt", n=(2 if scoped else 1))
            self.dve(lambda: V.tensor_scalar(t[:, :N], ps[:, :N], 1e-24, None, ALU.max), [ps], [t])
            self.rsqrt(t, t[:, :N], t, t[:, :N])
            self.dve(lambda: V.tensor_tensor(kap[:, p0:p0 + N], kk[:, :N], t[:, :N], ALU.mult), [kk, t], [kap])
        import os
        STOP = int(os.environ.get("RWKV_STOP", "99"))
        if STOP <= 1:
            jpop(); jpop(); return
        for c in range(nch):
            self.transpose_to(vB, vf(c * 128, c * 128 + 128), Vtm, Vtm[:, c, :], eng=("act" if c % 2 else "dve"))
        jpop()
        if STOP <= 2:
            jpop(); return
        jpush()
        for dd in range(2):
            if J["st0"] is None:
                self.pool(lambda dd=dd: G.memset(ST[dd][:, :], 0.0), [], [ST[dd]])
            else:
                src, sap = J["st0"](dd)
                fw.dma("sp", ST[dd], ST[dd][:, :], src, sap)
            self.act(lambda dd=dd: A.copy(STb[dd][:, :], ST[dd][:, :]), [ST[dd]], [STb[dd]])
        MK = self.mask
        def rw_segment(dd, sg, res, segpar):
            sfx = "fb"[dd]
            twB, twf = J["twd"][dd]
            s0 = sg * SEG
            N = SEG
            f32t = lambda nm: fw.rot([128, SEG], F32, nm + (str(dd) if segpar else ""), n=1)
            a = f32t("ra")
            ps = self.nps()
            self.pe(lambda: T_.matmul(ps[:, :N], J["aup"](dd), adf(s0, s0 + N), start=True, stop=True), [self.aup, adB], [ps])
            self.act(lambda: A.activation(a[:, :], ps[:, :N], AF.Tanh, bias=parh("a0_" + sfx), scale=0.5), [ps] + pbufs, [a])
            yield
            self.dve(lambda: V.tensor_scalar(a[:, :], a[:, :], 0.5, 0.5, ALU.mult, ALU.add), [a], [a])
            kt = f32t("rkt")
            self.dve(lambda: V.tensor_scalar(kt[:, :], a[:, :], 1.0, par("k_a"), ALU.subtract, ALU.mult), [a] + pbufs, [kt])
            self.dve(lambda: V.scalar_tensor_tensor(kt[:, :], kt[:, :], 1.0, kf(s0, s0 + N), ALU.add, ALU.mult), [kt, kB], [kt])
            b = f32t("rb")
            self.dve(lambda: V.tensor_tensor(b[:, :], a[:, :], kap[:, s0:s0 + N], ALU.mult), [a, kap], [b])
            lw = f32t("rlw")
            ps = self.nps()
            self.pe(lambda: T_.matmul(ps[:, :N], J["wup"](dd), twf(s0, s0 + N), start=True, stop=True), [self.wup, twB], [ps])
            self.act(lambda: A.activation(lw[:, :], ps[:, :N], AF.Tanh, bias=parh("w0_" + sfx), scale=0.5), [ps] + pbufs, [lw])
            yield
            self.dve(lambda: V.tensor_scalar(lw[:, :], lw[:, :], -0.3032653298563167, -0.3032653298563167, ALU.mult, ALU.add), [lw], [lw])
            rkr = fw.rot([128, SEG], BF16, "rkr" + str(dd), n=1)
            self.dve(lambda: V.scalar_tensor_tensor(rkr[:, :], kt[:, :], par("r_k"), rf(s0, s0 + N), ALU.mult, ALU.mult),
                     [kt, rB] + pbufs, [rkr])
            ps = self.nps()
            self.pe(lambda: T_.matmul(ps[:, :N], self.bones[:, :], rkr[:, :], start=True, stop=True), [self.bones, rkr], [ps])
            self.dve(lambda: V.tensor_tensor(Bacc[:, s0:s0 + N], Bacc[:, s0:s0 + N], ps[:, :N], ALU.add), [Bacc, ps], [Bacc])
            P = f32t("rP")
            for c in range(ncs):
                self.dve(lambda c=c: V.tensor_tensor_scan(P[:, c * 128:(c + 1) * 128], self.onesf[:, :], lw[:, c * 128:(c + 1) * 128],
                                                          0.0, ALU.mult, ALU.add), [self.onesf, lw], [P])
            Q = f32t("rQ")
            R = f32t("rR")
            self.dve(lambda: V.tensor_tensor(Q[:, :], P[:, :], lw[:, :], ALU.subtract), [P, lw], [Q])
            for c in range(ncs):
                self.dve(lambda c=c: V.tensor_scalar(R[:, c * 128:(c + 1) * 128], P[:, c * 128:(c + 1) * 128], -1.0,
                                                     P[:, c * 128 + 127:c * 128 + 128], ALU.mult, ALU.add), [P], [R])
            gL = fw.rot([128, 2], F32, "gL" + str(dd), n=2)
            self.act(lambda: A.activation(gL[:, 0:ncs], P[:, 127:SEG:128], AF.Exp), [P], [gL])
            yield
            if dd == 0:
                srcs = [(P, -1.0, a), (Q, 1.0, Q), (P, 1.0, P), (R, 1.0, R)]
            else:
                RL = f32t("rRL")
                self.dve(lambda: V.tensor_tensor(RL[:, :], R[:, :], lw[:, :], ALU.add), [R, lw], [RL])
                srcs = [(RL, -1.0, a), (R, 1.0, R), (RL, 1.0, RL), (Q, 1.0, Q)]
            E = [None] * 4
            for i, (sb_, sc_, dst_) in enumerate(srcs):
                self.act(lambda i=i, sb_=sb_, sc_=sc_, dst_=dst_: A.activation(dst_[:, :], sb_[:, :], AF.Exp, scale=sc_), [sb_], [dst_])
                E[i] = dst_
            Kd2 = fw.rot([128, 2, SEG], BF16, "Kd2" + str(dd), n=1)
            Bd2 = fw.rot([128, 2, SEG], BF16, "Bd2" + str(dd), n=1)
            KL = fw.rot([128, SEG], BF16, "KL" + str(dd), n=1)
            BL = fw.rot([128, SEG], BF16, "BL" + str(dd), n=1)
            KqRq2 = fw.rot([128, 2, ncs, 2, 128], BF16, "KqRq2" + str(dd), n=1)
            hm = self.hm
            kap3 = kap[:, s0:s0 + N].rearrange("p (c t) -> p c t", c=ncs)
            r3 = rf(s0, s0 + N).rearrange("p (c t) -> p c t", c=ncs)
            for h in range(2):
                self.dve(lambda h=h: V.scalar_tensor_tensor(Kd2[:, h, :], kt[:, :], hm[:, h:h + 1], E[0][:, :], ALU.mult, ALU.mult), [kt, hm, E[0]], [Kd2])
                self.dve(lambda h=h: V.scalar_tensor_tensor(Bd2[:, h, :], b[:, :], hm[:, h:h + 1], E[0][:, :], ALU.mult, ALU.mult), [b, hm, E[0]], [Bd2])
                self.dve(lambda h=h: V.scalar_tensor_tensor(KqRq2[:, h, :, 0, :], kap3, hm[:, h:h + 1], E[1][:, :].rearrange("p (c t) -> p c t", c=ncs),
                                                            ALU.mult, ALU.mult), [kap, hm, E[1]], [KqRq2])
                self.dve(lambda h=h: V.scalar_tensor_tensor(KqRq2[:, h, :, 1, :], r3, hm[:, h:h + 1], E[2][:, :].rearrange("p (c t) -> p c t", c=ncs),
                                                            ALU.mult, ALU.mult), [rB, hm, E[2]], [KqRq2])
            self.dve(lambda: V.tensor_tensor(KL[:, :], kt[:, :], E[3][:, :], ALU.mult), [kt, E[3]], [KL])
            self.dve(lambda: V.tensor_tensor(BL[:, :], b[:, :], E[3][:, :], ALU.mult), [b, E[3]], [BL])

            res.update(dict(Kd2=Kd2, Bd2=Bd2, KL=KL, BL=BL, KqRq2=KqRq2, gL=gL, sg=sg))
            yield

        def rw_pre(dd, c, Pd, res):
            sfx2 = f"{dd}{c}"
            mk0 = dd * 1280
            Kd2, Bd2, KqRq2 = Pd["Kd2"], Pd["Bd2"], Pd["KqRq2"]
            cs = slice(c * 128, (c + 1) * 128)
            Am = [fw.rot([128, 512], BF16, f"Am{h}_{sfx2}", n=1) for h in range(2)]
            psB = self.nps()
            for h in range(2):
                psA = self.nps()
                rhsA = KqRq2[:, h, c, :, :].rearrange("p a t -> p (a t)")
                self.pe(lambda h=h, psA=psA, rhsA=rhsA: T_.matmul(psA[:, 0:256], Kd2[:, h, cs], rhsA, start=True, stop=True),
                        [Kd2, KqRq2], [psA], signal=False)
                self.pe(lambda h=h, psA=psA, rhsA=rhsA: T_.matmul(psA[:, 256:512], Bd2[:, h, cs], rhsA, start=True, stop=True),
                        [Bd2, KqRq2], [psA])
                self.dve(lambda h=h, psA=psA: V.tensor_tensor(Am[h][:, :], psA[:, :], MK[:, mk0:mk0 + 512], ALU.mult), [psA, MK], [Am[h]])
                self.pe(lambda h=h: T_.matmul(psB[:, h * 128:(h + 1) * 128], KqRq2[:, h, c, 0, :], Bd2[:, h, cs], start=True, stop=True),
                        [KqRq2, Bd2], [psB], signal=(h == 1))
            PT = [fw.rot([128, 2, 128], BF16, f"PT{i}_{sfx2}", n=1) for i in range(2)]
            PX = [fw.rot([128, 2, 256], BF16, f"PX{i}_{sfx2}", n=1) for i in range(2)]
            C64T = fw.rot([128, 2, 128], BF16, "C64T" + sfx2, n=1)
            C128T = fw.rot([128, 2, 128], BF16, "C128T" + sfx2, n=1)
            f2 = lambda t: t[:, :, :].rearrange("p a t -> p (a t)")
            self.dve(lambda: V.tensor_tensor(f2(PT[0]), psB[:, 0:256], MK[:, mk0 + 512:mk0 + 768], ALU.mult), [psB, MK], [PT[0]])
            self.dve(lambda: V.tensor_tensor(f2(C64T), psB[:, 0:256], MK[:, mk0 + 768:mk0 + 1024], ALU.mult), [psB, MK], [C64T])
            self.dve(lambda: V.tensor_tensor(f2(C128T), psB[:, 0:256], MK[:, mk0 + 1024:mk0 + 1280], ALU.mult), [psB, MK], [C128T])
            for h in range(2):
                self.act(lambda h=h: A.copy(PX[0][:, h, 0:128], Am[h][:, 256:384]), [Am[h]], [PX[0]])
            yield
            cur = 0
            Xb = fw.rot([128, 2, 128], BF16, "Xb32" + sfx2, n=1)
            for j in range(1, 6):
                nxt = 1 - cur
                if j == 1:
                    ps = self.nps()
                    pst_ = self.nps()
                    for h in range(2):
                        self.pe(lambda h=h, ps=ps, cur=cur: T_.matmul(ps[:, h * 256:h * 256 + 128], PT[cur][:, h, :], PX[cur][:, h, 0:128],
                                                                      start=True, stop=True), [PT[cur], PX[cur]], [ps], signal=(h == 1))
                        self.pe(lambda h=h, pst_=pst_, cur=cur: T_.matmul(pst_[:, h * 128:(h + 1) * 128], PX[cur][:, h, 0:128], PT[cur][:, h, :],
                                                                          start=True, stop=True), [PT[cur], PX[cur]], [pst_], signal=(h == 1))
                    for h in range(2):
                        self.dve(lambda h=h, cur=cur, nxt=nxt: V.tensor_tensor(PX[nxt][:, h, 128:256], PX[cur][:, h, 0:128], self.ident[:, :], ALU.add),
                                 [PX[cur], self.ident], [PX[nxt]])
                    self.act(lambda ps=ps, nxt=nxt: A.copy(PX[nxt][:, :, 0:128], ps[:, :].rearrange("p (a t) -> p a t", a=2)[:, :, 0:128]),
                             [ps], [PX[nxt]])
                    self.act(lambda pst_=pst_, nxt=nxt: A.copy(f2(PT[nxt]), pst_[:, 0:256]), [pst_], [PT[nxt]])
                elif j < 5:
                    ps = self.nps()
                    pst_ = self.nps()
                    for h in range(2):
                        self.pe(lambda h=h, ps=ps, cur=cur: T_.matmul(ps[:, h * 256:(h + 1) * 256], PT[cur][:, h, :], PX[cur][:, h, :],
                                                                      start=True, stop=True), [PT[cur], PX[cur]], [ps], signal=(h == 1))
                        self.pe(lambda h=h, pst_=pst_, cur=cur: T_.matmul(pst_[:, h * 128:(h + 1) * 128], PX[cur][:, h, 0:128], PT[cur][:, h, :],
                                                                          start=True, stop=True), [PT[cur], PX[cur]], [pst_], signal=(h == 1))
                    ps3 = ps[:, :].rearrange("p (a t) -> p a t", a=2)
                    self.act(lambda ps3=ps3, ps=ps, nxt=nxt: A.copy(PX[nxt][:, :, 0:128], ps3[:, :, 0:128]), [ps], [PX[nxt]])
                    self.dve(lambda ps3=ps3, ps=ps, cur=cur, nxt=nxt: V.tensor_tensor(PX[nxt][:, :, 128:256], ps3[:, :, 128:256], PX[cur][:, :, 128:256], ALU.add),
                             [ps, PX[cur]], [PX[nxt]])
                    self.act(lambda pst_=pst_, nxt=nxt: A.copy(f2(PT[nxt]), pst_[:, 0:256]), [pst_], [PT[nxt]])
                else:
                    ps = self.nps()
                    for h in range(2):
                        self.pe(lambda h=h, ps=ps, cur=cur: T_.matmul(ps[:, h * 128:(h + 1) * 128], PT[cur][:, h, :], PX[cur][:, h, 128:256],
                                                                      start=True, stop=True), [PT[cur], PX[cur]], [ps], signal=(h == 1))
                    self.dve(lambda ps=ps, cur=cur: V.tensor_tensor(Xb[:, :, :], ps[:, 0:256].rearrange("p (a t) -> p a t", a=2), PX[cur][:, :, 128:256], ALU.add),
                             [ps, PX[cur]], [Xb])
                cur = nxt
                yield
            TT = None
            for lvl, CT in enumerate((C64T, C128T)):
                XT = fw.rot([128, 2, 128], BF16, "XTm" + sfx2, n=1)
                Zt = fw.rot([128, 2, 128], BF16, "Ztm" + sfx2, n=1)
                ptt = self.pst[self.pst_rr % 2]
                self.pst_rr += 1
                for h in range(2):
                    self.pe(lambda h=h, ptt=ptt, Xb=Xb: T_.transpose(ptt[:, h * 128:(h + 1) * 128], Xb[:, h, :], self.ident[:, :]),
                            [Xb, self.ident], [ptt], signal=(h == 1))
                self.act(lambda ptt=ptt, XT=XT: A.copy(f2(XT), ptt[:, 0:256]), [ptt], [XT])
                psz = self.nps()
                for h in range(2):
                    self.pe(lambda h=h, psz=psz, CT=CT, Xb=Xb: T_.matmul(psz[:, h * 128:(h + 1) * 128], CT[:, h, :], Xb[:, h, :], start=True, stop=True),
                            [CT, Xb], [psz], signal=(h == 1))
                self.dve(lambda psz=psz, Zt=Zt: V.tensor_copy(f2(Zt), psz[:, 0:256]), [psz], [Zt])
                psw = self.nps()
                for h in range(2):
                    self.pe(lambda h=h, psw=psw, XT=XT, Zt=Zt: T_.matmul(psw[:, h * 128:(h + 1) * 128], XT[:, h, :], Zt[:, h, :], start=True, stop=True),
                            [XT, Zt], [psw], signal=(h == 1))
                Xn = fw.rot([128, 2, 128], BF16, ("Xb64" if lvl == 0 else "TT") + sfx2, n=1)
                self.dve(lambda psw=psw, Xn=Xn, Xb=Xb: V.tensor_tensor(f2(Xn), psw[:, 0:256], f2(Xb), ALU.add), [psw, Xb], [Xn])
                Xb = Xn
                yield
            TT = Xb

            res["Am"] = Am
            res["TT"] = TT
            yield

        def rw_seq(dd, Pd, pres):
            KL, BL, KqRq2, gL, sg = Pd["KL"], Pd["BL"], Pd["KqRq2"], Pd["gL"], Pd["sg"]
            chunks = list(range(ncs)) if dd == 0 else list(reversed(range(ncs)))
            for c in chunks:
                cg = sg * ncs + c
                cs = slice(c * 128, (c + 1) * 128)
                Am, TT = pres[(dd, c)]["Am"], pres[(dd, c)]["TT"]
                Sb = STb[dd]
                psG = self.nps()
                for h in range(2):
                    hs = slice(64 * h, 64 * h + 64)
                    vs = slice(64 * h, 64 * h + 64)
                    self.pe(lambda h=h, vs=vs: T_.matmul(psG[:, vs], KqRq2[:, h, c, 0, :], Sb[:, :], start=(h == 0), stop=False, skip_group_check=True),
                            [KqRq2, Sb], [psG], signal=False)
                    self.pe(lambda h=h, vs=vs: T_.matmul(psG[:, vs], Am[h][:, 0:128], Vtm[:, cg, vs], start=False, stop=(h == 1), skip_group_check=True),
                            [Am[h], Vtm], [psG], signal=(h == 1))
                Gn = fw.rot([128, 128], BF16, "Gn" + str(dd), n=1)
                self.act(lambda: A.mul(Gn[:, :], psG[:, 0:128], -1.0), [psG], [Gn])
                yield
                psU = self.nps()
                for h in range(2):
                    vs = slice(64 * h, 64 * h + 64)
                    self.pe(lambda h=h, vs=vs: T_.matmul(psU[:, vs], TT[:, h, :], Gn[:, vs], start=(h == 0), stop=(h == 1), skip_group_check=True),
                            [TT, Gn], [psU], signal=(h == 1))
                U = fw.rot([128, 128], BF16, "U" + str(dd), n=1)
                self.dve(lambda: V.tensor_copy(U[:, :], psU[:, 0:128]), [psU], [U])
                yield
                yield
                psY = self.nps()
                for h in range(2):
                    hs = slice(64 * h, 64 * h + 64)
                    vs = slice(64 * h, 64 * h + 64)
                    self.pe(lambda h=h, vs=vs: T_.matmul(psY[:, vs], KqRq2[:, h, c, 1, :], Sb[:, :], start=(h == 0), stop=False, skip_group_check=True),
                            [KqRq2, Sb], [psY], signal=False)
                    self.pe(lambda h=h, vs=vs: T_.matmul(psY[:, vs], Am[h][:, 128:256], Vtm[:, cg, vs], start=False, stop=False, skip_group_check=True),
                            [Am[h], Vtm], [psY], signal=False)
                    self.pe(lambda h=h, vs=vs: T_.matmul(psY[:, vs], Am[h][:, 384:512], U[:, vs], start=False, stop=(h == 1), skip_group_check=True),
                            [Am[h], U], [psY], signal=(h == 1))
                self.dve(lambda: V.tensor_tensor(Yacc[:, cg, :], Yacc[:, cg, :], psY[:, 0:128], ALU.add), [Yacc, psY], [Yacc])
                yield
                yield
                KLt = fw.rot([128, 128], BF16, "KLt" + str(dd), n=1)
                BLt = fw.rot([128, 128], BF16, "BLt" + str(dd), n=1)
                self.transpose_to(KL, KL[:, cs], KLt, KLt[:, :], eng="act")
                self.transpose_to(BL, BL[:, cs], BLt, BLt[:, :], eng="dve")
                psS = self.nps()
                for h in range(2):
                    hs = slice(64 * h, 64 * h + 64)
                    vs = slice(64 * h, 64 * h + 64)
                    self.pe(lambda h=h, hs=hs, vs=vs: T_.matmul(psS[hs, 0:64], KLt[:, hs], Vtm[:, cg, vs], start=True, stop=False),
                            [KLt, Vtm], [psS], signal=False)
                    self.pe(lambda h=h, hs=hs, vs=vs: T_.matmul(psS[hs, 0:64], BLt[:, hs], U[:, vs], start=False, stop=True),
                            [BLt, U], [psS], signal=(h == 1))
                self.dve(lambda: V.scalar_tensor_tensor(ST[dd][:, :], ST[dd][:, :], gL[:, c:c + 1], psS[:, 0:64], ALU.mult, ALU.add),
                         [ST[dd], gL, psS], [ST[dd]])
                self.act(lambda: A.copy(STb[dd][:, :], ST[dd][:, :]), [ST[dd]], [STb[dd]])

        def run_rr(gens):
            gens = list(gens)
            while gens:
                for g in list(gens):
                    try:
                        next(g)
                    except StopIteration:
                        gens.remove(g)

        for step in range(nseg):
            sgs = (step, nseg - 1 - step)
            Pd = [{}, {}]
            segpar = J.get("segpar", False)
            if segpar:
                run_rr([rw_segment(dd, sgs[dd], Pd[dd], True) for dd in range(2)])
            else:
                for dd in range(2):
                    for _ in rw_segment(dd, sgs[dd], Pd[dd], False):
                        pass
            if STOP <= 3:
                continue
            pres = {(dd, c): {} for dd in range(2) for c in range(ncs)}
            run_rr([rw_pre(dd, c, Pd[dd], pres[(dd, c)]) for dd in range(2) for c in range(ncs)])
            if STOP <= 5:
                continue
            run_rr([rw_seq(dd, Pd[dd], pres) for dd in range(2)])
        if J["stout"] is not None:
            for dd in range(2):
                J["stout"](dd, ST[dd])
        jpop()
        if STOP <= 6:
            jpop(); return
        n2 = nch * 2
        sums = jt([128, n2], F32, "gsum")
        ssq = jt([128, n2], F32, "gssq")
        Ysq = jt([128, nch * 128], F32, "Ysq") if scoped else fw.rot([128, KW], F32, "kk", n=1)
        Yf = Yacc[:, :, :].rearrange("p c x -> p (c x)")
        self.dve(lambda: V.tensor_reduce(sums[:, :], Yf.rearrange("p (g x) -> p g x", x=64), AX.X, ALU.add), [Yacc], [sums])
        self.act(lambda: A.activation(Ysq[:, :], Yf, AF.Square), [Yacc], [Ysq])
        self.dve(lambda: V.tensor_reduce(ssq[:, :], Ysq[:, :].rearrange("p (g x) -> p g x", x=64), AX.X, ALU.add), [Ysq], [ssq])
        mean = jt([128, n2], F32, "gmean")
        var = jt([128, n2], F32, "gvar2")
        self.dve(lambda: V.tensor_scalar(mean[:, :], sums[:, :], 1.0 / 64, None, ALU.mult), [sums], [mean])
        self.dve(lambda: V.tensor_tensor(var[:, :], mean[:, :], mean[:, :], ALU.mult), [mean], [var])
        self.dve(lambda: V.scalar_tensor_tensor(var[:, :], ssq[:, :], 1.0 / 64, var[:, :], ALU.mult, ALU.subtract), [ssq, var], [var])
        self.dve(lambda: V.tensor_scalar(var[:, :], var[:, :], GN_EPS, None, ALU.add), [var], [var])
        self.rsqrt(var, var[:, :], var, var[:, :])
        yn = jt([128, nch, 128], BF16, "yn")
        for c in range(nch):
            for h in range(2):
                g = c * 2 + h
                self.dve(lambda c=c, h=h, g=g: V.tensor_scalar(yn[:, c, h * 64:(h + 1) * 64], Yacc[:, c, h * 64:(h + 1) * 64], mean[:, g:g + 1], var[:, g:g + 1],
                                                               ALU.subtract, ALU.mult), [Yacc, mean, var], [yn])
        for c in range(nch):
            pt = self.pst[self.pst_rr % 2]
            self.pst_rr += 1
            self.pe(lambda c=c, pt=pt: T_.transpose(pt[:, :128], yn[:, c, :], self.ident[:, :]), [yn, self.ident], [pt])
            yT = fw.rot([128, 128], F32, "yT", n=(2 if scoped else 1))
            self.act(lambda pt=pt, yT=yT: A.activation(yT[:, :], pt[:, :128], AF.Identity, bias=par("ln_b"), scale=par("ln_g")), [pt] + pbufs, [yT])
            bo = fw.rot([128, 128], F32, "bo", n=(2 if scoped else 1))
            self.dve(lambda c=c, bo=bo: V.tensor_tensor(bo[:, :], Bacc[:, c * 128:(c + 1) * 128], vf(c * 128, (c + 1) * 128), ALU.mult), [Bacc, vB], [bo])
            yfin = fw.rot([128, 128], F32, "yfin", n=(2 if scoped else 1))
            self.dve(lambda yT=yT, bo=bo, yfin=yfin: V.tensor_tensor(yfin[:, :], yT[:, :], bo[:, :], ALU.add), [yT, bo], [yfin])
            J["yout"](c, yfin)
        jpop()
        self.ps_range = (0, 4)

    def load_mla_w(self, l):
        fw, d = self.fw, self.d
        self.wq = fw.sb([128, 2 * 8 * 96], BF16, "wq")
        self.wqs = fw.sb([128, 2 * 8 * 32], BF16, "wqs")
        self.wkk = fw.sb([128, 512], BF16, "wkk")
        self.wkv = fw.sb([128, 512], BF16, "wkv")
        for nm in ("wq", "wqs", "wkk", "wkv"):
            t = getattr(self, nm)
            self.load(t, t[:], d[nm], d[nm][l, :, :])

    def mla_front(self, l, h0, rope, GB2, Qh, ckv_f, ckv_b, kr_f, kr_b):
        fw = self.fw
        base = l * BLK_PER_LAYER
        W = WIN_W["B0"]
        slot = self.next_block(base + WIN_IDX["B0"])
        qd = fw.sb([128, 2, 512], F32, "qd")
        kvd = fw.sb([128, 512], F32, "kvd")
        for c in range(2):
            ps = self.zmm(slot, W, c * 128, 128, h0, 512)
            self.act(lambda c=c, ps=ps: self.A.copy(qd[:, c, :], ps[:, :]), [ps], [qd])
        ps = self.zmm(slot, W, 256, 128, h0, 512)
        self.dve(lambda ps=ps: self.V.tensor_copy(kvd[:, :], ps[:, :]), [ps], [kvd])
        pk = self.zmm(slot, W, 384, 32, h0, 512, prow=64)
        R = self.rope
        if rope:
            pks = self.zmm(slot, W, 416, 32, h0, 512, prow=64)
            t1 = fw.sb([96, 512], F32, "krt1")
            self.dve(lambda: self.V.tensor_tensor(t1[64:96, :], pk[64:96, :], R[64:96, 0:512], ALU.mult), [pk, R], [t1])
            self.dve(lambda: self.V.tensor_tensor(kr_f[64:96, :], pks[64:96, :], R[64:96, 512:1024], ALU.mult), [pks, R], [kr_f])
            self.dve(lambda: self.V.tensor_tensor(kr_f[64:96, :], kr_f[64:96, :], t1[64:96, :], ALU.add), [kr_f, t1], [kr_f])
        else:
            self.act(lambda: self.A.copy(kr_f[64:96, :], pk[64:96, :]), [pk], [kr_f])
        self.act(lambda: self.A.copy(kr_b[64:96, :], kr_f[64:96, :]), [kr_f], [kr_b])
        self.prefetch_next()
        slot = self.next_block(base + WIN_IDX["B1"])
        for c in range(4):
            ps = self.zmm(slot, 512, c * 128, 128, h0, 512)
            self.silu2(ps, 128, 512, GB2[:, c, :], GB2)
        self.prefetch_next()
        rq = fw.sb([128, 512], F32, "rq")
        self.rstd_tile(qd, [qd[:, c, :] for c in range(2)], 256.0, rq, 512)
        qn = fw.sb([128, 2, 512], BF16, "qn")
        for c in range(2):
            self.dve(lambda c=c: self.V.scalar_tensor_tensor(qn[:, c, :], qd[:, c, :], self.spc("qn", c), rq[:, :], ALU.mult, ALU.mult),
                     [qd, self.sp, rq], [qn])
        rk = fw.sb([128, 512], F32, "rkv")
        self.rstd_tile(kvd, [kvd[:, :]], 128.0, rk, 512)
        self.dve(lambda: self.V.scalar_tensor_tensor(ckv_f[:, :], kvd[:, :], self.spc("kvn", 0), rk[:, :], ALU.mult, ALU.mult),
                 [kvd, self.sp, rk], [ckv_f])
        self.act(lambda: self.A.copy(ckv_b[:, :], ckv_f[:, :]), [ckv_f], [ckv_b])
        for h in range(8):
            ps = self.nps()
            for c in range(2):
                self.pe(lambda c=c, h=h, ps=ps: self.T.matmul(ps[:96, :], self.wq[:, (c * 8 + h) * 96:(c * 8 + h) * 96 + 96], qn[:, c, :],
                                                             start=(c == 0), stop=(c == 1)), [self.wq, qn], [ps], signal=(c == 1))
            if rope:
                ps2 = self.nps()
                for c in range(2):
                    self.pe(lambda c=c, h=h, ps2=ps2: self.T.matmul(ps2[64:96, :], self.wqs[:, (c * 8 + h) * 32:(c * 8 + h) * 32 + 32], qn[:, c, :],
                                                                   start=(c == 0), stop=(c == 1)), [self.wqs, qn], [ps2], signal=(c == 1))
                t1 = fw.rot([96, 512], F32, "qrt1")
                t2 = fw.rot([96, 512], F32, "qrt2")
                self.dve(lambda ps=ps, t1=t1: self.V.tensor_tensor(t1[64:96, :], ps[64:96, :], R[64:96, 0:512], ALU.mult), [ps, R], [t1])
                self.dve(lambda ps2=ps2, t2=t2: self.V.tensor_tensor(t2[64:96, :], ps2[64:96, :], R[64:96, 512:1024], ALU.mult), [ps2, R], [t2])
                self.dve(lambda h=h, t1=t1, t2=t2: self.V.tensor_tensor(Qh[h][64:96, :], t1[64:96, :], t2[64:96, :], ALU.add), [t1, t2], [Qh[h]])
                self.act(lambda h=h, ps=ps: self.A.copy(Qh[h][0:64, :], ps[0:64, :]), [ps], [Qh[h]])
            else:
                self.act(lambda h=h, ps=ps: self.A.copy(Qh[h][:, :], ps[:96, :]), [ps], [Qh[h]])

    def mla_kv_chunk(self, ckv_b, kr_b, heads, Kh, Vaug, nk):
        for i, h in enumerate(heads):
            ps = self.nps()
            self.pe(lambda h=h, ps=ps: self.T.matmul(ps[:64, :nk], self.wkk[:, h * 64:(h + 1) * 64], ckv_b[:, :nk], start=True, stop=True),
                    [self.wkk, ckv_b], [ps])
            self.act(lambda i=i, ps=ps: self.A.copy(Kh[i][0:64, :nk], ps[0:64, :nk]), [ps], [Kh[i]])
            self.dve(lambda i=i: self.V.tensor_copy(Kh[i][64:96, :nk], kr_b[64:96, :nk]), [kr_b], [Kh[i]])
        for kb in range(nk // 128):
            ps = self.nps()
            self.pe(lambda kb=kb, ps=ps: self.T.matmul(ps[:, :], ckv_b[:, kb * 128:(kb + 1) * 128], self.wkv[:, :], start=True, stop=True),
                    [ckv_b, self.wkv], [ps])
            for h8 in range(8):
                pass
            self.dve(lambda kb=kb, ps=ps: self.V.tensor_copy(
                Vaug[:, kb * 520:(kb + 1) * 520].rearrange("p (h e) -> p h e", e=65)[:, :, 0:64],
                ps[:, :].rearrange("p (h e) -> p h e", e=64)), [ps], [Vaug])

    def attn_accum(self, Qh4, q0, nq, Kh4, Vaug, heads, k0, nkb, first, last, Oacc):
        fw = self.fw
        nqs = nq // 128
        its = [(kb, i, h) for kb in range(nkb) for i, h in enumerate(heads)]

        def score(kb, i, h):
            pss = self.psb[4 + (self.sc_rr % 2)]
            self.sc_rr += 1
            self.pe(lambda: self.T.matmul(pss[:, :nq], Kh4[i][:, k0 + kb * 128:k0 + kb * 128 + 128],
                                          Qh4[i][:, q0:q0 + nq], start=True, stop=True), [Kh4[i], Qh4[i]], [pss])
            PT = fw.rot([128, 512], BF16, "PT", n=3)
            self.act(lambda: self.A.activation(PT[:, :nq], pss[:, :nq], AF.Exp, scale=96.0 ** -0.5), [pss], [PT])
            return PT

        def pv(kb, i, h, PT):
            for qs in range(nqs):
                self.pe(lambda qs=qs: self.T.matmul(
                    Oacc[qs][:, i * 65:(i + 1) * 65], PT[:, qs * 128:(qs + 1) * 128],
                    Vaug[:, (k0 // 128 + kb) * 520 + h * 65:(k0 // 128 + kb) * 520 + h * 65 + 65],
                    start=(first and kb == 0 and i == 0), stop=(last and kb == nkb - 1 and i == 3), skip_group_check=True),
                    [PT, Vaug], [Oacc[qs]], signal=(qs == nqs - 1))

        pend = score(*its[0])
        for n in range(len(its)):
            nxt = score(*its[n + 1]) if n + 1 < len(its) else None
            pv(*its[n], pend)
            pend = nxt

    def attn_finish(self, Oacc, nqs, ob, hh):
        fw = self.fw
        for qs in range(nqs):
            rec = fw.rot([128, 4], F32, "rec", n=4)
            self.dve(lambda qs=qs, rec=rec: self.V.reciprocal(rec[:, :], Oacc[qs][:, 64:260:65]), [Oacc[qs]], [rec])
            for i in range(4):
                self.dve(lambda qs=qs, i=i, rec=rec: self.V.tensor_scalar(ob[qs][:, hh * 256 + i * 64:hh * 256 + i * 64 + 64],
                                                                          Oacc[qs][:, i * 65:i * 65 + 64], rec[:, i:i + 1], None, ALU.mult),
                         [Oacc[qs], rec], [ob[qs]])

    def attn_out(self, ob, nqs, GB2, g0, o0):
        for qs in range(nqs):
            for c in range(4):
                pt = self.pst[self.pst_rr % 2]
                self.pst_rr += 1
                self.pe(lambda qs=qs, c=c, pt=pt: self.T.transpose(pt[:, :128], ob[qs][:, c * 128:(c + 1) * 128], self.ident[:, :]),
                        [ob[qs], self.ident], [pt])
                self.dve(lambda qs=qs, c=c, pt=pt: self.V.scalar_tensor_tensor(
                    self.oTs[o0 // 512][1][:, c, o0 % 512 + qs * 128:o0 % 512 + qs * 128 + 128], pt[:, :128], 0.5, GB2[:, c, g0 + qs * 128:g0 + qs * 128 + 128],
                    ALU.mult, ALU.mult), [pt, GB2], [self.oTs[o0 // 512][1]])

    def phaseB_prompt(self, l, half):
        fw = self.fw
        h0 = half * 512
        fw.push()
        self.load_mla_w(l)
        GB2 = fw.sb([128, 4, 512], BF16, "GB2")
        Qh = [fw.sb([96, 512], BF16, f"Qh{h}") for h in range(8)]
        ckv_f = fw.sb([128, 512], F32, "ckvf")
        ckv_b = fw.sb([128, 512], BF16, "ckvb")
        kr_f = fw.sb([96, 512], F32, "krf")
        kr_b = fw.sb([96, 512], BF16, "krb")
        fw.push()
        self.stream_begin(2)
        self.mla_front(l, h0, False, GB2, Qh, ckv_f, ckv_b, kr_f, kr_b)
        fw.pop()
        fw.dma("sp", self.d["ckvout"], self.d["ckvout"][l, :, h0:h0 + 512], ckv_f, ckv_f[:, :], sem_owner=self.outsem)
        fw.dma("sp", self.d["krout"], self.d["krout"][l, :, h0:h0 + 512], kr_f, kr_f[64:96, :], sem_owner=self.outsem)
        Kh = [fw.sb([96, 512], BF16, f"Kh{h}") for h in range(8)]
        Vaug = fw.sb([128, 4 * 520], BF16, "Vaug")
        self.pool(lambda: self.G.memset(Vaug[:, :], 1.0), [], [Vaug])
        self.mla_kv_chunk(ckv_b, kr_b, list(range(8)), Kh, Vaug, 512)
        self.sc_rr = 0
        import os
        if "dumpB" in os.environ.get("KDBG", "") and l == 0 and half == 0:
            so = self.d["stout"]
            fw.dma("pool", so, so[0:768, :].rearrange("(p a) b -> p (a b)", p=96), Qh[0], Qh[0][:, :])
            fw.dma("pool", so, so[768:1536, :].rearrange("(p a) b -> p (a b)", p=96), Kh[0], Kh[0][:, :])
            fw.dma("pool", so, so[1536:5632, :].rearrange("(p a) b -> p (a b)", p=128), Vaug, Vaug[:, 0:2048])
        for sq in range(2):
            ob = [fw.rot([128, 512], BF16, "ob", n=4) for _ in range(2)]
            for hh in range(2):
                heads = list(range(hh * 4, hh * 4 + 4))
                Oacc = [self.psb[0 + 2 * (hh % 2)], self.psb[1 + 2 * (hh % 2)]]
                self.attn_accum([Qh[h] for h in heads], sq * 256, 256, [Kh[h] for h in heads], Vaug, heads, sq * 256, 2, True, True, Oacc)
                self.attn_finish(Oacc, 2, ob, hh)
            if "dumpB" in os.environ.get("KDBG", "") and l == 0 and half == 0 and sq == 0:
                so = self.d["stout"]
                fw.dma("pool", so, so[5696:6720, :].rearrange("(p a) b -> p (a b)", p=128), ob[0], ob[0][:, :])
            self.attn_out(ob, 2, GB2, sq * 256, h0 + sq * 256)
        fw.pop()

    def phaseB_contrib(self, l):
        fw = self.fw
        self.load_mla_w(l)
        self.GB2 = fw.sb([128, 4, 512], BF16, "GB2s")
        self.Qh = [fw.sb([96, 512], BF16, f"Qhs{h}") for h in range(8)]
        fw.push()
        self.rope = fw.sb([96, 1024], F32, "rope")
        self.load(self.rope, self.rope[64:96, :], self.d["rope"], self.d["rope"].ap())
        self.stream_begin(2)
        ckv_f = fw.sb([128, 512], F32, "ckvf")
        ckv_b = fw.sb([128, 512], BF16, "ckvb")
        kr_f = fw.sb([96, 512], F32, "krf")
        kr_b = fw.sb([96, 512], BF16, "krb")
        self.mla_front(l, 0, True, self.GB2, self.Qh, ckv_f, ckv_b, kr_f, kr_b)
        ag = self.ag1_in[l]["vx"]
        fw.dma("sp", ag, ag[VX_OFF["ckv"]:VX_OFF["ckv"] + 128, :], ckv_b, ckv_b[:, :])
        fw.dma("sp", ag, ag[VX_OFF["kr"]:VX_OFF["kr"] + 32, :], kr_b, kr_b[64:96, :])
        fw.pop()

    def phaseB_consume(self, l):
        fw = self.fw
        ago = self.ag1_out[l]["vx"]
        fw.push()
        self.sc_rr = 0
        self.ps_range = (4, 6)
        ob = [fw.sb([128, 512], BF16, f"obs{i}") for i in range(4)]
        for hh in range(2):
            heads = list(range(hh * 4, hh * 4 + 4))
            Oacc = self.psb[0:4]
            for ch in range(5):
                ckv_b = fw.rot([128, 512], BF16, "ckvg", n=2)
                kr_b = fw.rot([96, 512], BF16, "krg", n=2)
                if ch < 4:
                    fw.dma("sp", ckv_b, ckv_b[:, :], ago, ago[ch * 864 + VX_OFF["ckv"]:ch * 864 + VX_OFF["ckv"] + 128, :])
                    fw.dma("sp", kr_b, kr_b[64:96, :], ago, ago[ch * 864 + VX_OFF["kr"]:ch * 864 + VX_OFF["kr"] + 32, :])
                else:
                    fw.dma("pool", ckv_b, ckv_b[:, :], self.d["cckv"], self.d["cckv"][l, :, :])
                    fw.dma("pool", kr_b, kr_b[64:96, :], self.d["ckr"], self.d["ckr"][l, :, :])
                Kh = [fw.rot([96, 512], BF16, f"Khs{i}", n=2) for i in range(4)]
                Vaug = fw.rot([128, 4 * 520], BF16, "Vaugs", n=2)
                self.pool(lambda Vaug=Vaug: self.G.memset(Vaug[:, :], 1.0), [], [Vaug])
                self.mla_kv_chunk(ckv_b, kr_b, heads, Kh, Vaug, 512)
                self.attn_accum([self.Qh[h] for h in heads], 0, 512, Kh, Vaug, heads, 0, 4, ch == 0, ch == 4, Oacc)
            self.attn_finish(Oacc, 4, ob, hh)
        self.ps_range = (0, 4)
        self.attn_out(ob, 4, self.GB2, 0, 0)
        fw.pop()

    def fnet_stage1(self, fT_buf, fT_ap_fn, dftd, G1):
        for hb in range(2):
            ps = self.psb[4 + hb]
            for gg in range(2):
                g = hb * 2 + gg
                self.pe(lambda g=g, gg=gg, ps=ps: self.T.matmul(ps[:, gg * 256:(gg + 1) * 256], fT_ap_fn(g), dftd[:, :],
                                                                 start=True, stop=True), [fT_buf, dftd], [ps], signal=(gg == 1))
            if hb == 0:
                self.act(lambda ps=ps: self.A.copy(G1[:, 0:512], ps[:, :]), [ps], [G1])
            else:
                self.dve(lambda ps=ps: self.V.tensor_copy(G1[:, 512:1024], ps[:, :]), [ps], [G1])

    def phaseD_prompt(self, l, half):
        fw = self.fw
        base = l * BLK_PER_LAYER
        h0 = half * 512
        fw.push()
        self.dftd_p = fw.sb([128, 256], BF16, "dftd_p")
        self.dftT_p = fw.sb([128, 1024], BF16, "dftT_p")
        for nm in ("dftd_p", "dftT_p"):
            t = getattr(self, nm)
            self.load(t, t[:], self.d[nm], self.d[nm].ap())
        fT = fw.sb([128, 4, 512], BF16, "fT")
        GD2 = fw.sb([128, 4, 512], BF16, "GD2")
        fw.push()
        self.stream_begin(2)
        slot = self.next_block(base + WIN_IDX["D0"])
        for c in range(4):
            ps = self.zmm(slot, 512, c * 128, 128, h0, 512)
            if c % 2:
                self.act(lambda c=c, ps=ps: self.A.copy(fT[:, c, :], ps[:, :]), [ps], [fT])
            else:
                self.dve(lambda c=c, ps=ps: self.V.tensor_copy(fT[:, c, :], ps[:, :]), [ps], [fT])
        self.prefetch_next()
        slot = self.next_block(base + WIN_IDX["D1"])
        for c in range(4):
            ps = self.zmm(slot, 512, c * 128, 128, h0, 512)
            self.silu2(ps, 128, 512, GD2[:, c, :], GD2)
        self.prefetch_next()
        fw.pop()
        for sq in range(2):
            t0 = sq * 256
            G1 = [fw.rot([128, 1024], BF16, "G1", n=4) for _ in range(2)]
            for tt in range(2):
                self.fnet_stage1(fT, lambda g, tt=tt: fT[:, g, t0 + tt * 128:t0 + tt * 128 + 128], self.dftd_p, G1[tt])
            for g in range(4):
                ps = self.nps()
                i = 0
                for tt in range(2):
                    for cs in range(2):
                        self.pe(lambda g=g, tt=tt, cs=cs, ps=ps, i=i: self.T.matmul(
                            ps[:, :256], G1[tt][:, g * 256 + cs * 128:g * 256 + cs * 128 + 128],
                            self.dftT_p[:, (tt * 2 + cs) * 256:(tt * 2 + cs) * 256 + 256], start=(i == 0), stop=(i == 3)),
                            [G1[tt], self.dftT_p], [ps], signal=(i == 3))
                        i += 1
                self.dve(lambda g=g, ps=ps: self.V.scalar_tensor_tensor(
                    self.oTs[half][3][:, g, t0:t0 + 256], ps[:, :256], 0.5, GD2[:, g, t0:t0 + 256], ALU.mult, ALU.mult),
                    [ps, GD2], [self.oTs[half][3]])
        fw.pop()

    def contrib_rows(self, l, slot, W, c0, w, part, row0):
        fw = self.fw
        ps = self.zmm(slot, W, c0, w, 0, 512)
        stg = fw.rot([128, 512], BF16, "agstg", n=3)
        if self.cflip % 2:
            self.act(lambda: self.A.copy(stg[:w, :], ps[:w, :]), [ps], [stg])
        else:
            self.dve(lambda: self.V.tensor_copy(stg[:w, :], ps[:w, :]), [ps], [stg])
        self.cflip += 1
        ag = self.ag1_in[l][part]
        fw.dma("sp", ag, ag[row0:row0 + w, :], stg, stg[:w, :])

    def phaseD_contrib(self, l):
        fw = self.fw
        base = l * BLK_PER_LAYER
        self.GD2 = fw.sb([128, 4, 512], BF16, "GD2s")
        fw.push()
        self.stream_begin(2)
        slot = self.next_block(base + WIN_IDX["D0"])
        for c in range(4):
            self.contrib_rows(l, slot, 512, c * 128, 128, "f", c * 128)
        self.prefetch_next()
        slot = self.next_block(base + WIN_IDX["D1"])
        for c in range(4):
            ps = self.zmm(slot, 512, c * 128, 128, 0, 512)
            self.silu2(ps, 128, 512, self.GD2[:, c, :], self.GD2)
        self.prefetch_next()
        fw.pop()

    def phaseD_consume(self, l):
        fw = self.fw
        ago = self.ag1_out[l]["f"]
        fw.push()
        self.dftd_s = fw.sb([128, 256], BF16, "dftd_s")
        self.load(self.dftd_s, self.dftd_s[:], self.d["dftd_s"], self.d["dftd_s"].ap())
        acc = self.psb[0:4]
        nt = 0
        for q in range(4):
            fq = fw.rot([128, 4, 512], BF16, "fq", n=2)
            src = ago[q * 512:q * 512 + 512, :].rearrange("(g d) t -> d g t", d=128)
            fw.dma("sp", fq, fq[:], ago, src)
            for s4 in range(4):
                tt = q * 4 + s4
                ct = fw.rot([128, 1024], BF16, "ct", n=3)
                fw.dma("pool", ct, ct[:], self.d["dftT_s"], self.d["dftT_s"][tt, :, :])
                G1 = fw.rot([128, 1024], BF16, "G1s", n=3)
                self.fnet_stage1(fq, lambda g, s4=s4, fq=fq: fq[:, g, s4 * 128:(s4 + 1) * 128], self.dftd_s, G1)
                for g in range(4):
                    for cs in range(2):
                        self.pe(lambda g=g, cs=cs, G1=G1, ct=ct, tt=tt: self.T.matmul(
                            acc[g][:, :], G1[:, g * 256 + cs * 128:g * 256 + cs * 128 + 128], ct[:, cs * 512:(cs + 1) * 512],
                            start=(tt == 0 and cs == 0), stop=(tt == 15 and cs == 1)),
                            [G1, ct], [acc[g]], signal=(g == 3 and cs == 1))
        for g in range(4):
            self.dve(lambda g=g: self.V.scalar_tensor_tensor(self.oTs[0][3][:, g, 0:512], acc[g][:, :], 0.5, self.GD2[:, g, :],
                                                             ALU.mult, ALU.mult), [acc[g], self.GD2], [self.oTs[0][3]])
        fw.pop()

    def allgather(self, dst, src):
        fw = self.fw
        fw.dma("pool", dst, None, src, None, sem_owner=dst, inc=1,
               fn=lambda: self.G.collective_compute("AllGather", ALU.bypass, replica_groups=[[0, 1, 2, 3], [4, 5, 6, 7]],
                                                     ins=[src.h.ap()], outs=[dst.h.ap()]))

    def sample_pass(self, l):
        import os
        fw = self.fw
        dbg = os.environ.get("KDBG", "")
        self.cflip = 0
        br = self.branches
        fw.push()
        if "A" in br:
            self.phaseA_contrib(l)
        fw.push()
        if "B" in br:
            self.phaseB_contrib(l)
        fw.push()
        if "D" in br:
            self.phaseD_contrib(l)
        if "noag" not in dbg:
            if "A" in br:
                self.allgather(self.ag1_out[l]["rk"], self.ag1_in[l]["rk"])
            if "A" in br or "B" in br:
                self.allgather(self.ag1_out[l]["vx"], self.ag1_in[l]["vx"])
            if "D" in br:
                self.allgather(self.ag1_out[l]["f"], self.ag1_in[l]["f"])
        if "C" in br:
            self.phaseC(l, 0, 512, 0)
        if "D" in br and "nocons" not in dbg:
            self.phaseD_consume(l)
        fw.pop()
        if "B" in br:
            self.phaseB_consume(l)
        fw.pop()
        if "A" in br and "noAcons" not in dbg:
            self.phaseA_consume(l)
            self.phaseA_final(l)
        fw.pop()

    def zero_branch(self, n):
        for hf in range(2):
            if self.oTs[hf] is not None:
                t = self.oTs[hf][n]
                self.pool(lambda t=t: self.G.memset(t[:], 0.0), [], [t])

    def win_plan(self, l, grp="p"):
        base = l * BLK_PER_LAYER
        ids = []
        order = (("A", ["A0", "A1", "A2", "A3", "A4"]), ("B", ["B0", "B1"]), ("C", ["C0", "C1", "C2"]), ("D", ["D0", "D1"]))
        if grp == "s":
            order = (order[0], order[1], order[3], order[2])
        for br, names in order:
            if br in self.branches:
                ids += [base + WIN_IDX[n] for n in names]
        return ids

    def tail_plan(self, l):
        base = l * BLK_PER_LAYER
        return [base + 18 + i for i in range(16)] + [base + 34, base + 35]

    def build(self):
        fw = self.fw
        self.outsem = Buf(None, "outsem", "none")
        for l in range(self.depth):
            self.stream_plan([l * BLK_PER_LAYER + b for b in range(6)])
            self.stream_plan(self.win_plan(l) * 2 + self.tail_plan(l))
            self.stream_plan(self.win_plan(l, 's') + self.tail_plan(l))
        for l in range(self.depth):
            fw.push()
            self.load_layer_small(l)
            self.ada(l)
            fw.push()
            self.hTs[1] = fw.sb([128, 8, 512], BF16, "hTb")
            self.oTs[1] = [fw.sb([128, 4, 512], BF16, f"oTb{n}") for n in range(4)]
            for n, br in enumerate("ABCD"):
                if br not in self.branches:
                    self.zero_branch(n)
            for half in range(2):
                self.make_h(0, half * 512, half * 512, 512)
            for half in range(2):
                self.phases(l, "p", half)
            self.merge_out(l, 0, 0, NPT)
            fw.pop()
            self.hTs[1] = None
            self.oTs[1] = None
            self.make_h(1, NPT, 0, 512)
            self.sample_pass(l)
            self.merge_out(l, 1, NPT, NST)
            fw.pop()
        fw.push()
        self.sp = fw.sb([128, NSP], F32, "spf")
        self.load(self.sp, self.sp[:], self.d["sp"], self.d["sp"][0, :, :])
        self.final_out()
        fw.pop()
        for e in ("sp",):
            for ev in list(fw.dma_out.values()):
                fw._wait(e, ev)
        fw.barrier()
        return self.nc

    def phases(self, l, grp, half):
        h0 = half * 512
        if "A" in self.branches:
            self.phaseA_prompt(l, half)
        if "B" in self.branches:
            self.phaseB_prompt(l, half)
        if "C" in self.branches:
            self.phaseC(l, h0, 512, h0)
        if "D" in self.branches:
            self.phaseD_prompt(l, half)


_CFG = {"branches": "ABCD", "depth": DEPTH}


def make_in_maps(inp):
    f32 = lambda a: np.ascontiguousarray(np.asarray(a, dtype=np.float32))
    inp = {k: np.asarray(v) for k, v in inp.items()}
    wst = build_stream(f32(inp["w_ada"]), f32(inp["w_in"]), f32(inp["w_branch"]), f32(inp["w_merge"]), f32(inp["w_out"]))
    sp = build_small(inp)
    cst = build_consts()
    L = DEPTH
    wup = f32(inp["rwkv_w_up"]).transpose(0, 2, 1, 3).reshape(L, 64, 1024)
    aup = f32(inp["rwkv_a_up"]).transpose(0, 2, 1, 3).reshape(L, 64, 1024)
    wqu = f32(inp["mla_w_q_up"]).reshape(L, 2, 128, 8, 96)
    wq = wqu.transpose(0, 2, 1, 3, 4).reshape(L, 128, 2 * 8 * 96)
    wqs = wqu[..., 64 + _SWAP32].transpose(0, 2, 1, 3, 4).reshape(L, 128, 2 * 8 * 32)
    wkvu = f32(inp["mla_w_kv_up"]).reshape(L, 128, 8, 128)
    wkk = np.ascontiguousarray(wkvu[..., :64]).reshape(L, 128, 512)
    wkv = np.ascontiguousarray(wkvu[..., 64:]).reshape(L, 128, 512)
    wsT = f32(inp["gmlp_w_s"]).transpose(0, 3, 1, 2).reshape(L, 128, 512)
    bsb = np.ascontiguousarray(np.broadcast_to(f32(inp["gmlp_b_s"]).reshape(L, 1, 512), (L, 128, 512)))
    maps = []
    xp = f32(inp["x_prompt"])
    xs = f32(inp["x_sample"])
    for c in range(NCORE):
        s, j = c // 4, c % 4
        dftT, rope = build_core_consts(j)
        cond = np.stack([f32(inp["c_ctx"]).reshape(8, 128).T, f32(inp["c"])[s].reshape(8, 128).T], axis=2).reshape(128, 16)
        o, _ = SP_OFF["rw"]
        om, _ = SP_OFF["mu_rkv"]
        spo = np.concatenate([sp[:, :, o:o + 36].reshape(L, 128, 9, 4)[:, :, :, j],
                              sp[:, :, om:om + 12].reshape(L, 128, 3, 4)[:, :, :, j]], axis=2)
        spo = np.ascontiguousarray(spo)
        st0 = np.stack([f32(inp["state_rwkv_fwd"])[s, :, 2 * j:2 * j + 2], f32(inp["state_rwkv_bwd"])[s, :, 2 * j:2 * j + 2]],
                       axis=1)
        st0 = st0.transpose(0, 1, 2, 4, 3).reshape(L, 2, 128, 64)
        p = np.arange(128)
        idx1 = np.zeros((128, 12), np.int32)
        for q in range(4):
            idx1[:, 0 * 4 + q] = q * 1024 + 128 * j + p
            idx1[:, 1 * 4 + q] = q * 1024 + 512 + 128 * j + p
            idx1[:, 2 * 4 + q] = q * 864 + 128 * j + p
        idx2 = np.zeros((128, 4), np.int32)
        for r in range(4):
            idx2[:, r] = (r * 4 + j) * 128 + p
        m = dict(
            xp=np.ascontiguousarray(xp[4 * c:4 * c + 4].reshape(NPT, D).T),
            xs=np.ascontiguousarray(xs[s, 512 * j:512 * j + 512].T),
            wst=wst, sp=sp, cond=np.ascontiguousarray(cond),
            ident=cst["ident"], bones=cst["bones"], mask=cst["mask"],
            dftd_p=cst["dftd_p"], dftd_s=cst["dftd_s"], dftT_p=cst["dftT_p"],
            dftT_s=np.ascontiguousarray(dftT.reshape(16, 128, 1024)), rope=np.ascontiguousarray(rope.reshape(32, 1024)),
            wup=wup, aup=aup,
            wupo=np.ascontiguousarray(wup.reshape(L, 64, 2, 4, 128)[:, :, :, j]).reshape(L, 64, 256),
            aupo=np.ascontiguousarray(aup.reshape(L, 64, 2, 4, 128)[:, :, :, j]).reshape(L, 64, 256),
            spo=spo, wq=wq, wqs=wqs, wkk=wkk, wkv=wkv, wsT=wsT, bsb=bsb,
            st0=np.ascontiguousarray(st0),
            cckv=np.ascontiguousarray(f32(inp["cache_mla_ckv"])[s].transpose(0, 2, 1)),
            ckr=np.ascontiguousarray(f32(inp["cache_mla_krope"])[s].transpose(0, 2, 1)),
            idx1=idx1, idx2=idx2,
        )
        maps.append(m)
    return maps


def assemble(results):
    B = 32
    yp = np.zeros((B, SEQ, D), np.float32)
    ys = np.zeros((2, DSEQ, D), np.float32)
    sf = np.zeros((B, DEPTH, 8, 64, 64), np.float32)
    sbw = np.zeros((B, DEPTH, 8, 64, 64), np.float32)
    ckv = np.zeros((B, DEPTH, SEQ, 128), np.float32)
    kr = np.zeros((B, DEPTH, SEQ, 32), np.float32)
    for c in range(NCORE):
        r = results[c]
        s, j = c // 4, c % 4
        yp[4 * c:4 * c + 4] = np.asarray(r["yp"]).T.reshape(4, SEQ, D)
        ys[s, 512 * j:512 * j + 512] = np.asarray(r["ys"]).T
        st = np.asarray(r["stout"]).reshape(DEPTH, 2, 4, 4, 2, 64, 64)
        st = st.transpose(1, 2, 0, 3, 4, 6, 5).reshape(2, 4, DEPTH, 8, 64, 64)
        sf[4 * c:4 * c + 4] = st[0]
        sbw[4 * c:4 * c + 4] = st[1]
        ck = np.asarray(r["ckvout"]).reshape(DEPTH, 128, 4, SEQ)
        ckv[4 * c:4 * c + 4] = ck.transpose(2, 0, 3, 1)
        k2 = np.asarray(r["krout"]).reshape(DEPTH, 32, 4, SEQ)
        kr[4 * c:4 * c + 4] = k2.transpose(2, 0, 3, 1)
    return yp, ys, sf, sbw, ckv, kr


def kernel(**inputs):
    prog = Prog(dict(_CFG))
    nc = prog.build()
    maps = make_in_maps(inputs)
    res = run_bass_kernel_spmd(nc, maps, core_ids=list(range(NCORE)))
    return assemble(res.results)
```

```python
import numpy as np
import ml_dtypes
import concourse.bass as bass
import concourse.mybir as mybir
from concourse.bass_utils import run_bass_kernel_spmd

F32 = mybir.dt.float32
BF16 = mybir.dt.bfloat16
I32 = mybir.dt.int32
ALU = mybir.AluOpType
AF = mybir.ActivationFunctionType
AX = mybir.AxisListType

D = 1024
DEPTH = 2
SEQ = 256
DSEQ = 2048
PAST = 512
NCORE = 8
NPT = 1024
NST = 512
NTOK = NPT + NST
EPS = 1e-6
GN_EPS = 64e-5
EP = 24000
AGP = {"rk": 1024, "vx": 864, "f": 512}
VX_OFF = dict(v=0, lora=512, ckv=704, kr=832)


class Buf:
    def __init__(self, h, name, space):
        self.h = h
        self.name = name
        self.space = space
        self.w = {}
        self.r = {}
        self.dsem = None
        self.dcnt = 0
        self.dsid = None

    def __getitem__(self, idx):
        return self.h[idx]

    def ap(self):
        return self.h.ap() if self.space == "dram" else self.h[:]


class FW:
    def __init__(self, nc):
        self.nc = nc
        self.E = {"pe": nc.tensor, "act": nc.scalar, "dve": nc.vector, "pool": nc.gpsimd, "sp": nc.sync}
        self.cnt = {e: 0 for e in self.E}
        self.esem = {e: [] for e in self.E}
        self.waited = {e: {} for e in self.E}
        self.pend = {e: [] for e in self.E}
        self.nbuf = 0
        self.ninst = 0
        self.dma_out = {}
        self.sem_pool = []
        self.stack = []
        self.rots = {}
        self.free_sems = []

    def sb(self, shape, dt, name=None):
        self.nbuf += 1
        name = name or "t"
        g = self.nc.sbuf_tensor(f"{name}_{self.nbuf}", list(shape), dt)
        h = g.__enter__()
        b = Buf(h, name, "sbuf")
        if self.stack:
            self.stack[-1].append((g, b))
        return b

    def ps(self, shape, dt=F32, name=None):
        self.nbuf += 1
        name = name or "p"
        h = self.nc.alloc_psum_tensor(f"{name}_{self.nbuf}", list(shape), dt)
        return Buf(h, name, "psum")

    def dram(self, name, shape, dt, kind=None):
        if kind is None:
            h = self.nc.dram_tensor(name, list(shape), dt)
        else:
            h = self.nc.dram_tensor(name, list(shape), dt, kind=kind)
        return Buf(h, name, "dram")

    def rot(self, shape, dt, name, n=2):
        key = (len(self.stack), name)
        if key not in self.rots:
            self.rots[key] = [[self.sb(shape, dt, name) for _ in range(n)], 0]
        ent = self.rots[key]
        t = ent[0][ent[1] % n]
        ent[1] += 1
        return t

    def push(self):
        self.stack.append([])

    def pop(self):
        self.barrier()
        depth = len(self.stack)
        for k in [k for k in self.rots if k[0] == depth]:
            del self.rots[k]
        for g, b in reversed(self.stack.pop()):
            if b.dsem is not None:
                self.free_sems.append((b.dsem, b.dcnt))
                b.dsem = None
            g.__exit__(None, None, None)

    def _sem_for(self, eng, k):
        i = (k - 1) // EP
        while len(self.esem[eng]) <= i:
            self.esem[eng].append(self.nc.alloc_semaphore(f"s_{eng}_{len(self.esem[eng])}"))
        return self.esem[eng][i], (k - 1) % EP + 1

    def _wait(self, eng, ev):
        if ev is None:
            return
        if ev[0] == "eng":
            _, e2, k = ev
            if e2 == eng and eng == "pe":
                return
            key = ("eng", e2, (k - 1) // EP)
            sem, val = self._sem_for(e2, k)
        else:
            _, sem, val, sid = ev
            key = ("sem", sid)
        if self.waited[eng].get(key, 0) >= val:
            return
        self.waited[eng][key] = val
        self.E[eng].wait_ge(sem, val)

    def _check_pend(self, eng, b):
        for e2, lst in self.pend.items():
            if e2 == eng:
                continue
            for (pb, _) in lst:
                if pb is b:
                    raise RuntimeError(f"buffer {b.name} has pending unsignalled access on {e2}, touched by {eng}")

    def _deps(self, eng, reads, writes):
        evs = []
        for b in reads:
            self._check_pend(eng, b)
            evs.extend(b.w.values())
        for b in writes:
            self._check_pend(eng, b)
            for wv in b.w.values():
                if not (wv[0] == "eng" and wv[1] == eng):
                    evs.append(wv)
            for ev in b.r.values():
                if ev[0] == "eng" and ev[1] == eng and eng == "pe":
                    continue
                evs.append(ev)
        for ev in evs:
            self._wait(eng, ev)

    def op(self, eng, fn, reads=(), writes=(), signal=True):
        self._deps(eng, reads, writes)
        ins = fn()
        self.ninst += 1
        if signal:
            self.cnt[eng] += 1
            k = self.cnt[eng]
            sem, val = self._sem_for(eng, k)
            ins.then_inc(sem, 1)
            ev = ("eng", eng, k)
            for (pb, kind) in self.pend[eng]:
                if kind == "r":
                    pb.r[eng] = ev
                else:
                    pb.w = {eng: ev}
                    pb.r = {}
            self.pend[eng] = []
            for b in reads:
                b.r[eng] = ev
            for b in writes:
                b.w = {eng: ev}
                b.r = {}
        else:
            for b in reads:
                self.pend[eng].append((b, "r"))
            for b in writes:
                self.pend[eng].append((b, "w"))
        return ins

    def _dma_sem(self, b):
        if b.dsem is None:
            self.nbuf += 1
            if self.free_sems:
                b.dsem, b.dcnt = self.free_sems.pop()
            else:
                b.dsem = self.nc.alloc_semaphore(f"d_{b.name}_{self.nbuf}")
            b.dsid = self.nbuf
        return b.dsem

    def dma(self, q, out_b, out_ap, in_b, in_ap, sem_owner=None, inc=16, fn=None, extra_reads=(), **kw):
        eng = q
        evs = []
        for b in (in_b, out_b) + tuple(extra_reads):
            self._check_pend(eng, b)
        evs.extend(in_b.w.values())
        for b in extra_reads:
            evs.extend(b.w.values())
        owner = sem_owner or (out_b if out_b.space != "dram" else in_b)
        sem = self._dma_sem(owner)
        for wv in out_b.w.values():
            if wv[0] != "sem":
                evs.append(wv)
        for ev in out_b.r.values():
            evs.append(ev)
        for ev in evs:
            self._wait(eng, ev)
        if fn is None:
            ins = self.E[eng].dma_start(out=out_ap, in_=in_ap, **kw)
        else:
            ins = fn()
        self.ninst += 1
        owner.dcnt += inc
        ins.then_inc(sem, inc)
        ev = ("sem", sem, owner.dcnt, owner.dsid)
        in_b.r[("dma", owner.dsid)] = ev
        for b in extra_reads:
            b.r[("dma", owner.dsid)] = ev
        out_b.w = {k: v for k, v in out_b.w.items() if v[0] == "sem"}
        out_b.w[("dma", owner.dsid)] = ev
        out_b.r = {}
        self.dma_out[owner.dsid] = ev
        return ev

    def wait_buf(self, eng, b):
        self._check_pend(eng, b)
        for ev in b.w.values():
            self._wait(eng, ev)
        for ev in b.r.values():
            self._wait(eng, ev)

    def barrier(self):
        for e in self.E:
            if self.pend[e]:
                raise RuntimeError(f"barrier with pending unsignalled ops on {e}")
        last = []
        for e in ("pe", "act", "dve", "pool"):
            if self.cnt[e] > 0:
                last.append(("eng", e, self.cnt[e]))
        for e in self.E:
            for ev in last:
                if ev[1] == e and e == "pe":
                    continue
                self._wait(e, ev)
            for ev in self.dma_out.values():
                self._wait(e, ev)
        self.dma_out = {}


COLS = dict(r=(0, 512), k=(512, 1024), v=(1024, 1536), wdf=(1536, 1600), wdb=(1600, 1664), ad=(1664, 1728),
            ga=(1728, 2240), qd=(2240, 2496), kvd=(2496, 2624), kr=(2624, 2656), gb=(2656, 3168),
            u=(3168, 3680), vc=(3680, 4192), gc=(4192, 4704), f=(4704, 5216), gd=(5216, 5728))


def _rng(name):
    a, b = COLS[name]
    return np.arange(a, b)


_SWAP32 = np.arange(32).reshape(16, 2)[:, ::-1].reshape(32)

WIN_BLOCKS = [
    ("A0", [_rng("r")]), ("A1", [_rng("k")]), ("A2", [_rng("v")]),
    ("A3", [_rng("wdf"), _rng("wdb"), _rng("ad")]), ("A4", [_rng("ga")]),
    ("B0", [_rng("qd"), _rng("kvd"), _rng("kr"), _rng("kr")[_SWAP32]]), ("B1", [_rng("gb")]),
    ("C0", [_rng("u")]), ("C1", [_rng("vc")]), ("C2", [_rng("gc")]),
    ("D0", [_rng("f")]), ("D1", [_rng("gd")]),
]
WIN_W = {n: int(sum(len(c) for c in cols)) for n, cols in WIN_BLOCKS}
WIN_IDX = {n: 6 + i for i, (n, _) in enumerate(WIN_BLOCKS)}
BLK_PER_LAYER = 36
FBLK = 4096


def _kcp(w, W):
    return w.reshape(8, 128, W).transpose(1, 0, 2).reshape(128, 8 * W)


def build_stream(w_ada, w_in, w_branch, w_merge, w_out):
    st = np.zeros((DEPTH * BLK_PER_LAYER, 128, FBLK), np.float32)
    for l in range(DEPTH):
        base = l * BLK_PER_LAYER
        for b in range(6):
            st[base + b, :, :] = _kcp(w_ada[l][:, 512 * b:512 * b + 512], 512)
        for i, (n, cols) in enumerate(WIN_BLOCKS):
            cc = np.concatenate(cols)
            W = len(cc)
            st[base + 6 + i, :, :8 * W] = _kcp(w_in[l][:, cc], W)
        for d in range(8):
            cc = np.concatenate([n * 1024 + d * 128 + np.arange(128) for n in range(4)])
            st[base + 18 + 2 * d, :, :] = _kcp(w_merge[l][:, cc], 512)
            wb = w_branch[l][:, :, d * 128:(d + 1) * 128]
            wb = wb.reshape(4, 4, 128, 128).transpose(2, 0, 1, 3)
            st[base + 19 + 2 * d, :, :2048] = wb.reshape(128, 2048)
        for b in range(2):
            st[base + 34 + b, :, :] = _kcp(w_out[l][:, 512 * b:512 * b + 512], 512)
    return st


SP_OFF = {}
_o = 0
for _n, _w in [("norm_g", 8), ("b_ada", 24), ("mu_rkv", 12), ("mu_lora", 3), ("b_merge", 32), ("rw", 36),
               ("qn", 2), ("kvn", 1), ("gln_g", 4), ("gln_b", 4), ("fin_g", 8)]:
    SP_OFF[_n] = (_o, _w)
    _o += _w
NSP = _o
RW_NAMES = ["w0_f", "w0_b", "a0_f", "a0_b", "k_k", "k_a", "r_k", "ln_g", "ln_b"]


def _pc(v, n):
    return np.asarray(v, np.float32).reshape(n, 128).T


def build_small(inp):
    sp = np.zeros((DEPTH, 128, NSP), np.float32)
    for l in range(DEPTH):
        def put(name, arr):
            o, w = SP_OFF[name]
            sp[l, :, o:o + w] = arr
        put("norm_g", _pc(inp["norm_g"][l], 8))
        put("b_ada", _pc(inp["b_ada"][l], 24))
        put("mu_rkv", _pc(inp["shift_mu"][l][:1536], 12))
        ml = np.zeros((128, 3), np.float32)
        ml[:64, :] = inp["shift_mu"][l][1536:1728].reshape(3, 64).T
        put("mu_lora", ml)
        bm = inp["b_merge"][l].reshape(4, 8, 128)
        put("b_merge", bm.transpose(2, 1, 0).reshape(128, 32))
        rwv = [inp["rwkv_w0"][l][0], inp["rwkv_w0"][l][1], inp["rwkv_a0"][l][0], inp["rwkv_a0"][l][1],
               inp["rwkv_k_k"][l], inp["rwkv_k_a"][l], inp["rwkv_r_k"][l].reshape(512), inp["rwkv_ln_g"][l],
               inp["rwkv_ln_b"][l]]
        rw = np.stack([_pc(v, 4) for v in rwv], axis=1)
        put("rw", rw.reshape(128, 36))
        put("qn", _pc(inp["mla_q_norm"][l], 2))
        put("kvn", _pc(inp["mla_kv_norm"][l], 1))
        put("gln_g", _pc(inp["gmlp_ln_g"][l], 4))
        put("gln_b", _pc(inp["gmlp_ln_b"][l], 4))
        put("fin_g", _pc(inp["final_norm_g"], 8))
    return sp


def rw_col(name, pair):
    o, _ = SP_OFF["rw"]
    return o + RW_NAMES.index(name) * 4 + pair


def build_consts():
    c = {}
    c["ident"] = np.eye(128, dtype=np.float32)
    hb = np.arange(128) // 64
    c["bones"] = (hb[:, None] == hb[None, :]).astype(np.float32)
    p = np.arange(128)[:, None]
    f = np.arange(128)[None, :]
    mk = {}
    bd32 = ((p // 32) == (f // 32)).astype(np.float32)
    od64 = (((p // 64) == (f // 64)) & ((p // 32) != (f // 32))).astype(np.float32)
    od128 = ((p // 64) != (f // 64)).astype(np.float32)
    for dname, ms, mi, mt in (("f", p < f, p <= f, f < p), ("b", p > f, p >= f, f > p)):
        ms = ms.astype(np.float32)
        mi = mi.astype(np.float32)
        mtf = -(mt.astype(np.float32))
        mk[dname] = np.concatenate([ms, mi, -ms * bd32, mi, mtf * bd32, mtf * bd32, mtf * od64, mtf * od64,
                                    mtf * od128, mtf * od128], axis=1)
    c["mask"] = np.stack([mk["f"], mk["b"]], axis=1).reshape(128, 2 * 1280)
    dd = np.arange(128)
    ang = 2 * np.pi * np.outer(dd, dd) / 128.0
    for nm, T in (("dftd_p", SEQ), ("dftd_s", DSEQ)):
        sc = 1.0 / np.sqrt(T * 128.0)
        c[nm] = np.concatenate([np.cos(ang) * sc, -np.sin(ang) * sc], axis=1).astype(np.float32)
    tt = np.arange(SEQ)
    angp = 2 * np.pi * np.outer(tt, tt) / SEQ
    cp = np.stack([np.cos(angp), np.sin(angp)], axis=1)
    c["dftT_p"] = cp.reshape(2, 128, 2, SEQ).transpose(1, 0, 2, 3).reshape(128, 2 * 2 * SEQ).astype(np.float32)
    return c


def build_core_consts(j):
    t = np.arange(DSEQ)
    k1 = 512 * j + np.arange(512)
    ang = 2 * np.pi * ((np.outer(t, k1)) % DSEQ) / DSEQ
    cs = np.stack([np.cos(ang), np.sin(ang)], axis=1)
    dftT = cs.reshape(16, 128, 2, 512).astype(np.float32)
    pos = 512 * j + np.arange(512)
    row = (pos // 64).astype(np.float32)
    col = (pos % 64).astype(np.float32)
    inv = (10000.0 ** (-np.arange(8, dtype=np.float32) / 8)).astype(np.float32)
    ang = np.concatenate([row[:, None] * inv, col[:, None] * inv], axis=-1).astype(np.float32)
    cos = np.cos(ang).astype(np.float32)
    sin = np.sin(ang).astype(np.float32)
    COS = np.repeat(cos, 2, axis=1).T
    SIN = np.stack([-sin, sin], axis=2).reshape(512, 32).T
    rope = np.stack([COS, SIN], axis=1).astype(np.float32)
    return dftT, rope


class Prog:
    def __init__(self, cfg):
        self.cfg = cfg
        self.branches = cfg.get("branches", "ABCD")
        self.depth = cfg.get("depth", DEPTH)
        nc = bass.Bass("TRN2", target_bir_lowering=False)
        self.nc = nc
        fw = FW(nc)
        self.fw = fw
        self.V, self.A, self.G, self.T = nc.vector, nc.scalar, nc.gpsimd, nc.tensor
        di = lambda n, s, dt=F32: fw.dram(n, s, dt, kind="ExternalInput")
        do = lambda n, s, dt=F32: fw.dram(n, s, dt, kind="ExternalOutput")
        self.d = dict(
            xp=di("xp", [D, NPT]), xs=di("xs", [D, NST]),
            wst=di("wst", [DEPTH * BLK_PER_LAYER, 128, FBLK]),
            sp=di("sp", [DEPTH, 128, NSP]), cond=di("cond", [128, 16]),
            ident=di("ident", [128, 128]), bones=di("bones", [128, 128]), mask=di("mask", [128, 2560]),
            dftd_p=di("dftd_p", [128, 256]), dftd_s=di("dftd_s", [128, 256]), dftT_p=di("dftT_p", [128, 1024]),
            dftT_s=di("dftT_s", [16, 128, 1024]), rope=di("rope", [32, 1024]),
            wup=di("wup", [DEPTH, 64, 1024]), aup=di("aup", [DEPTH, 64, 1024]),
            wupo=di("wupo", [DEPTH, 64, 256]), aupo=di("aupo", [DEPTH, 64, 256]),
            spo=di("spo", [DEPTH, 128, 12]),
            wq=di("wq", [DEPTH, 128, 2 * 8 * 96]), wqs=di("wqs", [DEPTH, 128, 2 * 8 * 32]),
            wkk=di("wkk", [DEPTH, 128, 512]), wkv=di("wkv", [DEPTH, 128, 512]),
            wsT=di("wsT", [DEPTH, 128, 512]), bsb=di("bsb", [DEPTH, 128, 512]),
            st0=di("st0", [DEPTH, 2, 128, 64]), cckv=di("cckv", [DEPTH, 128, PAST]),
            ckr=di("ckr", [DEPTH, 32, PAST]),
            idx1=di("idx1", [128, 12], I32), idx2=di("idx2", [128, 4], I32),
            yp=do("yp", [D, NPT]), ys=do("ys", [D, NST]),
            stout=do("stout", [DEPTH * 2 * 4 * 4 * 128, 64]),
            ckvout=do("ckvout", [DEPTH, 128, NPT]), krout=do("krout", [DEPTH, 32, NPT]),
        )
        self.ag1_in = [{k: fw.dram(f"ag1i{k}{l}", [n, 512], BF16) for k, n in AGP.items()} for l in range(DEPTH)]
        self.ag1_out = [{k: fw.dram(f"ag1o{k}{l}", [4 * n, 512], BF16) for k, n in AGP.items()} for l in range(DEPTH)]
        self.ag2_in = [fw.dram(f"ag2i{l}", [512, 512], BF16) for l in range(DEPTH)]
        self.ag2_out = [fw.dram(f"ag2o{l}", [2048, 512], BF16) for l in range(DEPTH)]
        self.psb = [fw.ps([128, 512], F32, f"bank{i}") for i in range(6)]
        self.pst = [fw.ps([128, 1024], BF16, f"pst{i}") for i in range(2)]
        self.pst_rr = 0
        self.ps_rr = 0
        self.xT = fw.sb([128, 8, NTOK], F32, "xT")
        self.hTs = [fw.sb([128, 8, 512], BF16, "hTa"), None]
        self.oTs = [[fw.sb([128, 4, 512], BF16, f"oTa{n}") for n in range(4)], None]
        self.slots = None
        self.slot_rr = 0
        self.plan = []
        self.plan_pos = 0
        self.stg = None
        self.blocks_left = 0
        self.dma_issued = {}
        self.load_consts()

    ps_range = (0, 4)

    def nps(self, lo=None, hi=None):
        lo = self.ps_range[0] if lo is None else lo
        hi = self.ps_range[1] if hi is None else hi
        n = hi - lo
        b = self.psb[lo + (self.ps_rr % n)]
        self.ps_rr += 1
        return b

    def dve(self, fn, r, w):
        return self.fw.op("dve", fn, r, w)

    def rsqrt(self, out_buf, out_ap, in_buf, in_ap):
        self.act(lambda: self.A.activation(in_ap, in_ap, AF.Sqrt), [in_buf], [in_buf])
        self.dve(lambda: self.V.reciprocal(out_ap, in_ap), [in_buf], [out_buf])

    def act(self, fn, r, w):
        return self.fw.op("act", fn, r, w)

    def pool(self, fn, r, w):
        return self.fw.op("pool", fn, r, w)

    def pe(self, fn, r, w, signal=True):
        return self.fw.op("pe", fn, r, w, signal=signal)

    def load(self, dst, dst_ap, src, src_ap, q="sp"):
        if dst_ap.dtype != src_ap.dtype:
            q = "pool"
        return self.fw.dma(q, dst, dst_ap, src, src_ap)

    def load_consts(self):
        fw, d = self.fw, self.d
        self.ident = fw.sb([128, 128], BF16, "ident")
        self.bones = fw.sb([128, 128], BF16, "bones")
        self.mask = fw.sb([128, 2560], BF16, "mask")
        self.ones = fw.sb([128, 128], BF16, "ones")
        self.onesf = fw.sb([128, 128], F32, "onesf")
        self.rope = None
        self.cond = fw.sb([128, 16], F32, "cond")
        self.idx1 = fw.sb([128, 12], I32, "idx1")
        self.idx2 = fw.sb([128, 4], I32, "idx2")
        for nm in ("ident", "bones", "mask", "cond", "idx1", "idx2"):
            t = getattr(self, nm)
            self.load(t, t[:], d[nm], d[nm].ap())
        self.pool(lambda: self.G.memset(self.ones[:], 1.0), [], [self.ones])
        self.pool(lambda: self.G.memset(self.onesf[:], 1.0), [], [self.onesf])
        self.hm = fw.sb([128, 2], F32, "hm")
        self.dve(lambda: self.V.tensor_copy(self.hm[:, 0:2], self.bones[:, 0:128:64]), [self.bones], [self.hm])
        xv = self.xT
        self.load(xv, xv[:, :, 0:NPT], d["xp"], d["xp"].ap().rearrange("(k p) t -> p k t", p=128))
        self.load(xv, xv[:, :, NPT:NTOK], d["xs"], d["xs"].ap().rearrange("(k p) t -> p k t", p=128))

    def stream_plan(self, ids):
        self.plan.extend(ids)

    def stream_begin(self, nblocks, depth=1):
        fw = self.fw
        self.sdepth = depth
        self.slots = [fw.sb([128, FBLK], BF16, f"slot{i}") for i in range(depth + 1)]
        self.stg = [fw.sb([128, FBLK // 2], F32, f"wstg{i}") for i in range(2 * depth)]
        self.blocks_left = nblocks
        self.scope_end = self.plan_pos + nblocks
        self.dma_issued = {}

    def _issue_dma(self, pos):
        blk = self.plan[pos]
        src = self.d["wst"]
        for hf in range(2):
            st = self.stg[(2 * pos + hf) % len(self.stg)]
            self.fw.dma("sp", st, st[:, :], src, src[blk, :, hf * (FBLK // 2):(hf + 1) * (FBLK // 2)])
        self.dma_issued[pos] = True

    def next_block(self, blk):
        pos = self.plan_pos
        assert self.plan[pos] == blk, (pos, self.plan[pos], blk)
        assert self.blocks_left > 0
        if pos not in self.dma_issued:
            self._issue_dma(pos)
        slot = self.slots[pos % len(self.slots)]
        for hf in range(2):
            st = self.stg[(2 * pos + hf) % len(self.stg)]
            if hf == 0:
                self.dve(lambda hf=hf, st=st: self.V.tensor_copy(slot[:, hf * (FBLK // 2):(hf + 1) * (FBLK // 2)], st[:, :]), [st], [slot])
            else:
                self.act(lambda hf=hf, st=st: self.A.copy(slot[:, hf * (FBLK // 2):(hf + 1) * (FBLK // 2)], st[:, :]), [st], [slot])
        self.plan_pos += 1
        self.blocks_left -= 1
        return slot

    def prefetch_next(self):
        for pos in range(self.plan_pos, min(self.plan_pos + self.sdepth, self.scope_end)):
            if pos not in self.dma_issued:
                self._issue_dma(pos)

    def load_layer_small(self, l):
        fw, d = self.fw, self.d
        self.sp = fw.sb([128, NSP], F32, "sp")
        self.load(self.sp, self.sp[:], d["sp"], d["sp"][l, :, :])
        sp = self.sp
        o, _ = SP_OFF["mu_rkv"]
        self.mu1 = fw.sb([128, 15], F32, "mu1")
        self.muh = fw.sb([128, 15], F32, "muh")
        self.dve(lambda: self.V.tensor_scalar(self.mu1[:], sp[:, o:o + 15], -1.0, 1.0, ALU.mult, ALU.add), [sp], [self.mu1])
        self.dve(lambda: self.V.tensor_scalar(self.muh[:], sp[:, o:o + 15], 0.5, None, ALU.mult), [sp], [self.muh])
        o2, _ = SP_OFF["rw"]
        self.rwh = fw.sb([128, 16], F32, "rwh")
        self.dve(lambda: self.V.tensor_scalar(self.rwh[:], sp[:, o2:o2 + 16], 0.5, None, ALU.mult), [sp], [self.rwh])
        ob, _ = SP_OFF["b_merge"]
        self.bmh = fw.sb([128, 32], F32, "bmh")
        self.dve(lambda: self.V.tensor_scalar(self.bmh[:], sp[:, ob:ob + 32], 0.5, None, ALU.mult), [sp], [self.bmh])

    def spc(self, name, i=0, n=1):
        o, _ = SP_OFF[name]
        return self.sp[:, o + i:o + i + n]

    def ada(self, l):
        fw = self.fw
        sc = fw.sb([128, 16], BF16, "scond")
        th = fw.sb([128, 16], F32, "cth")
        c = self.cond
        self.act(lambda: self.A.activation(th[:], c[:], AF.Tanh, scale=0.5), [c], [th])
        t2 = fw.sb([128, 16], F32, "ct2")
        self.dve(lambda: self.V.scalar_tensor_tensor(t2[:], th[:], 1.0, c[:], ALU.add, ALU.mult), [th, c], [t2])
        self.dve(lambda: self.V.tensor_scalar(sc[:], t2[:], 0.5, None, ALU.mult), [t2], [sc])
        ps = self.psb[5]
        self.mod = fw.sb([128, 24, 2], F32, "mod")
        self.gmod = fw.sb([128, 8, 2], F32, "gmod")
        fw.push()
        self.stream_begin(6, depth=2)
        for b in range(6):
            slot = self.next_block(l * BLK_PER_LAYER + b)
            for n in range(4):
                m = b * 4 + n
                for kc in range(8):
                    self.pe(lambda kc=kc, n=n, m=m, slot=slot: self.T.matmul(
                        ps[:, 2 * m:2 * m + 2], slot[:, kc * 512 + n * 128:kc * 512 + n * 128 + 128],
                        sc[:, 2 * kc:2 * kc + 2], start=(kc == 0), stop=(kc == 7)),
                        [slot, sc], [ps], signal=(kc == 7 and n == 3))
            self.prefetch_next()
        fw.pop()
        ob, _ = SP_OFF["b_ada"]
        for cc in range(2):
            self.dve(lambda cc=cc: self.V.tensor_tensor(self.mod[:, :, cc], ps[:, cc:48:2], self.sp[:, ob:ob + 24], ALU.add),
                     [ps, self.sp], [self.mod])
        og, _ = SP_OFF["norm_g"]
        for cc in range(2):
            self.dve(lambda cc=cc: self.V.scalar_tensor_tensor(self.gmod[:, :, cc], self.mod[:, 8:16, cc], 1.0,
                                                               self.sp[:, og:og + 8], ALU.add, ALU.mult),
                     [self.mod, self.sp], [self.gmod])

    def rstd_tile(self, src, views, nfeat, out, N):
        fw = self.fw
        ps = self.nps()
        nk = len(views)
        for i, v in enumerate(views):
            sq = fw.rot([128, 512], BF16, "sq")
            self.act(lambda v=v, sq=sq: self.A.activation(sq[:, :N], v, AF.Square), [src], [sq])
            self.pe(lambda i=i, sq=sq: self.T.matmul(ps[:, :N], self.ones[:], sq[:, :N], start=(i == 0), stop=(i == nk - 1)),
                    [self.ones, sq], [ps], signal=True)
        t = fw.sb([128, 512], F32, "rs_t")
        self.dve(lambda: self.V.tensor_scalar(t[:, :N], ps[:, :N], 1.0 / nfeat, EPS, ALU.mult, ALU.add), [ps], [t])
        self.rsqrt(out, out[:, :N], t, t[:, :N])

    def make_h(self, cc, x0, h0, N):
        fw = self.fw
        fw.push()
        rstd = fw.sb([128, 512], F32, "rstd")
        self.rstd_tile(self.xT, [self.xT[:, kc, x0:x0 + N] for kc in range(8)], float(D), rstd, N)
        for kc in range(8):
            tmp = fw.rot([128, 512], F32, "htmp")
            self.dve(lambda kc=kc, tmp=tmp: self.V.scalar_tensor_tensor(
                tmp[:, :N], self.xT[:, kc, x0:x0 + N], self.gmod[:, kc, cc:cc + 1], rstd[:, :N], ALU.mult, ALU.mult),
                [self.xT, self.gmod, rstd], [tmp])
            self.act(lambda kc=kc, tmp=tmp: self.A.activation(
                self.hTs[h0 // 512][:, kc, 0:N], tmp[:, :N], AF.Identity, bias=self.mod[:, kc, cc:cc + 1], scale=1.0),
                [tmp, self.mod], [self.hTs[h0 // 512]])
        fw.pop()

    def zmm(self, slot, W, c0, w, h0, N, ps=None, prow=0):
        ps = ps or self.nps()
        for kc in range(8):
            self.pe(lambda kc=kc: self.T.matmul(ps[prow:prow + w, :N], slot[:, kc * W + c0:kc * W + c0 + w],
                                                self.hTs[h0 // 512][:, kc, 0:N], start=(kc == 0), stop=(kc == 7)),
                    [slot, self.hTs[h0 // 512]], [ps], signal=(kc == 7))
        return ps

    def silu2(self, ps, rows, N, out_ap, out_buf):
        fw = self.fw
        th = fw.rot([128, 512], F32, "s2th")
        self.act(lambda: self.A.activation(th[:rows, :N], ps[:rows, :N], AF.Tanh, scale=0.5), [ps], [th])
        self.dve(lambda: self.V.scalar_tensor_tensor(out_ap, th[:rows, :N], 1.0, ps[:rows, :N], ALU.add, ALU.mult),
                 [th, ps], [out_buf])

    def gelu2(self, ps, rows, N, out_ap, out_buf):
        fw = self.fw
        u = fw.rot([128, 512], F32, "g2u")
        self.act(lambda: self.A.activation(u[:rows, :N], ps[:rows, :N], AF.Square), [ps], [u])
        self.dve(lambda: self.V.tensor_scalar(u[:rows, :N], u[:rows, :N], 0.044715, 1.0, ALU.mult, ALU.add), [u], [u])
        self.dve(lambda: self.V.tensor_tensor(u[:rows, :N], u[:rows, :N], ps[:rows, :N], ALU.mult), [u, ps], [u])
        self.act(lambda: self.A.activation(u[:rows, :N], u[:rows, :N], AF.Tanh, scale=0.7978845608028654), [u], [u])
        self.dve(lambda: self.V.scalar_tensor_tensor(out_ap, u[:rows, :N], 1.0, ps[:rows, :N], ALU.add, ALU.mult),
                 [u, ps], [out_buf])

    def transpose_to(self, src_buf, src_ap, dst_buf, dst_ap, rows=128, cols=128, eng="act"):
        pt = self.pst[self.pst_rr % 2]
        self.pst_rr += 1
        self.pe(lambda: self.T.transpose(pt[:cols, :rows], src_ap, self.ident[:rows, :rows]), [src_buf, self.ident], [pt])
        if eng == "act":
            self.act(lambda: self.A.copy(dst_ap, pt[:cols, :rows]), [pt], [dst_buf])
        else:
            self.dve(lambda: self.V.tensor_copy(dst_ap, pt[:cols, :rows]), [pt], [dst_buf])

    def phaseC(self, l, h0, N, o0):
        fw = self.fw
        base = l * BLK_PER_LAYER
        fw.push()
        self.stream_begin(3, depth=2)
        U2 = fw.sb([128, 4, 512], BF16, "U2")
        GV = fw.sb([128, 4, 512], BF16, "GV")
        GC2 = fw.sb([128, 4, 512], BF16, "GC2")
        wsT = fw.sb([128, 512], BF16, "wsT")
        bsb = fw.sb([128, 512], F32, "bsb")
        self.load(wsT, wsT[:], self.d["wsT"], self.d["wsT"][l, :, :])
        self.load(bsb, bsb[:], self.d["bsb"], self.d["bsb"][l, :, :])
        slot = self.next_block(base + WIN_IDX["C0"])
        for c in range(4):
            ps = self.zmm(slot, 512, c * 128, 128, h0, N)
            self.gelu2(ps, 128, N, U2[:, c, :N], U2)
        self.prefetch_next()
        slot = self.next_block(base + WIN_IDX["C1"])
        for c in range(4):
            ps = self.zmm(slot, 512, c * 128, 128, h0, N)
            self.gelu2(ps, 128, N, GV[:, c, :N], GV)
        self.prefetch_next()
        slot = self.next_block(base + WIN_IDX["C2"])
        for c in range(4):
            ps = self.zmm(slot, 512, c * 128, 128, h0, N)
            self.silu2(ps, 128, N, GC2[:, c, :N], GC2)
        self.prefetch_next()
        psm = self.nps()
        psq = self.nps()
        for c in range(4):
            self.pe(lambda c=c: self.T.matmul(psm[:, :N], self.ones[:], GV[:, c, :N], start=(c == 0), stop=(c == 3)),
                    [self.ones, GV], [psm], signal=(c == 3))
        for c in range(4):
            sq = fw.rot([128, 512], BF16, "gsq")
            self.act(lambda c=c, sq=sq: self.A.activation(sq[:, :N], GV[:, c, :N], AF.Square), [GV], [sq])
            self.pe(lambda c=c, sq=sq: self.T.matmul(psq[:, :N], self.ones[:], sq[:, :N], start=(c == 0), stop=(c == 3)),
                    [self.ones, sq], [psq], signal=True)
        mu = fw.sb([128, 512], F32, "gmu")
        msq = fw.sb([128, 512], F32, "gmsq")
        var = fw.sb([128, 512], F32, "gvar")
        rstd = fw.sb([128, 512], F32, "grstd")
        self.dve(lambda: self.V.tensor_scalar(mu[:, :N], psm[:, :N], 1.0 / 512, None, ALU.mult), [psm], [mu])
        self.dve(lambda: self.V.tensor_tensor(msq[:, :N], mu[:, :N], mu[:, :N], ALU.mult), [mu], [msq])
        self.dve(lambda: self.V.scalar_tensor_tensor(var[:, :N], psq[:, :N], 1.0 / 512, msq[:, :N], ALU.mult, ALU.subtract),
                 [psq, msq], [var])
        self.dve(lambda: self.V.tensor_scalar(var[:, :N], var[:, :N], 4e-5, None, ALU.add), [var], [var])
        self.rsqrt(rstd, rstd[:, :N], var, var[:, :N])
        VN = fw.sb([128, 4, 512], BF16, "VN")
        for c in range(4):
            t = fw.rot([128, 512], F32, "lnt")
            self.dve(lambda c=c, t=t: self.V.tensor_tensor(t[:, :N], GV[:, c, :N], mu[:, :N], ALU.subtract), [GV, mu], [t])
            self.dve(lambda t=t: self.V.tensor_tensor(t[:, :N], t[:, :N], rstd[:, :N], ALU.mult), [t, rstd], [t])
            self.act(lambda c=c, t=t: self.A.activation(VN[:, c, :N], t[:, :N], AF.Identity, bias=self.spc("gln_b", c),
                                                        scale=self.spc("gln_g", c)), [t, self.sp], [VN])
        nsub = N // 128
        for g in range(4):
            pmix = self.nps()
            for s in range(nsub):
                vtm = fw.rot([128, 128], BF16, "vtm")
                self.transpose_to(VN, VN[:, g, s * 128:(s + 1) * 128], vtm, vtm[:], eng=("act" if s % 2 else "dve"))
                self.pe(lambda g=g, s=s, vtm=vtm: self.T.matmul(pmix[:, s * 128:(s + 1) * 128], vtm[:],
                                                                 wsT[:, g * 128:(g + 1) * 128], start=True, stop=True),
                        [vtm, wsT], [pmix], signal=(s == nsub - 1))
            t = fw.rot([128, 512], F32, "mixt")
            for s in range(nsub):
                self.dve(lambda g=g, s=s, t=t: self.V.tensor_tensor(t[:, s * 128:(s + 1) * 128], pmix[:, s * 128:(s + 1) * 128],
                                                                     bsb[:, g * 128:(g + 1) * 128], ALU.add), [pmix, bsb], [t])
            self.dve(lambda g=g, t=t: self.V.scalar_tensor_tensor(t[:, :N], t[:, :N], 0.25, U2[:, g, :N], ALU.mult, ALU.mult),
                     [t, U2], [t])
            self.dve(lambda g=g, t=t: self.V.tensor_tensor(self.oTs[o0 // 512][2][:, g, 0:N], t[:, :N], GC2[:, g, :N], ALU.mult),
                     [t, GC2], [self.oTs[o0 // 512][2]])
        fw.pop()

    def merge_out(self, l, cc, x0, NT):
        fw = self.fw
        base = l * BLK_PER_LAYER
        ntile = NT // 512
        fw.push()
        self.stream_begin(18, depth=2)
        merged = fw.sb([128, 8, NT], BF16, "merged")
        for d in range(8):
            slotM = self.next_block(base + 18 + 2 * d)
            self.prefetch_next()
            slotB = self.next_block(base + 19 + 2 * d)
            for tt in range(ntile):
                t0 = tt * 512
                acc = fw.rot([128, 512], F32, "macc")
                for n in range(4):
                    psg = self.nps()
                    for kc in range(8):
                        self.pe(lambda kc=kc, n=n, tt=tt: self.T.matmul(psg[:, :], slotM[:, kc * 512 + n * 128:kc * 512 + n * 128 + 128],
                                                                 self.hTs[tt][:, kc, :], start=(kc == 0), stop=(kc == 7)),
                                [slotM, self.hTs[tt]], [psg], signal=(kc == 7))
                    psp = self.nps()
                    for k4 in range(4):
                        self.pe(lambda k4=k4, n=n, tt=tt: self.T.matmul(psp[:, :], slotB[:, (n * 4 + k4) * 128:(n * 4 + k4) * 128 + 128],
                                                                 self.oTs[tt][n][:, k4, :], start=(k4 == 0), stop=(k4 == 3)),
                                [slotB, self.oTs[tt][n]], [psp], signal=(k4 == 3))
                    th = fw.rot([128, 512], F32, "mth")
                    self.act(lambda n=n, th=th, psg=psg: self.A.activation(th[:], psg[:], AF.Tanh, bias=self.bmh[:, d * 4 + n:d * 4 + n + 1],
                                                                           scale=0.5), [psg, self.bmh], [th])
                    if n == 0:
                        self.dve(lambda th=th, psp=psp: self.V.scalar_tensor_tensor(acc[:], th[:], 1.0, psp[:], ALU.add, ALU.mult),
                                 [th, psp], [acc])
                    else:
                        self.dve(lambda th=th, psp=psp: self.V.scalar_tensor_tensor(th[:], th[:], 1.0, psp[:], ALU.add, ALU.mult),
                                 [th, psp], [th])
                        self.dve(lambda th=th: self.V.tensor_tensor(acc[:], acc[:], th[:], ALU.add), [acc, th], [acc])
                self.act(lambda acc=acc, t0=t0: self.A.mul(merged[:, d, t0:t0 + 512], acc[:], 0.5), [acc], [merged])
            self.prefetch_next()
        for b in range(2):
            slotO = self.next_block(base + 34 + b)
            self.prefetch_next()
            for dd in range(4):
                dch = b * 4 + dd
                for tt in range(ntile):
                    t0 = tt * 512
                    ps = self.nps()
                    for kc in range(8):
                        self.pe(lambda kc=kc, dd=dd: self.T.matmul(ps[:, :], slotO[:, kc * 512 + dd * 128:kc * 512 + dd * 128 + 128],
                                                                   merged[:, kc, t0:t0 + 512], start=(kc == 0), stop=(kc == 7)),
                                [slotO, merged], [ps], signal=(kc == 7))
                    self.dve(lambda dch=dch, t0=t0, ps=ps: self.V.scalar_tensor_tensor(
                        self.xT[:, dch, x0 + t0:x0 + t0 + 512], ps[:, :], self.mod[:, 16 + dch, cc:cc + 1],
                        self.xT[:, dch, x0 + t0:x0 + t0 + 512], ALU.mult, ALU.add), [ps, self.mod, self.xT], [self.xT])
        fw.pop()

    def final_out(self):
        fw = self.fw
        for (x0, N, dst) in ((0, 512, ("yp", 0)), (512, 512, ("yp", 512)), (NPT, 512, ("ys", 0))):
            fw.push()
            rstd = fw.sb([128, 512], F32, "frstd")
            self.rstd_tile(self.xT, [self.xT[:, kc, x0:x0 + N] for kc in range(8)], float(D), rstd, N)
            stg = fw.sb([128, 8, 512], F32, "fstg")
            for kc in range(8):
                self.dve(lambda kc=kc: self.V.scalar_tensor_tensor(stg[:, kc, :], self.xT[:, kc, x0:x0 + N], self.spc("fin_g", kc),
                                                                   rstd[:, :], ALU.mult, ALU.mult), [self.xT, self.sp, rstd], [stg])
            dt = self.d[dst[0]]
            ncol = NPT if dst[0] == "yp" else NST
            dview = dt.ap().rearrange("(k p) t -> p k t", p=128)[:, :, dst[1]:dst[1] + N]
            fw.dma("sp", dt, dview, stg, stg[:], sem_owner=self.outsem)
            fw.pop()

    def shift_evac(self, ps, rows, N, nseq, mu1, muh, out_ap, out_buf, tanh=False):
        fw = self.fw
        zt = fw.rot([128, 512], F32, "shz")
        o32 = fw.rot([128, 512], F32, "sho")
        self.act(lambda: self.A.copy(zt[:rows, :N], ps[:rows, :N]), [ps], [zt])
        self.dve(lambda: self.V.tensor_scalar(o32[:rows, :N], zt[:rows, :N], mu1, None, ALU.mult), [zt, self.mu1], [o32])
        z3 = zt[:rows, :N].rearrange("p (s t) -> p s t", s=nseq)
        o3 = o32[:rows, :N].rearrange("p (s t) -> p s t", s=nseq)
        Tq = N // nseq
        self.dve(lambda: self.V.scalar_tensor_tensor(o3[:, :, 1:Tq], z3[:, :, 0:Tq - 1], muh, o3[:, :, 1:Tq], ALU.mult, ALU.add),
                 [zt, self.muh, o32], [o32])
        self.dve(lambda: self.V.scalar_tensor_tensor(o3[:, :, 0:Tq - 1], z3[:, :, 1:Tq], muh, o3[:, :, 0:Tq - 1], ALU.mult, ALU.add),
                 [zt, self.muh, o32], [o32])
        if tanh:
            self.act(lambda: self.A.activation(out_ap, o32[:rows, :N], AF.Tanh), [o32], [out_buf])
        else:
            self.act(lambda: self.A.copy(out_ap, o32[:rows, :N]), [o32], [out_buf])

    def load_rwkv_w(self, l, own):
        fw, d = self.fw, self.d
        ncol = 256 if own else 1024
        self.wup = fw.sb([64, ncol], BF16, "wup")
        self.aup = fw.sb([64, ncol], BF16, "aup")
        sw, sa = (d["wupo"], d["aupo"]) if own else (d["wup"], d["aup"])
        self.load(self.wup, self.wup[:, :], sw, sw[l, :, :])
        self.load(self.aup, self.aup[:, :], sa, sa[l, :, :])

    def phaseA_prompt(self, l, half):
        fw = self.fw
        base = l * BLK_PER_LAYER
        h0 = half * 512
        fw.push()
        self.load_rwkv_w(l, False)
        zz = [fw.sb([128, 4, 512], BF16, nm) for nm in ("zr", "zk", "zv")]
        lo = [fw.sb([64, 512], BF16, nm) for nm in ("twdf", "twdb", "adT")]
        GA2 = fw.sb([128, 4, 512], BF16, "GA2")
        fw.push()
        self.stream_begin(5)
        for which in range(3):
            slot = self.next_block(base + WIN_IDX[f"A{which}"])
            for c in range(4):
                ps = self.zmm(slot, 512, c * 128, 128, h0, 512)
                i = which * 4 + c
                self.shift_evac(ps, 128, 512, 2, self.mu1[:, i:i + 1], self.muh[:, i:i + 1], zz[which][:, c, :], zz[which])
            self.prefetch_next()
        slot = self.next_block(base + WIN_IDX["A3"])
        for i in range(3):
            ps = self.zmm(slot, 192, i * 64, 64, h0, 512)
            self.shift_evac(ps, 64, 512, 2, self.mu1[0:64, 12 + i:13 + i], self.muh[0:64, 12 + i:13 + i], lo[i][:, :], lo[i], tanh=(i < 2))
        self.prefetch_next()
        slot = self.next_block(base + WIN_IDX["A4"])
        for c in range(4):
            ps = self.zmm(slot, 512, c * 128, 128, h0, 512)
            self.silu2(ps, 128, 512, GA2[:, c, :], GA2)
        self.prefetch_next()
        fw.pop()
        for sq in range(2):
            for pair in range(4):
                t0 = sq * 256
                seqi = half * 2 + sq

                def yout(c, yfin, pair=pair, t0=t0):
                    cs = t0 + c * 128
                    self.dve(lambda: self.V.scalar_tensor_tensor(self.oTs[half][0][:, pair, cs:cs + 128], yfin[:, :], 0.5,
                                                                 GA2[:, pair, cs:cs + 128], ALU.mult, ALU.mult), [yfin, GA2], [self.oTs[half][0]])

                def stout(dd, ST, pair=pair, seqi=seqi):
                    so = self.d["stout"]
                    row = (((l * 2 + dd) * 4 + seqi) * 4 + pair) * 128
                    fw.dma("sp", so, so[row:row + 128, :], ST, ST[:, :], sem_owner=self.outsem)

                J = dict(T=256, r=(zz[0], lambda a, b, pair=pair, t0=t0: zz[0][:, pair, t0 + a:t0 + b]),
                         k=(zz[1], lambda a, b, pair=pair, t0=t0: zz[1][:, pair, t0 + a:t0 + b]),
                         v=(zz[2], lambda a, b, pair=pair, t0=t0: zz[2][:, pair, t0 + a:t0 + b]),
                         twd=[(lo[0], lambda a, b, t0=t0: lo[0][:, t0 + a:t0 + b]), (lo[1], lambda a, b, t0=t0: lo[1][:, t0 + a:t0 + b])],
                         ad=(lo[2], lambda a, b, t0=t0: lo[2][:, t0 + a:t0 + b]),
                         par=lambda nm, pair=pair: self.sp[:, rw_col(nm, pair):rw_col(nm, pair) + 1],
                         parh=lambda nm, pair=pair: self.rwh[:, RW_NAMES.index(nm) * 4 + pair:RW_NAMES.index(nm) * 4 + pair + 1],
                         parbufs=[self.sp, self.rwh],
                         wup=lambda dd, pair=pair: self.wup[:, dd * 512 + pair * 128:dd * 512 + pair * 128 + 128],
                         aup=lambda dd, pair=pair: self.aup[:, dd * 512 + pair * 128:dd * 512 + pair * 128 + 128],
                         st0=None, yout=yout, stout=stout, scoped=False)
                self.rwkv_job(J)
        fw.pop()

    def phaseA_contrib(self, l):
        fw = self.fw
        base = l * BLK_PER_LAYER
        self.GA2 = fw.sb([128, 4, 512], BF16, "GA2s")
        fw.push()
        self.stream_begin(5)
        for which, (part, row0) in enumerate((("rk", 0), ("rk", 512), ("vx", VX_OFF["v"]))):
            slot = self.next_block(base + WIN_IDX[f"A{which}"])
            for c in range(4):
                self.contrib_rows(l, slot, 512, c * 128, 128, part, row0 + c * 128)
            self.prefetch_next()
        slot = self.next_block(base + WIN_IDX["A3"])
        for i in range(3):
            self.contrib_rows(l, slot, 192, i * 64, 64, "vx", VX_OFF["lora"] + i * 64)
        self.prefetch_next()
        slot = self.next_block(base + WIN_IDX["A4"])
        for c in range(4):
            ps = self.zmm(slot, 512, c * 128, 128, 0, 512)
            self.silu2(ps, 128, 512, self.GA2[:, c, :], self.GA2)
        self.prefetch_next()
        fw.pop()

    def gather_rows(self, dst, dst_ap, src, idx_col):
        fw = self.fw
        idx = self.idx1 if idx_col < 12 else self.idx2
        col = idx_col if idx_col < 12 else idx_col - 12
        fw.dma("pool", dst, None, src, None, extra_reads=[idx],
               fn=lambda: self.G.indirect_dma_start(out=dst_ap, out_offset=None, in_=src.h.ap(),
                                                    in_offset=bass.IndirectOffsetOnAxis(ap=idx[:, col:col + 1], axis=0)))

    def phaseA_consume(self, l):
        fw = self.fw
        V, A, G = self.V, self.A, self.G
        fw.push()
        self.load_rwkv_w(l, True)
        spo = fw.sb([128, 12], F32, "spo")
        self.load(spo, spo[:, :], self.d["spo"], self.d["spo"][l, :, :])
        spoh = fw.sb([128, 4], F32, "spoh")
        self.dve(lambda: V.tensor_scalar(spoh[:, :], spo[:, 0:4], 0.5, None, ALU.mult), [spo], [spoh])
        mu1o = fw.sb([128, 3], F32, "mu1o")
        muho = fw.sb([128, 3], F32, "muho")
        self.dve(lambda: V.tensor_scalar(mu1o[:, :], spo[:, 9:12], -1.0, 1.0, ALU.mult, ALU.add), [spo], [mu1o])
        self.dve(lambda: V.tensor_scalar(muho[:, :], spo[:, 9:12], 0.5, None, ALU.mult), [spo], [muho])
        T = DSEQ
        zz = [fw.sb([128, T], BF16, nm) for nm in ("sr", "sk", "sv")]
        lo = [fw.sb([64, T], BF16, nm) for nm in ("stwf", "stwb", "sad")]
        fw.push()
        raw = fw.sb([128, T], BF16, "sraw")
        o32 = fw.sb([128, T], F32, "so32")

        def shift_full(rows, src, m1, mh, dst, tanh=False):
            self.dve(lambda: V.tensor_scalar(o32[:rows, :], src[:rows, :], m1, None, ALU.mult), [src, mu1o, self.mu1], [o32])
            self.dve(lambda: V.scalar_tensor_tensor(o32[:rows, 1:T], src[:rows, 0:T - 1], mh, o32[:rows, 1:T], ALU.mult, ALU.add),
                     [src, muho, self.muh, o32], [o32])
            self.dve(lambda: V.scalar_tensor_tensor(o32[:rows, 0:T - 1], src[:rows, 1:T], mh, o32[:rows, 0:T - 1], ALU.mult, ALU.add),
                     [src, muho, self.muh, o32], [o32])
            if tanh:
                self.act(lambda: A.activation(dst[:rows, :], o32[:rows, :], AF.Tanh), [o32], [dst])
            else:
                self.act(lambda: A.copy(dst[:rows, :], o32[:rows, :]), [o32], [dst])

        for which in range(3):
            src = self.ag1_out[l]["rk" if which < 2 else "vx"]
            for q in range(4):
                self.gather_rows(raw, raw[:, q * 512:(q + 1) * 512], src, which * 4 + q)
            shift_full(128, raw, mu1o[:, which:which + 1], muho[:, which:which + 1], zz[which])
        agv = self.ag1_out[l]["vx"]
        for i in range(3):
            for q in range(4):
                r0 = q * 864 + VX_OFF["lora"] + i * 64
                fw.dma("sp", raw, raw[0:64, q * 512:(q + 1) * 512], agv, agv[r0:r0 + 64, :])
            shift_full(64, raw, self.mu1[0:64, 12 + i:13 + i], self.muh[0:64, 12 + i:13 + i], lo[i], tanh=(i < 2))
        fw.pop()
        stg = [None]

        def yout(c, yfin):
            q, cc = c // 4, c % 4
            if cc == 0:
                stg[0] = fw.rot([128, 512], BF16, "ystg", n=2)
            st = stg[0]
            self.act(lambda: A.copy(st[:, cc * 128:(cc + 1) * 128], yfin[:, :]), [yfin], [st])
            if cc == 3:
                ag = self.ag2_in[l]
                fw.dma("sp", ag, ag[q * 128:(q + 1) * 128, :], st, st[:, :])

        pidx = {nm: i for i, nm in enumerate(RW_NAMES)}
        J = dict(T=T, r=(zz[0], lambda a, b: zz[0][:, a:b]), k=(zz[1], lambda a, b: zz[1][:, a:b]), v=(zz[2], lambda a, b: zz[2][:, a:b]),
                 twd=[(lo[0], lambda a, b: lo[0][:, a:b]), (lo[1], lambda a, b: lo[1][:, a:b])],
                 ad=(lo[2], lambda a, b: lo[2][:, a:b]),
                 par=lambda nm: spo[:, pidx[nm]:pidx[nm] + 1],
                 parh=lambda nm: spoh[:, pidx[nm]:pidx[nm] + 1],
                 parbufs=[spo, spoh],
                 wup=lambda dd: self.wup[:, dd * 128:(dd + 1) * 128],
                 aup=lambda dd: self.aup[:, dd * 128:(dd + 1) * 128],
                 st0=lambda dd: (self.d["st0"], self.d["st0"][l, dd, :, :]), yout=yout, stout=None, seg=256, segpar=False)
        self.rwkv_job(J)
        self.allgather(self.ag2_out[l], self.ag2_in[l])
        fw.pop()

    def phaseA_final(self, l):
        fw = self.fw
        fw.push()
        for r in range(4):
            ya = fw.rot([128, 512], BF16, "ya", n=2)
            self.gather_rows(ya, ya[:, :], self.ag2_out[l], 12 + r)
            self.dve(lambda r=r, ya=ya: self.V.scalar_tensor_tensor(self.oTs[0][0][:, r, 0:512], ya[:, :], 0.5, self.GA2[:, r, :], ALU.mult, ALU.mult),
                     [ya, self.GA2], [self.oTs[0][0]])
        fw.pop()

    def rwkv_job(self, J):
        fw = self.fw
        V, A, G, T_ = self.V, self.A, self.G, self.T
        T = J["T"]
        nch = T // 128
        SEG = J.get("seg", 256)
        nseg = T // SEG
        ncs = SEG // 128
        rB, rf = J["r"]
        kB, kf = J["k"]
        vB, vf = J["v"]
        adB, adf = J["ad"]
        par, parh, pbufs = J["par"], J["parh"], J["parbufs"]
        self.ps_range = (0, 6)
        scoped = J.get("scoped", True)
        jpush = (lambda: fw.push()) if scoped else (lambda: None)
        jpop = (lambda: fw.pop()) if scoped else (lambda: None)
        jt = (lambda shp, dt, nm: fw.sb(shp, dt, nm)) if scoped else (lambda shp, dt, nm: fw.rot(shp, dt, "J" + nm, n=1))
        jpush()
        kap = jt([128, T], BF16, "kap")
        Vtm = jt([128, nch, 128], BF16, "Vtm")
        Yacc = jt([128, nch, 128], F32, "Yacc")
        Bacc = jt([128, T], F32, "Bacc")
        ST = [jt([128, 64], F32, f"ST{dd}") for dd in range(2)]
        STb = [jt([128, 64], BF16, f"STb{dd}") for dd in range(2)]
        KW = min(T, 512)
        self.pool(lambda: G.memset(Yacc[:, :, :], 0.0), [], [Yacc])
        self.pool(lambda: G.memset(Bacc[:, :], 0.0), [], [Bacc])
        jpush()
        for p0 in range(0, T, 512):
            N = min(512, T - p0)
            kk = fw.rot([128, KW], F32, "kk", n=(2 if scoped else 1))
            sq = fw.rot([128, KW], BF16, "kksq", n=(2 if scoped else 1))
            self.dve(lambda: V.tensor_scalar(kk[:, :N], kf(p0, p0 + N), par("k_k"), None, ALU.mult), [kB] + pbufs, [kk])
            self.act(lambda: A.activation(sq[:, :N], kk[:, :N], AF.Square), [kk], [sq])
            ps = self.nps()
            self.pe(lambda: T_.matmul(ps[:, :N], self.bones[:, :], sq[:, :N], start=True, stop=True), [self.bones, sq], [ps])
            t = fw.rot([128, KW], F32, "kkt", n=(2 if scoped else 1))
            self.dve(lambda: V.tensor_scalar(t[:, :N], ps[:, :N], 1e-24, None, ALU.max), [ps], [t])
            self.rsqrt(t, t[:, :N], t, t[:, :N])
            self.dve(lambda: V.tensor_tensor(kap[:, p0:p0 + N], kk[:, :N], t[:, :N], ALU.mult), [kk, t], [kap])
        import os
        STOP = int(os.environ.get("RWKV_STOP", "99"))
        if STOP <= 1:
            jpop(); jpop(); return
        for c in range(nch):
            self.transpose_to(vB, vf(c * 128, c * 128 + 128), Vtm, Vtm[:, c, :], eng=("act" if c % 2 else "dve"))
        jpop()
        if STOP <= 2:
            jpop(); return
        jpush()
        for dd in range(2):
            if J["st0"] is None:
                self.pool(lambda dd=dd: G.memset(ST[dd][:, :], 0.0), [], [ST[dd]])
            else:
                src, sap = J["st0"](dd)
                fw.dma("sp", ST[dd], ST[dd][:, :], src, sap)
            self.act(lambda dd=dd: A.copy(STb[dd][:, :], ST[dd][:, :]), [ST[dd]], [STb[dd]])
        MK = self.mask
        def rw_segment(dd, sg, res, segpar):
            sfx = "fb"[dd]
            twB, twf = J["twd"][dd]
            s0 = sg * SEG
            N = SEG
            f32t = lambda nm: fw.rot([128, SEG], F32, nm + (str(dd) if segpar else ""), n=1)
            a = f32t("ra")
            ps = self.nps()
            self.pe(lambda: T_.matmul(ps[:, :N], J["aup"](dd), adf(s0, s0 + N), start=True, stop=True), [self.aup, adB], [ps])
            self.act(lambda: A.activation(a[:, :], ps[:, :N], AF.Tanh, bias=parh("a0_" + sfx), scale=0.5), [ps] + pbufs, [a])
            yield
            self.dve(lambda: V.tensor_scalar(a[:, :], a[:, :], 0.5, 0.5, ALU.mult, ALU.add), [a], [a])
            kt = f32t("rkt")
            self.dve(lambda: V.tensor_scalar(kt[:, :], a[:, :], 1.0, par("k_a"), ALU.subtract, ALU.mult), [a] + pbufs, [kt])
            self.dve(lambda: V.scalar_tensor_tensor(kt[:, :], kt[:, :], 1.0, kf(s0, s0 + N), ALU.add, ALU.mult), [kt, kB], [kt])
            b = f32t("rb")
            self.dve(lambda: V.tensor_tensor(b[:, :], a[:, :], kap[:, s0:s0 + N], ALU.mult), [a, kap], [b])
            lw = f32t("rlw")
            ps = self.nps()
            self.pe(lambda: T_.matmul(ps[:, :N], J["wup"](dd), twf(s0, s0 + N), start=True, stop=True), [self.wup, twB], [ps])
            self.act(lambda: A.activation(lw[:, :], ps[:, :N], AF.Tanh, bias=parh("w0_" + sfx), scale=0.5), [ps] + pbufs, [lw])
            yield
            self.dve(lambda: V.tensor_scalar(lw[:, :], lw[:, :], -0.3032653298563167, -0.3032653298563167, ALU.mult, ALU.add), [lw], [lw])
            rkr = fw.rot([128, SEG], BF16, "rkr" + str(dd), n=1)
            self.dve(lambda: V.scalar_tensor_tensor(rkr[:, :], kt[:, :], par("r_k"), rf(s0, s0 + N), ALU.mult, ALU.mult),
                     [kt, rB] + pbufs, [rkr])
            ps = self.nps()
            self.pe(lambda: T_.matmul(ps[:, :N], self.bones[:, :], rkr[:, :], start=True, stop=True), [self.bones, rkr], [ps])
            self.dve(lambda: V.tensor_tensor(Bacc[:, s0:s0 + N], Bacc[:, s0:s0 + N], ps[:, :N], ALU.add), [Bacc, ps], [Bacc])
            P = f32t("rP")
            for c in range(ncs):
                self.dve(lambda c=c: V.tensor_tensor_scan(P[:, c * 128:(c + 1) * 128], self.onesf[:, :], lw[:, c * 128:(c + 1) * 128],
                                                          0.0, ALU.mult, ALU.add), [self.onesf, lw], [P])
            Q = f32t("rQ")
            R = f32t("rR")
            self.dve(lambda: V.tensor_tensor(Q[:, :], P[:, :], lw[:, :], ALU.subtract), [P, lw], [Q])
            for c in range(ncs):
                self.dve(lambda c=c: V.tensor_scalar(R[:, c * 128:(c + 1) * 128], P[:, c * 128:(c + 1) * 128], -1.0,
                                                     P[:, c * 128 + 127:c * 128 + 128], ALU.mult, ALU.add), [P], [R])
            gL = fw.rot([128, 2], F32, "gL" + str(dd), n=2)
            self.act(lambda: A.activation(gL[:, 0:ncs], P[:, 127:SEG:128], AF.Exp), [P], [gL])
            yield
            if dd == 0:
                srcs = [(P, -1.0, a), (Q, 1.0, Q), (P, 1.0, P), (R, 1.0, R)]
            else:
                RL = f32t("rRL")
                self.dve(lambda: V.tensor_tensor(RL[:, :], R[:, :], lw[:, :], ALU.add), [R, lw], [RL])
                srcs = [(RL, -1.0, a), (R, 1.0, R), (RL, 1.0, RL), (Q, 1.0, Q)]
            E = [None] * 4
            for i, (sb_, sc_, dst_) in enumerate(srcs):
                self.act(lambda i=i, sb_=sb_, sc_=sc_, dst_=dst_: A.activation(dst_[:, :], sb_[:, :], AF.Exp, scale=sc_), [sb_], [dst_])
                E[i] = dst_
            Kd2 = fw.rot([128, 2, SEG], BF16, "Kd2" + str(dd), n=1)
            Bd2 = fw.rot([128, 2, SEG], BF16, "Bd2" + str(dd), n=1)
            KL = fw.rot([128, SEG], BF16, "KL" + str(dd), n=1)
            BL = fw.rot([128, SEG], BF16, "BL" + str(dd), n=1)
            KqRq2 = fw.rot([128, 2, ncs, 2, 128], BF16, "KqRq2" + str(dd), n=1)
            hm = self.hm
            kap3 = kap[:, s0:s0 + N].rearrange("p (c t) -> p c t", c=ncs)
            r3 = rf(s0, s0 + N).rearrange("p (c t) -> p c t", c=ncs)
            for h in range(2):
                self.dve(lambda h=h: V.scalar_tensor_tensor(Kd2[:, h, :], kt[:, :], hm[:, h:h + 1], E[0][:, :], ALU.mult, ALU.mult), [kt, hm, E[0]], [Kd2])
                self.dve(lambda h=h: V.scalar_tensor_tensor(Bd2[:, h, :], b[:, :], hm[:, h:h + 1], E[0][:, :], ALU.mult, ALU.mult), [b, hm, E[0]], [Bd2])
                self.dve(lambda h=h: V.scalar_tensor_tensor(KqRq2[:, h, :, 0, :], kap3, hm[:, h:h + 1], E[1][:, :].rearrange("p (c t) -> p c t", c=ncs),
                                                            ALU.mult, ALU.mult), [kap, hm, E[1]], [KqRq2])
                self.dve(lambda h=h: V.scalar_tensor_tensor(KqRq2[:, h, :, 1, :], r3, hm[:, h:h + 1], E[2][:, :].rearrange("p (c t) -> p c t", c=ncs),
                                                            ALU.mult, ALU.mult), [rB, hm, E[2]], [KqRq2])
            self.dve(lambda: V.tensor_tensor(KL[:, :], kt[:, :], E[3][:, :], ALU.mult), [kt, E[3]], [KL])
            self.dve(lambda: V.tensor_tensor(BL[:, :], b[:, :], E[3][:, :], ALU.mult), [b, E[3]], [BL])

            res.update(dict(Kd2=Kd2, Bd2=Bd2, KL=KL, BL=BL, KqRq2=KqRq2, gL=gL, sg=sg))
            yield

        def rw_pre(dd, c, Pd, res):
            sfx2 = f"{dd}{c}"
            mk0 = dd * 1280
            Kd2, Bd2, KqRq2 = Pd["Kd2"], Pd["Bd2"], Pd["KqRq2"]
            cs = slice(c * 128, (c + 1) * 128)
            Am = [fw.rot([128, 512], BF16, f"Am{h}_{sfx2}", n=1) for h in range(2)]
            psB = self.nps()
            for h in range(2):
                psA = self.nps()
                rhsA = KqRq2[:, h, c, :, :].rearrange("p a t -> p (a t)")
                self.pe(lambda h=h, psA=psA, rhsA=rhsA: T_.matmul(psA[:, 0:256], Kd2[:, h, cs], rhsA, start=True, stop=True),
                        [Kd2, KqRq2], [psA], signal=False)
                self.pe(lambda h=h, psA=psA, rhsA=rhsA: T_.matmul(psA[:, 256:512], Bd2[:, h, cs], rhsA, start=True, stop=True),
                        [Bd2, KqRq2], [psA])
                self.dve(lambda h=h, psA=psA: V.tensor_tensor(Am[h][:, :], psA[:, :], MK[:, mk0:mk0 + 512], ALU.mult), [psA, MK], [Am[h]])
                self.pe(lambda h=h: T_.matmul(psB[:, h * 128:(h + 1) * 128], KqRq2[:, h, c, 0, :], Bd2[:, h, cs], start=True, stop=True),
                        [KqRq2, Bd2], [psB], signal=(h == 1))
            PT = [fw.rot([128, 2, 128], BF16, f"PT{i}_{sfx2}", n=1) for i in range(2)]
            PX = [fw.rot([128, 2, 256], BF16, f"PX{i}_{sfx2}", n=1) for i in range(2)]
            C64T = fw.rot([128, 2, 128], BF16, "C64T" + sfx2, n=1)
            C128T = fw.rot([128, 2, 128], BF16, "C128T" + sfx2, n=1)
            f2 = lambda t: t[:, :, :].rearrange("p a t -> p (a t)")
            self.dve(lambda: V.tensor_tensor(f2(PT[0]), psB[:, 0:256], MK[:, mk0 + 512:mk0 + 768], ALU.mult), [psB, MK], [PT[0]])
            self.dve(lambda: V.tensor_tensor(f2(C64T), psB[:, 0:256], MK[:, mk0 + 768:mk0 + 1024], ALU.mult), [psB, MK], [C64T])
            self.dve(lambda: V.tensor_tensor(f2(C128T), psB[:, 0:256], MK[:, mk0 + 1024:mk0 + 1280], ALU.mult), [psB, MK], [C128T])
            for h in range(2):
                self.act(lambda h=h: A.copy(PX[0][:, h, 0:128], Am[h][:, 256:384]), [Am[h]], [PX[0]])
            yield
            cur = 0
            Xb = fw.rot([128, 2, 128], BF16, "Xb32" + sfx2, n=1)
            for j in range(1, 6):
                nxt = 1 - cur
                if j == 1:
                    ps = self.nps()
                    pst_ = self.nps()
                    for h in range(2):
                        self.pe(lambda h=h, ps=ps, cur=cur: T_.matmul(ps[:, h * 256:h * 256 + 128], PT[cur][:, h, :], PX[cur][:, h, 0:128],
                                                                      start=True, stop=True), [PT[cur], PX[cur]], [ps], signal=(h == 1))
                        self.pe(lambda h=h, pst_=pst_, cur=cur: T_.matmul(pst_[:, h * 128:(h + 1) * 128], PX[cur][:, h, 0:128], PT[cur][:, h, :],
                                                                          start=True, stop=True), [PT[cur], PX[cur]], [pst_], signal=(h == 1))
                    for h in range(2):
                        self.dve(lambda h=h, cur=cur, nxt=nxt: V.tensor_tensor(PX[nxt][:, h, 128:256], PX[cur][:, h, 0:128], self.ident[:, :], ALU.add),
                                 [PX[cur], self.ident], [PX[nxt]])
                    self.act(lambda ps=ps, nxt=nxt: A.copy(PX[nxt][:, :, 0:128], ps[:, :].rearrange("p (a t) -> p a t", a=2)[:, :, 0:128]),
                             [ps], [PX[nxt]])
                    self.act(lambda pst_=pst_, nxt=nxt: A.copy(f2(PT[nxt]), pst_[:, 0:256]), [pst_], [PT[nxt]])
                elif j < 5:
                    ps = self.nps()
                    pst_ = self.nps()
                    for h in range(2):
                        self.pe(lambda h=h, ps=ps, cur=cur: T_.matmul(ps[:, h * 256:(h + 1) * 256], PT[cur][:, h, :], PX[cur][:, h, :],
                                                                      start=True, stop=True), [PT[cur], PX[cur]], [ps], signal=(h == 1))
                        self.pe(lambda h=h, pst_=pst_, cur=cur: T_.matmul(pst_[:, h * 128:(h + 1) * 128], PX[cur][:, h, 0:128], PT[cur][:, h, :],
                                                                          start=True, stop=True), [PT[cur], PX[cur]], [pst_], signal=(h == 1))
                    ps3 = ps[:, :].rearrange("p (a t) -> p a t", a=2)
                    self.act(lambda ps3=ps3, ps=ps, nxt=nxt: A.copy(PX[nxt][:, :, 0:128], ps3[:, :, 0:128]), [ps], [PX[nxt]])
                    self.dve(lambda ps3=ps3, ps=ps, cur=cur, nxt=nxt: V.tensor_tensor(PX[nxt][:, :, 128:256], ps3[:, :, 128:256], PX[cur][:, :, 128:256], ALU.add),
                             [ps, PX[cur]], [PX[nxt]])
                    self.act(lambda pst_=pst_, nxt=nxt: A.copy(f2(PT[nxt]), pst_[:, 0:256]), [pst_], [PT[nxt]])
                else:
                    ps = self.nps()
                    for h in range(2):
                        self.pe(lambda h=h, ps=ps, cur=cur: T_.matmul(ps[:, h * 128:(h + 1) * 128], PT[cur][:, h, :], PX[cur][:, h, 128:256],
                                                                      start=True, stop=True), [PT[cur], PX[cur]], [ps], signal=(h == 1))
                    self.dve(lambda ps=ps, cur=cur: V.tensor_tensor(Xb[:, :, :], ps[:, 0:256].rearrange("p (a t) -> p a t", a=2), PX[cur][:, :, 128:256], ALU.add),
                             [ps, PX[cur]], [Xb])
                cur = nxt
                yield
            TT = None
            for lvl, CT in enumerate((C64T, C128T)):
                XT = fw.rot([128, 2, 128], BF16, "XTm" + sfx2, n=1)
                Zt = fw.rot([128, 2, 128], BF16, "Ztm" + sfx2, n=1)
                ptt = self.pst[self.pst_rr % 2]
                self.pst_rr += 1
                for h in range(2):
                    self.pe(lambda h=h, ptt=ptt, Xb=Xb: T_.transpose(ptt[:, h * 128:(h + 1) * 128], Xb[:, h, :], self.ident[:, :]),
                            [Xb, self.ident], [ptt], signal=(h == 1))
                self.act(lambda ptt=ptt, XT=XT: A.copy(f2(XT), ptt[:, 0:256]), [ptt], [XT])
                psz = self.nps()
                for h in range(2):
                    self.pe(lambda h=h, psz=psz, CT=CT, Xb=Xb: T_.matmul(psz[:, h * 128:(h + 1) * 128], CT[:, h, :], Xb[:, h, :], start=True, stop=True),
                            [CT, Xb], [psz], signal=(h == 1))
                self.dve(lambda psz=psz, Zt=Zt: V.tensor_copy(f2(Zt), psz[:, 0:256]), [psz], [Zt])
                psw = self.nps()
                for h in range(2):
                    self.pe(lambda h=h, psw=psw, XT=XT, Zt=Zt: T_.matmul(psw[:, h * 128:(h + 1) * 128], XT[:, h, :], Zt[:, h, :], start=True, stop=True),
                            [XT, Zt], [psw], signal=(h == 1))
                Xn = fw.rot([128, 2, 128], BF16, ("Xb64" if lvl == 0 else "TT") + sfx2, n=1)
                self.dve(lambda psw=psw, Xn=Xn, Xb=Xb: V.tensor_tensor(f2(Xn), psw[:, 0:256], f2(Xb), ALU.add), [psw, Xb], [Xn])
                Xb = Xn
                yield
            TT = Xb

            res["Am"] = Am
            res["TT"] = TT
            yield

        def rw_seq(dd, Pd, pres):
            KL, BL, KqRq2, gL, sg = Pd["KL"], Pd["BL"], Pd["KqRq2"], Pd["gL"], Pd["sg"]
            chunks = list(range(ncs)) if dd == 0 else list(reversed(range(ncs)))
            for c in chunks:
                cg = sg * ncs + c
                cs = slice(c * 128, (c + 1) * 128)
                Am, TT = pres[(dd, c)]["Am"], pres[(dd, c)]["TT"]
                Sb = STb[dd]
                psG = self.nps()
                for h in range(2):
                    hs = slice(64 * h, 64 * h + 64)
                    vs = slice(64 * h, 64 * h + 64)
                    self.pe(lambda h=h, vs=vs: T_.matmul(psG[:, vs], KqRq2[:, h, c, 0, :], Sb[:, :], start=(h == 0), stop=False, skip_group_check=True),
                            [KqRq2, Sb], [psG], signal=False)
                    self.pe(lambda h=h, vs=vs: T_.matmul(psG[:, vs], Am[h][:, 0:128], Vtm[:, cg, vs], start=False, stop=(h == 1), skip_group_check=True),
                            [Am[h], Vtm], [psG], signal=(h == 1))
                Gn = fw.rot([128, 128], BF16, "Gn" + str(dd), n=1)
                self.act(lambda: A.mul(Gn[:, :], psG[:, 0:128], -1.0), [psG], [Gn])
                yield
                psU = self.nps()
                for h in range(2):
                    vs = slice(64 * h, 64 * h + 64)
                    self.pe(lambda h=h, vs=vs: T_.matmul(psU[:, vs], TT[:, h, :], Gn[:, vs], start=(h == 0), stop=(h == 1), skip_group_check=True),
                            [TT, Gn], [psU], signal=(h == 1))
                U = fw.rot([128, 128], BF16, "U" + str(dd), n=1)
                self.dve(lambda: V.tensor_copy(U[:, :], psU[:, 0:128]), [psU], [U])
                yield
                yield
                psY = self.nps()
                for h in range(2):
                    hs = slice(64 * h, 64 * h + 64)
                    vs = slice(64 * h, 64 * h + 64)
                    self.pe(lambda h=h, vs=vs: T_.matmul(psY[:, vs], KqRq2[:, h, c, 1, :], Sb[:, :], start=(h == 0), stop=False, skip_group_check=True),
                            [KqRq2, Sb], [psY], signal=False)
                    self.pe(lambda h=h, vs=vs: T_.matmul(psY[:, vs], Am[h][:, 128:256], Vtm[:, cg, vs], start=False, stop=False, skip_group_check=True),
                            [Am[h], Vtm], [psY], signal=False)
                    self.pe(lambda h=h, vs=vs: T_.matmul(psY[:, vs], Am[h][:, 384:512], U[:, vs], start=False, stop=(h == 1), skip_group_check=True),
                            [Am[h], U], [psY], signal=(h == 1))
                self.dve(lambda: V.tensor_tensor(Yacc[:, cg, :], Yacc[:, cg, :], psY[:, 0:128], ALU.add), [Yacc, psY], [Yacc])
                yield
                yield
                KLt = fw.rot([128, 128], BF16, "KLt" + str(dd), n=1)
                BLt = fw.rot([128, 128], BF16, "BLt" + str(dd), n=1)
                self.transpose_to(KL, KL[:, cs], KLt, KLt[:, :], eng="act")
                self.transpose_to(BL, BL[:, cs], BLt, BLt[:, :], eng="dve")
                psS = self.nps()
                for h in range(2):
                    hs = slice(64 * h, 64 * h + 64)
                    vs = slice(64 * h, 64 * h + 64)
                    self.pe(lambda h=h, hs=hs, vs=vs: T_.matmul(psS[hs, 0:64], KLt[:, hs], Vtm[:, cg, vs], start=True, stop=False),
                            [KLt, Vtm], [psS], signal=False)
                    self.pe(lambda h=h, hs=hs, vs=vs: T_.matmul(psS[hs, 0:64], BLt[:, hs], U[:, vs], start=False, stop=True),
                            [BLt, U], [psS], signal=(h == 1))
                self.dve(lambda: V.scalar_tensor_tensor(ST[dd][:, :], ST[dd][:, :], gL[:, c:c + 1], psS[:, 0:64], ALU.mult, ALU.add),
                         [ST[dd], gL, psS], [ST[dd]])
                self.act(lambda: A.copy(STb[dd][:, :], ST[dd][:, :]), [ST[dd]], [STb[dd]])

        def run_rr(gens):
            gens = list(gens)
            while gens:
                for g in list(gens):
                    try:
                        next(g)
                    except StopIteration:
                        gens.remove(g)

        for step in range(nseg):
            sgs = (step, nseg - 1 - step)
            Pd = [{}, {}]
            segpar = J.get("segpar", False)
            if segpar:
                run_rr([rw_segment(dd, sgs[dd], Pd[dd], True) for dd in range(2)])
            else:
                for dd in range(2):
                    for _ in rw_segment(dd, sgs[dd], Pd[dd], False):
                        pass
            if STOP <= 3:
                continue
            pres = {(dd, c): {} for dd in range(2) for c in range(ncs)}
            run_rr([rw_pre(dd, c, Pd[dd], pres[(dd, c)]) for dd in range(2) for c in range(ncs)])
            if STOP <= 5:
                continue
            run_rr([rw_seq(dd, Pd[dd], pres) for dd in range(2)])
        if J["stout"] is not None:
            for dd in range(2):
                J["stout"](dd, ST[dd])
        jpop()
        if STOP <= 6:
            jpop(); return
        n2 = nch * 2
        sums = jt([128, n2], F32, "gsum")
        ssq = jt([128, n2], F32, "gssq")
        Ysq = jt([128, nch * 128], F32, "Ysq") if scoped else fw.rot([128, KW], F32, "kk", n=1)
        Yf = Yacc[:, :, :].rearrange("p c x -> p (c x)")
        self.dve(lambda: V.tensor_reduce(sums[:, :], Yf.rearrange("p (g x) -> p g x", x=64), AX.X, ALU.add), [Yacc], [sums])
        self.act(lambda: A.activation(Ysq[:, :], Yf, AF.Square), [Yacc], [Ysq])
        self.dve(lambda: V.tensor_reduce(ssq[:, :], Ysq[:, :].rearrange("p (g x) -> p g x", x=64), AX.X, ALU.add), [Ysq], [ssq])
        mean = jt([128, n2], F32, "gmean")
        var = jt([128, n2], F32, "gvar2")
        self.dve(lambda: V.tensor_scalar(mean[:, :], sums[:, :], 1.0 / 64, None, ALU.mult), [sums], [mean])
        self.dve(lambda: V.tensor_tensor(var[:, :], mean[:, :], mean[:, :], ALU.mult), [mean], [var])
        self.dve(lambda: V.scalar_tensor_tensor(var[:, :], ssq[:, :], 1.0 / 64, var[:, :], ALU.mult, ALU.subtract), [ssq, var], [var])
        self.dve(lambda: V.tensor_scalar(var[:, :], var[:, :], GN_EPS, None, ALU.add), [var], [var])
        self.rsqrt(var, var[:, :], var, var[:, :])
        yn = jt([128, nch, 128], BF16, "yn")
        for c in range(nch):
            for h in range(2):
                g = c * 2 + h
                self.dve(lambda c=c, h=h, g=g: V.tensor_scalar(yn[:, c, h * 64:(h + 1) * 64], Yacc[:, c, h * 64:(h + 1) * 64], mean[:, g:g + 1], var[:, g:g + 1],
                                                               ALU.subtract, ALU.mult), [Yacc, mean, var], [yn])
        for c in range(nch):
            pt = self.pst[self.pst_rr % 2]
            self.pst_rr += 1
            self.pe(lambda c=c, pt=pt: T_.transpose(pt[:, :128], yn[:, c, :], self.ident[:, :]), [yn, self.ident], [pt])
            yT = fw.rot([128, 128], F32, "yT", n=(2 if scoped else 1))
            self.act(lambda pt=pt, yT=yT: A.activation(yT[:, :], pt[:, :128], AF.Identity, bias=par("ln_b"), scale=par("ln_g")), [pt] + pbufs, [yT])
            bo = fw.rot([128, 128], F32, "bo", n=(2 if scoped else 1))
            self.dve(lambda c=c, bo=bo: V.tensor_tensor(bo[:, :], Bacc[:, c * 128:(c + 1) * 128], vf(c * 128, (c + 1) * 128), ALU.mult), [Bacc, vB], [bo])
            yfin = fw.rot([128, 128], F32, "yfin", n=(2 if scoped else 1))
            self.dve(lambda yT=yT, bo=bo, yfin=yfin: V.tensor_tensor(yfin[:, :], yT[:, :], bo[:, :], ALU.add), [yT, bo], [yfin])
            J["yout"](c, yfin)
        jpop()
        self.ps_range = (0, 4)

    def load_mla_w(self, l):
        fw, d = self.fw, self.d
        self.wq = fw.sb([128, 2 * 8 * 96], BF16, "wq")
        self.wqs = fw.sb([128, 2 * 8 * 32], BF16, "wqs")
        self.wkk = fw.sb([128, 512], BF16, "wkk")
        self.wkv = fw.sb([128, 512], BF16, "wkv")
        for nm in ("wq", "wqs", "wkk", "wkv"):
            t = getattr(self, nm)
            self.load(t, t[:], d[nm], d[nm][l, :, :])

    def mla_front(self, l, h0, rope, GB2, Qh, ckv_f, ckv_b, kr_f, kr_b):
        fw = self.fw
        base = l * BLK_PER_LAYER
        W = WIN_W["B0"]
        slot = self.next_block(base + WIN_IDX["B0"])
        qd = fw.sb([128, 2, 512], F32, "qd")
        kvd = fw.sb([128, 512], F32, "kvd")
        for c in range(2):
            ps = self.zmm(slot, W, c * 128, 128, h0, 512)
            self.act(lambda c=c, ps=ps: self.A.copy(qd[:, c, :], ps[:, :]), [ps], [qd])
        ps = self.zmm(slot, W, 256, 128, h0, 512)
        self.dve(lambda ps=ps: self.V.tensor_copy(kvd[:, :], ps[:, :]), [ps], [kvd])
        pk = self.zmm(slot, W, 384, 32, h0, 512, prow=64)
        R = self.rope
        if rope:
            pks = self.zmm(slot, W, 416, 32, h0, 512, prow=64)
            t1 = fw.sb([96, 512], F32, "krt1")
            self.dve(lambda: self.V.tensor_tensor(t1[64:96, :], pk[64:96, :], R[64:96, 0:512], ALU.mult), [pk, R], [t1])
            self.dve(lambda: self.V.tensor_tensor(kr_f[64:96, :], pks[64:96, :], R[64:96, 512:1024], ALU.mult), [pks, R], [kr_f])
            self.dve(lambda: self.V.tensor_tensor(kr_f[64:96, :], kr_f[64:96, :], t1[64:96, :], ALU.add), [kr_f, t1], [kr_f])
        else:
            self.act(lambda: self.A.copy(kr_f[64:96, :], pk[64:96, :]), [pk], [kr_f])
        self.act(lambda: self.A.copy(kr_b[64:96, :], kr_f[64:96, :]), [kr_f], [kr_b])
        self.prefetch_next()
        slot = self.next_block(base + WIN_IDX["B1"])
        for c in range(4):
            ps = self.zmm(slot, 512, c * 128, 128, h0, 512)
            self.silu2(ps, 128, 512, GB2[:, c, :], GB2)
        self.prefetch_next()
        rq = fw.sb([128, 512], F32, "rq")
        self.rstd_tile(qd, [qd[:, c, :] for c in range(2)], 256.0, rq, 512)
        qn = fw.sb([128, 2, 512], BF16, "qn")
        for c in range(2):
            self.dve(lambda c=c: self.V.scalar_tensor_tensor(qn[:, c, :], qd[:, c, :], self.spc("qn", c), rq[:, :], ALU.mult, ALU.mult),
                     [qd, self.sp, rq], [qn])
        rk = fw.sb([128, 512], F32, "rkv")
        self.rstd_tile(kvd, [kvd[:, :]], 128.0, rk, 512)
        self.dve(lambda: self.V.scalar_tensor_tensor(ckv_f[:, :], kvd[:, :], self.spc("kvn", 0), rk[:, :], ALU.mult, ALU.mult),
                 [kvd, self.sp, rk], [ckv_f])
        self.act(lambda: self.A.copy(ckv_b[:, :], ckv_f[:, :]), [ckv_f], [ckv_b])
        for h in range(8):
            ps = self.nps()
            for c in range(2):
                self.pe(lambda c=c, h=h, ps=ps: self.T.matmul(ps[:96, :], self.wq[:, (c * 8 + h) * 96:(c * 8 + h) * 96 + 96], qn[:, c, :],
                                                             start=(c == 0), stop=(c == 1)), [self.wq, qn], [ps], signal=(c == 1))
            if rope:
                ps2 = self.nps()
                for c in range(2):
                    self.pe(lambda c=c, h=h, ps2=ps2: self.T.matmul(ps2[64:96, :], self.wqs[:, (c * 8 + h) * 32:(c * 8 + h) * 32 + 32], qn[:, c, :],
                                                                   start=(c == 0), stop=(c == 1)), [self.wqs, qn], [ps2], signal=(c == 1))
                t1 = fw.rot([96, 512], F32, "qrt1")
                t2 = fw.rot([96, 512], F32, "qrt2")
                self.dve(lambda ps=ps, t1=t1: self.V.tensor_tensor(t1[64:96, :], ps[64:96, :], R[64:96, 0:512], ALU.mult), [ps, R], [t1])
                self.dve(lambda ps2=ps2, t2=t2: self.V.tensor_tensor(t2[64:96, :], ps2[64:96, :], R[64:96, 512:1024], ALU.mult), [ps2, R], [t2])
                self.dve(lambda h=h, t1=t1, t2=t2: self.V.tensor_tensor(Qh[h][64:96, :], t1[64:96, :], t2[64:96, :], ALU.add), [t1, t2], [Qh[h]])
                self.act(lambda h=h, ps=ps: self.A.copy(Qh[h][0:64, :], ps[0:64, :]), [ps], [Qh[h]])
            else:
                self.act(lambda h=h, ps=ps: self.A.copy(Qh[h][:, :], ps[:96, :]), [ps], [Qh[h]])

    def mla_kv_chunk(self, ckv_b, kr_b, heads, Kh, Vaug, nk):
        for i, h in enumerate(heads):
            ps = self.nps()
            self.pe(lambda h=h, ps=ps: self.T.matmul(ps[:64, :nk], self.wkk[:, h * 64:(h + 1) * 64], ckv_b[:, :nk], start=True, stop=True),
                    [self.wkk, ckv_b], [ps])
            self.act(lambda i=i, ps=ps: self.A.copy(Kh[i][0:64, :nk], ps[0:64, :nk]), [ps], [Kh[i]])
            self.dve(lambda i=i: self.V.tensor_copy(Kh[i][64:96, :nk], kr_b[64:96, :nk]), [kr_b], [Kh[i]])
        for kb in range(nk // 128):
            ps = self.nps()
            self.pe(lambda kb=kb, ps=ps: self.T.matmul(ps[:, :], ckv_b[:, kb * 128:(kb + 1) * 128], self.wkv[:, :], start=True, stop=True),
                    [ckv_b, self.wkv], [ps])
            for h8 in range(8):
                pass
            self.dve(lambda kb=kb, ps=ps: self.V.tensor_copy(
                Vaug[:, kb * 520:(kb + 1) * 520].rearrange("p (h e) -> p h e", e=65)[:, :, 0:64],
                ps[:, :].rearrange("p (h e) -> p h e", e=64)), [ps], [Vaug])

    def attn_accum(self, Qh4, q0, nq, Kh4, Vaug, heads, k0, nkb, first, last, Oacc):
        fw = self.fw
        nqs = nq // 128
        its = [(kb, i, h) for kb in range(nkb) for i, h in enumerate(heads)]

        def score(kb, i, h):
            pss = self.psb[4 + (self.sc_rr % 2)]
            self.sc_rr += 1
            self.pe(lambda: self.T.matmul(pss[:, :nq], Kh4[i][:, k0 + kb * 128:k0 + kb * 128 + 128],
                                          Qh4[i][:, q0:q0 + nq], start=True, stop=True), [Kh4[i], Qh4[i]], [pss])
            PT = fw.rot([128, 512], BF16, "PT", n=3)
            self.act(lambda: self.A.activation(PT[:, :nq], pss[:, :nq], AF.Exp, scale=96.0 ** -0.5), [pss], [PT])
            return PT

        def pv(kb, i, h, PT):
            for qs in range(nqs):
                self.pe(lambda qs=qs: self.T.matmul(
                    Oacc[qs][:, i * 65:(i + 1) * 65], PT[:, qs * 128:(qs + 1) * 128],
                    Vaug[:, (k0 // 128 + kb) * 520 + h * 65:(k0 // 128 + kb) * 520 + h * 65 + 65],
                    start=(first and kb == 0 and i == 0), stop=(last and kb == nkb - 1 and i == 3), skip_group_check=True),
                    [PT, Vaug], [Oacc[qs]], signal=(qs == nqs - 1))

        pend = score(*its[0])
        for n in range(len(its)):
            nxt = score(*its[n + 1]) if n + 1 < len(its) else None
            pv(*its[n], pend)
            pend = nxt

    def attn_finish(self, Oacc, nqs, ob, hh):
        fw = self.fw
        for qs in range(nqs):
            rec = fw.rot([128, 4], F32, "rec", n=4)
            self.dve(lambda qs=qs, rec=rec: self.V.reciprocal(rec[:, :], Oacc[qs][:, 64:260:65]), [Oacc[qs]], [rec])
            for i in range(4):
                self.dve(lambda qs=qs, i=i, rec=rec: self.V.tensor_scalar(ob[qs][:, hh * 256 + i * 64:hh * 256 + i * 64 + 64],
                                                                          Oacc[qs][:, i * 65:i * 65 + 64], rec[:, i:i + 1], None, ALU.mult),
                         [Oacc[qs], rec], [ob[qs]])

    def attn_out(self, ob, nqs, GB2, g0, o0):
        for qs in range(nqs):
            for c in range(4):
                pt = self.pst[self.pst_rr % 2]
                self.pst_rr += 1
                self.pe(lambda qs=qs, c=c, pt=pt: self.T.transpose(pt[:, :128], ob[qs][:, c * 128:(c + 1) * 128], self.ident[:, :]),
                        [ob[qs], self.ident], [pt])
                self.dve(lambda qs=qs, c=c, pt=pt: self.V.scalar_tensor_tensor(
                    self.oTs[o0 // 512][1][:, c, o0 % 512 + qs * 128:o0 % 512 + qs * 128 + 128], pt[:, :128], 0.5, GB2[:, c, g0 + qs * 128:g0 + qs * 128 + 128],
                    ALU.mult, ALU.mult), [pt, GB2], [self.oTs[o0 // 512][1]])

    def phaseB_prompt(self, l, half):
        fw = self.fw
        h0 = half * 512
        fw.push()
        self.load_mla_w(l)
        GB2 = fw.sb([128, 4, 512], BF16, "GB2")
        Qh = [fw.sb([96, 512], BF16, f"Qh{h}") for h in range(8)]
        ckv_f = fw.sb([128, 512], F32, "ckvf")
        ckv_b = fw.sb([128, 512], BF16, "ckvb")
        kr_f = fw.sb([96, 512], F32, "krf")
        kr_b = fw.sb([96, 512], BF16, "krb")
        fw.push()
        self.stream_begin(2)
        self.mla_front(l, h0, False, GB2, Qh, ckv_f, ckv_b, kr_f, kr_b)
        fw.pop()
        fw.dma("sp", self.d["ckvout"], self.d["ckvout"][l, :, h0:h0 + 512], ckv_f, ckv_f[:, :], sem_owner=self.outsem)
        fw.dma("sp", self.d["krout"], self.d["krout"][l, :, h0:h0 + 512], kr_f, kr_f[64:96, :], sem_owner=self.outsem)
        Kh = [fw.sb([96, 512], BF16, f"Kh{h}") for h in range(8)]
        Vaug = fw.sb([128, 4 * 520], BF16, "Vaug")
        self.pool(lambda: self.G.memset(Vaug[:, :], 1.0), [], [Vaug])
        self.mla_kv_chunk(ckv_b, kr_b, list(range(8)), Kh, Vaug, 512)
        self.sc_rr = 0
        import os
        if "dumpB" in os.environ.get("KDBG", "") and l == 0 and half == 0:
            so = self.d["stout"]
            fw.dma("pool", so, so[0:768, :].rearrange("(p a) b -> p (a b)", p=96), Qh[0], Qh[0][:, :])
            fw.dma("pool", so, so[768:1536, :].rearrange("(p a) b -> p (a b)", p=96), Kh[0], Kh[0][:, :])
            fw.dma("pool", so, so[1536:5632, :].rearrange("(p a) b -> p (a b)", p=128), Vaug, Vaug[:, 0:2048])
        for sq in range(2):
            ob = [fw.rot([128, 512], BF16, "ob", n=4) for _ in range(2)]
            for hh in range(2):
                heads = list(range(hh * 4, hh * 4 + 4))
                Oacc = [self.psb[0 + 2 * (hh % 2)], self.psb[1 + 2 * (hh % 2)]]
                self.attn_accum([Qh[h] for h in heads], sq * 256, 256, [Kh[h] for h in heads], Vaug, heads, sq * 256, 2, True, True, Oacc)
                self.attn_finish(Oacc, 2, ob, hh)
            if "dumpB" in os.environ.get("KDBG", "") and l == 0 and half == 0 and sq == 0:
                so = self.d["stout"]
                fw.dma("pool", so, so[5696:6720, :].rearrange("(p a) b -> p (a b)", p=128), ob[0], ob[0][:, :])
            self.attn_out(ob, 2, GB2, sq * 256, h0 + sq * 256)
        fw.pop()

    def phaseB_contrib(self, l):
        fw = self.fw
        self.load_mla_w(l)
        self.GB2 = fw.sb([128, 4, 512], BF16, "GB2s")
        self.Qh = [fw.sb([96, 512], BF16, f"Qhs{h}") for h in range(8)]
        fw.push()
        self.rope = fw.sb([96, 1024], F32, "rope")
        self.load(self.rope, self.rope[64:96, :], self.d["rope"], self.d["rope"].ap())
        self.stream_begin(2)
        ckv_f = fw.sb([128, 512], F32, "ckvf")
        ckv_b = fw.sb([128, 512], BF16, "ckvb")
        kr_f = fw.sb([96, 512], F32, "krf")
        kr_b = fw.sb([96, 512], BF16, "krb")
        self.mla_front(l, 0, True, self.GB2, self.Qh, ckv_f, ckv_b, kr_f, kr_b)
        ag = self.ag1_in[l]["vx"]
        fw.dma("sp", ag, ag[VX_OFF["ckv"]:VX_OFF["ckv"] + 128, :], ckv_b, ckv_b[:, :])
        fw.dma("sp", ag, ag[VX_OFF["kr"]:VX_OFF["kr"] + 32, :], kr_b, kr_b[64:96, :])
        fw.pop()

    def phaseB_consume(self, l):
        fw = self.fw
        ago = self.ag1_out[l]["vx"]
        fw.push()
        self.sc_rr = 0
        self.ps_range = (4, 6)
        ob = [fw.sb([128, 512], BF16, f"obs{i}") for i in range(4)]
        for hh in range(2):
            heads = list(range(hh * 4, hh * 4 + 4))
            Oacc = self.psb[0:4]
            for ch in range(5):
                ckv_b = fw.rot([128, 512], BF16, "ckvg", n=2)
                kr_b = fw.rot([96, 512], BF16, "krg", n=2)
                if ch < 4:
                    fw.dma("sp", ckv_b, ckv_b[:, :], ago, ago[ch * 864 + VX_OFF["ckv"]:ch * 864 + VX_OFF["ckv"] + 128, :])
                    fw.dma("sp", kr_b, kr_b[64:96, :], ago, ago[ch * 864 + VX_OFF["kr"]:ch * 864 + VX_OFF["kr"] + 32, :])
                else:
                    fw.dma("pool", ckv_b, ckv_b[:, :], self.d["cckv"], self.d["cckv"][l, :, :])
                    fw.dma("pool", kr_b, kr_b[64:96, :], self.d["ckr"], self.d["ckr"][l, :, :])
                Kh = [fw.rot([96, 512], BF16, f"Khs{i}", n=2) for i in range(4)]
                Vaug = fw.rot([128, 4 * 520], BF16, "Vaugs", n=2)
                self.pool(lambda Vaug=Vaug: self.G.memset(Vaug[:, :], 1.0), [], [Vaug])
                self.mla_kv_chunk(ckv_b, kr_b, heads, Kh, Vaug, 512)
                self.attn_accum([self.Qh[h] for h in heads], 0, 512, Kh, Vaug, heads, 0, 4, ch == 0, ch == 4, Oacc)
            self.attn_finish(Oacc, 4, ob, hh)
        self.ps_range = (0, 4)
        self.attn_out(ob, 4, self.GB2, 0, 0)
        fw.pop()

    def fnet_stage1(self, fT_buf, fT_ap_fn, dftd, G1):
        for hb in range(2):
            ps = self.psb[4 + hb]
            for gg in range(2):
                g = hb * 2 + gg
                self.pe(lambda g=g, gg=gg, ps=ps: self.T.matmul(ps[:, gg * 256:(gg + 1) * 256], fT_ap_fn(g), dftd[:, :],
                                                                 start=True, stop=True), [fT_buf, dftd], [ps], signal=(gg == 1))
            if hb == 0:
                self.act(lambda ps=ps: self.A.copy(G1[:, 0:512], ps[:, :]), [ps], [G1])
            else:
                self.dve(lambda ps=ps: self.V.tensor_copy(G1[:, 512:1024], ps[:, :]), [ps], [G1])

    def phaseD_prompt(self, l, half):
        fw = self.fw
        base = l * BLK_PER_LAYER
        h0 = half * 512
        fw.push()
        self.dftd_p = fw.sb([128, 256], BF16, "dftd_p")
        self.dftT_p = fw.sb([128, 1024], BF16, "dftT_p")
        for nm in ("dftd_p", "dftT_p"):
            t = getattr(self, nm)
            self.load(t, t[:], self.d[nm], self.d[nm].ap())
        fT = fw.sb([128, 4, 512], BF16, "fT")
        GD2 = fw.sb([128, 4, 512], BF16, "GD2")
        fw.push()
        self.stream_begin(2)
        slot = self.next_block(base + WIN_IDX["D0"])
        for c in range(4):
            ps = self.zmm(slot, 512, c * 128, 128, h0, 512)
            if c % 2:
                self.act(lambda c=c, ps=ps: self.A.copy(fT[:, c, :], ps[:, :]), [ps], [fT])
            else:
                self.dve(lambda c=c, ps=ps: self.V.tensor_copy(fT[:, c, :], ps[:, :]), [ps], [fT])
        self.prefetch_next()
        slot = self.next_block(base + WIN_IDX["D1"])
        for c in range(4):
            ps = self.zmm(slot, 512, c * 128, 128, h0, 512)
            self.silu2(ps, 128, 512, GD2[:, c, :], GD2)
        self.prefetch_next()
        fw.pop()
        for sq in range(2):
            t0 = sq * 256
            G1 = [fw.rot([128, 1024], BF16, "G1", n=4) for _ in range(2)]
            for tt in range(2):
                self.fnet_stage1(fT, lambda g, tt=tt: fT[:, g, t0 + tt * 128:t0 + tt * 128 + 128], self.dftd_p, G1[tt])
            for g in range(4):
                ps = self.nps()
                i = 0
                for tt in range(2):
                    for cs in range(2):
                        self.pe(lambda g=g, tt=tt, cs=cs, ps=ps, i=i: self.T.matmul(
                            ps[:, :256], G1[tt][:, g * 256 + cs * 128:g * 256 + cs * 128 + 128],
                            self.dftT_p[:, (tt * 2 + cs) * 256:(tt * 2 + cs) * 256 + 256], start=(i == 0), stop=(i == 3)),
                            [G1[tt], self.dftT_p], [ps], signal=(i == 3))
                        i += 1
                self.dve(lambda g=g, ps=ps: self.V.scalar_tensor_tensor(
                    self.oTs[half][3][:, g, t0:t0 + 256], ps[:, :256], 0.5, GD2[:, g, t0:t0 + 256], ALU.mult, ALU.mult),
                    [ps, GD2], [self.oTs[half][3]])
        fw.pop()

    def contrib_rows(self, l, slot, W, c0, w, part, row0):
        fw = self.fw
        ps = self.zmm(slot, W, c0, w, 0, 512)
        stg = fw.rot([128, 512], BF16, "agstg", n=3)
        if self.cflip % 2:
            self.act(lambda: self.A.copy(stg[:w, :], ps[:w, :]), [ps], [stg])
        else:
            self.dve(lambda: self.V.tensor_copy(stg[:w, :], ps[:w, :]), [ps], [stg])
        self.cflip += 1
        ag = self.ag1_in[l][part]
        fw.dma("sp", ag, ag[row0:row0 + w, :], stg, stg[:w, :])

    def phaseD_contrib(self, l):
        fw = self.fw
        base = l * BLK_PER_LAYER
        self.GD2 = fw.sb([128, 4, 512], BF16, "GD2s")
        fw.push()
        self.stream_begin(2)
        slot = self.next_block(base + WIN_IDX["D0"])
        for c in range(4):
            self.contrib_rows(l, slot, 512, c * 128, 128, "f", c * 128)
        self.prefetch_next()
        slot = self.next_block(base + WIN_IDX["D1"])
        for c in range(4):
            ps = self.zmm(slot, 512, c * 128, 128, 0, 512)
            self.silu2(ps, 128, 512, self.GD2[:, c, :], self.GD2)
        self.prefetch_next()
        fw.pop()

    def phaseD_consume(self, l):
        fw = self.fw
        ago = self.ag1_out[l]["f"]
        fw.push()
        self.dftd_s = fw.sb([128, 256], BF16, "dftd_s")
        self.load(self.dftd_s, self.dftd_s[:], self.d["dftd_s"], self.d["dftd_s"].ap())
        acc = self.psb[0:4]
        nt = 0
        for q in range(4):
            fq = fw.rot([128, 4, 512], BF16, "fq", n=2)
            src = ago[q * 512:q * 512 + 512, :].rearrange("(g d) t -> d g t", d=128)
            fw.dma("sp", fq, fq[:], ago, src)
            for s4 in range(4):
                tt = q * 4 + s4
                ct = fw.rot([128, 1024], BF16, "ct", n=3)
                fw.dma("pool", ct, ct[:], self.d["dftT_s"], self.d["dftT_s"][tt, :, :])
                G1 = fw.rot([128, 1024], BF16, "G1s", n=3)
                self.fnet_stage1(fq, lambda g, s4=s4, fq=fq: fq[:, g, s4 * 128:(s4 + 1) * 128], self.dftd_s, G1)
                for g in range(4):
                    for cs in range(2):
                        self.pe(lambda g=g, cs=cs, G1=G1, ct=ct, tt=tt: self.T.matmul(
                            acc[g][:, :], G1[:, g * 256 + cs * 128:g * 256 + cs * 128 + 128], ct[:, cs * 512:(cs + 1) * 512],
                            start=(tt == 0 and cs == 0), stop=(tt == 15 and cs == 1)),
                            [G1, ct], [acc[g]], signal=(g == 3 and cs == 1))
        for g in range(4):
            self.dve(lambda g=g: self.V.scalar_tensor_tensor(self.oTs[0][3][:, g, 0:512], acc[g][:, :], 0.5, self.GD2[:, g, :],
                                                             ALU.mult, ALU.mult), [acc[g], self.GD2], [self.oTs[0][3]])
        fw.pop()

    def allgather(self, dst, src):
        fw = self.fw
        fw.dma("pool", dst, None, src, None, sem_owner=dst, inc=1,
               fn=lambda: self.G.collective_compute("AllGather", ALU.bypass, replica_groups=[[0, 1, 2, 3], [4, 5, 6, 7]],
                                                     ins=[src.h.ap()], outs=[dst.h.ap()]))

    def sample_pass(self, l):
        import os
        fw = self.fw
        dbg = os.environ.get("KDBG", "")
        self.cflip = 0
        br = self.branches
        fw.push()
        if "A" in br:
            self.phaseA_contrib(l)
        fw.push()
        if "B" in br:
            self.phaseB_contrib(l)
        fw.push()
        if "D" in br:
            self.phaseD_contrib(l)
        if "noag" not in dbg:
            if "A" in br:
                self.allgather(self.ag1_out[l]["rk"], self.ag1_in[l]["rk"])
            if "A" in br or "B" in br:
                self.allgather(self.ag1_out[l]["vx"], self.ag1_in[l]["vx"])
            if "D" in br:
                self.allgather(self.ag1_out[l]["f"], self.ag1_in[l]["f"])
        if "C" in br:
            self.phaseC(l, 0, 512, 0)
        if "D" in br and "nocons" not in dbg:
            self.phaseD_consume(l)
        fw.pop()
        if "B" in br:
            self.phaseB_consume(l)
        fw.pop()
        if "A" in br and "noAcons" not in dbg:
            self.phaseA_consume(l)
            self.phaseA_final(l)
        fw.pop()

    def zero_branch(self, n):
        for hf in range(2):
            if self.oTs[hf] is not None:
                t = self.oTs[hf][n]
                self.pool(lambda t=t: self.G.memset(t[:], 0.0), [], [t])

    def win_plan(self, l, grp="p"):
        base = l * BLK_PER_LAYER
        ids = []
        order = (("A", ["A0", "A1", "A2", "A3", "A4"]), ("B", ["B0", "B1"]), ("C", ["C0", "C1", "C2"]), ("D", ["D0", "D1"]))
        if grp == "s":
            order = (order[0], order[1], order[3], order[2])
        for br, names in order:
            if br in self.branches:
                ids += [base + WIN_IDX[n] for n in names]
        return ids

    def tail_plan(self, l):
        base = l * BLK_PER_LAYER
        return [base + 18 + i for i in range(16)] + [base + 34, base + 35]

    def build(self):
        fw = self.fw
        self.outsem = Buf(None, "outsem", "none")
        for l in range(self.depth):
            self.stream_plan([l * BLK_PER_LAYER + b for b in range(6)])
            self.stream_plan(self.win_plan(l) * 2 + self.tail_plan(l))
            self.stream_plan(self.win_plan(l, 's') + self.tail_plan(l))
        for l in range(self.depth):
            fw.push()
            self.load_layer_small(l)
            self.ada(l)
            fw.push()
            self.hTs[1] = fw.sb([128, 8, 512], BF16, "hTb")
            self.oTs[1] = [fw.sb([128, 4, 512], BF16, f"oTb{n}") for n in range(4)]
            for n, br in enumerate("ABCD"):
                if br not in self.branches:
                    self.zero_branch(n)
            for half in range(2):
                self.make_h(0, half * 512, half * 512, 512)
            for half in range(2):
                self.phases(l, "p", half)
            self.merge_out(l, 0, 0, NPT)
            fw.pop()
            self.hTs[1] = None
            self.oTs[1] = None
            self.make_h(1, NPT, 0, 512)
            self.sample_pass(l)
            self.merge_out(l, 1, NPT, NST)
            fw.pop()
        fw.push()
        self.sp = fw.sb([128, NSP], F32, "spf")
        self.load(self.sp, self.sp[:], self.d["sp"], self.d["sp"][0, :, :])
        self.final_out()
        fw.pop()
        for e in ("sp",):
            for ev in list(fw.dma_out.values()):
                fw._wait(e, ev)
        fw.barrier()
        return self.nc

    def phases(self, l, grp, half):
        h0 = half * 512
        if "A" in self.branches:
            self.phaseA_prompt(l, half)
        if "B" in self.branches:
            self.phaseB_prompt(l, half)
        if "C" in self.branches:
            self.phaseC(l, h0, 512, h0)
        if "D" in self.branches:
            self.phaseD_prompt(l, half)


_CFG = {"branches": "ABCD", "depth": DEPTH}


def make_in_maps(inp):
    f32 = lambda a: np.ascontiguousarray(np.asarray(a, dtype=np.float32))
    inp = {k: np.asarray(v) for k, v in inp.items()}
    wst = build_stream(f32(inp["w_ada"]), f32(inp["w_in"]), f32(inp["w_branch"]), f32(inp["w_merge"]), f32(inp["w_out"]))
    sp = build_small(inp)
    cst = build_consts()
    L = DEPTH
    wup = f32(inp["rwkv_w_up"]).transpose(0, 2, 1, 3).reshape(L, 64, 1024)
    aup = f32(inp["rwkv_a_up"]).transpose(0, 2, 1, 3).reshape(L, 64, 1024)
    wqu = f32(inp["mla_w_q_up"]).reshape(L, 2, 128, 8, 96)
    wq = wqu.transpose(0, 2, 1, 3, 4).reshape(L, 128, 2 * 8 * 96)
    wqs = wqu[..., 64 + _SWAP32].transpose(0, 2, 1, 3, 4).reshape(L, 128, 2 * 8 * 32)
    wkvu = f32(inp["mla_w_kv_up"]).reshape(L, 128, 8, 128)
    wkk = np.ascontiguousarray(wkvu[..., :64]).reshape(L, 128, 512)
    wkv = np.ascontiguousarray(wkvu[..., 64:]).reshape(L, 128, 512)
    wsT = f32(inp["gmlp_w_s"]).transpose(0, 3, 1, 2).reshape(L, 128, 512)
    bsb = np.ascontiguousarray(np.broadcast_to(f32(inp["gmlp_b_s"]).reshape(L, 1, 512), (L, 128, 512)))
    maps = []
    xp = f32(inp["x_prompt"])
    xs = f32(inp["x_sample"])
    for c in range(NCORE):
        s, j = c // 4, c % 4
        dftT, rope = build_core_consts(j)
        cond = np.stack([f32(inp["c_ctx"]).reshape(8, 128).T, f32(inp["c"])[s].reshape(8, 128).T], axis=2).reshape(128, 16)
        o, _ = SP_OFF["rw"]
        om, _ = SP_OFF["mu_rkv"]
        spo = np.concatenate([sp[:, :, o:o + 36].reshape(L, 128, 9, 4)[:, :, :, j],
                              sp[:, :, om:om + 12].reshape(L, 128, 3, 4)[:, :, :, j]], axis=2)
        spo = np.ascontiguousarray(spo)
        st0 = np.stack([f32(inp["state_rwkv_fwd"])[s, :, 2 * j:2 * j + 2], f32(inp["state_rwkv_bwd"])[s, :, 2 * j:2 * j + 2]],
                       axis=1)
        st0 = st0.transpose(0, 1, 2, 4, 3).reshape(L, 2, 128, 64)
        p = np.arange(128)
        idx1 = np.zeros((128, 12), np.int32)
        for q in range(4):
            idx1[:, 0 * 4 + q] = q * 1024 + 128 * j + p
            idx1[:, 1 * 4 + q] = q * 1024 + 512 + 128 * j + p
            idx1[:, 2 * 4 + q] = q * 864 + 128 * j + p
        idx2 = np.zeros((128, 4), np.int32)
        for r in range(4):
            idx2[:, r] = (r * 4 + j) * 128 + p
        m = dict(
            xp=np.ascontiguousarray(xp[4 * c:4 * c + 4].reshape(NPT, D).T),
            xs=np.ascontiguousarray(xs[s, 512 * j:512 * j + 512].T),
            wst=wst, sp=sp, cond=np.ascontiguousarray(cond),
            ident=cst["ident"], bones=cst["bones"], mask=cst["mask"],
            dftd_p=cst["dftd_p"], dftd_s=cst["dftd_s"], dftT_p=cst["dftT_p"],
            dftT_s=np.ascontiguousarray(dftT.reshape(16, 128, 1024)), rope=np.ascontiguousarray(rope.reshape(32, 1024)),
            wup=wup, aup=aup,
            wupo=np.ascontiguousarray(wup.reshape(L, 64, 2, 4, 128)[:, :, :, j]).reshape(L, 64, 256),
            aupo=np.ascontiguousarray(aup.reshape(L, 64, 2, 4, 128)[:, :, :, j]).reshape(L, 64, 256),
            spo=spo, wq=wq, wqs=wqs, wkk=wkk, wkv=wkv, wsT=wsT, bsb=bsb,
            st0=np.ascontiguousarray(st0),
            cckv=np.ascontiguousarray(f32(inp["cache_mla_ckv"])[s].transpose(0, 2, 1)),
            ckr=np.ascontiguousarray(f32(inp["cache_mla_krope"])[s].transpose(0, 2, 1)),
            idx1=idx1, idx2=idx2,
        )
        maps.append(m)
    return maps


def assemble(results):
    B = 32
    yp = np.zeros((B, SEQ, D), np.float32)
    ys = np.zeros((2, DSEQ, D), np.float32)
    sf = np.zeros((B, DEPTH, 8, 64, 64), np.float32)
    sbw = np.zeros((B, DEPTH, 8, 64, 64), np.float32)
    ckv = np.zeros((B, DEPTH, SEQ, 128), np.float32)
    kr = np.zeros((B, DEPTH, SEQ, 32), np.float32)
    for c in range(NCORE):
        r = results[c]
        s, j = c // 4, c % 4
        yp[4 * c:4 * c + 4] = np.asarray(r["yp"]).T.reshape(4, SEQ, D)
        ys[s, 512 * j:512 * j + 512] = np.asarray(r["ys"]).T
        st = np.asarray(r["stout"]).reshape(DEPTH, 2, 4, 4, 2, 64, 64)
        st = st.transpose(1, 2, 0, 3, 4, 6, 5).reshape(2, 4, DEPTH, 8, 64, 64)
        sf[4 * c:4 * c + 4] = st[0]
        sbw[4 * c:4 * c + 4] = st[1]
        ck = np.asarray(r["ckvout"]).reshape(DEPTH, 128, 4, SEQ)
        ckv[4 * c:4 * c + 4] = ck.transpose(2, 0, 3, 1)
        k2 = np.asarray(r["krout"]).reshape(DEPTH, 32, 4, SEQ)
        kr[4 * c:4 * c + 4] = k2.transpose(2, 0, 3, 1)
    return yp, ys, sf, sbw, ckv, kr


def kernel(**inputs):
    prog = Prog(dict(_CFG))
    nc = prog.build()
    maps = make_in_maps(inputs)
    res = run_bass_kernel_spmd(nc, maps, core_ids=list(range(NCORE)))
    return assemble(res.results)
```

```python
import numpy as np
import ml_dtypes
import concourse.bass as bass
import concourse.mybir as mybir
from concourse.bass_utils import run_bass_kernel_spmd

F32 = mybir.dt.float32
BF16 = mybir.dt.bfloat16
I32 = mybir.dt.int32
ALU = mybir.AluOpType
AF = mybir.ActivationFunctionType
AX = mybir.AxisListType

D = 1024
DEPTH = 2
SEQ = 256
DSEQ = 2048
PAST = 512
NCORE = 8
NPT = 1024
NST = 512
NTOK = NPT + NST
EPS = 1e-6
GN_EPS = 64e-5
EP = 24000
AGP = {"rk": 1024, "vx": 864, "f": 512}
VX_OFF = dict(v=0, lora=512, ckv=704, kr=832)


class Buf:
    def __init__(self, h, name, space):
        self.h = h
        self.name = name
        self.space = space
        self.w = {}
        self.r = {}
        self.dsem = None
        self.dcnt = 0
        self.dsid = None

    def __getitem__(self, idx):
        return self.h[idx]

    def ap(self):
        return self.h.ap() if self.space == "dram" else self.h[:]


class FW:
    def __init__(self, nc):
        self.nc = nc
        self.E = {"pe": nc.tensor, "act": nc.scalar, "dve": nc.vector, "pool": nc.gpsimd, "sp": nc.sync}
        self.cnt = {e: 0 for e in self.E}
        self.esem = {e: [] for e in self.E}
        self.waited = {e: {} for e in self.E}
        self.pend = {e: [] for e in self.E}
        self.nbuf = 0
        self.ninst = 0
        self.dma_out = {}
        self.sem_pool = []
        self.stack = []
        self.rots = {}
        self.free_sems = []

    def sb(self, shape, dt, name=None):
        self.nbuf += 1
        name = name or "t"
        g = self.nc.sbuf_tensor(f"{name}_{self.nbuf}", list(shape), dt)
        h = g.__enter__()
        b = Buf(h, name, "sbuf")
        if self.stack:
            self.stack[-1].append((g, b))
        return b

    def ps(self, shape, dt=F32, name=None):
        self.nbuf += 1
        name = name or "p"
        h = self.nc.alloc_psum_tensor(f"{name}_{self.nbuf}", list(shape), dt)
        return Buf(h, name, "psum")

    def dram(self, name, shape, dt, kind=None):
        if kind is None:
            h = self.nc.dram_tensor(name, list(shape), dt)
        else:
            h = self.nc.dram_tensor(name, list(shape), dt, kind=kind)
        return Buf(h, name, "dram")

    def rot(self, shape, dt, name, n=2):
        key = (len(self.stack), name)
        if key not in self.rots:
            self.rots[key] = [[self.sb(shape, dt, name) for _ in range(n)], 0]
        ent = self.rots[key]
        t = ent[0][ent[1] % n]
        ent[1] += 1
        return t

    def push(self):
        self.stack.append([])

    def pop(self):
        self.barrier()
        depth = len(self.stack)
        for k in [k for k in self.rots if k[0] == depth]:
            del self.rots[k]
        for g, b in reversed(self.stack.pop()):
            if b.dsem is not None:
                self.free_sems.append((b.dsem, b.dcnt))
                b.dsem = None
            g.__exit__(None, None, None)

    def _sem_for(self, eng, k):
        i = (k - 1) // EP
        while len(self.esem[eng]) <= i:
            self.esem[eng].append(self.nc.alloc_semaphore(f"s_{eng}_{len(self.esem[eng])}"))
        return self.esem[eng][i], (k - 1) % EP + 1

    def _wait(self, eng, ev):
        if ev is None:
            return
        if ev[0] == "eng":
            _, e2, k = ev
            if e2 == eng and eng == "pe":
                return
            key = ("eng", e2, (k - 1) // EP)
            sem, val = self._sem_for(e2, k)
        else:
            _, sem, val, sid = ev
            key = ("sem", sid)
        if self.waited[eng].get(key, 0) >= val:
            return
        self.waited[eng][key] = val
        self.E[eng].wait_ge(sem, val)

    def _check_pend(self, eng, b):
        for e2, lst in self.pend.items():
            if e2 == eng:
                continue
            for (pb, _) in lst:
                if pb is b:
                    raise RuntimeError(f"buffer {b.name} has pending unsignalled access on {e2}, touched by {eng}")

    def _deps(self, eng, reads, writes):
        evs = []
        for b in reads:
            self._check_pend(eng, b)
            evs.extend(b.w.values())
        for b in writes:
            self._check_pend(eng, b)
            for wv in b.w.values():
                if not (wv[0] == "eng" and wv[1] == eng):
                    evs.append(wv)
            for ev in b.r.values():
                if ev[0] == "eng" and ev[1] == eng and eng == "pe":
                    continue
                evs.append(ev)
        for ev in evs:
            self._wait(eng, ev)

    def op(self, eng, fn, reads=(), writes=(), signal=True):
        self._deps(eng, reads, writes)
        ins = fn()
        self.ninst += 1
        if signal:
            self.cnt[eng] += 1
            k = self.cnt[eng]
            sem, val = self._sem_for(eng, k)
            ins.then_inc(sem, 1)
            ev = ("eng", eng, k)
            for (pb, kind) in self.pend[eng]:
                if kind == "r":
                    pb.r[eng] = ev
                else:
                    pb.w = {eng: ev}
                    pb.r = {}
            self.pend[eng] = []
            for b in reads:
                b.r[eng] = ev
            for b in writes:
                b.w = {eng: ev}
                b.r = {}
        else:
            for b in reads:
                self.pend[eng].append((b, "r"))
            for b in writes:
                self.pend[eng].append((b, "w"))
        return ins

    def _dma_sem(self, b):
        if b.dsem is None:
            self.nbuf += 1
            if self.free_sems:
                b.dsem, b.dcnt = self.free_sems.pop()
            else:
                b.dsem = self.nc.alloc_semaphore(f"d_{b.name}_{self.nbuf}")
            b.dsid = self.nbuf
        return b.dsem

    def dma(self, q, out_b, out_ap, in_b, in_ap, sem_owner=None, inc=16, fn=None, extra_reads=(), **kw):
        eng = q
        evs = []
        for b in (in_b, out_b) + tuple(extra_reads):
            self._check_pend(eng, b)
        evs.extend(in_b.w.values())
        for b in extra_reads:
            evs.extend(b.w.values())
        owner = sem_owner or (out_b if out_b.space != "dram" else in_b)
        sem = self._dma_sem(owner)
        for wv in out_b.w.values():
            if wv[0] != "sem":
                evs.append(wv)
        for ev in out_b.r.values():
            evs.append(ev)
        for ev in evs:
            self._wait(eng, ev)
        if fn is None:
            ins = self.E[eng].dma_start(out=out_ap, in_=in_ap, **kw)
        else:
            ins = fn()
        self.ninst += 1
        owner.dcnt += inc
        ins.then_inc(sem, inc)
        ev = ("sem", sem, owner.dcnt, owner.dsid)
        in_b.r[("dma", owner.dsid)] = ev
        for b in extra_reads:
            b.r[("dma", owner.dsid)] = ev
        out_b.w = {k: v for k, v in out_b.w.items() if v[0] == "sem"}
        out_b.w[("dma", owner.dsid)] = ev
        out_b.r = {}
        self.dma_out[owner.dsid] = ev
        return ev

    def wait_buf(self, eng, b):
        self._check_pend(eng, b)
        for ev in b.w.values():
            self._wait(eng, ev)
        for ev in b.r.values():
            self._wait(eng, ev)

    def barrier(self):
        for e in self.E:
            if self.pend[e]:
                raise RuntimeError(f"barrier with pending unsignalled ops on {e}")
        last = []
        for e in ("pe", "act", "dve", "pool"):
            if self.cnt[e] > 0:
                last.append(("eng", e, self.cnt[e]))
        for e in self.E:
            for ev in last:
                if ev[1] == e and e == "pe":
                    continue
                self._wait(e, ev)
            for ev in self.dma_out.values():
                self._wait(e, ev)
        self.dma_out = {}


COLS = dict(r=(0, 512), k=(512, 1024), v=(1024, 1536), wdf=(1536, 1600), wdb=(1600, 1664), ad=(1664, 1728),
            ga=(1728, 2240), qd=(2240, 2496), kvd=(2496, 2624), kr=(2624, 2656), gb=(2656, 3168),
            u=(3168, 3680), vc=(3680, 4192), gc=(4192, 4704), f=(4704, 5216), gd=(5216, 5728))


def _rng(name):
    a, b = COLS[name]
    return np.arange(a, b)


_SWAP32 = np.arange(32).reshape(16, 2)[:, ::-1].reshape(32)

WIN_BLOCKS = [
    ("A0", [_rng("r")]), ("A1", [_rng("k")]), ("A2", [_rng("v")]),
    ("A3", [_rng("wdf"), _rng("wdb"), _rng("ad")]), ("A4", [_rng("ga")]),
    ("B0", [_rng("qd"), _rng("kvd"), _rng("kr"), _rng("kr")[_SWAP32]]), ("B1", [_rng("gb")]),
    ("C0", [_rng("u")]), ("C1", [_rng("vc")]), ("C2", [_rng("gc")]),
    ("D0", [_rng("f")]), ("D1", [_rng("gd")]),
]
WIN_W = {n: int(sum(len(c) for c in cols)) for n, cols in WIN_BLOCKS}
WIN_IDX = {n: 6 + i for i, (n, _) in enumerate(WIN_BLOCKS)}
BLK_PER_LAYER = 36
FBLK = 4096


def _kcp(w, W):
    return w.reshape(8, 128, W).transpose(1, 0, 2).reshape(128, 8 * W)


def build_stream(w_ada, w_in, w_branch, w_merge, w_out):
    st = np.zeros((DEPTH * BLK_PER_LAYER, 128, FBLK), np.float32)
    for l in range(DEPTH):
        base = l * BLK_PER_LAYER
        for b in range(6):
            st[base + b, :, :] = _kcp(w_ada[l][:, 512 * b:512 * b + 512], 512)
        for i, (n, cols) in enumerate(WIN_BLOCKS):
            cc = np.concatenate(cols)
            W = len(cc)
            st[base + 6 + i, :, :8 * W] = _kcp(w_in[l][:, cc], W)
        for d in range(8):
            cc = np.concatenate([n * 1024 + d * 128 + np.arange(128) for n in range(4)])
            st[base + 18 + 2 * d, :, :] = _kcp(w_merge[l][:, cc], 512)
            wb = w_branch[l][:, :, d * 128:(d + 1) * 128]
            wb = wb.reshape(4, 4, 128, 128).transpose(2, 0, 1, 3)
            st[base + 19 + 2 * d, :, :2048] = wb.reshape(128, 2048)
        for b in range(2):
            st[base + 34 + b, :, :] = _kcp(w_out[l][:, 512 * b:512 * b + 512], 512)
    return st


SP_OFF = {}
_o = 0
for _n, _w in [("norm_g", 8), ("b_ada", 24), ("mu_rkv", 12), ("mu_lora", 3), ("b_merge", 32), ("rw", 36),
               ("qn", 2), ("kvn", 1), ("gln_g", 4), ("gln_b", 4), ("fin_g", 8)]:
    SP_OFF[_n] = (_o, _w)
    _o += _w
NSP = _o
RW_NAMES = ["w0_f", "w0_b", "a0_f", "a0_b", "k_k", "k_a", "r_k", "ln_g", "ln_b"]


def _pc(v, n):
    return np.asarray(v, np.float32).reshape(n, 128).T


def build_small(inp):
    sp = np.zeros((DEPTH, 128, NSP), np.float32)
    for l in range(DEPTH):
        def put(name, arr):
            o, w = SP_OFF[name]
            sp[l, :, o:o + w] = arr
        put("norm_g", _pc(inp["norm_g"][l], 8))
        put("b_ada", _pc(inp["b_ada"][l], 24))
        put("mu_rkv", _pc(inp["shift_mu"][l][:1536], 12))
        ml = np.zeros((128, 3), np.float32)
        ml[:64, :] = inp["shift_mu"][l][1536:1728].reshape(3, 64).T
        put("mu_lora", ml)
        bm = inp["b_merge"][l].reshape(4, 8, 128)
        put("b_merge", bm.transpose(2, 1, 0).reshape(128, 32))
        rwv = [inp["rwkv_w0"][l][0], inp["rwkv_w0"][l][1], inp["rwkv_a0"][l][0], inp["rwkv_a0"][l][1],
               inp["rwkv_k_k"][l], inp["rwkv_k_a"][l], inp["rwkv_r_k"][l].reshape(512), inp["rwkv_ln_g"][l],
               inp["rwkv_ln_b"][l]]
        rw = np.stack([_pc(v, 4) for v in rwv], axis=1)
        put("rw", rw.reshape(128, 36))
        put("qn", _pc(inp["mla_q_norm"][l], 2))
        put("kvn", _pc(inp["mla_kv_norm"][l], 1))
        put("gln_g", _pc(inp["gmlp_ln_g"][l], 4))
        put("gln_b", _pc(inp["gmlp_ln_b"][l], 4))
        put("fin_g", _pc(inp["final_norm_g"], 8))
    return sp


def rw_col(name, pair):
    o, _ = SP_OFF["rw"]
    return o + RW_NAMES.index(name) * 4 + pair


def build_consts():
    c = {}
    c["ident"] = np.eye(128, dtype=np.float32)
    hb = np.arange(128) // 64
    c["bones"] = (hb[:, None] == hb[None, :]).astype(np.float32)
    p = np.arange(128)[:, None]
    f = np.arange(128)[None, :]
    mk = {}
    bd32 = ((p // 32) == (f // 32)).astype(np.float32)
    od64 = (((p // 64) == (f // 64)) & ((p // 32) != (f // 32))).astype(np.float32)
    od128 = ((p // 64) != (f // 64)).astype(np.float32)
    for dname, ms, mi, mt in (("f", p < f, p <= f, f < p), ("b", p > f, p >= f, f > p)):
        ms = ms.astype(np.float32)
        mi = mi.astype(np.float32)
        mtf = -(mt.astype(np.float32))
        mk[dname] = np.concatenate([ms, mi, -ms * bd32, mi, mtf * bd32, mtf * bd32, mtf * od64, mtf * od64,
                                    mtf * od128, mtf * od128], axis=1)
    c["mask"] = np.stack([mk["f"], mk["b"]], axis=1).reshape(128, 2 * 1280)
    dd = np.arange(128)
    ang = 2 * np.pi * np.outer(dd, dd) / 128.0
    for nm, T in (("dftd_p", SEQ), ("dftd_s", DSEQ)):
        sc = 1.0 / np.sqrt(T * 128.0)
        c[nm] = np.concatenate([np.cos(ang) * sc, -np.sin(ang) * sc], axis=1).astype(np.float32)
    tt = np.arange(SEQ)
    angp = 2 * np.pi * np.outer(tt, tt) / SEQ
    cp = np.stack([np.cos(angp), np.sin(angp)], axis=1)
    c["dftT_p"] = cp.reshape(2, 128, 2, SEQ).transpose(1, 0, 2, 3).reshape(128, 2 * 2 * SEQ).astype(np.float32)
    return c


def build_core_consts(j):
    t = np.arange(DSEQ)
    k1 = 512 * j + np.arange(512)
    ang = 2 * np.pi * ((np.outer(t, k1)) % DSEQ) / DSEQ
    cs = np.stack([np.cos(ang), np.sin(ang)], axis=1)
    dftT = cs.reshape(16, 128, 2, 512).astype(np.float32)
    pos = 512 * j + np.arange(512)
    row = (pos // 64).astype(np.float32)
    col = (pos % 64).astype(np.float32)
    inv = (10000.0 ** (-np.arange(8, dtype=np.float32) / 8)).astype(np.float32)
    ang = np.concatenate([row[:, None] * inv, col[:, None] * inv], axis=-1).astype(np.float32)
    cos = np.cos(ang).astype(np.float32)
    sin = np.sin(ang).astype(np.float32)
    COS = np.repeat(cos, 2, axis=1).T
    SIN = np.stack([-sin, sin], axis=2).reshape(512, 32).T
    rope = np.stack([COS, SIN], axis=1).astype(np.float32)
    return dftT, rope


class Prog:
    def __init__(self, cfg):
        self.cfg = cfg
        self.branches = cfg.get("branches", "ABCD")
        self.depth = cfg.get("depth", DEPTH)
        nc = bass.Bass("TRN2", target_bir_lowering=False)
        self.nc = nc
        fw = FW(nc)
        self.fw = fw
        self.V, self.A, self.G, self.T = nc.vector, nc.scalar, nc.gpsimd, nc.tensor
        di = lambda n, s, dt=F32: fw.dram(n, s, dt, kind="ExternalInput")
        do = lambda n, s, dt=F32: fw.dram(n, s, dt, kind="ExternalOutput")
        self.d = dict(
            xp=di("xp", [D, NPT]), xs=di("xs", [D, NST]),
            wst=di("wst", [DEPTH * BLK_PER_LAYER, 128, FBLK]),
            sp=di("sp", [DEPTH, 128, NSP]), cond=di("cond", [128, 16]),
            ident=di("ident", [128, 128]), bones=di("bones", [128, 128]), mask=di("mask", [128, 2560]),
            dftd_p=di("dftd_p", [128, 256]), dftd_s=di("dftd_s", [128, 256]), dftT_p=di("dftT_p", [128, 1024]),
            dftT_s=di("dftT_s", [16, 128, 1024]), rope=di("rope", [32, 1024]),
            wup=di("wup", [DEPTH, 64, 1024]), aup=di("aup", [DEPTH, 64, 1024]),
            wupo=di("wupo", [DEPTH, 64, 256]), aupo=di("aupo", [DEPTH, 64, 256]),
            spo=di("spo", [DEPTH, 128, 12]),
            wq=di("wq", [DEPTH, 128, 2 * 8 * 96]), wqs=di("wqs", [DEPTH, 128, 2 * 8 * 32]),
            wkk=di("wkk", [DEPTH, 128, 512]), wkv=di("wkv", [DEPTH, 128, 512]),
            wsT=di("wsT", [DEPTH, 128, 512]), bsb=di("bsb", [DEPTH, 128, 512]),
            st0=di("st0", [DEPTH, 2, 128, 64]), cckv=di("cckv", [DEPTH, 128, PAST]),
            ckr=di("ckr", [DEPTH, 32, PAST]),
            idx1=di("idx1", [128, 12], I32), idx2=di("idx2", [128, 4], I32),
            yp=do("yp", [D, NPT]), ys=do("ys", [D, NST]),
            stout=do("stout", [DEPTH * 2 * 4 * 4 * 128, 64]),
            ckvout=do("ckvout", [DEPTH, 128, NPT]), krout=do("krout", [DEPTH, 32, NPT]),
        )
        self.ag1_in = [{k: fw.dram(f"ag1i{k}{l}", [n, 512], BF16) for k, n in AGP.items()} for l in range(DEPTH)]
        self.ag1_out = [{k: fw.dram(f"ag1o{k}{l}", [4 * n, 512], BF16) for k, n in AGP.items()} for l in range(DEPTH)]
        self.ag2_in = [fw.dram(f"ag2i{l}", [512, 512], BF16) for l in range(DEPTH)]
        self.ag2_out = [fw.dram(f"ag2o{l}", [2048, 512], BF16) for l in range(DEPTH)]
        self.psb = [fw.ps([128, 512], F32, f"bank{i}") for i in range(6)]
        self.pst = [fw.ps([128, 1024], BF16, f"pst{i}") for i in range(2)]
        self.pst_rr = 0
        self.ps_rr = 0
        self.xT = fw.sb([128, 8, NTOK], F32, "xT")
        self.hTs = [fw.sb([128, 8, 512], BF16, "hTa"), None]
        self.oTs = [[fw.sb([128, 4, 512], BF16, f"oTa{n}") for n in range(4)], None]
        self.slots = None
        self.slot_rr = 0
        self.plan = []
        self.plan_pos = 0
        self.stg = None
        self.blocks_left = 0
        self.dma_issued = {}
        self.load_consts()

    ps_range = (0, 4)

    def nps(self, lo=None, hi=None):
        lo = self.ps_range[0] if lo is None else lo
        hi = self.ps_range[1] if hi is None else hi
        n = hi - lo
        b = self.psb[lo + (self.ps_rr % n)]
        self.ps_rr += 1
        return b

    def dve(self, fn, r, w):
        return self.fw.op("dve", fn, r, w)

    def rsqrt(self, out_buf, out_ap, in_buf, in_ap):
        self.act(lambda: self.A.activation(in_ap, in_ap, AF.Sqrt), [in_buf], [in_buf])
        self.dve(lambda: self.V.reciprocal(out_ap, in_ap), [in_buf], [out_buf])

    def act(self, fn, r, w):
        return self.fw.op("act", fn, r, w)

    def pool(self, fn, r, w):
        return self.fw.op("pool", fn, r, w)

    def pe(self, fn, r, w, signal=True):
        return self.fw.op("pe", fn, r, w, signal=signal)

    def load(self, dst, dst_ap, src, src_ap, q="sp"):
        if dst_ap.dtype != src_ap.dtype:
            q = "pool"
        return self.fw.dma(q, dst, dst_ap, src, src_ap)

    def load_consts(self):
        fw, d = self.fw, self.d
        self.ident = fw.sb([128, 128], BF16, "ident")
        self.bones = fw.sb([128, 128], BF16, "bones")
        self.mask = fw.sb([128, 2560], BF16, "mask")
        self.ones = fw.sb([128, 128], BF16, "ones")
        self.onesf = fw.sb([128, 128], F32, "onesf")
        self.rope = None
        self.cond = fw.sb([128, 16], F32, "cond")
        self.idx1 = fw.sb([128, 12], I32, "idx1")
        self.idx2 = fw.sb([128, 4], I32, "idx2")
        for nm in ("ident", "bones", "mask", "cond", "idx1", "idx2"):
            t = getattr(self, nm)
            self.load(t, t[:], d[nm], d[nm].ap())
        self.pool(lambda: self.G.memset(self.ones[:], 1.0), [], [self.ones])
        self.pool(lambda: self.G.memset(self.onesf[:], 1.0), [], [self.onesf])
        self.hm = fw.sb([128, 2], F32, "hm")
        self.dve(lambda: self.V.tensor_copy(self.hm[:, 0:2], self.bones[:, 0:128:64]), [self.bones], [self.hm])
        xv = self.xT
        self.load(xv, xv[:, :, 0:NPT], d["xp"], d["xp"].ap().rearrange("(k p) t -> p k t", p=128))
        self.load(xv, xv[:, :, NPT:NTOK], d["xs"], d["xs"].ap().rearrange("(k p) t -> p k t", p=128))

    def stream_plan(self, ids):
        self.plan.extend(ids)

    def stream_begin(self, nblocks, depth=1):
        fw = self.fw
        self.sdepth = depth
        self.slots = [fw.sb([128, FBLK], BF16, f"slot{i}") for i in range(depth + 1)]
        self.stg = [fw.sb([128, FBLK // 2], F32, f"wstg{i}") for i in range(2 * depth)]
        self.blocks_left = nblocks
        self.scope_end = self.plan_pos + nblocks
        self.dma_issued = {}

    def _issue_dma(self, pos):
        blk = self.plan[pos]
        src = self.d["wst"]
        for hf in range(2):
            st = self.stg[(2 * pos + hf) % len(self.stg)]
            self.fw.dma("sp", st, st[:, :], src, src[blk, :, hf * (FBLK // 2):(hf + 1) * (FBLK // 2)])
        self.dma_issued[pos] = True

    def next_block(self, blk):
        pos = self.plan_pos
        assert self.plan[pos] == blk, (pos, self.plan[pos], blk)
        assert self.blocks_left > 0
        if pos not in self.dma_issued:
            self._issue_dma(pos)
        slot = self.slots[pos % len(self.slots)]
        for hf in range(2):
            st = self.stg[(2 * pos + hf) % len(self.stg)]
            if hf == 0:
                self.dve(lambda hf=hf, st=st: self.V.tensor_copy(slot[:, hf * (FBLK // 2):(hf + 1) * (FBLK // 2)], st[:, :]), [st], [slot])
            else:
                self.act(lambda hf=hf, st=st: self.A.copy(slot[:, hf * (FBLK // 2):(hf + 1) * (FBLK // 2)], st[:, :]), [st], [slot])
        self.plan_pos += 1
        self.blocks_left -= 1
        return slot

    def prefetch_next(self):
        for pos in range(self.plan_pos, min(self.plan_pos + self.sdepth, self.scope_end)):
            if pos not in self.dma_issued:
                self._issue_dma(pos)

    def load_layer_small(self, l):
        fw, d = self.fw, self.d
        self.sp = fw.sb([128, NSP], F32, "sp")
        self.load(self.sp, self.sp[:], d["sp"], d["sp"][l, :, :])
        sp = self.sp
        o, _ = SP_OFF["mu_rkv"]
        self.mu1 = fw.sb([128, 15], F32, "mu1")
        self.muh = fw.sb([128, 15], F32, "muh")
        self.dve(lambda: self.V.tensor_scalar(self.mu1[:], sp[:, o:o + 15], -1.0, 1.0, ALU.mult, ALU.add), [sp], [self.mu1])
        self.dve(lambda: self.V.tensor_scalar(self.muh[:], sp[:, o:o + 15], 0.5, None, ALU.mult), [sp], [self.muh])
        o2, _ = SP_OFF["rw"]
        self.rwh = fw.sb([128, 16], F32, "rwh")
        self.dve(lambda: self.V.tensor_scalar(self.rwh[:], sp[:, o2:o2 + 16], 0.5, None, ALU.mult), [sp], [self.rwh])
        ob, _ = SP_OFF["b_merge"]
        self.bmh = fw.sb([128, 32], F32, "bmh")
        self.dve(lambda: self.V.tensor_scalar(self.bmh[:], sp[:, ob:ob + 32], 0.5, None, ALU.mult), [sp], [self.bmh])

    def spc(self, name, i=0, n=1):
        o, _ = SP_OFF[name]
        return self.sp[:, o + i:o + i + n]

    def ada(self, l):
        fw = self.fw
        sc = fw.sb([128, 16], BF16, "scond")
        th = fw.sb([128, 16], F32, "cth")
        c = self.cond
        self.act(lambda: self.A.activation(th[:], c[:], AF.Tanh, scale=0.5), [c], [th])
        t2 = fw.sb([128, 16], F32, "ct2")
        self.dve(lambda: self.V.scalar_tensor_tensor(t2[:], th[:], 1.0, c[:], ALU.add, ALU.mult), [th, c], [t2])
        self.dve(lambda: self.V.tensor_scalar(sc[:], t2[:], 0.5, None, ALU.mult), [t2], [sc])
        ps = self.psb[5]
        self.mod = fw.sb([128, 24, 2], F32, "mod")
        self.gmod = fw.sb([128, 8, 2], F32, "gmod")
        fw.push()
        self.stream_begin(6, depth=2)
        for b in range(6):
            slot = self.next_block(l * BLK_PER_LAYER + b)
            for n in range(4):
                m = b * 4 + n
                for kc in range(8):
                    self.pe(lambda kc=kc, n=n, m=m, slot=slot: self.T.matmul(
                        ps[:, 2 * m:2 * m + 2], slot[:, kc * 512 + n * 128:kc * 512 + n * 128 + 128],
                        sc[:, 2 * kc:2 * kc + 2], start=(kc == 0), stop=(kc == 7)),
                        [slot, sc], [ps], signal=(kc == 7 and n == 3))
            self.prefetch_next()
        fw.pop()
        ob, _ = SP_OFF["b_ada"]
        for cc in range(2):
            self.dve(lambda cc=cc: self.V.tensor_tensor(self.mod[:, :, cc], ps[:, cc:48:2], self.sp[:, ob:ob + 24], ALU.add),
                     [ps, self.sp], [self.mod])
        og, _ = SP_OFF["norm_g"]
        for cc in range(2):
            self.dve(lambda cc=cc: self.V.scalar_tensor_tensor(self.gmod[:, :, cc], self.mod[:, 8:16, cc], 1.0,
                                                               self.sp[:, og:og + 8], ALU.add, ALU.mult),
                     [self.mod, self.sp], [self.gmod])

    def rstd_tile(self, src, views, nfeat, out, N):
        fw = self.fw
        ps = self.nps()
        nk = len(views)
        for i, v in enumerate(views):
            sq = fw.rot([128, 512], BF16, "sq")
            self.act(lambda v=v, sq=sq: self.A.activation(sq[:, :N], v, AF.Square), [src], [sq])
            self.pe(lambda i=i, sq=sq: self.T.matmul(ps[:, :N], self.ones[:], sq[:, :N], start=(i == 0), stop=(i == nk - 1)),
                    [self.ones, sq], [ps], signal=True)
        t = fw.sb([128, 512], F32, "rs_t")
        self.dve(lambda: self.V.tensor_scalar(t[:, :N], ps[:, :N], 1.0 / nfeat, EPS, ALU.mult, ALU.add), [ps], [t])
        self.rsqrt(out, out[:, :N], t, t[:, :N])

    def make_h(self, cc, x0, h0, N):
        fw = self.fw
        fw.push()
        rstd = fw.sb([128, 512], F32, "rstd")
        self.rstd_tile(self.xT, [self.xT[:, kc, x0:x0 + N] for kc in range(8)], float(D), rstd, N)
        for kc in range(8):
            tmp = fw.rot([128, 512], F32, "htmp")
            self.dve(lambda kc=kc, tmp=tmp: self.V.scalar_tensor_tensor(
                tmp[:, :N], self.xT[:, kc, x0:x0 + N], self.gmod[:, kc, cc:cc + 1], rstd[:, :N], ALU.mult, ALU.mult),
                [self.xT, self.gmod, rstd], [tmp])
            self.act(lambda kc=kc, tmp=tmp: self.A.activation(
                self.hTs[h0 // 512][:, kc, 0:N], tmp[:, :N], AF.Identity, bias=self.mod[:, kc, cc:cc + 1], scale=1.0),
                [tmp, self.mod], [self.hTs[h0 // 512]])
        fw.pop()

    def zmm(self, slot, W, c0, w, h0, N, ps=None, prow=0):
        ps = ps or self.nps()
        for kc in range(8):
            self.pe(lambda kc=kc: self.T.matmul(ps[prow:prow + w, :N], slot[:, kc * W + c0:kc * W + c0 + w],
                                                self.hTs[h0 // 512][:, kc, 0:N], start=(kc == 0), stop=(kc == 7)),
                    [slot, self.hTs[h0 // 512]], [ps], signal=(kc == 7))
        return ps

    def silu2(self, ps, rows, N, out_ap, out_buf):
        fw = self.fw
        th = fw.rot([128, 512], F32, "s2th")
        self.act(lambda: self.A.activation(th[:rows, :N], ps[:rows, :N], AF.Tanh, scale=0.5), [ps], [th])
        self.dve(lambda: self.V.scalar_tensor_tensor(out_ap, th[:rows, :N], 1.0, ps[:rows, :N], ALU.add, ALU.mult),
                 [th, ps], [out_buf])

    def gelu2(self, ps, rows, N, out_ap, out_buf):
        fw = self.fw
        u = fw.rot([128, 512], F32, "g2u")
        self.act(lambda: self.A.activation(u[:rows, :N], ps[:rows, :N], AF.Square), [ps], [u])
        self.dve(lambda: self.V.tensor_scalar(u[:rows, :N], u[:rows, :N], 0.044715, 1.0, ALU.mult, ALU.add), [u], [u])
        self.dve(lambda: self.V.tensor_tensor(u[:rows, :N], u[:rows, :N], ps[:rows, :N], ALU.mult), [u, ps], [u])
        self.act(lambda: self.A.activation(u[:rows, :N], u[:rows, :N], AF.Tanh, scale=0.7978845608028654), [u], [u])
        self.dve(lambda: self.V.scalar_tensor_tensor(out_ap, u[:rows, :N], 1.0, ps[:rows, :N], ALU.add, ALU.mult),
                 [u, ps], [out_buf])

    def transpose_to(self, src_buf, src_ap, dst_buf, dst_ap, rows=128, cols=128, eng="act"):
        pt = self.pst[self.pst_rr % 2]
        self.pst_rr += 1
        self.pe(lambda: self.T.transpose(pt[:cols, :rows], src_ap, self.ident[:rows, :rows]), [src_buf, self.ident], [pt])
        if eng == "act":
            self.act(lambda: self.A.copy(dst_ap, pt[:cols, :rows]), [pt], [dst_buf])
        else:
            self.dve(lambda: self.V.tensor_copy(dst_ap, pt[:cols, :rows]), [pt], [dst_buf])

    def phaseC(self, l, h0, N, o0):
        fw = self.fw
        base = l * BLK_PER_LAYER
        fw.push()
        self.stream_begin(3, depth=2)
        U2 = fw.sb([128, 4, 512], BF16, "U2")
        GV = fw.sb([128, 4, 512], BF16, "GV")
        GC2 = fw.sb([128, 4, 512], BF16, "GC2")
        wsT = fw.sb([128, 512], BF16, "wsT")
        bsb = fw.sb([128, 512], F32, "bsb")
        self.load(wsT, wsT[:], self.d["wsT"], self.d["wsT"][l, :, :])
        self.load(bsb, bsb[:], self.d["bsb"], self.d["bsb"][l, :, :])
        slot = self.next_block(base + WIN_IDX["C0"])
        for c in range(4):
            ps = self.zmm(slot, 512, c * 128, 128, h0, N)
            self.gelu2(ps, 128, N, U2[:, c, :N], U2)
        self.prefetch_next()
        slot = self.next_block(base + WIN_IDX["C1"])
        for c in range(4):
            ps = self.zmm(slot, 512, c * 128, 128, h0, N)
            self.gelu2(ps, 128, N, GV[:, c, :N], GV)
        self.prefetch_next()
        slot = self.next_block(base + WIN_IDX["C2"])
        for c in range(4):
            ps = self.zmm(slot, 512, c * 128, 128, h0, N)
            self.silu2(ps, 128, N, GC2[:, c, :N], GC2)
        self.prefetch_next()
        psm = self.nps()
        psq = self.nps()
        for c in range(4):
            self.pe(lambda c=c: self.T.matmul(psm[:, :N], self.ones[:], GV[:, c, :N], start=(c == 0), stop=(c == 3)),
                    [self.ones, GV], [psm], signal=(c == 3))
        for c in range(4):
            sq = fw.rot([128, 512], BF16, "gsq")
            self.act(lambda c=c, sq=sq: self.A.activation(sq[:, :N], GV[:, c, :N], AF.Square), [GV], [sq])
            self.pe(lambda c=c, sq=sq: self.T.matmul(psq[:, :N], self.ones[:], sq[:, :N], start=(c == 0), stop=(c == 3)),
                    [self.ones, sq], [psq], signal=True)
        mu = fw.sb([128, 512], F32, "gmu")
        msq = fw.sb([128, 512], F32, "gmsq")
        var = fw.sb([128, 512], F32, "gvar")
        rstd = fw.sb([128, 512], F32, "grstd")
        self.dve(lambda: self.V.tensor_scalar(mu[:, :N], psm[:, :N], 1.0 / 512, None, ALU.mult), [psm], [mu])
        self.dve(lambda: self.V.tensor_tensor(msq[:, :N], mu[:, :N], mu[:, :N], ALU.mult), [mu], [msq])
        self.dve(lambda: self.V.scalar_tensor_tensor(var[:, :N], psq[:, :N], 1.0 / 512, msq[:, :N], ALU.mult, ALU.subtract),
                 [psq, msq], [var])
        self.dve(lambda: self.V.tensor_scalar(var[:, :N], var[:, :N], 4e-5, None, ALU.add), [var], [var])
        self.rsqrt(rstd, rstd[:, :N], var, var[:, :N])
        VN = fw.sb([128, 4, 512], BF16, "VN")
        for c in range(4):
            t = fw.rot([128, 512], F32, "lnt")
            self.dve(lambda c=c, t=t: self.V.tensor_tensor(t[:, :N], GV[:, c, :N], mu[:, :N], ALU.subtract), [GV, mu], [t])
            self.dve(lambda t=t: self.V.tensor_tensor(t[:, :N], t[:, :N], rstd[:, :N], ALU.mult), [t, rstd], [t])
            self.act(lambda c=c, t=t: self.A.activation(VN[:, c, :N], t[:, :N], AF.Identity, bias=self.spc("gln_b", c),
                                                        scale=self.spc("gln_g", c)), [t, self.sp], [VN])
        nsub = N // 128
        for g in range(4):
            pmix = self.nps()
            for s in range(nsub):
                vtm = fw.rot([128, 128], BF16, "vtm")
                self.transpose_to(VN, VN[:, g, s * 128:(s + 1) * 128], vtm, vtm[:], eng=("act" if s % 2 else "dve"))
                self.pe(lambda g=g, s=s, vtm=vtm: self.T.matmul(pmix[:, s * 128:(s + 1) * 128], vtm[:],
                                                                 wsT[:, g * 128:(g + 1) * 128], start=True, stop=True),
                        [vtm, wsT], [pmix], signal=(s == nsub - 1))
            t = fw.rot([128, 512], F32, "mixt")
            for s in range(nsub):
                self.dve(lambda g=g, s=s, t=t: self.V.tensor_tensor(t[:, s * 128:(s + 1) * 128], pmix[:, s * 128:(s + 1) * 128],
                                                                     bsb[:, g * 128:(g + 1) * 128], ALU.add), [pmix, bsb], [t])
            self.dve(lambda g=g, t=t: self.V.scalar_tensor_tensor(t[:, :N], t[:, :N], 0.25, U2[:, g, :N], ALU.mult, ALU.mult),
                     [t, U2], [t])
            self.dve(lambda g=g, t=t: self.V.tensor_tensor(self.oTs[o0 // 512][2][:, g, 0:N], t[:, :N], GC2[:, g, :N], ALU.mult),
                     [t, GC2], [self.oTs[o0 // 512][2]])
        fw.pop()

    def merge_out(self, l, cc, x0, NT):
        fw = self.fw
        base = l * BLK_PER_LAYER
        ntile = NT // 512
        fw.push()
        self.stream_begin(18, depth=2)
        merged = fw.sb([128, 8, NT], BF16, "merged")
        for d in range(8):
            slotM = self.next_block(base + 18 + 2 * d)
            self.prefetch_next()
            slotB = self.next_block(base + 19 + 2 * d)
            for tt in range(ntile):
                t0 = tt * 512
                acc = fw.rot([128, 512], F32, "macc")
                for n in range(4):
                    psg = self.nps()
                    for kc in range(8):
                        self.pe(lambda kc=kc, n=n, tt=tt: self.T.matmul(psg[:, :], slotM[:, kc * 512 + n * 128:kc * 512 + n * 128 + 128],
                                                                 self.hTs[tt][:, kc, :], start=(kc == 0), stop=(kc == 7)),
                                [slotM, self.hTs[tt]], [psg], signal=(kc == 7))
                    psp = self.nps()
                    for k4 in range(4):
                        self.pe(lambda k4=k4, n=n, tt=tt: self.T.matmul(psp[:, :], slotB[:, (n * 4 + k4) * 128:(n * 4 + k4) * 128 + 128],
                                                                 self.oTs[tt][n][:, k4, :], start=(k4 == 0), stop=(k4 == 3)),
                                [slotB, self.oTs[tt][n]], [psp], signal=(k4 == 3))
                    th = fw.rot([128, 512], F32, "mth")
                    self.act(lambda n=n, th=th, psg=psg: self.A.activation(th[:], psg[:], AF.Tanh, bias=self.bmh[:, d * 4 + n:d * 4 + n + 1],
                                                                           scale=0.5), [psg, self.bmh], [th])
                    if n == 0:
                        self.dve(lambda th=th, psp=psp: self.V.scalar_tensor_tensor(acc[:], th[:], 1.0, psp[:], ALU.add, ALU.mult),
                                 [th, psp], [acc])
                    else:
                        self.dve(lambda th=th, psp=psp: self.V.scalar_tensor_tensor(th[:], th[:], 1.0, psp[:], ALU.add, ALU.mult),
                                 [th, psp], [th])
                        self.dve(lambda th=th: self.V.tensor_tensor(acc[:], acc[:], th[:], ALU.add), [acc, th], [acc])
                self.act(lambda acc=acc, t0=t0: self.A.mul(merged[:, d, t0:t0 + 512], acc[:], 0.5), [acc], [merged])
            self.prefetch_next()
        for b in range(2):
            slotO = self.next_block(base + 34 + b)
            self.prefetch_next()
            for dd in range(4):
                dch = b * 4 + dd
                for tt in range(ntile):
                    t0 = tt * 512
                    ps = self.nps()
                    for kc in range(8):
                        self.pe(lambda kc=kc, dd=dd: self.T.matmul(ps[:, :], slotO[:, kc * 512 + dd * 128:kc * 512 + dd * 128 + 128],
                                                                   merged[:, kc, t0:t0 + 512], start=(kc == 0), stop=(kc == 7)),
                                [slotO, merged], [ps], signal=(kc == 7))
                    self.dve(lambda dch=dch, t0=t0, ps=ps: self.V.scalar_tensor_tensor(
                        self.xT[:, dch, x0 + t0:x0 + t0 + 512], ps[:, :], self.mod[:, 16 + dch, cc:cc + 1],
                        self.xT[:, dch, x0 + t0:x0 + t0 + 512], ALU.mult, ALU.add), [ps, self.mod, self.xT], [self.xT])
        fw.pop()

    def final_out(self):
        fw = self.fw
        for (x0, N, dst) in ((0, 512, ("yp", 0)), (512, 512, ("yp", 512)), (NPT, 512, ("ys", 0))):
            fw.push()
            rstd = fw.sb([128, 512], F32, "frstd")
            self.rstd_tile(self.xT, [self.xT[:, kc, x0:x0 + N] for kc in range(8)], float(D), rstd, N)
            stg = fw.sb([128, 8, 512], F32, "fstg")
            for kc in range(8):
                self.dve(lambda kc=kc: self.V.scalar_tensor_tensor(stg[:, kc, :], self.xT[:, kc, x0:x0 + N], self.spc("fin_g", kc),
                                                                   rstd[:, :], ALU.mult, ALU.mult), [self.xT, self.sp, rstd], [stg])
            dt = self.d[dst[0]]
            ncol = NPT if dst[0] == "yp" else NST
            dview = dt.ap().rearrange("(k p) t -> p k t", p=128)[:, :, dst[1]:dst[1] + N]
            fw.dma("sp", dt, dview, stg, stg[:], sem_owner=self.outsem)
            fw.pop()

    def shift_evac(self, ps, rows, N, nseq, mu1, muh, out_ap, out_buf, tanh=False):
        fw = self.fw
        zt = fw.rot([128, 512], F32, "shz")
        o32 = fw.rot([128, 512], F32, "sho")
        self.act(lambda: self.A.copy(zt[:rows, :N], ps[:rows, :N]), [ps], [zt])
        self.dve(lambda: self.V.tensor_scalar(o32[:rows, :N], zt[:rows, :N], mu1, None, ALU.mult), [zt, self.mu1], [o32])
        z3 = zt[:rows, :N].rearrange("p (s t) -> p s t", s=nseq)
        o3 = o32[:rows, :N].rearrange("p (s t) -> p s t", s=nseq)
        Tq = N // nseq
        self.dve(lambda: self.V.scalar_tensor_tensor(o3[:, :, 1:Tq], z3[:, :, 0:Tq - 1], muh, o3[:, :, 1:Tq], ALU.mult, ALU.add),
                 [zt, self.muh, o32], [o32])
        self.dve(lambda: self.V.scalar_tensor_tensor(o3[:, :, 0:Tq - 1], z3[:, :, 1:Tq], muh, o3[:, :, 0:Tq - 1], ALU.mult, ALU.add),
                 [zt, self.muh, o32], [o32])
        if tanh:
            self.act(lambda: self.A.activation(out_ap, o32[:rows, :N], AF.Tanh), [o32], [out_buf])
        else:
            self.act(lambda: self.A.copy(out_ap, o32[:rows, :N]), [o32], [out_buf])

    def load_rwkv_w(self, l, own):
        fw, d = self.fw, self.d
        ncol = 256 if own else 1024
        self.wup = fw.sb([64, ncol], BF16, "wup")
        self.aup = fw.sb([64, ncol], BF16, "aup")
        sw, sa = (d["wupo"], d["aupo"]) if own else (d["wup"], d["aup"])
        self.load(self.wup, self.wup[:, :], sw, sw[l, :, :])
        self.load(self.aup, self.aup[:, :], sa, sa[l, :, :])

    def phaseA_prompt(self, l, half):
        fw = self.fw
        base = l * BLK_PER_LAYER
        h0 = half * 512
        fw.push()
        self.load_rwkv_w(l, False)
        zz = [fw.sb([128, 4, 512], BF16, nm) for nm in ("zr", "zk", "zv")]
        lo = [fw.sb([64, 512], BF16, nm) for nm in ("twdf", "twdb", "adT")]
        GA2 = fw.sb([128, 4, 512], BF16, "GA2")
        fw.push()
        self.stream_begin(5, depth=2)
        for which in range(3):
            slot = self.next_block(base + WIN_IDX[f"A{which}"])
            for c in range(4):
                ps = self.zmm(slot, 512, c * 128, 128, h0, 512)
                i = which * 4 + c
                self.shift_evac(ps, 128, 512, 2, self.mu1[:, i:i + 1], self.muh[:, i:i + 1], zz[which][:, c, :], zz[which])
            self.prefetch_next()
        slot = self.next_block(base + WIN_IDX["A3"])
        for i in range(3):
            ps = self.zmm(slot, 192, i * 64, 64, h0, 512)
            self.shift_evac(ps, 64, 512, 2, self.mu1[0:64, 12 + i:13 + i], self.muh[0:64, 12 + i:13 + i], lo[i][:, :], lo[i], tanh=(i < 2))
        self.prefetch_next()
        slot = self.next_block(base + WIN_IDX["A4"])
        for c in range(4):
            ps = self.zmm(slot, 512, c * 128, 128, h0, 512)
            self.silu2(ps, 128, 512, GA2[:, c, :], GA2)
        self.prefetch_next()
        fw.pop()
        for sq in range(2):
            for pair in range(4):
                t0 = sq * 256
                seqi = half * 2 + sq

                def yout(c, yfin, pair=pair, t0=t0):
                    cs = t0 + c * 128
                    self.dve(lambda: self.V.scalar_tensor_tensor(self.oTs[half][0][:, pair, cs:cs + 128], yfin[:, :], 0.5,
                                                                 GA2[:, pair, cs:cs + 128], ALU.mult, ALU.mult), [yfin, GA2], [self.oTs[half][0]])

                def stout(dd, ST, pair=pair, seqi=seqi):
                    so = self.d["stout"]
                    row = (((l * 2 + dd) * 4 + seqi) * 4 + pair) * 128
                    fw.dma("sp", so, so[row:row + 128, :], ST, ST[:, :], sem_owner=self.outsem)

                J = dict(T=256, r=(zz[0], lambda a, b, pair=pair, t0=t0: zz[0][:, pair, t0 + a:t0 + b]),
                         k=(zz[1], lambda a, b, pair=pair, t0=t0: zz[1][:, pair, t0 + a:t0 + b]),
                         v=(zz[2], lambda a, b, pair=pair, t0=t0: zz[2][:, pair, t0 + a:t0 + b]),
                         twd=[(lo[0], lambda a, b, t0=t0: lo[0][:, t0 + a:t0 + b]), (lo[1], lambda a, b, t0=t0: lo[1][:, t0 + a:t0 + b])],
                         ad=(lo[2], lambda a, b, t0=t0: lo[2][:, t0 + a:t0 + b]),
                         par=lambda nm, pair=pair: self.sp[:, rw_col(nm, pair):rw_col(nm, pair) + 1],
                         parh=lambda nm, pair=pair: self.rwh[:, RW_NAMES.index(nm) * 4 + pair:RW_NAMES.index(nm) * 4 + pair + 1],
                         parbufs=[self.sp, self.rwh],
                         wup=lambda dd, pair=pair: self.wup[:, dd * 512 + pair * 128:dd * 512 + pair * 128 + 128],
                         aup=lambda dd, pair=pair: self.aup[:, dd * 512 + pair * 128:dd * 512 + pair * 128 + 128],
                         st0=None, yout=yout, stout=stout, scoped=False)
                self.rwkv_job(J)
        fw.pop()

    def phaseA_contrib(self, l):
        fw = self.fw
        base = l * BLK_PER_LAYER
        self.GA2 = fw.sb([128, 4, 512], BF16, "GA2s")
        fw.push()
        self.stream_begin(5, depth=2)
        for which, (part, row0) in enumerate((("rk", 0), ("rk", 512), ("vx", VX_OFF["v"]))):
            slot = self.next_block(base + WIN_IDX[f"A{which}"])
            for c in range(4):
                self.contrib_rows(l, slot, 512, c * 128, 128, part, row0 + c * 128)
            self.prefetch_next()
        slot = self.next_block(base + WIN_IDX["A3"])
        for i in range(3):
            self.contrib_rows(l, slot, 192, i * 64, 64, "vx", VX_OFF["lora"] + i * 64)
        self.prefetch_next()
        slot = self.next_block(base + WIN_IDX["A4"])
        for c in range(4):
            ps = self.zmm(slot, 512, c * 128, 128, 0, 512)
            self.silu2(ps, 128, 512, self.GA2[:, c, :], self.GA2)
        self.prefetch_next()
        fw.pop()

    def gather_rows(self, dst, dst_ap, src, idx_col):
        fw = self.fw
        idx = self.idx1 if idx_col < 12 else self.idx2
        col = idx_col if idx_col < 12 else idx_col - 12
        fw.dma("pool", dst, None, src, None, extra_reads=[idx],
               fn=lambda: self.G.indirect_dma_start(out=dst_ap, out_offset=None, in_=src.h.ap(),
                                                    in_offset=bass.IndirectOffsetOnAxis(ap=idx[:, col:col + 1], axis=0)))

    def phaseA_consume(self, l):
        fw = self.fw
        V, A, G = self.V, self.A, self.G
        fw.push()
        self.load_rwkv_w(l, True)
        spo = fw.sb([128, 12], F32, "spo")
        self.load(spo, spo[:, :], self.d["spo"], self.d["spo"][l, :, :])
        spoh = fw.sb([128, 4], F32, "spoh")
        self.dve(lambda: V.tensor_scalar(spoh[:, :], spo[:, 0:4], 0.5, None, ALU.mult), [spo], [spoh])
        mu1o = fw.sb([128, 3], F32, "mu1o")
        muho = fw.sb([128, 3], F32, "muho")
        self.dve(lambda: V.tensor_scalar(mu1o[:, :], spo[:, 9:12], -1.0, 1.0, ALU.mult, ALU.add), [spo], [mu1o])
        self.dve(lambda: V.tensor_scalar(muho[:, :], spo[:, 9:12], 0.5, None, ALU.mult), [spo], [muho])
        T = DSEQ
        zz = [fw.sb([128, T], BF16, nm) for nm in ("sr", "sk", "sv")]
        lo = [fw.sb([64, T], BF16, nm) for nm in ("stwf", "stwb", "sad")]
        fw.push()
        raw = fw.sb([128, T], BF16, "sraw")
        o32 = fw.sb([128, T], F32, "so32")

        def shift_full(rows, src, m1, mh, dst, tanh=False):
            self.dve(lambda: V.tensor_scalar(o32[:rows, :], src[:rows, :], m1, None, ALU.mult), [src, mu1o, self.mu1], [o32])
            self.dve(lambda: V.scalar_tensor_tensor(o32[:rows, 1:T], src[:rows, 0:T - 1], mh, o32[:rows, 1:T], ALU.mult, ALU.add),
                     [src, muho, self.muh, o32], [o32])
            self.dve(lambda: V.scalar_tensor_tensor(o32[:rows, 0:T - 1], src[:rows, 1:T], mh, o32[:rows, 0:T - 1], ALU.mult, ALU.add),
                     [src, muho, self.muh, o32], [o32])
            if tanh:
                self.act(lambda: A.activation(dst[:rows, :], o32[:rows, :], AF.Tanh), [o32], [dst])
            else:
                self.act(lambda: A.copy(dst[:rows, :], o32[:rows, :]), [o32], [dst])

        for which in range(3):
            src = self.ag1_out[l]["rk" if which < 2 else "vx"]
            for q in range(4):
                self.gather_rows(raw, raw[:, q * 512:(q + 1) * 512], src, which * 4 + q)
            shift_full(128, raw, mu1o[:, which:which + 1], muho[:, which:which + 1], zz[which])
        agv = self.ag1_out[l]["vx"]
        for i in range(3):
            for q in range(4):
                r0 = q * 864 + VX_OFF["lora"] + i * 64
                fw.dma("sp", raw, raw[0:64, q * 512:(q + 1) * 512], agv, agv[r0:r0 + 64, :])
            shift_full(64, raw, self.mu1[0:64, 12 + i:13 + i], self.muh[0:64, 12 + i:13 + i], lo[i], tanh=(i < 2))
        fw.pop()
        stg = [None]

        def yout(c, yfin):
            q, cc = c // 4, c % 4
            if cc == 0:
                stg[0] = fw.rot([128, 512], BF16, "ystg", n=2)
            st = stg[0]
            self.act(lambda: A.copy(st[:, cc * 128:(cc + 1) * 128], yfin[:, :]), [yfin], [st])
            if cc == 3:
                ag = self.ag2_in[l]
                fw.dma("sp", ag, ag[q * 128:(q + 1) * 128, :], st, st[:, :])

        pidx = {nm: i for i, nm in enumerate(RW_NAMES)}
        J = dict(T=T, r=(zz[0], lambda a, b: zz[0][:, a:b]), k=(zz[1], lambda a, b: zz[1][:, a:b]), v=(zz[2], lambda a, b: zz[2][:, a:b]),
                 twd=[(lo[0], lambda a, b: lo[0][:, a:b]), (lo[1], lambda a, b: lo[1][:, a:b])],
                 ad=(lo[2], lambda a, b: lo[2][:, a:b]),
                 par=lambda nm: spo[:, pidx[nm]:pidx[nm] + 1],
                 parh=lambda nm: spoh[:, pidx[nm]:pidx[nm] + 1],
                 parbufs=[spo, spoh],
                 wup=lambda dd: self.wup[:, dd * 128:(dd + 1) * 128],
                 aup=lambda dd: self.aup[:, dd * 128:(dd + 1) * 128],
                 st0=lambda dd: (self.d["st0"], self.d["st0"][l, dd, :, :]), yout=yout, stout=None, seg=256, segpar=False)
        self.rwkv_job(J)
        self.allgather(self.ag2_out[l], self.ag2_in[l])
        fw.pop()

    def phaseA_final(self, l):
        fw = self.fw
        fw.push()
        for r in range(4):
            ya = fw.rot([128, 512], BF16, "ya", n=2)
            self.gather_rows(ya, ya[:, :], self.ag2_out[l], 12 + r)
            self.dve(lambda r=r, ya=ya: self.V.scalar_tensor_tensor(self.oTs[0][0][:, r, 0:512], ya[:, :], 0.5, self.GA2[:, r, :], ALU.mult, ALU.mult),
                     [ya, self.GA2], [self.oTs[0][0]])
        fw.pop()

    def rwkv_job(self, J):
        fw = self.fw
        V, A, G, T_ = self.V, self.A, self.G, self.T
        T = J["T"]
        nch = T // 128
        SEG = J.get("seg", 256)
        nseg = T // SEG
        ncs = SEG // 128
        rB, rf = J["r"]
        kB, kf = J["k"]
        vB, vf = J["v"]
        adB, adf = J["ad"]
        par, parh, pbufs = J["par"], J["parh"], J["parbufs"]
        self.ps_range = (0, 6)
        scoped = J.get("scoped", True)
        jpush = (lambda: fw.push()) if scoped else (lambda: None)
        jpop = (lambda: fw.pop()) if scoped else (lambda: None)
        jt = (lambda shp, dt, nm: fw.sb(shp, dt, nm)) if scoped else (lambda shp, dt, nm: fw.rot(shp, dt, "J" + nm, n=1))
        jpush()
        kap = jt([128, T], BF16, "kap")
        Vtm = jt([128, nch, 128], BF16, "Vtm")
        Yacc = jt([128, nch, 128], F32, "Yacc")
        Bacc = jt([128, T], F32, "Bacc")
        ST = [jt([128, 64], F32, f"ST{dd}") for dd in range(2)]
        STb = [jt([128, 64], BF16, f"STb{dd}") for dd in range(2)]
        KW = min(T, 512)
        self.pool(lambda: G.memset(Yacc[:, :, :], 0.0), [], [Yacc])
        self.pool(lambda: G.memset(Bacc[:, :], 0.0), [], [Bacc])
        jpush()
        for p0 in range(0, T, 512):
            N = min(512, T - p0)
            kk = fw.rot([128, KW], F32, "kk", n=(2 if scoped else 1))
            sq = fw.rot([128, KW], BF16, "kksq", n=(2 if scoped else 1))
            self.dve(lambda: V.tensor_scalar(kk[:, :N], kf(p0, p0 + N), par("k_k"), None, ALU.mult), [kB] + pbufs, [kk])
            self.act(lambda: A.activation(sq[:, :N], kk[:, :N], AF.Square), [kk], [sq])
            ps = self.nps()
            self.pe(lambda: T_.matmul(ps[:, :N], self.bones[:, :], sq[:, :N], start=True, stop=True), [self.bones, sq], [ps])
            t = fw.rot([128, KW], F32, "kkt", n=(2 if scoped else 1))
            self.dve(lambda: V.tensor_scalar(t[:, :N], ps[:, :N], 1e-24, None, ALU.max), [ps], [t])
            self.rsqrt(t, t[:, :N], t, t[:, :N])
            self.dve(lambda: V.tensor_tensor(kap[:, p0:p0 + N], kk[:, :N], t[:, :N], ALU.mult), [kk, t], [kap])
        import os
        STOP = int(os.environ.get("RWKV_STOP", "99"))
        if STOP <= 1:
            jpop(); jpop(); return
        for c in range(nch):
            self.transpose_to(vB, vf(c * 128, c * 128 + 128), Vtm, Vtm[:, c, :], eng=("act" if c % 2 else "dve"))
        jpop()
        if STOP <= 2:
            jpop(); return
        jpush()
        for dd in range(2):
            if J["st0"] is None:
                self.pool(lambda dd=dd: G.memset(ST[dd][:, :], 0.0), [], [ST[dd]])
            else:
                src, sap = J["st0"](dd)
                fw.dma("sp", ST[dd], ST[dd][:, :], src, sap)
            self.act(lambda dd=dd: A.copy(STb[dd][:, :], ST[dd][:, :]), [ST[dd]], [STb[dd]])
        MK = self.mask
        def rw_segment(dd, sg, res, segpar):
            sfx = "fb"[dd]
            twB, twf = J["twd"][dd]
            s0 = sg * SEG
            N = SEG
            f32t = lambda nm: fw.rot([128, SEG], F32, nm + (str(dd) if segpar else ""), n=1)
            a = f32t("ra")
            ps = self.nps()
            self.pe(lambda: T_.matmul(ps[:, :N], J["aup"](dd), adf(s0, s0 + N), start=True, stop=True), [self.aup, adB], [ps])
            self.act(lambda: A.activation(a[:, :], ps[:, :N], AF.Tanh, bias=parh("a0_" + sfx), scale=0.5), [ps] + pbufs, [a])
            yield
            self.dve(lambda: V.tensor_scalar(a[:, :], a[:, :], 0.5, 0.5, ALU.mult, ALU.add), [a], [a])
            kt = f32t("rkt")
            self.dve(lambda: V.tensor_scalar(kt[:, :], a[:, :], 1.0, par("k_a"), ALU.subtract, ALU.mult), [a] + pbufs, [kt])
            self.dve(lambda: V.scalar_tensor_tensor(kt[:, :], kt[:, :], 1.0, kf(s0, s0 + N), ALU.add, ALU.mult), [kt, kB], [kt])
            b = f32t("rb")
            self.dve(lambda: V.tensor_tensor(b[:, :], a[:, :], kap[:, s0:s0 + N], ALU.mult), [a, kap], [b])
            lw = f32t("rlw")
            ps = self.nps()
            self.pe(lambda: T_.matmul(ps[:, :N], J["wup"](dd), twf(s0, s0 + N), start=True, stop=True), [self.wup, twB], [ps])
            self.act(lambda: A.activation(lw[:, :], ps[:, :N], AF.Tanh, bias=parh("w0_" + sfx), scale=0.5), [ps] + pbufs, [lw])
            yield
            self.dve(lambda: V.tensor_scalar(lw[:, :], lw[:, :], -0.3032653298563167, -0.3032653298563167, ALU.mult, ALU.add), [lw], [lw])
            rkr = fw.rot([128, SEG], BF16, "rkr" + str(dd), n=1)
            self.dve(lambda: V.scalar_tensor_tensor(rkr[:, :], kt[:, :], par("r_k"), rf(s0, s0 + N), ALU.mult, ALU.mult),
                     [kt, rB] + pbufs, [rkr])
            ps = self.nps()
            self.pe(lambda: T_.matmul(ps[:, :N], self.bones[:, :], rkr[:, :], start=True, stop=True), [self.bones, rkr], [ps])
            self.dve(lambda: V.tensor_tensor(Bacc[:, s0:s0 + N], Bacc[:, s0:s0 + N], ps[:, :N], ALU.add), [Bacc, ps], [Bacc])
            P = f32t("rP")
            for c in range(ncs):
                self.dve(lambda c=c: V.tensor_tensor_scan(P[:, c * 128:(c + 1) * 128], self.onesf[:, :], lw[:, c * 128:(c + 1) * 128],
                                                          0.0, ALU.mult, ALU.add), [self.onesf, lw], [P])
            Q = f32t("rQ")
            R = f32t("rR")
            self.dve(lambda: V.tensor_tensor(Q[:, :], P[:, :], lw[:, :], ALU.subtract), [P, lw], [Q])
            for c in range(ncs):
                self.dve(lambda c=c: V.tensor_scalar(R[:, c * 128:(c + 1) * 128], P[:, c * 128:(c + 1) * 128], -1.0,
                                                     P[:, c * 128 + 127:c * 128 + 128], ALU.mult, ALU.add), [P], [R])
            gL = fw.rot([128, 2], F32, "gL" + str(dd), n=2)
            self.act(lambda: A.activation(gL[:, 0:ncs], P[:, 127:SEG:128], AF.Exp), [P], [gL])
            yield
            if dd == 0:
                srcs = [(P, -1.0, a), (Q, 1.0, Q), (P, 1.0, P), (R, 1.0, R)]
            else:
                RL = f32t("rRL")
                self.dve(lambda: V.tensor_tensor(RL[:, :], R[:, :], lw[:, :], ALU.add), [R, lw], [RL])
                srcs = [(RL, -1.0, a), (R, 1.0, R), (RL, 1.0, RL), (Q, 1.0, Q)]
            E = [None] * 4
            for i, (sb_, sc_, dst_) in enumerate(srcs):
                self.act(lambda i=i, sb_=sb_, sc_=sc_, dst_=dst_: A.activation(dst_[:, :], sb_[:, :], AF.Exp, scale=sc_), [sb_], [dst_])
                E[i] = dst_
            Kd2 = fw.rot([128, 2, SEG], BF16, "Kd2" + str(dd), n=1)
            Bd2 = fw.rot([128, 2, SEG], BF16, "Bd2" + str(dd), n=1)
            KL = fw.rot([128, SEG], BF16, "KL" + str(dd), n=1)
            BL = fw.rot([128, SEG], BF16, "BL" + str(dd), n=1)
            KqRq2 = fw.rot([128, 2, ncs, 2, 128], BF16, "KqRq2" + str(dd), n=1)
            hm = self.hm
            kap3 = kap[:, s0:s0 + N].rearrange("p (c t) -> p c t", c=ncs)
            r3 = rf(s0, s0 + N).rearrange("p (c t) -> p c t", c=ncs)
            for h in range(2):
                self.dve(lambda h=h: V.scalar_tensor_tensor(Kd2[:, h, :], kt[:, :], hm[:, h:h + 1], E[0][:, :], ALU.mult, ALU.mult), [kt, hm, E[0]], [Kd2])
                self.dve(lambda h=h: V.scalar_tensor_tensor(Bd2[:, h, :], b[:, :], hm[:, h:h + 1], E[0][:, :], ALU.mult, ALU.mult), [b, hm, E[0]], [Bd2])
                self.dve(lambda h=h: V.scalar_tensor_tensor(KqRq2[:, h, :, 0, :], kap3, hm[:, h:h + 1], E[1][:, :].rearrange("p (c t) -> p c t", c=ncs),
                                                            ALU.mult, ALU.mult), [kap, hm, E[1]], [KqRq2])
                self.dve(lambda h=h: V.scalar_tensor_tensor(KqRq2[:, h, :, 1, :], r3, hm[:, h:h + 1], E[2][:, :].rearrange("p (c t) -> p c t", c=ncs),
                                                            ALU.mult, ALU.mult), [rB, hm, E[2]], [KqRq2])
            self.dve(lambda: V.tensor_tensor(KL[:, :], kt[:, :], E[3][:, :], ALU.mult), [kt, E[3]], [KL])
            self.dve(lambda: V.tensor_tensor(BL[:, :], b[:, :], E[3][:, :], ALU.mult), [b, E[3]], [BL])

            res.update(dict(Kd2=Kd2, Bd2=Bd2, KL=KL, BL=BL, KqRq2=KqRq2, gL=gL, sg=sg))
            yield

        def rw_pre(dd, c, Pd, res):
            sfx2 = f"{dd}{c}"
            mk0 = dd * 1280
            Kd2, Bd2, KqRq2 = Pd["Kd2"], Pd["Bd2"], Pd["KqRq2"]
            cs = slice(c * 128, (c + 1) * 128)
            Am = [fw.rot([128, 512], BF16, f"Am{h}_{sfx2}", n=1) for h in range(2)]
            psB = self.nps()
            for h in range(2):
                psA = self.nps()
                rhsA = KqRq2[:, h, c, :, :].rearrange("p a t -> p (a t)")
                self.pe(lambda h=h, psA=psA, rhsA=rhsA: T_.matmul(psA[:, 0:256], Kd2[:, h, cs], rhsA, start=True, stop=True),
                        [Kd2, KqRq2], [psA], signal=False)
                self.pe(lambda h=h, psA=psA, rhsA=rhsA: T_.matmul(psA[:, 256:512], Bd2[:, h, cs], rhsA, start=True, stop=True),
                        [Bd2, KqRq2], [psA])
                self.dve(lambda h=h, psA=psA: V.tensor_tensor(Am[h][:, :], psA[:, :], MK[:, mk0:mk0 + 512], ALU.mult), [psA, MK], [Am[h]])
                self.pe(lambda h=h: T_.matmul(psB[:, h * 128:(h + 1) * 128], KqRq2[:, h, c, 0, :], Bd2[:, h, cs], start=True, stop=True),
                        [KqRq2, Bd2], [psB], signal=(h == 1))
            PT = [fw.rot([128, 2, 128], BF16, f"PT{i}_{sfx2}", n=1) for i in range(2)]
            PX = [fw.rot([128, 2, 256], BF16, f"PX{i}_{sfx2}", n=1) for i in range(2)]
            C64T = fw.rot([128, 2, 128], BF16, "C64T" + sfx2, n=1)
            C128T = fw.rot([128, 2, 128], BF16, "C128T" + sfx2, n=1)
            f2 = lambda t: t[:, :, :].rearrange("p a t -> p (a t)")
            self.dve(lambda: V.tensor_tensor(f2(PT[0]), psB[:, 0:256], MK[:, mk0 + 512:mk0 + 768], ALU.mult), [psB, MK], [PT[0]])
            self.dve(lambda: V.tensor_tensor(f2(C64T), psB[:, 0:256], MK[:, mk0 + 768:mk0 + 1024], ALU.mult), [psB, MK], [C64T])
            self.dve(lambda: V.tensor_tensor(f2(C128T), psB[:, 0:256], MK[:, mk0 + 1024:mk0 + 1280], ALU.mult), [psB, MK], [C128T])
            for h in range(2):
                self.act(lambda h=h: A.copy(PX[0][:, h, 0:128], Am[h][:, 256:384]), [Am[h]], [PX[0]])
            yield
            cur = 0
            Xb = fw.rot([128, 2, 128], BF16, "Xb32" + sfx2, n=1)
            for j in range(1, 6):
                nxt = 1 - cur
                if j == 1:
                    ps = self.nps()
                    pst_ = self.nps()
                    for h in range(2):
                        self.pe(lambda h=h, ps=ps, cur=cur: T_.matmul(ps[:, h * 256:h * 256 + 128], PT[cur][:, h, :], PX[cur][:, h, 0:128],
                                                                      start=True, stop=True), [PT[cur], PX[cur]], [ps], signal=(h == 1))
                        self.pe(lambda h=h, pst_=pst_, cur=cur: T_.matmul(pst_[:, h * 128:(h + 1) * 128], PX[cur][:, h, 0:128], PT[cur][:, h, :],
                                                                          start=True, stop=True), [PT[cur], PX[cur]], [pst_], signal=(h == 1))
                    for h in range(2):
                        self.dve(lambda h=h, cur=cur, nxt=nxt: V.tensor_tensor(PX[nxt][:, h, 128:256], PX[cur][:, h, 0:128], self.ident[:, :], ALU.add),
                                 [PX[cur], self.ident], [PX[nxt]])
                    self.act(lambda ps=ps, nxt=nxt: A.copy(PX[nxt][:, :, 0:128], ps[:, :].rearrange("p (a t) -> p a t", a=2)[:, :, 0:128]),
                             [ps], [PX[nxt]])
                    self.act(lambda pst_=pst_, nxt=nxt: A.copy(f2(PT[nxt]), pst_[:, 0:256]), [pst_], [PT[nxt]])
                elif j < 5:
                    ps = self.nps()
                    pst_ = self.nps()
                    for h in range(2):
                        self.pe(lambda h=h, ps=ps, cur=cur: T_.matmul(ps[:, h * 256:(h + 1) * 256], PT[cur][:, h, :], PX[cur][:, h, :],
                                                                      start=True, stop=True), [PT[cur], PX[cur]], [ps], signal=(h == 1))
                        self.pe(lambda h=h, pst_=pst_, cur=cur: T_.matmul(pst_[:, h * 128:(h + 1) * 128], PX[cur][:, h, 0:128], PT[cur][:, h, :],
                                                                          start=True, stop=True), [PT[cur], PX[cur]], [pst_], signal=(h == 1))
                    ps3 = ps[:, :].rearrange("p (a t) -> p a t", a=2)
                    self.act(lambda ps3=ps3, ps=ps, nxt=nxt: A.copy(PX[nxt][:, :, 0:128], ps3[:, :, 0:128]), [ps], [PX[nxt]])
                    self.dve(lambda ps3=ps3, ps=ps, cur=cur, nxt=nxt: V.tensor_tensor(PX[nxt][:, :, 128:256], ps3[:, :, 128:256], PX[cur][:, :, 128:256], ALU.add),
                             [ps, PX[cur]], [PX[nxt]])
                    self.act(lambda pst_=pst_, nxt=nxt: A.copy(f2(PT[nxt]), pst_[:, 0:256]), [pst_], [PT[nxt]])
                else:
                    ps = self.nps()
                    for h in range(2):
                        self.pe(lambda h=h, ps=ps, cur=cur: T_.matmul(ps[:, h * 128:(h + 1) * 128], PT[cur][:, h, :], PX[cur][:, h, 128:256],
                                                                      start=True, stop=True), [PT[cur], PX[cur]], [ps], signal=(h == 1))
                    self.dve(lambda ps=ps, cur=cur: V.tensor_tensor(Xb[:, :, :], ps[:, 0:256].rearrange("p (a t) -> p a t", a=2), PX[cur][:, :, 128:256], ALU.add),
                             [ps, PX[cur]], [Xb])
                cur = nxt
                yield
            TT = None
            for lvl, CT in enumerate((C64T, C128T)):
                XT = fw.rot([128, 2, 128], BF16, "XTm" + sfx2, n=1)
                Zt = fw.rot([128, 2, 128], BF16, "Ztm" + sfx2, n=1)
                ptt = self.pst[self.pst_rr % 2]
                self.pst_rr += 1
                for h in range(2):
                    self.pe(lambda h=h, ptt=ptt, Xb=Xb: T_.transpose(ptt[:, h * 128:(h + 1) * 128], Xb[:, h, :], self.ident[:, :]),
                            [Xb, self.ident], [ptt], signal=(h == 1))
                self.act(lambda ptt=ptt, XT=XT: A.copy(f2(XT), ptt[:, 0:256]), [ptt], [XT])
                psz = self.nps()
                for h in range(2):
                    self.pe(lambda h=h, psz=psz, CT=CT, Xb=Xb: T_.matmul(psz[:, h * 128:(h + 1) * 128], CT[:, h, :], Xb[:, h, :], start=True, stop=True),
                            [CT, Xb], [psz], signal=(h == 1))
                self.dve(lambda psz=psz, Zt=Zt: V.tensor_copy(f2(Zt), psz[:, 0:256]), [psz], [Zt])
                psw = self.nps()
                for h in range(2):
                    self.pe(lambda h=h, psw=psw, XT=XT, Zt=Zt: T_.matmul(psw[:, h * 128:(h + 1) * 128], XT[:, h, :], Zt[:, h, :], start=True, stop=True),
                            [XT, Zt], [psw], signal=(h == 1))
                Xn = fw.rot([128, 2, 128], BF16, ("Xb64" if lvl == 0 else "TT") + sfx2, n=1)
                self.dve(lambda psw=psw, Xn=Xn, Xb=Xb: V.tensor_tensor(f2(Xn), psw[:, 0:256], f2(Xb), ALU.add), [psw, Xb], [Xn])
                Xb = Xn
                yield
            TT = Xb

            res["Am"] = Am
            res["TT"] = TT
            yield

        def rw_seq(dd, Pd, pres):
            KL, BL, KqRq2, gL, sg = Pd["KL"], Pd["BL"], Pd["KqRq2"], Pd["gL"], Pd["sg"]
            chunks = list(range(ncs)) if dd == 0 else list(reversed(range(ncs)))
            for c in chunks:
                cg = sg * ncs + c
                cs = slice(c * 128, (c + 1) * 128)
                Am, TT = pres[(dd, c)]["Am"], pres[(dd, c)]["TT"]
                Sb = STb[dd]
                psG = self.nps()
                for h in range(2):
                    hs = slice(64 * h, 64 * h + 64)
                    vs = slice(64 * h, 64 * h + 64)
                    self.pe(lambda h=h, vs=vs: T_.matmul(psG[:, vs], KqRq2[:, h, c, 0, :], Sb[:, :], start=(h == 0), stop=False, skip_group_check=True),
                            [KqRq2, Sb], [psG], signal=False)
                    self.pe(lambda h=h, vs=vs: T_.matmul(psG[:, vs], Am[h][:, 0:128], Vtm[:, cg, vs], start=False, stop=(h == 1), skip_group_check=True),
                            [Am[h], Vtm], [psG], signal=(h == 1))
                Gn = fw.rot([128, 128], BF16, "Gn" + str(dd), n=1)
                self.act(lambda: A.mul(Gn[:, :], psG[:, 0:128], -1.0), [psG], [Gn])
                yield
                psU = self.nps()
                for h in range(2):
                    vs = slice(64 * h, 64 * h + 64)
                    self.pe(lambda h=h, vs=vs: T_.matmul(psU[:, vs], TT[:, h, :], Gn[:, vs], start=(h == 0), stop=(h == 1), skip_group_check=True),
                            [TT, Gn], [psU], signal=(h == 1))
                U = fw.rot([128, 128], BF16, "U" + str(dd), n=1)
                self.dve(lambda: V.tensor_copy(U[:, :], psU[:, 0:128]), [psU], [U])
                yield
                yield
                psY = self.nps()
                for h in range(2):
                    hs = slice(64 * h, 64 * h + 64)
                    vs = slice(64 * h, 64 * h + 64)
                    self.pe(lambda h=h, vs=vs: T_.matmul(psY[:, vs], KqRq2[:, h, c, 1, :], Sb[:, :], start=(h == 0), stop=False, skip_group_check=True),
                            [KqRq2, Sb], [psY], signal=False)
                    self.pe(lambda h=h, vs=vs: T_.matmul(psY[:, vs], Am[h][:, 128:256], Vtm[:, cg, vs], start=False, stop=False, skip_group_check=True),
                            [Am[h], Vtm], [psY], signal=False)
                    self.pe(lambda h=h, vs=vs: T_.matmul(psY[:, vs], Am[h][:, 384:512], U[:, vs], start=False, stop=(h == 1), skip_group_check=True),
                            [Am[h], U], [psY], signal=(h == 1))
                self.dve(lambda: V.tensor_tensor(Yacc[:, cg, :], Yacc[:, cg, :], psY[:, 0:128], ALU.add), [Yacc, psY], [Yacc])
                yield
                yield
                KLt = fw.rot([128, 128], BF16, "KLt" + str(dd), n=1)
                BLt = fw.rot([128, 128], BF16, "BLt" + str(dd), n=1)
                self.transpose_to(KL, KL[:, cs], KLt, KLt[:, :], eng="act")
                self.transpose_to(BL, BL[:, cs], BLt, BLt[:, :], eng="dve")
                psS = self.nps()
                for h in range(2):
                    hs = slice(64 * h, 64 * h + 64)
                    vs = slice(64 * h, 64 * h + 64)
                    self.pe(lambda h=h, hs=hs, vs=vs: T_.matmul(psS[hs, 0:64], KLt[:, hs], Vtm[:, cg, vs], start=True, stop=False),
                            [KLt, Vtm], [psS], signal=False)
                    self.pe(lambda h=h, hs=hs, vs=vs: T_.matmul(psS[hs, 0:64], BLt[:, hs], U[:, vs], start=False, stop=True),
                            [BLt, U], [psS], signal=(h == 1))
                self.dve(lambda: V.scalar_tensor_tensor(ST[dd][:, :], ST[dd][:, :], gL[:, c:c + 1], psS[:, 0:64], ALU.mult, ALU.add),
                         [ST[dd], gL, psS], [ST[dd]])
                self.act(lambda: A.copy(STb[dd][:, :], ST[dd][:, :]), [ST[dd]], [STb[dd]])

        def run_rr(gens):
            gens = list(gens)
            while gens:
                for g in list(gens):
                    try:
                        next(g)
                    except StopIteration:
                        gens.remove(g)

        for step in range(nseg):
            sgs = (step, nseg - 1 - step)
            Pd = [{}, {}]
            segpar = J.get("segpar", False)
            if segpar:
                run_rr([rw_segment(dd, sgs[dd], Pd[dd], True) for dd in range(2)])
            else:
                for dd in range(2):
                    for _ in rw_segment(dd, sgs[dd], Pd[dd], False):
                        pass
            if STOP <= 3:
                continue
            pres = {(dd, c): {} for dd in range(2) for c in range(ncs)}
            run_rr([rw_pre(dd, c, Pd[dd], pres[(dd, c)]) for dd in range(2) for c in range(ncs)])
            if STOP <= 5:
                continue
            run_rr([rw_seq(dd, Pd[dd], pres) for dd in range(2)])
        if J["stout"] is not None:
            for dd in range(2):
                J["stout"](dd, ST[dd])
        jpop()
        if STOP <= 6:
            jpop(); return
        n2 = nch * 2
        sums = jt([128, n2], F32, "gsum")
        ssq = jt([128, n2], F32, "gssq")
        Ysq = jt([128, nch * 128], F32, "Ysq") if scoped else fw.rot([128, KW], F32, "kk", n=1)
        Yf = Yacc[:, :, :].rearrange("p c x -> p (c x)")
        self.dve(lambda: V.tensor_reduce(sums[:, :], Yf.rearrange("p (g x) -> p g x", x=64), AX.X, ALU.add), [Yacc], [sums])
        self.act(lambda: A.activation(Ysq[:, :], Yf, AF.Square), [Yacc], [Ysq])
        self.dve(lambda: V.tensor_reduce(ssq[:, :], Ysq[:, :].rearrange("p (g x) -> p g x", x=64), AX.X, ALU.add), [Ysq], [ssq])
        mean = jt([128, n2], F32, "gmean")
        var = jt([128, n2], F32, "gvar2")
        self.dve(lambda: V.tensor_scalar(mean[:, :], sums[:, :], 1.0 / 64, None, ALU.mult), [sums], [mean])
        self.dve(lambda: V.tensor_tensor(var[:, :], mean[:, :], mean[:, :], ALU.mult), [mean], [var])
        self.dve(lambda: V.scalar_tensor_tensor(var[:, :], ssq[:, :], 1.0 / 64, var[:, :], ALU.mult, ALU.subtract), [ssq, var], [var])
        self.dve(lambda: V.tensor_scalar(var[:, :], var[:, :], GN_EPS, None, ALU.add), [var], [var])
        self.rsqrt(var, var[:, :], var, var[:, :])
        yn = jt([128, nch, 128], BF16, "yn")
        for c in range(nch):
            for h in range(2):
                g = c * 2 + h
                self.dve(lambda c=c, h=h, g=g: V.tensor_scalar(yn[:, c, h * 64:(h + 1) * 64], Yacc[:, c, h * 64:(h + 1) * 64], mean[:, g:g + 1], var[:, g:g + 1],
                                                               ALU.subtract, ALU.mult), [Yacc, mean, var], [yn])
        for c in range(nch):
            pt = self.pst[self.pst_rr % 2]
            self.pst_rr += 1
            self.pe(lambda c=c, pt=pt: T_.transpose(pt[:, :128], yn[:, c, :], self.ident[:, :]), [yn, self.ident], [pt])
            yT = fw.rot([128, 128], F32, "yT", n=(2 if scoped else 1))
            self.act(lambda pt=pt, yT=yT: A.activation(yT[:, :], pt[:, :128], AF.Identity, bias=par("ln_b"), scale=par("ln_g")), [pt] + pbufs, [yT])
            bo = fw.rot([128, 128], F32, "bo", n=(2 if scoped else 1))
            self.dve(lambda c=c, bo=bo: V.tensor_tensor(bo[:, :], Bacc[:, c * 128:(c + 1) * 128], vf(c * 128, (c + 1) * 128), ALU.mult), [Bacc, vB], [bo])
            yfin = fw.rot([128, 128], F32, "yfin", n=(2 if scoped else 1))
            self.dve(lambda yT=yT, bo=bo, yfin=yfin: V.tensor_tensor(yfin[:, :], yT[:, :], bo[:, :], ALU.add), [yT, bo], [yfin])
            J["yout"](c, yfin)
        jpop()
        self.ps_range = (0, 4)

    def load_mla_w(self, l):
        fw, d = self.fw, self.d
        self.wq = fw.sb([128, 2 * 8 * 96], BF16, "wq")
        self.wqs = fw.sb([128, 2 * 8 * 32], BF16, "wqs")
        self.wkk = fw.sb([128, 512], BF16, "wkk")
        self.wkv = fw.sb([128, 512], BF16, "wkv")
        for nm in ("wq", "wqs", "wkk", "wkv"):
            t = getattr(self, nm)
            self.load(t, t[:], d[nm], d[nm][l, :, :])

    def mla_front(self, l, h0, rope, GB2, Qh, ckv_f, ckv_b, kr_f, kr_b):
        fw = self.fw
        base = l * BLK_PER_LAYER
        W = WIN_W["B0"]
        slot = self.next_block(base + WIN_IDX["B0"])
        qd = fw.sb([128, 2, 512], F32, "qd")
        kvd = fw.sb([128, 512], F32, "kvd")
        for c in range(2):
            ps = self.zmm(slot, W, c * 128, 128, h0, 512)
            self.act(lambda c=c, ps=ps: self.A.copy(qd[:, c, :], ps[:, :]), [ps], [qd])
        ps = self.zmm(slot, W, 256, 128, h0, 512)
        self.dve(lambda ps=ps: self.V.tensor_copy(kvd[:, :], ps[:, :]), [ps], [kvd])
        pk = self.zmm(slot, W, 384, 32, h0, 512, prow=64)
        R = self.rope
        if rope:
            pks = self.zmm(slot, W, 416, 32, h0, 512, prow=64)
            t1 = fw.sb([96, 512], F32, "krt1")
            self.dve(lambda: self.V.tensor_tensor(t1[64:96, :], pk[64:96, :], R[64:96, 0:512], ALU.mult), [pk, R], [t1])
            self.dve(lambda: self.V.tensor_tensor(kr_f[64:96, :], pks[64:96, :], R[64:96, 512:1024], ALU.mult), [pks, R], [kr_f])
            self.dve(lambda: self.V.tensor_tensor(kr_f[64:96, :], kr_f[64:96, :], t1[64:96, :], ALU.add), [kr_f, t1], [kr_f])
        else:
            self.act(lambda: self.A.copy(kr_f[64:96, :], pk[64:96, :]), [pk], [kr_f])
        self.act(lambda: self.A.copy(kr_b[64:96, :], kr_f[64:96, :]), [kr_f], [kr_b])
        self.prefetch_next()
        slot = self.next_block(base + WIN_IDX["B1"])
        for c in range(4):
            ps = self.zmm(slot, 512, c * 128, 128, h0, 512)
            self.silu2(ps, 128, 512, GB2[:, c, :], GB2)
        self.prefetch_next()
        rq = fw.sb([128, 512], F32, "rq")
        self.rstd_tile(qd, [qd[:, c, :] for c in range(2)], 256.0, rq, 512)
        qn = fw.sb([128, 2, 512], BF16, "qn")
        for c in range(2):
            self.dve(lambda c=c: self.V.scalar_tensor_tensor(qn[:, c, :], qd[:, c, :], self.spc("qn", c), rq[:, :], ALU.mult, ALU.mult),
                     [qd, self.sp, rq], [qn])
        rk = fw.sb([128, 512], F32, "rkv")
        self.rstd_tile(kvd, [kvd[:, :]], 128.0, rk, 512)
        self.dve(lambda: self.V.scalar_tensor_tensor(ckv_f[:, :], kvd[:, :], self.spc("kvn", 0), rk[:, :], ALU.mult, ALU.mult),
                 [kvd, self.sp, rk], [ckv_f])
        self.act(lambda: self.A.copy(ckv_b[:, :], ckv_f[:, :]), [ckv_f], [ckv_b])
        for h in range(8):
            ps = self.nps()
            for c in range(2):
                self.pe(lambda c=c, h=h, ps=ps: self.T.matmul(ps[:96, :], self.wq[:, (c * 8 + h) * 96:(c * 8 + h) * 96 + 96], qn[:, c, :],
                                                             start=(c == 0), stop=(c == 1)), [self.wq, qn], [ps], signal=(c == 1))
            if rope:
                ps2 = self.nps()
                for c in range(2):
                    self.pe(lambda c=c, h=h, ps2=ps2: self.T.matmul(ps2[64:96, :], self.wqs[:, (c * 8 + h) * 32:(c * 8 + h) * 32 + 32], qn[:, c, :],
                                                                   start=(c == 0), stop=(c == 1)), [self.wqs, qn], [ps2], signal=(c == 1))
                t1 = fw.rot([96, 512], F32, "qrt1")
                t2 = fw.rot([96, 512], F32, "qrt2")
                self.dve(lambda ps=ps, t1=t1: self.V.tensor_tensor(t1[64:96, :], ps[64:96, :], R[64:96, 0:512], ALU.mult), [ps, R], [t1])
                self.dve(lambda ps2=ps2, t2=t2: self.V.tensor_tensor(t2[64:96, :], ps2[64:96, :], R[64:96, 512:1024], ALU.mult), [ps2, R], [t2])
                self.dve(lambda h=h, t1=t1, t2=t2: self.V.tensor_tensor(Qh[h][64:96, :], t1[64:96, :], t2[64:96, :], ALU.add), [t1, t2], [Qh[h]])
                self.act(lambda h=h, ps=ps: self.A.copy(Qh[h][0:64, :], ps[0:64, :]), [ps], [Qh[h]])
            else:
                self.act(lambda h=h, ps=ps: self.A.copy(Qh[h][:, :], ps[:96, :]), [ps], [Qh[h]])

    def mla_kv_chunk(self, ckv_b, kr_b, heads, Kh, Vaug, nk):
        for i, h in enumerate(heads):
            ps = self.nps()
            self.pe(lambda h=h, ps=ps: self.T.matmul(ps[:64, :nk], self.wkk[:, h * 64:(h + 1) * 64], ckv_b[:, :nk], start=True, stop=True),
                    [self.wkk, ckv_b], [ps])
            self.act(lambda i=i, ps=ps: self.A.copy(Kh[i][0:64, :nk], ps[0:64, :nk]), [ps], [Kh[i]])
            self.dve(lambda i=i: self.V.tensor_copy(Kh[i][64:96, :nk], kr_b[64:96, :nk]), [kr_b], [Kh[i]])
        for kb in range(nk // 128):
            ps = self.nps()
            self.pe(lambda kb=kb, ps=ps: self.T.matmul(ps[:, :], ckv_b[:, kb * 128:(kb + 1) * 128], self.wkv[:, :], start=True, stop=True),
                    [ckv_b, self.wkv], [ps])
            for h8 in range(8):
                pass
            self.dve(lambda kb=kb, ps=ps: self.V.tensor_copy(
                Vaug[:, kb * 520:(kb + 1) * 520].rearrange("p (h e) -> p h e", e=65)[:, :, 0:64],
                ps[:, :].rearrange("p (h e) -> p h e", e=64)), [ps], [Vaug])

    def attn_accum(self, Qh4, q0, nq, Kh4, Vaug, heads, k0, nkb, first, last, Oacc):
        fw = self.fw
        nqs = nq // 128
        its = [(kb, i, h) for kb in range(nkb) for i, h in enumerate(heads)]

        def score(kb, i, h):
            pss = self.psb[4 + (self.sc_rr % 2)]
            self.sc_rr += 1
            self.pe(lambda: self.T.matmul(pss[:, :nq], Kh4[i][:, k0 + kb * 128:k0 + kb * 128 + 128],
                                          Qh4[i][:, q0:q0 + nq], start=True, stop=True), [Kh4[i], Qh4[i]], [pss])
            PT = fw.rot([128, 512], BF16, "PT", n=3)
            self.act(lambda: self.A.activation(PT[:, :nq], pss[:, :nq], AF.Exp, scale=96.0 ** -0.5), [pss], [PT])
            return PT

        def pv(kb, i, h, PT):
            for qs in range(nqs):
                self.pe(lambda qs=qs: self.T.matmul(
                    Oacc[qs][:, i * 65:(i + 1) * 65], PT[:, qs * 128:(qs + 1) * 128],
                    Vaug[:, (k0 // 128 + kb) * 520 + h * 65:(k0 // 128 + kb) * 520 + h * 65 + 65],
                    start=(first and kb == 0 and i == 0), stop=(last and kb == nkb - 1 and i == 3), skip_group_check=True),
                    [PT, Vaug], [Oacc[qs]], signal=(qs == nqs - 1))

        pend = score(*its[0])
        for n in range(len(its)):
            nxt = score(*its[n + 1]) if n + 1 < len(its) else None
            pv(*its[n], pend)
            pend = nxt

    def attn_finish(self, Oacc, nqs, ob, hh):
        fw = self.fw
        for qs in range(nqs):
            rec = fw.rot([128, 4], F32, "rec", n=4)
            self.dve(lambda qs=qs, rec=rec: self.V.reciprocal(rec[:, :], Oacc[qs][:, 64:260:65]), [Oacc[qs]], [rec])
            for i in range(4):
                self.dve(lambda qs=qs, i=i, rec=rec: self.V.tensor_scalar(ob[qs][:, hh * 256 + i * 64:hh * 256 + i * 64 + 64],
                                                                          Oacc[qs][:, i * 65:i * 65 + 64], rec[:, i:i + 1], None, ALU.mult),
                         [Oacc[qs], rec], [ob[qs]])

    def attn_out(self, ob, nqs, GB2, g0, o0):
        for qs in range(nqs):
            for c in range(4):
                pt = self.pst[self.pst_rr % 2]
                self.pst_rr += 1
                self.pe(lambda qs=qs, c=c, pt=pt: self.T.transpose(pt[:, :128], ob[qs][:, c * 128:(c + 1) * 128], self.ident[:, :]),
                        [ob[qs], self.ident], [pt])
                self.dve(lambda qs=qs, c=c, pt=pt: self.V.scalar_tensor_tensor(
                    self.oTs[o0 // 512][1][:, c, o0 % 512 + qs * 128:o0 % 512 + qs * 128 + 128], pt[:, :128], 0.5, GB2[:, c, g0 + qs * 128:g0 + qs * 128 + 128],
                    ALU.mult, ALU.mult), [pt, GB2], [self.oTs[o0 // 512][1]])

    def phaseB_prompt(self, l, half):
        fw = self.fw
        h0 = half * 512
        fw.push()
        self.load_mla_w(l)
        GB2 = fw.sb([128, 4, 512], BF16, "GB2")
        Qh = [fw.sb([96, 512], BF16, f"Qh{h}") for h in range(8)]
        ckv_f = fw.sb([128, 512], F32, "ckvf")
        ckv_b = fw.sb([128, 512], BF16, "ckvb")
        kr_f = fw.sb([96, 512], F32, "krf")
        kr_b = fw.sb([96, 512], BF16, "krb")
        fw.push()
        self.stream_begin(2, depth=2)
        self.mla_front(l, h0, False, GB2, Qh, ckv_f, ckv_b, kr_f, kr_b)
        fw.pop()
        fw.dma("sp", self.d["ckvout"], self.d["ckvout"][l, :, h0:h0 + 512], ckv_f, ckv_f[:, :], sem_owner=self.outsem)
        fw.dma("sp", self.d["krout"], self.d["krout"][l, :, h0:h0 + 512], kr_f, kr_f[64:96, :], sem_owner=self.outsem)
        Kh = [fw.sb([96, 512], BF16, f"Kh{h}") for h in range(8)]
        Vaug = fw.sb([128, 4 * 520], BF16, "Vaug")
        self.pool(lambda: self.G.memset(Vaug[:, :], 1.0), [], [Vaug])
        self.mla_kv_chunk(ckv_b, kr_b, list(range(8)), Kh, Vaug, 512)
        self.sc_rr = 0
        import os
        if "dumpB" in os.environ.get("KDBG", "") and l == 0 and half == 0:
            so = self.d["stout"]
            fw.dma("pool", so, so[0:768, :].rearrange("(p a) b -> p (a b)", p=96), Qh[0], Qh[0][:, :])
            fw.dma("pool", so, so[768:1536, :].rearrange("(p a) b -> p (a b)", p=96), Kh[0], Kh[0][:, :])
            fw.dma("pool", so, so[1536:5632, :].rearrange("(p a) b -> p (a b)", p=128), Vaug, Vaug[:, 0:2048])
        for sq in range(2):
            ob = [fw.rot([128, 512], BF16, "ob", n=4) for _ in range(2)]
            for hh in range(2):
                heads = list(range(hh * 4, hh * 4 + 4))
                Oacc = [self.psb[0 + 2 * (hh % 2)], self.psb[1 + 2 * (hh % 2)]]
                self.attn_accum([Qh[h] for h in heads], sq * 256, 256, [Kh[h] for h in heads], Vaug, heads, sq * 256, 2, True, True, Oacc)
                self.attn_finish(Oacc, 2, ob, hh)
            if "dumpB" in os.environ.get("KDBG", "") and l == 0 and half == 0 and sq == 0:
                so = self.d["stout"]
                fw.dma("pool", so, so[5696:6720, :].rearrange("(p a) b -> p (a b)", p=128), ob[0], ob[0][:, :])
            self.attn_out(ob, 2, GB2, sq * 256, h0 + sq * 256)
        fw.pop()

    def phaseB_contrib(self, l):
        fw = self.fw
        self.load_mla_w(l)
        self.GB2 = fw.sb([128, 4, 512], BF16, "GB2s")
        self.Qh = [fw.sb([96, 512], BF16, f"Qhs{h}") for h in range(8)]
        fw.push()
        self.rope = fw.sb([96, 1024], F32, "rope")
        self.load(self.rope, self.rope[64:96, :], self.d["rope"], self.d["rope"].ap())
        self.stream_begin(2, depth=2)
        ckv_f = fw.sb([128, 512], F32, "ckvf")
        ckv_b = fw.sb([128, 512], BF16, "ckvb")
        kr_f = fw.sb([96, 512], F32, "krf")
        kr_b = fw.sb([96, 512], BF16, "krb")
        self.mla_front(l, 0, True, self.GB2, self.Qh, ckv_f, ckv_b, kr_f, kr_b)
        ag = self.ag1_in[l]["vx"]
        fw.dma("sp", ag, ag[VX_OFF["ckv"]:VX_OFF["ckv"] + 128, :], ckv_b, ckv_b[:, :])
        fw.dma("sp", ag, ag[VX_OFF["kr"]:VX_OFF["kr"] + 32, :], kr_b, kr_b[64:96, :])
        fw.pop()

    def phaseB_consume(self, l):
        fw = self.fw
        ago = self.ag1_out[l]["vx"]
        fw.push()
        self.sc_rr = 0
        self.ps_range = (4, 6)
        ob = [fw.sb([128, 512], BF16, f"obs{i}") for i in range(4)]
        for hh in range(2):
            heads = list(range(hh * 4, hh * 4 + 4))
            Oacc = self.psb[0:4]
            for ch in range(5):
                ckv_b = fw.rot([128, 512], BF16, "ckvg", n=2)
                kr_b = fw.rot([96, 512], BF16, "krg", n=2)
                if ch < 4:
                    fw.dma("sp", ckv_b, ckv_b[:, :], ago, ago[ch * 864 + VX_OFF["ckv"]:ch * 864 + VX_OFF["ckv"] + 128, :])
                    fw.dma("sp", kr_b, kr_b[64:96, :], ago, ago[ch * 864 + VX_OFF["kr"]:ch * 864 + VX_OFF["kr"] + 32, :])
                else:
                    fw.dma("pool", ckv_b, ckv_b[:, :], self.d["cckv"], self.d["cckv"][l, :, :])
                    fw.dma("pool", kr_b, kr_b[64:96, :], self.d["ckr"], self.d["ckr"][l, :, :])
                Kh = [fw.rot([96, 512], BF16, f"Khs{i}", n=2) for i in range(4)]
                Vaug = fw.rot([128, 4 * 520], BF16, "Vaugs", n=2)
                self.pool(lambda Vaug=Vaug: self.G.memset(Vaug[:, :], 1.0), [], [Vaug])
                self.mla_kv_chunk(ckv_b, kr_b, heads, Kh, Vaug, 512)
                self.attn_accum([self.Qh[h] for h in heads], 0, 512, Kh, Vaug, heads, 0, 4, ch == 0, ch == 4, Oacc)
            self.attn_finish(Oacc, 4, ob, hh)
        self.ps_range = (0, 4)
        self.attn_out(ob, 4, self.GB2, 0, 0)
        fw.pop()

    def fnet_stage1(self, fT_buf, fT_ap_fn, dftd, G1):
        for hb in range(2):
            ps = self.psb[4 + hb]
            for gg in range(2):
                g = hb * 2 + gg
                self.pe(lambda g=g, gg=gg, ps=ps: self.T.matmul(ps[:, gg * 256:(gg + 1) * 256], fT_ap_fn(g), dftd[:, :],
                                                                 start=True, stop=True), [fT_buf, dftd], [ps], signal=(gg == 1))
            if hb == 0:
                self.act(lambda ps=ps: self.A.copy(G1[:, 0:512], ps[:, :]), [ps], [G1])
            else:
                self.dve(lambda ps=ps: self.V.tensor_copy(G1[:, 512:1024], ps[:, :]), [ps], [G1])

    def phaseD_prompt(self, l, half):
        fw = self.fw
        base = l * BLK_PER_LAYER
        h0 = half * 512
        fw.push()
        self.dftd_p = fw.sb([128, 256], BF16, "dftd_p")
        self.dftT_p = fw.sb([128, 1024], BF16, "dftT_p")
        for nm in ("dftd_p", "dftT_p"):
            t = getattr(self, nm)
            self.load(t, t[:], self.d[nm], self.d[nm].ap())
        fT = fw.sb([128, 4, 512], BF16, "fT")
        GD2 = fw.sb([128, 4, 512], BF16, "GD2")
        fw.push()
        self.stream_begin(2, depth=2)
        slot = self.next_block(base + WIN_IDX["D0"])
        for c in range(4):
            ps = self.zmm(slot, 512, c * 128, 128, h0, 512)
            if c % 2:
                self.act(lambda c=c, ps=ps: self.A.copy(fT[:, c, :], ps[:, :]), [ps], [fT])
            else:
                self.dve(lambda c=c, ps=ps: self.V.tensor_copy(fT[:, c, :], ps[:, :]), [ps], [fT])
        self.prefetch_next()
        slot = self.next_block(base + WIN_IDX["D1"])
        for c in range(4):
            ps = self.zmm(slot, 512, c * 128, 128, h0, 512)
            self.silu2(ps, 128, 512, GD2[:, c, :], GD2)
        self.prefetch_next()
        fw.pop()
        for sq in range(2):
            t0 = sq * 256
            G1 = [fw.rot([128, 1024], BF16, "G1", n=4) for _ in range(2)]
            for tt in range(2):
                self.fnet_stage1(fT, lambda g, tt=tt: fT[:, g, t0 + tt * 128:t0 + tt * 128 + 128], self.dftd_p, G1[tt])
            for g in range(4):
                ps = self.nps()
                i = 0
                for tt in range(2):
                    for cs in range(2):
                        self.pe(lambda g=g, tt=tt, cs=cs, ps=ps, i=i: self.T.matmul(
                            ps[:, :256], G1[tt][:, g * 256 + cs * 128:g * 256 + cs * 128 + 128],
                            self.dftT_p[:, (tt * 2 + cs) * 256:(tt * 2 + cs) * 256 + 256], start=(i == 0), stop=(i == 3)),
                            [G1[tt], self.dftT_p], [ps], signal=(i == 3))
                        i += 1
                self.dve(lambda g=g, ps=ps: self.V.scalar_tensor_tensor(
                    self.oTs[half][3][:, g, t0:t0 + 256], ps[:, :256], 0.5, GD2[:, g, t0:t0 + 256], ALU.mult, ALU.mult),
                    [ps, GD2], [self.oTs[half][3]])
        fw.pop()

    def contrib_rows(self, l, slot, W, c0, w, part, row0):
        fw = self.fw
        ps = self.zmm(slot, W, c0, w, 0, 512)
        stg = fw.rot([128, 512], BF16, "agstg", n=3)
        if self.cflip % 2:
            self.act(lambda: self.A.copy(stg[:w, :], ps[:w, :]), [ps], [stg])
        else:
            self.dve(lambda: self.V.tensor_copy(stg[:w, :], ps[:w, :]), [ps], [stg])
        self.cflip += 1
        ag = self.ag1_in[l][part]
        fw.dma("sp", ag, ag[row0:row0 + w, :], stg, stg[:w, :])

    def phaseD_contrib(self, l):
        fw = self.fw
        base = l * BLK_PER_LAYER
        self.GD2 = fw.sb([128, 4, 512], BF16, "GD2s")
        fw.push()
        self.stream_begin(2, depth=2)
        slot = self.next_block(base + WIN_IDX["D0"])
        for c in range(4):
            self.contrib_rows(l, slot, 512, c * 128, 128, "f", c * 128)
        self.prefetch_next()
        slot = self.next_block(base + WIN_IDX["D1"])
        for c in range(4):
            ps = self.zmm(slot, 512, c * 128, 128, 0, 512)
            self.silu2(ps, 128, 512, self.GD2[:, c, :], self.GD2)
        self.prefetch_next()
        fw.pop()

    def phaseD_consume(self, l):
        fw = self.fw
        ago = self.ag1_out[l]["f"]
        fw.push()
        self.dftd_s = fw.sb([128, 256], BF16, "dftd_s")
        self.load(self.dftd_s, self.dftd_s[:], self.d["dftd_s"], self.d["dftd_s"].ap())
        acc = self.psb[0:4]
        nt = 0
        for q in range(4):
            fq = fw.rot([128, 4, 512], BF16, "fq", n=2)
            src = ago[q * 512:q * 512 + 512, :].rearrange("(g d) t -> d g t", d=128)
            fw.dma("sp", fq, fq[:], ago, src)
            for s4 in range(4):
                tt = q * 4 + s4
                ct = fw.rot([128, 1024], BF16, "ct", n=3)
                fw.dma("pool", ct, ct[:], self.d["dftT_s"], self.d["dftT_s"][tt, :, :])
                G1 = fw.rot([128, 1024], BF16, "G1s", n=3)
                self.fnet_stage1(fq, lambda g, s4=s4, fq=fq: fq[:, g, s4 * 128:(s4 + 1) * 128], self.dftd_s, G1)
                for g in range(4):
                    for cs in range(2):
                        self.pe(lambda g=g, cs=cs, G1=G1, ct=ct, tt=tt: self.T.matmul(
                            acc[g][:, :], G1[:, g * 256 + cs * 128:g * 256 + cs * 128 + 128], ct[:, cs * 512:(cs + 1) * 512],
                            start=(tt == 0 and cs == 0), stop=(tt == 15 and cs == 1)),
                            [G1, ct], [acc[g]], signal=(g == 3 and cs == 1))
        for g in range(4):
            self.dve(lambda g=g: self.V.scalar_tensor_tensor(self.oTs[0][3][:, g, 0:512], acc[g][:, :], 0.5, self.GD2[:, g, :],
                                                             ALU.mult, ALU.mult), [acc[g], self.GD2], [self.oTs[0][3]])
        fw.pop()

    def allgather(self, dst, src):
        fw = self.fw
        fw.dma("pool", dst, None, src, None, sem_owner=dst, inc=1,
               fn=lambda: self.G.collective_compute("AllGather", ALU.bypass, replica_groups=[[0, 1, 2, 3], [4, 5, 6, 7]],
                                                     ins=[src.h.ap()], outs=[dst.h.ap()]))

    def sample_pass(self, l):
        import os
        fw = self.fw
        dbg = os.environ.get("KDBG", "")
        self.cflip = 0
        br = self.branches
        fw.push()
        if "A" in br:
            self.phaseA_contrib(l)
        fw.push()
        if "B" in br:
            self.phaseB_contrib(l)
        fw.push()
        if "D" in br:
            self.phaseD_contrib(l)
        if "noag" not in dbg:
            if "A" in br:
                self.allgather(self.ag1_out[l]["rk"], self.ag1_in[l]["rk"])
            if "A" in br or "B" in br:
                self.allgather(self.ag1_out[l]["vx"], self.ag1_in[l]["vx"])
            if "D" in br:
                self.allgather(self.ag1_out[l]["f"], self.ag1_in[l]["f"])
        if "C" in br:
            self.phaseC(l, 0, 512, 0)
        if "D" in br and "nocons" not in dbg:
            self.phaseD_consume(l)
        fw.pop()
        if "B" in br:
            self.phaseB_consume(l)
        fw.pop()
        if "A" in br and "noAcons" not in dbg:
            self.phaseA_consume(l)
            self.phaseA_final(l)
        fw.pop()

    def zero_branch(self, n):
        for hf in range(2):
            if self.oTs[hf] is not None:
                t = self.oTs[hf][n]
                self.pool(lambda t=t: self.G.memset(t[:], 0.0), [], [t])

    def win_plan(self, l, grp="p"):
        base = l * BLK_PER_LAYER
        ids = []
        order = (("A", ["A0", "A1", "A2", "A3", "A4"]), ("B", ["B0", "B1"]), ("C", ["C0", "C1", "C2"]), ("D", ["D0", "D1"]))
        if grp == "s":
            order = (order[0], order[1], order[3], order[2])
        for br, names in order:
            if br in self.branches:
                ids += [base + WIN_IDX[n] for n in names]
        return ids

    def tail_plan(self, l):
        base = l * BLK_PER_LAYER
        return [base + 18 + i for i in range(16)] + [base + 34, base + 35]

    def build(self):
        fw = self.fw
        self.outsem = Buf(None, "outsem", "none")
        for l in range(self.depth):
            self.stream_plan([l * BLK_PER_LAYER + b for b in range(6)])
            self.stream_plan(self.win_plan(l) * 2 + self.tail_plan(l))
            self.stream_plan(self.win_plan(l, 's') + self.tail_plan(l))
        for l in range(self.depth):
            fw.push()
            self.load_layer_small(l)
            self.ada(l)
            fw.push()
            self.hTs[1] = fw.sb([128, 8, 512], BF16, "hTb")
            self.oTs[1] = [fw.sb([128, 4, 512], BF16, f"oTb{n}") for n in range(4)]
            for n, br in enumerate("ABCD"):
                if br not in self.branches:
                    self.zero_branch(n)
            for half in range(2):
                self.make_h(0, half * 512, half * 512, 512)
            for half in range(2):
                self.phases(l, "p", half)
            self.merge_out(l, 0, 0, NPT)
            fw.pop()
            self.hTs[1] = None
            self.oTs[1] = None
            self.make_h(1, NPT, 0, 512)
            self.sample_pass(l)
            self.merge_out(l, 1, NPT, NST)
            fw.pop()
        fw.push()
        self.sp = fw.sb([128, NSP], F32, "spf")
        self.load(self.sp, self.sp[:], self.d["sp"], self.d["sp"][0, :, :])
        self.final_out()
        fw.pop()
        for e in ("sp",):
            for ev in list(fw.dma_out.values()):
                fw._wait(e, ev)
        fw.barrier()
        return self.nc

    def phases(self, l, grp, half):
        h0 = half * 512
        if "A" in self.branches:
            self.phaseA_prompt(l, half)
        if "B" in self.branches:
            self.phaseB_prompt(l, half)
        if "C" in self.branches:
            self.phaseC(l, h0, 512, h0)
        if "D" in self.branches:
            self.phaseD_prompt(l, half)


_CFG = {"branches": "ABCD", "depth": DEPTH}


def make_in_maps(inp):
    f32 = lambda a: np.ascontiguousarray(np.asarray(a, dtype=np.float32))
    inp = {k: np.asarray(v) for k, v in inp.items()}
    wst = build_stream(f32(inp["w_ada"]), f32(inp["w_in"]), f32(inp["w_branch"]), f32(inp["w_merge"]), f32(inp["w_out"]))
    sp = build_small(inp)
    cst = build_consts()
    L = DEPTH
    wup = f32(inp["rwkv_w_up"]).transpose(0, 2, 1, 3).reshape(L, 64, 1024)
    aup = f32(inp["rwkv_a_up"]).transpose(0, 2, 1, 3).reshape(L, 64, 1024)
    wqu = f32(inp["mla_w_q_up"]).reshape(L, 2, 128, 8, 96)
    wq = wqu.transpose(0, 2, 1, 3, 4).reshape(L, 128, 2 * 8 * 96)
    wqs = wqu[..., 64 + _SWAP32].transpose(0, 2, 1, 3, 4).reshape(L, 128, 2 * 8 * 32)
    wkvu = f32(inp["mla_w_kv_up"]).reshape(L, 128, 8, 128)
    wkk = np.ascontiguousarray(wkvu[..., :64]).reshape(L, 128, 512)
    wkv = np.ascontiguousarray(wkvu[..., 64:]).reshape(L, 128, 512)
    wsT = f32(inp["gmlp_w_s"]).transpose(0, 3, 1, 2).reshape(L, 128, 512)
    bsb = np.ascontiguousarray(np.broadcast_to(f32(inp["gmlp_b_s"]).reshape(L, 1, 512), (L, 128, 512)))
    maps = []
    xp = f32(inp["x_prompt"])
    xs = f32(inp["x_sample"])
    for c in range(NCORE):
        s, j = c // 4, c % 4
        dftT, rope = build_core_consts(j)
        cond = np.stack([f32(inp["c_ctx"]).reshape(8, 128).T, f32(inp["c"])[s].reshape(8, 128).T], axis=2).reshape(128, 16)
        o, _ = SP_OFF["rw"]
        om, _ = SP_OFF["mu_rkv"]
        spo = np.concatenate([sp[:, :, o:o + 36].reshape(L, 128, 9, 4)[:, :, :, j],
                              sp[:, :, om:om + 12].reshape(L, 128, 3, 4)[:, :, :, j]], axis=2)
        spo = np.ascontiguousarray(spo)
        st0 = np.stack([f32(inp["state_rwkv_fwd"])[s, :, 2 * j:2 * j + 2], f32(inp["state_rwkv_bwd"])[s, :, 2 * j:2 * j + 2]],
                       axis=1)
        st0 = st0.transpose(0, 1, 2, 4, 3).reshape(L, 2, 128, 64)
        p = np.arange(128)
        idx1 = np.zeros((128, 12), np.int32)
        for q in range(4):
            idx1[:, 0 * 4 + q] = q * 1024 + 128 * j + p
            idx1[:, 1 * 4 + q] = q * 1024 + 512 + 128 * j + p
            idx1[:, 2 * 4 + q] = q * 864 + 128 * j + p
        idx2 = np.zeros((128, 4), np.int32)
        for r in range(4):
            idx2[:, r] = (r * 4 + j) * 128 + p
        m = dict(
            xp=np.ascontiguousarray(xp[4 * c:4 * c + 4].reshape(NPT, D).T),
            xs=np.ascontiguousarray(xs[s, 512 * j:512 * j + 512].T),
            wst=wst, sp=sp, cond=np.ascontiguousarray(cond),
            ident=cst["ident"], bones=cst["bones"], mask=cst["mask"],
            dftd_p=cst["dftd_p"], dftd_s=cst["dftd_s"], dftT_p=cst["dftT_p"],
            dftT_s=np.ascontiguousarray(dftT.reshape(16, 128, 1024)), rope=np.ascontiguousarray(rope.reshape(32, 1024)),
            wup=wup, aup=aup,
            wupo=np.ascontiguousarray(wup.reshape(L, 64, 2, 4, 128)[:, :, :, j]).reshape(L, 64, 256),
            aupo=np.ascontiguousarray(aup.reshape(L, 64, 2, 4, 128)[:, :, :, j]).reshape(L, 64, 256),
            spo=spo, wq=wq, wqs=wqs, wkk=wkk, wkv=wkv, wsT=wsT, bsb=bsb,
            st0=np.ascontiguousarray(st0),
            cckv=np.ascontiguousarray(f32(inp["cache_mla_ckv"])[s].transpose(0, 2, 1)),
            ckr=np.ascontiguousarray(f32(inp["cache_mla_krope"])[s].transpose(0, 2, 1)),
            idx1=idx1, idx2=idx2,
        )
        maps.append(m)
    return maps


def assemble(results):
    B = 32
    yp = np.zeros((B, SEQ, D), np.float32)
    ys = np.zeros((2, DSEQ, D), np.float32)
    sf = np.zeros((B, DEPTH, 8, 64, 64), np.float32)
    sbw = np.zeros((B, DEPTH, 8, 64, 64), np.float32)
    ckv = np.zeros((B, DEPTH, SEQ, 128), np.float32)
    kr = np.zeros((B, DEPTH, SEQ, 32), np.float32)
    for c in range(NCORE):
        r = results[c]
        s, j = c // 4, c % 4
        yp[4 * c:4 * c + 4] = np.asarray(r["yp"]).T.reshape(4, SEQ, D)
        ys[s, 512 * j:512 * j + 512] = np.asarray(r["ys"]).T
        st = np.asarray(r["stout"]).reshape(DEPTH, 2, 4, 4, 2, 64, 64)
        st = st.transpose(1, 2, 0, 3, 4, 6, 5).reshape(2, 4, DEPTH, 8, 64, 64)
        sf[4 * c:4 * c + 4] = st[0]
        sbw[4 * c:4 * c + 4] = st[1]
        ck = np.asarray(r["ckvout"]).reshape(DEPTH, 128, 4, SEQ)
        ckv[4 * c:4 * c + 4] = ck.transpose(2, 0, 3, 1)
        k2 = np.asarray(r["krout"]).reshape(DEPTH, 32, 4, SEQ)
        kr[4 * c:4 * c + 4] = k2.transpose(2, 0, 3, 1)
    return yp, ys, sf, sbw, ckv, kr


def kernel(**inputs):
    prog = Prog(dict(_CFG))
    nc = prog.build()
    maps = make_in_maps(inputs)
    res = run_bass_kernel_spmd(nc, maps, core_ids=list(range(NCORE)))
    return assemble(res.results)
```

```python
import numpy as np
import ml_dtypes
import concourse.bass as bass
import concourse.mybir as mybir
from concourse.bass_utils import run_bass_kernel_spmd

F32 = mybir.dt.float32
BF16 = mybir.dt.bfloat16
I32 = mybir.dt.int32
ALU = mybir.AluOpType
AF = mybir.ActivationFunctionType
AX = mybir.AxisListType

D = 1024
DEPTH = 2
SEQ = 256
DSEQ = 2048
PAST = 512
NCORE = 8
NPT = 1024
NST = 512
NTOK = NPT + NST
EPS = 1e-6
GN_EPS = 64e-5
EP = 24000
AGP = {"rk": 1024, "vx": 864, "f": 512}
VX_OFF = dict(v=0, lora=512, ckv=704, kr=832)


class Buf:
    def __init__(self, h, name, space):
        self.h = h
        self.name = name
        self.space = space
        self.w = {}
        self.r = {}
        self.dsem = None
        self.dcnt = 0
        self.dsid = None
        self.dcls = None

    def __getitem__(self, idx):
        return self.h[idx]

    def ap(self):
        return self.h.ap() if self.space == "dram" else self.h[:]


class FW:
    def __init__(self, nc):
        self.nc = nc
        self.E = {"pe": nc.tensor, "act": nc.scalar, "dve": nc.vector, "pool": nc.gpsimd, "sp": nc.sync}
        self.cnt = {e: 0 for e in self.E}
        self.esem = {e: [] for e in self.E}
        self.waited = {e: {} for e in self.E}
        self.pend = {e: [] for e in self.E}
        self.nbuf = 0
        self.ninst = 0
        self.dma_out = {}
        self.sem_pool = []
        self.stack = []
        self.rots = {}
        self.free_sems = {"pool": [], "hw": []}

    def sb(self, shape, dt, name=None):
        self.nbuf += 1
        name = name or "t"
        g = self.nc.sbuf_tensor(f"{name}_{self.nbuf}", list(shape), dt)
        h = g.__enter__()
        b = Buf(h, name, "sbuf")
        if self.stack:
            self.stack[-1].append((g, b))
        return b

    def ps(self, shape, dt=F32, name=None):
        self.nbuf += 1
        name = name or "p"
        h = self.nc.alloc_psum_tensor(f"{name}_{self.nbuf}", list(shape), dt)
        return Buf(h, name, "psum")

    def dram(self, name, shape, dt, kind=None):
        if kind is None:
            h = self.nc.dram_tensor(name, list(shape), dt)
        else:
            h = self.nc.dram_tensor(name, list(shape), dt, kind=kind)
        return Buf(h, name, "dram")

    def rot(self, shape, dt, name, n=2):
        key = (len(self.stack), name)
        if key not in self.rots:
            self.rots[key] = [[self.sb(shape, dt, name) for _ in range(n)], 0]
        ent = self.rots[key]
        t = ent[0][ent[1] % n]
        ent[1] += 1
        return t

    def push(self):
        self.stack.append([])

    def pop(self):
        self.barrier()
        depth = len(self.stack)
        for k in [k for k in self.rots if k[0] == depth]:
            del self.rots[k]
        for g, b in reversed(self.stack.pop()):
            if b.dsem is not None:
                self.free_sems[b.dcls].append((b.dsem, b.dcnt))
                b.dsem = None
            g.__exit__(None, None, None)

    def _sem_for(self, eng, k):
        i = (k - 1) // EP
        while len(self.esem[eng]) <= i:
            self.esem[eng].append(self.nc.alloc_semaphore(f"s_{eng}_{len(self.esem[eng])}"))
        return self.esem[eng][i], (k - 1) % EP + 1

    def _wait(self, eng, ev):
        if ev is None:
            return
        if ev[0] == "eng":
            _, e2, k = ev
            if e2 == eng and eng == "pe":
                return
            key = ("eng", e2, (k - 1) // EP)
            sem, val = self._sem_for(e2, k)
        else:
            _, sem, val, sid = ev
            key = ("sem", sid)
        if self.waited[eng].get(key, 0) >= val:
            return
        self.waited[eng][key] = val
        self.E[eng].wait_ge(sem, val)

    def _check_pend(self, eng, b):
        for e2, lst in self.pend.items():
            if e2 == eng:
                continue
            for (pb, _) in lst:
                if pb is b:
                    raise RuntimeError(f"buffer {b.name} has pending unsignalled access on {e2}, touched by {eng}")

    def _deps(self, eng, reads, writes):
        evs = []
        for b in reads:
            self._check_pend(eng, b)
            evs.extend(b.w.values())
        for b in writes:
            self._check_pend(eng, b)
            for wv in b.w.values():
                if not (wv[0] == "eng" and wv[1] == eng):
                    evs.append(wv)
            for ev in b.r.values():
                if ev[0] == "eng" and ev[1] == eng and eng == "pe":
                    continue
                evs.append(ev)
        for ev in evs:
            self._wait(eng, ev)

    def op(self, eng, fn, reads=(), writes=(), signal=True):
        self._deps(eng, reads, writes)
        ins = fn()
        self.ninst += 1
        if signal:
            self.cnt[eng] += 1
            k = self.cnt[eng]
            sem, val = self._sem_for(eng, k)
            ins.then_inc(sem, 1)
            ev = ("eng", eng, k)
            for (pb, kind) in self.pend[eng]:
                if kind == "r":
                    pb.r[eng] = ev
                else:
                    pb.w = {eng: ev}
                    pb.r = {}
            self.pend[eng] = []
            for b in reads:
                b.r[eng] = ev
            for b in writes:
                b.w = {eng: ev}
                b.r = {}
        else:
            for b in reads:
                self.pend[eng].append((b, "r"))
            for b in writes:
                self.pend[eng].append((b, "w"))
        return ins

    def _dma_sem(self, b, q="sp"):
        cls = "pool" if q == "pool" else "hw"
        if b.dsem is None:
            self.nbuf += 1
            if self.free_sems[cls]:
                b.dsem, b.dcnt = self.free_sems[cls].pop()
            else:
                b.dsem = self.nc.alloc_semaphore(f"d_{b.name}_{self.nbuf}")
            b.dsid = self.nbuf
            b.dcls = cls
        elif cls == "pool" and b.dcls != "pool":
            raise RuntimeError(f"buffer {b.name}: software DMA on a semaphore first used by a hardware-DGE DMA")
        return b.dsem

    def dma(self, q, out_b, out_ap, in_b, in_ap, sem_owner=None, inc=16, fn=None, extra_reads=(), **kw):
        eng = q
        evs = []
        for b in (in_b, out_b) + tuple(extra_reads):
            self._check_pend(eng, b)
        evs.extend(in_b.w.values())
        for b in extra_reads:
            evs.extend(b.w.values())
        owner = sem_owner or (out_b if out_b.space != "dram" else in_b)
        sem = self._dma_sem(owner, q)
        for wv in out_b.w.values():
            if wv[0] != "sem":
                evs.append(wv)
        for ev in out_b.r.values():
            evs.append(ev)
        for ev in evs:
            self._wait(eng, ev)
        if fn is None:
            ins = self.E[eng].dma_start(out=out_ap, in_=in_ap, **kw)
        else:
            ins = fn()
        self.ninst += 1
        owner.dcnt += inc
        ins.then_inc(sem, inc)
        ev = ("sem", sem, owner.dcnt, owner.dsid)
        in_b.r[("dma", owner.dsid)] = ev
        for b in extra_reads:
            b.r[("dma", owner.dsid)] = ev
        out_b.w = {k: v for k, v in out_b.w.items() if v[0] == "sem"}
        out_b.w[("dma", owner.dsid)] = ev
        out_b.r = {}
        self.dma_out[owner.dsid] = ev
        return ev

    def wait_buf(self, eng, b):
        self._check_pend(eng, b)
        for ev in b.w.values():
            self._wait(eng, ev)
        for ev in b.r.values():
            self._wait(eng, ev)

    def barrier(self):
        for e in self.E:
            if self.pend[e]:
                raise RuntimeError(f"barrier with pending unsignalled ops on {e}")
        last = []
        for e in ("pe", "act", "dve", "pool"):
            if self.cnt[e] > 0:
                last.append(("eng", e, self.cnt[e]))
        for e in self.E:
            for ev in last:
                if ev[1] == e and e == "pe":
                    continue
                self._wait(e, ev)
            for ev in self.dma_out.values():
                self._wait(e, ev)
        self.dma_out = {}


COLS = dict(r=(0, 512), k=(512, 1024), v=(1024, 1536), wdf=(1536, 1600), wdb=(1600, 1664), ad=(1664, 1728),
            ga=(1728, 2240), qd=(2240, 2496), kvd=(2496, 2624), kr=(2624, 2656), gb=(2656, 3168),
            u=(3168, 3680), vc=(3680, 4192), gc=(4192, 4704), f=(4704, 5216), gd=(5216, 5728))


def _rng(name):
    a, b = COLS[name]
    return np.arange(a, b)


_SWAP32 = np.arange(32).reshape(16, 2)[:, ::-1].reshape(32)

WIN_BLOCKS = [
    ("A0", [_rng("r")]), ("A1", [_rng("k")]), ("A2", [_rng("v")]),
    ("A3", [_rng("wdf"), _rng("wdb"), _rng("ad")]), ("A4", [_rng("ga")]),
    ("B0", [_rng("qd"), _rng("kvd"), _rng("kr"), _rng("kr")[_SWAP32]]), ("B1", [_rng("gb")]),
    ("C0", [_rng("u")]), ("C1", [_rng("vc")]), ("C2", [_rng("gc")]),
    ("D0", [_rng("f")]), ("D1", [_rng("gd")]),
]
WIN_W = {n: int(sum(len(c) for c in cols)) for n, cols in WIN_BLOCKS}
WIN_IDX = {n: 6 + i for i, (n, _) in enumerate(WIN_BLOCKS)}
BLK_PER_LAYER = 36
FBLK = 4096


def _kcp(w, W):
    return w.reshape(8, 128, W).transpose(1, 0, 2).reshape(128, 8 * W)


def build_stream(w_ada, w_in, w_branch, w_merge, w_out):
    st = np.zeros((DEPTH * BLK_PER_LAYER, 128, FBLK), np.float32)
    for l in range(DEPTH):
        base = l * BLK_PER_LAYER
        for b in range(6):
            st[base + b, :, :] = _kcp(w_ada[l][:, 512 * b:512 * b + 512], 512)
        for i, (n, cols) in enumerate(WIN_BLOCKS):
            cc = np.concatenate(cols)
            W = len(cc)
            st[base + 6 + i, :, :8 * W] = _kcp(w_in[l][:, cc], W)
        for d in range(8):
            cc = np.concatenate([n * 1024 + d * 128 + np.arange(128) for n in range(4)])
            st[base + 18 + 2 * d, :, :] = _kcp(w_merge[l][:, cc], 512)
            wb = w_branch[l][:, :, d * 128:(d + 1) * 128]
            wb = wb.reshape(4, 4, 128, 128).transpose(2, 0, 1, 3)
            st[base + 19 + 2 * d, :, :2048] = wb.reshape(128, 2048)
        for b in range(2):
            st[base + 34 + b, :, :] = _kcp(w_out[l][:, 512 * b:512 * b + 512], 512)
    return st


SP_OFF = {}
_o = 0
for _n, _w in [("norm_g", 8), ("b_ada", 24), ("mu_rkv", 12), ("mu_lora", 3), ("b_merge", 32), ("rw", 36),
               ("qn", 2), ("kvn", 1), ("gln_g", 4), ("gln_b", 4), ("fin_g", 8)]:
    SP_OFF[_n] = (_o, _w)
    _o += _w
NSP = _o
RW_NAMES = ["w0_f", "w0_b", "a0_f", "a0_b", "k_k", "k_a", "r_k", "ln_g", "ln_b"]


def _pc(v, n):
    return np.asarray(v, np.float32).reshape(n, 128).T


def build_small(inp):
    sp = np.zeros((DEPTH, 128, NSP), np.float32)
    for l in range(DEPTH):
        def put(name, arr):
            o, w = SP_OFF[name]
            sp[l, :, o:o + w] = arr
        put("norm_g", _pc(inp["norm_g"][l], 8))
        put("b_ada", _pc(inp["b_ada"][l], 24))
        put("mu_rkv", _pc(inp["shift_mu"][l][:1536], 12))
        ml = np.zeros((128, 3), np.float32)
        ml[:64, :] = inp["shift_mu"][l][1536:1728].reshape(3, 64).T
        put("mu_lora", ml)
        bm = inp["b_merge"][l].reshape(4, 8, 128)
        put("b_merge", bm.transpose(2, 1, 0).reshape(128, 32))
        rwv = [inp["rwkv_w0"][l][0], inp["rwkv_w0"][l][1], inp["rwkv_a0"][l][0], inp["rwkv_a0"][l][1],
               inp["rwkv_k_k"][l], inp["rwkv_k_a"][l], inp["rwkv_r_k"][l].reshape(512), inp["rwkv_ln_g"][l],
               inp["rwkv_ln_b"][l]]
        rw = np.stack([_pc(v, 4) for v in rwv], axis=1)
        put("rw", rw.reshape(128, 36))
        put("qn", _pc(inp["mla_q_norm"][l], 2))
        put("kvn", _pc(inp["mla_kv_norm"][l], 1))
        put("gln_g", _pc(inp["gmlp_ln_g"][l], 4))
        put("gln_b", _pc(inp["gmlp_ln_b"][l], 4))
        put("fin_g", _pc(inp["final_norm_g"], 8))
    return sp


def rw_col(name, pair):
    o, _ = SP_OFF["rw"]
    return o + RW_NAMES.index(name) * 4 + pair


def build_consts():
    c = {}
    c["ident"] = np.eye(128, dtype=np.float32)
    hb = np.arange(128) // 64
    c["bones"] = (hb[:, None] == hb[None, :]).astype(np.float32)
    p = np.arange(128)[:, None]
    f = np.arange(128)[None, :]
    mk = {}
    bd32 = ((p // 32) == (f // 32)).astype(np.float32)
    od64 = (((p // 64) == (f // 64)) & ((p // 32) != (f // 32))).astype(np.float32)
    od128 = ((p // 64) != (f // 64)).astype(np.float32)
    for dname, ms, mi, mt in (("f", p < f, p <= f, f < p), ("b", p > f, p >= f, f > p)):
        ms = ms.astype(np.float32)
        mi = mi.astype(np.float32)
        mtf = -(mt.astype(np.float32))
        mk[dname] = np.concatenate([ms, mi, -ms * bd32, mi, mtf * bd32, mtf * bd32, mtf * od64, mtf * od64,
                                    mtf * od128, mtf * od128], axis=1)
    c["mask"] = np.stack([mk["f"], mk["b"]], axis=1).reshape(128, 2 * 1280)
    dd = np.arange(128)
    ang = 2 * np.pi * np.outer(dd, dd) / 128.0
    for nm, T in (("dftd_p", SEQ), ("dftd_s", DSEQ)):
        sc = 1.0 / np.sqrt(T * 128.0)
        c[nm] = np.concatenate([np.cos(ang) * sc, -np.sin(ang) * sc], axis=1).astype(np.float32)
    tt = np.arange(SEQ)
    angp = 2 * np.pi * np.outer(tt, tt) / SEQ
    cp = np.stack([np.cos(angp), np.sin(angp)], axis=1)
    c["dftT_p"] = cp.reshape(2, 128, 2, SEQ).transpose(1, 0, 2, 3).reshape(128, 2 * 2 * SEQ).astype(np.float32)
    return c


def build_core_consts(j):
    t = np.arange(DSEQ)
    k1 = 512 * j + np.arange(512)
    ang = 2 * np.pi * ((np.outer(t, k1)) % DSEQ) / DSEQ
    cs = np.stack([np.cos(ang), np.sin(ang)], axis=1)
    dftT = cs.reshape(16, 128, 2, 512).astype(np.float32)
    pos = 512 * j + np.arange(512)
    row = (pos // 64).astype(np.float32)
    col = (pos % 64).astype(np.float32)
    inv = (10000.0 ** (-np.arange(8, dtype=np.float32) / 8)).astype(np.float32)
    ang = np.concatenate([row[:, None] * inv, col[:, None] * inv], axis=-1).astype(np.float32)
    cos = np.cos(ang).astype(np.float32)
    sin = np.sin(ang).astype(np.float32)
    COS = np.repeat(cos, 2, axis=1).T
    SIN = np.stack([-sin, sin], axis=2).reshape(512, 32).T
    rope = np.stack([COS, SIN], axis=1).astype(np.float32)
    return dftT, rope


class Prog:
    def __init__(self, cfg):
        self.cfg = cfg
        self.branches = cfg.get("branches", "ABCD")
        self.depth = cfg.get("depth", DEPTH)
        nc = bass.Bass("TRN2", target_bir_lowering=False)
        self.nc = nc
        fw = FW(nc)
        self.fw = fw
        self.V, self.A, self.G, self.T = nc.vector, nc.scalar, nc.gpsimd, nc.tensor
        di = lambda n, s, dt=F32: fw.dram(n, s, dt, kind="ExternalInput")
        do = lambda n, s, dt=F32: fw.dram(n, s, dt, kind="ExternalOutput")
        self.d = dict(
            xp=di("xp", [D, NPT]), xs=di("xs", [D, NST]),
            wst=di("wst", [DEPTH * BLK_PER_LAYER, 128, FBLK]),
            sp=di("sp", [DEPTH, 128, NSP]), cond=di("cond", [128, 16]),
            ident=di("ident", [128, 128]), bones=di("bones", [128, 128]), mask=di("mask", [128, 2560]),
            dftd_p=di("dftd_p", [128, 256]), dftd_s=di("dftd_s", [128, 256]), dftT_p=di("dftT_p", [128, 1024]),
            dftT_s=di("dftT_s", [16, 128, 1024]), rope=di("rope", [32, 1024]),
            wup=di("wup", [DEPTH, 64, 1024]), aup=di("aup", [DEPTH, 64, 1024]),
            wupo=di("wupo", [DEPTH, 64, 256]), aupo=di("aupo", [DEPTH, 64, 256]),
            spo=di("spo", [DEPTH, 128, 12]),
            wq=di("wq", [DEPTH, 128, 2 * 8 * 96]), wqs=di("wqs", [DEPTH, 128, 2 * 8 * 32]),
            wkk=di("wkk", [DEPTH, 128, 512]), wkv=di("wkv", [DEPTH, 128, 512]),
            wsT=di("wsT", [DEPTH, 128, 512]), bsb=di("bsb", [DEPTH, 128, 512]),
            st0=di("st0", [DEPTH, 2, 128, 64]), cckv=di("cckv", [DEPTH, 128, PAST]),
            ckr=di("ckr", [DEPTH, 32, PAST]),
            idx1=di("idx1", [128, 12], I32), idx2=di("idx2", [128, 4], I32),
            yp=do("yp", [D, NPT]), ys=do("ys", [D, NST]),
            stout=do("stout", [DEPTH * 2 * 4 * 4 * 128, 64]),
            ckvout=do("ckvout", [DEPTH, 128, NPT]), krout=do("krout", [DEPTH, 32, NPT]),
        )
        self.ag1_in = [{k: fw.dram(f"ag1i{k}{l}", [n, 512], BF16) for k, n in AGP.items()} for l in range(DEPTH)]
        self.ag1_out = [{k: fw.dram(f"ag1o{k}{l}", [4 * n, 512], BF16) for k, n in AGP.items()} for l in range(DEPTH)]
        self.ag2_in = [fw.dram(f"ag2i{l}", [512, 512], BF16) for l in range(DEPTH)]
        self.ag2_out = [fw.dram(f"ag2o{l}", [2048, 512], BF16) for l in range(DEPTH)]
        self.psb = [fw.ps([128, 512], F32, f"bank{i}") for i in range(6)]
        self.pst = [fw.ps([128, 1024], BF16, f"pst{i}") for i in range(2)]
        self.pst_rr = 0
        self.ps_rr = 0
        self.xT = fw.sb([128, 8, NTOK], F32, "xT")
        self.hTs = [fw.sb([128, 8, 512], BF16, "hTa"), None]
        self.oTs = [[fw.sb([128, 4, 512], BF16, f"oTa{n}") for n in range(4)], None]
        self.slots = None
        self.slot_rr = 0
        self.plan = []
        self.plan_pos = 0
        self.stg = None
        self.blocks_left = 0
        self.dma_issued = {}
        self.load_consts()

    ps_range = (0, 4)

    def nps(self, lo=None, hi=None):
        lo = self.ps_range[0] if lo is None else lo
        hi = self.ps_range[1] if hi is None else hi
        n = hi - lo
        b = self.psb[lo + (self.ps_rr % n)]
        self.ps_rr += 1
        return b

    def dve(self, fn, r, w):
        return self.fw.op("dve", fn, r, w)

    def rsqrt(self, out_buf, out_ap, in_buf, in_ap):
        self.act(lambda: self.A.activation(in_ap, in_ap, AF.Sqrt), [in_buf], [in_buf])
        self.dve(lambda: self.V.reciprocal(out_ap, in_ap), [in_buf], [out_buf])

    def act(self, fn, r, w):
        return self.fw.op("act", fn, r, w)

    def pool(self, fn, r, w):
        return self.fw.op("pool", fn, r, w)

    def pe(self, fn, r, w, signal=True):
        return self.fw.op("pe", fn, r, w, signal=signal)

    def load(self, dst, dst_ap, src, src_ap, q="sp"):
        if dst_ap.dtype != src_ap.dtype:
            q = "pool"
        return self.fw.dma(q, dst, dst_ap, src, src_ap)

    def load_consts(self):
        fw, d = self.fw, self.d
        self.ident = fw.sb([128, 128], BF16, "ident")
        self.bones = fw.sb([128, 128], BF16, "bones")
        self.mask = fw.sb([128, 2560], BF16, "mask")
        self.ones = fw.sb([128, 128], BF16, "ones")
        self.onesf = fw.sb([128, 128], F32, "onesf")
        self.rope = None
        self.cond = fw.sb([128, 16], F32, "cond")
        self.idx1 = fw.sb([128, 12], I32, "idx1")
        self.idx2 = fw.sb([128, 4], I32, "idx2")
        for nm in ("ident", "bones", "mask", "cond", "idx1", "idx2"):
            t = getattr(self, nm)
            self.load(t, t[:], d[nm], d[nm].ap())
        self.pool(lambda: self.G.memset(self.ones[:], 1.0), [], [self.ones])
        self.pool(lambda: self.G.memset(self.onesf[:], 1.0), [], [self.onesf])
        self.hm = fw.sb([128, 2], F32, "hm")
        self.dve(lambda: self.V.tensor_copy(self.hm[:, 0:2], self.bones[:, 0:128:64]), [self.bones], [self.hm])
        xv = self.xT
        self.load(xv, xv[:, :, 0:NPT], d["xp"], d["xp"].ap().rearrange("(k p) t -> p k t", p=128))
        self.load(xv, xv[:, :, NPT:NTOK], d["xs"], d["xs"].ap().rearrange("(k p) t -> p k t", p=128))

    def stream_plan(self, ids):
        self.plan.extend(ids)

    def stream_begin(self, nblocks, depth=1):
        fw = self.fw
        self.sdepth = depth
        self.slots = [fw.sb([128, FBLK], BF16, f"slot{i}") for i in range(depth + 1)]
        self.stg = [fw.sb([128, FBLK // 2], F32, f"wstg{i}") for i in range(2 * depth)]
        self.blocks_left = nblocks
        self.scope_end = self.plan_pos + nblocks
        self.dma_issued = {}

    def _issue_dma(self, pos):
        blk = self.plan[pos]
        src = self.d["wst"]
        for hf in range(2):
            st = self.stg[(2 * pos + hf) % len(self.stg)]
            self.fw.dma("sp", st, st[:, :], src, src[blk, :, hf * (FBLK // 2):(hf + 1) * (FBLK // 2)])
        self.dma_issued[pos] = True

    def next_block(self, blk):
        pos = self.plan_pos
        assert self.plan[pos] == blk, (pos, self.plan[pos], blk)
        assert self.blocks_left > 0
        if pos not in self.dma_issued:
            self._issue_dma(pos)
        slot = self.slots[pos % len(self.slots)]
        for hf in range(2):
            st = self.stg[(2 * pos + hf) % len(self.stg)]
            if hf == 0:
                self.dve(lambda hf=hf, st=st: self.V.tensor_copy(slot[:, hf * (FBLK // 2):(hf + 1) * (FBLK // 2)], st[:, :]), [st], [slot])
            else:
                self.act(lambda hf=hf, st=st: self.A.copy(slot[:, hf * (FBLK // 2):(hf + 1) * (FBLK // 2)], st[:, :]), [st], [slot])
        self.plan_pos += 1
        self.blocks_left -= 1
        return slot

    def prefetch_next(self):
        for pos in range(self.plan_pos, min(self.plan_pos + self.sdepth, self.scope_end)):
            if pos not in self.dma_issued:
                self._issue_dma(pos)

    def load_layer_small(self, l):
        fw, d = self.fw, self.d
        self.sp = fw.sb([128, NSP], F32, "sp")
        self.load(self.sp, self.sp[:], d["sp"], d["sp"][l, :, :])
        sp = self.sp
        o, _ = SP_OFF["mu_rkv"]
        self.mu1 = fw.sb([128, 15], F32, "mu1")
        self.muh = fw.sb([128, 15], F32, "muh")
        self.dve(lambda: self.V.tensor_scalar(self.mu1[:], sp[:, o:o + 15], -1.0, 1.0, ALU.mult, ALU.add), [sp], [self.mu1])
        self.dve(lambda: self.V.tensor_scalar(self.muh[:], sp[:, o:o + 15], 0.5, None, ALU.mult), [sp], [self.muh])
        o2, _ = SP_OFF["rw"]
        self.rwh = fw.sb([128, 16], F32, "rwh")
        self.dve(lambda: self.V.tensor_scalar(self.rwh[:], sp[:, o2:o2 + 16], 0.5, None, ALU.mult), [sp], [self.rwh])
        ob, _ = SP_OFF["b_merge"]
        self.bmh = fw.sb([128, 32], F32, "bmh")
        self.dve(lambda: self.V.tensor_scalar(self.bmh[:], sp[:, ob:ob + 32], 0.5, None, ALU.mult), [sp], [self.bmh])

    def spc(self, name, i=0, n=1):
        o, _ = SP_OFF[name]
        return self.sp[:, o + i:o + i + n]

    def ada(self, l):
        fw = self.fw
        sc = fw.sb([128, 16], BF16, "scond")
        th = fw.sb([128, 16], F32, "cth")
        c = self.cond
        self.act(lambda: self.A.activation(th[:], c[:], AF.Tanh, scale=0.5), [c], [th])
        t2 = fw.sb([128, 16], F32, "ct2")
        self.dve(lambda: self.V.scalar_tensor_tensor(t2[:], th[:], 1.0, c[:], ALU.add, ALU.mult), [th, c], [t2])
        self.dve(lambda: self.V.tensor_scalar(sc[:], t2[:], 0.5, None, ALU.mult), [t2], [sc])
        ps = self.psb[5]
        self.mod = fw.sb([128, 24, 2], F32, "mod")
        self.gmod = fw.sb([128, 8, 2], F32, "gmod")
        fw.push()
        self.stream_begin(6, depth=2)
        for b in range(6):
            slot = self.next_block(l * BLK_PER_LAYER + b)
            for n in range(4):
                m = b * 4 + n
                for kc in range(8):
                    self.pe(lambda kc=kc, n=n, m=m, slot=slot: self.T.matmul(
                        ps[:, 2 * m:2 * m + 2], slot[:, kc * 512 + n * 128:kc * 512 + n * 128 + 128],
                        sc[:, 2 * kc:2 * kc + 2], start=(kc == 0), stop=(kc == 7)),
                        [slot, sc], [ps], signal=(kc == 7 and n == 3))
            self.prefetch_next()
        fw.pop()
        ob, _ = SP_OFF["b_ada"]
        for cc in range(2):
            self.dve(lambda cc=cc: self.V.tensor_tensor(self.mod[:, :, cc], ps[:, cc:48:2], self.sp[:, ob:ob + 24], ALU.add),
                     [ps, self.sp], [self.mod])
        og, _ = SP_OFF["norm_g"]
        for cc in range(2):
            self.dve(lambda cc=cc: self.V.scalar_tensor_tensor(self.gmod[:, :, cc], self.mod[:, 8:16, cc], 1.0,
                                                               self.sp[:, og:og + 8], ALU.add, ALU.mult),
                     [self.mod, self.sp], [self.gmod])

    def rstd_tile(self, src, views, nfeat, out, N):
        fw = self.fw
        ps = self.nps()
        nk = len(views)
        for i, v in enumerate(views):
            sq = fw.rot([128, 512], BF16, "sq")
            self.act(lambda v=v, sq=sq: self.A.activation(sq[:, :N], v, AF.Square), [src], [sq])
            self.pe(lambda i=i, sq=sq: self.T.matmul(ps[:, :N], self.ones[:], sq[:, :N], start=(i == 0), stop=(i == nk - 1)),
                    [self.ones, sq], [ps], signal=True)
        t = fw.sb([128, 512], F32, "rs_t")
        self.dve(lambda: self.V.tensor_scalar(t[:, :N], ps[:, :N], 1.0 / nfeat, EPS, ALU.mult, ALU.add), [ps], [t])
        self.rsqrt(out, out[:, :N], t, t[:, :N])

    def make_h(self, cc, x0, h0, N):
        fw = self.fw
        fw.push()
        rstd = fw.sb([128, 512], F32, "rstd")
        self.rstd_tile(self.xT, [self.xT[:, kc, x0:x0 + N] for kc in range(8)], float(D), rstd, N)
        for kc in range(8):
            tmp = fw.rot([128, 512], F32, "htmp")
            self.dve(lambda kc=kc, tmp=tmp: self.V.scalar_tensor_tensor(
                tmp[:, :N], self.xT[:, kc, x0:x0 + N], self.gmod[:, kc, cc:cc + 1], rstd[:, :N], ALU.mult, ALU.mult),
                [self.xT, self.gmod, rstd], [tmp])
            self.act(lambda kc=kc, tmp=tmp: self.A.activation(
                self.hTs[h0 // 512][:, kc, 0:N], tmp[:, :N], AF.Identity, bias=self.mod[:, kc, cc:cc + 1], scale=1.0),
                [tmp, self.mod], [self.hTs[h0 // 512]])
        fw.pop()

    def zmm(self, slot, W, c0, w, h0, N, ps=None, prow=0):
        ps = ps or self.nps()
        for kc in range(8):
            self.pe(lambda kc=kc: self.T.matmul(ps[prow:prow + w, :N], slot[:, kc * W + c0:kc * W + c0 + w],
                                                self.hTs[h0 // 512][:, kc, 0:N], start=(kc == 0), stop=(kc == 7)),
                    [slot, self.hTs[h0 // 512]], [ps], signal=(kc == 7))
        return ps

    def silu2(self, ps, rows, N, out_ap, out_buf):
        fw = self.fw
        th = fw.rot([128, 512], F32, "s2th")
        self.act(lambda: self.A.activation(th[:rows, :N], ps[:rows, :N], AF.Tanh, scale=0.5), [ps], [th])
        self.dve(lambda: self.V.scalar_tensor_tensor(out_ap, th[:rows, :N], 1.0, ps[:rows, :N], ALU.add, ALU.mult),
                 [th, ps], [out_buf])

    def gelu2(self, ps, rows, N, out_ap, out_buf):
        fw = self.fw
        u = fw.rot([128, 512], F32, "g2u")
        self.act(lambda: self.A.activation(u[:rows, :N], ps[:rows, :N], AF.Square), [ps], [u])
        self.dve(lambda: self.V.tensor_scalar(u[:rows, :N], u[:rows, :N], 0.044715, 1.0, ALU.mult, ALU.add), [u], [u])
        self.dve(lambda: self.V.tensor_tensor(u[:rows, :N], u[:rows, :N], ps[:rows, :N], ALU.mult), [u, ps], [u])
        self.act(lambda: self.A.activation(u[:rows, :N], u[:rows, :N], AF.Tanh, scale=0.7978845608028654), [u], [u])
        self.dve(lambda: self.V.scalar_tensor_tensor(out_ap, u[:rows, :N], 1.0, ps[:rows, :N], ALU.add, ALU.mult),
                 [u, ps], [out_buf])

    def transpose_to(self, src_buf, src_ap, dst_buf, dst_ap, rows=128, cols=128, eng="act"):
        pt = self.pst[self.pst_rr % 2]
        self.pst_rr += 1
        self.pe(lambda: self.T.transpose(pt[:cols, :rows], src_ap, self.ident[:rows, :rows]), [src_buf, self.ident], [pt])
        if eng == "act":
            self.act(lambda: self.A.copy(dst_ap, pt[:cols, :rows]), [pt], [dst_buf])
        else:
            self.dve(lambda: self.V.tensor_copy(dst_ap, pt[:cols, :rows]), [pt], [dst_buf])

    def phaseC(self, l, h0, N, o0):
        fw = self.fw
        base = l * BLK_PER_LAYER
        fw.push()
        self.stream_begin(3, depth=2)
        U2 = fw.sb([128, 4, 512], BF16, "U2")
        GV = fw.sb([128, 4, 512], BF16, "GV")
        GC2 = fw.sb([128, 4, 512], BF16, "GC2")
        wsT = fw.sb([128, 512], BF16, "wsT")
        bsb = fw.sb([128, 512], F32, "bsb")
        self.load(wsT, wsT[:], self.d["wsT"], self.d["wsT"][l, :, :])
        self.load(bsb, bsb[:], self.d["bsb"], self.d["bsb"][l, :, :])
        slot = self.next_block(base + WIN_IDX["C0"])
        for c in range(4):
            ps = self.zmm(slot, 512, c * 128, 128, h0, N)
            self.gelu2(ps, 128, N, U2[:, c, :N], U2)
        self.prefetch_next()
        slot = self.next_block(base + WIN_IDX["C1"])
        for c in range(4):
            ps = self.zmm(slot, 512, c * 128, 128, h0, N)
            self.gelu2(ps, 128, N, GV[:, c, :N], GV)
        self.prefetch_next()
        slot = self.next_block(base + WIN_IDX["C2"])
        for c in range(4):
            ps = self.zmm(slot, 512, c * 128, 128, h0, N)
            self.silu2(ps, 128, N, GC2[:, c, :N], GC2)
        self.prefetch_next()
        psm = self.nps()
        psq = self.nps()
        for c in range(4):
            self.pe(lambda c=c: self.T.matmul(psm[:, :N], self.ones[:], GV[:, c, :N], start=(c == 0), stop=(c == 3)),
                    [self.ones, GV], [psm], signal=(c == 3))
        for c in range(4):
            sq = fw.rot([128, 512], BF16, "gsq")
            self.act(lambda c=c, sq=sq: self.A.activation(sq[:, :N], GV[:, c, :N], AF.Square), [GV], [sq])
            self.pe(lambda c=c, sq=sq: self.T.matmul(psq[:, :N], self.ones[:], sq[:, :N], start=(c == 0), stop=(c == 3)),
                    [self.ones, sq], [psq], signal=True)
        mu = fw.sb([128, 512], F32, "gmu")
        msq = fw.sb([128, 512], F32, "gmsq")
        var = fw.sb([128, 512], F32, "gvar")
        rstd = fw.sb([128, 512], F32, "grstd")
        self.dve(lambda: self.V.tensor_scalar(mu[:, :N], psm[:, :N], 1.0 / 512, None, ALU.mult), [psm], [mu])
        self.dve(lambda: self.V.tensor_tensor(msq[:, :N], mu[:, :N], mu[:, :N], ALU.mult), [mu], [msq])
        self.dve(lambda: self.V.scalar_tensor_tensor(var[:, :N], psq[:, :N], 1.0 / 512, msq[:, :N], ALU.mult, ALU.subtract),
                 [psq, msq], [var])
        self.dve(lambda: self.V.tensor_scalar(var[:, :N], var[:, :N], 4e-5, None, ALU.add), [var], [var])
        self.rsqrt(rstd, rstd[:, :N], var, var[:, :N])
        VN = fw.sb([128, 4, 512], BF16, "VN")
        for c in range(4):
            t = fw.rot([128, 512], F32, "lnt")
            self.dve(lambda c=c, t=t: self.V.tensor_tensor(t[:, :N], GV[:, c, :N], mu[:, :N], ALU.subtract), [GV, mu], [t])
            self.dve(lambda t=t: self.V.tensor_tensor(t[:, :N], t[:, :N], rstd[:, :N], ALU.mult), [t, rstd], [t])
            self.act(lambda c=c, t=t: self.A.activation(VN[:, c, :N], t[:, :N], AF.Identity, bias=self.spc("gln_b", c),
                                                        scale=self.spc("gln_g", c)), [t, self.sp], [VN])
        nsub = N // 128
        for g in range(4):
            pmix = self.nps()
            for s in range(nsub):
                vtm = fw.rot([128, 128], BF16, "vtm")
                self.transpose_to(VN, VN[:, g, s * 128:(s + 1) * 128], vtm, vtm[:], eng=("act" if s % 2 else "dve"))
                self.pe(lambda g=g, s=s, vtm=vtm: self.T.matmul(pmix[:, s * 128:(s + 1) * 128], vtm[:],
                                                                 wsT[:, g * 128:(g + 1) * 128], start=True, stop=True),
                        [vtm, wsT], [pmix], signal=(s == nsub - 1))
            t = fw.rot([128, 512], F32, "mixt")
            for s in range(nsub):
                self.dve(lambda g=g, s=s, t=t: self.V.tensor_tensor(t[:, s * 128:(s + 1) * 128], pmix[:, s * 128:(s + 1) * 128],
                                                                     bsb[:, g * 128:(g + 1) * 128], ALU.add), [pmix, bsb], [t])
            self.dve(lambda g=g, t=t: self.V.scalar_tensor_tensor(t[:, :N], t[:, :N], 0.25, U2[:, g, :N], ALU.mult, ALU.mult),
                     [t, U2], [t])
            self.dve(lambda g=g, t=t: self.V.tensor_tensor(self.oTs[o0 // 512][2][:, g, 0:N], t[:, :N], GC2[:, g, :N], ALU.mult),
                     [t, GC2], [self.oTs[o0 // 512][2]])
        fw.pop()

    def merge_out(self, l, cc, x0, NT):
        fw = self.fw
        base = l * BLK_PER_LAYER
        ntile = NT // 512
        fw.push()
        self.stream_begin(18, depth=2)
        merged = fw.sb([128, 8, NT], BF16, "merged")
        for d in range(8):
            slotM = self.next_block(base + 18 + 2 * d)
            self.prefetch_next()
            slotB = self.next_block(base + 19 + 2 * d)
            for tt in range(ntile):
                t0 = tt * 512
                acc = fw.rot([128, 512], F32, "macc")
                for n in range(4):
                    psg = self.nps()
                    for kc in range(8):
                        self.pe(lambda kc=kc, n=n, tt=tt: self.T.matmul(psg[:, :], slotM[:, kc * 512 + n * 128:kc * 512 + n * 128 + 128],
                                                                 self.hTs[tt][:, kc, :], start=(kc == 0), stop=(kc == 7)),
                                [slotM, self.hTs[tt]], [psg], signal=(kc == 7))
                    psp = self.nps()
                    for k4 in range(4):
                        self.pe(lambda k4=k4, n=n, tt=tt: self.T.matmul(psp[:, :], slotB[:, (n * 4 + k4) * 128:(n * 4 + k4) * 128 + 128],
                                                                 self.oTs[tt][n][:, k4, :], start=(k4 == 0), stop=(k4 == 3)),
                                [slotB, self.oTs[tt][n]], [psp], signal=(k4 == 3))
                    th = fw.rot([128, 512], F32, "mth")
                    self.act(lambda n=n, th=th, psg=psg: self.A.activation(th[:], psg[:], AF.Tanh, bias=self.bmh[:, d * 4 + n:d * 4 + n + 1],
                                                                           scale=0.5), [psg, self.bmh], [th])
                    if n == 0:
                        self.dve(lambda th=th, psp=psp: self.V.scalar_tensor_tensor(acc[:], th[:], 1.0, psp[:], ALU.add, ALU.mult),
                                 [th, psp], [acc])
                    else:
                        self.dve(lambda th=th, psp=psp: self.V.scalar_tensor_tensor(th[:], th[:], 1.0, psp[:], ALU.add, ALU.mult),
                                 [th, psp], [th])
                        self.dve(lambda th=th: self.V.tensor_tensor(acc[:], acc[:], th[:], ALU.add), [acc, th], [acc])
                self.act(lambda acc=acc, t0=t0: self.A.mul(merged[:, d, t0:t0 + 512], acc[:], 0.5), [acc], [merged])
            self.prefetch_next()
        for b in range(2):
            slotO = self.next_block(base + 34 + b)
            self.prefetch_next()
            for dd in range(4):
                dch = b * 4 + dd
                for tt in range(ntile):
                    t0 = tt * 512
                    ps = self.nps()
                    for kc in range(8):
                        self.pe(lambda kc=kc, dd=dd: self.T.matmul(ps[:, :], slotO[:, kc * 512 + dd * 128:kc * 512 + dd * 128 + 128],
                                                                   merged[:, kc, t0:t0 + 512], start=(kc == 0), stop=(kc == 7)),
                                [slotO, merged], [ps], signal=(kc == 7))
                    self.dve(lambda dch=dch, t0=t0, ps=ps: self.V.scalar_tensor_tensor(
                        self.xT[:, dch, x0 + t0:x0 + t0 + 512], ps[:, :], self.mod[:, 16 + dch, cc:cc + 1],
                        self.xT[:, dch, x0 + t0:x0 + t0 + 512], ALU.mult, ALU.add), [ps, self.mod, self.xT], [self.xT])
        fw.pop()

    def final_out(self):
        fw = self.fw
        for (x0, N, dst) in ((0, 512, ("yp", 0)), (512, 512, ("yp", 512)), (NPT, 512, ("ys", 0))):
            fw.push()
            rstd = fw.sb([128, 512], F32, "frstd")
            self.rstd_tile(self.xT, [self.xT[:, kc, x0:x0 + N] for kc in range(8)], float(D), rstd, N)
            stg = fw.sb([128, 8, 512], F32, "fstg")
            for kc in range(8):
                self.dve(lambda kc=kc: self.V.scalar_tensor_tensor(stg[:, kc, :], self.xT[:, kc, x0:x0 + N], self.spc("fin_g", kc),
                                                                   rstd[:, :], ALU.mult, ALU.mult), [self.xT, self.sp, rstd], [stg])
            dt = self.d[dst[0]]
            ncol = NPT if dst[0] == "yp" else NST
            dview = dt.ap().rearrange("(k p) t -> p k t", p=128)[:, :, dst[1]:dst[1] + N]
            fw.dma("sp", dt, dview, stg, stg[:], sem_owner=self.outsem)
            fw.pop()

    def shift_evac(self, ps, rows, N, nseq, mu1, muh, out_ap, out_buf, tanh=False):
        fw = self.fw
        zt = fw.rot([128, 512], F32, "shz")
        o32 = fw.rot([128, 512], F32, "sho")
        self.act(lambda: self.A.copy(zt[:rows, :N], ps[:rows, :N]), [ps], [zt])
        self.dve(lambda: self.V.tensor_scalar(o32[:rows, :N], zt[:rows, :N], mu1, None, ALU.mult), [zt, self.mu1], [o32])
        z3 = zt[:rows, :N].rearrange("p (s t) -> p s t", s=nseq)
        o3 = o32[:rows, :N].rearrange("p (s t) -> p s t", s=nseq)
        Tq = N // nseq
        self.dve(lambda: self.V.scalar_tensor_tensor(o3[:, :, 1:Tq], z3[:, :, 0:Tq - 1], muh, o3[:, :, 1:Tq], ALU.mult, ALU.add),
                 [zt, self.muh, o32], [o32])
        self.dve(lambda: self.V.scalar_tensor_tensor(o3[:, :, 0:Tq - 1], z3[:, :, 1:Tq], muh, o3[:, :, 0:Tq - 1], ALU.mult, ALU.add),
                 [zt, self.muh, o32], [o32])
        if tanh:
            self.act(lambda: self.A.activation(out_ap, o32[:rows, :N], AF.Tanh), [o32], [out_buf])
        else:
            self.act(lambda: self.A.copy(out_ap, o32[:rows, :N]), [o32], [out_buf])

    def load_rwkv_w(self, l, own):
        fw, d = self.fw, self.d
        ncol = 256 if own else 1024
        self.wup = fw.sb([64, ncol], BF16, "wup")
        self.aup = fw.sb([64, ncol], BF16, "aup")
        sw, sa = (d["wupo"], d["aupo"]) if own else (d["wup"], d["aup"])
        self.load(self.wup, self.wup[:, :], sw, sw[l, :, :])
        self.load(self.aup, self.aup[:, :], sa, sa[l, :, :])

    def phaseA_prompt(self, l, half):
        fw = self.fw
        base = l * BLK_PER_LAYER
        h0 = half * 512
        fw.push()
        self.load_rwkv_w(l, False)
        zz = [fw.sb([128, 4, 512], BF16, nm) for nm in ("zr", "zk", "zv")]
        lo = [fw.sb([64, 512], BF16, nm) for nm in ("twdf", "twdb", "adT")]
        GA2 = fw.sb([128, 4, 512], BF16, "GA2")
        fw.push()
        self.stream_begin(5, depth=2)
        for which in range(3):
            slot = self.next_block(base + WIN_IDX[f"A{which}"])
            for c in range(4):
                ps = self.zmm(slot, 512, c * 128, 128, h0, 512)
                i = which * 4 + c
                self.shift_evac(ps, 128, 512, 2, self.mu1[:, i:i + 1], self.muh[:, i:i + 1], zz[which][:, c, :], zz[which])
            self.prefetch_next()
        slot = self.next_block(base + WIN_IDX["A3"])
        for i in range(3):
            ps = self.zmm(slot, 192, i * 64, 64, h0, 512)
            self.shift_evac(ps, 64, 512, 2, self.mu1[0:64, 12 + i:13 + i], self.muh[0:64, 12 + i:13 + i], lo[i][:, :], lo[i], tanh=(i < 2))
        self.prefetch_next()
        slot = self.next_block(base + WIN_IDX["A4"])
        for c in range(4):
            ps = self.zmm(slot, 512, c * 128, 128, h0, 512)
            self.silu2(ps, 128, 512, GA2[:, c, :], GA2)
        self.prefetch_next()
        fw.pop()
        for sq in range(2):
            for pair in range(4):
                t0 = sq * 256
                seqi = half * 2 + sq

                def yout(c, yfin, pair=pair, t0=t0):
                    cs = t0 + c * 128
                    self.dve(lambda: self.V.scalar_tensor_tensor(self.oTs[half][0][:, pair, cs:cs + 128], yfin[:, :], 0.5,
                                                                 GA2[:, pair, cs:cs + 128], ALU.mult, ALU.mult), [yfin, GA2], [self.oTs[half][0]])

                def stout(dd, ST, pair=pair, seqi=seqi):
                    so = self.d["stout"]
                    row = (((l * 2 + dd) * 4 + seqi) * 4 + pair) * 128
                    fw.dma("sp", so, so[row:row + 128, :], ST, ST[:, :], sem_owner=self.outsem)

                J = dict(T=256, r=(zz[0], lambda a, b, pair=pair, t0=t0: zz[0][:, pair, t0 + a:t0 + b]),
                         k=(zz[1], lambda a, b, pair=pair, t0=t0: zz[1][:, pair, t0 + a:t0 + b]),
                         v=(zz[2], lambda a, b, pair=pair, t0=t0: zz[2][:, pair, t0 + a:t0 + b]),
                         twd=[(lo[0], lambda a, b, t0=t0: lo[0][:, t0 + a:t0 + b]), (lo[1], lambda a, b, t0=t0: lo[1][:, t0 + a:t0 + b])],
                         ad=(lo[2], lambda a, b, t0=t0: lo[2][:, t0 + a:t0 + b]),
                         par=lambda nm, pair=pair: self.sp[:, rw_col(nm, pair):rw_col(nm, pair) + 1],
                         parh=lambda nm, pair=pair: self.rwh[:, RW_NAMES.index(nm) * 4 + pair:RW_NAMES.index(nm) * 4 + pair + 1],
                         parbufs=[self.sp, self.rwh],
                         wup=lambda dd, pair=pair: self.wup[:, dd * 512 + pair * 128:dd * 512 + pair * 128 + 128],
                         aup=lambda dd, pair=pair: self.aup[:, dd * 512 + pair * 128:dd * 512 + pair * 128 + 128],
                         st0=None, yout=yout, stout=stout, scoped=False)
                self.rwkv_job(J)
        fw.pop()

    def phaseA_contrib(self, l):
        fw = self.fw
        base = l * BLK_PER_LAYER
        self.GA2 = fw.sb([128, 4, 512], BF16, "GA2s")
        fw.push()
        self.stream_begin(5, depth=2)
        for which, (part, row0) in enumerate((("rk", 0), ("rk", 512), ("vx", VX_OFF["v"]))):
            slot = self.next_block(base + WIN_IDX[f"A{which}"])
            for c in range(4):
                self.contrib_rows(l, slot, 512, c * 128, 128, part, row0 + c * 128)
            self.prefetch_next()
        slot = self.next_block(base + WIN_IDX["A3"])
        for i in range(3):
            self.contrib_rows(l, slot, 192, i * 64, 64, "vx", VX_OFF["lora"] + i * 64)
        self.prefetch_next()
        slot = self.next_block(base + WIN_IDX["A4"])
        for c in range(4):
            ps = self.zmm(slot, 512, c * 128, 128, 0, 512)
            self.silu2(ps, 128, 512, self.GA2[:, c, :], self.GA2)
        self.prefetch_next()
        fw.pop()

    def gather_rows(self, dst, dst_ap, src, idx_col):
        fw = self.fw
        idx = self.idx1 if idx_col < 12 else self.idx2
        col = idx_col if idx_col < 12 else idx_col - 12
        fw.dma("pool", dst, None, src, None, extra_reads=[idx],
               fn=lambda: self.G.indirect_dma_start(out=dst_ap, out_offset=None, in_=src.h.ap(),
                                                    in_offset=bass.IndirectOffsetOnAxis(ap=idx[:, col:col + 1], axis=0)))

    def phaseA_consume(self, l):
        fw = self.fw
        V, A, G = self.V, self.A, self.G
        fw.push()
        self.load_rwkv_w(l, True)
        spo = fw.sb([128, 12], F32, "spo")
        self.load(spo, spo[:, :], self.d["spo"], self.d["spo"][l, :, :])
        spoh = fw.sb([128, 4], F32, "spoh")
        self.dve(lambda: V.tensor_scalar(spoh[:, :], spo[:, 0:4], 0.5, None, ALU.mult), [spo], [spoh])
        mu1o = fw.sb([128, 3], F32, "mu1o")
        muho = fw.sb([128, 3], F32, "muho")
        self.dve(lambda: V.tensor_scalar(mu1o[:, :], spo[:, 9:12], -1.0, 1.0, ALU.mult, ALU.add), [spo], [mu1o])
        self.dve(lambda: V.tensor_scalar(muho[:, :], spo[:, 9:12], 0.5, None, ALU.mult), [spo], [muho])
        T = DSEQ
        zz = [fw.sb([128, T], BF16, nm) for nm in ("sr", "sk", "sv")]
        lo = [fw.sb([64, T], BF16, nm) for nm in ("stwf", "stwb", "sad")]
        fw.push()
        raw = fw.sb([128, T], BF16, "sraw")
        o32 = fw.sb([128, T], F32, "so32")

        def shift_full(rows, src, m1, mh, dst, tanh=False):
            self.dve(lambda: V.tensor_scalar(o32[:rows, :], src[:rows, :], m1, None, ALU.mult), [src, mu1o, self.mu1], [o32])
            self.dve(lambda: V.scalar_tensor_tensor(o32[:rows, 1:T], src[:rows, 0:T - 1], mh, o32[:rows, 1:T], ALU.mult, ALU.add),
                     [src, muho, self.muh, o32], [o32])
            self.dve(lambda: V.scalar_tensor_tensor(o32[:rows, 0:T - 1], src[:rows, 1:T], mh, o32[:rows, 0:T - 1], ALU.mult, ALU.add),
                     [src, muho, self.muh, o32], [o32])
            if tanh:
                self.act(lambda: A.activation(dst[:rows, :], o32[:rows, :], AF.Tanh), [o32], [dst])
            else:
                self.act(lambda: A.copy(dst[:rows, :], o32[:rows, :]), [o32], [dst])

        for which in range(3):
            src = self.ag1_out[l]["rk" if which < 2 else "vx"]
            for q in range(4):
                self.gather_rows(raw, raw[:, q * 512:(q + 1) * 512], src, which * 4 + q)
            shift_full(128, raw, mu1o[:, which:which + 1], muho[:, which:which + 1], zz[which])
        agv = self.ag1_out[l]["vx"]
        for i in range(3):
            for q in range(4):
                r0 = q * 864 + VX_OFF["lora"] + i * 64
                fw.dma("sp", raw, raw[0:64, q * 512:(q + 1) * 512], agv, agv[r0:r0 + 64, :])
            shift_full(64, raw, self.mu1[0:64, 12 + i:13 + i], self.muh[0:64, 12 + i:13 + i], lo[i], tanh=(i < 2))
        fw.pop()
        stg = [None]

        def yout(c, yfin):
            q, cc = c // 4, c % 4
            if cc == 0:
                stg[0] = fw.rot([128, 512], BF16, "ystg", n=2)
            st = stg[0]
            self.act(lambda: A.copy(st[:, cc * 128:(cc + 1) * 128], yfin[:, :]), [yfin], [st])
            if cc == 3:
                ag = self.ag2_in[l]
                fw.dma("sp", ag, ag[q * 128:(q + 1) * 128, :], st, st[:, :])

        pidx = {nm: i for i, nm in enumerate(RW_NAMES)}
        J = dict(T=T, r=(zz[0], lambda a, b: zz[0][:, a:b]), k=(zz[1], lambda a, b: zz[1][:, a:b]), v=(zz[2], lambda a, b: zz[2][:, a:b]),
                 twd=[(lo[0], lambda a, b: lo[0][:, a:b]), (lo[1], lambda a, b: lo[1][:, a:b])],
                 ad=(lo[2], lambda a, b: lo[2][:, a:b]),
                 par=lambda nm: spo[:, pidx[nm]:pidx[nm] + 1],
                 parh=lambda nm: spoh[:, pidx[nm]:pidx[nm] + 1],
                 parbufs=[spo, spoh],
                 wup=lambda dd: self.wup[:, dd * 128:(dd + 1) * 128],
                 aup=lambda dd: self.aup[:, dd * 128:(dd + 1) * 128],
                 st0=lambda dd: (self.d["st0"], self.d["st0"][l, dd, :, :]), yout=yout, stout=None, seg=256, segpar=False)
        self.rwkv_job(J)
        self.allgather(self.ag2_out[l], self.ag2_in[l])
        fw.pop()

    def phaseA_final(self, l):
        fw = self.fw
        fw.push()
        for r in range(4):
            ya = fw.rot([128, 512], BF16, "ya", n=2)
            self.gather_rows(ya, ya[:, :], self.ag2_out[l], 12 + r)
            self.dve(lambda r=r, ya=ya: self.V.scalar_tensor_tensor(self.oTs[0][0][:, r, 0:512], ya[:, :], 0.5, self.GA2[:, r, :], ALU.mult, ALU.mult),
                     [ya, self.GA2], [self.oTs[0][0]])
        fw.pop()

    def rwkv_job(self, J):
        fw = self.fw
        V, A, G, T_ = self.V, self.A, self.G, self.T
        T = J["T"]
        nch = T // 128
        SEG = J.get("seg", 256)
        nseg = T // SEG
        ncs = SEG // 128
        rB, rf = J["r"]
        kB, kf = J["k"]
        vB, vf = J["v"]
        adB, adf = J["ad"]
        par, parh, pbufs = J["par"], J["parh"], J["parbufs"]
        self.ps_range = (0, 6)
        scoped = J.get("scoped", True)
        jpush = (lambda: fw.push()) if scoped else (lambda: None)
        jpop = (lambda: fw.pop()) if scoped else (lambda: None)
        jt = (lambda shp, dt, nm: fw.sb(shp, dt, nm)) if scoped else (lambda shp, dt, nm: fw.rot(shp, dt, "J" + nm, n=1))
        jpush()
        kap = jt([128, T], BF16, "kap")
        Vtm = jt([128, nch, 128], BF16, "Vtm")
        Yacc = jt([128, nch, 128], F32, "Yacc")
        Bacc = jt([128, T], F32, "Bacc")
        ST = [jt([128, 64], F32, f"ST{dd}") for dd in range(2)]
        STb = [jt([128, 64], BF16, f"STb{dd}") for dd in range(2)]
        KW = min(T, 512)
        self.pool(lambda: G.memset(Yacc[:, :, :], 0.0), [], [Yacc])
        self.pool(lambda: G.memset(Bacc[:, :], 0.0), [], [Bacc])
        jpush()
        for p0 in range(0, T, 512):
            N = min(512, T - p0)
            kk = fw.rot([128, KW], F32, "kk", n=(2 if scoped else 1))
            sq = fw.rot([128, KW], BF16, "kksq", n=(2 if scoped else 1))
            self.dve(lambda: V.tensor_scalar(kk[:, :N], kf(p0, p0 + N), par("k_k"), None, ALU.mult), [kB] + pbufs, [kk])
            self.act(lambda: A.activation(sq[:, :N], kk[:, :N], AF.Square), [kk], [sq])
            ps = self.nps()
            self.pe(lambda: T_.matmul(ps[:, :N], self.bones[:, :], sq[:, :N], start=True, stop=True), [self.bones, sq], [ps])
            t = fw.rot([128, KW], F32, "kkt", n=(2 if scoped else 1))
            self.dve(lambda: V.tensor_scalar(t[:, :N], ps[:, :N], 1e-24, None, ALU.max), [ps], [t])
            self.rsqrt(t, t[:, :N], t, t[:, :N])
            self.dve(lambda: V.tensor_tensor(kap[:, p0:p0 + N], kk[:, :N], t[:, :N], ALU.mult), [kk, t], [kap])
        import os
        STOP = int(os.environ.get("RWKV_STOP", "99"))
        if STOP <= 1:
            jpop(); jpop(); return
        for c in range(nch):
            self.transpose_to(vB, vf(c * 128, c * 128 + 128), Vtm, Vtm[:, c, :], eng=("act" if c % 2 else "dve"))
        jpop()
        if STOP <= 2:
            jpop(); return
        jpush()
        for dd in range(2):
            if J["st0"] is None:
                self.pool(lambda dd=dd: G.memset(ST[dd][:, :], 0.0), [], [ST[dd]])
            else:
                src, sap = J["st0"](dd)
                fw.dma("sp", ST[dd], ST[dd][:, :], src, sap)
            self.act(lambda dd=dd: A.copy(STb[dd][:, :], ST[dd][:, :]), [ST[dd]], [STb[dd]])
        MK = self.mask
        def rw_segment(dd, sg, res, segpar):
            sfx = "fb"[dd]
            twB, twf = J["twd"][dd]
            s0 = sg * SEG
            N = SEG
            f32t = lambda nm: fw.rot([128, SEG], F32, nm + (str(dd) if segpar else ""), n=1)
            a = f32t("ra")
            ps = self.nps()
            self.pe(lambda: T_.matmul(ps[:, :N], J["aup"](dd), adf(s0, s0 + N), start=True, stop=True), [self.aup, adB], [ps])
            self.act(lambda: A.activation(a[:, :], ps[:, :N], AF.Tanh, bias=parh("a0_" + sfx), scale=0.5), [ps] + pbufs, [a])
            yield
            self.dve(lambda: V.tensor_scalar(a[:, :], a[:, :], 0.5, 0.5, ALU.mult, ALU.add), [a], [a])
            kt = f32t("rkt")
            self.dve(lambda: V.tensor_scalar(kt[:, :], a[:, :], 1.0, par("k_a"), ALU.subtract, ALU.mult), [a] + pbufs, [kt])
            self.dve(lambda: V.scalar_tensor_tensor(kt[:, :], kt[:, :], 1.0, kf(s0, s0 + N), ALU.add, ALU.mult), [kt, kB], [kt])
            b = f32t("rb")
            self.dve(lambda: V.tensor_tensor(b[:, :], a[:, :], kap[:, s0:s0 + N], ALU.mult), [a, kap], [b])
            lw = f32t("rlw")
            ps = self.nps()
            self.pe(lambda: T_.matmul(ps[:, :N], J["wup"](dd), twf(s0, s0 + N), start=True, stop=True), [self.wup, twB], [ps])
            self.act(lambda: A.activation(lw[:, :], ps[:, :N], AF.Tanh, bias=parh("w0_" + sfx), scale=0.5), [ps] + pbufs, [lw])
            yield
            self.dve(lambda: V.tensor_scalar(lw[:, :], lw[:, :], -0.3032653298563167, -0.3032653298563167, ALU.mult, ALU.add), [lw], [lw])
            rkr = fw.rot([128, SEG], BF16, "rkr" + str(dd), n=1)
            self.dve(lambda: V.scalar_tensor_tensor(rkr[:, :], kt[:, :], par("r_k"), rf(s0, s0 + N), ALU.mult, ALU.mult),
                     [kt, rB] + pbufs, [rkr])
            ps = self.nps()
            self.pe(lambda: T_.matmul(ps[:, :N], self.bones[:, :], rkr[:, :], start=True, stop=True), [self.bones, rkr], [ps])
            self.dve(lambda: V.tensor_tensor(Bacc[:, s0:s0 + N], Bacc[:, s0:s0 + N], ps[:, :N], ALU.add), [Bacc, ps], [Bacc])
            P = f32t("rP")
            for c in range(ncs):
                self.dve(lambda c=c: V.tensor_tensor_scan(P[:, c * 128:(c + 1) * 128], self.onesf[:, :], lw[:, c * 128:(c + 1) * 128],
                                                          0.0, ALU.mult, ALU.add), [self.onesf, lw], [P])
            Q = f32t("rQ")
            R = f32t("rR")
            self.dve(lambda: V.tensor_tensor(Q[:, :], P[:, :], lw[:, :], ALU.subtract), [P, lw], [Q])
            for c in range(ncs):
                self.dve(lambda c=c: V.tensor_scalar(R[:, c * 128:(c + 1) * 128], P[:, c * 128:(c + 1) * 128], -1.0,
                                                     P[:, c * 128 + 127:c * 128 + 128], ALU.mult, ALU.add), [P], [R])
            gL = fw.rot([128, 2], F32, "gL" + str(dd), n=2)
            self.act(lambda: A.activation(gL[:, 0:ncs], P[:, 127:SEG:128], AF.Exp), [P], [gL])
            yield
            if dd == 0:
                srcs = [(P, -1.0, a), (Q, 1.0, Q), (P, 1.0, P), (R, 1.0, R)]
            else:
                RL = f32t("rRL")
                self.dve(lambda: V.tensor_tensor(RL[:, :], R[:, :], lw[:, :], ALU.add), [R, lw], [RL])
                srcs = [(RL, -1.0, a), (R, 1.0, R), (RL, 1.0, RL), (Q, 1.0, Q)]
            E = [None] * 4
            for i, (sb_, sc_, dst_) in enumerate(srcs):
                self.act(lambda i=i, sb_=sb_, sc_=sc_, dst_=dst_: A.activation(dst_[:, :], sb_[:, :], AF.Exp, scale=sc_), [sb_], [dst_])
                E[i] = dst_
            Kd2 = fw.rot([128, 2, SEG], BF16, "Kd2" + str(dd), n=1)
            Bd2 = fw.rot([128, 2, SEG], BF16, "Bd2" + str(dd), n=1)
            KL = fw.rot([128, SEG], BF16, "KL" + str(dd), n=1)
            BL = fw.rot([128, SEG], BF16, "BL" + str(dd), n=1)
            KqRq2 = fw.rot([128, 2, ncs, 2, 128], BF16, "KqRq2" + str(dd), n=1)
            hm = self.hm
            kap3 = kap[:, s0:s0 + N].rearrange("p (c t) -> p c t", c=ncs)
            r3 = rf(s0, s0 + N).rearrange("p (c t) -> p c t", c=ncs)
            for h in range(2):
                self.dve(lambda h=h: V.scalar_tensor_tensor(Kd2[:, h, :], kt[:, :], hm[:, h:h + 1], E[0][:, :], ALU.mult, ALU.mult), [kt, hm, E[0]], [Kd2])
                self.dve(lambda h=h: V.scalar_tensor_tensor(Bd2[:, h, :], b[:, :], hm[:, h:h + 1], E[0][:, :], ALU.mult, ALU.mult), [b, hm, E[0]], [Bd2])
                self.dve(lambda h=h: V.scalar_tensor_tensor(KqRq2[:, h, :, 0, :], kap3, hm[:, h:h + 1], E[1][:, :].rearrange("p (c t) -> p c t", c=ncs),
                                                            ALU.mult, ALU.mult), [kap, hm, E[1]], [KqRq2])
                self.dve(lambda h=h: V.scalar_tensor_tensor(KqRq2[:, h, :, 1, :], r3, hm[:, h:h + 1], E[2][:, :].rearrange("p (c t) -> p c t", c=ncs),
                                                            ALU.mult, ALU.mult), [rB, hm, E[2]], [KqRq2])
            self.dve(lambda: V.tensor_tensor(KL[:, :], kt[:, :], E[3][:, :], ALU.mult), [kt, E[3]], [KL])
            self.dve(lambda: V.tensor_tensor(BL[:, :], b[:, :], E[3][:, :], ALU.mult), [b, E[3]], [BL])

            res.update(dict(Kd2=Kd2, Bd2=Bd2, KL=KL, BL=BL, KqRq2=KqRq2, gL=gL, sg=sg))
            yield

        def rw_pre(dd, c, Pd, res):
            sfx2 = f"{dd}{c}"
            mk0 = dd * 1280
            Kd2, Bd2, KqRq2 = Pd["Kd2"], Pd["Bd2"], Pd["KqRq2"]
            cs = slice(c * 128, (c + 1) * 128)
            Am = [fw.rot([128, 512], BF16, f"Am{h}_{sfx2}", n=1) for h in range(2)]
            psB = self.nps()
            for h in range(2):
                psA = self.nps()
                rhsA = KqRq2[:, h, c, :, :].rearrange("p a t -> p (a t)")
                self.pe(lambda h=h, psA=psA, rhsA=rhsA: T_.matmul(psA[:, 0:256], Kd2[:, h, cs], rhsA, start=True, stop=True),
                        [Kd2, KqRq2], [psA], signal=False)
                self.pe(lambda h=h, psA=psA, rhsA=rhsA: T_.matmul(psA[:, 256:512], Bd2[:, h, cs], rhsA, start=True, stop=True),
                        [Bd2, KqRq2], [psA])
                self.dve(lambda h=h, psA=psA: V.tensor_tensor(Am[h][:, :], psA[:, :], MK[:, mk0:mk0 + 512], ALU.mult), [psA, MK], [Am[h]])
                self.pe(lambda h=h: T_.matmul(psB[:, h * 128:(h + 1) * 128], KqRq2[:, h, c, 0, :], Bd2[:, h, cs], start=True, stop=True),
                        [KqRq2, Bd2], [psB], signal=(h == 1))
            PT = [fw.rot([128, 2, 128], BF16, f"PT{i}_{sfx2}", n=1) for i in range(2)]
            PX = [fw.rot([128, 2, 256], BF16, f"PX{i}_{sfx2}", n=1) for i in range(2)]
            C64T = fw.rot([128, 2, 128], BF16, "C64T" + sfx2, n=1)
            C128T = fw.rot([128, 2, 128], BF16, "C128T" + sfx2, n=1)
            f2 = lambda t: t[:, :, :].rearrange("p a t -> p (a t)")
            self.dve(lambda: V.tensor_tensor(f2(PT[0]), psB[:, 0:256], MK[:, mk0 + 512:mk0 + 768], ALU.mult), [psB, MK], [PT[0]])
            self.dve(lambda: V.tensor_tensor(f2(C64T), psB[:, 0:256], MK[:, mk0 + 768:mk0 + 1024], ALU.mult), [psB, MK], [C64T])
            self.dve(lambda: V.tensor_tensor(f2(C128T), psB[:, 0:256], MK[:, mk0 + 1024:mk0 + 1280], ALU.mult), [psB, MK], [C128T])
            for h in range(2):
                self.act(lambda h=h: A.copy(PX[0][:, h, 0:128], Am[h][:, 256:384]), [Am[h]], [PX[0]])
            yield
            cur = 0
            Xb = fw.rot([128, 2, 128], BF16, "Xb32" + sfx2, n=1)
            for j in range(1, 6):
                nxt = 1 - cur
                if j == 1:
                    ps = self.nps()
                    pst_ = self.nps()
                    for h in range(2):
                        self.pe(lambda h=h, ps=ps, cur=cur: T_.matmul(ps[:, h * 256:h * 256 + 128], PT[cur][:, h, :], PX[cur][:, h, 0:128],
                                                                      start=True, stop=True), [PT[cur], PX[cur]], [ps], signal=(h == 1))
                        self.pe(lambda h=h, pst_=pst_, cur=cur: T_.matmul(pst_[:, h * 128:(h + 1) * 128], PX[cur][:, h, 0:128], PT[cur][:, h, :],
                                                                          start=True, stop=True), [PT[cur], PX[cur]], [pst_], signal=(h == 1))
                    for h in range(2):
                        self.dve(lambda h=h, cur=cur, nxt=nxt: V.tensor_tensor(PX[nxt][:, h, 128:256], PX[cur][:, h, 0:128], self.ident[:, :], ALU.add),
                                 [PX[cur], self.ident], [PX[nxt]])
                    self.act(lambda ps=ps, nxt=nxt: A.copy(PX[nxt][:, :, 0:128], ps[:, :].rearrange("p (a t) -> p a t", a=2)[:, :, 0:128]),
                             [ps], [PX[nxt]])
                    self.act(lambda pst_=pst_, nxt=nxt: A.copy(f2(PT[nxt]), pst_[:, 0:256]), [pst_], [PT[nxt]])
                elif j < 5:
                    ps = self.nps()
                    pst_ = self.nps()
                    for h in range(2):
                        self.pe(lambda h=h, ps=ps, cur=cur: T_.matmul(ps[:, h * 256:(h + 1) * 256], PT[cur][:, h, :], PX[cur][:, h, :],
                                                                      start=True, stop=True), [PT[cur], PX[cur]], [ps], signal=(h == 1))
                        self.pe(lambda h=h, pst_=pst_, cur=cur: T_.matmul(pst_[:, h * 128:(h + 1) * 128], PX[cur][:, h, 0:128], PT[cur][:, h, :],
                                                                          start=True, stop=True), [PT[cur], PX[cur]], [pst_], signal=(h == 1))
                    ps3 = ps[:, :].rearrange("p (a t) -> p a t", a=2)
                    self.act(lambda ps3=ps3, ps=ps, nxt=nxt: A.copy(PX[nxt][:, :, 0:128], ps3[:, :, 0:128]), [ps], [PX[nxt]])
                    self.dve(lambda ps3=ps3, ps=ps, cur=cur, nxt=nxt: V.tensor_tensor(PX[nxt][:, :, 128:256], ps3[:, :, 128:256], PX[cur][:, :, 128:256], ALU.add),
                             [ps, PX[cur]], [PX[nxt]])
                    self.act(lambda pst_=pst_, nxt=nxt: A.copy(f2(PT[nxt]), pst_[:, 0:256]), [pst_], [PT[nxt]])
                else:
                    ps = self.nps()
                    for h in range(2):
                        self.pe(lambda h=h, ps=ps, cur=cur: T_.matmul(ps[:, h * 128:(h + 1) * 128], PT[cur][:, h, :], PX[cur][:, h, 128:256],
                                                                      start=True, stop=True), [PT[cur], PX[cur]], [ps], signal=(h == 1))
                    self.dve(lambda ps=ps, cur=cur: V.tensor_tensor(Xb[:, :, :], ps[:, 0:256].rearrange("p (a t) -> p a t", a=2), PX[cur][:, :, 128:256], ALU.add),
                             [ps, PX[cur]], [Xb])
                cur = nxt
                yield
            TT = None
            for lvl, CT in enumerate((C64T, C128T)):
                XT = fw.rot([128, 2, 128], BF16, "XTm" + sfx2, n=1)
                Zt = fw.rot([128, 2, 128], BF16, "Ztm" + sfx2, n=1)
                ptt = self.pst[self.pst_rr % 2]
                self.pst_rr += 1
                for h in range(2):
                    self.pe(lambda h=h, ptt=ptt, Xb=Xb: T_.transpose(ptt[:, h * 128:(h + 1) * 128], Xb[:, h, :], self.ident[:, :]),
                            [Xb, self.ident], [ptt], signal=(h == 1))
                self.act(lambda ptt=ptt, XT=XT: A.copy(f2(XT), ptt[:, 0:256]), [ptt], [XT])
                psz = self.nps()
                for h in range(2):
                    self.pe(lambda h=h, psz=psz, CT=CT, Xb=Xb: T_.matmul(psz[:, h * 128:(h + 1) * 128], CT[:, h, :], Xb[:, h, :], start=True, stop=True),
                            [CT, Xb], [psz], signal=(h == 1))
                self.dve(lambda psz=psz, Zt=Zt: V.tensor_copy(f2(Zt), psz[:, 0:256]), [psz], [Zt])
                psw = self.nps()
                for h in range(2):
                    self.pe(lambda h=h, psw=psw, XT=XT, Zt=Zt: T_.matmul(psw[:, h * 128:(h + 1) * 128], XT[:, h, :], Zt[:, h, :], start=True, stop=True),
                            [XT, Zt], [psw], signal=(h == 1))
                Xn = fw.rot([128, 2, 128], BF16, ("Xb64" if lvl == 0 else "TT") + sfx2, n=1)
                self.dve(lambda psw=psw, Xn=Xn, Xb=Xb: V.tensor_tensor(f2(Xn), psw[:, 0:256], f2(Xb), ALU.add), [psw, Xb], [Xn])
                Xb = Xn
                yield
            TT = Xb

            res["Am"] = Am
            res["TT"] = TT
            yield

        def rw_seq(dd, Pd, pres):
            KL, BL, KqRq2, gL, sg = Pd["KL"], Pd["BL"], Pd["KqRq2"], Pd["gL"], Pd["sg"]
            chunks = list(range(ncs)) if dd == 0 else list(reversed(range(ncs)))
            for c in chunks:
                cg = sg * ncs + c
                cs = slice(c * 128, (c + 1) * 128)
                Am, TT = pres[(dd, c)]["Am"], pres[(dd, c)]["TT"]
                Sb = STb[dd]
                psG = self.nps()
                for h in range(2):
                    hs = slice(64 * h, 64 * h + 64)
                    vs = slice(64 * h, 64 * h + 64)
                    self.pe(lambda h=h, vs=vs: T_.matmul(psG[:, vs], KqRq2[:, h, c, 0, :], Sb[:, :], start=(h == 0), stop=False, skip_group_check=True),
                            [KqRq2, Sb], [psG], signal=False)
                    self.pe(lambda h=h, vs=vs: T_.matmul(psG[:, vs], Am[h][:, 0:128], Vtm[:, cg, vs], start=False, stop=(h == 1), skip_group_check=True),
                            [Am[h], Vtm], [psG], signal=(h == 1))
                Gn = fw.rot([128, 128], BF16, "Gn" + str(dd), n=1)
                self.act(lambda: A.mul(Gn[:, :], psG[:, 0:128], -1.0), [psG], [Gn])
                yield
                psU = self.nps()
                for h in range(2):
                    vs = slice(64 * h, 64 * h + 64)
                    self.pe(lambda h=h, vs=vs: T_.matmul(psU[:, vs], TT[:, h, :], Gn[:, vs], start=(h == 0), stop=(h == 1), skip_group_check=True),
                            [TT, Gn], [psU], signal=(h == 1))
                U = fw.rot([128, 128], BF16, "U" + str(dd), n=1)
                self.dve(lambda: V.tensor_copy(U[:, :], psU[:, 0:128]), [psU], [U])
                yield
                yield
                psY = self.nps()
                for h in range(2):
                    hs = slice(64 * h, 64 * h + 64)
                    vs = slice(64 * h, 64 * h + 64)
                    self.pe(lambda h=h, vs=vs: T_.matmul(psY[:, vs], KqRq2[:, h, c, 1, :], Sb[:, :], start=(h == 0), stop=False, skip_group_check=True),
                            [KqRq2, Sb], [psY], signal=False)
                    self.pe(lambda h=h, vs=vs: T_.matmul(psY[:, vs], Am[h][:, 128:256], Vtm[:, cg, vs], start=False, stop=False, skip_group_check=True),
                            [Am[h], Vtm], [psY], signal=False)
                    self.pe(lambda h=h, vs=vs: T_.matmul(psY[:, vs], Am[h][:, 384:512], U[:, vs], start=False, stop=(h == 1), skip_group_check=True),
                            [Am[h], U], [psY], signal=(h == 1))
                self.dve(lambda: V.tensor_tensor(Yacc[:, cg, :], Yacc[:, cg, :], psY[:, 0:128], ALU.add), [Yacc, psY], [Yacc])
                yield
                yield
                KLt = fw.rot([128, 128], BF16, "KLt" + str(dd), n=1)
                BLt = fw.rot([128, 128], BF16, "BLt" + str(dd), n=1)
                self.transpose_to(KL, KL[:, cs], KLt, KLt[:, :], eng="act")
                self.transpose_to(BL, BL[:, cs], BLt, BLt[:, :], eng="dve")
                psS = self.nps()
                for h in range(2):
                    hs = slice(64 * h, 64 * h + 64)
                    vs = slice(64 * h, 64 * h + 64)
                    self.pe(lambda h=h, hs=hs, vs=vs: T_.matmul(psS[hs, 0:64], KLt[:, hs], Vtm[:, cg, vs], start=True, stop=False),
                            [KLt, Vtm], [psS], signal=False)
                    self.pe(lambda h=h, hs=hs, vs=vs: T_.matmul(psS[hs, 0:64], BLt[:, hs], U[:, vs], start=False, stop=True),
                            [BLt, U], [psS], signal=(h == 1))
                self.dve(lambda: V.scalar_tensor_tensor(ST[dd][:, :], ST[dd][:, :], gL[:, c:c + 1], psS[:, 0:64], ALU.mult, ALU.add),
                         [ST[dd], gL, psS], [ST[dd]])
                self.act(lambda: A.copy(STb[dd][:, :], ST[dd][:, :]), [ST[dd]], [STb[dd]])

        def run_rr(gens):
            gens = list(gens)
            while gens:
                for g in list(gens):
                    try:
                        next(g)
                    except StopIteration:
                        gens.remove(g)

        for step in range(nseg):
            sgs = (step, nseg - 1 - step)
            Pd = [{}, {}]
            segpar = J.get("segpar", False)
            if segpar:
                run_rr([rw_segment(dd, sgs[dd], Pd[dd], True) for dd in range(2)])
            else:
                for dd in range(2):
                    for _ in rw_segment(dd, sgs[dd], Pd[dd], False):
                        pass
            if STOP <= 3:
                continue
            pres = {(dd, c): {} for dd in range(2) for c in range(ncs)}
            run_rr([rw_pre(dd, c, Pd[dd], pres[(dd, c)]) for dd in range(2) for c in range(ncs)])
            if STOP <= 5:
                continue
            run_rr([rw_seq(dd, Pd[dd], pres) for dd in range(2)])
        if J["stout"] is not None:
            for dd in range(2):
                J["stout"](dd, ST[dd])
        jpop()
        if STOP <= 6:
            jpop(); return
        n2 = nch * 2
        sums = jt([128, n2], F32, "gsum")
        ssq = jt([128, n2], F32, "gssq")
        Ysq = jt([128, nch * 128], F32, "Ysq") if scoped else fw.rot([128, KW], F32, "kk", n=1)
        Yf = Yacc[:, :, :].rearrange("p c x -> p (c x)")
        self.dve(lambda: V.tensor_reduce(sums[:, :], Yf.rearrange("p (g x) -> p g x", x=64), AX.X, ALU.add), [Yacc], [sums])
        self.act(lambda: A.activation(Ysq[:, :], Yf, AF.Square), [Yacc], [Ysq])
        self.dve(lambda: V.tensor_reduce(ssq[:, :], Ysq[:, :].rearrange("p (g x) -> p g x", x=64), AX.X, ALU.add), [Ysq], [ssq])
        mean = jt([128, n2], F32, "gmean")
        var = jt([128, n2], F32, "gvar2")
        self.dve(lambda: V.tensor_scalar(mean[:, :], sums[:, :], 1.0 / 64, None, ALU.mult), [sums], [mean])
        self.dve(lambda: V.tensor_tensor(var[:, :], mean[:, :], mean[:, :], ALU.mult), [mean], [var])
        self.dve(lambda: V.scalar_tensor_tensor(var[:, :], ssq[:, :], 1.0 / 64, var[:, :], ALU.mult, ALU.subtract), [ssq, var], [var])
        self.dve(lambda: V.tensor_scalar(var[:, :], var[:, :], GN_EPS, None, ALU.add), [var], [var])
        self.rsqrt(var, var[:, :], var, var[:, :])
        yn = jt([128, nch, 128], BF16, "yn")
        for c in range(nch):
            for h in range(2):
                g = c * 2 + h
                self.dve(lambda c=c, h=h, g=g: V.tensor_scalar(yn[:, c, h * 64:(h + 1) * 64], Yacc[:, c, h * 64:(h + 1) * 64], mean[:, g:g + 1], var[:, g:g + 1],
                                                               ALU.subtract, ALU.mult), [Yacc, mean, var], [yn])
        for c in range(nch):
            pt = self.pst[self.pst_rr % 2]
            self.pst_rr += 1
            self.pe(lambda c=c, pt=pt: T_.transpose(pt[:, :128], yn[:, c, :], self.ident[:, :]), [yn, self.ident], [pt])
            yT = fw.rot([128, 128], F32, "yT", n=(2 if scoped else 1))
            self.act(lambda pt=pt, yT=yT: A.activation(yT[:, :], pt[:, :128], AF.Identity, bias=par("ln_b"), scale=par("ln_g")), [pt] + pbufs, [yT])
            bo = fw.rot([128, 128], F32, "bo", n=(2 if scoped else 1))
            self.dve(lambda c=c, bo=bo: V.tensor_tensor(bo[:, :], Bacc[:, c * 128:(c + 1) * 128], vf(c * 128, (c + 1) * 128), ALU.mult), [Bacc, vB], [bo])
            yfin = fw.rot([128, 128], F32, "yfin", n=(2 if scoped else 1))
            self.dve(lambda yT=yT, bo=bo, yfin=yfin: V.tensor_tensor(yfin[:, :], yT[:, :], bo[:, :], ALU.add), [yT, bo], [yfin])
            J["yout"](c, yfin)
        jpop()
        self.ps_range = (0, 4)

    def load_mla_w(self, l):
        fw, d = self.fw, self.d
        self.wq = fw.sb([128, 2 * 8 * 96], BF16, "wq")
        self.wqs = fw.sb([128, 2 * 8 * 32], BF16, "wqs")
        self.wkk = fw.sb([128, 512], BF16, "wkk")
        self.wkv = fw.sb([128, 512], BF16, "wkv")
        for nm in ("wq", "wqs", "wkk", "wkv"):
            t = getattr(self, nm)
            self.load(t, t[:], d[nm], d[nm][l, :, :])

    def mla_front(self, l, h0, rope, GB2, Qh, ckv_f, ckv_b, kr_f, kr_b):
        fw = self.fw
        base = l * BLK_PER_LAYER
        W = WIN_W["B0"]
        slot = self.next_block(base + WIN_IDX["B0"])
        qd = fw.sb([128, 2, 512], F32, "qd")
        kvd = fw.sb([128, 512], F32, "kvd")
        for c in range(2):
            ps = self.zmm(slot, W, c * 128, 128, h0, 512)
            self.act(lambda c=c, ps=ps: self.A.copy(qd[:, c, :], ps[:, :]), [ps], [qd])
        ps = self.zmm(slot, W, 256, 128, h0, 512)
        self.dve(lambda ps=ps: self.V.tensor_copy(kvd[:, :], ps[:, :]), [ps], [kvd])
        pk = self.zmm(slot, W, 384, 32, h0, 512, prow=64)
        R = self.rope
        if rope:
            pks = self.zmm(slot, W, 416, 32, h0, 512, prow=64)
            t1 = fw.sb([96, 512], F32, "krt1")
            self.dve(lambda: self.V.tensor_tensor(t1[64:96, :], pk[64:96, :], R[64:96, 0:512], ALU.mult), [pk, R], [t1])
            self.dve(lambda: self.V.tensor_tensor(kr_f[64:96, :], pks[64:96, :], R[64:96, 512:1024], ALU.mult), [pks, R], [kr_f])
            self.dve(lambda: self.V.tensor_tensor(kr_f[64:96, :], kr_f[64:96, :], t1[64:96, :], ALU.add), [kr_f, t1], [kr_f])
        else:
            self.act(lambda: self.A.copy(kr_f[64:96, :], pk[64:96, :]), [pk], [kr_f])
        self.act(lambda: self.A.copy(kr_b[64:96, :], kr_f[64:96, :]), [kr_f], [kr_b])
        self.prefetch_next()
        slot = self.next_block(base + WIN_IDX["B1"])
        for c in range(4):
            ps = self.zmm(slot, 512, c * 128, 128, h0, 512)
            self.silu2(ps, 128, 512, GB2[:, c, :], GB2)
        self.prefetch_next()
        rq = fw.sb([128, 512], F32, "rq")
        self.rstd_tile(qd, [qd[:, c, :] for c in range(2)], 256.0, rq, 512)
        qn = fw.sb([128, 2, 512], BF16, "qn")
        for c in range(2):
            self.dve(lambda c=c: self.V.scalar_tensor_tensor(qn[:, c, :], qd[:, c, :], self.spc("qn", c), rq[:, :], ALU.mult, ALU.mult),
                     [qd, self.sp, rq], [qn])
        rk = fw.sb([128, 512], F32, "rkv")
        self.rstd_tile(kvd, [kvd[:, :]], 128.0, rk, 512)
        self.dve(lambda: self.V.scalar_tensor_tensor(ckv_f[:, :], kvd[:, :], self.spc("kvn", 0), rk[:, :], ALU.mult, ALU.mult),
                 [kvd, self.sp, rk], [ckv_f])
        self.act(lambda: self.A.copy(ckv_b[:, :], ckv_f[:, :]), [ckv_f], [ckv_b])
        for h in range(8):
            ps = self.nps()
            for c in range(2):
                self.pe(lambda c=c, h=h, ps=ps: self.T.matmul(ps[:96, :], self.wq[:, (c * 8 + h) * 96:(c * 8 + h) * 96 + 96], qn[:, c, :],
                                                             start=(c == 0), stop=(c == 1)), [self.wq, qn], [ps], signal=(c == 1))
            if rope:
                ps2 = self.nps()
                for c in range(2):
                    self.pe(lambda c=c, h=h, ps2=ps2: self.T.matmul(ps2[64:96, :], self.wqs[:, (c * 8 + h) * 32:(c * 8 + h) * 32 + 32], qn[:, c, :],
                                                                   start=(c == 0), stop=(c == 1)), [self.wqs, qn], [ps2], signal=(c == 1))
                t1 = fw.rot([96, 512], F32, "qrt1")
                t2 = fw.rot([96, 512], F32, "qrt2")
                self.dve(lambda ps=ps, t1=t1: self.V.tensor_tensor(t1[64:96, :], ps[64:96, :], R[64:96, 0:512], ALU.mult), [ps, R], [t1])
                self.dve(lambda ps2=ps2, t2=t2: self.V.tensor_tensor(t2[64:96, :], ps2[64:96, :], R[64:96, 512:1024], ALU.mult), [ps2, R], [t2])
                self.dve(lambda h=h, t1=t1, t2=t2: self.V.tensor_tensor(Qh[h][64:96, :], t1[64:96, :], t2[64:96, :], ALU.add), [t1, t2], [Qh[h]])
                self.act(lambda h=h, ps=ps: self.A.copy(Qh[h][0:64, :], ps[0:64, :]), [ps], [Qh[h]])
            else:
                self.act(lambda h=h, ps=ps: self.A.copy(Qh[h][:, :], ps[:96, :]), [ps], [Qh[h]])

    def mla_kv_chunk(self, ckv_b, kr_b, heads, Kh, Vaug, nk):
        for i, h in enumerate(heads):
            ps = self.nps()
            self.pe(lambda h=h, ps=ps: self.T.matmul(ps[:64, :nk], self.wkk[:, h * 64:(h + 1) * 64], ckv_b[:, :nk], start=True, stop=True),
                    [self.wkk, ckv_b], [ps])
            self.act(lambda i=i, ps=ps: self.A.copy(Kh[i][0:64, :nk], ps[0:64, :nk]), [ps], [Kh[i]])
            self.dve(lambda i=i: self.V.tensor_copy(Kh[i][64:96, :nk], kr_b[64:96, :nk]), [kr_b], [Kh[i]])
        for kb in range(nk // 128):
            ps = self.nps()
            self.pe(lambda kb=kb, ps=ps: self.T.matmul(ps[:, :], ckv_b[:, kb * 128:(kb + 1) * 128], self.wkv[:, :], start=True, stop=True),
                    [ckv_b, self.wkv], [ps])
            for h8 in range(8):
                pass
            self.dve(lambda kb=kb, ps=ps: self.V.tensor_copy(
                Vaug[:, kb * 520:(kb + 1) * 520].rearrange("p (h e) -> p h e", e=65)[:, :, 0:64],
                ps[:, :].rearrange("p (h e) -> p h e", e=64)), [ps], [Vaug])

    def attn_accum(self, Qh4, q0, nq, Kh4, Vaug, heads, k0, nkb, first, last, Oacc):
        fw = self.fw
        nqs = nq // 128
        its = [(kb, i, h) for kb in range(nkb) for i, h in enumerate(heads)]

        def score(kb, i, h):
            pss = self.psb[4 + (self.sc_rr % 2)]
            self.sc_rr += 1
            self.pe(lambda: self.T.matmul(pss[:, :nq], Kh4[i][:, k0 + kb * 128:k0 + kb * 128 + 128],
                                          Qh4[i][:, q0:q0 + nq], start=True, stop=True), [Kh4[i], Qh4[i]], [pss])
            PT = fw.rot([128, 512], BF16, "PT", n=3)
            self.act(lambda: self.A.activation(PT[:, :nq], pss[:, :nq], AF.Exp, scale=96.0 ** -0.5), [pss], [PT])
            return PT

        def pv(kb, i, h, PT):
            for qs in range(nqs):
                self.pe(lambda qs=qs: self.T.matmul(
                    Oacc[qs][:, i * 65:(i + 1) * 65], PT[:, qs * 128:(qs + 1) * 128],
                    Vaug[:, (k0 // 128 + kb) * 520 + h * 65:(k0 // 128 + kb) * 520 + h * 65 + 65],
                    start=(first and kb == 0 and i == 0), stop=(last and kb == nkb - 1 and i == 3), skip_group_check=True),
                    [PT, Vaug], [Oacc[qs]], signal=(qs == nqs - 1))

        pend = score(*its[0])
        for n in range(len(its)):
            nxt = score(*its[n + 1]) if n + 1 < len(its) else None
            pv(*its[n], pend)
            pend = nxt

    def attn_finish(self, Oacc, nqs, ob, hh):
        fw = self.fw
        for qs in range(nqs):
            rec = fw.rot([128, 4], F32, "rec", n=4)
            self.dve(lambda qs=qs, rec=rec: self.V.reciprocal(rec[:, :], Oacc[qs][:, 64:260:65]), [Oacc[qs]], [rec])
            for i in range(4):
                self.dve(lambda qs=qs, i=i, rec=rec: self.V.tensor_scalar(ob[qs][:, hh * 256 + i * 64:hh * 256 + i * 64 + 64],
                                                                          Oacc[qs][:, i * 65:i * 65 + 64], rec[:, i:i + 1], None, ALU.mult),
                         [Oacc[qs], rec], [ob[qs]])

    def attn_out(self, ob, nqs, GB2, g0, o0):
        for qs in range(nqs):
            for c in range(4):
                pt = self.pst[self.pst_rr % 2]
                self.pst_rr += 1
                self.pe(lambda qs=qs, c=c, pt=pt: self.T.transpose(pt[:, :128], ob[qs][:, c * 128:(c + 1) * 128], self.ident[:, :]),
                        [ob[qs], self.ident], [pt])
                self.dve(lambda qs=qs, c=c, pt=pt: self.V.scalar_tensor_tensor(
                    self.oTs[o0 // 512][1][:, c, o0 % 512 + qs * 128:o0 % 512 + qs * 128 + 128], pt[:, :128], 0.5, GB2[:, c, g0 + qs * 128:g0 + qs * 128 + 128],
                    ALU.mult, ALU.mult), [pt, GB2], [self.oTs[o0 // 512][1]])

    def phaseB_prompt(self, l, half):
        fw = self.fw
        h0 = half * 512
        fw.push()
        self.load_mla_w(l)
        GB2 = fw.sb([128, 4, 512], BF16, "GB2")
        Qh = [fw.sb([96, 512], BF16, f"Qh{h}") for h in range(8)]
        ckv_f = fw.sb([128, 512], F32, "ckvf")
        ckv_b = fw.sb([128, 512], BF16, "ckvb")
        kr_f = fw.sb([96, 512], F32, "krf")
        kr_b = fw.sb([96, 512], BF16, "krb")
        fw.push()
        self.stream_begin(2, depth=2)
        self.mla_front(l, h0, False, GB2, Qh, ckv_f, ckv_b, kr_f, kr_b)
        fw.pop()
        fw.dma("sp", self.d["ckvout"], self.d["ckvout"][l, :, h0:h0 + 512], ckv_f, ckv_f[:, :], sem_owner=self.outsem)
        fw.dma("sp", self.d["krout"], self.d["krout"][l, :, h0:h0 + 512], kr_f, kr_f[64:96, :], sem_owner=self.outsem)
        Kh = [fw.sb([96, 512], BF16, f"Kh{h}") for h in range(8)]
        Vaug = fw.sb([128, 4 * 520], BF16, "Vaug")
        self.pool(lambda: self.G.memset(Vaug[:, :], 1.0), [], [Vaug])
        self.mla_kv_chunk(ckv_b, kr_b, list(range(8)), Kh, Vaug, 512)
        self.sc_rr = 0
        import os
        if "dumpB" in os.environ.get("KDBG", "") and l == 0 and half == 0:
            so = self.d["stout"]
            fw.dma("pool", so, so[0:768, :].rearrange("(p a) b -> p (a b)", p=96), Qh[0], Qh[0][:, :])
            fw.dma("pool", so, so[768:1536, :].rearrange("(p a) b -> p (a b)", p=96), Kh[0], Kh[0][:, :])
            fw.dma("pool", so, so[1536:5632, :].rearrange("(p a) b -> p (a b)", p=128), Vaug, Vaug[:, 0:2048])
        for sq in range(2):
            ob = [fw.rot([128, 512], BF16, "ob", n=4) for _ in range(2)]
            for hh in range(2):
                heads = list(range(hh * 4, hh * 4 + 4))
                Oacc = [self.psb[0 + 2 * (hh % 2)], self.psb[1 + 2 * (hh % 2)]]
                self.attn_accum([Qh[h] for h in heads], sq * 256, 256, [Kh[h] for h in heads], Vaug, heads, sq * 256, 2, True, True, Oacc)
                self.attn_finish(Oacc, 2, ob, hh)
            if "dumpB" in os.environ.get("KDBG", "") and l == 0 and half == 0 and sq == 0:
                so = self.d["stout"]
                fw.dma("pool", so, so[5696:6720, :].rearrange("(p a) b -> p (a b)", p=128), ob[0], ob[0][:, :])
            self.attn_out(ob, 2, GB2, sq * 256, h0 + sq * 256)
        fw.pop()

    def phaseB_contrib(self, l):
        fw = self.fw
        self.load_mla_w(l)
        self.GB2 = fw.sb([128, 4, 512], BF16, "GB2s")
        self.Qh = [fw.sb([96, 512], BF16, f"Qhs{h}") for h in range(8)]
        fw.push()
        self.rope = fw.sb([96, 1024], F32, "rope")
        self.load(self.rope, self.rope[64:96, :], self.d["rope"], self.d["rope"].ap())
        self.stream_begin(2, depth=2)
        ckv_f = fw.sb([128, 512], F32, "ckvf")
        ckv_b = fw.sb([128, 512], BF16, "ckvb")
        kr_f = fw.sb([96, 512], F32, "krf")
        kr_b = fw.sb([96, 512], BF16, "krb")
        self.mla_front(l, 0, True, self.GB2, self.Qh, ckv_f, ckv_b, kr_f, kr_b)
        ag = self.ag1_in[l]["vx"]
        fw.dma("sp", ag, ag[VX_OFF["ckv"]:VX_OFF["ckv"] + 128, :], ckv_b, ckv_b[:, :])
        fw.dma("sp", ag, ag[VX_OFF["kr"]:VX_OFF["kr"] + 32, :], kr_b, kr_b[64:96, :])
        fw.pop()

    def phaseB_consume(self, l):
        fw = self.fw
        ago = self.ag1_out[l]["vx"]
        fw.push()
        self.sc_rr = 0
        self.ps_range = (4, 6)
        ob = [fw.sb([128, 512], BF16, f"obs{i}") for i in range(4)]
        for hh in range(2):
            heads = list(range(hh * 4, hh * 4 + 4))
            Oacc = self.psb[0:4]
            for ch in range(5):
                ckv_b = fw.rot([128, 512], BF16, "ckvg", n=2)
                kr_b = fw.rot([96, 512], BF16, "krg", n=2)
                if ch < 4:
                    fw.dma("sp", ckv_b, ckv_b[:, :], ago, ago[ch * 864 + VX_OFF["ckv"]:ch * 864 + VX_OFF["ckv"] + 128, :])
                    fw.dma("sp", kr_b, kr_b[64:96, :], ago, ago[ch * 864 + VX_OFF["kr"]:ch * 864 + VX_OFF["kr"] + 32, :])
                else:
                    c32 = fw.rot([128, 512], F32, "cc32", n=1)
                    k32 = fw.rot([96, 512], F32, "ck32", n=1)
                    fw.dma("sp", c32, c32[:, :], self.d["cckv"], self.d["cckv"][l, :, :])
                    fw.dma("sp", k32, k32[64:96, :], self.d["ckr"], self.d["ckr"][l, :, :])
                    self.act(lambda c32=c32, ckv_b=ckv_b: self.A.copy(ckv_b[:, :], c32[:, :]), [c32], [ckv_b])
                    self.dve(lambda k32=k32, kr_b=kr_b: self.V.tensor_copy(kr_b[64:96, :], k32[64:96, :]), [k32], [kr_b])
                Kh = [fw.rot([96, 512], BF16, f"Khs{i}", n=2) for i in range(4)]
                Vaug = fw.rot([128, 4 * 520], BF16, "Vaugs", n=2)
                self.pool(lambda Vaug=Vaug: self.G.memset(Vaug[:, :], 1.0), [], [Vaug])
                self.mla_kv_chunk(ckv_b, kr_b, heads, Kh, Vaug, 512)
                self.attn_accum([self.Qh[h] for h in heads], 0, 512, Kh, Vaug, heads, 0, 4, ch == 0, ch == 4, Oacc)
            self.attn_finish(Oacc, 4, ob, hh)
        self.ps_range = (0, 4)
        self.attn_out(ob, 4, self.GB2, 0, 0)
        fw.pop()

    def fnet_stage1(self, fT_buf, fT_ap_fn, dftd, G1):
        for hb in range(2):
            ps = self.psb[4 + hb]
            for gg in range(2):
                g = hb * 2 + gg
                self.pe(lambda g=g, gg=gg, ps=ps: self.T.matmul(ps[:, gg * 256:(gg + 1) * 256], fT_ap_fn(g), dftd[:, :],
                                                                 start=True, stop=True), [fT_buf, dftd], [ps], signal=(gg == 1))
            if hb == 0:
                self.act(lambda ps=ps: self.A.copy(G1[:, 0:512], ps[:, :]), [ps], [G1])
            else:
                self.dve(lambda ps=ps: self.V.tensor_copy(G1[:, 512:1024], ps[:, :]), [ps], [G1])

    def phaseD_prompt(self, l, half):
        fw = self.fw
        base = l * BLK_PER_LAYER
        h0 = half * 512
        fw.push()
        self.dftd_p = fw.sb([128, 256], BF16, "dftd_p")
        self.dftT_p = fw.sb([128, 1024], BF16, "dftT_p")
        for nm in ("dftd_p", "dftT_p"):
            t = getattr(self, nm)
            self.load(t, t[:], self.d[nm], self.d[nm].ap())
        fT = fw.sb([128, 4, 512], BF16, "fT")
        GD2 = fw.sb([128, 4, 512], BF16, "GD2")
        fw.push()
        self.stream_begin(2, depth=2)
        slot = self.next_block(base + WIN_IDX["D0"])
        for c in range(4):
            ps = self.zmm(slot, 512, c * 128, 128, h0, 512)
            if c % 2:
                self.act(lambda c=c, ps=ps: self.A.copy(fT[:, c, :], ps[:, :]), [ps], [fT])
            else:
                self.dve(lambda c=c, ps=ps: self.V.tensor_copy(fT[:, c, :], ps[:, :]), [ps], [fT])
        self.prefetch_next()
        slot = self.next_block(base + WIN_IDX["D1"])
        for c in range(4):
            ps = self.zmm(slot, 512, c * 128, 128, h0, 512)
            self.silu2(ps, 128, 512, GD2[:, c, :], GD2)
        self.prefetch_next()
        fw.pop()
        for sq in range(2):
            t0 = sq * 256
            G1 = [fw.rot([128, 1024], BF16, "G1", n=4) for _ in range(2)]
            for tt in range(2):
                self.fnet_stage1(fT, lambda g, tt=tt: fT[:, g, t0 + tt * 128:t0 + tt * 128 + 128], self.dftd_p, G1[tt])
            for g in range(4):
                ps = self.nps()
                i = 0
                for tt in range(2):
                    for cs in range(2):
                        self.pe(lambda g=g, tt=tt, cs=cs, ps=ps, i=i: self.T.matmul(
                            ps[:, :256], G1[tt][:, g * 256 + cs * 128:g * 256 + cs * 128 + 128],
                            self.dftT_p[:, (tt * 2 + cs) * 256:(tt * 2 + cs) * 256 + 256], start=(i == 0), stop=(i == 3)),
                            [G1[tt], self.dftT_p], [ps], signal=(i == 3))
                        i += 1
                self.dve(lambda g=g, ps=ps: self.V.scalar_tensor_tensor(
                    self.oTs[half][3][:, g, t0:t0 + 256], ps[:, :256], 0.5, GD2[:, g, t0:t0 + 256], ALU.mult, ALU.mult),
                    [ps, GD2], [self.oTs[half][3]])
        fw.pop()

    def contrib_rows(self, l, slot, W, c0, w, part, row0):
        fw = self.fw
        ps = self.zmm(slot, W, c0, w, 0, 512)
        stg = fw.rot([128, 512], BF16, "agstg", n=3)
        if self.cflip % 2:
            self.act(lambda: self.A.copy(stg[:w, :], ps[:w, :]), [ps], [stg])
        else:
            self.dve(lambda: self.V.tensor_copy(stg[:w, :], ps[:w, :]), [ps], [stg])
        self.cflip += 1
        ag = self.ag1_in[l][part]
        fw.dma("sp", ag, ag[row0:row0 + w, :], stg, stg[:w, :])

    def phaseD_contrib(self, l):
        fw = self.fw
        base = l * BLK_PER_LAYER
        self.GD2 = fw.sb([128, 4, 512], BF16, "GD2s")
        fw.push()
        self.stream_begin(2, depth=2)
        slot = self.next_block(base + WIN_IDX["D0"])
        for c in range(4):
            self.contrib_rows(l, slot, 512, c * 128, 128, "f", c * 128)
        self.prefetch_next()
        slot = self.next_block(base + WIN_IDX["D1"])
        for c in range(4):
            ps = self.zmm(slot, 512, c * 128, 128, 0, 512)
            self.silu2(ps, 128, 512, self.GD2[:, c, :], self.GD2)
        self.prefetch_next()
        fw.pop()

    def phaseD_consume(self, l):
        fw = self.fw
        ago = self.ag1_out[l]["f"]
        fw.push()
        self.dftd_s = fw.sb([128, 256], BF16, "dftd_s")
        self.load(self.dftd_s, self.dftd_s[:], self.d["dftd_s"], self.d["dftd_s"].ap())
        acc = self.psb[0:4]
        nt = 0
        for q in range(4):
            fq = fw.rot([128, 4, 512], BF16, "fq", n=2)
            src = ago[q * 512:q * 512 + 512, :].rearrange("(g d) t -> d g t", d=128)
            fw.dma("sp", fq, fq[:], ago, src)
            for s4 in range(4):
                tt = q * 4 + s4
                ct = fw.rot([128, 1024], BF16, "ct", n=3)
                fw.dma("pool", ct, ct[:], self.d["dftT_s"], self.d["dftT_s"][tt, :, :])
                G1 = fw.rot([128, 1024], BF16, "G1s", n=3)
                self.fnet_stage1(fq, lambda g, s4=s4, fq=fq: fq[:, g, s4 * 128:(s4 + 1) * 128], self.dftd_s, G1)
                for g in range(4):
                    for cs in range(2):
                        self.pe(lambda g=g, cs=cs, G1=G1, ct=ct, tt=tt: self.T.matmul(
                            acc[g][:, :], G1[:, g * 256 + cs * 128:g * 256 + cs * 128 + 128], ct[:, cs * 512:(cs + 1) * 512],
                            start=(tt == 0 and cs == 0), stop=(tt == 15 and cs == 1)),
                            [G1, ct], [acc[g]], signal=(g == 3 and cs == 1))
        for g in range(4):
            self.dve(lambda g=g: self.V.scalar_tensor_tensor(self.oTs[0][3][:, g, 0:512], acc[g][:, :], 0.5, self.GD2[:, g, :],
                                                             ALU.mult, ALU.mult), [acc[g], self.GD2], [self.oTs[0][3]])
        fw.pop()

    def allgather(self, dst, src):
        fw = self.fw
        fw.dma("pool", dst, None, src, None, sem_owner=dst, inc=1,
               fn=lambda: self.G.collective_compute("AllGather", ALU.bypass, replica_groups=[[0, 1, 2, 3], [4, 5, 6, 7]],
                                                     ins=[src.h.ap()], outs=[dst.h.ap()]))

    def sample_pass(self, l):
        import os
        fw = self.fw
        dbg = os.environ.get("KDBG", "")
        self.cflip = 0
        br = self.branches
        fw.push()
        if "A" in br:
            self.phaseA_contrib(l)
        fw.push()
        if "B" in br:
            self.phaseB_contrib(l)
        fw.push()
        if "D" in br:
            self.phaseD_contrib(l)
        if "noag" not in dbg:
            if "A" in br:
                self.allgather(self.ag1_out[l]["rk"], self.ag1_in[l]["rk"])
            if "A" in br or "B" in br:
                self.allgather(self.ag1_out[l]["vx"], self.ag1_in[l]["vx"])
            if "D" in br:
                self.allgather(self.ag1_out[l]["f"], self.ag1_in[l]["f"])
        if "C" in br:
            self.phaseC(l, 0, 512, 0)
        if "D" in br and "nocons" not in dbg:
            self.phaseD_consume(l)
        fw.pop()
        if "B" in br:
            self.phaseB_consume(l)
        fw.pop()
        if "A" in br and "noAcons" not in dbg:
            self.phaseA_consume(l)
            self.phaseA_final(l)
        fw.pop()

    def zero_branch(self, n):
        for hf in range(2):
            if self.oTs[hf] is not None:
                t = self.oTs[hf][n]
                self.pool(lambda t=t: self.G.memset(t[:], 0.0), [], [t])

    def win_plan(self, l, grp="p"):
        base = l * BLK_PER_LAYER
        ids = []
        order = (("A", ["A0", "A1", "A2", "A3", "A4"]), ("B", ["B0", "B1"]), ("C", ["C0", "C1", "C2"]), ("D", ["D0", "D1"]))
        if grp == "s":
            order = (order[0], order[1], order[3], order[2])
        for br, names in order:
            if br in self.branches:
                ids += [base + WIN_IDX[n] for n in names]
        return ids

    def tail_plan(self, l):
        base = l * BLK_PER_LAYER
        return [base + 18 + i for i in range(16)] + [base + 34, base + 35]

    def build(self):
        fw = self.fw
        self.outsem = Buf(None, "outsem", "none")
        for l in range(self.depth):
            self.stream_plan([l * BLK_PER_LAYER + b for b in range(6)])
            self.stream_plan(self.win_plan(l) * 2 + self.tail_plan(l))
            self.stream_plan(self.win_plan(l, 's') + self.tail_plan(l))
        for l in range(self.depth):
            fw.push()
            self.load_layer_small(l)
            self.ada(l)
            fw.push()
            self.hTs[1] = fw.sb([128, 8, 512], BF16, "hTb")
            self.oTs[1] = [fw.sb([128, 4, 512], BF16, f"oTb{n}") for n in range(4)]
            for n, br in enumerate("ABCD"):
                if br not in self.branches:
                    self.zero_branch(n)
            for half in range(2):
                self.make_h(0, half * 512, half * 512, 512)
            for half in range(2):
                self.phases(l, "p", half)
            self.merge_out(l, 0, 0, NPT)
            fw.pop()
            self.hTs[1] = None
            self.oTs[1] = None
            self.make_h(1, NPT, 0, 512)
            self.sample_pass(l)
            self.merge_out(l, 1, NPT, NST)
            fw.pop()
        fw.push()
        self.sp = fw.sb([128, NSP], F32, "spf")
        self.load(self.sp, self.sp[:], self.d["sp"], self.d["sp"][0, :, :])
        self.final_out()
        fw.pop()
        for e in ("sp",):
            for ev in list(fw.dma_out.values()):
                fw._wait(e, ev)
        fw.barrier()
        return self.nc

    def phases(self, l, grp, half):
        h0 = half * 512
        if "A" in self.branches:
            self.phaseA_prompt(l, half)
        if "B" in self.branches:
            self.phaseB_prompt(l, half)
        if "C" in self.branches:
            self.phaseC(l, h0, 512, h0)
        if "D" in self.branches:
            self.phaseD_prompt(l, half)


_CFG = {"branches": "ABCD", "depth": DEPTH}


def make_in_maps(inp):
    f32 = lambda a: np.ascontiguousarray(np.asarray(a, dtype=np.float32))
    inp = {k: np.asarray(v) for k, v in inp.items()}
    wst = build_stream(f32(inp["w_ada"]), f32(inp["w_in"]), f32(inp["w_branch"]), f32(inp["w_merge"]), f32(inp["w_out"]))
    sp = build_small(inp)
    cst = build_consts()
    L = DEPTH
    wup = f32(inp["rwkv_w_up"]).transpose(0, 2, 1, 3).reshape(L, 64, 1024)
    aup = f32(inp["rwkv_a_up"]).transpose(0, 2, 1, 3).reshape(L, 64, 1024)
    wqu = f32(inp["mla_w_q_up"]).reshape(L, 2, 128, 8, 96)
    wq = wqu.transpose(0, 2, 1, 3, 4).reshape(L, 128, 2 * 8 * 96)
    wqs = wqu[..., 64 + _SWAP32].transpose(0, 2, 1, 3, 4).reshape(L, 128, 2 * 8 * 32)
    wkvu = f32(inp["mla_w_kv_up"]).reshape(L, 128, 8, 128)
    wkk = np.ascontiguousarray(wkvu[..., :64]).reshape(L, 128, 512)
    wkv = np.ascontiguousarray(wkvu[..., 64:]).reshape(L, 128, 512)
    wsT = f32(inp["gmlp_w_s"]).transpose(0, 3, 1, 2).reshape(L, 128, 512)
    bsb = np.ascontiguousarray(np.broadcast_to(f32(inp["gmlp_b_s"]).reshape(L, 1, 512), (L, 128, 512)))
    maps = []
    xp = f32(inp["x_prompt"])
    xs = f32(inp["x_sample"])
    for c in range(NCORE):
        s, j = c // 4, c % 4
        dftT, rope = build_core_consts(j)
        cond = np.stack([f32(inp["c_ctx"]).reshape(8, 128).T, f32(inp["c"])[s].reshape(8, 128).T], axis=2).reshape(128, 16)
        o, _ = SP_OFF["rw"]
        om, _ = SP_OFF["mu_rkv"]
        spo = np.concatenate([sp[:, :, o:o + 36].reshape(L, 128, 9, 4)[:, :, :, j],
                              sp[:, :, om:om + 12].reshape(L, 128, 3, 4)[:, :, :, j]], axis=2)
        spo = np.ascontiguousarray(spo)
        st0 = np.stack([f32(inp["state_rwkv_fwd"])[s, :, 2 * j:2 * j + 2], f32(inp["state_rwkv_bwd"])[s, :, 2 * j:2 * j + 2]],
                       axis=1)
        st0 = st0.transpose(0, 1, 2, 4, 3).reshape(L, 2, 128, 64)
        p = np.arange(128)
        idx1 = np.zeros((128, 12), np.int32)
        for q in range(4):
            idx1[:, 0 * 4 + q] = q * 1024 + 128 * j + p
            idx1[:, 1 * 4 + q] = q * 1024 + 512 + 128 * j + p
            idx1[:, 2 * 4 + q] = q * 864 + 128 * j + p
        idx2 = np.zeros((128, 4), np.int32)
        for r in range(4):
            idx2[:, r] = (r * 4 + j) * 128 + p
        m = dict(
            xp=np.ascontiguousarray(xp[4 * c:4 * c + 4].reshape(NPT, D).T),
            xs=np.ascontiguousarray(xs[s, 512 * j:512 * j + 512].T),
            wst=wst, sp=sp, cond=np.ascontiguousarray(cond),
            ident=cst["ident"], bones=cst["bones"], mask=cst["mask"],
            dftd_p=cst["dftd_p"], dftd_s=cst["dftd_s"], dftT_p=cst["dftT_p"],
            dftT_s=np.ascontiguousarray(dftT.reshape(16, 128, 1024)), rope=np.ascontiguousarray(rope.reshape(32, 1024)),
            wup=wup, aup=aup,
            wupo=np.ascontiguousarray(wup.reshape(L, 64, 2, 4, 128)[:, :, :, j]).reshape(L, 64, 256),
            aupo=np.ascontiguousarray(aup.reshape(L, 64, 2, 4, 128)[:, :, :, j]).reshape(L, 64, 256),
            spo=spo, wq=wq, wqs=wqs, wkk=wkk, wkv=wkv, wsT=wsT, bsb=bsb,
            st0=np.ascontiguousarray(st0),
            cckv=np.ascontiguousarray(f32(inp["cache_mla_ckv"])[s].transpose(0, 2, 1)),
            ckr=np.ascontiguousarray(f32(inp["cache_mla_krope"])[s].transpose(0, 2, 1)),
            idx1=idx1, idx2=idx2,
        )
        maps.append(m)
    return maps


def assemble(results):
    B = 32
    yp = np.zeros((B, SEQ, D), np.float32)
    ys = np.zeros((2, DSEQ, D), np.float32)
    sf = np.zeros((B, DEPTH, 8, 64, 64), np.float32)
    sbw = np.zeros((B, DEPTH, 8, 64, 64), np.float32)
    ckv = np.zeros((B, DEPTH, SEQ, 128), np.float32)
    kr = np.zeros((B, DEPTH, SEQ, 32), np.float32)
    for c in range(NCORE):
        r = results[c]
        s, j = c // 4, c % 4
        yp[4 * c:4 * c + 4] = np.asarray(r["yp"]).T.reshape(4, SEQ, D)
        ys[s, 512 * j:512 * j + 512] = np.asarray(r["ys"]).T
        st = np.asarray(r["stout"]).reshape(DEPTH, 2, 4, 4, 2, 64, 64)
        st = st.transpose(1, 2, 0, 3, 4, 6, 5).reshape(2, 4, DEPTH, 8, 64, 64)
        sf[4 * c:4 * c + 4] = st[0]
        sbw[4 * c:4 * c + 4] = st[1]
        ck = np.asarray(r["ckvout"]).reshape(DEPTH, 128, 4, SEQ)
        ckv[4 * c:4 * c + 4] = ck.transpose(2, 0, 3, 1)
        k2 = np.asarray(r["krout"]).reshape(DEPTH, 32, 4, SEQ)
        kr[4 * c:4 * c + 4] = k2.transpose(2, 0, 3, 1)
    return yp, ys, sf, sbw, ckv, kr


def kernel(**inputs):
    prog = Prog(dict(_CFG))
    nc = prog.build()
    maps = make_in_maps(inputs)
    res = run_bass_kernel_spmd(nc, maps, core_ids=list(range(NCORE)))
    return assemble(res.results)
```

```python
import numpy as np
import ml_dtypes
import concourse.bass as bass
import concourse.mybir as mybir
from concourse.bass_utils import run_bass_kernel_spmd

F32 = mybir.dt.float32
BF16 = mybir.dt.bfloat16
I32 = mybir.dt.int32
ALU = mybir.AluOpType
AF = mybir.ActivationFunctionType
AX = mybir.AxisListType

D = 1024
DEPTH = 2
SEQ = 256
DSEQ = 2048
PAST = 512
NCORE = 8
NPT = 1024
NST = 512
NTOK = NPT + NST
EPS = 1e-6
GN_EPS = 64e-5
EP = 24000
AGP = {"rk": 1024, "vx": 864, "f": 512}
VX_OFF = dict(v=0, lora=512, ckv=704, kr=832)


class Buf:
    def __init__(self, h, name, space):
        self.h = h
        self.name = name
        self.space = space
        self.w = {}
        self.r = {}
        self.dsem = None
        self.dcnt = 0
        self.dsid = None
        self.dcls = None

    def __getitem__(self, idx):
        return self.h[idx]

    def ap(self):
        return self.h.ap() if self.space == "dram" else self.h[:]


class FW:
    def __init__(self, nc):
        self.nc = nc
        self.E = {"pe": nc.tensor, "act": nc.scalar, "dve": nc.vector, "pool": nc.gpsimd, "sp": nc.sync}
        self.cnt = {e: 0 for e in self.E}
        self.esem = {e: [] for e in self.E}
        self.waited = {e: {} for e in self.E}
        self.pend = {e: [] for e in self.E}
        self.nbuf = 0
        self.ninst = 0
        self.dma_out = {}
        self.sem_pool = []
        self.stack = []
        self.rots = {}
        self.free_sems = {"pool": [], "hw": []}

    def sb(self, shape, dt, name=None):
        self.nbuf += 1
        name = name or "t"
        g = self.nc.sbuf_tensor(f"{name}_{self.nbuf}", list(shape), dt)
        h = g.__enter__()
        b = Buf(h, name, "sbuf")
        if self.stack:
            self.stack[-1].append((g, b))
        return b

    def ps(self, shape, dt=F32, name=None):
        self.nbuf += 1
        name = name or "p"
        h = self.nc.alloc_psum_tensor(f"{name}_{self.nbuf}", list(shape), dt)
        return Buf(h, name, "psum")

    def dram(self, name, shape, dt, kind=None):
        if kind is None:
            h = self.nc.dram_tensor(name, list(shape), dt)
        else:
            h = self.nc.dram_tensor(name, list(shape), dt, kind=kind)
        return Buf(h, name, "dram")

    def rot(self, shape, dt, name, n=2):
        key = (len(self.stack), name)
        if key not in self.rots:
            self.rots[key] = [[self.sb(shape, dt, name) for _ in range(n)], 0]
        ent = self.rots[key]
        t = ent[0][ent[1] % n]
        ent[1] += 1
        return t

    def push(self):
        self.stack.append([])

    def pop(self):
        self.barrier()
        depth = len(self.stack)
        for k in [k for k in self.rots if k[0] == depth]:
            del self.rots[k]
        for g, b in reversed(self.stack.pop()):
            if b.dsem is not None:
                self.free_sems[b.dcls].append((b.dsem, b.dcnt))
                b.dsem = None
            g.__exit__(None, None, None)

    def _sem_for(self, eng, k):
        i = (k - 1) // EP
        while len(self.esem[eng]) <= i:
            self.esem[eng].append(self.nc.alloc_semaphore(f"s_{eng}_{len(self.esem[eng])}"))
        return self.esem[eng][i], (k - 1) % EP + 1

    def _wait(self, eng, ev):
        if ev is None:
            return
        if ev[0] == "eng":
            _, e2, k = ev
            if e2 == eng and eng == "pe":
                return
            key = ("eng", e2, (k - 1) // EP)
            sem, val = self._sem_for(e2, k)
        else:
            _, sem, val, sid = ev
            key = ("sem", sid)
        if self.waited[eng].get(key, 0) >= val:
            return
        self.waited[eng][key] = val
        self.E[eng].wait_ge(sem, val)

    def _check_pend(self, eng, b):
        for e2, lst in self.pend.items():
            if e2 == eng:
                continue
            for (pb, _) in lst:
                if pb is b:
                    raise RuntimeError(f"buffer {b.name} has pending unsignalled access on {e2}, touched by {eng}")

    def _deps(self, eng, reads, writes):
        evs = []
        for b in reads:
            self._check_pend(eng, b)
            evs.extend(b.w.values())
        for b in writes:
            self._check_pend(eng, b)
            for wv in b.w.values():
                if not (wv[0] == "eng" and wv[1] == eng):
                    evs.append(wv)
            for ev in b.r.values():
                if ev[0] == "eng" and ev[1] == eng and eng == "pe":
                    continue
                evs.append(ev)
        for ev in evs:
            self._wait(eng, ev)

    def op(self, eng, fn, reads=(), writes=(), signal=True):
        self._deps(eng, reads, writes)
        ins = fn()
        self.ninst += 1
        if signal:
            self.cnt[eng] += 1
            k = self.cnt[eng]
            sem, val = self._sem_for(eng, k)
            ins.then_inc(sem, 1)
            ev = ("eng", eng, k)
            for (pb, kind) in self.pend[eng]:
                if kind == "r":
                    pb.r[eng] = ev
                else:
                    pb.w = {eng: ev}
                    pb.r = {}
            self.pend[eng] = []
            for b in reads:
                b.r[eng] = ev
            for b in writes:
                b.w = {eng: ev}
                b.r = {}
        else:
            for b in reads:
                self.pend[eng].append((b, "r"))
            for b in writes:
                self.pend[eng].append((b, "w"))
        return ins

    def _dma_sem(self, b, q="sp"):
        cls = "pool" if q == "pool" else "hw"
        if b.dsem is None:
            self.nbuf += 1
            if self.free_sems[cls]:
                b.dsem, b.dcnt = self.free_sems[cls].pop()
            else:
                b.dsem = self.nc.alloc_semaphore(f"d_{b.name}_{self.nbuf}")
            b.dsid = self.nbuf
            b.dcls = cls
        elif cls == "pool" and b.dcls != "pool":
            raise RuntimeError(f"buffer {b.name}: software DMA on a semaphore first used by a hardware-DGE DMA")
        return b.dsem

    def dma(self, q, out_b, out_ap, in_b, in_ap, sem_owner=None, inc=16, fn=None, extra_reads=(), **kw):
        eng = q
        evs = []
        for b in (in_b, out_b) + tuple(extra_reads):
            self._check_pend(eng, b)
        evs.extend(in_b.w.values())
        for b in extra_reads:
            evs.extend(b.w.values())
        owner = sem_owner or (out_b if out_b.space != "dram" else in_b)
        sem = self._dma_sem(owner, q)
        for wv in out_b.w.values():
            if wv[0] != "sem":
                evs.append(wv)
        for ev in out_b.r.values():
            evs.append(ev)
        for ev in evs:
            self._wait(eng, ev)
        if fn is None:
            ins = self.E[eng].dma_start(out=out_ap, in_=in_ap, **kw)
        else:
            ins = fn()
        self.ninst += 1
        owner.dcnt += inc
        ins.then_inc(sem, inc)
        ev = ("sem", sem, owner.dcnt, owner.dsid)
        in_b.r[("dma", owner.dsid)] = ev
        for b in extra_reads:
            b.r[("dma", owner.dsid)] = ev
        out_b.w = {k: v for k, v in out_b.w.items() if v[0] == "sem"}
        out_b.w[("dma", owner.dsid)] = ev
        out_b.r = {}
        self.dma_out[owner.dsid] = ev
        return ev

    def wait_buf(self, eng, b):
        self._check_pend(eng, b)
        for ev in b.w.values():
            self._wait(eng, ev)
        for ev in b.r.values():
            self._wait(eng, ev)

    def barrier(self):
        for e in self.E:
            if self.pend[e]:
                raise RuntimeError(f"barrier with pending unsignalled ops on {e}")
        last = []
        for e in ("pe", "act", "dve", "pool"):
            if self.cnt[e] > 0:
                last.append(("eng", e, self.cnt[e]))
        for e in self.E:
            for ev in last:
                if ev[1] == e and e == "pe":
                    continue
                self._wait(e, ev)
            for ev in self.dma_out.values():
                self._wait(e, ev)
        self.dma_out = {}


COLS = dict(r=(0, 512), k=(512, 1024), v=(1024, 1536), wdf=(1536, 1600), wdb=(1600, 1664), ad=(1664, 1728),
            ga=(1728, 2240), qd=(2240, 2496), kvd=(2496, 2624), kr=(2624, 2656), gb=(2656, 3168),
            u=(3168, 3680), vc=(3680, 4192), gc=(4192, 4704), f=(4704, 5216), gd=(5216, 5728))


def _rng(name):
    a, b = COLS[name]
    return np.arange(a, b)


_SWAP32 = np.arange(32).reshape(16, 2)[:, ::-1].reshape(32)

WIN_BLOCKS = [
    ("A0", [_rng("r")]), ("A1", [_rng("k")]), ("A2", [_rng("v")]),
    ("A3", [_rng("wdf"), _rng("wdb"), _rng("ad")]), ("A4", [_rng("ga")]),
    ("B0", [_rng("qd"), _rng("kvd"), _rng("kr"), _rng("kr")[_SWAP32]]), ("B1", [_rng("gb")]),
    ("C0", [_rng("u")]), ("C1", [_rng("vc")]), ("C2", [_rng("gc")]),
    ("D0", [_rng("f")]), ("D1", [_rng("gd")]),
]
WIN_W = {n: int(sum(len(c) for c in cols)) for n, cols in WIN_BLOCKS}
WIN_IDX = {n: 6 + i for i, (n, _) in enumerate(WIN_BLOCKS)}
BLK_PER_LAYER = 36
FBLK = 4096


def _kcp(w, W):
    return w.reshape(8, 128, W).transpose(1, 0, 2).reshape(128, 8 * W)


def build_stream(w_ada, w_in, w_branch, w_merge, w_out):
    st = np.zeros((DEPTH * BLK_PER_LAYER, 128, FBLK), np.float32)
    for l in range(DEPTH):
        base = l * BLK_PER_LAYER
        for b in range(6):
            st[base + b, :, :] = _kcp(w_ada[l][:, 512 * b:512 * b + 512], 512)
        for i, (n, cols) in enumerate(WIN_BLOCKS):
            cc = np.concatenate(cols)
            W = len(cc)
            st[base + 6 + i, :, :8 * W] = _kcp(w_in[l][:, cc], W)
        for d in range(8):
            cc = np.concatenate([n * 1024 + d * 128 + np.arange(128) for n in range(4)])
            st[base + 18 + 2 * d, :, :] = _kcp(w_merge[l][:, cc], 512)
            wb = w_branch[l][:, :, d * 128:(d + 1) * 128]
            wb = wb.reshape(4, 4, 128, 128).transpose(2, 0, 1, 3)
            st[base + 19 + 2 * d, :, :2048] = wb.reshape(128, 2048)
        for b in range(2):
            st[base + 34 + b, :, :] = _kcp(w_out[l][:, 512 * b:512 * b + 512], 512)
    return st


SP_OFF = {}
_o = 0
for _n, _w in [("norm_g", 8), ("b_ada", 24), ("mu_rkv", 12), ("mu_lora", 3), ("b_merge", 32), ("rw", 36),
               ("qn", 2), ("kvn", 1), ("gln_g", 4), ("gln_b", 4), ("fin_g", 8)]:
    SP_OFF[_n] = (_o, _w)
    _o += _w
NSP = _o
RW_NAMES = ["w0_f", "w0_b", "a0_f", "a0_b", "k_k", "k_a", "r_k", "ln_g", "ln_b"]


def _pc(v, n):
    return np.asarray(v, np.float32).reshape(n, 128).T


def build_small(inp):
    sp = np.zeros((DEPTH, 128, NSP), np.float32)
    for l in range(DEPTH):
        def put(name, arr):
            o, w = SP_OFF[name]
            sp[l, :, o:o + w] = arr
        put("norm_g", _pc(inp["norm_g"][l], 8))
        put("b_ada", _pc(inp["b_ada"][l], 24))
        put("mu_rkv", _pc(inp["shift_mu"][l][:1536], 12))
        ml = np.zeros((128, 3), np.float32)
        ml[:64, :] = inp["shift_mu"][l][1536:1728].reshape(3, 64).T
        put("mu_lora", ml)
        bm = inp["b_merge"][l].reshape(4, 8, 128)
        put("b_merge", bm.transpose(2, 1, 0).reshape(128, 32))
        rwv = [inp["rwkv_w0"][l][0], inp["rwkv_w0"][l][1], inp["rwkv_a0"][l][0], inp["rwkv_a0"][l][1],
               inp["rwkv_k_k"][l], inp["rwkv_k_a"][l], inp["rwkv_r_k"][l].reshape(512), inp["rwkv_ln_g"][l],
               inp["rwkv_ln_b"][l]]
        rw = np.stack([_pc(v, 4) for v in rwv], axis=1)
        put("rw", rw.reshape(128, 36))
        put("qn", _pc(inp["mla_q_norm"][l], 2))
        put("kvn", _pc(inp["mla_kv_norm"][l], 1))
        put("gln_g", _pc(inp["gmlp_ln_g"][l], 4))
        put("gln_b", _pc(inp["gmlp_ln_b"][l], 4))
        put("fin_g", _pc(inp["final_norm_g"], 8))
    return sp


def rw_col(name, pair):
    o, _ = SP_OFF["rw"]
    return o + RW_NAMES.index(name) * 4 + pair


def build_consts():
    c = {}
    c["ident"] = np.eye(128, dtype=np.float32)
    hb = np.arange(128) // 64
    c["bones"] = (hb[:, None] == hb[None, :]).astype(np.float32)
    p = np.arange(128)[:, None]
    f = np.arange(128)[None, :]
    mk = {}
    bd32 = ((p // 32) == (f // 32)).astype(np.float32)
    od64 = (((p // 64) == (f // 64)) & ((p // 32) != (f // 32))).astype(np.float32)
    od128 = ((p // 64) != (f // 64)).astype(np.float32)
    for dname, ms, mi, mt in (("f", p < f, p <= f, f < p), ("b", p > f, p >= f, f > p)):
        ms = ms.astype(np.float32)
        mi = mi.astype(np.float32)
        mtf = -(mt.astype(np.float32))
        mk[dname] = np.concatenate([ms, mi, -ms * bd32, mi, mtf * bd32, mtf * bd32, mtf * od64, mtf * od64,
                                    mtf * od128, mtf * od128], axis=1)
    c["mask"] = np.stack([mk["f"], mk["b"]], axis=1).reshape(128, 2 * 1280)
    dd = np.arange(128)
    ang = 2 * np.pi * np.outer(dd, dd) / 128.0
    for nm, T in (("dftd_p", SEQ), ("dftd_s", DSEQ)):
        sc = 1.0 / np.sqrt(T * 128.0)
        c[nm] = np.concatenate([np.cos(ang) * sc, -np.sin(ang) * sc], axis=1).astype(np.float32)
    tt = np.arange(SEQ)
    angp = 2 * np.pi * np.outer(tt, tt) / SEQ
    cp = np.stack([np.cos(angp), np.sin(angp)], axis=1)
    c["dftT_p"] = cp.reshape(2, 128, 2, SEQ).transpose(1, 0, 2, 3).reshape(128, 2 * 2 * SEQ).astype(np.float32)
    return c


def build_core_consts(j):
    t = np.arange(DSEQ)
    k1 = 512 * j + np.arange(512)
    ang = 2 * np.pi * ((np.outer(t, k1)) % DSEQ) / DSEQ
    cs = np.stack([np.cos(ang), np.sin(ang)], axis=1)
    dftT = cs.reshape(16, 128, 2, 512).astype(np.float32)
    pos = 512 * j + np.arange(512)
    row = (pos // 64).astype(np.float32)
    col = (pos % 64).astype(np.float32)
    inv = (10000.0 ** (-np.arange(8, dtype=np.float32) / 8)).astype(np.float32)
    ang = np.concatenate([row[:, None] * inv, col[:, None] * inv], axis=-1).astype(np.float32)
    cos = np.cos(ang).astype(np.float32)
    sin = np.sin(ang).astype(np.float32)
    COS = np.repeat(cos, 2, axis=1).T
    SIN = np.stack([-sin, sin], axis=2).reshape(512, 32).T
    rope = np.stack([COS, SIN], axis=1).astype(np.float32)
    return dftT, rope


class Prog:
    def __init__(self, cfg):
        self.cfg = cfg
        self.branches = cfg.get("branches", "ABCD")
        self.depth = cfg.get("depth", DEPTH)
        nc = bass.Bass("TRN2", target_bir_lowering=False)
        self.nc = nc
        fw = FW(nc)
        self.fw = fw
        self.V, self.A, self.G, self.T = nc.vector, nc.scalar, nc.gpsimd, nc.tensor
        di = lambda n, s, dt=F32: fw.dram(n, s, dt, kind="ExternalInput")
        do = lambda n, s, dt=F32: fw.dram(n, s, dt, kind="ExternalOutput")
        self.d = dict(
            xp=di("xp", [D, NPT]), xs=di("xs", [D, NST]),
            wst=di("wst", [DEPTH * BLK_PER_LAYER, 128, FBLK]),
            sp=di("sp", [DEPTH, 128, NSP]), cond=di("cond", [128, 16]),
            ident=di("ident", [128, 128]), bones=di("bones", [128, 128]), mask=di("mask", [128, 2560]),
            dftd_p=di("dftd_p", [128, 256]), dftd_s=di("dftd_s", [128, 256]), dftT_p=di("dftT_p", [128, 1024]),
            dftT_s=di("dftT_s", [16, 128, 1024]), rope=di("rope", [32, 1024]),
            wup=di("wup", [DEPTH, 64, 1024]), aup=di("aup", [DEPTH, 64, 1024]),
            wupo=di("wupo", [DEPTH, 64, 256]), aupo=di("aupo", [DEPTH, 64, 256]),
            spo=di("spo", [DEPTH, 128, 12]),
            wq=di("wq", [DEPTH, 128, 2 * 8 * 96]), wqs=di("wqs", [DEPTH, 128, 2 * 8 * 32]),
            wkk=di("wkk", [DEPTH, 128, 512]), wkv=di("wkv", [DEPTH, 128, 512]),
            wsT=di("wsT", [DEPTH, 128, 512]), bsb=di("bsb", [DEPTH, 128, 512]),
            st0=di("st0", [DEPTH, 2, 128, 64]), cckv=di("cckv", [DEPTH, 128, PAST]),
            ckr=di("ckr", [DEPTH, 32, PAST]),
            idx1=di("idx1", [128, 12], I32), idx2=di("idx2", [128, 4], I32),
            yp=do("yp", [D, NPT]), ys=do("ys", [D, NST]),
            stout=do("stout", [DEPTH * 2 * 4 * 4 * 128, 64]),
            ckvout=do("ckvout", [DEPTH, 128, NPT]), krout=do("krout", [DEPTH, 32, NPT]),
        )
        self.ag1_in = [{k: fw.dram(f"ag1i{k}{l}", [n, 512], BF16) for k, n in AGP.items()} for l in range(DEPTH)]
        self.ag1_out = [{k: fw.dram(f"ag1o{k}{l}", [4 * n, 512], BF16) for k, n in AGP.items()} for l in range(DEPTH)]
        self.ag2_in = [fw.dram(f"ag2i{l}", [512, 512], BF16) for l in range(DEPTH)]
        self.ag2_out = [fw.dram(f"ag2o{l}", [2048, 512], BF16) for l in range(DEPTH)]
        self.psb = [fw.ps([128, 512], F32, f"bank{i}") for i in range(6)]
        self.pst = [fw.ps([128, 1024], BF16, f"pst{i}") for i in range(2)]
        self.pst_rr = 0
        self.ps_rr = 0
        self.xT = fw.sb([128, 8, NTOK], F32, "xT")
        self.hTs = [fw.sb([128, 8, 512], BF16, "hTa"), None]
        self.oTs = [[fw.sb([128, 4, 512], BF16, f"oTa{n}") for n in range(4)], None]
        self.slots = None
        self.slot_rr = 0
        self.plan = []
        self.plan_pos = 0
        self.stg = None
        self.blocks_left = 0
        self.dma_issued = {}
        self.load_consts()

    ps_range = (0, 4)

    def nps(self, lo=None, hi=None):
        lo = self.ps_range[0] if lo is None else lo
        hi = self.ps_range[1] if hi is None else hi
        n = hi - lo
        b = self.psb[lo + (self.ps_rr % n)]
        self.ps_rr += 1
        return b

    def dve(self, fn, r, w):
        return self.fw.op("dve", fn, r, w)

    def rsqrt(self, out_buf, out_ap, in_buf, in_ap):
        self.act(lambda: self.A.activation(in_ap, in_ap, AF.Sqrt), [in_buf], [in_buf])
        self.dve(lambda: self.V.reciprocal(out_ap, in_ap), [in_buf], [out_buf])

    def act(self, fn, r, w):
        return self.fw.op("act", fn, r, w)

    def pool(self, fn, r, w):
        return self.fw.op("pool", fn, r, w)

    def pe(self, fn, r, w, signal=True):
        return self.fw.op("pe", fn, r, w, signal=signal)

    def load(self, dst, dst_ap, src, src_ap, q="sp"):
        if dst_ap.dtype != src_ap.dtype:
            q = "pool"
        return self.fw.dma(q, dst, dst_ap, src, src_ap)

    def load_consts(self):
        fw, d = self.fw, self.d
        self.ident = fw.sb([128, 128], BF16, "ident")
        self.bones = fw.sb([128, 128], BF16, "bones")
        self.mask = fw.sb([128, 2560], BF16, "mask")
        self.ones = fw.sb([128, 128], BF16, "ones")
        self.onesf = fw.sb([128, 128], F32, "onesf")
        self.rope = None
        self.cond = fw.sb([128, 16], F32, "cond")
        self.idx1 = fw.sb([128, 12], I32, "idx1")
        self.idx2 = fw.sb([128, 4], I32, "idx2")
        for nm in ("ident", "bones", "mask", "cond", "idx1", "idx2"):
            t = getattr(self, nm)
            self.load(t, t[:], d[nm], d[nm].ap())
        self.pool(lambda: self.G.memset(self.ones[:], 1.0), [], [self.ones])
        self.pool(lambda: self.G.memset(self.onesf[:], 1.0), [], [self.onesf])
        self.hm = fw.sb([128, 2], F32, "hm")
        self.dve(lambda: self.V.tensor_copy(self.hm[:, 0:2], self.bones[:, 0:128:64]), [self.bones], [self.hm])
        xv = self.xT
        self.load(xv, xv[:, :, 0:NPT], d["xp"], d["xp"].ap().rearrange("(k p) t -> p k t", p=128))
        self.load(xv, xv[:, :, NPT:NTOK], d["xs"], d["xs"].ap().rearrange("(k p) t -> p k t", p=128))

    def stream_plan(self, ids):
        self.plan.extend(ids)

    def stream_begin(self, nblocks, depth=1):
        fw = self.fw
        self.sdepth = depth
        self.slots = [fw.sb([128, FBLK], BF16, f"slot{i}") for i in range(depth + 1)]
        self.stg = [fw.sb([128, FBLK // 2], F32, f"wstg{i}") for i in range(2 * depth)]
        self.blocks_left = nblocks
        self.scope_end = self.plan_pos + nblocks
        self.dma_issued = {}

    def _issue_dma(self, pos):
        blk = self.plan[pos]
        src = self.d["wst"]
        for hf in range(2):
            st = self.stg[(2 * pos + hf) % len(self.stg)]
            self.fw.dma("sp", st, st[:, :], src, src[blk, :, hf * (FBLK // 2):(hf + 1) * (FBLK // 2)])
        self.dma_issued[pos] = True

    def next_block(self, blk):
        pos = self.plan_pos
        assert self.plan[pos] == blk, (pos, self.plan[pos], blk)
        assert self.blocks_left > 0
        if pos not in self.dma_issued:
            self._issue_dma(pos)
        slot = self.slots[pos % len(self.slots)]
        for hf in range(2):
            st = self.stg[(2 * pos + hf) % len(self.stg)]
            if hf == 0:
                self.dve(lambda hf=hf, st=st: self.V.tensor_copy(slot[:, hf * (FBLK // 2):(hf + 1) * (FBLK // 2)], st[:, :]), [st], [slot])
            else:
                self.act(lambda hf=hf, st=st: self.A.copy(slot[:, hf * (FBLK // 2):(hf + 1) * (FBLK // 2)], st[:, :]), [st], [slot])
        self.plan_pos += 1
        self.blocks_left -= 1
        return slot

    def prefetch_next(self):
        for pos in range(self.plan_pos, min(self.plan_pos + self.sdepth, self.scope_end)):
            if pos not in self.dma_issued:
                self._issue_dma(pos)

    def load_layer_small(self, l):
        fw, d = self.fw, self.d
        self.sp = fw.sb([128, NSP], F32, "sp")
        self.load(self.sp, self.sp[:], d["sp"], d["sp"][l, :, :])
        sp = self.sp
        o, _ = SP_OFF["mu_rkv"]
        self.mu1 = fw.sb([128, 15], F32, "mu1")
        self.muh = fw.sb([128, 15], F32, "muh")
        self.dve(lambda: self.V.tensor_scalar(self.mu1[:], sp[:, o:o + 15], -1.0, 1.0, ALU.mult, ALU.add), [sp], [self.mu1])
        self.dve(lambda: self.V.tensor_scalar(self.muh[:], sp[:, o:o + 15], 0.5, None, ALU.mult), [sp], [self.muh])
        o2, _ = SP_OFF["rw"]
        self.rwh = fw.sb([128, 16], F32, "rwh")
        self.dve(lambda: self.V.tensor_scalar(self.rwh[:], sp[:, o2:o2 + 16], 0.5, None, ALU.mult), [sp], [self.rwh])
        ob, _ = SP_OFF["b_merge"]
        self.bmh = fw.sb([128, 32], F32, "bmh")
        self.dve(lambda: self.V.tensor_scalar(self.bmh[:], sp[:, ob:ob + 32], 0.5, None, ALU.mult), [sp], [self.bmh])

    def spc(self, name, i=0, n=1):
        o, _ = SP_OFF[name]
        return self.sp[:, o + i:o + i + n]

    def ada(self, l):
        fw = self.fw
        sc = fw.sb([128, 16], BF16, "scond")
        th = fw.sb([128, 16], F32, "cth")
        c = self.cond
        self.act(lambda: self.A.activation(th[:], c[:], AF.Tanh, scale=0.5), [c], [th])
        t2 = fw.sb([128, 16], F32, "ct2")
        self.dve(lambda: self.V.scalar_tensor_tensor(t2[:], th[:], 1.0, c[:], ALU.add, ALU.mult), [th, c], [t2])
        self.dve(lambda: self.V.tensor_scalar(sc[:], t2[:], 0.5, None, ALU.mult), [t2], [sc])
        ps = self.psb[5]
        self.mod = fw.sb([128, 24, 2], F32, "mod")
        self.gmod = fw.sb([128, 8, 2], F32, "gmod")
        fw.push()
        self.stream_begin(6, depth=2)
        for b in range(6):
            slot = self.next_block(l * BLK_PER_LAYER + b)
            for n in range(4):
                m = b * 4 + n
                for kc in range(8):
                    self.pe(lambda kc=kc, n=n, m=m, slot=slot: self.T.matmul(
                        ps[:, 2 * m:2 * m + 2], slot[:, kc * 512 + n * 128:kc * 512 + n * 128 + 128],
                        sc[:, 2 * kc:2 * kc + 2], start=(kc == 0), stop=(kc == 7)),
                        [slot, sc], [ps], signal=(kc == 7 and n == 3))
            self.prefetch_next()
        fw.pop()
        ob, _ = SP_OFF["b_ada"]
        for cc in range(2):
            self.dve(lambda cc=cc: self.V.tensor_tensor(self.mod[:, :, cc], ps[:, cc:48:2], self.sp[:, ob:ob + 24], ALU.add),
                     [ps, self.sp], [self.mod])
        og, _ = SP_OFF["norm_g"]
        for cc in range(2):
            self.dve(lambda cc=cc: self.V.scalar_tensor_tensor(self.gmod[:, :, cc], self.mod[:, 8:16, cc], 1.0,
                                                               self.sp[:, og:og + 8], ALU.add, ALU.mult),
                     [self.mod, self.sp], [self.gmod])

    def rstd_tile(self, src, views, nfeat, out, N):
        fw = self.fw
        ps = self.nps()
        nk = len(views)
        for i, v in enumerate(views):
            sq = fw.rot([128, 512], BF16, "sq")
            self.act(lambda v=v, sq=sq: self.A.activation(sq[:, :N], v, AF.Square), [src], [sq])
            self.pe(lambda i=i, sq=sq: self.T.matmul(ps[:, :N], self.ones[:], sq[:, :N], start=(i == 0), stop=(i == nk - 1)),
                    [self.ones, sq], [ps], signal=True)
        t = fw.sb([128, 512], F32, "rs_t")
        self.dve(lambda: self.V.tensor_scalar(t[:, :N], ps[:, :N], 1.0 / nfeat, EPS, ALU.mult, ALU.add), [ps], [t])
        self.rsqrt(out, out[:, :N], t, t[:, :N])

    def make_h(self, cc, x0, h0, N):
        fw = self.fw
        fw.push()
        rstd = fw.sb([128, 512], F32, "rstd")
        self.rstd_tile(self.xT, [self.xT[:, kc, x0:x0 + N] for kc in range(8)], float(D), rstd, N)
        for kc in range(8):
            tmp = fw.rot([128, 512], F32, "htmp")
            self.dve(lambda kc=kc, tmp=tmp: self.V.scalar_tensor_tensor(
                tmp[:, :N], self.xT[:, kc, x0:x0 + N], self.gmod[:, kc, cc:cc + 1], rstd[:, :N], ALU.mult, ALU.mult),
                [self.xT, self.gmod, rstd], [tmp])
            self.act(lambda kc=kc, tmp=tmp: self.A.activation(
                self.hTs[h0 // 512][:, kc, 0:N], tmp[:, :N], AF.Identity, bias=self.mod[:, kc, cc:cc + 1], scale=1.0),
                [tmp, self.mod], [self.hTs[h0 // 512]])
        fw.pop()

    def zmm(self, slot, W, c0, w, h0, N, ps=None, prow=0):
        ps = ps or self.nps()
        for kc in range(8):
            self.pe(lambda kc=kc: self.T.matmul(ps[prow:prow + w, :N], slot[:, kc * W + c0:kc * W + c0 + w],
                                                self.hTs[h0 // 512][:, kc, 0:N], start=(kc == 0), stop=(kc == 7)),
                    [slot, self.hTs[h0 // 512]], [ps], signal=(kc == 7))
        return ps

    def silu2(self, ps, rows, N, out_ap, out_buf):
        fw = self.fw
        th = fw.rot([128, 512], F32, "s2th")
        self.act(lambda: self.A.activation(th[:rows, :N], ps[:rows, :N], AF.Tanh, scale=0.5), [ps], [th])
        self.dve(lambda: self.V.scalar_tensor_tensor(out_ap, th[:rows, :N], 1.0, ps[:rows, :N], ALU.add, ALU.mult),
                 [th, ps], [out_buf])

    def gelu2(self, ps, rows, N, out_ap, out_buf):
        fw = self.fw
        u = fw.rot([128, 512], F32, "g2u")
        self.act(lambda: self.A.activation(u[:rows, :N], ps[:rows, :N], AF.Square), [ps], [u])
        self.dve(lambda: self.V.tensor_scalar(u[:rows, :N], u[:rows, :N], 0.044715, 1.0, ALU.mult, ALU.add), [u], [u])
        self.dve(lambda: self.V.tensor_tensor(u[:rows, :N], u[:rows, :N], ps[:rows, :N], ALU.mult), [u, ps], [u])
        self.act(lambda: self.A.activation(u[:rows, :N], u[:rows, :N], AF.Tanh, scale=0.7978845608028654), [u], [u])
        self.dve(lambda: self.V.scalar_tensor_tensor(out_ap, u[:rows, :N], 1.0, ps[:rows, :N], ALU.add, ALU.mult),
                 [u, ps], [out_buf])

    def transpose_to(self, src_buf, src_ap, dst_buf, dst_ap, rows=128, cols=128, eng="act"):
        pt = self.pst[self.pst_rr % 2]
        self.pst_rr += 1
        self.pe(lambda: self.T.transpose(pt[:cols, :rows], src_ap, self.ident[:rows, :rows]), [src_buf, self.ident], [pt])
        if eng == "act":
            self.act(lambda: self.A.copy(dst_ap, pt[:cols, :rows]), [pt], [dst_buf])
        else:
            self.dve(lambda: self.V.tensor_copy(dst_ap, pt[:cols, :rows]), [pt], [dst_buf])

    def phaseC(self, l, h0, N, o0):
        fw = self.fw
        base = l * BLK_PER_LAYER
        fw.push()
        self.stream_begin(3, depth=2)
        U2 = fw.sb([128, 4, 512], BF16, "U2")
        GV = fw.sb([128, 4, 512], BF16, "GV")
        GC2 = fw.sb([128, 4, 512], BF16, "GC2")
        wsT = fw.sb([128, 512], BF16, "wsT")
        bsb = fw.sb([128, 512], F32, "bsb")
        self.load(wsT, wsT[:], self.d["wsT"], self.d["wsT"][l, :, :])
        self.load(bsb, bsb[:], self.d["bsb"], self.d["bsb"][l, :, :])
        slot = self.next_block(base + WIN_IDX["C0"])
        for c in range(4):
            ps = self.zmm(slot, 512, c * 128, 128, h0, N)
            self.gelu2(ps, 128, N, U2[:, c, :N], U2)
        self.prefetch_next()
        slot = self.next_block(base + WIN_IDX["C1"])
        for c in range(4):
            ps = self.zmm(slot, 512, c * 128, 128, h0, N)
            self.gelu2(ps, 128, N, GV[:, c, :N], GV)
        self.prefetch_next()
        slot = self.next_block(base + WIN_IDX["C2"])
        for c in range(4):
            ps = self.zmm(slot, 512, c * 128, 128, h0, N)
            self.silu2(ps, 128, N, GC2[:, c, :N], GC2)
        self.prefetch_next()
        psm = self.nps()
        psq = self.nps()
        for c in range(4):
            self.pe(lambda c=c: self.T.matmul(psm[:, :N], self.ones[:], GV[:, c, :N], start=(c == 0), stop=(c == 3)),
                    [self.ones, GV], [psm], signal=(c == 3))
        for c in range(4):
            sq = fw.rot([128, 512], BF16, "gsq")
            self.act(lambda c=c, sq=sq: self.A.activation(sq[:, :N], GV[:, c, :N], AF.Square), [GV], [sq])
            self.pe(lambda c=c, sq=sq: self.T.matmul(psq[:, :N], self.ones[:], sq[:, :N], start=(c == 0), stop=(c == 3)),
                    [self.ones, sq], [psq], signal=True)
        mu = fw.sb([128, 512], F32, "gmu")
        msq = fw.sb([128, 512], F32, "gmsq")
        var = fw.sb([128, 512], F32, "gvar")
        rstd = fw.sb([128, 512], F32, "grstd")
        self.dve(lambda: self.V.tensor_scalar(mu[:, :N], psm[:, :N], 1.0 / 512, None, ALU.mult), [psm], [mu])
        self.dve(lambda: self.V.tensor_tensor(msq[:, :N], mu[:, :N], mu[:, :N], ALU.mult), [mu], [msq])
        self.dve(lambda: self.V.scalar_tensor_tensor(var[:, :N], psq[:, :N], 1.0 / 512, msq[:, :N], ALU.mult, ALU.subtract),
                 [psq, msq], [var])
        self.dve(lambda: self.V.tensor_scalar(var[:, :N], var[:, :N], 4e-5, None, ALU.add), [var], [var])
        self.rsqrt(rstd, rstd[:, :N], var, var[:, :N])
        VN = fw.sb([128, 4, 512], BF16, "VN")
        for c in range(4):
            t = fw.rot([128, 512], F32, "lnt")
            self.dve(lambda c=c, t=t: self.V.tensor_tensor(t[:, :N], GV[:, c, :N], mu[:, :N], ALU.subtract), [GV, mu], [t])
            self.dve(lambda t=t: self.V.tensor_tensor(t[:, :N], t[:, :N], rstd[:, :N], ALU.mult), [t, rstd], [t])
            self.act(lambda c=c, t=t: self.A.activation(VN[:, c, :N], t[:, :N], AF.Identity, bias=self.spc("gln_b", c),
                                                        scale=self.spc("gln_g", c)), [t, self.sp], [VN])
        nsub = N // 128
        for g in range(4):
            pmix = self.nps()
            for s in range(nsub):
                vtm = fw.rot([128, 128], BF16, "vtm")
                self.transpose_to(VN, VN[:, g, s * 128:(s + 1) * 128], vtm, vtm[:], eng=("act" if s % 2 else "dve"))
                self.pe(lambda g=g, s=s, vtm=vtm: self.T.matmul(pmix[:, s * 128:(s + 1) * 128], vtm[:],
                                                                 wsT[:, g * 128:(g + 1) * 128], start=True, stop=True),
                        [vtm, wsT], [pmix], signal=(s == nsub - 1))
            t = fw.rot([128, 512], F32, "mixt")
            for s in range(nsub):
                self.dve(lambda g=g, s=s, t=t: self.V.tensor_tensor(t[:, s * 128:(s + 1) * 128], pmix[:, s * 128:(s + 1) * 128],
                                                                     bsb[:, g * 128:(g + 1) * 128], ALU.add), [pmix, bsb], [t])
            self.dve(lambda g=g, t=t: self.V.scalar_tensor_tensor(t[:, :N], t[:, :N], 0.25, U2[:, g, :N], ALU.mult, ALU.mult),
                     [t, U2], [t])
            self.dve(lambda g=g, t=t: self.V.tensor_tensor(self.oTs[o0 // 512][2][:, g, 0:N], t[:, :N], GC2[:, g, :N], ALU.mult),
                     [t, GC2], [self.oTs[o0 // 512][2]])
        fw.pop()

    def merge_out(self, l, cc, x0, NT):
        fw = self.fw
        base = l * BLK_PER_LAYER
        ntile = NT // 512
        fw.push()
        self.stream_begin(18, depth=2)
        merged = fw.sb([128, 8, NT], BF16, "merged")
        for d in range(8):
            slotM = self.next_block(base + 18 + 2 * d)
            self.prefetch_next()
            slotB = self.next_block(base + 19 + 2 * d)
            for tt in range(ntile):
                t0 = tt * 512
                acc = fw.rot([128, 512], F32, "macc")
                for n in range(4):
                    psg = self.nps()
                    for kc in range(8):
                        self.pe(lambda kc=kc, n=n, tt=tt: self.T.matmul(psg[:, :], slotM[:, kc * 512 + n * 128:kc * 512 + n * 128 + 128],
                                                                 self.hTs[tt][:, kc, :], start=(kc == 0), stop=(kc == 7)),
                                [slotM, self.hTs[tt]], [psg], signal=(kc == 7))
                    psp = self.nps()
                    for k4 in range(4):
                        self.pe(lambda k4=k4, n=n, tt=tt: self.T.matmul(psp[:, :], slotB[:, (n * 4 + k4) * 128:(n * 4 + k4) * 128 + 128],
                                                                 self.oTs[tt][n][:, k4, :], start=(k4 == 0), stop=(k4 == 3)),
                                [slotB, self.oTs[tt][n]], [psp], signal=(k4 == 3))
                    th = fw.rot([128, 512], F32, "mth")
                    self.act(lambda n=n, th=th, psg=psg: self.A.activation(th[:], psg[:], AF.Tanh, bias=self.bmh[:, d * 4 + n:d * 4 + n + 1],
                                                                           scale=0.5), [psg, self.bmh], [th])
                    if n == 0:
                        self.dve(lambda th=th, psp=psp: self.V.scalar_tensor_tensor(acc[:], th[:], 1.0, psp[:], ALU.add, ALU.mult),
                                 [th, psp], [acc])
                    else:
                        self.dve(lambda th=th, psp=psp: self.V.scalar_tensor_tensor(th[:], th[:], 1.0, psp[:], ALU.add, ALU.mult),
                                 [th, psp], [th])
                        self.dve(lambda th=th: self.V.tensor_tensor(acc[:], acc[:], th[:], ALU.add), [acc, th], [acc])
                self.act(lambda acc=acc, t0=t0: self.A.mul(merged[:, d, t0:t0 + 512], acc[:], 0.5), [acc], [merged])
            self.prefetch_next()
        for b in range(2):
            slotO = self.next_block(base + 34 + b)
            self.prefetch_next()
            for dd in range(4):
                dch = b * 4 + dd
                for tt in range(ntile):
                    t0 = tt * 512
                    ps = self.nps()
                    for kc in range(8):
                        self.pe(lambda kc=kc, dd=dd: self.T.matmul(ps[:, :], slotO[:, kc * 512 + dd * 128:kc * 512 + dd * 128 + 128],
                                                                   merged[:, kc, t0:t0 + 512], start=(kc == 0), stop=(kc == 7)),
                                [slotO, merged], [ps], signal=(kc == 7))
                    self.dve(lambda dch=dch, t0=t0, ps=ps: self.V.scalar_tensor_tensor(
                        self.xT[:, dch, x0 + t0:x0 + t0 + 512], ps[:, :], self.mod[:, 16 + dch, cc:cc + 1],
                        self.xT[:, dch, x0 + t0:x0 + t0 + 512], ALU.mult, ALU.add), [ps, self.mod, self.xT], [self.xT])
        fw.pop()

    def final_out(self):
        fw = self.fw
        for (x0, N, dst) in ((0, 512, ("yp", 0)), (512, 512, ("yp", 512)), (NPT, 512, ("ys", 0))):
            fw.push()
            rstd = fw.sb([128, 512], F32, "frstd")
            self.rstd_tile(self.xT, [self.xT[:, kc, x0:x0 + N] for kc in range(8)], float(D), rstd, N)
            stg = fw.sb([128, 8, 512], F32, "fstg")
            for kc in range(8):
                self.dve(lambda kc=kc: self.V.scalar_tensor_tensor(stg[:, kc, :], self.xT[:, kc, x0:x0 + N], self.spc("fin_g", kc),
                                                                   rstd[:, :], ALU.mult, ALU.mult), [self.xT, self.sp, rstd], [stg])
            dt = self.d[dst[0]]
            ncol = NPT if dst[0] == "yp" else NST
            dview = dt.ap().rearrange("(k p) t -> p k t", p=128)[:, :, dst[1]:dst[1] + N]
            fw.dma("sp", dt, dview, stg, stg[:], sem_owner=self.outsem)
            fw.pop()

    def shift_evac(self, ps, rows, N, nseq, mu1, muh, out_ap, out_buf, tanh=False):
        fw = self.fw
        zt = fw.rot([128, 512], F32, "shz")
        o32 = fw.rot([128, 512], F32, "sho")
        self.act(lambda: self.A.copy(zt[:rows, :N], ps[:rows, :N]), [ps], [zt])
        self.dve(lambda: self.V.tensor_scalar(o32[:rows, :N], zt[:rows, :N], mu1, None, ALU.mult), [zt, self.mu1], [o32])
        z3 = zt[:rows, :N].rearrange("p (s t) -> p s t", s=nseq)
        o3 = o32[:rows, :N].rearrange("p (s t) -> p s t", s=nseq)
        Tq = N // nseq
        self.dve(lambda: self.V.scalar_tensor_tensor(o3[:, :, 1:Tq], z3[:, :, 0:Tq - 1], muh, o3[:, :, 1:Tq], ALU.mult, ALU.add),
                 [zt, self.muh, o32], [o32])
        self.dve(lambda: self.V.scalar_tensor_tensor(o3[:, :, 0:Tq - 1], z3[:, :, 1:Tq], muh, o3[:, :, 0:Tq - 1], ALU.mult, ALU.add),
                 [zt, self.muh, o32], [o32])
        if tanh:
            self.act(lambda: self.A.activation(out_ap, o32[:rows, :N], AF.Tanh), [o32], [out_buf])
        else:
            self.act(lambda: self.A.copy(out_ap, o32[:rows, :N]), [o32], [out_buf])

    def load_rwkv_w(self, l, own):
        fw, d = self.fw, self.d
        ncol = 256 if own else 1024
        self.wup = fw.sb([64, ncol], BF16, "wup")
        self.aup = fw.sb([64, ncol], BF16, "aup")
        sw, sa = (d["wupo"], d["aupo"]) if own else (d["wup"], d["aup"])
        self.load(self.wup, self.wup[:, :], sw, sw[l, :, :])
        self.load(self.aup, self.aup[:, :], sa, sa[l, :, :])

    def phaseA_prompt(self, l, half):
        fw = self.fw
        base = l * BLK_PER_LAYER
        h0 = half * 512
        fw.push()
        self.load_rwkv_w(l, False)
        zz = [fw.sb([128, 4, 512], BF16, nm) for nm in ("zr", "zk", "zv")]
        lo = [fw.sb([64, 512], BF16, nm) for nm in ("twdf", "twdb", "adT")]
        GA2 = fw.sb([128, 4, 512], BF16, "GA2")
        fw.push()
        self.stream_begin(5, depth=2)
        for which in range(3):
            slot = self.next_block(base + WIN_IDX[f"A{which}"])
            for c in range(4):
                ps = self.zmm(slot, 512, c * 128, 128, h0, 512)
                i = which * 4 + c
                self.shift_evac(ps, 128, 512, 2, self.mu1[:, i:i + 1], self.muh[:, i:i + 1], zz[which][:, c, :], zz[which])
            self.prefetch_next()
        slot = self.next_block(base + WIN_IDX["A3"])
        for i in range(3):
            ps = self.zmm(slot, 192, i * 64, 64, h0, 512)
            self.shift_evac(ps, 64, 512, 2, self.mu1[0:64, 12 + i:13 + i], self.muh[0:64, 12 + i:13 + i], lo[i][:, :], lo[i], tanh=(i < 2))
        self.prefetch_next()
        slot = self.next_block(base + WIN_IDX["A4"])
        for c in range(4):
            ps = self.zmm(slot, 512, c * 128, 128, h0, 512)
            self.silu2(ps, 128, 512, GA2[:, c, :], GA2)
        self.prefetch_next()
        fw.pop()
        jobs = []
        for sq in range(2):
            for pair in range(4):
                t0 = sq * 256
                seqi = half * 2 + sq

                def yout(c, yfin, pair=pair, t0=t0):
                    cs = t0 + c * 128
                    self.dve(lambda: self.V.scalar_tensor_tensor(self.oTs[half][0][:, pair, cs:cs + 128], yfin[:, :], 0.5,
                                                                 GA2[:, pair, cs:cs + 128], ALU.mult, ALU.mult), [yfin, GA2], [self.oTs[half][0]])

                def stout(dd, ST, pair=pair, seqi=seqi):
                    so = self.d["stout"]
                    row = (((l * 2 + dd) * 4 + seqi) * 4 + pair) * 128
                    fw.dma("sp", so, so[row:row + 128, :], ST, ST[:, :], sem_owner=self.outsem)

                J = dict(T=256, r=(zz[0], lambda a, b, pair=pair, t0=t0: zz[0][:, pair, t0 + a:t0 + b]),
                         k=(zz[1], lambda a, b, pair=pair, t0=t0: zz[1][:, pair, t0 + a:t0 + b]),
                         v=(zz[2], lambda a, b, pair=pair, t0=t0: zz[2][:, pair, t0 + a:t0 + b]),
                         twd=[(lo[0], lambda a, b, t0=t0: lo[0][:, t0 + a:t0 + b]), (lo[1], lambda a, b, t0=t0: lo[1][:, t0 + a:t0 + b])],
                         ad=(lo[2], lambda a, b, t0=t0: lo[2][:, t0 + a:t0 + b]),
                         par=lambda nm, pair=pair: self.sp[:, rw_col(nm, pair):rw_col(nm, pair) + 1],
                         parh=lambda nm, pair=pair: self.rwh[:, RW_NAMES.index(nm) * 4 + pair:RW_NAMES.index(nm) * 4 + pair + 1],
                         parbufs=[self.sp, self.rwh],
                         wup=lambda dd, pair=pair: self.wup[:, dd * 512 + pair * 128:dd * 512 + pair * 128 + 128],
                         aup=lambda dd, pair=pair: self.aup[:, dd * 512 + pair * 128:dd * 512 + pair * 128 + 128],
                         st0=None, yout=yout, stout=stout, scoped=False, tag=len(jobs) % 2)
                jobs.append(J)
        def until_early(g):
            for m in g:
                if m == "early_done":
                    return True
                yield
        cur = self.rwkv_job_gen(jobs[0])
        for _ in until_early(cur):
            pass
        for k in range(len(jobs)):
            nxt = self.rwkv_job_gen(jobs[k + 1]) if k + 1 < len(jobs) else None
            ne = until_early(nxt) if nxt is not None else None
            cur_alive, ne_alive = True, ne is not None
            while cur_alive or ne_alive:
                if cur_alive:
                    try:
                        next(cur)
                    except StopIteration:
                        cur_alive = False
                if ne_alive:
                    try:
                        next(ne)
                    except StopIteration:
                        ne_alive = False
            cur = nxt
        self.ps_range = (0, 4)
        fw.pop()

    def phaseA_contrib(self, l):
        fw = self.fw
        base = l * BLK_PER_LAYER
        self.GA2 = fw.sb([128, 4, 512], BF16, "GA2s")
        fw.push()
        self.stream_begin(5, depth=2)
        for which, (part, row0) in enumerate((("rk", 0), ("rk", 512), ("vx", VX_OFF["v"]))):
            slot = self.next_block(base + WIN_IDX[f"A{which}"])
            for c in range(4):
                self.contrib_rows(l, slot, 512, c * 128, 128, part, row0 + c * 128)
            self.prefetch_next()
        slot = self.next_block(base + WIN_IDX["A3"])
        for i in range(3):
            self.contrib_rows(l, slot, 192, i * 64, 64, "vx", VX_OFF["lora"] + i * 64)
        self.prefetch_next()
        slot = self.next_block(base + WIN_IDX["A4"])
        for c in range(4):
            ps = self.zmm(slot, 512, c * 128, 128, 0, 512)
            self.silu2(ps, 128, 512, self.GA2[:, c, :], self.GA2)
        self.prefetch_next()
        fw.pop()

    def gather_rows(self, dst, dst_ap, src, idx_col):
        fw = self.fw
        idx = self.idx1 if idx_col < 12 else self.idx2
        col = idx_col if idx_col < 12 else idx_col - 12
        fw.dma("pool", dst, None, src, None, extra_reads=[idx],
               fn=lambda: self.G.indirect_dma_start(out=dst_ap, out_offset=None, in_=src.h.ap(),
                                                    in_offset=bass.IndirectOffsetOnAxis(ap=idx[:, col:col + 1], axis=0)))

    def phaseA_consume(self, l):
        fw = self.fw
        V, A, G = self.V, self.A, self.G
        fw.push()
        self.load_rwkv_w(l, True)
        spo = fw.sb([128, 12], F32, "spo")
        self.load(spo, spo[:, :], self.d["spo"], self.d["spo"][l, :, :])
        spoh = fw.sb([128, 4], F32, "spoh")
        self.dve(lambda: V.tensor_scalar(spoh[:, :], spo[:, 0:4], 0.5, None, ALU.mult), [spo], [spoh])
        mu1o = fw.sb([128, 3], F32, "mu1o")
        muho = fw.sb([128, 3], F32, "muho")
        self.dve(lambda: V.tensor_scalar(mu1o[:, :], spo[:, 9:12], -1.0, 1.0, ALU.mult, ALU.add), [spo], [mu1o])
        self.dve(lambda: V.tensor_scalar(muho[:, :], spo[:, 9:12], 0.5, None, ALU.mult), [spo], [muho])
        T = DSEQ
        zz = [fw.sb([128, T], BF16, nm) for nm in ("sr", "sk", "sv")]
        lo = [fw.sb([64, T], BF16, nm) for nm in ("stwf", "stwb", "sad")]
        fw.push()
        raw = fw.sb([128, T], BF16, "sraw")
        o32 = fw.sb([128, T], F32, "so32")

        def shift_full(rows, src, m1, mh, dst, tanh=False):
            self.dve(lambda: V.tensor_scalar(o32[:rows, :], src[:rows, :], m1, None, ALU.mult), [src, mu1o, self.mu1], [o32])
            self.dve(lambda: V.scalar_tensor_tensor(o32[:rows, 1:T], src[:rows, 0:T - 1], mh, o32[:rows, 1:T], ALU.mult, ALU.add),
                     [src, muho, self.muh, o32], [o32])
            self.dve(lambda: V.scalar_tensor_tensor(o32[:rows, 0:T - 1], src[:rows, 1:T], mh, o32[:rows, 0:T - 1], ALU.mult, ALU.add),
                     [src, muho, self.muh, o32], [o32])
            if tanh:
                self.act(lambda: A.activation(dst[:rows, :], o32[:rows, :], AF.Tanh), [o32], [dst])
            else:
                self.act(lambda: A.copy(dst[:rows, :], o32[:rows, :]), [o32], [dst])

        for which in range(3):
            src = self.ag1_out[l]["rk" if which < 2 else "vx"]
            for q in range(4):
                self.gather_rows(raw, raw[:, q * 512:(q + 1) * 512], src, which * 4 + q)
            shift_full(128, raw, mu1o[:, which:which + 1], muho[:, which:which + 1], zz[which])
        agv = self.ag1_out[l]["vx"]
        for i in range(3):
            for q in range(4):
                r0 = q * 864 + VX_OFF["lora"] + i * 64
                fw.dma("sp", raw, raw[0:64, q * 512:(q + 1) * 512], agv, agv[r0:r0 + 64, :])
            shift_full(64, raw, self.mu1[0:64, 12 + i:13 + i], self.muh[0:64, 12 + i:13 + i], lo[i], tanh=(i < 2))
        fw.pop()
        stg = [None]

        def yout(c, yfin):
            q, cc = c // 4, c % 4
            if cc == 0:
                stg[0] = fw.rot([128, 512], BF16, "ystg", n=2)
            st = stg[0]
            self.act(lambda: A.copy(st[:, cc * 128:(cc + 1) * 128], yfin[:, :]), [yfin], [st])
            if cc == 3:
                ag = self.ag2_in[l]
                fw.dma("sp", ag, ag[q * 128:(q + 1) * 128, :], st, st[:, :])

        pidx = {nm: i for i, nm in enumerate(RW_NAMES)}
        J = dict(T=T, r=(zz[0], lambda a, b: zz[0][:, a:b]), k=(zz[1], lambda a, b: zz[1][:, a:b]), v=(zz[2], lambda a, b: zz[2][:, a:b]),
                 twd=[(lo[0], lambda a, b: lo[0][:, a:b]), (lo[1], lambda a, b: lo[1][:, a:b])],
                 ad=(lo[2], lambda a, b: lo[2][:, a:b]),
                 par=lambda nm: spo[:, pidx[nm]:pidx[nm] + 1],
                 parh=lambda nm: spoh[:, pidx[nm]:pidx[nm] + 1],
                 parbufs=[spo, spoh],
                 wup=lambda dd: self.wup[:, dd * 128:(dd + 1) * 128],
                 aup=lambda dd: self.aup[:, dd * 128:(dd + 1) * 128],
                 st0=lambda dd: (self.d["st0"], self.d["st0"][l, dd, :, :]), yout=yout, stout=None, seg=256, segpar=False)
        self.rwkv_job(J)
        self.allgather(self.ag2_out[l], self.ag2_in[l])
        fw.pop()

    def phaseA_final(self, l):
        fw = self.fw
        fw.push()
        for r in range(4):
            ya = fw.rot([128, 512], BF16, "ya", n=2)
            self.gather_rows(ya, ya[:, :], self.ag2_out[l], 12 + r)
            self.dve(lambda r=r, ya=ya: self.V.scalar_tensor_tensor(self.oTs[0][0][:, r, 0:512], ya[:, :], 0.5, self.GA2[:, r, :], ALU.mult, ALU.mult),
                     [ya, self.GA2], [self.oTs[0][0]])
        fw.pop()

    def rwkv_job(self, J):
        for _ in self.rwkv_job_gen(J):
            pass

    def rwkv_job_gen(self, J):
        fw = self.fw
        V, A, G, T_ = self.V, self.A, self.G, self.T
        T = J["T"]
        nch = T // 128
        SEG = J.get("seg", 256)
        nseg = T // SEG
        ncs = SEG // 128
        rB, rf = J["r"]
        kB, kf = J["k"]
        vB, vf = J["v"]
        adB, adf = J["ad"]
        par, parh, pbufs = J["par"], J["parh"], J["parbufs"]
        self.ps_range = (0, 6)
        scoped = J.get("scoped", True)
        tag = str(J.get("tag", ""))
        jpush = (lambda: fw.push()) if scoped else (lambda: None)
        jpop = (lambda: fw.pop()) if scoped else (lambda: None)
        jt = (lambda shp, dt, nm: fw.sb(shp, dt, nm)) if scoped else (lambda shp, dt, nm: fw.rot(shp, dt, "J" + nm + tag, n=1))
        jpush()
        kap = jt([128, T], BF16, "kap")
        Vtm = jt([128, nch, 128], BF16, "Vtm")
        Yacc = jt([128, nch, 128], F32, "Yacc")
        Bacc = jt([128, T], F32, "Bacc")
        ST = [jt([128, 64], F32, f"ST{dd}") for dd in range(2)]
        STb = [jt([128, 64], BF16, f"STb{dd}") for dd in range(2)]
        KW = min(T, 512)
        self.pool(lambda: G.memset(Yacc[:, :, :], 0.0), [], [Yacc])
        self.pool(lambda: G.memset(Bacc[:, :], 0.0), [], [Bacc])
        jpush()
        for p0 in range(0, T, 512):
            N = min(512, T - p0)
            kk = fw.rot([128, KW], F32, "kk", n=(2 if scoped else 1))
            sq = fw.rot([128, KW], BF16, "kksq", n=(2 if scoped else 1))
            self.dve(lambda: V.tensor_scalar(kk[:, :N], kf(p0, p0 + N), par("k_k"), None, ALU.mult), [kB] + pbufs, [kk])
            self.act(lambda: A.activation(sq[:, :N], kk[:, :N], AF.Square), [kk], [sq])
            ps = self.nps()
            self.pe(lambda: T_.matmul(ps[:, :N], self.bones[:, :], sq[:, :N], start=True, stop=True), [self.bones, sq], [ps])
            t = fw.rot([128, KW], F32, "kkt", n=(2 if scoped else 1))
            self.dve(lambda: V.tensor_scalar(t[:, :N], ps[:, :N], 1e-24, None, ALU.max), [ps], [t])
            self.rsqrt(t, t[:, :N], t, t[:, :N])
            self.dve(lambda: V.tensor_tensor(kap[:, p0:p0 + N], kk[:, :N], t[:, :N], ALU.mult), [kk, t], [kap])
            yield
        import os
        STOP = int(os.environ.get("RWKV_STOP", "99"))
        if STOP <= 1:
            jpop(); jpop(); return
        for c in range(nch):
            self.transpose_to(vB, vf(c * 128, c * 128 + 128), Vtm, Vtm[:, c, :], eng=("act" if c % 2 else "dve"))
            yield
        jpop()
        if STOP <= 2:
            jpop(); return
        jpush()
        for dd in range(2):
            if J["st0"] is None:
                self.pool(lambda dd=dd: G.memset(ST[dd][:, :], 0.0), [], [ST[dd]])
            else:
                src, sap = J["st0"](dd)
                fw.dma("sp", ST[dd], ST[dd][:, :], src, sap)
            self.act(lambda dd=dd: A.copy(STb[dd][:, :], ST[dd][:, :]), [ST[dd]], [STb[dd]])
        MK = self.mask
        def rw_segment(dd, sg, res, segpar):
            sfx = "fb"[dd]
            twB, twf = J["twd"][dd]
            s0 = sg * SEG
            N = SEG
            f32t = lambda nm: fw.rot([128, SEG], F32, nm + (str(dd) if segpar else ""), n=1)
            a = f32t("ra")
            ps = self.nps()
            self.pe(lambda: T_.matmul(ps[:, :N], J["aup"](dd), adf(s0, s0 + N), start=True, stop=True), [self.aup, adB], [ps])
            self.act(lambda: A.activation(a[:, :], ps[:, :N], AF.Tanh, bias=parh("a0_" + sfx), scale=0.5), [ps] + pbufs, [a])
            yield
            self.dve(lambda: V.tensor_scalar(a[:, :], a[:, :], 0.5, 0.5, ALU.mult, ALU.add), [a], [a])
            kt = f32t("rkt")
            self.dve(lambda: V.tensor_scalar(kt[:, :], a[:, :], 1.0, par("k_a"), ALU.subtract, ALU.mult), [a] + pbufs, [kt])
            self.dve(lambda: V.scalar_tensor_tensor(kt[:, :], kt[:, :], 1.0, kf(s0, s0 + N), ALU.add, ALU.mult), [kt, kB], [kt])
            b = f32t("rb")
            self.dve(lambda: V.tensor_tensor(b[:, :], a[:, :], kap[:, s0:s0 + N], ALU.mult), [a, kap], [b])
            lw = f32t("rlw")
            ps = self.nps()
            self.pe(lambda: T_.matmul(ps[:, :N], J["wup"](dd), twf(s0, s0 + N), start=True, stop=True), [self.wup, twB], [ps])
            self.act(lambda: A.activation(lw[:, :], ps[:, :N], AF.Tanh, bias=parh("w0_" + sfx), scale=0.5), [ps] + pbufs, [lw])
            yield
            self.dve(lambda: V.tensor_scalar(lw[:, :], lw[:, :], -0.3032653298563167, -0.3032653298563167, ALU.mult, ALU.add), [lw], [lw])
            rkr = fw.rot([128, SEG], BF16, "rkr" + str(dd), n=1)
            self.dve(lambda: V.scalar_tensor_tensor(rkr[:, :], kt[:, :], par("r_k"), rf(s0, s0 + N), ALU.mult, ALU.mult),
                     [kt, rB] + pbufs, [rkr])
            ps = self.nps()
            self.pe(lambda: T_.matmul(ps[:, :N], self.bones[:, :], rkr[:, :], start=True, stop=True), [self.bones, rkr], [ps])
            self.dve(lambda: V.tensor_tensor(Bacc[:, s0:s0 + N], Bacc[:, s0:s0 + N], ps[:, :N], ALU.add), [Bacc, ps], [Bacc])
            P = f32t("rP")
            for c in range(ncs):
                self.dve(lambda c=c: V.tensor_tensor_scan(P[:, c * 128:(c + 1) * 128], self.onesf[:, :], lw[:, c * 128:(c + 1) * 128],
                                                          0.0, ALU.mult, ALU.add), [self.onesf, lw], [P])
            Q = f32t("rQ")
            R = f32t("rR")
            self.dve(lambda: V.tensor_tensor(Q[:, :], P[:, :], lw[:, :], ALU.subtract), [P, lw], [Q])
            for c in range(ncs):
                self.dve(lambda c=c: V.tensor_scalar(R[:, c * 128:(c + 1) * 128], P[:, c * 128:(c + 1) * 128], -1.0,
                                                     P[:, c * 128 + 127:c * 128 + 128], ALU.mult, ALU.add), [P], [R])
            gL = fw.rot([128, 2], F32, "gL" + str(dd) + tag, n=2)
            self.act(lambda: A.activation(gL[:, 0:ncs], P[:, 127:SEG:128], AF.Exp), [P], [gL])
            yield
            if dd == 0:
                srcs = [(P, -1.0, a), (Q, 1.0, Q), (P, 1.0, P), (R, 1.0, R)]
            else:
                RL = f32t("rRL")
                self.dve(lambda: V.tensor_tensor(RL[:, :], R[:, :], lw[:, :], ALU.add), [R, lw], [RL])
                srcs = [(RL, -1.0, a), (R, 1.0, R), (RL, 1.0, RL), (Q, 1.0, Q)]
            E = [None] * 4
            for i, (sb_, sc_, dst_) in enumerate(srcs):
                self.act(lambda i=i, sb_=sb_, sc_=sc_, dst_=dst_: A.activation(dst_[:, :], sb_[:, :], AF.Exp, scale=sc_), [sb_], [dst_])
                E[i] = dst_
            Kd2 = fw.rot([128, 2, SEG], BF16, "Kd2" + str(dd) + tag, n=1)
            Bd2 = fw.rot([128, 2, SEG], BF16, "Bd2" + str(dd) + tag, n=1)
            KL = fw.rot([128, SEG], BF16, "KL" + str(dd) + tag, n=1)
            BL = fw.rot([128, SEG], BF16, "BL" + str(dd) + tag, n=1)
            KqRq2 = fw.rot([128, 2, ncs, 2, 128], BF16, "KqRq2" + str(dd) + tag, n=1)
            hm = self.hm
            kap3 = kap[:, s0:s0 + N].rearrange("p (c t) -> p c t", c=ncs)
            r3 = rf(s0, s0 + N).rearrange("p (c t) -> p c t", c=ncs)
            for h in range(2):
                self.dve(lambda h=h: V.scalar_tensor_tensor(Kd2[:, h, :], kt[:, :], hm[:, h:h + 1], E[0][:, :], ALU.mult, ALU.mult), [kt, hm, E[0]], [Kd2])
                self.dve(lambda h=h: V.scalar_tensor_tensor(Bd2[:, h, :], b[:, :], hm[:, h:h + 1], E[0][:, :], ALU.mult, ALU.mult), [b, hm, E[0]], [Bd2])
                self.dve(lambda h=h: V.scalar_tensor_tensor(KqRq2[:, h, :, 0, :], kap3, hm[:, h:h + 1], E[1][:, :].rearrange("p (c t) -> p c t", c=ncs),
                                                            ALU.mult, ALU.mult), [kap, hm, E[1]], [KqRq2])
                self.dve(lambda h=h: V.scalar_tensor_tensor(KqRq2[:, h, :, 1, :], r3, hm[:, h:h + 1], E[2][:, :].rearrange("p (c t) -> p c t", c=ncs),
                                                            ALU.mult, ALU.mult), [rB, hm, E[2]], [KqRq2])
            self.dve(lambda: V.tensor_tensor(KL[:, :], kt[:, :], E[3][:, :], ALU.mult), [kt, E[3]], [KL])
            self.dve(lambda: V.tensor_tensor(BL[:, :], b[:, :], E[3][:, :], ALU.mult), [b, E[3]], [BL])

            res.update(dict(Kd2=Kd2, Bd2=Bd2, KL=KL, BL=BL, KqRq2=KqRq2, gL=gL, sg=sg))
            yield

        def rw_pre(dd, c, Pd, res):
            sfx2 = f"{dd}{c}"
            mk0 = dd * 1280
            Kd2, Bd2, KqRq2 = Pd["Kd2"], Pd["Bd2"], Pd["KqRq2"]
            cs = slice(c * 128, (c + 1) * 128)
            Am = [fw.rot([128, 512], BF16, f"Am{h}_{sfx2}", n=1) for h in range(2)]
            psB = self.nps()
            for h in range(2):
                psA = self.nps()
                rhsA = KqRq2[:, h, c, :, :].rearrange("p a t -> p (a t)")
                self.pe(lambda h=h, psA=psA, rhsA=rhsA: T_.matmul(psA[:, 0:256], Kd2[:, h, cs], rhsA, start=True, stop=True),
                        [Kd2, KqRq2], [psA], signal=False)
                self.pe(lambda h=h, psA=psA, rhsA=rhsA: T_.matmul(psA[:, 256:512], Bd2[:, h, cs], rhsA, start=True, stop=True),
                        [Bd2, KqRq2], [psA])
                self.dve(lambda h=h, psA=psA: V.tensor_tensor(Am[h][:, :], psA[:, :], MK[:, mk0:mk0 + 512], ALU.mult), [psA, MK], [Am[h]])
                self.pe(lambda h=h: T_.matmul(psB[:, h * 128:(h + 1) * 128], KqRq2[:, h, c, 0, :], Bd2[:, h, cs], start=True, stop=True),
                        [KqRq2, Bd2], [psB], signal=(h == 1))
            PT = [fw.rot([128, 2, 128], BF16, f"PT{i}_{sfx2}", n=1) for i in range(2)]
            PX = [fw.rot([128, 2, 256], BF16, f"PX{i}_{sfx2}", n=1) for i in range(2)]
            C64T = fw.rot([128, 2, 128], BF16, "C64T" + sfx2, n=1)
            C128T = fw.rot([128, 2, 128], BF16, "C128T" + sfx2, n=1)
            f2 = lambda t: t[:, :, :].rearrange("p a t -> p (a t)")
            self.dve(lambda: V.tensor_tensor(f2(PT[0]), psB[:, 0:256], MK[:, mk0 + 512:mk0 + 768], ALU.mult), [psB, MK], [PT[0]])
            self.dve(lambda: V.tensor_tensor(f2(C64T), psB[:, 0:256], MK[:, mk0 + 768:mk0 + 1024], ALU.mult), [psB, MK], [C64T])
            self.dve(lambda: V.tensor_tensor(f2(C128T), psB[:, 0:256], MK[:, mk0 + 1024:mk0 + 1280], ALU.mult), [psB, MK], [C128T])
            for h in range(2):
                self.act(lambda h=h: A.copy(PX[0][:, h, 0:128], Am[h][:, 256:384]), [Am[h]], [PX[0]])
            yield
            cur = 0
            Xb = fw.rot([128, 2, 128], BF16, "Xb32" + sfx2, n=1)
            for j in range(1, 6):
                nxt = 1 - cur
                if j == 1:
                    ps = self.nps()
                    pst_ = self.nps()
                    for h in range(2):
                        self.pe(lambda h=h, ps=ps, cur=cur: T_.matmul(ps[:, h * 256:h * 256 + 128], PT[cur][:, h, :], PX[cur][:, h, 0:128],
                                                                      start=True, stop=True), [PT[cur], PX[cur]], [ps], signal=(h == 1))
                        self.pe(lambda h=h, pst_=pst_, cur=cur: T_.matmul(pst_[:, h * 128:(h + 1) * 128], PX[cur][:, h, 0:128], PT[cur][:, h, :],
                                                                          start=True, stop=True), [PT[cur], PX[cur]], [pst_], signal=(h == 1))
                    for h in range(2):
                        self.dve(lambda h=h, cur=cur, nxt=nxt: V.tensor_tensor(PX[nxt][:, h, 128:256], PX[cur][:, h, 0:128], self.ident[:, :], ALU.add),
                                 [PX[cur], self.ident], [PX[nxt]])
                    self.act(lambda ps=ps, nxt=nxt: A.copy(PX[nxt][:, :, 0:128], ps[:, :].rearrange("p (a t) -> p a t", a=2)[:, :, 0:128]),
                             [ps], [PX[nxt]])
                    self.act(lambda pst_=pst_, nxt=nxt: A.copy(f2(PT[nxt]), pst_[:, 0:256]), [pst_], [PT[nxt]])
                elif j < 5:
                    ps = self.nps()
                    pst_ = self.nps()
                    for h in range(2):
                        self.pe(lambda h=h, ps=ps, cur=cur: T_.matmul(ps[:, h * 256:(h + 1) * 256], PT[cur][:, h, :], PX[cur][:, h, :],
                                                                      start=True, stop=True), [PT[cur], PX[cur]], [ps], signal=(h == 1))
                        self.pe(lambda h=h, pst_=pst_, cur=cur: T_.matmul(pst_[:, h * 128:(h + 1) * 128], PX[cur][:, h, 0:128], PT[cur][:, h, :],
                                                                          start=True, stop=True), [PT[cur], PX[cur]], [pst_], signal=(h == 1))
                    ps3 = ps[:, :].rearrange("p (a t) -> p a t", a=2)
                    self.act(lambda ps3=ps3, ps=ps, nxt=nxt: A.copy(PX[nxt][:, :, 0:128], ps3[:, :, 0:128]), [ps], [PX[nxt]])
                    self.dve(lambda ps3=ps3, ps=ps, cur=cur, nxt=nxt: V.tensor_tensor(PX[nxt][:, :, 128:256], ps3[:, :, 128:256], PX[cur][:, :, 128:256], ALU.add),
                             [ps, PX[cur]], [PX[nxt]])
                    self.act(lambda pst_=pst_, nxt=nxt: A.copy(f2(PT[nxt]), pst_[:, 0:256]), [pst_], [PT[nxt]])
                else:
                    ps = self.nps()
                    for h in range(2):
                        self.pe(lambda h=h, ps=ps, cur=cur: T_.matmul(ps[:, h * 128:(h + 1) * 128], PT[cur][:, h, :], PX[cur][:, h, 128:256],
                                                                      start=True, stop=True), [PT[cur], PX[cur]], [ps], signal=(h == 1))
                    self.dve(lambda ps=ps, cur=cur: V.tensor_tensor(Xb[:, :, :], ps[:, 0:256].rearrange("p (a t) -> p a t", a=2), PX[cur][:, :, 128:256], ALU.add),
                             [ps, PX[cur]], [Xb])
                cur = nxt
                yield
            TT = None
            for lvl, CT in enumerate((C64T, C128T)):
                XT = fw.rot([128, 2, 128], BF16, "XTm" + sfx2, n=1)
                Zt = fw.rot([128, 2, 128], BF16, "Ztm" + sfx2, n=1)
                ptt = self.pst[self.pst_rr % 2]
                self.pst_rr += 1
                for h in range(2):
                    self.pe(lambda h=h, ptt=ptt, Xb=Xb: T_.transpose(ptt[:, h * 128:(h + 1) * 128], Xb[:, h, :], self.ident[:, :]),
                            [Xb, self.ident], [ptt], signal=(h == 1))
                self.act(lambda ptt=ptt, XT=XT: A.copy(f2(XT), ptt[:, 0:256]), [ptt], [XT])
                psz = self.nps()
                for h in range(2):
                    self.pe(lambda h=h, psz=psz, CT=CT, Xb=Xb: T_.matmul(psz[:, h * 128:(h + 1) * 128], CT[:, h, :], Xb[:, h, :], start=True, stop=True),
                            [CT, Xb], [psz], signal=(h == 1))
                self.dve(lambda psz=psz, Zt=Zt: V.tensor_copy(f2(Zt), psz[:, 0:256]), [psz], [Zt])
                psw = self.nps()
                for h in range(2):
                    self.pe(lambda h=h, psw=psw, XT=XT, Zt=Zt: T_.matmul(psw[:, h * 128:(h + 1) * 128], XT[:, h, :], Zt[:, h, :], start=True, stop=True),
                            [XT, Zt], [psw], signal=(h == 1))
                Xn = fw.rot([128, 2, 128], BF16, ("Xb64" if lvl == 0 else "TT") + sfx2, n=1)
                self.dve(lambda psw=psw, Xn=Xn, Xb=Xb: V.tensor_tensor(f2(Xn), psw[:, 0:256], f2(Xb), ALU.add), [psw, Xb], [Xn])
                Xb = Xn
                yield
            TT = Xb

            res["Am"] = Am
            res["TT"] = TT
            yield

        def rw_seq(dd, Pd, pres):
            KL, BL, KqRq2, gL, sg = Pd["KL"], Pd["BL"], Pd["KqRq2"], Pd["gL"], Pd["sg"]
            chunks = list(range(ncs)) if dd == 0 else list(reversed(range(ncs)))
            for c in chunks:
                cg = sg * ncs + c
                cs = slice(c * 128, (c + 1) * 128)
                Am, TT = pres[(dd, c)]["Am"], pres[(dd, c)]["TT"]
                Sb = STb[dd]
                psG = self.nps()
                for h in range(2):
                    hs = slice(64 * h, 64 * h + 64)
                    vs = slice(64 * h, 64 * h + 64)
                    self.pe(lambda h=h, vs=vs: T_.matmul(psG[:, vs], KqRq2[:, h, c, 0, :], Sb[:, :], start=(h == 0), stop=False, skip_group_check=True),
                            [KqRq2, Sb], [psG], signal=False)
                    self.pe(lambda h=h, vs=vs: T_.matmul(psG[:, vs], Am[h][:, 0:128], Vtm[:, cg, vs], start=False, stop=(h == 1), skip_group_check=True),
                            [Am[h], Vtm], [psG], signal=(h == 1))
                Gn = fw.rot([128, 128], BF16, "Gn" + str(dd), n=1)
                self.act(lambda: A.mul(Gn[:, :], psG[:, 0:128], -1.0), [psG], [Gn])
                yield
                psU = self.nps()
                for h in range(2):
                    vs = slice(64 * h, 64 * h + 64)
                    self.pe(lambda h=h, vs=vs: T_.matmul(psU[:, vs], TT[:, h, :], Gn[:, vs], start=(h == 0), stop=(h == 1), skip_group_check=True),
                            [TT, Gn], [psU], signal=(h == 1))
                U = fw.rot([128, 128], BF16, "U" + str(dd), n=1)
                self.dve(lambda: V.tensor_copy(U[:, :], psU[:, 0:128]), [psU], [U])
                yield
                yield
                psY = self.nps()
                for h in range(2):
                    hs = slice(64 * h, 64 * h + 64)
                    vs = slice(64 * h, 64 * h + 64)
                    self.pe(lambda h=h, vs=vs: T_.matmul(psY[:, vs], KqRq2[:, h, c, 1, :], Sb[:, :], start=(h == 0), stop=False, skip_group_check=True),
                            [KqRq2, Sb], [psY], signal=False)
                    self.pe(lambda h=h, vs=vs: T_.matmul(psY[:, vs], Am[h][:, 128:256], Vtm[:, cg, vs], start=False, stop=False, skip_group_check=True),
                            [Am[h], Vtm], [psY], signal=False)
                    self.pe(lambda h=h, vs=vs: T_.matmul(psY[:, vs], Am[h][:, 384:512], U[:, vs], start=False, stop=(h == 1), skip_group_check=True),
                            [Am[h], U], [psY], signal=(h == 1))
                self.dve(lambda: V.tensor_tensor(Yacc[:, cg, :], Yacc[:, cg, :], psY[:, 0:128], ALU.add), [Yacc, psY], [Yacc])
                yield
                yield
                KLt = fw.rot([128, 128], BF16, "KLt" + str(dd), n=1)
                BLt = fw.rot([128, 128], BF16, "BLt" + str(dd), n=1)
                self.transpose_to(KL, KL[:, cs], KLt, KLt[:, :], eng="act")
                self.transpose_to(BL, BL[:, cs], BLt, BLt[:, :], eng="dve")
                psS = self.nps()
                for h in range(2):
                    hs = slice(64 * h, 64 * h + 64)
                    vs = slice(64 * h, 64 * h + 64)
                    self.pe(lambda h=h, hs=hs, vs=vs: T_.matmul(psS[hs, 0:64], KLt[:, hs], Vtm[:, cg, vs], start=True, stop=False),
                            [KLt, Vtm], [psS], signal=False)
                    self.pe(lambda h=h, hs=hs, vs=vs: T_.matmul(psS[hs, 0:64], BLt[:, hs], U[:, vs], start=False, stop=True),
                            [BLt, U], [psS], signal=(h == 1))
                self.dve(lambda: V.scalar_tensor_tensor(ST[dd][:, :], ST[dd][:, :], gL[:, c:c + 1], psS[:, 0:64], ALU.mult, ALU.add),
                         [ST[dd], gL, psS], [ST[dd]])
                self.act(lambda: A.copy(STb[dd][:, :], ST[dd][:, :]), [ST[dd]], [STb[dd]])

        def run_rr(gens):
            gens = list(gens)
            while gens:
                for g in list(gens):
                    try:
                        next(g)
                    except StopIteration:
                        gens.remove(g)
                yield

        for step in range(nseg):
            sgs = (step, nseg - 1 - step)
            Pd = [{}, {}]
            segpar = J.get("segpar", False)
            if segpar:
                yield from run_rr([rw_segment(dd, sgs[dd], Pd[dd], True) for dd in range(2)])
            else:
                for dd in range(2):
                    for _ in rw_segment(dd, sgs[dd], Pd[dd], False):
                        yield
            if step == 0:
                yield "early_done"
            if STOP <= 3:
                continue
            pres = {(dd, c): {} for dd in range(2) for c in range(ncs)}
            yield from run_rr([rw_pre(dd, c, Pd[dd], pres[(dd, c)]) for dd in range(2) for c in range(ncs)])
            if STOP <= 5:
                continue
            yield from run_rr([rw_seq(dd, Pd[dd], pres) for dd in range(2)])
        if J["stout"] is not None:
            for dd in range(2):
                J["stout"](dd, ST[dd])
        jpop()
        if STOP <= 6:
            jpop(); return
        n2 = nch * 2
        sums = jt([128, n2], F32, "gsum")
        ssq = jt([128, n2], F32, "gssq")
        Ysq = jt([128, nch * 128], F32, "Ysq") if scoped else fw.rot([128, nch * 128], F32, "JYsq", n=1)
        Yf = Yacc[:, :, :].rearrange("p c x -> p (c x)")
        self.dve(lambda: V.tensor_reduce(sums[:, :], Yf.rearrange("p (g x) -> p g x", x=64), AX.X, ALU.add), [Yacc], [sums])
        self.act(lambda: A.activation(Ysq[:, :], Yf, AF.Square), [Yacc], [Ysq])
        self.dve(lambda: V.tensor_reduce(ssq[:, :], Ysq[:, :].rearrange("p (g x) -> p g x", x=64), AX.X, ALU.add), [Ysq], [ssq])
        mean = jt([128, n2], F32, "gmean")
        var = jt([128, n2], F32, "gvar2")
        self.dve(lambda: V.tensor_scalar(mean[:, :], sums[:, :], 1.0 / 64, None, ALU.mult), [sums], [mean])
        self.dve(lambda: V.tensor_tensor(var[:, :], mean[:, :], mean[:, :], ALU.mult), [mean], [var])
        self.dve(lambda: V.scalar_tensor_tensor(var[:, :], ssq[:, :], 1.0 / 64, var[:, :], ALU.mult, ALU.subtract), [ssq, var], [var])
        self.dve(lambda: V.tensor_scalar(var[:, :], var[:, :], GN_EPS, None, ALU.add), [var], [var])
        self.rsqrt(var, var[:, :], var, var[:, :])
        yn = jt([128, nch, 128], BF16, "yn")
        for c in range(nch):
            for h in range(2):
                g = c * 2 + h
                self.dve(lambda c=c, h=h, g=g: V.tensor_scalar(yn[:, c, h * 64:(h + 1) * 64], Yacc[:, c, h * 64:(h + 1) * 64], mean[:, g:g + 1], var[:, g:g + 1],
                                                               ALU.subtract, ALU.mult), [Yacc, mean, var], [yn])
        for c in range(nch):
            pt = self.pst[self.pst_rr % 2]
            self.pst_rr += 1
            self.pe(lambda c=c, pt=pt: T_.transpose(pt[:, :128], yn[:, c, :], self.ident[:, :]), [yn, self.ident], [pt])
            yT = fw.rot([128, 128], F32, "yT", n=(2 if scoped else 1))
            self.act(lambda pt=pt, yT=yT: A.activation(yT[:, :], pt[:, :128], AF.Identity, bias=par("ln_b"), scale=par("ln_g")), [pt] + pbufs, [yT])
            bo = fw.rot([128, 128], F32, "bo", n=(2 if scoped else 1))
            self.dve(lambda c=c, bo=bo: V.tensor_tensor(bo[:, :], Bacc[:, c * 128:(c + 1) * 128], vf(c * 128, (c + 1) * 128), ALU.mult), [Bacc, vB], [bo])
            yfin = fw.rot([128, 128], F32, "yfin", n=(2 if scoped else 1))
            self.dve(lambda yT=yT, bo=bo, yfin=yfin: V.tensor_tensor(yfin[:, :], yT[:, :], bo[:, :], ALU.add), [yT, bo], [yfin])
            J["yout"](c, yfin)
            yield
        jpop()
        if scoped:
            self.ps_range = (0, 4)

    def load_mla_w(self, l):
        fw, d = self.fw, self.d
        self.wq = fw.sb([128, 2 * 8 * 96], BF16, "wq")
        self.wqs = fw.sb([128, 2 * 8 * 32], BF16, "wqs")
        self.wkk = fw.sb([128, 512], BF16, "wkk")
        self.wkv = fw.sb([128, 512], BF16, "wkv")
        for nm in ("wq", "wqs", "wkk", "wkv"):
            t = getattr(self, nm)
            self.load(t, t[:], d[nm], d[nm][l, :, :])

    def mla_front(self, l, h0, rope, GB2, Qh, ckv_f, ckv_b, kr_f, kr_b):
        fw = self.fw
        base = l * BLK_PER_LAYER
        W = WIN_W["B0"]
        slot = self.next_block(base + WIN_IDX["B0"])
        qd = fw.sb([128, 2, 512], F32, "qd")
        kvd = fw.sb([128, 512], F32, "kvd")
        for c in range(2):
            ps = self.zmm(slot, W, c * 128, 128, h0, 512)
            self.act(lambda c=c, ps=ps: self.A.copy(qd[:, c, :], ps[:, :]), [ps], [qd])
        ps = self.zmm(slot, W, 256, 128, h0, 512)
        self.dve(lambda ps=ps: self.V.tensor_copy(kvd[:, :], ps[:, :]), [ps], [kvd])
        pk = self.zmm(slot, W, 384, 32, h0, 512, prow=64)
        R = self.rope
        if rope:
            pks = self.zmm(slot, W, 416, 32, h0, 512, prow=64)
            t1 = fw.sb([96, 512], F32, "krt1")
            self.dve(lambda: self.V.tensor_tensor(t1[64:96, :], pk[64:96, :], R[64:96, 0:512], ALU.mult), [pk, R], [t1])
            self.dve(lambda: self.V.tensor_tensor(kr_f[64:96, :], pks[64:96, :], R[64:96, 512:1024], ALU.mult), [pks, R], [kr_f])
            self.dve(lambda: self.V.tensor_tensor(kr_f[64:96, :], kr_f[64:96, :], t1[64:96, :], ALU.add), [kr_f, t1], [kr_f])
        else:
            self.act(lambda: self.A.copy(kr_f[64:96, :], pk[64:96, :]), [pk], [kr_f])
        self.act(lambda: self.A.copy(kr_b[64:96, :], kr_f[64:96, :]), [kr_f], [kr_b])
        self.prefetch_next()
        slot = self.next_block(base + WIN_IDX["B1"])
        for c in range(4):
            ps = self.zmm(slot, 512, c * 128, 128, h0, 512)
            self.silu2(ps, 128, 512, GB2[:, c, :], GB2)
        self.prefetch_next()
        rq = fw.sb([128, 512], F32, "rq")
        self.rstd_tile(qd, [qd[:, c, :] for c in range(2)], 256.0, rq, 512)
        qn = fw.sb([128, 2, 512], BF16, "qn")
        for c in range(2):
            self.dve(lambda c=c: self.V.scalar_tensor_tensor(qn[:, c, :], qd[:, c, :], self.spc("qn", c), rq[:, :], ALU.mult, ALU.mult),
                     [qd, self.sp, rq], [qn])
        rk = fw.sb([128, 512], F32, "rkv")
        self.rstd_tile(kvd, [kvd[:, :]], 128.0, rk, 512)
        self.dve(lambda: self.V.scalar_tensor_tensor(ckv_f[:, :], kvd[:, :], self.spc("kvn", 0), rk[:, :], ALU.mult, ALU.mult),
                 [kvd, self.sp, rk], [ckv_f])
        self.act(lambda: self.A.copy(ckv_b[:, :], ckv_f[:, :]), [ckv_f], [ckv_b])
        for h in range(8):
            ps = self.nps()
            for c in range(2):
                self.pe(lambda c=c, h=h, ps=ps: self.T.matmul(ps[:96, :], self.wq[:, (c * 8 + h) * 96:(c * 8 + h) * 96 + 96], qn[:, c, :],
                                                             start=(c == 0), stop=(c == 1)), [self.wq, qn], [ps], signal=(c == 1))
            if rope:
                ps2 = self.nps()
                for c in range(2):
                    self.pe(lambda c=c, h=h, ps2=ps2: self.T.matmul(ps2[64:96, :], self.wqs[:, (c * 8 + h) * 32:(c * 8 + h) * 32 + 32], qn[:, c, :],
                                                                   start=(c == 0), stop=(c == 1)), [self.wqs, qn], [ps2], signal=(c == 1))
                t1 = fw.rot([96, 512], F32, "qrt1")
                t2 = fw.rot([96, 512], F32, "qrt2")
                self.dve(lambda ps=ps, t1=t1: self.V.tensor_tensor(t1[64:96, :], ps[64:96, :], R[64:96, 0:512], ALU.mult), [ps, R], [t1])
                self.dve(lambda ps2=ps2, t2=t2: self.V.tensor_tensor(t2[64:96, :], ps2[64:96, :], R[64:96, 512:1024], ALU.mult), [ps2, R], [t2])
                self.dve(lambda h=h, t1=t1, t2=t2: self.V.tensor_tensor(Qh[h][64:96, :], t1[64:96, :], t2[64:96, :], ALU.add), [t1, t2], [Qh[h]])
                self.act(lambda h=h, ps=ps: self.A.copy(Qh[h][0:64, :], ps[0:64, :]), [ps], [Qh[h]])
            else:
                self.act(lambda h=h, ps=ps: self.A.copy(Qh[h][:, :], ps[:96, :]), [ps], [Qh[h]])

    def mla_kv_chunk(self, ckv_b, kr_b, heads, Kh, Vaug, nk):
        for i, h in enumerate(heads):
            ps = self.nps()
            self.pe(lambda h=h, ps=ps: self.T.matmul(ps[:64, :nk], self.wkk[:, h * 64:(h + 1) * 64], ckv_b[:, :nk], start=True, stop=True),
                    [self.wkk, ckv_b], [ps])
            self.act(lambda i=i, ps=ps: self.A.copy(Kh[i][0:64, :nk], ps[0:64, :nk]), [ps], [Kh[i]])
            self.dve(lambda i=i: self.V.tensor_copy(Kh[i][64:96, :nk], kr_b[64:96, :nk]), [kr_b], [Kh[i]])
        for kb in range(nk // 128):
            ps = self.nps()
            self.pe(lambda kb=kb, ps=ps: self.T.matmul(ps[:, :], ckv_b[:, kb * 128:(kb + 1) * 128], self.wkv[:, :], start=True, stop=True),
                    [ckv_b, self.wkv], [ps])
            for h8 in range(8):
                pass
            self.dve(lambda kb=kb, ps=ps: self.V.tensor_copy(
                Vaug[:, kb * 520:(kb + 1) * 520].rearrange("p (h e) -> p h e", e=65)[:, :, 0:64],
                ps[:, :].rearrange("p (h e) -> p h e", e=64)), [ps], [Vaug])

    def attn_accum(self, Qh4, q0, nq, Kh4, Vaug, heads, k0, nkb, first, last, Oacc):
        fw = self.fw
        nqs = nq // 128
        its = [(kb, i, h) for kb in range(nkb) for i, h in enumerate(heads)]

        def score(kb, i, h):
            pss = self.psb[4 + (self.sc_rr % 2)]
            self.sc_rr += 1
            self.pe(lambda: self.T.matmul(pss[:, :nq], Kh4[i][:, k0 + kb * 128:k0 + kb * 128 + 128],
                                          Qh4[i][:, q0:q0 + nq], start=True, stop=True), [Kh4[i], Qh4[i]], [pss])
            PT = fw.rot([128, 512], BF16, "PT", n=3)
            self.act(lambda: self.A.activation(PT[:, :nq], pss[:, :nq], AF.Exp, scale=96.0 ** -0.5), [pss], [PT])
            return PT

        def pv(kb, i, h, PT):
            for qs in range(nqs):
                self.pe(lambda qs=qs: self.T.matmul(
                    Oacc[qs][:, i * 65:(i + 1) * 65], PT[:, qs * 128:(qs + 1) * 128],
                    Vaug[:, (k0 // 128 + kb) * 520 + h * 65:(k0 // 128 + kb) * 520 + h * 65 + 65],
                    start=(first and kb == 0 and i == 0), stop=(last and kb == nkb - 1 and i == 3), skip_group_check=True),
                    [PT, Vaug], [Oacc[qs]], signal=(qs == nqs - 1))

        pend = score(*its[0])
        for n in range(len(its)):
            nxt = score(*its[n + 1]) if n + 1 < len(its) else None
            pv(*its[n], pend)
            pend = nxt

    def attn_finish(self, Oacc, nqs, ob, hh):
        fw = self.fw
        for qs in range(nqs):
            rec = fw.rot([128, 4], F32, "rec", n=4)
            self.dve(lambda qs=qs, rec=rec: self.V.reciprocal(rec[:, :], Oacc[qs][:, 64:260:65]), [Oacc[qs]], [rec])
            for i in range(4):
                self.dve(lambda qs=qs, i=i, rec=rec: self.V.tensor_scalar(ob[qs][:, hh * 256 + i * 64:hh * 256 + i * 64 + 64],
                                                                          Oacc[qs][:, i * 65:i * 65 + 64], rec[:, i:i + 1], None, ALU.mult),
                         [Oacc[qs], rec], [ob[qs]])

    def attn_out(self, ob, nqs, GB2, g0, o0):
        for qs in range(nqs):
            for c in range(4):
                pt = self.pst[self.pst_rr % 2]
                self.pst_rr += 1
                self.pe(lambda qs=qs, c=c, pt=pt: self.T.transpose(pt[:, :128], ob[qs][:, c * 128:(c + 1) * 128], self.ident[:, :]),
                        [ob[qs], self.ident], [pt])
                self.dve(lambda qs=qs, c=c, pt=pt: self.V.scalar_tensor_tensor(
                    self.oTs[o0 // 512][1][:, c, o0 % 512 + qs * 128:o0 % 512 + qs * 128 + 128], pt[:, :128], 0.5, GB2[:, c, g0 + qs * 128:g0 + qs * 128 + 128],
                    ALU.mult, ALU.mult), [pt, GB2], [self.oTs[o0 // 512][1]])

    def phaseB_prompt(self, l, half):
        fw = self.fw
        h0 = half * 512
        fw.push()
        self.load_mla_w(l)
        GB2 = fw.sb([128, 4, 512], BF16, "GB2")
        Qh = [fw.sb([96, 512], BF16, f"Qh{h}") for h in range(8)]
        ckv_f = fw.sb([128, 512], F32, "ckvf")
        ckv_b = fw.sb([128, 512], BF16, "ckvb")
        kr_f = fw.sb([96, 512], F32, "krf")
        kr_b = fw.sb([96, 512], BF16, "krb")
        fw.push()
        self.stream_begin(2, depth=2)
        self.mla_front(l, h0, False, GB2, Qh, ckv_f, ckv_b, kr_f, kr_b)
        fw.pop()
        fw.dma("sp", self.d["ckvout"], self.d["ckvout"][l, :, h0:h0 + 512], ckv_f, ckv_f[:, :], sem_owner=self.outsem)
        fw.dma("sp", self.d["krout"], self.d["krout"][l, :, h0:h0 + 512], kr_f, kr_f[64:96, :], sem_owner=self.outsem)
        Kh = [fw.sb([96, 512], BF16, f"Kh{h}") for h in range(8)]
        Vaug = fw.sb([128, 4 * 520], BF16, "Vaug")
        self.pool(lambda: self.G.memset(Vaug[:, :], 1.0), [], [Vaug])
        self.mla_kv_chunk(ckv_b, kr_b, list(range(8)), Kh, Vaug, 512)
        self.sc_rr = 0
        import os
        if "dumpB" in os.environ.get("KDBG", "") and l == 0 and half == 0:
            so = self.d["stout"]
            fw.dma("pool", so, so[0:768, :].rearrange("(p a) b -> p (a b)", p=96), Qh[0], Qh[0][:, :])
            fw.dma("pool", so, so[768:1536, :].rearrange("(p a) b -> p (a b)", p=96), Kh[0], Kh[0][:, :])
            fw.dma("pool", so, so[1536:5632, :].rearrange("(p a) b -> p (a b)", p=128), Vaug, Vaug[:, 0:2048])
        for sq in range(2):
            ob = [fw.rot([128, 512], BF16, "ob", n=4) for _ in range(2)]
            for hh in range(2):
                heads = list(range(hh * 4, hh * 4 + 4))
                Oacc = [self.psb[0 + 2 * (hh % 2)], self.psb[1 + 2 * (hh % 2)]]
                self.attn_accum([Qh[h] for h in heads], sq * 256, 256, [Kh[h] for h in heads], Vaug, heads, sq * 256, 2, True, True, Oacc)
                self.attn_finish(Oacc, 2, ob, hh)
            if "dumpB" in os.environ.get("KDBG", "") and l == 0 and half == 0 and sq == 0:
                so = self.d["stout"]
                fw.dma("pool", so, so[5696:6720, :].rearrange("(p a) b -> p (a b)", p=128), ob[0], ob[0][:, :])
            self.attn_out(ob, 2, GB2, sq * 256, h0 + sq * 256)
        fw.pop()

    def phaseB_contrib(self, l):
        fw = self.fw
        self.load_mla_w(l)
        self.GB2 = fw.sb([128, 4, 512], BF16, "GB2s")
        self.Qh = [fw.sb([96, 512], BF16, f"Qhs{h}") for h in range(8)]
        fw.push()
        self.rope = fw.sb([96, 1024], F32, "rope")
        self.load(self.rope, self.rope[64:96, :], self.d["rope"], self.d["rope"].ap())
        self.stream_begin(2, depth=2)
        ckv_f = fw.sb([128, 512], F32, "ckvf")
        ckv_b = fw.sb([128, 512], BF16, "ckvb")
        kr_f = fw.sb([96, 512], F32, "krf")
        kr_b = fw.sb([96, 512], BF16, "krb")
        self.mla_front(l, 0, True, self.GB2, self.Qh, ckv_f, ckv_b, kr_f, kr_b)
        ag = self.ag1_in[l]["vx"]
        fw.dma("sp", ag, ag[VX_OFF["ckv"]:VX_OFF["ckv"] + 128, :], ckv_b, ckv_b[:, :])
        fw.dma("sp", ag, ag[VX_OFF["kr"]:VX_OFF["kr"] + 32, :], kr_b, kr_b[64:96, :])
        fw.pop()

    def phaseB_consume(self, l):
        fw = self.fw
        ago = self.ag1_out[l]["vx"]
        fw.push()
        self.sc_rr = 0
        self.ps_range = (4, 6)
        ob = [fw.sb([128, 512], BF16, f"obs{i}") for i in range(4)]
        for hh in range(2):
            heads = list(range(hh * 4, hh * 4 + 4))
            Oacc = self.psb[0:4]
            for ch in range(5):
                ckv_b = fw.rot([128, 512], BF16, "ckvg", n=2)
                kr_b = fw.rot([96, 512], BF16, "krg", n=2)
                if ch < 4:
                    fw.dma("sp", ckv_b, ckv_b[:, :], ago, ago[ch * 864 + VX_OFF["ckv"]:ch * 864 + VX_OFF["ckv"] + 128, :])
                    fw.dma("sp", kr_b, kr_b[64:96, :], ago, ago[ch * 864 + VX_OFF["kr"]:ch * 864 + VX_OFF["kr"] + 32, :])
                else:
                    c32 = fw.rot([128, 512], F32, "cc32", n=1)
                    k32 = fw.rot([96, 512], F32, "ck32", n=1)
                    fw.dma("sp", c32, c32[:, :], self.d["cckv"], self.d["cckv"][l, :, :])
                    fw.dma("sp", k32, k32[64:96, :], self.d["ckr"], self.d["ckr"][l, :, :])
                    self.act(lambda c32=c32, ckv_b=ckv_b: self.A.copy(ckv_b[:, :], c32[:, :]), [c32], [ckv_b])
                    self.dve(lambda k32=k32, kr_b=kr_b: self.V.tensor_copy(kr_b[64:96, :], k32[64:96, :]), [k32], [kr_b])
                Kh = [fw.rot([96, 512], BF16, f"Khs{i}", n=2) for i in range(4)]
                Vaug = fw.rot([128, 4 * 520], BF16, "Vaugs", n=2)
                self.pool(lambda Vaug=Vaug: self.G.memset(Vaug[:, :], 1.0), [], [Vaug])
                self.mla_kv_chunk(ckv_b, kr_b, heads, Kh, Vaug, 512)
                self.attn_accum([self.Qh[h] for h in heads], 0, 512, Kh, Vaug, heads, 0, 4, ch == 0, ch == 4, Oacc)
            self.attn_finish(Oacc, 4, ob, hh)
        self.ps_range = (0, 4)
        self.attn_out(ob, 4, self.GB2, 0, 0)
        fw.pop()

    def fnet_stage1(self, fT_buf, fT_ap_fn, dftd, G1):
        for hb in range(2):
            ps = self.psb[4 + hb]
            for gg in range(2):
                g = hb * 2 + gg
                self.pe(lambda g=g, gg=gg, ps=ps: self.T.matmul(ps[:, gg * 256:(gg + 1) * 256], fT_ap_fn(g), dftd[:, :],
                                                                 start=True, stop=True), [fT_buf, dftd], [ps], signal=(gg == 1))
            if hb == 0:
                self.act(lambda ps=ps: self.A.copy(G1[:, 0:512], ps[:, :]), [ps], [G1])
            else:
                self.dve(lambda ps=ps: self.V.tensor_copy(G1[:, 512:1024], ps[:, :]), [ps], [G1])

    def phaseD_prompt(self, l, half):
        fw = self.fw
        base = l * BLK_PER_LAYER
        h0 = half * 512
        fw.push()
        self.dftd_p = fw.sb([128, 256], BF16, "dftd_p")
        self.dftT_p = fw.sb([128, 1024], BF16, "dftT_p")
        for nm in ("dftd_p", "dftT_p"):
            t = getattr(self, nm)
            self.load(t, t[:], self.d[nm], self.d[nm].ap())
        fT = fw.sb([128, 4, 512], BF16, "fT")
        GD2 = fw.sb([128, 4, 512], BF16, "GD2")
        fw.push()
        self.stream_begin(2, depth=2)
        slot = self.next_block(base + WIN_IDX["D0"])
        for c in range(4):
            ps = self.zmm(slot, 512, c * 128, 128, h0, 512)
            if c % 2:
                self.act(lambda c=c, ps=ps: self.A.copy(fT[:, c, :], ps[:, :]), [ps], [fT])
            else:
                self.dve(lambda c=c, ps=ps: self.V.tensor_copy(fT[:, c, :], ps[:, :]), [ps], [fT])
        self.prefetch_next()
        slot = self.next_block(base + WIN_IDX["D1"])
        for c in range(4):
            ps = self.zmm(slot, 512, c * 128, 128, h0, 512)
            self.silu2(ps, 128, 512, GD2[:, c, :], GD2)
        self.prefetch_next()
        fw.pop()
        for sq in range(2):
            t0 = sq * 256
            G1 = [fw.rot([128, 1024], BF16, "G1", n=4) for _ in range(2)]
            for tt in range(2):
                self.fnet_stage1(fT, lambda g, tt=tt: fT[:, g, t0 + tt * 128:t0 + tt * 128 + 128], self.dftd_p, G1[tt])
            for g in range(4):
                ps = self.nps()
                i = 0
                for tt in range(2):
                    for cs in range(2):
                        self.pe(lambda g=g, tt=tt, cs=cs, ps=ps, i=i: self.T.matmul(
                            ps[:, :256], G1[tt][:, g * 256 + cs * 128:g * 256 + cs * 128 + 128],
                            self.dftT_p[:, (tt * 2 + cs) * 256:(tt * 2 + cs) * 256 + 256], start=(i == 0), stop=(i == 3)),
                            [G1[tt], self.dftT_p], [ps], signal=(i == 3))
                        i += 1
                self.dve(lambda g=g, ps=ps: self.V.scalar_tensor_tensor(
                    self.oTs[half][3][:, g, t0:t0 + 256], ps[:, :256], 0.5, GD2[:, g, t0:t0 + 256], ALU.mult, ALU.mult),
                    [ps, GD2], [self.oTs[half][3]])
        fw.pop()

    def contrib_rows(self, l, slot, W, c0, w, part, row0):
        fw = self.fw
        ps = self.zmm(slot, W, c0, w, 0, 512)
        stg = fw.rot([128, 512], BF16, "agstg", n=3)
        if self.cflip % 2:
            self.act(lambda: self.A.copy(stg[:w, :], ps[:w, :]), [ps], [stg])
        else:
            self.dve(lambda: self.V.tensor_copy(stg[:w, :], ps[:w, :]), [ps], [stg])
        self.cflip += 1
        ag = self.ag1_in[l][part]
        fw.dma("sp", ag, ag[row0:row0 + w, :], stg, stg[:w, :])

    def phaseD_contrib(self, l):
        fw = self.fw
        base = l * BLK_PER_LAYER
        self.GD2 = fw.sb([128, 4, 512], BF16, "GD2s")
        fw.push()
        self.stream_begin(2, depth=2)
        slot = self.next_block(base + WIN_IDX["D0"])
        for c in range(4):
            self.contrib_rows(l, slot, 512, c * 128, 128, "f", c * 128)
        self.prefetch_next()
        slot = self.next_block(base + WIN_IDX["D1"])
        for c in range(4):
            ps = self.zmm(slot, 512, c * 128, 128, 0, 512)
            self.silu2(ps, 128, 512, self.GD2[:, c, :], self.GD2)
        self.prefetch_next()
        fw.pop()

    def phaseD_consume(self, l):
        fw = self.fw
        ago = self.ag1_out[l]["f"]
        fw.push()
        self.dftd_s = fw.sb([128, 256], BF16, "dftd_s")
        self.load(self.dftd_s, self.dftd_s[:], self.d["dftd_s"], self.d["dftd_s"].ap())
        acc = self.psb[0:4]
        nt = 0
        for q in range(4):
            fq = fw.rot([128, 4, 512], BF16, "fq", n=2)
            src = ago[q * 512:q * 512 + 512, :].rearrange("(g d) t -> d g t", d=128)
            fw.dma("sp", fq, fq[:], ago, src)
            for s4 in range(4):
                tt = q * 4 + s4
                ct = fw.rot([128, 1024], BF16, "ct", n=3)
                fw.dma("pool", ct, ct[:], self.d["dftT_s"], self.d["dftT_s"][tt, :, :])
                G1 = fw.rot([128, 1024], BF16, "G1s", n=3)
                self.fnet_stage1(fq, lambda g, s4=s4, fq=fq: fq[:, g, s4 * 128:(s4 + 1) * 128], self.dftd_s, G1)
                for g in range(4):
                    for cs in range(2):
                        self.pe(lambda g=g, cs=cs, G1=G1, ct=ct, tt=tt: self.T.matmul(
                            acc[g][:, :], G1[:, g * 256 + cs * 128:g * 256 + cs * 128 + 128], ct[:, cs * 512:(cs + 1) * 512],
                            start=(tt == 0 and cs == 0), stop=(tt == 15 and cs == 1)),
                            [G1, ct], [acc[g]], signal=(g == 3 and cs == 1))
        for g in range(4):
            self.dve(lambda g=g: self.V.scalar_tensor_tensor(self.oTs[0][3][:, g, 0:512], acc[g][:, :], 0.5, self.GD2[:, g, :],
                                                             ALU.mult, ALU.mult), [acc[g], self.GD2], [self.oTs[0][3]])
        fw.pop()

    def allgather(self, dst, src):
        fw = self.fw
        fw.dma("pool", dst, None, src, None, sem_owner=dst, inc=1,
               fn=lambda: self.G.collective_compute("AllGather", ALU.bypass, replica_groups=[[0, 1, 2, 3], [4, 5, 6, 7]],
                                                     ins=[src.h.ap()], outs=[dst.h.ap()]))

    def sample_pass(self, l):
        import os
        fw = self.fw
        dbg = os.environ.get("KDBG", "")
        self.cflip = 0
        br = self.branches
        fw.push()
        if "A" in br:
            self.phaseA_contrib(l)
        fw.push()
        if "B" in br:
            self.phaseB_contrib(l)
        fw.push()
        if "D" in br:
            self.phaseD_contrib(l)
        if "noag" not in dbg:
            if "A" in br:
                self.allgather(self.ag1_out[l]["rk"], self.ag1_in[l]["rk"])
            if "A" in br or "B" in br:
                self.allgather(self.ag1_out[l]["vx"], self.ag1_in[l]["vx"])
            if "D" in br:
                self.allgather(self.ag1_out[l]["f"], self.ag1_in[l]["f"])
        if "C" in br:
            self.phaseC(l, 0, 512, 0)
        if "D" in br and "nocons" not in dbg:
            self.phaseD_consume(l)
        fw.pop()
        if "B" in br:
            self.phaseB_consume(l)
        fw.pop()
        if "A" in br and "noAcons" not in dbg:
            self.phaseA_consume(l)
            self.phaseA_final(l)
        fw.pop()

    def zero_branch(self, n):
        for hf in range(2):
            if self.oTs[hf] is not None:
                t = self.oTs[hf][n]
                self.pool(lambda t=t: self.G.memset(t[:], 0.0), [], [t])

    def win_plan(self, l, grp="p"):
        base = l * BLK_PER_LAYER
        ids = []
        order = (("A", ["A0", "A1", "A2", "A3", "A4"]), ("B", ["B0", "B1"]), ("C", ["C0", "C1", "C2"]), ("D", ["D0", "D1"]))
        if grp == "s":
            order = (order[0], order[1], order[3], order[2])
        for br, names in order:
            if br in self.branches:
                ids += [base + WIN_IDX[n] for n in names]
        return ids

    def tail_plan(self, l):
        base = l * BLK_PER_LAYER
        return [base + 18 + i for i in range(16)] + [base + 34, base + 35]

    def build(self):
        fw = self.fw
        self.outsem = Buf(None, "outsem", "none")
        for l in range(self.depth):
            self.stream_plan([l * BLK_PER_LAYER + b for b in range(6)])
            self.stream_plan(self.win_plan(l) * 2 + self.tail_plan(l))
            self.stream_plan(self.win_plan(l, 's') + self.tail_plan(l))
        for l in range(self.depth):
            fw.push()
            self.load_layer_small(l)
            self.ada(l)
            fw.push()
            self.hTs[1] = fw.sb([128, 8, 512], BF16, "hTb")
            self.oTs[1] = [fw.sb([128, 4, 512], BF16, f"oTb{n}") for n in range(4)]
            for n, br in enumerate("ABCD"):
                if br not in self.branches:
                    self.zero_branch(n)
            for half in range(2):
                self.make_h(0, half * 512, half * 512, 512)
            for half in range(2):
                self.phases(l, "p", half)
            self.merge_out(l, 0, 0, NPT)
            fw.pop()
            self.hTs[1] = None
            self.oTs[1] = None
            self.make_h(1, NPT, 0, 512)
            self.sample_pass(l)
            self.merge_out(l, 1, NPT, NST)
            fw.pop()
        fw.push()
        self.sp = fw.sb([128, NSP], F32, "spf")
        self.load(self.sp, self.sp[:], self.d["sp"], self.d["sp"][0, :, :])
        self.final_out()
        fw.pop()
        for e in ("sp",):
            for ev in list(fw.dma_out.values()):
                fw._wait(e, ev)
        fw.barrier()
        return self.nc

    def phases(self, l, grp, half):
        h0 = half * 512
        if "A" in self.branches:
            self.phaseA_prompt(l, half)
        if "B" in self.branches:
            self.phaseB_prompt(l, half)
        if "C" in self.branches:
            self.phaseC(l, h0, 512, h0)
        if "D" in self.branches:
            self.phaseD_prompt(l, half)


_CFG = {"branches": "ABCD", "depth": DEPTH}


def make_in_maps(inp):
    f32 = lambda a: np.ascontiguousarray(np.asarray(a, dtype=np.float32))
    inp = {k: np.asarray(v) for k, v in inp.items()}
    wst = build_stream(f32(inp["w_ada"]), f32(inp["w_in"]), f32(inp["w_branch"]), f32(inp["w_merge"]), f32(inp["w_out"]))
    sp = build_small(inp)
    cst = build_consts()
    L = DEPTH
    wup = f32(inp["rwkv_w_up"]).transpose(0, 2, 1, 3).reshape(L, 64, 1024)
    aup = f32(inp["rwkv_a_up"]).transpose(0, 2, 1, 3).reshape(L, 64, 1024)
    wqu = f32(inp["mla_w_q_up"]).reshape(L, 2, 128, 8, 96)
    wq = wqu.transpose(0, 2, 1, 3, 4).reshape(L, 128, 2 * 8 * 96)
    wqs = wqu[..., 64 + _SWAP32].transpose(0, 2, 1, 3, 4).reshape(L, 128, 2 * 8 * 32)
    wkvu = f32(inp["mla_w_kv_up"]).reshape(L, 128, 8, 128)
    wkk = np.ascontiguousarray(wkvu[..., :64]).reshape(L, 128, 512)
    wkv = np.ascontiguousarray(wkvu[..., 64:]).reshape(L, 128, 512)
    wsT = f32(inp["gmlp_w_s"]).transpose(0, 3, 1, 2).reshape(L, 128, 512)
    bsb = np.ascontiguousarray(np.broadcast_to(f32(inp["gmlp_b_s"]).reshape(L, 1, 512), (L, 128, 512)))
    maps = []
    xp = f32(inp["x_prompt"])
    xs = f32(inp["x_sample"])
    for c in range(NCORE):
        s, j = c // 4, c % 4
        dftT, rope = build_core_consts(j)
        cond = np.stack([f32(inp["c_ctx"]).reshape(8, 128).T, f32(inp["c"])[s].reshape(8, 128).T], axis=2).reshape(128, 16)
        o, _ = SP_OFF["rw"]
        om, _ = SP_OFF["mu_rkv"]
        spo = np.concatenate([sp[:, :, o:o + 36].reshape(L, 128, 9, 4)[:, :, :, j],
                              sp[:, :, om:om + 12].reshape(L, 128, 3, 4)[:, :, :, j]], axis=2)
        spo = np.ascontiguousarray(spo)
        st0 = np.stack([f32(inp["state_rwkv_fwd"])[s, :, 2 * j:2 * j + 2], f32(inp["state_rwkv_bwd"])[s, :, 2 * j:2 * j + 2]],
                       axis=1)
        st0 = st0.transpose(0, 1, 2, 4, 3).reshape(L, 2, 128, 64)
        p = np.arange(128)
        idx1 = np.zeros((128, 12), np.int32)
        for q in range(4):
            idx1[:, 0 * 4 + q] = q * 1024 + 128 * j + p
            idx1[:, 1 * 4 + q] = q * 1024 + 512 + 128 * j + p
            idx1[:, 2 * 4 + q] = q * 864 + 128 * j + p
        idx2 = np.zeros((128, 4), np.int32)
        for r in range(4):
            idx2[:, r] = (r * 4 + j) * 128 + p
        m = dict(
            xp=np.ascontiguousarray(xp[4 * c:4 * c + 4].reshape(NPT, D).T),
            xs=np.ascontiguousarray(xs[s, 512 * j:512 * j + 512].T),
            wst=wst, sp=sp, cond=np.ascontiguousarray(cond),
            ident=cst["ident"], bones=cst["bones"], mask=cst["mask"],
            dftd_p=cst["dftd_p"], dftd_s=cst["dftd_s"], dftT_p=cst["dftT_p"],
            dftT_s=np.ascontiguousarray(dftT.reshape(16, 128, 1024)), rope=np.ascontiguousarray(rope.reshape(32, 1024)),
            wup=wup, aup=aup,
            wupo=np.ascontiguousarray(wup.reshape(L, 64, 2, 4, 128)[:, :, :, j]).reshape(L, 64, 256),
            aupo=np.ascontiguousarray(aup.reshape(L, 64, 2, 4, 128)[:, :, :, j]).reshape(L, 64, 256),
            spo=spo, wq=wq, wqs=wqs, wkk=wkk, wkv=wkv, wsT=wsT, bsb=bsb,
            st0=np.ascontiguousarray(st0),
            cckv=np.ascontiguousarray(f32(inp["cache_mla_ckv"])[s].transpose(0, 2, 1)),
            ckr=np.ascontiguousarray(f32(inp["cache_mla_krope"])[s].transpose(0, 2, 1)),
            idx1=idx1, idx2=idx2,
        )
        maps.append(m)
    return maps


def assemble(results):
    B = 32
    yp = np.zeros((B, SEQ, D), np.float32)
    ys = np.zeros((2, DSEQ, D), np.float32)
    sf = np.zeros((B, DEPTH, 8, 64, 64), np.float32)
    sbw = np.zeros((B, DEPTH, 8, 64, 64), np.float32)
    ckv = np.zeros((B, DEPTH, SEQ, 128), np.float32)
    kr = np.zeros((B, DEPTH, SEQ, 32), np.float32)
    for c in range(NCORE):
        r = results[c]
        s, j = c // 4, c % 4
        yp[4 * c:4 * c + 4] = np.asarray(r["yp"]).T.reshape(4, SEQ, D)
        ys[s, 512 * j:512 * j + 512] = np.asarray(r["ys"]).T
        st = np.asarray(r["stout"]).reshape(DEPTH, 2, 4, 4, 2, 64, 64)
        st = st.transpose(1, 2, 0, 3, 4, 6, 5).reshape(2, 4, DEPTH, 8, 64, 64)
        sf[4 * c:4 * c + 4] = st[0]
        sbw[4 * c:4 * c + 4] = st[1]
        ck = np.asarray(r["ckvout"]).reshape(DEPTH, 128, 4, SEQ)
        ckv[4 * c:4 * c + 4] = ck.transpose(2, 0, 3, 1)
        k2 = np.asarray(r["krout"]).reshape(DEPTH, 32, 4, SEQ)
        kr[4 * c:4 * c + 4] = k2.transpose(2, 0, 3, 1)
    return yp, ys, sf, sbw, ckv, kr


def kernel(**inputs):
    prog = Prog(dict(_CFG))
    nc = prog.build()
    maps = make_in_maps(inputs)
    res = run_bass_kernel_spmd(nc, maps, core_ids=list(range(NCORE)))
    return assemble(res.results)
```

```python
import numpy as np
import ml_dtypes
import concourse.bass as bass
import concourse.mybir as mybir
from concourse.bass_utils import run_bass_kernel_spmd

F32 = mybir.dt.float32
BF16 = mybir.dt.bfloat16
I32 = mybir.dt.int32
ALU = mybir.AluOpType
AF = mybir.ActivationFunctionType
AX = mybir.AxisListType

D = 1024
DEPTH = 2
SEQ = 256
DSEQ = 2048
PAST = 512
NCORE = 8
NPT = 1024
NST = 512
NTOK = NPT + NST
EPS = 1e-6
GN_EPS = 64e-5
EP = 24000
AGP = {"rk": 1024, "vx": 864, "f": 512}
VX_OFF = dict(v=0, lora=512, ckv=704, kr=832)


class Buf:
    def __init__(self, h, name, space):
        self.h = h
        self.name = name
        self.space = space
        self.w = {}
        self.r = {}
        self.dsem = None
        self.dcnt = 0
        self.dsid = None
        self.dcls = None

    def __getitem__(self, idx):
        return self.h[idx]

    def ap(self):
        return self.h.ap() if self.space == "dram" else self.h[:]


class FW:
    def __init__(self, nc):
        self.nc = nc
        self.E = {"pe": nc.tensor, "act": nc.scalar, "dve": nc.vector, "pool": nc.gpsimd, "sp": nc.sync}
        self.cnt = {e: 0 for e in self.E}
        self.esem = {e: [] for e in self.E}
        self.waited = {e: {} for e in self.E}
        self.pend = {e: [] for e in self.E}
        self.nbuf = 0
        self.ninst = 0
        self.dma_out = {}
        self.sem_pool = []
        self.stack = []
        self.rots = {}
        self.free_sems = {"pool": [], "hw": []}

    def sb(self, shape, dt, name=None):
        self.nbuf += 1
        name = name or "t"
        g = self.nc.sbuf_tensor(f"{name}_{self.nbuf}", list(shape), dt)
        h = g.__enter__()
        b = Buf(h, name, "sbuf")
        if self.stack:
            self.stack[-1].append((g, b))
        return b

    def ps(self, shape, dt=F32, name=None):
        self.nbuf += 1
        name = name or "p"
        h = self.nc.alloc_psum_tensor(f"{name}_{self.nbuf}", list(shape), dt)
        return Buf(h, name, "psum")

    def dram(self, name, shape, dt, kind=None):
        if kind is None:
            h = self.nc.dram_tensor(name, list(shape), dt)
        else:
            h = self.nc.dram_tensor(name, list(shape), dt, kind=kind)
        return Buf(h, name, "dram")

    def rot(self, shape, dt, name, n=2):
        key = (len(self.stack), name)
        if key not in self.rots:
            self.rots[key] = [[self.sb(shape, dt, name) for _ in range(n)], 0]
        ent = self.rots[key]
        t = ent[0][ent[1] % n]
        ent[1] += 1
        return t

    def push(self):
        self.stack.append([])

    def pop(self):
        self.barrier()
        depth = len(self.stack)
        for k in [k for k in self.rots if k[0] == depth]:
            del self.rots[k]
        for g, b in reversed(self.stack.pop()):
            if b.dsem is not None:
                self.free_sems[b.dcls].append((b.dsem, b.dcnt))
                b.dsem = None
            g.__exit__(None, None, None)

    def _sem_for(self, eng, k):
        i = (k - 1) // EP
        while len(self.esem[eng]) <= i:
            self.esem[eng].append(self.nc.alloc_semaphore(f"s_{eng}_{len(self.esem[eng])}"))
        return self.esem[eng][i], (k - 1) % EP + 1

    def _wait(self, eng, ev):
        if ev is None:
            return
        if ev[0] == "eng":
            _, e2, k = ev
            if e2 == eng and eng == "pe":
                return
            key = ("eng", e2, (k - 1) // EP)
            sem, val = self._sem_for(e2, k)
        else:
            _, sem, val, sid = ev
            key = ("sem", sid)
        if self.waited[eng].get(key, 0) >= val:
            return
        self.waited[eng][key] = val
        self.E[eng].wait_ge(sem, val)

    def _check_pend(self, eng, b):
        for e2, lst in self.pend.items():
            if e2 == eng:
                continue
            for (pb, _) in lst:
                if pb is b:
                    raise RuntimeError(f"buffer {b.name} has pending unsignalled access on {e2}, touched by {eng}")

    def _deps(self, eng, reads, writes):
        evs = []
        for b in reads:
            self._check_pend(eng, b)
            evs.extend(b.w.values())
        for b in writes:
            self._check_pend(eng, b)
            for wv in b.w.values():
                if not (wv[0] == "eng" and wv[1] == eng):
                    evs.append(wv)
            for ev in b.r.values():
                if ev[0] == "eng" and ev[1] == eng and eng == "pe":
                    continue
                evs.append(ev)
        for ev in evs:
            self._wait(eng, ev)

    def op(self, eng, fn, reads=(), writes=(), signal=True):
        self._deps(eng, reads, writes)
        ins = fn()
        self.ninst += 1
        if signal:
            self.cnt[eng] += 1
            k = self.cnt[eng]
            sem, val = self._sem_for(eng, k)
            ins.then_inc(sem, 1)
            ev = ("eng", eng, k)
            for (pb, kind) in self.pend[eng]:
                if kind == "r":
                    pb.r[eng] = ev
                else:
                    pb.w = {eng: ev}
                    pb.r = {}
            self.pend[eng] = []
            for b in reads:
                b.r[eng] = ev
            for b in writes:
                b.w = {eng: ev}
                b.r = {}
        else:
            for b in reads:
                self.pend[eng].append((b, "r"))
            for b in writes:
                self.pend[eng].append((b, "w"))
        return ins

    def _dma_sem(self, b, q="sp"):
        cls = "pool" if q == "pool" else "hw"
        if b.dsem is None:
            self.nbuf += 1
            if self.free_sems[cls]:
                b.dsem, b.dcnt = self.free_sems[cls].pop()
            else:
                b.dsem = self.nc.alloc_semaphore(f"d_{b.name}_{self.nbuf}")
            b.dsid = self.nbuf
            b.dcls = cls
        elif cls == "pool" and b.dcls != "pool":
            raise RuntimeError(f"buffer {b.name}: software DMA on a semaphore first used by a hardware-DGE DMA")
        return b.dsem

    def dma(self, q, out_b, out_ap, in_b, in_ap, sem_owner=None, inc=16, fn=None, extra_reads=(), **kw):
        eng = q
        evs = []
        for b in (in_b, out_b) + tuple(extra_reads):
            self._check_pend(eng, b)
        evs.extend(in_b.w.values())
        for b in extra_reads:
            evs.extend(b.w.values())
        owner = sem_owner or (out_b if out_b.space != "dram" else in_b)
        sem = self._dma_sem(owner, q)
        for wv in out_b.w.values():
            if wv[0] != "sem":
                evs.append(wv)
        for ev in out_b.r.values():
            evs.append(ev)
        for ev in evs:
            self._wait(eng, ev)
        if fn is None:
            ins = self.E[eng].dma_start(out=out_ap, in_=in_ap, **kw)
        else:
            ins = fn()
        self.ninst += 1
        owner.dcnt += inc
        ins.then_inc(sem, inc)
        ev = ("sem", sem, owner.dcnt, owner.dsid)
        in_b.r[("dma", owner.dsid)] = ev
        for b in extra_reads:
            b.r[("dma", owner.dsid)] = ev
        out_b.w = {k: v for k, v in out_b.w.items() if v[0] == "sem"}
        out_b.w[("dma", owner.dsid)] = ev
        out_b.r = {}
        self.dma_out[owner.dsid] = ev
        return ev

    def wait_buf(self, eng, b):
        self._check_pend(eng, b)
        for ev in b.w.values():
            self._wait(eng, ev)
        for ev in b.r.values():
            self._wait(eng, ev)

    def barrier(self):
        for e in self.E:
            if self.pend[e]:
                raise RuntimeError(f"barrier with pending unsignalled ops on {e}")
        last = []
        for e in ("pe", "act", "dve", "pool"):
            if self.cnt[e] > 0:
                last.append(("eng", e, self.cnt[e]))
        for e in self.E:
            for ev in last:
                if ev[1] == e and e == "pe":
                    continue
                self._wait(e, ev)
            for ev in self.dma_out.values():
                self._wait(e, ev)
        self.dma_out = {}


COLS = dict(r=(0, 512), k=(512, 1024), v=(1024, 1536), wdf=(1536, 1600), wdb=(1600, 1664), ad=(1664, 1728),
            ga=(1728, 2240), qd=(2240, 2496), kvd=(2496, 2624), kr=(2624, 2656), gb=(2656, 3168),
            u=(3168, 3680), vc=(3680, 4192), gc=(4192, 4704), f=(4704, 5216), gd=(5216, 5728))


def _rng(name):
    a, b = COLS[name]
    return np.arange(a, b)


_SWAP32 = np.arange(32).reshape(16, 2)[:, ::-1].reshape(32)

WIN_BLOCKS = [
    ("A0", [_rng("r")]), ("A1", [_rng("k")]), ("A2", [_rng("v")]),
    ("A3", [_rng("wdf"), _rng("wdb"), _rng("ad")]), ("A4", [_rng("ga")]),
    ("B0", [_rng("qd"), _rng("kvd"), _rng("kr"), _rng("kr")[_SWAP32]]), ("B1", [_rng("gb")]),
    ("C0", [_rng("u")]), ("C1", [_rng("vc")]), ("C2", [_rng("gc")]),
    ("D0", [_rng("f")]), ("D1", [_rng("gd")]),
]
WIN_W = {n: int(sum(len(c) for c in cols)) for n, cols in WIN_BLOCKS}
WIN_IDX = {n: 6 + i for i, (n, _) in enumerate(WIN_BLOCKS)}
BLK_PER_LAYER = 36
FBLK = 4096


def _kcp(w, W):
    return w.reshape(8, 128, W).transpose(1, 0, 2).reshape(128, 8 * W)


def build_stream(w_ada, w_in, w_branch, w_merge, w_out):
    st = np.zeros((DEPTH * BLK_PER_LAYER, 128, FBLK), np.float32)
    for l in range(DEPTH):
        base = l * BLK_PER_LAYER
        for b in range(6):
            st[base + b, :, :] = _kcp(w_ada[l][:, 512 * b:512 * b + 512], 512)
        for i, (n, cols) in enumerate(WIN_BLOCKS):
            cc = np.concatenate(cols)
            W = len(cc)
            st[base + 6 + i, :, :8 * W] = _kcp(w_in[l][:, cc], W)
        for d in range(8):
            cc = np.concatenate([n * 1024 + d * 128 + np.arange(128) for n in range(4)])
            st[base + 18 + 2 * d, :, :] = _kcp(w_merge[l][:, cc], 512)
            wb = w_branch[l][:, :, d * 128:(d + 1) * 128]
            wb = wb.reshape(4, 4, 128, 128).transpose(2, 0, 1, 3)
            st[base + 19 + 2 * d, :, :2048] = wb.reshape(128, 2048)
        for b in range(2):
            st[base + 34 + b, :, :] = _kcp(w_out[l][:, 512 * b:512 * b + 512], 512)
    return st


SP_OFF = {}
_o = 0
for _n, _w in [("norm_g", 8), ("b_ada", 24), ("mu_rkv", 12), ("mu_lora", 3), ("b_merge", 32), ("rw", 36),
               ("qn", 2), ("kvn", 1), ("gln_g", 4), ("gln_b", 4), ("fin_g", 8)]:
    SP_OFF[_n] = (_o, _w)
    _o += _w
NSP = _o
RW_NAMES = ["w0_f", "w0_b", "a0_f", "a0_b", "k_k", "k_a", "r_k", "ln_g", "ln_b"]


def _pc(v, n):
    return np.asarray(v, np.float32).reshape(n, 128).T


def build_small(inp):
    sp = np.zeros((DEPTH, 128, NSP), np.float32)
    for l in range(DEPTH):
        def put(name, arr):
            o, w = SP_OFF[name]
            sp[l, :, o:o + w] = arr
        put("norm_g", _pc(inp["norm_g"][l], 8))
        put("b_ada", _pc(inp["b_ada"][l], 24))
        put("mu_rkv", _pc(inp["shift_mu"][l][:1536], 12))
        ml = np.zeros((128, 3), np.float32)
        ml[:64, :] = inp["shift_mu"][l][1536:1728].reshape(3, 64).T
        put("mu_lora", ml)
        bm = inp["b_merge"][l].reshape(4, 8, 128)
        put("b_merge", bm.transpose(2, 1, 0).reshape(128, 32))
        rwv = [inp["rwkv_w0"][l][0], inp["rwkv_w0"][l][1], inp["rwkv_a0"][l][0], inp["rwkv_a0"][l][1],
               inp["rwkv_k_k"][l], inp["rwkv_k_a"][l], inp["rwkv_r_k"][l].reshape(512), inp["rwkv_ln_g"][l],
               inp["rwkv_ln_b"][l]]
        rw = np.stack([_pc(v, 4) for v in rwv], axis=1)
        put("rw", rw.reshape(128, 36))
        put("qn", _pc(inp["mla_q_norm"][l], 2))
        put("kvn", _pc(inp["mla_kv_norm"][l], 1))
        put("gln_g", _pc(inp["gmlp_ln_g"][l], 4))
        put("gln_b", _pc(inp["gmlp_ln_b"][l], 4))
        put("fin_g", _pc(inp["final_norm_g"], 8))
    return sp


def rw_col(name, pair):
    o, _ = SP_OFF["rw"]
    return o + RW_NAMES.index(name) * 4 + pair


def build_consts():
    c = {}
    c["ident"] = np.eye(128, dtype=np.float32)
    hb = np.arange(128) // 64
    c["bones"] = (hb[:, None] == hb[None, :]).astype(np.float32)
    p = np.arange(128)[:, None]
    f = np.arange(128)[None, :]
    mk = {}
    bd32 = ((p // 32) == (f // 32)).astype(np.float32)
    od64 = (((p // 64) == (f // 64)) & ((p // 32) != (f // 32))).astype(np.float32)
    od128 = ((p // 64) != (f // 64)).astype(np.float32)
    for dname, ms, mi, mt in (("f", p < f, p <= f, f < p), ("b", p > f, p >= f, f > p)):
        ms = ms.astype(np.float32)
        mi = mi.astype(np.float32)
        mtf = -(mt.astype(np.float32))
        mk[dname] = np.concatenate([ms, mi, -ms * bd32, mi, mtf * bd32, mtf * bd32, mtf * od64, mtf * od64,
                                    mtf * od128, mtf * od128], axis=1)
    c["mask"] = np.stack([mk["f"], mk["b"]], axis=1).reshape(128, 2 * 1280)
    dd = np.arange(128)
    ang = 2 * np.pi * np.outer(dd, dd) / 128.0
    for nm, T in (("dftd_p", SEQ), ("dftd_s", DSEQ)):
        sc = 1.0 / np.sqrt(T * 128.0)
        c[nm] = np.concatenate([np.cos(ang) * sc, -np.sin(ang) * sc], axis=1).astype(np.float32)
    tt = np.arange(SEQ)
    angp = 2 * np.pi * np.outer(tt, tt) / SEQ
    cp = np.stack([np.cos(angp), np.sin(angp)], axis=1)
    c["dftT_p"] = cp.reshape(2, 128, 2, SEQ).transpose(1, 0, 2, 3).reshape(128, 2 * 2 * SEQ).astype(np.float32)
    return c


def build_core_consts(j):
    t = np.arange(DSEQ)
    k1 = 512 * j + np.arange(512)
    ang = 2 * np.pi * ((np.outer(t, k1)) % DSEQ) / DSEQ
    cs = np.stack([np.cos(ang), np.sin(ang)], axis=1)
    dftT = cs.reshape(16, 128, 2, 512).astype(np.float32)
    pos = 512 * j + np.arange(512)
    row = (pos // 64).astype(np.float32)
    col = (pos % 64).astype(np.float32)
    inv = (10000.0 ** (-np.arange(8, dtype=np.float32) / 8)).astype(np.float32)
    ang = np.concatenate([row[:, None] * inv, col[:, None] * inv], axis=-1).astype(np.float32)
    cos = np.cos(ang).astype(np.float32)
    sin = np.sin(ang).astype(np.float32)
    COS = np.repeat(cos, 2, axis=1).T
    SIN = np.stack([-sin, sin], axis=2).reshape(512, 32).T
    rope = np.stack([COS, SIN], axis=1).astype(np.float32)
    return dftT, rope


class Prog:
    def __init__(self, cfg):
        self.cfg = cfg
        self.branches = cfg.get("branches", "ABCD")
        self.depth = cfg.get("depth", DEPTH)
        nc = bass.Bass("TRN2", target_bir_lowering=False)
        self.nc = nc
        fw = FW(nc)
        self.fw = fw
        self.V, self.A, self.G, self.T = nc.vector, nc.scalar, nc.gpsimd, nc.tensor
        di = lambda n, s, dt=F32: fw.dram(n, s, dt, kind="ExternalInput")
        do = lambda n, s, dt=F32: fw.dram(n, s, dt, kind="ExternalOutput")
        self.d = dict(
            xp=di("xp", [D, NPT]), xs=di("xs", [D, NST]),
            wst=di("wst", [DEPTH * BLK_PER_LAYER, 128, FBLK]),
            sp=di("sp", [DEPTH, 128, NSP]), cond=di("cond", [128, 16]),
            ident=di("ident", [128, 128]), bones=di("bones", [128, 128]), mask=di("mask", [128, 2560]),
            dftd_p=di("dftd_p", [128, 256]), dftd_s=di("dftd_s", [128, 256]), dftT_p=di("dftT_p", [128, 1024]),
            dftT_s=di("dftT_s", [16, 128, 1024]), rope=di("rope", [32, 1024]),
            wup=di("wup", [DEPTH, 64, 1024]), aup=di("aup", [DEPTH, 64, 1024]),
            wupo=di("wupo", [DEPTH, 64, 256]), aupo=di("aupo", [DEPTH, 64, 256]),
            spo=di("spo", [DEPTH, 128, 12]),
            wq=di("wq", [DEPTH, 128, 2 * 8 * 96]), wqs=di("wqs", [DEPTH, 128, 2 * 8 * 32]),
            wkk=di("wkk", [DEPTH, 128, 512]), wkv=di("wkv", [DEPTH, 128, 512]),
            wsT=di("wsT", [DEPTH, 128, 512]), bsb=di("bsb", [DEPTH, 128, 512]),
            st0=di("st0", [DEPTH, 2, 128, 64]), cckv=di("cckv", [DEPTH, 128, PAST]),
            ckr=di("ckr", [DEPTH, 32, PAST]),
            idx1=di("idx1", [128, 12], I32), idx2=di("idx2", [128, 4], I32),
            yp=do("yp", [D, NPT]), ys=do("ys", [D, NST]),
            stout=do("stout", [DEPTH * 2 * 4 * 4 * 128, 64]),
            ckvout=do("ckvout", [DEPTH, 128, NPT]), krout=do("krout", [DEPTH, 32, NPT]),
        )
        self.ag1_in = [{k: fw.dram(f"ag1i{k}{l}", [n, 512], BF16) for k, n in AGP.items()} for l in range(DEPTH)]
        self.ag1_out = [{k: fw.dram(f"ag1o{k}{l}", [4 * n, 512], BF16) for k, n in AGP.items()} for l in range(DEPTH)]
        self.ag2_in = [fw.dram(f"ag2i{l}", [512, 512], BF16) for l in range(DEPTH)]
        self.ag2_out = [fw.dram(f"ag2o{l}", [2048, 512], BF16) for l in range(DEPTH)]
        self.psb = [fw.ps([128, 512], F32, f"bank{i}") for i in range(6)]
        self.pst = [fw.ps([128, 1024], BF16, f"pst{i}") for i in range(2)]
        self.pst_rr = 0
        self.ps_rr = 0
        self.xT = fw.sb([128, 8, NTOK], F32, "xT")
        self.hTs = [fw.sb([128, 8, 512], BF16, "hTa"), None]
        self.oTs = [[fw.sb([128, 4, 512], BF16, f"oTa{n}") for n in range(4)], None]
        self.slots = None
        self.slot_rr = 0
        self.plan = []
        self.plan_pos = 0
        self.stg = None
        self.blocks_left = 0
        self.dma_issued = {}
        self.load_consts()

    ps_range = (0, 4)

    def nps(self, lo=None, hi=None):
        lo = self.ps_range[0] if lo is None else lo
        hi = self.ps_range[1] if hi is None else hi
        n = hi - lo
        b = self.psb[lo + (self.ps_rr % n)]
        self.ps_rr += 1
        return b

    def dve(self, fn, r, w):
        return self.fw.op("dve", fn, r, w)

    def rsqrt(self, out_buf, out_ap, in_buf, in_ap):
        self.act(lambda: self.A.activation(in_ap, in_ap, AF.Sqrt), [in_buf], [in_buf])
        self.dve(lambda: self.V.reciprocal(out_ap, in_ap), [in_buf], [out_buf])

    def act(self, fn, r, w):
        return self.fw.op("act", fn, r, w)

    def pool(self, fn, r, w):
        return self.fw.op("pool", fn, r, w)

    def pe(self, fn, r, w, signal=True):
        return self.fw.op("pe", fn, r, w, signal=signal)

    def load(self, dst, dst_ap, src, src_ap, q="sp"):
        if dst_ap.dtype != src_ap.dtype:
            q = "pool"
        return self.fw.dma(q, dst, dst_ap, src, src_ap)

    def load_consts(self):
        fw, d = self.fw, self.d
        self.ident = fw.sb([128, 128], BF16, "ident")
        self.bones = fw.sb([128, 128], BF16, "bones")
        self.mask = fw.sb([128, 2560], BF16, "mask")
        self.ones = fw.sb([128, 128], BF16, "ones")
        self.onesf = fw.sb([128, 128], F32, "onesf")
        self.rope = None
        self.cond = fw.sb([128, 16], F32, "cond")
        self.idx1 = fw.sb([128, 12], I32, "idx1")
        self.idx2 = fw.sb([128, 4], I32, "idx2")
        for nm in ("ident", "bones", "mask", "cond", "idx1", "idx2"):
            t = getattr(self, nm)
            self.load(t, t[:], d[nm], d[nm].ap())
        self.pool(lambda: self.G.memset(self.ones[:], 1.0), [], [self.ones])
        self.pool(lambda: self.G.memset(self.onesf[:], 1.0), [], [self.onesf])
        self.hm = fw.sb([128, 2], F32, "hm")
        self.dve(lambda: self.V.tensor_copy(self.hm[:, 0:2], self.bones[:, 0:128:64]), [self.bones], [self.hm])
        xv = self.xT
        self.load(xv, xv[:, :, 0:NPT], d["xp"], d["xp"].ap().rearrange("(k p) t -> p k t", p=128))
        self.load(xv, xv[:, :, NPT:NTOK], d["xs"], d["xs"].ap().rearrange("(k p) t -> p k t", p=128))

    def stream_plan(self, ids):
        self.plan.extend(ids)

    def stream_begin(self, nblocks, depth=1):
        fw = self.fw
        self.sdepth = depth
        self.slots = [fw.sb([128, FBLK], BF16, f"slot{i}") for i in range(depth + 1)]
        self.stg = [fw.sb([128, FBLK // 2], F32, f"wstg{i}") for i in range(2 * depth)]
        self.blocks_left = nblocks
        self.scope_end = self.plan_pos + nblocks
        self.dma_issued = {}

    def _issue_dma(self, pos):
        blk = self.plan[pos]
        src = self.d["wst"]
        for hf in range(2):
            st = self.stg[(2 * pos + hf) % len(self.stg)]
            self.fw.dma("sp", st, st[:, :], src, src[blk, :, hf * (FBLK // 2):(hf + 1) * (FBLK // 2)])
        self.dma_issued[pos] = True

    def next_block(self, blk):
        pos = self.plan_pos
        assert self.plan[pos] == blk, (pos, self.plan[pos], blk)
        assert self.blocks_left > 0
        if pos not in self.dma_issued:
            self._issue_dma(pos)
        slot = self.slots[pos % len(self.slots)]
        for hf in range(2):
            st = self.stg[(2 * pos + hf) % len(self.stg)]
            if hf == 0:
                self.dve(lambda hf=hf, st=st: self.V.tensor_copy(slot[:, hf * (FBLK // 2):(hf + 1) * (FBLK // 2)], st[:, :]), [st], [slot])
            else:
                self.act(lambda hf=hf, st=st: self.A.copy(slot[:, hf * (FBLK // 2):(hf + 1) * (FBLK // 2)], st[:, :]), [st], [slot])
        self.plan_pos += 1
        self.blocks_left -= 1
        return slot

    def prefetch_next(self):
        for pos in range(self.plan_pos, min(self.plan_pos + self.sdepth, self.scope_end)):
            if pos not in self.dma_issued:
                self._issue_dma(pos)

    def load_layer_small(self, l):
        fw, d = self.fw, self.d
        self.sp = fw.sb([128, NSP], F32, "sp")
        self.load(self.sp, self.sp[:], d["sp"], d["sp"][l, :, :])
        sp = self.sp
        o, _ = SP_OFF["mu_rkv"]
        self.mu1 = fw.sb([128, 15], F32, "mu1")
        self.muh = fw.sb([128, 15], F32, "muh")
        self.dve(lambda: self.V.tensor_scalar(self.mu1[:], sp[:, o:o + 15], -1.0, 1.0, ALU.mult, ALU.add), [sp], [self.mu1])
        self.dve(lambda: self.V.tensor_scalar(self.muh[:], sp[:, o:o + 15], 0.5, None, ALU.mult), [sp], [self.muh])
        o2, _ = SP_OFF["rw"]
        self.rwh = fw.sb([128, 16], F32, "rwh")
        self.dve(lambda: self.V.tensor_scalar(self.rwh[:], sp[:, o2:o2 + 16], 0.5, None, ALU.mult), [sp], [self.rwh])
        ob, _ = SP_OFF["b_merge"]
        self.bmh = fw.sb([128, 32], F32, "bmh")
        self.dve(lambda: self.V.tensor_scalar(self.bmh[:], sp[:, ob:ob + 32], 0.5, None, ALU.mult), [sp], [self.bmh])

    def spc(self, name, i=0, n=1):
        o, _ = SP_OFF[name]
        return self.sp[:, o + i:o + i + n]

    def ada(self, l):
        fw = self.fw
        sc = fw.sb([128, 16], BF16, "scond")
        th = fw.sb([128, 16], F32, "cth")
        c = self.cond
        self.act(lambda: self.A.activation(th[:], c[:], AF.Tanh, scale=0.5), [c], [th])
        t2 = fw.sb([128, 16], F32, "ct2")
        self.dve(lambda: self.V.scalar_tensor_tensor(t2[:], th[:], 1.0, c[:], ALU.add, ALU.mult), [th, c], [t2])
        self.dve(lambda: self.V.tensor_scalar(sc[:], t2[:], 0.5, None, ALU.mult), [t2], [sc])
        ps = self.psb[5]
        self.mod = fw.sb([128, 24, 2], F32, "mod")
        self.gmod = fw.sb([128, 8, 2], F32, "gmod")
        fw.push()
        self.stream_begin(6, depth=2)
        for b in range(6):
            slot = self.next_block(l * BLK_PER_LAYER + b)
            for n in range(4):
                m = b * 4 + n
                for kc in range(8):
                    self.pe(lambda kc=kc, n=n, m=m, slot=slot: self.T.matmul(
                        ps[:, 2 * m:2 * m + 2], slot[:, kc * 512 + n * 128:kc * 512 + n * 128 + 128],
                        sc[:, 2 * kc:2 * kc + 2], start=(kc == 0), stop=(kc == 7)),
                        [slot, sc], [ps], signal=(kc == 7 and n == 3))
            self.prefetch_next()
        fw.pop()
        ob, _ = SP_OFF["b_ada"]
        for cc in range(2):
            self.dve(lambda cc=cc: self.V.tensor_tensor(self.mod[:, :, cc], ps[:, cc:48:2], self.sp[:, ob:ob + 24], ALU.add),
                     [ps, self.sp], [self.mod])
        og, _ = SP_OFF["norm_g"]
        for cc in range(2):
            self.dve(lambda cc=cc: self.V.scalar_tensor_tensor(self.gmod[:, :, cc], self.mod[:, 8:16, cc], 1.0,
                                                               self.sp[:, og:og + 8], ALU.add, ALU.mult),
                     [self.mod, self.sp], [self.gmod])

    def rstd_tile(self, src, views, nfeat, out, N):
        fw = self.fw
        ps = self.nps()
        nk = len(views)
        for i, v in enumerate(views):
            sq = fw.rot([128, 512], BF16, "sq")
            self.act(lambda v=v, sq=sq: self.A.activation(sq[:, :N], v, AF.Square), [src], [sq])
            self.pe(lambda i=i, sq=sq: self.T.matmul(ps[:, :N], self.ones[:], sq[:, :N], start=(i == 0), stop=(i == nk - 1)),
                    [self.ones, sq], [ps], signal=True)
        t = fw.sb([128, 512], F32, "rs_t")
        self.dve(lambda: self.V.tensor_scalar(t[:, :N], ps[:, :N], 1.0 / nfeat, EPS, ALU.mult, ALU.add), [ps], [t])
        self.rsqrt(out, out[:, :N], t, t[:, :N])

    def make_h(self, cc, x0, h0, N):
        fw = self.fw
        fw.push()
        rstd = fw.sb([128, 512], F32, "rstd")
        self.rstd_tile(self.xT, [self.xT[:, kc, x0:x0 + N] for kc in range(8)], float(D), rstd, N)
        for kc in range(8):
            tmp = fw.rot([128, 512], F32, "htmp")
            self.dve(lambda kc=kc, tmp=tmp: self.V.scalar_tensor_tensor(
                tmp[:, :N], self.xT[:, kc, x0:x0 + N], self.gmod[:, kc, cc:cc + 1], rstd[:, :N], ALU.mult, ALU.mult),
                [self.xT, self.gmod, rstd], [tmp])
            self.act(lambda kc=kc, tmp=tmp: self.A.activation(
                self.hTs[h0 // 512][:, kc, 0:N], tmp[:, :N], AF.Identity, bias=self.mod[:, kc, cc:cc + 1], scale=1.0),
                [tmp, self.mod], [self.hTs[h0 // 512]])
        fw.pop()

    def zmm(self, slot, W, c0, w, h0, N, ps=None, prow=0):
        ps = ps or self.nps()
        for kc in range(8):
            self.pe(lambda kc=kc: self.T.matmul(ps[prow:prow + w, :N], slot[:, kc * W + c0:kc * W + c0 + w],
                                                self.hTs[h0 // 512][:, kc, 0:N], start=(kc == 0), stop=(kc == 7)),
                    [slot, self.hTs[h0 // 512]], [ps], signal=(kc == 7))
        return ps

    def silu2(self, ps, rows, N, out_ap, out_buf):
        fw = self.fw
        th = fw.rot([128, 512], F32, "s2th")
        self.act(lambda: self.A.activation(th[:rows, :N], ps[:rows, :N], AF.Tanh, scale=0.5), [ps], [th])
        self.dve(lambda: self.V.scalar_tensor_tensor(out_ap, th[:rows, :N], 1.0, ps[:rows, :N], ALU.add, ALU.mult),
                 [th, ps], [out_buf])

    def gelu2(self, ps, rows, N, out_ap, out_buf):
        fw = self.fw
        u = fw.rot([128, 512], F32, "g2u")
        self.act(lambda: self.A.activation(u[:rows, :N], ps[:rows, :N], AF.Square), [ps], [u])
        self.dve(lambda: self.V.tensor_scalar(u[:rows, :N], u[:rows, :N], 0.044715, 1.0, ALU.mult, ALU.add), [u], [u])
        self.dve(lambda: self.V.tensor_tensor(u[:rows, :N], u[:rows, :N], ps[:rows, :N], ALU.mult), [u, ps], [u])
        self.act(lambda: self.A.activation(u[:rows, :N], u[:rows, :N], AF.Tanh, scale=0.7978845608028654), [u], [u])
        self.dve(lambda: self.V.scalar_tensor_tensor(out_ap, u[:rows, :N], 1.0, ps[:rows, :N], ALU.add, ALU.mult),
                 [u, ps], [out_buf])

    def transpose_to(self, src_buf, src_ap, dst_buf, dst_ap, rows=128, cols=128, eng="act"):
        pt = self.pst[self.pst_rr % 2]
        self.pst_rr += 1
        self.pe(lambda: self.T.transpose(pt[:cols, :rows], src_ap, self.ident[:rows, :rows]), [src_buf, self.ident], [pt])
        if eng == "act":
            self.act(lambda: self.A.copy(dst_ap, pt[:cols, :rows]), [pt], [dst_buf])
        else:
            self.dve(lambda: self.V.tensor_copy(dst_ap, pt[:cols, :rows]), [pt], [dst_buf])

    def phaseC(self, l, h0, N, o0):
        fw = self.fw
        base = l * BLK_PER_LAYER
        fw.push()
        self.stream_begin(3, depth=2)
        U2 = fw.sb([128, 4, 512], BF16, "U2")
        GV = fw.sb([128, 4, 512], BF16, "GV")
        GC2 = fw.sb([128, 4, 512], BF16, "GC2")
        wsT = fw.sb([128, 512], BF16, "wsT")
        bsb = fw.sb([128, 512], F32, "bsb")
        self.load(wsT, wsT[:], self.d["wsT"], self.d["wsT"][l, :, :])
        self.load(bsb, bsb[:], self.d["bsb"], self.d["bsb"][l, :, :])
        slot = self.next_block(base + WIN_IDX["C0"])
        for c in range(4):
            ps = self.zmm(slot, 512, c * 128, 128, h0, N)
            self.gelu2(ps, 128, N, U2[:, c, :N], U2)
        self.prefetch_next()
        slot = self.next_block(base + WIN_IDX["C1"])
        for c in range(4):
            ps = self.zmm(slot, 512, c * 128, 128, h0, N)
            self.gelu2(ps, 128, N, GV[:, c, :N], GV)
        self.prefetch_next()
        slot = self.next_block(base + WIN_IDX["C2"])
        for c in range(4):
            ps = self.zmm(slot, 512, c * 128, 128, h0, N)
            self.silu2(ps, 128, N, GC2[:, c, :N], GC2)
        self.prefetch_next()
        psm = self.nps()
        psq = self.nps()
        for c in range(4):
            self.pe(lambda c=c: self.T.matmul(psm[:, :N], self.ones[:], GV[:, c, :N], start=(c == 0), stop=(c == 3)),
                    [self.ones, GV], [psm], signal=(c == 3))
        for c in range(4):
            sq = fw.rot([128, 512], BF16, "gsq")
            self.act(lambda c=c, sq=sq: self.A.activation(sq[:, :N], GV[:, c, :N], AF.Square), [GV], [sq])
            self.pe(lambda c=c, sq=sq: self.T.matmul(psq[:, :N], self.ones[:], sq[:, :N], start=(c == 0), stop=(c == 3)),
                    [self.ones, sq], [psq], signal=True)
        mu = fw.sb([128, 512], F32, "gmu")
        msq = fw.sb([128, 512], F32, "gmsq")
        var = fw.sb([128, 512], F32, "gvar")
        rstd = fw.sb([128, 512], F32, "grstd")
        self.dve(lambda: self.V.tensor_scalar(mu[:, :N], psm[:, :N], 1.0 / 512, None, ALU.mult), [psm], [mu])
        self.dve(lambda: self.V.tensor_tensor(msq[:, :N], mu[:, :N], mu[:, :N], ALU.mult), [mu], [msq])
        self.dve(lambda: self.V.scalar_tensor_tensor(var[:, :N], psq[:, :N], 1.0 / 512, msq[:, :N], ALU.mult, ALU.subtract),
                 [psq, msq], [var])
        self.dve(lambda: self.V.tensor_scalar(var[:, :N], var[:, :N], 4e-5, None, ALU.add), [var], [var])
        self.rsqrt(rstd, rstd[:, :N], var, var[:, :N])
        VN = fw.sb([128, 4, 512], BF16, "VN")
        for c in range(4):
            t = fw.rot([128, 512], F32, "lnt")
            self.dve(lambda c=c, t=t: self.V.tensor_tensor(t[:, :N], GV[:, c, :N], mu[:, :N], ALU.subtract), [GV, mu], [t])
            self.dve(lambda t=t: self.V.tensor_tensor(t[:, :N], t[:, :N], rstd[:, :N], ALU.mult), [t, rstd], [t])
            self.act(lambda c=c, t=t: self.A.activation(VN[:, c, :N], t[:, :N], AF.Identity, bias=self.spc("gln_b", c),
                                                        scale=self.spc("gln_g", c)), [t, self.sp], [VN])
        nsub = N // 128
        for g in range(4):
            pmix = self.nps()
            for s in range(nsub):
                vtm = fw.rot([128, 128], BF16, "vtm")
                self.transpose_to(VN, VN[:, g, s * 128:(s + 1) * 128], vtm, vtm[:], eng=("act" if s % 2 else "dve"))
                self.pe(lambda g=g, s=s, vtm=vtm: self.T.matmul(pmix[:, s * 128:(s + 1) * 128], vtm[:],
                                                                 wsT[:, g * 128:(g + 1) * 128], start=True, stop=True),
                        [vtm, wsT], [pmix], signal=(s == nsub - 1))
            t = fw.rot([128, 512], F32, "mixt")
            for s in range(nsub):
                self.dve(lambda g=g, s=s, t=t: self.V.tensor_tensor(t[:, s * 128:(s + 1) * 128], pmix[:, s * 128:(s + 1) * 128],
                                                                     bsb[:, g * 128:(g + 1) * 128], ALU.add), [pmix, bsb], [t])
            self.dve(lambda g=g, t=t: self.V.scalar_tensor_tensor(t[:, :N], t[:, :N], 0.25, U2[:, g, :N], ALU.mult, ALU.mult),
                     [t, U2], [t])
            self.dve(lambda g=g, t=t: self.V.tensor_tensor(self.oTs[o0 // 512][2][:, g, 0:N], t[:, :N], GC2[:, g, :N], ALU.mult),
                     [t, GC2], [self.oTs[o0 // 512][2]])
        fw.pop()

    def merge_out(self, l, cc, x0, NT):
        fw = self.fw
        base = l * BLK_PER_LAYER
        ntile = NT // 512
        fw.push()
        self.stream_begin(18, depth=2)
        merged = fw.sb([128, 8, NT], BF16, "merged")
        for d in range(8):
            slotM = self.next_block(base + 18 + 2 * d)
            self.prefetch_next()
            slotB = self.next_block(base + 19 + 2 * d)
            for tt in range(ntile):
                t0 = tt * 512
                acc = fw.rot([128, 512], F32, "macc")
                for n in range(4):
                    psg = self.nps()
                    for kc in range(8):
                        self.pe(lambda kc=kc, n=n, tt=tt: self.T.matmul(psg[:, :], slotM[:, kc * 512 + n * 128:kc * 512 + n * 128 + 128],
                                                                 self.hTs[tt][:, kc, :], start=(kc == 0), stop=(kc == 7)),
                                [slotM, self.hTs[tt]], [psg], signal=(kc == 7))
                    psp = self.nps()
                    for k4 in range(4):
                        self.pe(lambda k4=k4, n=n, tt=tt: self.T.matmul(psp[:, :], slotB[:, (n * 4 + k4) * 128:(n * 4 + k4) * 128 + 128],
                                                                 self.oTs[tt][n][:, k4, :], start=(k4 == 0), stop=(k4 == 3)),
                                [slotB, self.oTs[tt][n]], [psp], signal=(k4 == 3))
                    th = fw.rot([128, 512], F32, "mth")
                    self.act(lambda n=n, th=th, psg=psg: self.A.activation(th[:], psg[:], AF.Tanh, bias=self.bmh[:, d * 4 + n:d * 4 + n + 1],
                                                                           scale=0.5), [psg, self.bmh], [th])
                    if n == 0:
                        self.dve(lambda th=th, psp=psp: self.V.scalar_tensor_tensor(acc[:], th[:], 1.0, psp[:], ALU.add, ALU.mult),
                                 [th, psp], [acc])
                    else:
                        self.dve(lambda th=th, psp=psp: self.V.scalar_tensor_tensor(th[:], th[:], 1.0, psp[:], ALU.add, ALU.mult),
                                 [th, psp], [th])
                        self.dve(lambda th=th: self.V.tensor_tensor(acc[:], acc[:], th[:], ALU.add), [acc, th], [acc])
                self.act(lambda acc=acc, t0=t0: self.A.mul(merged[:, d, t0:t0 + 512], acc[:], 0.5), [acc], [merged])
            self.prefetch_next()
        for b in range(2):
            slotO = self.next_block(base + 34 + b)
            self.prefetch_next()
            for dd in range(4):
                dch = b * 4 + dd
                for tt in range(ntile):
                    t0 = tt * 512
                    ps = self.nps()
                    for kc in range(8):
                        self.pe(lambda kc=kc, dd=dd: self.T.matmul(ps[:, :], slotO[:, kc * 512 + dd * 128:kc * 512 + dd * 128 + 128],
                                                                   merged[:, kc, t0:t0 + 512], start=(kc == 0), stop=(kc == 7)),
                                [slotO, merged], [ps], signal=(kc == 7))
                    self.dve(lambda dch=dch, t0=t0, ps=ps: self.V.scalar_tensor_tensor(
                        self.xT[:, dch, x0 + t0:x0 + t0 + 512], ps[:, :], self.mod[:, 16 + dch, cc:cc + 1],
                        self.xT[:, dch, x0 + t0:x0 + t0 + 512], ALU.mult, ALU.add), [ps, self.mod, self.xT], [self.xT])
        fw.pop()

    def final_out(self):
        fw = self.fw
        for (x0, N, dst) in ((0, 512, ("yp", 0)), (512, 512, ("yp", 512)), (NPT, 512, ("ys", 0))):
            fw.push()
            rstd = fw.sb([128, 512], F32, "frstd")
            self.rstd_tile(self.xT, [self.xT[:, kc, x0:x0 + N] for kc in range(8)], float(D), rstd, N)
            stg = fw.sb([128, 8, 512], F32, "fstg")
            for kc in range(8):
                self.dve(lambda kc=kc: self.V.scalar_tensor_tensor(stg[:, kc, :], self.xT[:, kc, x0:x0 + N], self.spc("fin_g", kc),
                                                                   rstd[:, :], ALU.mult, ALU.mult), [self.xT, self.sp, rstd], [stg])
            dt = self.d[dst[0]]
            ncol = NPT if dst[0] == "yp" else NST
            dview = dt.ap().rearrange("(k p) t -> p k t", p=128)[:, :, dst[1]:dst[1] + N]
            fw.dma("sp", dt, dview, stg, stg[:], sem_owner=self.outsem)
            fw.pop()

    def shift_evac(self, ps, rows, N, nseq, mu1, muh, out_ap, out_buf, tanh=False):
        fw = self.fw
        zt = fw.rot([128, 512], F32, "shz")
        o32 = fw.rot([128, 512], F32, "sho")
        self.act(lambda: self.A.copy(zt[:rows, :N], ps[:rows, :N]), [ps], [zt])
        self.dve(lambda: self.V.tensor_scalar(o32[:rows, :N], zt[:rows, :N], mu1, None, ALU.mult), [zt, self.mu1], [o32])
        z3 = zt[:rows, :N].rearrange("p (s t) -> p s t", s=nseq)
        o3 = o32[:rows, :N].rearrange("p (s t) -> p s t", s=nseq)
        Tq = N // nseq
        self.dve(lambda: self.V.scalar_tensor_tensor(o3[:, :, 1:Tq], z3[:, :, 0:Tq - 1], muh, o3[:, :, 1:Tq], ALU.mult, ALU.add),
                 [zt, self.muh, o32], [o32])
        self.dve(lambda: self.V.scalar_tensor_tensor(o3[:, :, 0:Tq - 1], z3[:, :, 1:Tq], muh, o3[:, :, 0:Tq - 1], ALU.mult, ALU.add),
                 [zt, self.muh, o32], [o32])
        if tanh:
            self.act(lambda: self.A.activation(out_ap, o32[:rows, :N], AF.Tanh), [o32], [out_buf])
        else:
            self.act(lambda: self.A.copy(out_ap, o32[:rows, :N]), [o32], [out_buf])

    def load_rwkv_w(self, l, own):
        fw, d = self.fw, self.d
        ncol = 256 if own else 1024
        self.wup = fw.sb([64, ncol], BF16, "wup")
        self.aup = fw.sb([64, ncol], BF16, "aup")
        sw, sa = (d["wupo"], d["aupo"]) if own else (d["wup"], d["aup"])
        self.load(self.wup, self.wup[:, :], sw, sw[l, :, :])
        self.load(self.aup, self.aup[:, :], sa, sa[l, :, :])

    def phaseA_prompt(self, l, half):
        fw = self.fw
        base = l * BLK_PER_LAYER
        h0 = half * 512
        fw.push()
        self.load_rwkv_w(l, False)
        zz = [fw.sb([128, 4, 512], BF16, nm) for nm in ("zr", "zk", "zv")]
        lo = [fw.sb([64, 512], BF16, nm) for nm in ("twdf", "twdb", "adT")]
        GA2 = fw.sb([128, 4, 512], BF16, "GA2")
        fw.push()
        self.stream_begin(5, depth=2)
        for which in range(3):
            slot = self.next_block(base + WIN_IDX[f"A{which}"])
            for c in range(4):
                ps = self.zmm(slot, 512, c * 128, 128, h0, 512)
                i = which * 4 + c
                self.shift_evac(ps, 128, 512, 2, self.mu1[:, i:i + 1], self.muh[:, i:i + 1], zz[which][:, c, :], zz[which])
            self.prefetch_next()
        slot = self.next_block(base + WIN_IDX["A3"])
        for i in range(3):
            ps = self.zmm(slot, 192, i * 64, 64, h0, 512)
            self.shift_evac(ps, 64, 512, 2, self.mu1[0:64, 12 + i:13 + i], self.muh[0:64, 12 + i:13 + i], lo[i][:, :], lo[i], tanh=(i < 2))
        self.prefetch_next()
        slot = self.next_block(base + WIN_IDX["A4"])
        for c in range(4):
            ps = self.zmm(slot, 512, c * 128, 128, h0, 512)
            self.silu2(ps, 128, 512, GA2[:, c, :], GA2)
        self.prefetch_next()
        fw.pop()
        jobs = []
        for sq in range(2):
            for pair in range(4):
                t0 = sq * 256
                seqi = half * 2 + sq

                def yout(c, yfin, pair=pair, t0=t0):
                    cs = t0 + c * 128
                    self.dve(lambda: self.V.scalar_tensor_tensor(self.oTs[half][0][:, pair, cs:cs + 128], yfin[:, :], 0.5,
                                                                 GA2[:, pair, cs:cs + 128], ALU.mult, ALU.mult), [yfin, GA2], [self.oTs[half][0]])

                def stout(dd, ST, pair=pair, seqi=seqi):
                    so = self.d["stout"]
                    row = (((l * 2 + dd) * 4 + seqi) * 4 + pair) * 128
                    fw.dma("sp", so, so[row:row + 128, :], ST, ST[:, :], sem_owner=self.outsem)

                J = dict(T=256, r=(zz[0], lambda a, b, pair=pair, t0=t0: zz[0][:, pair, t0 + a:t0 + b]),
                         k=(zz[1], lambda a, b, pair=pair, t0=t0: zz[1][:, pair, t0 + a:t0 + b]),
                         v=(zz[2], lambda a, b, pair=pair, t0=t0: zz[2][:, pair, t0 + a:t0 + b]),
                         twd=[(lo[0], lambda a, b, t0=t0: lo[0][:, t0 + a:t0 + b]), (lo[1], lambda a, b, t0=t0: lo[1][:, t0 + a:t0 + b])],
                         ad=(lo[2], lambda a, b, t0=t0: lo[2][:, t0 + a:t0 + b]),
                         par=lambda nm, pair=pair: self.sp[:, rw_col(nm, pair):rw_col(nm, pair) + 1],
                         parh=lambda nm, pair=pair: self.rwh[:, RW_NAMES.index(nm) * 4 + pair:RW_NAMES.index(nm) * 4 + pair + 1],
                         parbufs=[self.sp, self.rwh],
                         wup=lambda dd, pair=pair: self.wup[:, dd * 512 + pair * 128:dd * 512 + pair * 128 + 128],
                         aup=lambda dd, pair=pair: self.aup[:, dd * 512 + pair * 128:dd * 512 + pair * 128 + 128],
                         st0=None, yout=yout, stout=stout, scoped=False, tag=len(jobs) % 2)
                jobs.append(J)
        def until_early(g):
            for m in g:
                if m == "early_done":
                    return True
                yield
        cur = self.rwkv_job_gen(jobs[0])
        for _ in until_early(cur):
            pass
        for k in range(len(jobs)):
            nxt = self.rwkv_job_gen(jobs[k + 1]) if k + 1 < len(jobs) else None
            ne = until_early(nxt) if nxt is not None else None
            cur_alive, ne_alive = True, ne is not None
            while cur_alive or ne_alive:
                if cur_alive:
                    try:
                        next(cur)
                    except StopIteration:
                        cur_alive = False
                if ne_alive:
                    try:
                        next(ne)
                    except StopIteration:
                        ne_alive = False
            cur = nxt
        self.ps_range = (0, 4)
        fw.pop()

    def phaseA_contrib(self, l):
        fw = self.fw
        base = l * BLK_PER_LAYER
        self.GA2 = fw.sb([128, 4, 512], BF16, "GA2s")
        fw.push()
        self.stream_begin(5, depth=2)
        for which, (part, row0) in enumerate((("rk", 0), ("rk", 512), ("vx", VX_OFF["v"]))):
            slot = self.next_block(base + WIN_IDX[f"A{which}"])
            for c in range(4):
                self.contrib_rows(l, slot, 512, c * 128, 128, part, row0 + c * 128)
            self.prefetch_next()
        slot = self.next_block(base + WIN_IDX["A3"])
        for i in range(3):
            self.contrib_rows(l, slot, 192, i * 64, 64, "vx", VX_OFF["lora"] + i * 64)
        self.prefetch_next()
        slot = self.next_block(base + WIN_IDX["A4"])
        for c in range(4):
            ps = self.zmm(slot, 512, c * 128, 128, 0, 512)
            self.silu2(ps, 128, 512, self.GA2[:, c, :], self.GA2)
        self.prefetch_next()
        fw.pop()

    def gather_rows(self, dst, dst_ap, src, idx_col):
        fw = self.fw
        idx = self.idx1 if idx_col < 12 else self.idx2
        col = idx_col if idx_col < 12 else idx_col - 12
        fw.dma("pool", dst, None, src, None, extra_reads=[idx],
               fn=lambda: self.G.indirect_dma_start(out=dst_ap, out_offset=None, in_=src.h.ap(),
                                                    in_offset=bass.IndirectOffsetOnAxis(ap=idx[:, col:col + 1], axis=0)))

    def phaseA_consume(self, l):
        fw = self.fw
        V, A, G = self.V, self.A, self.G
        fw.push()
        self.load_rwkv_w(l, True)
        spo = fw.sb([128, 12], F32, "spo")
        self.load(spo, spo[:, :], self.d["spo"], self.d["spo"][l, :, :])
        spoh = fw.sb([128, 4], F32, "spoh")
        self.dve(lambda: V.tensor_scalar(spoh[:, :], spo[:, 0:4], 0.5, None, ALU.mult), [spo], [spoh])
        mu1o = fw.sb([128, 3], F32, "mu1o")
        muho = fw.sb([128, 3], F32, "muho")
        self.dve(lambda: V.tensor_scalar(mu1o[:, :], spo[:, 9:12], -1.0, 1.0, ALU.mult, ALU.add), [spo], [mu1o])
        self.dve(lambda: V.tensor_scalar(muho[:, :], spo[:, 9:12], 0.5, None, ALU.mult), [spo], [muho])
        T = DSEQ
        zz = [fw.sb([128, T], BF16, nm) for nm in ("sr", "sk", "sv")]
        lo = [fw.sb([64, T], BF16, nm) for nm in ("stwf", "stwb", "sad")]
        fw.push()
        raw = fw.sb([128, T], BF16, "sraw")
        o32 = fw.sb([128, T], F32, "so32")

        def shift_full(rows, src, m1, mh, dst, tanh=False):
            self.dve(lambda: V.tensor_scalar(o32[:rows, :], src[:rows, :], m1, None, ALU.mult), [src, mu1o, self.mu1], [o32])
            self.dve(lambda: V.scalar_tensor_tensor(o32[:rows, 1:T], src[:rows, 0:T - 1], mh, o32[:rows, 1:T], ALU.mult, ALU.add),
                     [src, muho, self.muh, o32], [o32])
            self.dve(lambda: V.scalar_tensor_tensor(o32[:rows, 0:T - 1], src[:rows, 1:T], mh, o32[:rows, 0:T - 1], ALU.mult, ALU.add),
                     [src, muho, self.muh, o32], [o32])
            if tanh:
                self.act(lambda: A.activation(dst[:rows, :], o32[:rows, :], AF.Tanh), [o32], [dst])
            else:
                self.act(lambda: A.copy(dst[:rows, :], o32[:rows, :]), [o32], [dst])

        for which in range(3):
            src = self.ag1_out[l]["rk" if which < 2 else "vx"]
            for q in range(4):
                self.gather_rows(raw, raw[:, q * 512:(q + 1) * 512], src, which * 4 + q)
            shift_full(128, raw, mu1o[:, which:which + 1], muho[:, which:which + 1], zz[which])
        agv = self.ag1_out[l]["vx"]
        for i in range(3):
            for q in range(4):
                r0 = q * 864 + VX_OFF["lora"] + i * 64
                fw.dma("sp", raw, raw[0:64, q * 512:(q + 1) * 512], agv, agv[r0:r0 + 64, :])
            shift_full(64, raw, self.mu1[0:64, 12 + i:13 + i], self.muh[0:64, 12 + i:13 + i], lo[i], tanh=(i < 2))
        fw.pop()
        stg = [None]

        def yout(c, yfin):
            q, cc = c // 4, c % 4
            if cc == 0:
                stg[0] = fw.rot([128, 512], BF16, "ystg", n=2)
            st = stg[0]
            self.act(lambda: A.copy(st[:, cc * 128:(cc + 1) * 128], yfin[:, :]), [yfin], [st])
            if cc == 3:
                ag = self.ag2_in[l]
                fw.dma("sp", ag, ag[q * 128:(q + 1) * 128, :], st, st[:, :])

        pidx = {nm: i for i, nm in enumerate(RW_NAMES)}
        J = dict(T=T, r=(zz[0], lambda a, b: zz[0][:, a:b]), k=(zz[1], lambda a, b: zz[1][:, a:b]), v=(zz[2], lambda a, b: zz[2][:, a:b]),
                 twd=[(lo[0], lambda a, b: lo[0][:, a:b]), (lo[1], lambda a, b: lo[1][:, a:b])],
                 ad=(lo[2], lambda a, b: lo[2][:, a:b]),
                 par=lambda nm: spo[:, pidx[nm]:pidx[nm] + 1],
                 parh=lambda nm: spoh[:, pidx[nm]:pidx[nm] + 1],
                 parbufs=[spo, spoh],
                 wup=lambda dd: self.wup[:, dd * 128:(dd + 1) * 128],
                 aup=lambda dd: self.aup[:, dd * 128:(dd + 1) * 128],
                 st0=lambda dd: (self.d["st0"], self.d["st0"][l, dd, :, :]), yout=yout, stout=None, seg=256, segpar=False)
        self.rwkv_job(J)
        self.allgather(self.ag2_out[l], self.ag2_in[l])
        fw.pop()

    def phaseA_final(self, l):
        fw = self.fw
        fw.push()
        for r in range(4):
            ya = fw.rot([128, 512], BF16, "ya", n=2)
            self.gather_rows(ya, ya[:, :], self.ag2_out[l], 12 + r)
            self.dve(lambda r=r, ya=ya: self.V.scalar_tensor_tensor(self.oTs[0][0][:, r, 0:512], ya[:, :], 0.5, self.GA2[:, r, :], ALU.mult, ALU.mult),
                     [ya, self.GA2], [self.oTs[0][0]])
        fw.pop()

    def rwkv_job(self, J):
        for _ in self.rwkv_job_gen(J):
            pass

    def rwkv_job_gen(self, J):
        fw = self.fw
        V, A, G, T_ = self.V, self.A, self.G, self.T
        T = J["T"]
        nch = T // 128
        SEG = J.get("seg", 256)
        nseg = T // SEG
        ncs = SEG // 128
        rB, rf = J["r"]
        kB, kf = J["k"]
        vB, vf = J["v"]
        adB, adf = J["ad"]
        par, parh, pbufs = J["par"], J["parh"], J["parbufs"]
        self.ps_range = (0, 6)
        scoped = J.get("scoped", True)
        tag = str(J.get("tag", ""))
        jpush = (lambda: fw.push()) if scoped else (lambda: None)
        jpop = (lambda: fw.pop()) if scoped else (lambda: None)
        jt = (lambda shp, dt, nm: fw.sb(shp, dt, nm)) if scoped else (lambda shp, dt, nm: fw.rot(shp, dt, "J" + nm + tag, n=1))
        jpush()
        kap = jt([128, T], BF16, "kap")
        Vtm = jt([128, nch, 128], BF16, "Vtm")
        Yacc = jt([128, nch, 128], F32, "Yacc")
        Bacc = jt([128, T], F32, "Bacc")
        ST = [jt([128, 64], F32, f"ST{dd}") for dd in range(2)]
        STb = [jt([128, 64], BF16, f"STb{dd}") for dd in range(2)]
        KW = min(T, 512)
        self.pool(lambda: G.memset(Yacc[:, :, :], 0.0), [], [Yacc])
        self.pool(lambda: G.memset(Bacc[:, :], 0.0), [], [Bacc])
        jpush()
        for p0 in range(0, T, 512):
            N = min(512, T - p0)
            kk = fw.rot([128, KW], F32, "kk", n=(2 if scoped else 1))
            sq = fw.rot([128, KW], BF16, "kksq", n=(2 if scoped else 1))
            self.dve(lambda: V.tensor_scalar(kk[:, :N], kf(p0, p0 + N), par("k_k"), None, ALU.mult), [kB] + pbufs, [kk])
            self.act(lambda: A.activation(sq[:, :N], kk[:, :N], AF.Square), [kk], [sq])
            ps = self.nps()
            self.pe(lambda: T_.matmul(ps[:, :N], self.bones[:, :], sq[:, :N], start=True, stop=True), [self.bones, sq], [ps])
            t = fw.rot([128, KW], F32, "kkt", n=(2 if scoped else 1))
            self.dve(lambda: V.tensor_scalar(t[:, :N], ps[:, :N], 1e-24, None, ALU.max), [ps], [t])
            self.rsqrt(t, t[:, :N], t, t[:, :N])
            self.dve(lambda: V.tensor_tensor(kap[:, p0:p0 + N], kk[:, :N], t[:, :N], ALU.mult), [kk, t], [kap])
            yield
        import os
        STOP = int(os.environ.get("RWKV_STOP", "99"))
        if STOP <= 1:
            jpop(); jpop(); return
        for c in range(nch):
            self.transpose_to(vB, vf(c * 128, c * 128 + 128), Vtm, Vtm[:, c, :], eng=("act" if c % 2 else "dve"))
            yield
        jpop()
        if STOP <= 2:
            jpop(); return
        jpush()
        for dd in range(2):
            if J["st0"] is None:
                self.pool(lambda dd=dd: G.memset(ST[dd][:, :], 0.0), [], [ST[dd]])
            else:
                src, sap = J["st0"](dd)
                fw.dma("sp", ST[dd], ST[dd][:, :], src, sap)
            self.act(lambda dd=dd: A.copy(STb[dd][:, :], ST[dd][:, :]), [ST[dd]], [STb[dd]])
        MK = self.mask
        def rw_segment(dd, sg, res, segpar):
            sfx = "fb"[dd]
            twB, twf = J["twd"][dd]
            s0 = sg * SEG
            N = SEG
            f32t = lambda nm: fw.rot([128, SEG], F32, nm + (str(dd) if segpar else ""), n=1)
            a = f32t("ra")
            ps = self.nps()
            self.pe(lambda: T_.matmul(ps[:, :N], J["aup"](dd), adf(s0, s0 + N), start=True, stop=True), [self.aup, adB], [ps])
            self.act(lambda: A.activation(a[:, :], ps[:, :N], AF.Tanh, bias=parh("a0_" + sfx), scale=0.5), [ps] + pbufs, [a])
            yield
            self.act(lambda: A.activation(a[:, :], a[:, :], AF.Identity, bias=0.5, scale=0.5), [a], [a])
            kt = f32t("rkt")
            self.dve(lambda: V.tensor_scalar(kt[:, :], a[:, :], 1.0, par("k_a"), ALU.subtract, ALU.mult), [a] + pbufs, [kt])
            self.dve(lambda: V.scalar_tensor_tensor(kt[:, :], kt[:, :], 1.0, kf(s0, s0 + N), ALU.add, ALU.mult), [kt, kB], [kt])
            b = f32t("rb")
            self.dve(lambda: V.tensor_tensor(b[:, :], a[:, :], kap[:, s0:s0 + N], ALU.mult), [a, kap], [b])
            lw = f32t("rlw")
            ps = self.nps()
            self.pe(lambda: T_.matmul(ps[:, :N], J["wup"](dd), twf(s0, s0 + N), start=True, stop=True), [self.wup, twB], [ps])
            self.act(lambda: A.activation(lw[:, :], ps[:, :N], AF.Tanh, bias=parh("w0_" + sfx), scale=0.5), [ps] + pbufs, [lw])
            yield
            self.act(lambda: A.activation(lw[:, :], lw[:, :], AF.Identity, bias=-0.3032653298563167, scale=-0.3032653298563167), [lw], [lw])
            rkr = fw.rot([128, SEG], BF16, "rkr" + str(dd), n=1)
            self.dve(lambda: V.scalar_tensor_tensor(rkr[:, :], kt[:, :], par("r_k"), rf(s0, s0 + N), ALU.mult, ALU.mult),
                     [kt, rB] + pbufs, [rkr])
            ps = self.nps()
            self.pe(lambda: T_.matmul(ps[:, :N], self.bones[:, :], rkr[:, :], start=True, stop=True), [self.bones, rkr], [ps])
            self.dve(lambda: V.tensor_tensor(Bacc[:, s0:s0 + N], Bacc[:, s0:s0 + N], ps[:, :N], ALU.add), [Bacc, ps], [Bacc])
            P = f32t("rP")
            for c in range(ncs):
                self.dve(lambda c=c: V.tensor_tensor_scan(P[:, c * 128:(c + 1) * 128], self.onesf[:, :], lw[:, c * 128:(c + 1) * 128],
                                                          0.0, ALU.mult, ALU.add), [self.onesf, lw], [P])
            Q = f32t("rQ")
            R = f32t("rR")
            self.dve(lambda: V.tensor_tensor(Q[:, :], P[:, :], lw[:, :], ALU.subtract), [P, lw], [Q])
            for c in range(ncs):
                self.dve(lambda c=c: V.tensor_scalar(R[:, c * 128:(c + 1) * 128], P[:, c * 128:(c + 1) * 128], -1.0,
                                                     P[:, c * 128 + 127:c * 128 + 128], ALU.mult, ALU.add), [P], [R])
            gL = fw.rot([128, 2], F32, "gL" + str(dd) + tag, n=2)
            self.act(lambda: A.activation(gL[:, 0:ncs], P[:, 127:SEG:128], AF.Exp), [P], [gL])
            yield
            if dd == 0:
                srcs = [(P, -1.0, a), (Q, 1.0, Q), (P, 1.0, P), (R, 1.0, R)]
            else:
                RL = f32t("rRL")
                self.dve(lambda: V.tensor_tensor(RL[:, :], R[:, :], lw[:, :], ALU.add), [R, lw], [RL])
                srcs = [(RL, -1.0, a), (R, 1.0, R), (RL, 1.0, RL), (Q, 1.0, Q)]
            E = [None] * 4
            for i, (sb_, sc_, dst_) in enumerate(srcs):
                self.act(lambda i=i, sb_=sb_, sc_=sc_, dst_=dst_: A.activation(dst_[:, :], sb_[:, :], AF.Exp, scale=sc_), [sb_], [dst_])
                E[i] = dst_
            Kd2 = fw.rot([128, 2, SEG], BF16, "Kd2" + str(dd) + tag, n=1)
            Bd2 = fw.rot([128, 2, SEG], BF16, "Bd2" + str(dd) + tag, n=1)
            KL = fw.rot([128, SEG], BF16, "KL" + str(dd) + tag, n=1)
            BL = fw.rot([128, SEG], BF16, "BL" + str(dd) + tag, n=1)
            KqRq2 = fw.rot([128, 2, ncs, 2, 128], BF16, "KqRq2" + str(dd) + tag, n=1)
            hm = self.hm
            kap3 = kap[:, s0:s0 + N].rearrange("p (c t) -> p c t", c=ncs)
            r3 = rf(s0, s0 + N).rearrange("p (c t) -> p c t", c=ncs)
            for h in range(2):
                self.dve(lambda h=h: V.scalar_tensor_tensor(Kd2[:, h, :], kt[:, :], hm[:, h:h + 1], E[0][:, :], ALU.mult, ALU.mult), [kt, hm, E[0]], [Kd2])
                self.dve(lambda h=h: V.scalar_tensor_tensor(Bd2[:, h, :], b[:, :], hm[:, h:h + 1], E[0][:, :], ALU.mult, ALU.mult), [b, hm, E[0]], [Bd2])
                self.dve(lambda h=h: V.scalar_tensor_tensor(KqRq2[:, h, :, 0, :], kap3, hm[:, h:h + 1], E[1][:, :].rearrange("p (c t) -> p c t", c=ncs),
                                                            ALU.mult, ALU.mult), [kap, hm, E[1]], [KqRq2])
                self.dve(lambda h=h: V.scalar_tensor_tensor(KqRq2[:, h, :, 1, :], r3, hm[:, h:h + 1], E[2][:, :].rearrange("p (c t) -> p c t", c=ncs),
                                                            ALU.mult, ALU.mult), [rB, hm, E[2]], [KqRq2])
            self.dve(lambda: V.tensor_tensor(KL[:, :], kt[:, :], E[3][:, :], ALU.mult), [kt, E[3]], [KL])
            self.dve(lambda: V.tensor_tensor(BL[:, :], b[:, :], E[3][:, :], ALU.mult), [b, E[3]], [BL])

            res.update(dict(Kd2=Kd2, Bd2=Bd2, KL=KL, BL=BL, KqRq2=KqRq2, gL=gL, sg=sg))
            yield

        def rw_pre(dd, c, Pd, res):
            sfx2 = f"{dd}{c}"
            mk0 = dd * 1280
            Kd2, Bd2, KqRq2 = Pd["Kd2"], Pd["Bd2"], Pd["KqRq2"]
            cs = slice(c * 128, (c + 1) * 128)
            Am = [fw.rot([128, 512], BF16, f"Am{h}_{sfx2}", n=1) for h in range(2)]
            psB = self.nps()
            for h in range(2):
                psA = self.nps()
                rhsA = KqRq2[:, h, c, :, :].rearrange("p a t -> p (a t)")
                self.pe(lambda h=h, psA=psA, rhsA=rhsA: T_.matmul(psA[:, 0:256], Kd2[:, h, cs], rhsA, start=True, stop=True),
                        [Kd2, KqRq2], [psA], signal=False)
                self.pe(lambda h=h, psA=psA, rhsA=rhsA: T_.matmul(psA[:, 256:512], Bd2[:, h, cs], rhsA, start=True, stop=True),
                        [Bd2, KqRq2], [psA])
                self.dve(lambda h=h, psA=psA: V.tensor_tensor(Am[h][:, :], psA[:, :], MK[:, mk0:mk0 + 512], ALU.mult), [psA, MK], [Am[h]])
                self.pe(lambda h=h: T_.matmul(psB[:, h * 128:(h + 1) * 128], KqRq2[:, h, c, 0, :], Bd2[:, h, cs], start=True, stop=True),
                        [KqRq2, Bd2], [psB], signal=(h == 1))
            PT = [fw.rot([128, 2, 128], BF16, f"PT{i}_{sfx2}", n=1) for i in range(2)]
            PX = [fw.rot([128, 2, 256], BF16, f"PX{i}_{sfx2}", n=1) for i in range(2)]
            C64T = fw.rot([128, 2, 128], BF16, "C64T" + sfx2, n=1)
            C128T = fw.rot([128, 2, 128], BF16, "C128T" + sfx2, n=1)
            f2 = lambda t: t[:, :, :].rearrange("p a t -> p (a t)")
            self.dve(lambda: V.tensor_tensor(f2(PT[0]), psB[:, 0:256], MK[:, mk0 + 512:mk0 + 768], ALU.mult), [psB, MK], [PT[0]])
            self.dve(lambda: V.tensor_tensor(f2(C64T), psB[:, 0:256], MK[:, mk0 + 768:mk0 + 1024], ALU.mult), [psB, MK], [C64T])
            self.dve(lambda: V.tensor_tensor(f2(C128T), psB[:, 0:256], MK[:, mk0 + 1024:mk0 + 1280], ALU.mult), [psB, MK], [C128T])
            for h in range(2):
                self.act(lambda h=h: A.copy(PX[0][:, h, 0:128], Am[h][:, 256:384]), [Am[h]], [PX[0]])
            yield
            cur = 0
            Xb = fw.rot([128, 2, 128], BF16, "Xb32" + sfx2, n=1)
            for j in range(1, 6):
                nxt = 1 - cur
                if j == 1:
                    ps = self.nps()
                    pst_ = self.nps()
                    for h in range(2):
                        self.pe(lambda h=h, ps=ps, cur=cur: T_.matmul(ps[:, h * 256:h * 256 + 128], PT[cur][:, h, :], PX[cur][:, h, 0:128],
                                                                      start=True, stop=True), [PT[cur], PX[cur]], [ps], signal=(h == 1))
                        self.pe(lambda h=h, pst_=pst_, cur=cur: T_.matmul(pst_[:, h * 128:(h + 1) * 128], PX[cur][:, h, 0:128], PT[cur][:, h, :],
                                                                          start=True, stop=True), [PT[cur], PX[cur]], [pst_], signal=(h == 1))
                    for h in range(2):
                        self.dve(lambda h=h, cur=cur, nxt=nxt: V.tensor_tensor(PX[nxt][:, h, 128:256], PX[cur][:, h, 0:128], self.ident[:, :], ALU.add),
                                 [PX[cur], self.ident], [PX[nxt]])
                    self.act(lambda ps=ps, nxt=nxt: A.copy(PX[nxt][:, :, 0:128], ps[:, :].rearrange("p (a t) -> p a t", a=2)[:, :, 0:128]),
                             [ps], [PX[nxt]])
                    self.act(lambda pst_=pst_, nxt=nxt: A.copy(f2(PT[nxt]), pst_[:, 0:256]), [pst_], [PT[nxt]])
                elif j < 5:
                    ps = self.nps()
                    pst_ = self.nps()
                    for h in range(2):
                        self.pe(lambda h=h, ps=ps, cur=cur: T_.matmul(ps[:, h * 256:(h + 1) * 256], PT[cur][:, h, :], PX[cur][:, h, :],
                                                                      start=True, stop=True), [PT[cur], PX[cur]], [ps], signal=(h == 1))
                        self.pe(lambda h=h, pst_=pst_, cur=cur: T_.matmul(pst_[:, h * 128:(h + 1) * 128], PX[cur][:, h, 0:128], PT[cur][:, h, :],
                                                                          start=True, stop=True), [PT[cur], PX[cur]], [pst_], signal=(h == 1))
                    ps3 = ps[:, :].rearrange("p (a t) -> p a t", a=2)
                    self.act(lambda ps3=ps3, ps=ps, nxt=nxt: A.copy(PX[nxt][:, :, 0:128], ps3[:, :, 0:128]), [ps], [PX[nxt]])
                    self.dve(lambda ps3=ps3, ps=ps, cur=cur, nxt=nxt: V.tensor_tensor(PX[nxt][:, :, 128:256], ps3[:, :, 128:256], PX[cur][:, :, 128:256], ALU.add),
                             [ps, PX[cur]], [PX[nxt]])
                    self.act(lambda pst_=pst_, nxt=nxt: A.copy(f2(PT[nxt]), pst_[:, 0:256]), [pst_], [PT[nxt]])
                else:
                    ps = self.nps()
                    for h in range(2):
                        self.pe(lambda h=h, ps=ps, cur=cur: T_.matmul(ps[:, h * 128:(h + 1) * 128], PT[cur][:, h, :], PX[cur][:, h, 128:256],
                                                                      start=True, stop=True), [PT[cur], PX[cur]], [ps], signal=(h == 1))
                    self.dve(lambda ps=ps, cur=cur: V.tensor_tensor(Xb[:, :, :], ps[:, 0:256].rearrange("p (a t) -> p a t", a=2), PX[cur][:, :, 128:256], ALU.add),
                             [ps, PX[cur]], [Xb])
                cur = nxt
                yield
            TT = None
            for lvl, CT in enumerate((C64T, C128T)):
                XT = fw.rot([128, 2, 128], BF16, "XTm" + sfx2, n=1)
                Zt = fw.rot([128, 2, 128], BF16, "Ztm" + sfx2, n=1)
                ptt = self.pst[self.pst_rr % 2]
                self.pst_rr += 1
                for h in range(2):
                    self.pe(lambda h=h, ptt=ptt, Xb=Xb: T_.transpose(ptt[:, h * 128:(h + 1) * 128], Xb[:, h, :], self.ident[:, :]),
                            [Xb, self.ident], [ptt], signal=(h == 1))
                self.act(lambda ptt=ptt, XT=XT: A.copy(f2(XT), ptt[:, 0:256]), [ptt], [XT])
                psz = self.nps()
                for h in range(2):
                    self.pe(lambda h=h, psz=psz, CT=CT, Xb=Xb: T_.matmul(psz[:, h * 128:(h + 1) * 128], CT[:, h, :], Xb[:, h, :], start=True, stop=True),
                            [CT, Xb], [psz], signal=(h == 1))
                self.act(lambda psz=psz, Zt=Zt: A.copy(f2(Zt), psz[:, 0:256]), [psz], [Zt])
                psw = self.nps()
                for h in range(2):
                    self.pe(lambda h=h, psw=psw, XT=XT, Zt=Zt: T_.matmul(psw[:, h * 128:(h + 1) * 128], XT[:, h, :], Zt[:, h, :], start=True, stop=True),
                            [XT, Zt], [psw], signal=(h == 1))
                Xn = fw.rot([128, 2, 128], BF16, ("Xb64" if lvl == 0 else "TT") + sfx2, n=1)
                self.dve(lambda psw=psw, Xn=Xn, Xb=Xb: V.tensor_tensor(f2(Xn), psw[:, 0:256], f2(Xb), ALU.add), [psw, Xb], [Xn])
                Xb = Xn
                yield
            TT = Xb

            res["Am"] = Am
            res["TT"] = TT
            yield

        def rw_seq(dd, Pd, pres):
            KL, BL, KqRq2, gL, sg = Pd["KL"], Pd["BL"], Pd["KqRq2"], Pd["gL"], Pd["sg"]
            chunks = list(range(ncs)) if dd == 0 else list(reversed(range(ncs)))
            for c in chunks:
                cg = sg * ncs + c
                cs = slice(c * 128, (c + 1) * 128)
                Am, TT = pres[(dd, c)]["Am"], pres[(dd, c)]["TT"]
                Sb = STb[dd]
                psG = self.nps()
                for h in range(2):
                    hs = slice(64 * h, 64 * h + 64)
                    vs = slice(64 * h, 64 * h + 64)
                    self.pe(lambda h=h, vs=vs: T_.matmul(psG[:, vs], KqRq2[:, h, c, 0, :], Sb[:, :], start=(h == 0), stop=False, skip_group_check=True),
                            [KqRq2, Sb], [psG], signal=False)
                    self.pe(lambda h=h, vs=vs: T_.matmul(psG[:, vs], Am[h][:, 0:128], Vtm[:, cg, vs], start=False, stop=(h == 1), skip_group_check=True),
                            [Am[h], Vtm], [psG], signal=(h == 1))
                Gn = fw.rot([128, 128], BF16, "Gn" + str(dd), n=1)
                self.act(lambda: A.mul(Gn[:, :], psG[:, 0:128], -1.0), [psG], [Gn])
                yield
                psU = self.nps()
                for h in range(2):
                    vs = slice(64 * h, 64 * h + 64)
                    self.pe(lambda h=h, vs=vs: T_.matmul(psU[:, vs], TT[:, h, :], Gn[:, vs], start=(h == 0), stop=(h == 1), skip_group_check=True),
                            [TT, Gn], [psU], signal=(h == 1))
                U = fw.rot([128, 128], BF16, "U" + str(dd), n=1)
                self.act(lambda: A.copy(U[:, :], psU[:, 0:128]), [psU], [U])
                yield
                yield
                psY = self.nps()
                for h in range(2):
                    hs = slice(64 * h, 64 * h + 64)
                    vs = slice(64 * h, 64 * h + 64)
                    self.pe(lambda h=h, vs=vs: T_.matmul(psY[:, vs], KqRq2[:, h, c, 1, :], Sb[:, :], start=(h == 0), stop=False, skip_group_check=True),
                            [KqRq2, Sb], [psY], signal=False)
                    self.pe(lambda h=h, vs=vs: T_.matmul(psY[:, vs], Am[h][:, 128:256], Vtm[:, cg, vs], start=False, stop=False, skip_group_check=True),
                            [Am[h], Vtm], [psY], signal=False)
                    self.pe(lambda h=h, vs=vs: T_.matmul(psY[:, vs], Am[h][:, 384:512], U[:, vs], start=False, stop=(h == 1), skip_group_check=True),
                            [Am[h], U], [psY], signal=(h == 1))
                self.dve(lambda: V.tensor_tensor(Yacc[:, cg, :], Yacc[:, cg, :], psY[:, 0:128], ALU.add), [Yacc, psY], [Yacc])
                yield
                yield
                KLt = fw.rot([128, 128], BF16, "KLt" + str(dd), n=1)
                BLt = fw.rot([128, 128], BF16, "BLt" + str(dd), n=1)
                self.transpose_to(KL, KL[:, cs], KLt, KLt[:, :], eng="act")
                self.transpose_to(BL, BL[:, cs], BLt, BLt[:, :], eng="dve")
                psS = self.nps()
                for h in range(2):
                    hs = slice(64 * h, 64 * h + 64)
                    vs = slice(64 * h, 64 * h + 64)
                    self.pe(lambda h=h, hs=hs, vs=vs: T_.matmul(psS[hs, 0:64], KLt[:, hs], Vtm[:, cg, vs], start=True, stop=False),
                            [KLt, Vtm], [psS], signal=False)
                    self.pe(lambda h=h, hs=hs, vs=vs: T_.matmul(psS[hs, 0:64], BLt[:, hs], U[:, vs], start=False, stop=True),
                            [BLt, U], [psS], signal=(h == 1))
                self.dve(lambda: V.scalar_tensor_tensor(ST[dd][:, :], ST[dd][:, :], gL[:, c:c + 1], psS[:, 0:64], ALU.mult, ALU.add),
                         [ST[dd], gL, psS], [ST[dd]])
                self.act(lambda: A.copy(STb[dd][:, :], ST[dd][:, :]), [ST[dd]], [STb[dd]])

        def run_rr(gens):
            gens = list(gens)
            while gens:
                for g in list(gens):
                    try:
                        next(g)
                    except StopIteration:
                        gens.remove(g)
                yield

        for step in range(nseg):
            sgs = (step, nseg - 1 - step)
            Pd = [{}, {}]
            segpar = J.get("segpar", False)
            if segpar:
                yield from run_rr([rw_segment(dd, sgs[dd], Pd[dd], True) for dd in range(2)])
            else:
                for dd in range(2):
                    for _ in rw_segment(dd, sgs[dd], Pd[dd], False):
                        yield
            if step == 0:
                yield "early_done"
            if STOP <= 3:
                continue
            pres = {(dd, c): {} for dd in range(2) for c in range(ncs)}
            yield from run_rr([rw_pre(dd, c, Pd[dd], pres[(dd, c)]) for dd in range(2) for c in range(ncs)])
            if STOP <= 5:
                continue
            yield from run_rr([rw_seq(dd, Pd[dd], pres) for dd in range(2)])
        if J["stout"] is not None:
            for dd in range(2):
                J["stout"](dd, ST[dd])
        jpop()
        if STOP <= 6:
            jpop(); return
        n2 = nch * 2
        sums = jt([128, n2], F32, "gsum")
        ssq = jt([128, n2], F32, "gssq")
        Ysq = jt([128, nch * 128], F32, "Ysq") if scoped else fw.rot([128, nch * 128], F32, "JYsq", n=1)
        Yf = Yacc[:, :, :].rearrange("p c x -> p (c x)")
        self.dve(lambda: V.tensor_reduce(sums[:, :], Yf.rearrange("p (g x) -> p g x", x=64), AX.X, ALU.add), [Yacc], [sums])
        self.act(lambda: A.activation(Ysq[:, :], Yf, AF.Square), [Yacc], [Ysq])
        self.dve(lambda: V.tensor_reduce(ssq[:, :], Ysq[:, :].rearrange("p (g x) -> p g x", x=64), AX.X, ALU.add), [Ysq], [ssq])
        mean = jt([128, n2], F32, "gmean")
        var = jt([128, n2], F32, "gvar2")
        self.dve(lambda: V.tensor_scalar(mean[:, :], sums[:, :], 1.0 / 64, None, ALU.mult), [sums], [mean])
        self.dve(lambda: V.tensor_tensor(var[:, :], mean[:, :], mean[:, :], ALU.mult), [mean], [var])
        self.dve(lambda: V.scalar_tensor_tensor(var[:, :], ssq[:, :], 1.0 / 64, var[:, :], ALU.mult, ALU.subtract), [ssq, var], [var])
        self.dve(lambda: V.tensor_scalar(var[:, :], var[:, :], GN_EPS, None, ALU.add), [var], [var])
        self.rsqrt(var, var[:, :], var, var[:, :])
        yn = jt([128, nch, 128], BF16, "yn")
        for c in range(nch):
            for h in range(2):
                g = c * 2 + h
                self.dve(lambda c=c, h=h, g=g: V.tensor_scalar(yn[:, c, h * 64:(h + 1) * 64], Yacc[:, c, h * 64:(h + 1) * 64], mean[:, g:g + 1], var[:, g:g + 1],
                                                               ALU.subtract, ALU.mult), [Yacc, mean, var], [yn])
        for c in range(nch):
            pt = self.pst[self.pst_rr % 2]
            self.pst_rr += 1
            self.pe(lambda c=c, pt=pt: T_.transpose(pt[:, :128], yn[:, c, :], self.ident[:, :]), [yn, self.ident], [pt])
            yT = fw.rot([128, 128], F32, "yT", n=(2 if scoped else 1))
            self.act(lambda pt=pt, yT=yT: A.activation(yT[:, :], pt[:, :128], AF.Identity, bias=par("ln_b"), scale=par("ln_g")), [pt] + pbufs, [yT])
            bo = fw.rot([128, 128], F32, "bo", n=(2 if scoped else 1))
            self.dve(lambda c=c, bo=bo: V.tensor_tensor(bo[:, :], Bacc[:, c * 128:(c + 1) * 128], vf(c * 128, (c + 1) * 128), ALU.mult), [Bacc, vB], [bo])
            yfin = fw.rot([128, 128], F32, "yfin", n=(2 if scoped else 1))
            self.dve(lambda yT=yT, bo=bo, yfin=yfin: V.tensor_tensor(yfin[:, :], yT[:, :], bo[:, :], ALU.add), [yT, bo], [yfin])
            J["yout"](c, yfin)
            yield
        jpop()
        if scoped:
            self.ps_range = (0, 4)

    def load_mla_w(self, l):
        fw, d = self.fw, self.d
        self.wq = fw.sb([128, 2 * 8 * 96], BF16, "wq")
        self.wqs = fw.sb([128, 2 * 8 * 32], BF16, "wqs")
        self.wkk = fw.sb([128, 512], BF16, "wkk")
        self.wkv = fw.sb([128, 512], BF16, "wkv")
        for nm in ("wq", "wqs", "wkk", "wkv"):
            t = getattr(self, nm)
            self.load(t, t[:], d[nm], d[nm][l, :, :])

    def mla_front(self, l, h0, rope, GB2, Qh, ckv_f, ckv_b, kr_f, kr_b):
        fw = self.fw
        base = l * BLK_PER_LAYER
        W = WIN_W["B0"]
        slot = self.next_block(base + WIN_IDX["B0"])
        qd = fw.sb([128, 2, 512], F32, "qd")
        kvd = fw.sb([128, 512], F32, "kvd")
        for c in range(2):
            ps = self.zmm(slot, W, c * 128, 128, h0, 512)
            self.act(lambda c=c, ps=ps: self.A.copy(qd[:, c, :], ps[:, :]), [ps], [qd])
        ps = self.zmm(slot, W, 256, 128, h0, 512)
        self.dve(lambda ps=ps: self.V.tensor_copy(kvd[:, :], ps[:, :]), [ps], [kvd])
        pk = self.zmm(slot, W, 384, 32, h0, 512, prow=64)
        R = self.rope
        if rope:
            pks = self.zmm(slot, W, 416, 32, h0, 512, prow=64)
            t1 = fw.sb([96, 512], F32, "krt1")
            self.dve(lambda: self.V.tensor_tensor(t1[64:96, :], pk[64:96, :], R[64:96, 0:512], ALU.mult), [pk, R], [t1])
            self.dve(lambda: self.V.tensor_tensor(kr_f[64:96, :], pks[64:96, :], R[64:96, 512:1024], ALU.mult), [pks, R], [kr_f])
            self.dve(lambda: self.V.tensor_tensor(kr_f[64:96, :], kr_f[64:96, :], t1[64:96, :], ALU.add), [kr_f, t1], [kr_f])
        else:
            self.act(lambda: self.A.copy(kr_f[64:96, :], pk[64:96, :]), [pk], [kr_f])
        self.act(lambda: self.A.copy(kr_b[64:96, :], kr_f[64:96, :]), [kr_f], [kr_b])
        self.prefetch_next()
        slot = self.next_block(base + WIN_IDX["B1"])
        for c in range(4):
            ps = self.zmm(slot, 512, c * 128, 128, h0, 512)
            self.silu2(ps, 128, 512, GB2[:, c, :], GB2)
        self.prefetch_next()
        rq = fw.sb([128, 512], F32, "rq")
        self.rstd_tile(qd, [qd[:, c, :] for c in range(2)], 256.0, rq, 512)
        qn = fw.sb([128, 2, 512], BF16, "qn")
        for c in range(2):
            self.dve(lambda c=c: self.V.scalar_tensor_tensor(qn[:, c, :], qd[:, c, :], self.spc("qn", c), rq[:, :], ALU.mult, ALU.mult),
                     [qd, self.sp, rq], [qn])
        rk = fw.sb([128, 512], F32, "rkv")
        self.rstd_tile(kvd, [kvd[:, :]], 128.0, rk, 512)
        self.dve(lambda: self.V.scalar_tensor_tensor(ckv_f[:, :], kvd[:, :], self.spc("kvn", 0), rk[:, :], ALU.mult, ALU.mult),
                 [kvd, self.sp, rk], [ckv_f])
        self.act(lambda: self.A.copy(ckv_b[:, :], ckv_f[:, :]), [ckv_f], [ckv_b])
        for h in range(8):
            ps = self.nps()
            for c in range(2):
                self.pe(lambda c=c, h=h, ps=ps: self.T.matmul(ps[:96, :], self.wq[:, (c * 8 + h) * 96:(c * 8 + h) * 96 + 96], qn[:, c, :],
                                                             start=(c == 0), stop=(c == 1)), [self.wq, qn], [ps], signal=(c == 1))
            if rope:
                ps2 = self.nps()
                for c in range(2):
                    self.pe(lambda c=c, h=h, ps2=ps2: self.T.matmul(ps2[64:96, :], self.wqs[:, (c * 8 + h) * 32:(c * 8 + h) * 32 + 32], qn[:, c, :],
                                                                   start=(c == 0), stop=(c == 1)), [self.wqs, qn], [ps2], signal=(c == 1))
                t1 = fw.rot([96, 512], F32, "qrt1")
                t2 = fw.rot([96, 512], F32, "qrt2")
                self.dve(lambda ps=ps, t1=t1: self.V.tensor_tensor(t1[64:96, :], ps[64:96, :], R[64:96, 0:512], ALU.mult), [ps, R], [t1])
                self.dve(lambda ps2=ps2, t2=t2: self.V.tensor_tensor(t2[64:96, :], ps2[64:96, :], R[64:96, 512:1024], ALU.mult), [ps2, R], [t2])
                self.dve(lambda h=h, t1=t1, t2=t2: self.V.tensor_tensor(Qh[h][64:96, :], t1[64:96, :], t2[64:96, :], ALU.add), [t1, t2], [Qh[h]])
                self.act(lambda h=h, ps=ps: self.A.copy(Qh[h][0:64, :], ps[0:64, :]), [ps], [Qh[h]])
            else:
                self.act(lambda h=h, ps=ps: self.A.copy(Qh[h][:, :], ps[:96, :]), [ps], [Qh[h]])

    def mla_kv_chunk(self, ckv_b, kr_b, heads, Kh, Vaug, nk):
        for i, h in enumerate(heads):
            ps = self.nps()
            self.pe(lambda h=h, ps=ps: self.T.matmul(ps[:64, :nk], self.wkk[:, h * 64:(h + 1) * 64], ckv_b[:, :nk], start=True, stop=True),
                    [self.wkk, ckv_b], [ps])
            self.act(lambda i=i, ps=ps: self.A.copy(Kh[i][0:64, :nk], ps[0:64, :nk]), [ps], [Kh[i]])
            self.dve(lambda i=i: self.V.tensor_copy(Kh[i][64:96, :nk], kr_b[64:96, :nk]), [kr_b], [Kh[i]])
        for kb in range(nk // 128):
            ps = self.nps()
            self.pe(lambda kb=kb, ps=ps: self.T.matmul(ps[:, :], ckv_b[:, kb * 128:(kb + 1) * 128], self.wkv[:, :], start=True, stop=True),
                    [ckv_b, self.wkv], [ps])
            for h8 in range(8):
                pass
            self.dve(lambda kb=kb, ps=ps: self.V.tensor_copy(
                Vaug[:, kb * 520:(kb + 1) * 520].rearrange("p (h e) -> p h e", e=65)[:, :, 0:64],
                ps[:, :].rearrange("p (h e) -> p h e", e=64)), [ps], [Vaug])

    def attn_accum(self, Qh4, q0, nq, Kh4, Vaug, heads, k0, nkb, first, last, Oacc):
        fw = self.fw
        nqs = nq // 128
        its = [(kb, i, h) for kb in range(nkb) for i, h in enumerate(heads)]

        def score(kb, i, h):
            pss = self.psb[4 + (self.sc_rr % 2)]
            self.sc_rr += 1
            self.pe(lambda: self.T.matmul(pss[:, :nq], Kh4[i][:, k0 + kb * 128:k0 + kb * 128 + 128],
                                          Qh4[i][:, q0:q0 + nq], start=True, stop=True), [Kh4[i], Qh4[i]], [pss])
            PT = fw.rot([128, 512], BF16, "PT", n=3)
            self.act(lambda: self.A.activation(PT[:, :nq], pss[:, :nq], AF.Exp, scale=96.0 ** -0.5), [pss], [PT])
            return PT

        def pv(kb, i, h, PT):
            for qs in range(nqs):
                self.pe(lambda qs=qs: self.T.matmul(
                    Oacc[qs][:, i * 65:(i + 1) * 65], PT[:, qs * 128:(qs + 1) * 128],
                    Vaug[:, (k0 // 128 + kb) * 520 + h * 65:(k0 // 128 + kb) * 520 + h * 65 + 65],
                    start=(first and kb == 0 and i == 0), stop=(last and kb == nkb - 1 and i == 3), skip_group_check=True),
                    [PT, Vaug], [Oacc[qs]], signal=(qs == nqs - 1))

        pend = score(*its[0])
        for n in range(len(its)):
            nxt = score(*its[n + 1]) if n + 1 < len(its) else None
            pv(*its[n], pend)
            pend = nxt

    def attn_finish(self, Oacc, nqs, ob, hh):
        fw = self.fw
        for qs in range(nqs):
            rec = fw.rot([128, 4], F32, "rec", n=4)
            self.dve(lambda qs=qs, rec=rec: self.V.reciprocal(rec[:, :], Oacc[qs][:, 64:260:65]), [Oacc[qs]], [rec])
            for i in range(4):
                self.dve(lambda qs=qs, i=i, rec=rec: self.V.tensor_scalar(ob[qs][:, hh * 256 + i * 64:hh * 256 + i * 64 + 64],
                                                                          Oacc[qs][:, i * 65:i * 65 + 64], rec[:, i:i + 1], None, ALU.mult),
                         [Oacc[qs], rec], [ob[qs]])

    def attn_out(self, ob, nqs, GB2, g0, o0):
        for qs in range(nqs):
            for c in range(4):
                pt = self.pst[self.pst_rr % 2]
                self.pst_rr += 1
                self.pe(lambda qs=qs, c=c, pt=pt: self.T.transpose(pt[:, :128], ob[qs][:, c * 128:(c + 1) * 128], self.ident[:, :]),
                        [ob[qs], self.ident], [pt])
                self.dve(lambda qs=qs, c=c, pt=pt: self.V.scalar_tensor_tensor(
                    self.oTs[o0 // 512][1][:, c, o0 % 512 + qs * 128:o0 % 512 + qs * 128 + 128], pt[:, :128], 0.5, GB2[:, c, g0 + qs * 128:g0 + qs * 128 + 128],
                    ALU.mult, ALU.mult), [pt, GB2], [self.oTs[o0 // 512][1]])

    def phaseB_prompt(self, l, half):
        fw = self.fw
        h0 = half * 512
        fw.push()
        self.load_mla_w(l)
        GB2 = fw.sb([128, 4, 512], BF16, "GB2")
        Qh = [fw.sb([96, 512], BF16, f"Qh{h}") for h in range(8)]
        ckv_f = fw.sb([128, 512], F32, "ckvf")
        ckv_b = fw.sb([128, 512], BF16, "ckvb")
        kr_f = fw.sb([96, 512], F32, "krf")
        kr_b = fw.sb([96, 512], BF16, "krb")
        fw.push()
        self.stream_begin(2, depth=2)
        self.mla_front(l, h0, False, GB2, Qh, ckv_f, ckv_b, kr_f, kr_b)
        fw.pop()
        fw.dma("sp", self.d["ckvout"], self.d["ckvout"][l, :, h0:h0 + 512], ckv_f, ckv_f[:, :], sem_owner=self.outsem)
        fw.dma("sp", self.d["krout"], self.d["krout"][l, :, h0:h0 + 512], kr_f, kr_f[64:96, :], sem_owner=self.outsem)
        Kh = [fw.sb([96, 512], BF16, f"Kh{h}") for h in range(8)]
        Vaug = fw.sb([128, 4 * 520], BF16, "Vaug")
        self.pool(lambda: self.G.memset(Vaug[:, :], 1.0), [], [Vaug])
        self.mla_kv_chunk(ckv_b, kr_b, list(range(8)), Kh, Vaug, 512)
        self.sc_rr = 0
        import os
        if "dumpB" in os.environ.get("KDBG", "") and l == 0 and half == 0:
            so = self.d["stout"]
            fw.dma("pool", so, so[0:768, :].rearrange("(p a) b -> p (a b)", p=96), Qh[0], Qh[0][:, :])
            fw.dma("pool", so, so[768:1536, :].rearrange("(p a) b -> p (a b)", p=96), Kh[0], Kh[0][:, :])
            fw.dma("pool", so, so[1536:5632, :].rearrange("(p a) b -> p (a b)", p=128), Vaug, Vaug[:, 0:2048])
        for sq in range(2):
            ob = [fw.rot([128, 512], BF16, "ob", n=4) for _ in range(2)]
            for hh in range(2):
                heads = list(range(hh * 4, hh * 4 + 4))
                Oacc = [self.psb[0 + 2 * (hh % 2)], self.psb[1 + 2 * (hh % 2)]]
                self.attn_accum([Qh[h] for h in heads], sq * 256, 256, [Kh[h] for h in heads], Vaug, heads, sq * 256, 2, True, True, Oacc)
                self.attn_finish(Oacc, 2, ob, hh)
            if "dumpB" in os.environ.get("KDBG", "") and l == 0 and half == 0 and sq == 0:
                so = self.d["stout"]
                fw.dma("pool", so, so[5696:6720, :].rearrange("(p a) b -> p (a b)", p=128), ob[0], ob[0][:, :])
            self.attn_out(ob, 2, GB2, sq * 256, h0 + sq * 256)
        fw.pop()

    def phaseB_contrib(self, l):
        fw = self.fw
        self.load_mla_w(l)
        self.GB2 = fw.sb([128, 4, 512], BF16, "GB2s")
        self.Qh = [fw.sb([96, 512], BF16, f"Qhs{h}") for h in range(8)]
        fw.push()
        self.rope = fw.sb([96, 1024], F32, "rope")
        self.load(self.rope, self.rope[64:96, :], self.d["rope"], self.d["rope"].ap())
        self.stream_begin(2, depth=2)
        ckv_f = fw.sb([128, 512], F32, "ckvf")
        ckv_b = fw.sb([128, 512], BF16, "ckvb")
        kr_f = fw.sb([96, 512], F32, "krf")
        kr_b = fw.sb([96, 512], BF16, "krb")
        self.mla_front(l, 0, True, self.GB2, self.Qh, ckv_f, ckv_b, kr_f, kr_b)
        ag = self.ag1_in[l]["vx"]
        fw.dma("sp", ag, ag[VX_OFF["ckv"]:VX_OFF["ckv"] + 128, :], ckv_b, ckv_b[:, :])
        fw.dma("sp", ag, ag[VX_OFF["kr"]:VX_OFF["kr"] + 32, :], kr_b, kr_b[64:96, :])
        fw.pop()

    def phaseB_consume(self, l):
        fw = self.fw
        ago = self.ag1_out[l]["vx"]
        fw.push()
        self.sc_rr = 0
        self.ps_range = (4, 6)
        ob = [fw.sb([128, 512], BF16, f"obs{i}") for i in range(4)]
        for hh in range(2):
            heads = list(range(hh * 4, hh * 4 + 4))
            Oacc = self.psb[0:4]
            for ch in range(5):
                ckv_b = fw.rot([128, 512], BF16, "ckvg", n=2)
                kr_b = fw.rot([96, 512], BF16, "krg", n=2)
                if ch < 4:
                    fw.dma("sp", ckv_b, ckv_b[:, :], ago, ago[ch * 864 + VX_OFF["ckv"]:ch * 864 + VX_OFF["ckv"] + 128, :])
                    fw.dma("sp", kr_b, kr_b[64:96, :], ago, ago[ch * 864 + VX_OFF["kr"]:ch * 864 + VX_OFF["kr"] + 32, :])
                else:
                    c32 = fw.rot([128, 512], F32, "cc32", n=1)
                    k32 = fw.rot([96, 512], F32, "ck32", n=1)
                    fw.dma("sp", c32, c32[:, :], self.d["cckv"], self.d["cckv"][l, :, :])
                    fw.dma("sp", k32, k32[64:96, :], self.d["ckr"], self.d["ckr"][l, :, :])
                    self.act(lambda c32=c32, ckv_b=ckv_b: self.A.copy(ckv_b[:, :], c32[:, :]), [c32], [ckv_b])
                    self.dve(lambda k32=k32, kr_b=kr_b: self.V.tensor_copy(kr_b[64:96, :], k32[64:96, :]), [k32], [kr_b])
                Kh = [fw.rot([96, 512], BF16, f"Khs{i}", n=2) for i in range(4)]
                Vaug = fw.rot([128, 4 * 520], BF16, "Vaugs", n=2)
                self.pool(lambda Vaug=Vaug: self.G.memset(Vaug[:, :], 1.0), [], [Vaug])
                self.mla_kv_chunk(ckv_b, kr_b, heads, Kh, Vaug, 512)
                self.attn_accum([self.Qh[h] for h in heads], 0, 512, Kh, Vaug, heads, 0, 4, ch == 0, ch == 4, Oacc)
            self.attn_finish(Oacc, 4, ob, hh)
        self.ps_range = (0, 4)
        self.attn_out(ob, 4, self.GB2, 0, 0)
        fw.pop()

    def fnet_stage1(self, fT_buf, fT_ap_fn, dftd, G1):
        for hb in range(2):
            ps = self.psb[4 + hb]
            for gg in range(2):
                g = hb * 2 + gg
                self.pe(lambda g=g, gg=gg, ps=ps: self.T.matmul(ps[:, gg * 256:(gg + 1) * 256], fT_ap_fn(g), dftd[:, :],
                                                                 start=True, stop=True), [fT_buf, dftd], [ps], signal=(gg == 1))
            if hb == 0:
                self.act(lambda ps=ps: self.A.copy(G1[:, 0:512], ps[:, :]), [ps], [G1])
            else:
                self.dve(lambda ps=ps: self.V.tensor_copy(G1[:, 512:1024], ps[:, :]), [ps], [G1])

    def phaseD_prompt(self, l, half):
        fw = self.fw
        base = l * BLK_PER_LAYER
        h0 = half * 512
        fw.push()
        self.dftd_p = fw.sb([128, 256], BF16, "dftd_p")
        self.dftT_p = fw.sb([128, 1024], BF16, "dftT_p")
        for nm in ("dftd_p", "dftT_p"):
            t = getattr(self, nm)
            self.load(t, t[:], self.d[nm], self.d[nm].ap())
        fT = fw.sb([128, 4, 512], BF16, "fT")
        GD2 = fw.sb([128, 4, 512], BF16, "GD2")
        fw.push()
        self.stream_begin(2, depth=2)
        slot = self.next_block(base + WIN_IDX["D0"])
        for c in range(4):
            ps = self.zmm(slot, 512, c * 128, 128, h0, 512)
            if c % 2:
                self.act(lambda c=c, ps=ps: self.A.copy(fT[:, c, :], ps[:, :]), [ps], [fT])
            else:
                self.dve(lambda c=c, ps=ps: self.V.tensor_copy(fT[:, c, :], ps[:, :]), [ps], [fT])
        self.prefetch_next()
        slot = self.next_block(base + WIN_IDX["D1"])
        for c in range(4):
            ps = self.zmm(slot, 512, c * 128, 128, h0, 512)
            self.silu2(ps, 128, 512, GD2[:, c, :], GD2)
        self.prefetch_next()
        fw.pop()
        for sq in range(2):
            t0 = sq * 256
            G1 = [fw.rot([128, 1024], BF16, "G1", n=4) for _ in range(2)]
            for tt in range(2):
                self.fnet_stage1(fT, lambda g, tt=tt: fT[:, g, t0 + tt * 128:t0 + tt * 128 + 128], self.dftd_p, G1[tt])
            for g in range(4):
                ps = self.nps()
                i = 0
                for tt in range(2):
                    for cs in range(2):
                        self.pe(lambda g=g, tt=tt, cs=cs, ps=ps, i=i: self.T.matmul(
                            ps[:, :256], G1[tt][:, g * 256 + cs * 128:g * 256 + cs * 128 + 128],
                            self.dftT_p[:, (tt * 2 + cs) * 256:(tt * 2 + cs) * 256 + 256], start=(i == 0), stop=(i == 3)),
                            [G1[tt], self.dftT_p], [ps], signal=(i == 3))
                        i += 1
                self.dve(lambda g=g, ps=ps: self.V.scalar_tensor_tensor(
                    self.oTs[half][3][:, g, t0:t0 + 256], ps[:, :256], 0.5, GD2[:, g, t0:t0 + 256], ALU.mult, ALU.mult),
                    [ps, GD2], [self.oTs[half][3]])
        fw.pop()

    def contrib_rows(self, l, slot, W, c0, w, part, row0):
        fw = self.fw
        ps = self.zmm(slot, W, c0, w, 0, 512)
        stg = fw.rot([128, 512], BF16, "agstg", n=3)
        if self.cflip % 2:
            self.act(lambda: self.A.copy(stg[:w, :], ps[:w, :]), [ps], [stg])
        else:
            self.dve(lambda: self.V.tensor_copy(stg[:w, :], ps[:w, :]), [ps], [stg])
        self.cflip += 1
        ag = self.ag1_in[l][part]
        fw.dma("sp", ag, ag[row0:row0 + w, :], stg, stg[:w, :])

    def phaseD_contrib(self, l):
        fw = self.fw
        base = l * BLK_PER_LAYER
        self.GD2 = fw.sb([128, 4, 512], BF16, "GD2s")
        fw.push()
        self.stream_begin(2, depth=2)
        slot = self.next_block(base + WIN_IDX["D0"])
        for c in range(4):
            self.contrib_rows(l, slot, 512, c * 128, 128, "f", c * 128)
        self.prefetch_next()
        slot = self.next_block(base + WIN_IDX["D1"])
        for c in range(4):
            ps = self.zmm(slot, 512, c * 128, 128, 0, 512)
            self.silu2(ps, 128, 512, self.GD2[:, c, :], self.GD2)
        self.prefetch_next()
        fw.pop()

    def phaseD_consume(self, l):
        fw = self.fw
        ago = self.ag1_out[l]["f"]
        fw.push()
        self.dftd_s = fw.sb([128, 256], BF16, "dftd_s")
        self.load(self.dftd_s, self.dftd_s[:], self.d["dftd_s"], self.d["dftd_s"].ap())
        acc = self.psb[0:4]
        nt = 0
        for q in range(4):
            fq = fw.rot([128, 4, 512], BF16, "fq", n=2)
            src = ago[q * 512:q * 512 + 512, :].rearrange("(g d) t -> d g t", d=128)
            fw.dma("sp", fq, fq[:], ago, src)
            for s4 in range(4):
                tt = q * 4 + s4
                ct = fw.rot([128, 1024], BF16, "ct", n=3)
                fw.dma("pool", ct, ct[:], self.d["dftT_s"], self.d["dftT_s"][tt, :, :])
                G1 = fw.rot([128, 1024], BF16, "G1s", n=3)
                self.fnet_stage1(fq, lambda g, s4=s4, fq=fq: fq[:, g, s4 * 128:(s4 + 1) * 128], self.dftd_s, G1)
                for g in range(4):
                    for cs in range(2):
                        self.pe(lambda g=g, cs=cs, G1=G1, ct=ct, tt=tt: self.T.matmul(
                            acc[g][:, :], G1[:, g * 256 + cs * 128:g * 256 + cs * 128 + 128], ct[:, cs * 512:(cs + 1) * 512],
                            start=(tt == 0 and cs == 0), stop=(tt == 15 and cs == 1)),
                            [G1, ct], [acc[g]], signal=(g == 3 and cs == 1))
        for g in range(4):
            self.dve(lambda g=g: self.V.scalar_tensor_tensor(self.oTs[0][3][:, g, 0:512], acc[g][:, :], 0.5, self.GD2[:, g, :],
                                                             ALU.mult, ALU.mult), [acc[g], self.GD2], [self.oTs[0][3]])
        fw.pop()

    def allgather(self, dst, src):
        fw = self.fw
        fw.dma("pool", dst, None, src, None, sem_owner=dst, inc=1,
               fn=lambda: self.G.collective_compute("AllGather", ALU.bypass, replica_groups=[[0, 1, 2, 3], [4, 5, 6, 7]],
                                                     ins=[src.h.ap()], outs=[dst.h.ap()]))

    def sample_pass(self, l):
        import os
        fw = self.fw
        dbg = os.environ.get("KDBG", "")
        self.cflip = 0
        br = self.branches
        fw.push()
        if "A" in br:
            self.phaseA_contrib(l)
        fw.push()
        if "B" in br:
            self.phaseB_contrib(l)
        fw.push()
        if "D" in br:
            self.phaseD_contrib(l)
        if "noag" not in dbg:
            if "A" in br:
                self.allgather(self.ag1_out[l]["rk"], self.ag1_in[l]["rk"])
            if "A" in br or "B" in br:
                self.allgather(self.ag1_out[l]["vx"], self.ag1_in[l]["vx"])
            if "D" in br:
                self.allgather(self.ag1_out[l]["f"], self.ag1_in[l]["f"])
        if "C" in br:
            self.phaseC(l, 0, 512, 0)
        if "D" in br and "nocons" not in dbg:
            self.phaseD_consume(l)
        fw.pop()
        if "B" in br:
            self.phaseB_consume(l)
        fw.pop()
        if "A" in br and "noAcons" not in dbg:
            self.phaseA_consume(l)
            self.phaseA_final(l)
        fw.pop()

    def zero_branch(self, n):
        for hf in range(2):
            if self.oTs[hf] is not None:
                t = self.oTs[hf][n]
                self.pool(lambda t=t: self.G.memset(t[:], 0.0), [], [t])

    def win_plan(self, l, grp="p"):
        base = l * BLK_PER_LAYER
        ids = []
        order = (("A", ["A0", "A1", "A2", "A3", "A4"]), ("B", ["B0", "B1"]), ("C", ["C0", "C1", "C2"]), ("D", ["D0", "D1"]))
        if grp == "s":
            order = (order[0], order[1], order[3], order[2])
        for br, names in order:
            if br in self.branches:
                ids += [base + WIN_IDX[n] for n in names]
        return ids

    def tail_plan(self, l):
        base = l * BLK_PER_LAYER
        return [base + 18 + i for i in range(16)] + [base + 34, base + 35]

    def build(self):
        fw = self.fw
        self.outsem = Buf(None, "outsem", "none")
        for l in range(self.depth):
            self.stream_plan([l * BLK_PER_LAYER + b for b in range(6)])
            self.stream_plan(self.win_plan(l) * 2 + self.tail_plan(l))
            self.stream_plan(self.win_plan(l, 's') + self.tail_plan(l))
        for l in range(self.depth):
            fw.push()
            self.load_layer_small(l)
            self.ada(l)
            fw.push()
            self.hTs[1] = fw.sb([128, 8, 512], BF16, "hTb")
            self.oTs[1] = [fw.sb([128, 4, 512], BF16, f"oTb{n}") for n in range(4)]
            for n, br in enumerate("ABCD"):
                if br not in self.branches:
                    self.zero_branch(n)
            for half in range(2):
                self.make_h(0, half * 512, half * 512, 512)
            for half in range(2):
                self.phases(l, "p", half)
            self.merge_out(l, 0, 0, NPT)
            fw.pop()
            self.hTs[1] = None
            self.oTs[1] = None
            self.make_h(1, NPT, 0, 512)
            self.sample_pass(l)
            self.merge_out(l, 1, NPT, NST)
            fw.pop()
        fw.push()
        self.sp = fw.sb([128, NSP], F32, "spf")
        self.load(self.sp, self.sp[:], self.d["sp"], self.d["sp"][0, :, :])
        self.final_out()
        fw.pop()
        for e in ("sp",):
            for ev in list(fw.dma_out.values()):
                fw._wait(e, ev)
        fw.barrier()
        return self.nc

    def phases(self, l, grp, half):
        h0 = half * 512
        if "A" in self.branches:
            self.phaseA_prompt(l, half)
        if "B" in self.branches:
            self.phaseB_prompt(l, half)
        if "C" in self.branches:
            self.phaseC(l, h0, 512, h0)
        if "D" in self.branches:
            self.phaseD_prompt(l, half)


_CFG = {"branches": "ABCD", "depth": DEPTH}


def make_in_maps(inp):
    f32 = lambda a: np.ascontiguousarray(np.asarray(a, dtype=np.float32))
    inp = {k: np.asarray(v) for k, v in inp.items()}
    wst = build_stream(f32(inp["w_ada"]), f32(inp["w_in"]), f32(inp["w_branch"]), f32(inp["w_merge"]), f32(inp["w_out"]))
    sp = build_small(inp)
    cst = build_consts()
    L = DEPTH
    wup = f32(inp["rwkv_w_up"]).transpose(0, 2, 1, 3).reshape(L, 64, 1024)
    aup = f32(inp["rwkv_a_up"]).transpose(0, 2, 1, 3).reshape(L, 64, 1024)
    wqu = f32(inp["mla_w_q_up"]).reshape(L, 2, 128, 8, 96)
    wq = wqu.transpose(0, 2, 1, 3, 4).reshape(L, 128, 2 * 8 * 96)
    wqs = wqu[..., 64 + _SWAP32].transpose(0, 2, 1, 3, 4).reshape(L, 128, 2 * 8 * 32)
    wkvu = f32(inp["mla_w_kv_up"]).reshape(L, 128, 8, 128)
    wkk = np.ascontiguousarray(wkvu[..., :64]).reshape(L, 128, 512)
    wkv = np.ascontiguousarray(wkvu[..., 64:]).reshape(L, 128, 512)
    wsT = f32(inp["gmlp_w_s"]).transpose(0, 3, 1, 2).reshape(L, 128, 512)
    bsb = np.ascontiguousarray(np.broadcast_to(f32(inp["gmlp_b_s"]).reshape(L, 1, 512), (L, 128, 512)))
    maps = []
    xp = f32(inp["x_prompt"])
    xs = f32(inp["x_sample"])
    for c in range(NCORE):
        s, j = c // 4, c % 4
        dftT, rope = build_core_consts(j)
        cond = np.stack([f32(inp["c_ctx"]).reshape(8, 128).T, f32(inp["c"])[s].reshape(8, 128).T], axis=2).reshape(128, 16)
        o, _ = SP_OFF["rw"]
        om, _ = SP_OFF["mu_rkv"]
        spo = np.concatenate([sp[:, :, o:o + 36].reshape(L, 128, 9, 4)[:, :, :, j],
                              sp[:, :, om:om + 12].reshape(L, 128, 3, 4)[:, :, :, j]], axis=2)
        spo = np.ascontiguousarray(spo)
        st0 = np.stack([f32(inp["state_rwkv_fwd"])[s, :, 2 * j:2 * j + 2], f32(inp["state_rwkv_bwd"])[s, :, 2 * j:2 * j + 2]],
                       axis=1)
        st0 = st0.transpose(0, 1, 2, 4, 3).reshape(L, 2, 128, 64)
        p = np.arange(128)
        idx1 = np.zeros((128, 12), np.int32)
        for q in range(4):
            idx1[:, 0 * 4 + q] = q * 1024 + 128 * j + p
            idx1[:, 1 * 4 + q] = q * 1024 + 512 + 128 * j + p
            idx1[:, 2 * 4 + q] = q * 864 + 128 * j + p
        idx2 = np.zeros((128, 4), np.int32)
        for r in range(4):
            idx2[:, r] = (r * 4 + j) * 128 + p
        m = dict(
            xp=np.ascontiguousarray(xp[4 * c:4 * c + 4].reshape(NPT, D).T),
            xs=np.ascontiguousarray(xs[s, 512 * j:512 * j + 512].T),
            wst=wst, sp=sp, cond=np.ascontiguousarray(cond),
            ident=cst["ident"], bones=cst["bones"], mask=cst["mask"],
            dftd_p=cst["dftd_p"], dftd_s=cst["dftd_s"], dftT_p=cst["dftT_p"],
            dftT_s=np.ascontiguousarray(dftT.reshape(16, 128, 1024)), rope=np.ascontiguousarray(rope.reshape(32, 1024)),
            wup=wup, aup=aup,
            wupo=np.ascontiguousarray(wup.reshape(L, 64, 2, 4, 128)[:, :, :, j]).reshape(L, 64, 256),
            aupo=np.ascontiguousarray(aup.reshape(L, 64, 2, 4, 128)[:, :, :, j]).reshape(L, 64, 256),
            spo=spo, wq=wq, wqs=wqs, wkk=wkk, wkv=wkv, wsT=wsT, bsb=bsb,
            st0=np.ascontiguousarray(st0),
            cckv=np.ascontiguousarray(f32(inp["cache_mla_ckv"])[s].transpose(0, 2, 1)),
            ckr=np.ascontiguousarray(f32(inp["cache_mla_krope"])[s].transpose(0, 2, 1)),
            idx1=idx1, idx2=idx2,
        )
        maps.append(m)
    return maps


def assemble(results):
    B = 32
    yp = np.zeros((B, SEQ, D), np.float32)
    ys = np.zeros((2, DSEQ, D), np.float32)
    sf = np.zeros((B, DEPTH, 8, 64, 64), np.float32)
    sbw = np.zeros((B, DEPTH, 8, 64, 64), np.float32)
    ckv = np.zeros((B, DEPTH, SEQ, 128), np.float32)
    kr = np.zeros((B, DEPTH, SEQ, 32), np.float32)
    for c in range(NCORE):
        r = results[c]
        s, j = c // 4, c % 4
        yp[4 * c:4 * c + 4] = np.asarray(r["yp"]).T.reshape(4, SEQ, D)
        ys[s, 512 * j:512 * j + 512] = np.asarray(r["ys"]).T
        st = np.asarray(r["stout"]).reshape(DEPTH, 2, 4, 4, 2, 64, 64)
        st = st.transpose(1, 2, 0, 3, 4, 6, 5).reshape(2, 4, DEPTH, 8, 64, 64)
        sf[4 * c:4 * c + 4] = st[0]
        sbw[4 * c:4 * c + 4] = st[1]
        ck = np.asarray(r["ckvout"]).reshape(DEPTH, 128, 4, SEQ)
        ckv[4 * c:4 * c + 4] = ck.transpose(2, 0, 3, 1)
        k2 = np.asarray(r["krout"]).reshape(DEPTH, 32, 4, SEQ)
        kr[4 * c:4 * c + 4] = k2.transpose(2, 0, 3, 1)
    return yp, ys, sf, sbw, ckv, kr


def kernel(**inputs):
    prog = Prog(dict(_CFG))
    nc = prog.build()
    maps = make_in_maps(inputs)
    res = run_bass_kernel_spmd(nc, maps, core_ids=list(range(NCORE)))
    return assemble(res.results)
```

```python
import numpy as np
import ml_dtypes
import concourse.bass as bass
import concourse.mybir as mybir
from concourse.bass_utils import run_bass_kernel_spmd

F32 = mybir.dt.float32
BF16 = mybir.dt.bfloat16
I32 = mybir.dt.int32
ALU = mybir.AluOpType
AF = mybir.ActivationFunctionType
AX = mybir.AxisListType

D = 1024
DEPTH = 2
SEQ = 256
DSEQ = 2048
PAST = 512
NCORE = 8
NPT = 1024
NST = 512
NTOK = NPT + NST
EPS = 1e-6
GN_EPS = 64e-5
EP = 24000
AGP = {"rk": 1024, "vx": 864, "f": 512}
VX_OFF = dict(v=0, lora=512, ckv=704, kr=832)


class Buf:
    def __init__(self, h, name, space):
        self.h = h
        self.name = name
        self.space = space
        self.w = {}
        self.r = {}
        self.dsem = None
        self.dcnt = 0
        self.dsid = None
        self.dcls = None

    def __getitem__(self, idx):
        return self.h[idx]

    def ap(self):
        return self.h.ap() if self.space == "dram" else self.h[:]


class FW:
    def __init__(self, nc):
        self.nc = nc
        self.E = {"pe": nc.tensor, "act": nc.scalar, "dve": nc.vector, "pool": nc.gpsimd, "sp": nc.sync}
        self.cnt = {e: 0 for e in self.E}
        self.esem = {e: [] for e in self.E}
        self.waited = {e: {} for e in self.E}
        self.pend = {e: [] for e in self.E}
        self.nbuf = 0
        self.ninst = 0
        self.dma_out = {}
        self.sem_pool = []
        self.stack = []
        self.rots = {}
        self.free_sems = {"pool": [], "hw": []}

    def sb(self, shape, dt, name=None):
        self.nbuf += 1
        name = name or "t"
        g = self.nc.sbuf_tensor(f"{name}_{self.nbuf}", list(shape), dt)
        h = g.__enter__()
        b = Buf(h, name, "sbuf")
        if self.stack:
            self.stack[-1].append((g, b))
        return b

    def ps(self, shape, dt=F32, name=None):
        self.nbuf += 1
        name = name or "p"
        h = self.nc.alloc_psum_tensor(f"{name}_{self.nbuf}", list(shape), dt)
        return Buf(h, name, "psum")

    def dram(self, name, shape, dt, kind=None):
        if kind is None:
            h = self.nc.dram_tensor(name, list(shape), dt)
        else:
            h = self.nc.dram_tensor(name, list(shape), dt, kind=kind)
        return Buf(h, name, "dram")

    def rot(self, shape, dt, name, n=2):
        key = (len(self.stack), name)
        if key not in self.rots:
            self.rots[key] = [[self.sb(shape, dt, name) for _ in range(n)], 0]
        ent = self.rots[key]
        t = ent[0][ent[1] % n]
        ent[1] += 1
        return t

    def push(self):
        self.stack.append([])

    def pop(self):
        self.barrier()
        depth = len(self.stack)
        for k in [k for k in self.rots if k[0] == depth]:
            del self.rots[k]
        for g, b in reversed(self.stack.pop()):
            if b.dsem is not None:
                self.free_sems[b.dcls].append((b.dsem, b.dcnt))
                b.dsem = None
            g.__exit__(None, None, None)

    def _sem_for(self, eng, k):
        i = (k - 1) // EP
        while len(self.esem[eng]) <= i:
            self.esem[eng].append(self.nc.alloc_semaphore(f"s_{eng}_{len(self.esem[eng])}"))
        return self.esem[eng][i], (k - 1) % EP + 1

    def _wait(self, eng, ev):
        if ev is None:
            return
        if ev[0] == "eng":
            _, e2, k = ev
            if e2 == eng and eng == "pe":
                return
            key = ("eng", e2, (k - 1) // EP)
            sem, val = self._sem_for(e2, k)
        else:
            _, sem, val, sid = ev
            key = ("sem", sid)
        if self.waited[eng].get(key, 0) >= val:
            return
        self.waited[eng][key] = val
        self.E[eng].wait_ge(sem, val)

    def _check_pend(self, eng, b):
        for e2, lst in self.pend.items():
            if e2 == eng:
                continue
            for (pb, _) in lst:
                if pb is b:
                    raise RuntimeError(f"buffer {b.name} has pending unsignalled access on {e2}, touched by {eng}")

    def _deps(self, eng, reads, writes):
        evs = []
        for b in reads:
            self._check_pend(eng, b)
            evs.extend(b.w.values())
        for b in writes:
            self._check_pend(eng, b)
            for wv in b.w.values():
                if not (wv[0] == "eng" and wv[1] == eng):
                    evs.append(wv)
            for ev in b.r.values():
                if ev[0] == "eng" and ev[1] == eng and eng == "pe":
                    continue
                evs.append(ev)
        for ev in evs:
            self._wait(eng, ev)

    def op(self, eng, fn, reads=(), writes=(), signal=True):
        self._deps(eng, reads, writes)
        ins = fn()
        self.ninst += 1
        if signal:
            self.cnt[eng] += 1
            k = self.cnt[eng]
            sem, val = self._sem_for(eng, k)
            ins.then_inc(sem, 1)
            ev = ("eng", eng, k)
            for (pb, kind) in self.pend[eng]:
                if kind == "r":
                    pb.r[eng] = ev
                else:
                    pb.w = {eng: ev}
                    pb.r = {}
            self.pend[eng] = []
            for b in reads:
                b.r[eng] = ev
            for b in writes:
                b.w = {eng: ev}
                b.r = {}
        else:
            for b in reads:
                self.pend[eng].append((b, "r"))
            for b in writes:
                self.pend[eng].append((b, "w"))
        return ins

    def _dma_sem(self, b, q="sp"):
        cls = "pool" if q == "pool" else "hw"
        if b.dsem is None:
            self.nbuf += 1
            if self.free_sems[cls]:
                b.dsem, b.dcnt = self.free_sems[cls].pop()
            else:
                b.dsem = self.nc.alloc_semaphore(f"d_{b.name}_{self.nbuf}")
            b.dsid = self.nbuf
            b.dcls = cls
        elif cls == "pool" and b.dcls != "pool":
            raise RuntimeError(f"buffer {b.name}: software DMA on a semaphore first used by a hardware-DGE DMA")
        return b.dsem

    def dma(self, q, out_b, out_ap, in_b, in_ap, sem_owner=None, inc=16, fn=None, extra_reads=(), **kw):
        eng = q
        evs = []
        for b in (in_b, out_b) + tuple(extra_reads):
            self._check_pend(eng, b)
        evs.extend(in_b.w.values())
        for b in extra_reads:
            evs.extend(b.w.values())
        owner = sem_owner or (out_b if out_b.space != "dram" else in_b)
        sem = self._dma_sem(owner, q)
        for wv in out_b.w.values():
            if wv[0] != "sem":
                evs.append(wv)
        for ev in out_b.r.values():
            evs.append(ev)
        for ev in evs:
            self._wait(eng, ev)
        if fn is None:
            ins = self.E[eng].dma_start(out=out_ap, in_=in_ap, **kw)
        else:
            ins = fn()
        self.ninst += 1
        owner.dcnt += inc
        ins.then_inc(sem, inc)
        ev = ("sem", sem, owner.dcnt, owner.dsid)
        in_b.r[("dma", owner.dsid)] = ev
        for b in extra_reads:
            b.r[("dma", owner.dsid)] = ev
        out_b.w = {k: v for k, v in out_b.w.items() if v[0] == "sem"}
        out_b.w[("dma", owner.dsid)] = ev
        out_b.r = {}
        self.dma_out[owner.dsid] = ev
        return ev

    def wait_buf(self, eng, b):
        self._check_pend(eng, b)
        for ev in b.w.values():
            self._wait(eng, ev)
        for ev in b.r.values():
            self._wait(eng, ev)

    def barrier(self):
        for e in self.E:
            if self.pend[e]:
                raise RuntimeError(f"barrier with pending unsignalled ops on {e}")
        last = []
        for e in ("pe", "act", "dve", "pool"):
            if self.cnt[e] > 0:
                last.append(("eng", e, self.cnt[e]))
        for e in self.E:
            for ev in last:
                if ev[1] == e and e == "pe":
                    continue
                self._wait(e, ev)
            for ev in self.dma_out.values():
                self._wait(e, ev)
        self.dma_out = {}


COLS = dict(r=(0, 512), k=(512, 1024), v=(1024, 1536), wdf=(1536, 1600), wdb=(1600, 1664), ad=(1664, 1728),
            ga=(1728, 2240), qd=(2240, 2496), kvd=(2496, 2624), kr=(2624, 2656), gb=(2656, 3168),
            u=(3168, 3680), vc=(3680, 4192), gc=(4192, 4704), f=(4704, 5216), gd=(5216, 5728))


def _rng(name):
    a, b = COLS[name]
    return np.arange(a, b)


_SWAP32 = np.arange(32).reshape(16, 2)[:, ::-1].reshape(32)

WIN_BLOCKS = [
    ("A0", [_rng("r")]), ("A1", [_rng("k")]), ("A2", [_rng("v")]),
    ("A3", [_rng("wdf"), _rng("wdb"), _rng("ad")]), ("A4", [_rng("ga")]),
    ("B0", [_rng("qd"), _rng("kvd"), _rng("kr"), _rng("kr")[_SWAP32]]), ("B1", [_rng("gb")]),
    ("C0", [_rng("u")]), ("C1", [_rng("vc")]), ("C2", [_rng("gc")]),
    ("D0", [_rng("f")]), ("D1", [_rng("gd")]),
]
WIN_W = {n: int(sum(len(c) for c in cols)) for n, cols in WIN_BLOCKS}
WIN_IDX = {n: 6 + i for i, (n, _) in enumerate(WIN_BLOCKS)}
BLK_PER_LAYER = 36
FBLK = 4096


def _kcp(w, W):
    return w.reshape(8, 128, W).transpose(1, 0, 2).reshape(128, 8 * W)


def build_stream(w_ada, w_in, w_branch, w_merge, w_out):
    st = np.zeros((DEPTH * BLK_PER_LAYER, 128, FBLK), np.float32)
    for l in range(DEPTH):
        base = l * BLK_PER_LAYER
        for b in range(6):
            st[base + b, :, :] = _kcp(w_ada[l][:, 512 * b:512 * b + 512], 512)
        for i, (n, cols) in enumerate(WIN_BLOCKS):
            cc = np.concatenate(cols)
            W = len(cc)
            st[base + 6 + i, :, :8 * W] = _kcp(w_in[l][:, cc], W)
        for d in range(8):
            cc = np.concatenate([n * 1024 + d * 128 + np.arange(128) for n in range(4)])
            st[base + 18 + 2 * d, :, :] = _kcp(w_merge[l][:, cc], 512)
            wb = w_branch[l][:, :, d * 128:(d + 1) * 128]
            wb = wb.reshape(4, 4, 128, 128).transpose(2, 0, 1, 3)
            st[base + 19 + 2 * d, :, :2048] = wb.reshape(128, 2048)
        for b in range(2):
            st[base + 34 + b, :, :] = _kcp(w_out[l][:, 512 * b:512 * b + 512], 512)
    return st


SP_OFF = {}
_o = 0
for _n, _w in [("norm_g", 8), ("b_ada", 24), ("mu_rkv", 12), ("mu_lora", 3), ("b_merge", 32), ("rw", 36),
               ("qn", 2), ("kvn", 1), ("gln_g", 4), ("gln_b", 4), ("fin_g", 8)]:
    SP_OFF[_n] = (_o, _w)
    _o += _w
NSP = _o
RW_NAMES = ["w0_f", "w0_b", "a0_f", "a0_b", "k_k", "k_a", "r_k", "ln_g", "ln_b"]


def _pc(v, n):
    return np.asarray(v, np.float32).reshape(n, 128).T


def build_small(inp):
    sp = np.zeros((DEPTH, 128, NSP), np.float32)
    for l in range(DEPTH):
        def put(name, arr):
            o, w = SP_OFF[name]
            sp[l, :, o:o + w] = arr
        put("norm_g", _pc(inp["norm_g"][l], 8))
        put("b_ada", _pc(inp["b_ada"][l], 24))
        put("mu_rkv", _pc(inp["shift_mu"][l][:1536], 12))
        ml = np.zeros((128, 3), np.float32)
        ml[:64, :] = inp["shift_mu"][l][1536:1728].reshape(3, 64).T
        put("mu_lora", ml)
        bm = inp["b_merge"][l].reshape(4, 8, 128)
        put("b_merge", bm.transpose(2, 1, 0).reshape(128, 32))
        rwv = [inp["rwkv_w0"][l][0], inp["rwkv_w0"][l][1], inp["rwkv_a0"][l][0], inp["rwkv_a0"][l][1],
               inp["rwkv_k_k"][l], inp["rwkv_k_a"][l], inp["rwkv_r_k"][l].reshape(512), inp["rwkv_ln_g"][l],
               inp["rwkv_ln_b"][l]]
        rw = np.stack([_pc(v, 4) for v in rwv], axis=1)
        put("rw", rw.reshape(128, 36))
        put("qn", _pc(inp["mla_q_norm"][l], 2))
        put("kvn", _pc(inp["mla_kv_norm"][l], 1))
        put("gln_g", _pc(inp["gmlp_ln_g"][l], 4))
        put("gln_b", _pc(inp["gmlp_ln_b"][l], 4))
        put("fin_g", _pc(inp["final_norm_g"], 8))
    return sp


def rw_col(name, pair):
    o, _ = SP_OFF["rw"]
    return o + RW_NAMES.index(name) * 4 + pair


def build_consts():
    c = {}
    c["ident"] = np.eye(128, dtype=np.float32)
    hb = np.arange(128) // 64
    c["bones"] = (hb[:, None] == hb[None, :]).astype(np.float32)
    p = np.arange(128)[:, None]
    f = np.arange(128)[None, :]
    mk = {}
    bd32 = ((p // 32) == (f // 32)).astype(np.float32)
    od64 = (((p // 64) == (f // 64)) & ((p // 32) != (f // 32))).astype(np.float32)
    od128 = ((p // 64) != (f // 64)).astype(np.float32)
    for dname, ms, mi, mt in (("f", p < f, p <= f, f < p), ("b", p > f, p >= f, f > p)):
        ms = ms.astype(np.float32)
        mi = mi.astype(np.float32)
        mtf = -(mt.astype(np.float32))
        mk[dname] = np.concatenate([ms, mi, -ms * bd32, mi, mtf * bd32, mtf * bd32, mtf * od64, mtf * od64,
                                    mtf * od128, mtf * od128], axis=1)
    c["mask"] = np.stack([mk["f"], mk["b"]], axis=1).reshape(128, 2 * 1280)
    dd = np.arange(128)
    ang = 2 * np.pi * np.outer(dd, dd) / 128.0
    for nm, T in (("dftd_p", SEQ), ("dftd_s", DSEQ)):
        sc = 1.0 / np.sqrt(T * 128.0)
        c[nm] = np.concatenate([np.cos(ang) * sc, -np.sin(ang) * sc], axis=1).astype(np.float32)
    tt = np.arange(SEQ)
    angp = 2 * np.pi * np.outer(tt, tt) / SEQ
    cp = np.stack([np.cos(angp), np.sin(angp)], axis=1)
    c["dftT_p"] = cp.reshape(2, 128, 2, SEQ).transpose(1, 0, 2, 3).reshape(128, 2 * 2 * SEQ).astype(np.float32)
    return c


def build_core_consts(j):
    t = np.arange(DSEQ)
    k1 = 512 * j + np.arange(512)
    ang = 2 * np.pi * ((np.outer(t, k1)) % DSEQ) / DSEQ
    cs = np.stack([np.cos(ang), np.sin(ang)], axis=1)
    dftT = cs.reshape(16, 128, 2, 512).astype(np.float32)
    pos = 512 * j + np.arange(512)
    row = (pos // 64).astype(np.float32)
    col = (pos % 64).astype(np.float32)
    inv = (10000.0 ** (-np.arange(8, dtype=np.float32) / 8)).astype(np.float32)
    ang = np.concatenate([row[:, None] * inv, col[:, None] * inv], axis=-1).astype(np.float32)
    cos = np.cos(ang).astype(np.float32)
    sin = np.sin(ang).astype(np.float32)
    COS = np.repeat(cos, 2, axis=1).T
    SIN = np.stack([-sin, sin], axis=2).reshape(512, 32).T
    rope = np.stack([COS, SIN], axis=1).astype(np.float32)
    return dftT, rope


class Prog:
    def __init__(self, cfg):
        self.cfg = cfg
        self.branches = cfg.get("branches", "ABCD")
        self.depth = cfg.get("depth", DEPTH)
        nc = bass.Bass("TRN2", target_bir_lowering=False)
        self.nc = nc
        fw = FW(nc)
        self.fw = fw
        self.V, self.A, self.G, self.T = nc.vector, nc.scalar, nc.gpsimd, nc.tensor
        di = lambda n, s, dt=F32: fw.dram(n, s, dt, kind="ExternalInput")
        do = lambda n, s, dt=F32: fw.dram(n, s, dt, kind="ExternalOutput")
        self.d = dict(
            xp=di("xp", [D, NPT]), xs=di("xs", [D, NST]),
            wst=di("wst", [DEPTH * BLK_PER_LAYER, 128, FBLK]),
            sp=di("sp", [DEPTH, 128, NSP]), cond=di("cond", [128, 16]),
            ident=di("ident", [128, 128]), bones=di("bones", [128, 128]), mask=di("mask", [128, 2560]),
            dftd_p=di("dftd_p", [128, 256]), dftd_s=di("dftd_s", [128, 256]), dftT_p=di("dftT_p", [128, 1024]),
            dftT_s=di("dftT_s", [16, 128, 1024]), rope=di("rope", [32, 1024]),
            wup=di("wup", [DEPTH, 64, 1024]), aup=di("aup", [DEPTH, 64, 1024]),
            wupo=di("wupo", [DEPTH, 64, 256]), aupo=di("aupo", [DEPTH, 64, 256]),
            spo=di("spo", [DEPTH, 128, 12]),
            wq=di("wq", [DEPTH, 128, 2 * 8 * 96]), wqs=di("wqs", [DEPTH, 128, 2 * 8 * 32]),
            wkk=di("wkk", [DEPTH, 128, 512]), wkv=di("wkv", [DEPTH, 128, 512]),
            wsT=di("wsT", [DEPTH, 128, 512]), bsb=di("bsb", [DEPTH, 128, 512]),
            st0=di("st0", [DEPTH, 2, 128, 64]), cckv=di("cckv", [DEPTH, 128, PAST]),
            ckr=di("ckr", [DEPTH, 32, PAST]),
            idx1=di("idx1", [128, 12], I32), idx2=di("idx2", [128, 4], I32),
            yp=do("yp", [D, NPT]), ys=do("ys", [D, NST]),
            stout=do("stout", [DEPTH * 2 * 4 * 4 * 128, 64]),
            ckvout=do("ckvout", [DEPTH, 128, NPT]), krout=do("krout", [DEPTH, 32, NPT]),
        )
        self.ag1_in = [{k: fw.dram(f"ag1i{k}{l}", [n, 512], BF16) for k, n in AGP.items()} for l in range(DEPTH)]
        self.ag1_out = [{k: fw.dram(f"ag1o{k}{l}", [4 * n, 512], BF16) for k, n in AGP.items()} for l in range(DEPTH)]
        self.ag2_in = [fw.dram(f"ag2i{l}", [512, 512], BF16) for l in range(DEPTH)]
        self.ag2_out = [fw.dram(f"ag2o{l}", [2048, 512], BF16) for l in range(DEPTH)]
        self.psb = [fw.ps([128, 512], F32, f"bank{i}") for i in range(6)]
        self.pst = [fw.ps([128, 1024], BF16, f"pst{i}") for i in range(2)]
        self.pst_rr = 0
        self.ps_rr = 0
        self.xT = fw.sb([128, 8, NTOK], F32, "xT")
        self.hTs = [fw.sb([128, 8, 512], BF16, "hTa"), None]
        self.oTs = [[fw.sb([128, 4, 512], BF16, f"oTa{n}") for n in range(4)], None]
        self.slots = None
        self.slot_rr = 0
        self.plan = []
        self.plan_pos = 0
        self.stg = None
        self.blocks_left = 0
        self.dma_issued = {}
        self.load_consts()

    ps_range = (0, 4)

    def nps(self, lo=None, hi=None):
        lo = self.ps_range[0] if lo is None else lo
        hi = self.ps_range[1] if hi is None else hi
        n = hi - lo
        b = self.psb[lo + (self.ps_rr % n)]
        self.ps_rr += 1
        return b

    def dve(self, fn, r, w):
        return self.fw.op("dve", fn, r, w)

    def rsqrt(self, out_buf, out_ap, in_buf, in_ap):
        self.act(lambda: self.A.activation(in_ap, in_ap, AF.Sqrt), [in_buf], [in_buf])
        self.dve(lambda: self.V.reciprocal(out_ap, in_ap), [in_buf], [out_buf])

    def act(self, fn, r, w):
        return self.fw.op("act", fn, r, w)

    def pool(self, fn, r, w):
        return self.fw.op("pool", fn, r, w)

    def pe(self, fn, r, w, signal=True):
        return self.fw.op("pe", fn, r, w, signal=signal)

    def load(self, dst, dst_ap, src, src_ap, q="sp"):
        if dst_ap.dtype != src_ap.dtype:
            q = "pool"
        return self.fw.dma(q, dst, dst_ap, src, src_ap)

    def load_consts(self):
        fw, d = self.fw, self.d
        self.ident = fw.sb([128, 128], BF16, "ident")
        self.bones = fw.sb([128, 128], BF16, "bones")
        self.mask = fw.sb([128, 2560], BF16, "mask")
        self.ones = fw.sb([128, 128], BF16, "ones")
        self.onesf = fw.sb([128, 128], F32, "onesf")
        self.rope = None
        self.cond = fw.sb([128, 16], F32, "cond")
        self.idx1 = fw.sb([128, 12], I32, "idx1")
        self.idx2 = fw.sb([128, 4], I32, "idx2")
        for nm in ("ident", "bones", "mask", "cond", "idx1", "idx2"):
            t = getattr(self, nm)
            self.load(t, t[:], d[nm], d[nm].ap())
        self.pool(lambda: self.G.memset(self.ones[:], 1.0), [], [self.ones])
        self.pool(lambda: self.G.memset(self.onesf[:], 1.0), [], [self.onesf])
        self.hm = fw.sb([128, 2], F32, "hm")
        self.dve(lambda: self.V.tensor_copy(self.hm[:, 0:2], self.bones[:, 0:128:64]), [self.bones], [self.hm])
        xv = self.xT
        self.load(xv, xv[:, :, 0:NPT], d["xp"], d["xp"].ap().rearrange("(k p) t -> p k t", p=128))
        self.load(xv, xv[:, :, NPT:NTOK], d["xs"], d["xs"].ap().rearrange("(k p) t -> p k t", p=128))

    def stream_plan(self, ids):
        self.plan.extend(ids)

    def stream_begin(self, nblocks, depth=1):
        fw = self.fw
        self.sdepth = depth
        self.slots = [fw.sb([128, FBLK], BF16, f"slot{i}") for i in range(depth + 1)]
        self.stg = [fw.sb([128, FBLK // 2], F32, f"wstg{i}") for i in range(2 * depth)]
        self.blocks_left = nblocks
        self.scope_end = self.plan_pos + nblocks
        self.dma_issued = {}

    def _issue_dma(self, pos):
        blk = self.plan[pos]
        src = self.d["wst"]
        for hf in range(2):
            st = self.stg[(2 * pos + hf) % len(self.stg)]
            self.fw.dma("sp", st, st[:, :], src, src[blk, :, hf * (FBLK // 2):(hf + 1) * (FBLK // 2)])
        self.dma_issued[pos] = True

    def next_block(self, blk):
        pos = self.plan_pos
        assert self.plan[pos] == blk, (pos, self.plan[pos], blk)
        assert self.blocks_left > 0
        if pos not in self.dma_issued:
            self._issue_dma(pos)
        slot = self.slots[pos % len(self.slots)]
        for hf in range(2):
            st = self.stg[(2 * pos + hf) % len(self.stg)]
            if hf == 0:
                self.dve(lambda hf=hf, st=st: self.V.tensor_copy(slot[:, hf * (FBLK // 2):(hf + 1) * (FBLK // 2)], st[:, :]), [st], [slot])
            else:
                self.act(lambda hf=hf, st=st: self.A.copy(slot[:, hf * (FBLK // 2):(hf + 1) * (FBLK // 2)], st[:, :]), [st], [slot])
        self.plan_pos += 1
        self.blocks_left -= 1
        return slot

    def prefetch_next(self):
        for pos in range(self.plan_pos, min(self.plan_pos + self.sdepth, self.scope_end)):
            if pos not in self.dma_issued:
                self._issue_dma(pos)

    def load_layer_small(self, l):
        fw, d = self.fw, self.d
        self.sp = fw.sb([128, NSP], F32, "sp")
        self.load(self.sp, self.sp[:], d["sp"], d["sp"][l, :, :])
        sp = self.sp
        o, _ = SP_OFF["mu_rkv"]
        self.mu1 = fw.sb([128, 15], F32, "mu1")
        self.muh = fw.sb([128, 15], F32, "muh")
        self.dve(lambda: self.V.tensor_scalar(self.mu1[:], sp[:, o:o + 15], -1.0, 1.0, ALU.mult, ALU.add), [sp], [self.mu1])
        self.dve(lambda: self.V.tensor_scalar(self.muh[:], sp[:, o:o + 15], 0.5, None, ALU.mult), [sp], [self.muh])
        o2, _ = SP_OFF["rw"]
        self.rwh = fw.sb([128, 16], F32, "rwh")
        self.dve(lambda: self.V.tensor_scalar(self.rwh[:], sp[:, o2:o2 + 16], 0.5, None, ALU.mult), [sp], [self.rwh])
        ob, _ = SP_OFF["b_merge"]
        self.bmh = fw.sb([128, 32], F32, "bmh")
        self.dve(lambda: self.V.tensor_scalar(self.bmh[:], sp[:, ob:ob + 32], 0.5, None, ALU.mult), [sp], [self.bmh])

    def spc(self, name, i=0, n=1):
        o, _ = SP_OFF[name]
        return self.sp[:, o + i:o + i + n]

    def ada(self, l):
        fw = self.fw
        sc = fw.sb([128, 16], BF16, "scond")
        th = fw.sb([128, 16], F32, "cth")
        c = self.cond
        self.act(lambda: self.A.activation(th[:], c[:], AF.Tanh, scale=0.5), [c], [th])
        t2 = fw.sb([128, 16], F32, "ct2")
        self.dve(lambda: self.V.scalar_tensor_tensor(t2[:], th[:], 1.0, c[:], ALU.add, ALU.mult), [th, c], [t2])
        self.dve(lambda: self.V.tensor_scalar(sc[:], t2[:], 0.5, None, ALU.mult), [t2], [sc])
        ps = self.psb[5]
        self.mod = fw.sb([128, 24, 2], F32, "mod")
        self.gmod = fw.sb([128, 8, 2], F32, "gmod")
        fw.push()
        self.stream_begin(6, depth=2)
        for b in range(6):
            slot = self.next_block(l * BLK_PER_LAYER + b)
            for n in range(4):
                m = b * 4 + n
                for kc in range(8):
                    self.pe(lambda kc=kc, n=n, m=m, slot=slot: self.T.matmul(
                        ps[:, 2 * m:2 * m + 2], slot[:, kc * 512 + n * 128:kc * 512 + n * 128 + 128],
                        sc[:, 2 * kc:2 * kc + 2], start=(kc == 0), stop=(kc == 7)),
                        [slot, sc], [ps], signal=(kc == 7 and n == 3))
            self.prefetch_next()
        fw.pop()
        ob, _ = SP_OFF["b_ada"]
        for cc in range(2):
            self.dve(lambda cc=cc: self.V.tensor_tensor(self.mod[:, :, cc], ps[:, cc:48:2], self.sp[:, ob:ob + 24], ALU.add),
                     [ps, self.sp], [self.mod])
        og, _ = SP_OFF["norm_g"]
        for cc in range(2):
            self.dve(lambda cc=cc: self.V.scalar_tensor_tensor(self.gmod[:, :, cc], self.mod[:, 8:16, cc], 1.0,
                                                               self.sp[:, og:og + 8], ALU.add, ALU.mult),
                     [self.mod, self.sp], [self.gmod])

    def rstd_tile(self, src, views, nfeat, out, N):
        fw = self.fw
        ps = self.nps()
        nk = len(views)
        for i, v in enumerate(views):
            sq = fw.rot([128, 512], BF16, "sq")
            self.act(lambda v=v, sq=sq: self.A.activation(sq[:, :N], v, AF.Square), [src], [sq])
            self.pe(lambda i=i, sq=sq: self.T.matmul(ps[:, :N], self.ones[:], sq[:, :N], start=(i == 0), stop=(i == nk - 1)),
                    [self.ones, sq], [ps], signal=True)
        t = fw.sb([128, 512], F32, "rs_t")
        self.dve(lambda: self.V.tensor_scalar(t[:, :N], ps[:, :N], 1.0 / nfeat, EPS, ALU.mult, ALU.add), [ps], [t])
        self.rsqrt(out, out[:, :N], t, t[:, :N])

    def make_h(self, cc, x0, h0, N):
        fw = self.fw
        fw.push()
        rstd = fw.sb([128, 512], F32, "rstd")
        self.rstd_tile(self.xT, [self.xT[:, kc, x0:x0 + N] for kc in range(8)], float(D), rstd, N)
        for kc in range(8):
            tmp = fw.rot([128, 512], F32, "htmp")
            self.dve(lambda kc=kc, tmp=tmp: self.V.scalar_tensor_tensor(
                tmp[:, :N], self.xT[:, kc, x0:x0 + N], self.gmod[:, kc, cc:cc + 1], rstd[:, :N], ALU.mult, ALU.mult),
                [self.xT, self.gmod, rstd], [tmp])
            self.act(lambda kc=kc, tmp=tmp: self.A.activation(
                self.hTs[h0 // 512][:, kc, 0:N], tmp[:, :N], AF.Identity, bias=self.mod[:, kc, cc:cc + 1], scale=1.0),
                [tmp, self.mod], [self.hTs[h0 // 512]])
        fw.pop()

    def zmm(self, slot, W, c0, w, h0, N, ps=None, prow=0):
        ps = ps or self.nps()
        for kc in range(8):
            self.pe(lambda kc=kc: self.T.matmul(ps[prow:prow + w, :N], slot[:, kc * W + c0:kc * W + c0 + w],
                                                self.hTs[h0 // 512][:, kc, 0:N], start=(kc == 0), stop=(kc == 7)),
                    [slot, self.hTs[h0 // 512]], [ps], signal=(kc == 7))
        return ps

    def silu2(self, ps, rows, N, out_ap, out_buf):
        fw = self.fw
        th = fw.rot([128, 512], F32, "s2th")
        self.act(lambda: self.A.activation(th[:rows, :N], ps[:rows, :N], AF.Tanh, scale=0.5), [ps], [th])
        self.dve(lambda: self.V.scalar_tensor_tensor(out_ap, th[:rows, :N], 1.0, ps[:rows, :N], ALU.add, ALU.mult),
                 [th, ps], [out_buf])

    def gelu2(self, ps, rows, N, out_ap, out_buf):
        fw = self.fw
        u = fw.rot([128, 512], F32, "g2u")
        self.act(lambda: self.A.activation(u[:rows, :N], ps[:rows, :N], AF.Square), [ps], [u])
        self.dve(lambda: self.V.tensor_scalar(u[:rows, :N], u[:rows, :N], 0.044715, 1.0, ALU.mult, ALU.add), [u], [u])
        self.dve(lambda: self.V.tensor_tensor(u[:rows, :N], u[:rows, :N], ps[:rows, :N], ALU.mult), [u, ps], [u])
        self.act(lambda: self.A.activation(u[:rows, :N], u[:rows, :N], AF.Tanh, scale=0.7978845608028654), [u], [u])
        self.dve(lambda: self.V.scalar_tensor_tensor(out_ap, u[:rows, :N], 1.0, ps[:rows, :N], ALU.add, ALU.mult),
                 [u, ps], [out_buf])

    def transpose_to(self, src_buf, src_ap, dst_buf, dst_ap, rows=128, cols=128, eng="act"):
        pt = self.pst[self.pst_rr % 2]
        self.pst_rr += 1
        self.pe(lambda: self.T.transpose(pt[:cols, :rows], src_ap, self.ident[:rows, :rows]), [src_buf, self.ident], [pt])
        if eng == "act":
            self.act(lambda: self.A.copy(dst_ap, pt[:cols, :rows]), [pt], [dst_buf])
        else:
            self.dve(lambda: self.V.tensor_copy(dst_ap, pt[:cols, :rows]), [pt], [dst_buf])

    def phaseC(self, l, h0, N, o0):
        fw = self.fw
        base = l * BLK_PER_LAYER
        fw.push()
        self.stream_begin(3, depth=2)
        U2 = fw.sb([128, 4, 512], BF16, "U2")
        GV = fw.sb([128, 4, 512], BF16, "GV")
        GC2 = fw.sb([128, 4, 512], BF16, "GC2")
        wsT = fw.sb([128, 512], BF16, "wsT")
        bsb = fw.sb([128, 512], F32, "bsb")
        self.load(wsT, wsT[:], self.d["wsT"], self.d["wsT"][l, :, :])
        self.load(bsb, bsb[:], self.d["bsb"], self.d["bsb"][l, :, :])
        slot = self.next_block(base + WIN_IDX["C0"])
        for c in range(4):
            ps = self.zmm(slot, 512, c * 128, 128, h0, N)
            self.gelu2(ps, 128, N, U2[:, c, :N], U2)
        self.prefetch_next()
        slot = self.next_block(base + WIN_IDX["C1"])
        for c in range(4):
            ps = self.zmm(slot, 512, c * 128, 128, h0, N)
            self.gelu2(ps, 128, N, GV[:, c, :N], GV)
        self.prefetch_next()
        slot = self.next_block(base + WIN_IDX["C2"])
        for c in range(4):
            ps = self.zmm(slot, 512, c * 128, 128, h0, N)
            self.silu2(ps, 128, N, GC2[:, c, :N], GC2)
        self.prefetch_next()
        psm = self.nps()
        psq = self.nps()
        for c in range(4):
            self.pe(lambda c=c: self.T.matmul(psm[:, :N], self.ones[:], GV[:, c, :N], start=(c == 0), stop=(c == 3)),
                    [self.ones, GV], [psm], signal=(c == 3))
        for c in range(4):
            sq = fw.rot([128, 512], BF16, "gsq")
            self.act(lambda c=c, sq=sq: self.A.activation(sq[:, :N], GV[:, c, :N], AF.Square), [GV], [sq])
            self.pe(lambda c=c, sq=sq: self.T.matmul(psq[:, :N], self.ones[:], sq[:, :N], start=(c == 0), stop=(c == 3)),
                    [self.ones, sq], [psq], signal=True)
        mu = fw.sb([128, 512], F32, "gmu")
        msq = fw.sb([128, 512], F32, "gmsq")
        var = fw.sb([128, 512], F32, "gvar")
        rstd = fw.sb([128, 512], F32, "grstd")
        self.dve(lambda: self.V.tensor_scalar(mu[:, :N], psm[:, :N], 1.0 / 512, None, ALU.mult), [psm], [mu])
        self.dve(lambda: self.V.tensor_tensor(msq[:, :N], mu[:, :N], mu[:, :N], ALU.mult), [mu], [msq])
        self.dve(lambda: self.V.scalar_tensor_tensor(var[:, :N], psq[:, :N], 1.0 / 512, msq[:, :N], ALU.mult, ALU.subtract),
                 [psq, msq], [var])
        self.dve(lambda: self.V.tensor_scalar(var[:, :N], var[:, :N], 4e-5, None, ALU.add), [var], [var])
        self.rsqrt(rstd, rstd[:, :N], var, var[:, :N])
        VN = fw.sb([128, 4, 512], BF16, "VN")
        for c in range(4):
            t = fw.rot([128, 512], F32, "lnt")
            self.dve(lambda c=c, t=t: self.V.tensor_tensor(t[:, :N], GV[:, c, :N], mu[:, :N], ALU.subtract), [GV, mu], [t])
            self.dve(lambda t=t: self.V.tensor_tensor(t[:, :N], t[:, :N], rstd[:, :N], ALU.mult), [t, rstd], [t])
            self.act(lambda c=c, t=t: self.A.activation(VN[:, c, :N], t[:, :N], AF.Identity, bias=self.spc("gln_b", c),
                                                        scale=self.spc("gln_g", c)), [t, self.sp], [VN])
        nsub = N // 128
        for g in range(4):
            pmix = self.nps()
            for s in range(nsub):
                vtm = fw.rot([128, 128], BF16, "vtm")
                self.transpose_to(VN, VN[:, g, s * 128:(s + 1) * 128], vtm, vtm[:], eng=("act" if s % 2 else "dve"))
                self.pe(lambda g=g, s=s, vtm=vtm: self.T.matmul(pmix[:, s * 128:(s + 1) * 128], vtm[:],
                                                                 wsT[:, g * 128:(g + 1) * 128], start=True, stop=True),
                        [vtm, wsT], [pmix], signal=(s == nsub - 1))
            t = fw.rot([128, 512], F32, "mixt")
            for s in range(nsub):
                self.dve(lambda g=g, s=s, t=t: self.V.tensor_tensor(t[:, s * 128:(s + 1) * 128], pmix[:, s * 128:(s + 1) * 128],
                                                                     bsb[:, g * 128:(g + 1) * 128], ALU.add), [pmix, bsb], [t])
            self.dve(lambda g=g, t=t: self.V.scalar_tensor_tensor(t[:, :N], t[:, :N], 0.25, U2[:, g, :N], ALU.mult, ALU.mult),
                     [t, U2], [t])
            self.dve(lambda g=g, t=t: self.V.tensor_tensor(self.oTs[o0 // 512][2][:, g, 0:N], t[:, :N], GC2[:, g, :N], ALU.mult),
                     [t, GC2], [self.oTs[o0 // 512][2]])
        fw.pop()

    def merge_out(self, l, cc, x0, NT):
        fw = self.fw
        base = l * BLK_PER_LAYER
        ntile = NT // 512
        fw.push()
        self.stream_begin(18, depth=2)
        merged = fw.sb([128, 8, NT], BF16, "merged")
        for d in range(8):
            slotM = self.next_block(base + 18 + 2 * d)
            self.prefetch_next()
            slotB = self.next_block(base + 19 + 2 * d)
            for tt in range(ntile):
                t0 = tt * 512
                acc = fw.rot([128, 512], F32, "macc")
                for n in range(4):
                    psg = self.nps()
                    for kc in range(8):
                        self.pe(lambda kc=kc, n=n, tt=tt: self.T.matmul(psg[:, :], slotM[:, kc * 512 + n * 128:kc * 512 + n * 128 + 128],
                                                                 self.hTs[tt][:, kc, :], start=(kc == 0), stop=(kc == 7)),
                                [slotM, self.hTs[tt]], [psg], signal=(kc == 7))
                    psp = self.nps()
                    for k4 in range(4):
                        self.pe(lambda k4=k4, n=n, tt=tt: self.T.matmul(psp[:, :], slotB[:, (n * 4 + k4) * 128:(n * 4 + k4) * 128 + 128],
                                                                 self.oTs[tt][n][:, k4, :], start=(k4 == 0), stop=(k4 == 3)),
                                [slotB, self.oTs[tt][n]], [psp], signal=(k4 == 3))
                    th = fw.rot([128, 512], F32, "mth")
                    self.act(lambda n=n, th=th, psg=psg: self.A.activation(th[:], psg[:], AF.Tanh, bias=self.bmh[:, d * 4 + n:d * 4 + n + 1],
                                                                           scale=0.5), [psg, self.bmh], [th])
                    if n == 0:
                        self.dve(lambda th=th, psp=psp: self.V.scalar_tensor_tensor(acc[:], th[:], 1.0, psp[:], ALU.add, ALU.mult),
                                 [th, psp], [acc])
                    else:
                        self.dve(lambda th=th, psp=psp: self.V.scalar_tensor_tensor(th[:], th[:], 1.0, psp[:], ALU.add, ALU.mult),
                                 [th, psp], [th])
                        self.dve(lambda th=th: self.V.tensor_tensor(acc[:], acc[:], th[:], ALU.add), [acc, th], [acc])
                self.act(lambda acc=acc, t0=t0: self.A.mul(merged[:, d, t0:t0 + 512], acc[:], 0.5), [acc], [merged])
            self.prefetch_next()
        for b in range(2):
            slotO = self.next_block(base + 34 + b)
            self.prefetch_next()
            for dd in range(4):
                dch = b * 4 + dd
                for tt in range(ntile):
                    t0 = tt * 512
                    ps = self.nps()
                    for kc in range(8):
                        self.pe(lambda kc=kc, dd=dd: self.T.matmul(ps[:, :], slotO[:, kc * 512 + dd * 128:kc * 512 + dd * 128 + 128],
                                                                   merged[:, kc, t0:t0 + 512], start=(kc == 0), stop=(kc == 7)),
                                [slotO, merged], [ps], signal=(kc == 7))
                    self.dve(lambda dch=dch, t0=t0, ps=ps: self.V.scalar_tensor_tensor(
                        self.xT[:, dch, x0 + t0:x0 + t0 + 512], ps[:, :], self.mod[:, 16 + dch, cc:cc + 1],
                        self.xT[:, dch, x0 + t0:x0 + t0 + 512], ALU.mult, ALU.add), [ps, self.mod, self.xT], [self.xT])
        fw.pop()

    def final_out(self):
        fw = self.fw
        for (x0, N, dst) in ((0, 512, ("yp", 0)), (512, 512, ("yp", 512)), (NPT, 512, ("ys", 0))):
            fw.push()
            rstd = fw.sb([128, 512], F32, "frstd")
            self.rstd_tile(self.xT, [self.xT[:, kc, x0:x0 + N] for kc in range(8)], float(D), rstd, N)
            stg = fw.sb([128, 8, 512], F32, "fstg")
            for kc in range(8):
                self.dve(lambda kc=kc: self.V.scalar_tensor_tensor(stg[:, kc, :], self.xT[:, kc, x0:x0 + N], self.spc("fin_g", kc),
                                                                   rstd[:, :], ALU.mult, ALU.mult), [self.xT, self.sp, rstd], [stg])
            dt = self.d[dst[0]]
            ncol = NPT if dst[0] == "yp" else NST
            dview = dt.ap().rearrange("(k p) t -> p k t", p=128)[:, :, dst[1]:dst[1] + N]
            fw.dma("sp", dt, dview, stg, stg[:], sem_owner=self.outsem)
            fw.pop()

    def shift_evac(self, ps, rows, N, nseq, mu1, muh, out_ap, out_buf, tanh=False):
        fw = self.fw
        zt = fw.rot([128, 512], F32, "shz")
        o32 = fw.rot([128, 512], F32, "sho")
        self.act(lambda: self.A.copy(zt[:rows, :N], ps[:rows, :N]), [ps], [zt])
        self.dve(lambda: self.V.tensor_scalar(o32[:rows, :N], zt[:rows, :N], mu1, None, ALU.mult), [zt, self.mu1], [o32])
        z3 = zt[:rows, :N].rearrange("p (s t) -> p s t", s=nseq)
        o3 = o32[:rows, :N].rearrange("p (s t) -> p s t", s=nseq)
        Tq = N // nseq
        self.dve(lambda: self.V.scalar_tensor_tensor(o3[:, :, 1:Tq], z3[:, :, 0:Tq - 1], muh, o3[:, :, 1:Tq], ALU.mult, ALU.add),
                 [zt, self.muh, o32], [o32])
        self.dve(lambda: self.V.scalar_tensor_tensor(o3[:, :, 0:Tq - 1], z3[:, :, 1:Tq], muh, o3[:, :, 0:Tq - 1], ALU.mult, ALU.add),
                 [zt, self.muh, o32], [o32])
        if tanh:
            self.act(lambda: self.A.activation(out_ap, o32[:rows, :N], AF.Tanh), [o32], [out_buf])
        else:
            self.act(lambda: self.A.copy(out_ap, o32[:rows, :N]), [o32], [out_buf])

    def load_rwkv_w(self, l, own):
        fw, d = self.fw, self.d
        ncol = 256 if own else 1024
        self.wup = fw.sb([64, ncol], BF16, "wup")
        self.aup = fw.sb([64, ncol], BF16, "aup")
        sw, sa = (d["wupo"], d["aupo"]) if own else (d["wup"], d["aup"])
        self.load(self.wup, self.wup[:, :], sw, sw[l, :, :])
        self.load(self.aup, self.aup[:, :], sa, sa[l, :, :])

    def phaseA_prompt(self, l, half):
        fw = self.fw
        base = l * BLK_PER_LAYER
        h0 = half * 512
        fw.push()
        self.load_rwkv_w(l, False)
        zz = [fw.sb([128, 4, 512], BF16, nm) for nm in ("zr", "zk", "zv")]
        lo = [fw.sb([64, 512], BF16, nm) for nm in ("twdf", "twdb", "adT")]
        GA2 = fw.sb([128, 4, 512], BF16, "GA2")
        fw.push()
        self.stream_begin(5, depth=2)
        for which in range(3):
            slot = self.next_block(base + WIN_IDX[f"A{which}"])
            for c in range(4):
                ps = self.zmm(slot, 512, c * 128, 128, h0, 512)
                i = which * 4 + c
                self.shift_evac(ps, 128, 512, 2, self.mu1[:, i:i + 1], self.muh[:, i:i + 1], zz[which][:, c, :], zz[which])
            self.prefetch_next()
        slot = self.next_block(base + WIN_IDX["A3"])
        for i in range(3):
            ps = self.zmm(slot, 192, i * 64, 64, h0, 512)
            self.shift_evac(ps, 64, 512, 2, self.mu1[0:64, 12 + i:13 + i], self.muh[0:64, 12 + i:13 + i], lo[i][:, :], lo[i], tanh=(i < 2))
        self.prefetch_next()
        slot = self.next_block(base + WIN_IDX["A4"])
        for c in range(4):
            ps = self.zmm(slot, 512, c * 128, 128, h0, 512)
            self.silu2(ps, 128, 512, GA2[:, c, :], GA2)
        self.prefetch_next()
        fw.pop()
        jobs = []
        for sq in range(2):
            for pair in range(4):
                t0 = sq * 256
                seqi = half * 2 + sq

                def yout(c, yfin, pair=pair, t0=t0):
                    cs = t0 + c * 128
                    self.dve(lambda: self.V.scalar_tensor_tensor(self.oTs[half][0][:, pair, cs:cs + 128], yfin[:, :], 0.5,
                                                                 GA2[:, pair, cs:cs + 128], ALU.mult, ALU.mult), [yfin, GA2], [self.oTs[half][0]])

                def stout(dd, ST, pair=pair, seqi=seqi):
                    so = self.d["stout"]
                    row = (((l * 2 + dd) * 4 + seqi) * 4 + pair) * 128
                    fw.dma("sp", so, so[row:row + 128, :], ST, ST[:, :], sem_owner=self.outsem)

                J = dict(T=256, r=(zz[0], lambda a, b, pair=pair, t0=t0: zz[0][:, pair, t0 + a:t0 + b]),
                         k=(zz[1], lambda a, b, pair=pair, t0=t0: zz[1][:, pair, t0 + a:t0 + b]),
                         v=(zz[2], lambda a, b, pair=pair, t0=t0: zz[2][:, pair, t0 + a:t0 + b]),
                         twd=[(lo[0], lambda a, b, t0=t0: lo[0][:, t0 + a:t0 + b]), (lo[1], lambda a, b, t0=t0: lo[1][:, t0 + a:t0 + b])],
                         ad=(lo[2], lambda a, b, t0=t0: lo[2][:, t0 + a:t0 + b]),
                         par=lambda nm, pair=pair: self.sp[:, rw_col(nm, pair):rw_col(nm, pair) + 1],
                         parh=lambda nm, pair=pair: self.rwh[:, RW_NAMES.index(nm) * 4 + pair:RW_NAMES.index(nm) * 4 + pair + 1],
                         parbufs=[self.sp, self.rwh],
                         wup=lambda dd, pair=pair: self.wup[:, dd * 512 + pair * 128:dd * 512 + pair * 128 + 128],
                         aup=lambda dd, pair=pair: self.aup[:, dd * 512 + pair * 128:dd * 512 + pair * 128 + 128],
                         st0=None, yout=yout, stout=stout, scoped=False, tag=len(jobs) % 2)
                jobs.append(J)
        def until_early(g):
            for m in g:
                if m == "early_done":
                    return True
                yield
        cur = self.rwkv_job_gen(jobs[0])
        for _ in until_early(cur):
            pass
        for k in range(len(jobs)):
            nxt = self.rwkv_job_gen(jobs[k + 1]) if k + 1 < len(jobs) else None
            ne = until_early(nxt) if nxt is not None else None
            cur_alive, ne_alive = True, ne is not None
            while cur_alive or ne_alive:
                if cur_alive:
                    try:
                        next(cur)
                    except StopIteration:
                        cur_alive = False
                if ne_alive:
                    try:
                        next(ne)
                    except StopIteration:
                        ne_alive = False
            cur = nxt
        self.ps_range = (0, 4)
        fw.pop()

    def phaseA_contrib(self, l):
        fw = self.fw
        base = l * BLK_PER_LAYER
        self.GA2 = fw.sb([128, 4, 512], BF16, "GA2s")
        fw.push()
        self.stream_begin(5, depth=2)
        for which, (part, row0) in enumerate((("rk", 0), ("rk", 512), ("vx", VX_OFF["v"]))):
            slot = self.next_block(base + WIN_IDX[f"A{which}"])
            for c in range(4):
                self.contrib_rows(l, slot, 512, c * 128, 128, part, row0 + c * 128)
            self.prefetch_next()
        slot = self.next_block(base + WIN_IDX["A3"])
        for i in range(3):
            self.contrib_rows(l, slot, 192, i * 64, 64, "vx", VX_OFF["lora"] + i * 64)
        self.prefetch_next()
        slot = self.next_block(base + WIN_IDX["A4"])
        for c in range(4):
            ps = self.zmm(slot, 512, c * 128, 128, 0, 512)
            self.silu2(ps, 128, 512, self.GA2[:, c, :], self.GA2)
        self.prefetch_next()
        fw.pop()

    def gather_rows(self, dst, dst_ap, src, idx_col):
        fw = self.fw
        idx = self.idx1 if idx_col < 12 else self.idx2
        col = idx_col if idx_col < 12 else idx_col - 12
        fw.dma("pool", dst, None, src, None, extra_reads=[idx],
               fn=lambda: self.G.indirect_dma_start(out=dst_ap, out_offset=None, in_=src.h.ap(),
                                                    in_offset=bass.IndirectOffsetOnAxis(ap=idx[:, col:col + 1], axis=0)))

    def phaseA_consume(self, l):
        fw = self.fw
        V, A, G = self.V, self.A, self.G
        fw.push()
        self.load_rwkv_w(l, True)
        spo = fw.sb([128, 12], F32, "spo")
        self.load(spo, spo[:, :], self.d["spo"], self.d["spo"][l, :, :])
        spoh = fw.sb([128, 4], F32, "spoh")
        self.dve(lambda: V.tensor_scalar(spoh[:, :], spo[:, 0:4], 0.5, None, ALU.mult), [spo], [spoh])
        mu1o = fw.sb([128, 3], F32, "mu1o")
        muho = fw.sb([128, 3], F32, "muho")
        self.dve(lambda: V.tensor_scalar(mu1o[:, :], spo[:, 9:12], -1.0, 1.0, ALU.mult, ALU.add), [spo], [mu1o])
        self.dve(lambda: V.tensor_scalar(muho[:, :], spo[:, 9:12], 0.5, None, ALU.mult), [spo], [muho])
        T = DSEQ
        zz = [fw.sb([128, T], BF16, nm) for nm in ("sr", "sk", "sv")]
        lo = [fw.sb([64, T], BF16, nm) for nm in ("stwf", "stwb", "sad")]
        fw.push()
        raw = fw.sb([128, T], BF16, "sraw")
        o32 = fw.sb([128, T], F32, "so32")

        def shift_full(rows, src, m1, mh, dst, tanh=False):
            self.dve(lambda: V.tensor_scalar(o32[:rows, :], src[:rows, :], m1, None, ALU.mult), [src, mu1o, self.mu1], [o32])
            self.dve(lambda: V.scalar_tensor_tensor(o32[:rows, 1:T], src[:rows, 0:T - 1], mh, o32[:rows, 1:T], ALU.mult, ALU.add),
                     [src, muho, self.muh, o32], [o32])
            self.dve(lambda: V.scalar_tensor_tensor(o32[:rows, 0:T - 1], src[:rows, 1:T], mh, o32[:rows, 0:T - 1], ALU.mult, ALU.add),
                     [src, muho, self.muh, o32], [o32])
            if tanh:
                self.act(lambda: A.activation(dst[:rows, :], o32[:rows, :], AF.Tanh), [o32], [dst])
            else:
                self.act(lambda: A.copy(dst[:rows, :], o32[:rows, :]), [o32], [dst])

        for which in range(3):
            src = self.ag1_out[l]["rk" if which < 2 else "vx"]
            for q in range(4):
                self.gather_rows(raw, raw[:, q * 512:(q + 1) * 512], src, which * 4 + q)
            shift_full(128, raw, mu1o[:, which:which + 1], muho[:, which:which + 1], zz[which])
        agv = self.ag1_out[l]["vx"]
        for i in range(3):
            for q in range(4):
                r0 = q * 864 + VX_OFF["lora"] + i * 64
                fw.dma("sp", raw, raw[0:64, q * 512:(q + 1) * 512], agv, agv[r0:r0 + 64, :])
            shift_full(64, raw, self.mu1[0:64, 12 + i:13 + i], self.muh[0:64, 12 + i:13 + i], lo[i], tanh=(i < 2))
        fw.pop()
        stg = [None]

        def yout(c, yfin):
            q, cc = c // 4, c % 4
            if cc == 0:
                stg[0] = fw.rot([128, 512], BF16, "ystg", n=2)
            st = stg[0]
            self.act(lambda: A.copy(st[:, cc * 128:(cc + 1) * 128], yfin[:, :]), [yfin], [st])
            if cc == 3:
                ag = self.ag2_in[l]
                fw.dma("sp", ag, ag[q * 128:(q + 1) * 128, :], st, st[:, :])

        pidx = {nm: i for i, nm in enumerate(RW_NAMES)}
        J = dict(T=T, r=(zz[0], lambda a, b: zz[0][:, a:b]), k=(zz[1], lambda a, b: zz[1][:, a:b]), v=(zz[2], lambda a, b: zz[2][:, a:b]),
                 twd=[(lo[0], lambda a, b: lo[0][:, a:b]), (lo[1], lambda a, b: lo[1][:, a:b])],
                 ad=(lo[2], lambda a, b: lo[2][:, a:b]),
                 par=lambda nm: spo[:, pidx[nm]:pidx[nm] + 1],
                 parh=lambda nm: spoh[:, pidx[nm]:pidx[nm] + 1],
                 parbufs=[spo, spoh],
                 wup=lambda dd: self.wup[:, dd * 128:(dd + 1) * 128],
                 aup=lambda dd: self.aup[:, dd * 128:(dd + 1) * 128],
                 st0=lambda dd: (self.d["st0"], self.d["st0"][l, dd, :, :]), yout=yout, stout=None, seg=256, segpar=False)
        self.rwkv_job(J)
        self.allgather(self.ag2_out[l], self.ag2_in[l])
        fw.pop()

    def phaseA_final(self, l):
        fw = self.fw
        fw.push()
        for r in range(4):
            ya = fw.rot([128, 512], BF16, "ya", n=2)
            self.gather_rows(ya, ya[:, :], self.ag2_out[l], 12 + r)
            self.dve(lambda r=r, ya=ya: self.V.scalar_tensor_tensor(self.oTs[0][0][:, r, 0:512], ya[:, :], 0.5, self.GA2[:, r, :], ALU.mult, ALU.mult),
                     [ya, self.GA2], [self.oTs[0][0]])
        fw.pop()

    def rwkv_job(self, J):
        for _ in self.rwkv_job_gen(J):
            pass

    def rwkv_job_gen(self, J):
        fw = self.fw
        V, A, G, T_ = self.V, self.A, self.G, self.T
        T = J["T"]
        nch = T // 128
        SEG = J.get("seg", 256)
        nseg = T // SEG
        ncs = SEG // 128
        rB, rf = J["r"]
        kB, kf = J["k"]
        vB, vf = J["v"]
        adB, adf = J["ad"]
        par, parh, pbufs = J["par"], J["parh"], J["parbufs"]
        self.ps_range = (0, 6)
        scoped = J.get("scoped", True)
        tag = str(J.get("tag", ""))
        jpush = (lambda: fw.push()) if scoped else (lambda: None)
        jpop = (lambda: fw.pop()) if scoped else (lambda: None)
        jt = (lambda shp, dt, nm: fw.sb(shp, dt, nm)) if scoped else (lambda shp, dt, nm: fw.rot(shp, dt, "J" + nm + (tag if nm[:2] in ("ka", "Vt", "Ya", "Ba", "ST") else ""), n=1))
        jpush()
        kap = jt([128, T], BF16, "kap")
        Vtm = jt([128, nch, 128], BF16, "Vtm")
        Yacc = jt([128, nch, 128], F32, "Yacc")
        Bacc = jt([128, T], F32, "Bacc")
        ST = [jt([128, 64], F32, f"ST{dd}") for dd in range(2)]
        STb = [jt([128, 64], BF16, f"STb{dd}") for dd in range(2)]
        KW = min(T, 512)
        self.pool(lambda: G.memset(Yacc[:, :, :], 0.0), [], [Yacc])
        self.pool(lambda: G.memset(Bacc[:, :], 0.0), [], [Bacc])
        jpush()
        for p0 in range(0, T, 512):
            N = min(512, T - p0)
            kk = fw.rot([128, KW], F32, "kk", n=(2 if scoped else 1))
            sq = fw.rot([128, KW], BF16, "kksq", n=(2 if scoped else 1))
            self.dve(lambda: V.tensor_scalar(kk[:, :N], kf(p0, p0 + N), par("k_k"), None, ALU.mult), [kB] + pbufs, [kk])
            self.act(lambda: A.activation(sq[:, :N], kk[:, :N], AF.Square), [kk], [sq])
            ps = self.nps()
            self.pe(lambda: T_.matmul(ps[:, :N], self.bones[:, :], sq[:, :N], start=True, stop=True), [self.bones, sq], [ps])
            t = fw.rot([128, KW], F32, "kkt", n=(2 if scoped else 1))
            self.dve(lambda: V.tensor_scalar(t[:, :N], ps[:, :N], 1e-24, None, ALU.max), [ps], [t])
            self.rsqrt(t, t[:, :N], t, t[:, :N])
            self.dve(lambda: V.tensor_tensor(kap[:, p0:p0 + N], kk[:, :N], t[:, :N], ALU.mult), [kk, t], [kap])
            yield
        import os
        STOP = int(os.environ.get("RWKV_STOP", "99"))
        if STOP <= 1:
            jpop(); jpop(); return
        for c in range(nch):
            self.transpose_to(vB, vf(c * 128, c * 128 + 128), Vtm, Vtm[:, c, :], eng=("act" if c % 2 else "dve"))
            yield
        jpop()
        if STOP <= 2:
            jpop(); return
        jpush()
        for dd in range(2):
            if J["st0"] is None:
                self.pool(lambda dd=dd: G.memset(ST[dd][:, :], 0.0), [], [ST[dd]])
            else:
                src, sap = J["st0"](dd)
                fw.dma("sp", ST[dd], ST[dd][:, :], src, sap)
            self.act(lambda dd=dd: A.copy(STb[dd][:, :], ST[dd][:, :]), [ST[dd]], [STb[dd]])
        MK = self.mask
        def rw_segment(dd, sg, res, segpar):
            sfx = "fb"[dd]
            twB, twf = J["twd"][dd]
            s0 = sg * SEG
            N = SEG
            f32t = lambda nm: fw.rot([128, SEG], F32, nm + (str(dd) if segpar else ""), n=1)
            a = f32t("ra")
            ps = self.nps()
            self.pe(lambda: T_.matmul(ps[:, :N], J["aup"](dd), adf(s0, s0 + N), start=True, stop=True), [self.aup, adB], [ps])
            self.act(lambda: A.activation(a[:, :], ps[:, :N], AF.Tanh, bias=parh("a0_" + sfx), scale=0.5), [ps] + pbufs, [a])
            yield
            self.act(lambda: A.activation(a[:, :], a[:, :], AF.Identity, bias=0.5, scale=0.5), [a], [a])
            kt = f32t("rkt")
            self.dve(lambda: V.tensor_scalar(kt[:, :], a[:, :], 1.0, par("k_a"), ALU.subtract, ALU.mult), [a] + pbufs, [kt])
            self.dve(lambda: V.scalar_tensor_tensor(kt[:, :], kt[:, :], 1.0, kf(s0, s0 + N), ALU.add, ALU.mult), [kt, kB], [kt])
            b = f32t("rb")
            self.dve(lambda: V.tensor_tensor(b[:, :], a[:, :], kap[:, s0:s0 + N], ALU.mult), [a, kap], [b])
            lw = f32t("rlw")
            ps = self.nps()
            self.pe(lambda: T_.matmul(ps[:, :N], J["wup"](dd), twf(s0, s0 + N), start=True, stop=True), [self.wup, twB], [ps])
            self.act(lambda: A.activation(lw[:, :], ps[:, :N], AF.Tanh, bias=parh("w0_" + sfx), scale=0.5), [ps] + pbufs, [lw])
            yield
            self.act(lambda: A.activation(lw[:, :], lw[:, :], AF.Identity, bias=-0.3032653298563167, scale=-0.3032653298563167), [lw], [lw])
            rkr = fw.rot([128, SEG], BF16, "rkr" + str(dd), n=1)
            self.dve(lambda: V.scalar_tensor_tensor(rkr[:, :], kt[:, :], par("r_k"), rf(s0, s0 + N), ALU.mult, ALU.mult),
                     [kt, rB] + pbufs, [rkr])
            ps = self.nps()
            self.pe(lambda: T_.matmul(ps[:, :N], self.bones[:, :], rkr[:, :], start=True, stop=True), [self.bones, rkr], [ps])
            self.dve(lambda: V.tensor_tensor(Bacc[:, s0:s0 + N], Bacc[:, s0:s0 + N], ps[:, :N], ALU.add), [Bacc, ps], [Bacc])
            P = f32t("rP")
            for c in range(ncs):
                self.dve(lambda c=c: V.tensor_tensor_scan(P[:, c * 128:(c + 1) * 128], self.onesf[:, :], lw[:, c * 128:(c + 1) * 128],
                                                          0.0, ALU.mult, ALU.add), [self.onesf, lw], [P])
            Q = f32t("rQ")
            R = f32t("rR")
            self.dve(lambda: V.tensor_tensor(Q[:, :], P[:, :], lw[:, :], ALU.subtract), [P, lw], [Q])
            for c in range(ncs):
                self.dve(lambda c=c: V.tensor_scalar(R[:, c * 128:(c + 1) * 128], P[:, c * 128:(c + 1) * 128], -1.0,
                                                     P[:, c * 128 + 127:c * 128 + 128], ALU.mult, ALU.add), [P], [R])
            gL = fw.rot([128, 2], F32, "gL" + str(dd) + tag, n=2)
            self.act(lambda: A.activation(gL[:, 0:ncs], P[:, 127:SEG:128], AF.Exp), [P], [gL])
            yield
            if dd == 0:
                srcs = [(P, -1.0, a), (Q, 1.0, Q), (P, 1.0, P), (R, 1.0, R)]
            else:
                RL = f32t("rRL")
                self.dve(lambda: V.tensor_tensor(RL[:, :], R[:, :], lw[:, :], ALU.add), [R, lw], [RL])
                srcs = [(RL, -1.0, a), (R, 1.0, R), (RL, 1.0, RL), (Q, 1.0, Q)]
            E = [None] * 4
            for i, (sb_, sc_, dst_) in enumerate(srcs):
                self.act(lambda i=i, sb_=sb_, sc_=sc_, dst_=dst_: A.activation(dst_[:, :], sb_[:, :], AF.Exp, scale=sc_), [sb_], [dst_])
                E[i] = dst_
            Kd2 = fw.rot([128, 2, SEG], BF16, "Kd2" + str(dd) + tag, n=1)
            Bd2 = fw.rot([128, 2, SEG], BF16, "Bd2" + str(dd) + tag, n=1)
            KL = fw.rot([128, SEG], BF16, "KL" + str(dd) + tag, n=1)
            BL = fw.rot([128, SEG], BF16, "BL" + str(dd) + tag, n=1)
            KqRq2 = fw.rot([128, 2, ncs, 2, 128], BF16, "KqRq2" + str(dd) + tag, n=1)
            hm = self.hm
            kap3 = kap[:, s0:s0 + N].rearrange("p (c t) -> p c t", c=ncs)
            r3 = rf(s0, s0 + N).rearrange("p (c t) -> p c t", c=ncs)
            for h in range(2):
                self.dve(lambda h=h: V.scalar_tensor_tensor(Kd2[:, h, :], kt[:, :], hm[:, h:h + 1], E[0][:, :], ALU.mult, ALU.mult), [kt, hm, E[0]], [Kd2])
                self.dve(lambda h=h: V.scalar_tensor_tensor(Bd2[:, h, :], b[:, :], hm[:, h:h + 1], E[0][:, :], ALU.mult, ALU.mult), [b, hm, E[0]], [Bd2])
                self.dve(lambda h=h: V.scalar_tensor_tensor(KqRq2[:, h, :, 0, :], kap3, hm[:, h:h + 1], E[1][:, :].rearrange("p (c t) -> p c t", c=ncs),
                                                            ALU.mult, ALU.mult), [kap, hm, E[1]], [KqRq2])
                self.dve(lambda h=h: V.scalar_tensor_tensor(KqRq2[:, h, :, 1, :], r3, hm[:, h:h + 1], E[2][:, :].rearrange("p (c t) -> p c t", c=ncs),
                                                            ALU.mult, ALU.mult), [rB, hm, E[2]], [KqRq2])
            self.dve(lambda: V.tensor_tensor(KL[:, :], kt[:, :], E[3][:, :], ALU.mult), [kt, E[3]], [KL])
            self.dve(lambda: V.tensor_tensor(BL[:, :], b[:, :], E[3][:, :], ALU.mult), [b, E[3]], [BL])

            res.update(dict(Kd2=Kd2, Bd2=Bd2, KL=KL, BL=BL, KqRq2=KqRq2, gL=gL, sg=sg))
            yield

        def rw_pre(dd, c, Pd, res):
            sfx2 = f"{dd}{c}"
            mk0 = dd * 1280
            Kd2, Bd2, KqRq2 = Pd["Kd2"], Pd["Bd2"], Pd["KqRq2"]
            cs = slice(c * 128, (c + 1) * 128)
            Am = [fw.rot([128, 512], BF16, f"Am{h}_{sfx2}", n=1) for h in range(2)]
            psB = self.nps()
            for h in range(2):
                psA = self.nps()
                rhsA = KqRq2[:, h, c, :, :].rearrange("p a t -> p (a t)")
                self.pe(lambda h=h, psA=psA, rhsA=rhsA: T_.matmul(psA[:, 0:256], Kd2[:, h, cs], rhsA, start=True, stop=True),
                        [Kd2, KqRq2], [psA], signal=False)
                self.pe(lambda h=h, psA=psA, rhsA=rhsA: T_.matmul(psA[:, 256:512], Bd2[:, h, cs], rhsA, start=True, stop=True),
                        [Bd2, KqRq2], [psA])
                Araw = fw.rot([128, 512], BF16, "Araw", n=2)
                self.act(lambda psA=psA, Araw=Araw: A.copy(Araw[:, :], psA[:, :]), [psA], [Araw])
                self.dve(lambda h=h, Araw=Araw: V.tensor_tensor(Am[h][:, :], Araw[:, :], MK[:, mk0:mk0 + 512], ALU.mult), [Araw, MK], [Am[h]])
                self.pe(lambda h=h: T_.matmul(psB[:, h * 128:(h + 1) * 128], KqRq2[:, h, c, 0, :], Bd2[:, h, cs], start=True, stop=True),
                        [KqRq2, Bd2], [psB], signal=(h == 1))
            PT = [fw.rot([128, 2, 128], BF16, f"PT{i}_{sfx2}", n=1) for i in range(2)]
            PX = [fw.rot([128, 2, 256], BF16, f"PX{i}_{sfx2}", n=1) for i in range(2)]
            C64T = fw.rot([128, 2, 128], BF16, "C64T" + sfx2, n=1)
            C128T = fw.rot([128, 2, 128], BF16, "C128T" + sfx2, n=1)
            f2 = lambda t: t[:, :, :].rearrange("p a t -> p (a t)")
            Braw = fw.rot([128, 256], BF16, "Braw", n=1)
            self.act(lambda: A.copy(Braw[:, :], psB[:, 0:256]), [psB], [Braw])
            self.dve(lambda: V.tensor_tensor(f2(PT[0]), Braw[:, :], MK[:, mk0 + 512:mk0 + 768], ALU.mult), [Braw, MK], [PT[0]])
            self.dve(lambda: V.tensor_tensor(f2(C64T), Braw[:, :], MK[:, mk0 + 768:mk0 + 1024], ALU.mult), [Braw, MK], [C64T])
            self.dve(lambda: V.tensor_tensor(f2(C128T), Braw[:, :], MK[:, mk0 + 1024:mk0 + 1280], ALU.mult), [Braw, MK], [C128T])
            for h in range(2):
                self.act(lambda h=h: A.copy(PX[0][:, h, 0:128], Am[h][:, 256:384]), [Am[h]], [PX[0]])
            yield
            cur = 0
            Xb = fw.rot([128, 2, 128], BF16, "Xb32" + sfx2, n=1)
            for j in range(1, 6):
                nxt = 1 - cur
                if j == 1:
                    ps = self.nps()
                    pst_ = self.nps()
                    for h in range(2):
                        self.pe(lambda h=h, ps=ps, cur=cur: T_.matmul(ps[:, h * 256:h * 256 + 128], PT[cur][:, h, :], PX[cur][:, h, 0:128],
                                                                      start=True, stop=True), [PT[cur], PX[cur]], [ps], signal=(h == 1))
                        self.pe(lambda h=h, pst_=pst_, cur=cur: T_.matmul(pst_[:, h * 128:(h + 1) * 128], PX[cur][:, h, 0:128], PT[cur][:, h, :],
                                                                          start=True, stop=True), [PT[cur], PX[cur]], [pst_], signal=(h == 1))
                    for h in range(2):
                        self.dve(lambda h=h, cur=cur, nxt=nxt: V.tensor_tensor(PX[nxt][:, h, 128:256], PX[cur][:, h, 0:128], self.ident[:, :], ALU.add),
                                 [PX[cur], self.ident], [PX[nxt]])
                    self.act(lambda ps=ps, nxt=nxt: A.copy(PX[nxt][:, :, 0:128], ps[:, :].rearrange("p (a t) -> p a t", a=2)[:, :, 0:128]),
                             [ps], [PX[nxt]])
                    self.act(lambda pst_=pst_, nxt=nxt: A.copy(f2(PT[nxt]), pst_[:, 0:256]), [pst_], [PT[nxt]])
                elif j < 5:
                    ps = self.nps()
                    pst_ = self.nps()
                    for h in range(2):
                        self.pe(lambda h=h, ps=ps, cur=cur: T_.matmul(ps[:, h * 256:(h + 1) * 256], PT[cur][:, h, :], PX[cur][:, h, :],
                                                                      start=True, stop=True), [PT[cur], PX[cur]], [ps], signal=(h == 1))
                        self.pe(lambda h=h, pst_=pst_, cur=cur: T_.matmul(pst_[:, h * 128:(h + 1) * 128], PX[cur][:, h, 0:128], PT[cur][:, h, :],
                                                                          start=True, stop=True), [PT[cur], PX[cur]], [pst_], signal=(h == 1))
                    ps3 = ps[:, :].rearrange("p (a t) -> p a t", a=2)
                    self.act(lambda ps3=ps3, ps=ps, nxt=nxt: A.copy(PX[nxt][:, :, 0:128], ps3[:, :, 0:128]), [ps], [PX[nxt]])
                    self.dve(lambda ps3=ps3, ps=ps, cur=cur, nxt=nxt: V.tensor_tensor(PX[nxt][:, :, 128:256], ps3[:, :, 128:256], PX[cur][:, :, 128:256], ALU.add),
                             [ps, PX[cur]], [PX[nxt]])
                    self.act(lambda pst_=pst_, nxt=nxt: A.copy(f2(PT[nxt]), pst_[:, 0:256]), [pst_], [PT[nxt]])
                else:
                    ps = self.nps()
                    for h in range(2):
                        self.pe(lambda h=h, ps=ps, cur=cur: T_.matmul(ps[:, h * 128:(h + 1) * 128], PT[cur][:, h, :], PX[cur][:, h, 128:256],
                                                                      start=True, stop=True), [PT[cur], PX[cur]], [ps], signal=(h == 1))
                    self.dve(lambda ps=ps, cur=cur: V.tensor_tensor(Xb[:, :, :], ps[:, 0:256].rearrange("p (a t) -> p a t", a=2), PX[cur][:, :, 128:256], ALU.add),
                             [ps, PX[cur]], [Xb])
                cur = nxt
                yield
            TT = None
            for lvl, CT in enumerate((C64T, C128T)):
                XT = fw.rot([128, 2, 128], BF16, "XTm" + sfx2, n=1)
                Zt = fw.rot([128, 2, 128], BF16, "Ztm" + sfx2, n=1)
                ptt = self.pst[self.pst_rr % 2]
                self.pst_rr += 1
                for h in range(2):
                    self.pe(lambda h=h, ptt=ptt, Xb=Xb: T_.transpose(ptt[:, h * 128:(h + 1) * 128], Xb[:, h, :], self.ident[:, :]),
                            [Xb, self.ident], [ptt], signal=(h == 1))
                self.act(lambda ptt=ptt, XT=XT: A.copy(f2(XT), ptt[:, 0:256]), [ptt], [XT])
                psz = self.nps()
                for h in range(2):
                    self.pe(lambda h=h, psz=psz, CT=CT, Xb=Xb: T_.matmul(psz[:, h * 128:(h + 1) * 128], CT[:, h, :], Xb[:, h, :], start=True, stop=True),
                            [CT, Xb], [psz], signal=(h == 1))
                self.act(lambda psz=psz, Zt=Zt: A.copy(f2(Zt), psz[:, 0:256]), [psz], [Zt])
                psw = self.nps()
                for h in range(2):
                    self.pe(lambda h=h, psw=psw, XT=XT, Zt=Zt: T_.matmul(psw[:, h * 128:(h + 1) * 128], XT[:, h, :], Zt[:, h, :], start=True, stop=True),
                            [XT, Zt], [psw], signal=(h == 1))
                Xn = fw.rot([128, 2, 128], BF16, ("Xb64" if lvl == 0 else "TT") + sfx2, n=1)
                self.dve(lambda psw=psw, Xn=Xn, Xb=Xb: V.tensor_tensor(f2(Xn), psw[:, 0:256], f2(Xb), ALU.add), [psw, Xb], [Xn])
                Xb = Xn
                yield
            TT = Xb

            res["Am"] = Am
            res["TT"] = TT
            yield

        def rw_seq(dd, Pd, pres):
            KL, BL, KqRq2, gL, sg = Pd["KL"], Pd["BL"], Pd["KqRq2"], Pd["gL"], Pd["sg"]
            chunks = list(range(ncs)) if dd == 0 else list(reversed(range(ncs)))
            for c in chunks:
                cg = sg * ncs + c
                cs = slice(c * 128, (c + 1) * 128)
                Am, TT = pres[(dd, c)]["Am"], pres[(dd, c)]["TT"]
                Sb = STb[dd]
                psG = self.nps()
                for h in range(2):
                    hs = slice(64 * h, 64 * h + 64)
                    vs = slice(64 * h, 64 * h + 64)
                    self.pe(lambda h=h, vs=vs: T_.matmul(psG[:, vs], KqRq2[:, h, c, 0, :], Sb[:, :], start=(h == 0), stop=False, skip_group_check=True),
                            [KqRq2, Sb], [psG], signal=False)
                    self.pe(lambda h=h, vs=vs: T_.matmul(psG[:, vs], Am[h][:, 0:128], Vtm[:, cg, vs], start=False, stop=(h == 1), skip_group_check=True),
                            [Am[h], Vtm], [psG], signal=(h == 1))
                Gn = fw.rot([128, 128], BF16, "Gn" + str(dd), n=1)
                self.act(lambda: A.mul(Gn[:, :], psG[:, 0:128], -1.0), [psG], [Gn])
                yield
                psU = self.nps()
                for h in range(2):
                    vs = slice(64 * h, 64 * h + 64)
                    self.pe(lambda h=h, vs=vs: T_.matmul(psU[:, vs], TT[:, h, :], Gn[:, vs], start=(h == 0), stop=(h == 1), skip_group_check=True),
                            [TT, Gn], [psU], signal=(h == 1))
                U = fw.rot([128, 128], BF16, "U" + str(dd), n=1)
                self.act(lambda: A.copy(U[:, :], psU[:, 0:128]), [psU], [U])
                yield
                yield
                psY = self.nps()
                for h in range(2):
                    hs = slice(64 * h, 64 * h + 64)
                    vs = slice(64 * h, 64 * h + 64)
                    self.pe(lambda h=h, vs=vs: T_.matmul(psY[:, vs], KqRq2[:, h, c, 1, :], Sb[:, :], start=(h == 0), stop=False, skip_group_check=True),
                            [KqRq2, Sb], [psY], signal=False)
                    self.pe(lambda h=h, vs=vs: T_.matmul(psY[:, vs], Am[h][:, 128:256], Vtm[:, cg, vs], start=False, stop=False, skip_group_check=True),
                            [Am[h], Vtm], [psY], signal=False)
                    self.pe(lambda h=h, vs=vs: T_.matmul(psY[:, vs], Am[h][:, 384:512], U[:, vs], start=False, stop=(h == 1), skip_group_check=True),
                            [Am[h], U], [psY], signal=(h == 1))
                self.dve(lambda: V.tensor_tensor(Yacc[:, cg, :], Yacc[:, cg, :], psY[:, 0:128], ALU.add), [Yacc, psY], [Yacc])
                yield
                yield
                KLt = fw.rot([128, 128], BF16, "KLt" + str(dd), n=1)
                BLt = fw.rot([128, 128], BF16, "BLt" + str(dd), n=1)
                self.transpose_to(KL, KL[:, cs], KLt, KLt[:, :], eng="act")
                self.transpose_to(BL, BL[:, cs], BLt, BLt[:, :], eng="dve")
                psS = self.nps()
                for h in range(2):
                    hs = slice(64 * h, 64 * h + 64)
                    vs = slice(64 * h, 64 * h + 64)
                    self.pe(lambda h=h, hs=hs, vs=vs: T_.matmul(psS[hs, 0:64], KLt[:, hs], Vtm[:, cg, vs], start=True, stop=False),
                            [KLt, Vtm], [psS], signal=False)
                    self.pe(lambda h=h, hs=hs, vs=vs: T_.matmul(psS[hs, 0:64], BLt[:, hs], U[:, vs], start=False, stop=True),
                            [BLt, U], [psS], signal=(h == 1))
                self.dve(lambda: V.scalar_tensor_tensor(ST[dd][:, :], ST[dd][:, :], gL[:, c:c + 1], psS[:, 0:64], ALU.mult, ALU.add),
                         [ST[dd], gL, psS], [ST[dd]])
                self.act(lambda: A.copy(STb[dd][:, :], ST[dd][:, :]), [ST[dd]], [STb[dd]])

        def run_rr(gens):
            gens = list(gens)
            while gens:
                for g in list(gens):
                    try:
                        next(g)
                    except StopIteration:
                        gens.remove(g)
                yield

        for step in range(nseg):
            sgs = (step, nseg - 1 - step)
            Pd = [{}, {}]
            segpar = J.get("segpar", False)
            if segpar:
                yield from run_rr([rw_segment(dd, sgs[dd], Pd[dd], True) for dd in range(2)])
            else:
                for dd in range(2):
                    for _ in rw_segment(dd, sgs[dd], Pd[dd], False):
                        yield
            if step == 0:
                yield "early_done"
            if STOP <= 3:
                continue
            pres = {(dd, c): {} for dd in range(2) for c in range(ncs)}
            yield from run_rr([rw_pre(dd, c, Pd[dd], pres[(dd, c)]) for dd in range(2) for c in range(ncs)])
            if STOP <= 5:
                continue
            yield from run_rr([rw_seq(dd, Pd[dd], pres) for dd in range(2)])
        if J["stout"] is not None:
            for dd in range(2):
                J["stout"](dd, ST[dd])
        jpop()
        if STOP <= 6:
            jpop(); return
        n2 = nch * 2
        sums = jt([128, n2], F32, "gsum")
        ssq = jt([128, n2], F32, "gssq")
        Ysq = jt([128, nch * 128], F32, "Ysq") if scoped else fw.rot([128, nch * 128], F32, "JYsq", n=1)
        Yf = Yacc[:, :, :].rearrange("p c x -> p (c x)")
        self.dve(lambda: V.tensor_reduce(sums[:, :], Yf.rearrange("p (g x) -> p g x", x=64), AX.X, ALU.add), [Yacc], [sums])
        self.act(lambda: A.activation(Ysq[:, :], Yf, AF.Square), [Yacc], [Ysq])
        self.dve(lambda: V.tensor_reduce(ssq[:, :], Ysq[:, :].rearrange("p (g x) -> p g x", x=64), AX.X, ALU.add), [Ysq], [ssq])
        mean = jt([128, n2], F32, "gmean")
        var = jt([128, n2], F32, "gvar2")
        self.dve(lambda: V.tensor_scalar(mean[:, :], sums[:, :], 1.0 / 64, None, ALU.mult), [sums], [mean])
        self.dve(lambda: V.tensor_tensor(var[:, :], mean[:, :], mean[:, :], ALU.mult), [mean], [var])
        self.dve(lambda: V.scalar_tensor_tensor(var[:, :], ssq[:, :], 1.0 / 64, var[:, :], ALU.mult, ALU.subtract), [ssq, var], [var])
        self.dve(lambda: V.tensor_scalar(var[:, :], var[:, :], GN_EPS, None, ALU.add), [var], [var])
        self.rsqrt(var, var[:, :], var, var[:, :])
        yn = jt([128, nch, 128], BF16, "yn")
        for c in range(nch):
            for h in range(2):
                g = c * 2 + h
                self.dve(lambda c=c, h=h, g=g: V.tensor_scalar(yn[:, c, h * 64:(h + 1) * 64], Yacc[:, c, h * 64:(h + 1) * 64], mean[:, g:g + 1], var[:, g:g + 1],
                                                               ALU.subtract, ALU.mult), [Yacc, mean, var], [yn])
        for c in range(nch):
            pt = self.pst[self.pst_rr % 2]
            self.pst_rr += 1
            self.pe(lambda c=c, pt=pt: T_.transpose(pt[:, :128], yn[:, c, :], self.ident[:, :]), [yn, self.ident], [pt])
            yT = fw.rot([128, 128], F32, "yT", n=(2 if scoped else 1))
            self.act(lambda pt=pt, yT=yT: A.activation(yT[:, :], pt[:, :128], AF.Identity, bias=par("ln_b"), scale=par("ln_g")), [pt] + pbufs, [yT])
            bo = fw.rot([128, 128], F32, "bo", n=(2 if scoped else 1))
            self.dve(lambda c=c, bo=bo: V.tensor_tensor(bo[:, :], Bacc[:, c * 128:(c + 1) * 128], vf(c * 128, (c + 1) * 128), ALU.mult), [Bacc, vB], [bo])
            yfin = fw.rot([128, 128], F32, "yfin", n=(2 if scoped else 1))
            self.dve(lambda yT=yT, bo=bo, yfin=yfin: V.tensor_tensor(yfin[:, :], yT[:, :], bo[:, :], ALU.add), [yT, bo], [yfin])
            J["yout"](c, yfin)
            yield
        jpop()
        if scoped:
            self.ps_range = (0, 4)

    def load_mla_w(self, l):
        fw, d = self.fw, self.d
        self.wq = fw.sb([128, 2 * 8 * 96], BF16, "wq")
        self.wqs = fw.sb([128, 2 * 8 * 32], BF16, "wqs")
        self.wkk = fw.sb([128, 512], BF16, "wkk")
        self.wkv = fw.sb([128, 512], BF16, "wkv")
        for nm in ("wq", "wqs", "wkk", "wkv"):
            t = getattr(self, nm)
            self.load(t, t[:], d[nm], d[nm][l, :, :])

    def mla_front(self, l, h0, rope, GB2, Qh, ckv_f, ckv_b, kr_f, kr_b):
        fw = self.fw
        base = l * BLK_PER_LAYER
        W = WIN_W["B0"]
        slot = self.next_block(base + WIN_IDX["B0"])
        qd = fw.sb([128, 2, 512], F32, "qd")
        kvd = fw.sb([128, 512], F32, "kvd")
        for c in range(2):
            ps = self.zmm(slot, W, c * 128, 128, h0, 512)
            self.act(lambda c=c, ps=ps: self.A.copy(qd[:, c, :], ps[:, :]), [ps], [qd])
        ps = self.zmm(slot, W, 256, 128, h0, 512)
        self.dve(lambda ps=ps: self.V.tensor_copy(kvd[:, :], ps[:, :]), [ps], [kvd])
        pk = self.zmm(slot, W, 384, 32, h0, 512, prow=64)
        R = self.rope
        if rope:
            pks = self.zmm(slot, W, 416, 32, h0, 512, prow=64)
            t1 = fw.sb([96, 512], F32, "krt1")
            self.dve(lambda: self.V.tensor_tensor(t1[64:96, :], pk[64:96, :], R[64:96, 0:512], ALU.mult), [pk, R], [t1])
            self.dve(lambda: self.V.tensor_tensor(kr_f[64:96, :], pks[64:96, :], R[64:96, 512:1024], ALU.mult), [pks, R], [kr_f])
            self.dve(lambda: self.V.tensor_tensor(kr_f[64:96, :], kr_f[64:96, :], t1[64:96, :], ALU.add), [kr_f, t1], [kr_f])
        else:
            self.act(lambda: self.A.copy(kr_f[64:96, :], pk[64:96, :]), [pk], [kr_f])
        self.act(lambda: self.A.copy(kr_b[64:96, :], kr_f[64:96, :]), [kr_f], [kr_b])
        self.prefetch_next()
        slot = self.next_block(base + WIN_IDX["B1"])
        for c in range(4):
            ps = self.zmm(slot, 512, c * 128, 128, h0, 512)
            self.silu2(ps, 128, 512, GB2[:, c, :], GB2)
        self.prefetch_next()
        rq = fw.sb([128, 512], F32, "rq")
        self.rstd_tile(qd, [qd[:, c, :] for c in range(2)], 256.0, rq, 512)
        qn = fw.sb([128, 2, 512], BF16, "qn")
        for c in range(2):
            self.dve(lambda c=c: self.V.scalar_tensor_tensor(qn[:, c, :], qd[:, c, :], self.spc("qn", c), rq[:, :], ALU.mult, ALU.mult),
                     [qd, self.sp, rq], [qn])
        rk = fw.sb([128, 512], F32, "rkv")
        self.rstd_tile(kvd, [kvd[:, :]], 128.0, rk, 512)
        self.dve(lambda: self.V.scalar_tensor_tensor(ckv_f[:, :], kvd[:, :], self.spc("kvn", 0), rk[:, :], ALU.mult, ALU.mult),
                 [kvd, self.sp, rk], [ckv_f])
        self.act(lambda: self.A.copy(ckv_b[:, :], ckv_f[:, :]), [ckv_f], [ckv_b])
        for h in range(8):
            ps = self.nps()
            for c in range(2):
                self.pe(lambda c=c, h=h, ps=ps: self.T.matmul(ps[:96, :], self.wq[:, (c * 8 + h) * 96:(c * 8 + h) * 96 + 96], qn[:, c, :],
                                                             start=(c == 0), stop=(c == 1)), [self.wq, qn], [ps], signal=(c == 1))
            if rope:
                ps2 = self.nps()
                for c in range(2):
                    self.pe(lambda c=c, h=h, ps2=ps2: self.T.matmul(ps2[64:96, :], self.wqs[:, (c * 8 + h) * 32:(c * 8 + h) * 32 + 32], qn[:, c, :],
                                                                   start=(c == 0), stop=(c == 1)), [self.wqs, qn], [ps2], signal=(c == 1))
                t1 = fw.rot([96, 512], F32, "qrt1")
                t2 = fw.rot([96, 512], F32, "qrt2")
                self.dve(lambda ps=ps, t1=t1: self.V.tensor_tensor(t1[64:96, :], ps[64:96, :], R[64:96, 0:512], ALU.mult), [ps, R], [t1])
                self.dve(lambda ps2=ps2, t2=t2: self.V.tensor_tensor(t2[64:96, :], ps2[64:96, :], R[64:96, 512:1024], ALU.mult), [ps2, R], [t2])
                self.dve(lambda h=h, t1=t1, t2=t2: self.V.tensor_tensor(Qh[h][64:96, :], t1[64:96, :], t2[64:96, :], ALU.add), [t1, t2], [Qh[h]])
                self.act(lambda h=h, ps=ps: self.A.copy(Qh[h][0:64, :], ps[0:64, :]), [ps], [Qh[h]])
            else:
                self.act(lambda h=h, ps=ps: self.A.copy(Qh[h][:, :], ps[:96, :]), [ps], [Qh[h]])

    def mla_kv_chunk(self, ckv_b, kr_b, heads, Kh, Vaug, nk):
        for i, h in enumerate(heads):
            ps = self.nps()
            self.pe(lambda h=h, ps=ps: self.T.matmul(ps[:64, :nk], self.wkk[:, h * 64:(h + 1) * 64], ckv_b[:, :nk], start=True, stop=True),
                    [self.wkk, ckv_b], [ps])
            self.act(lambda i=i, ps=ps: self.A.copy(Kh[i][0:64, :nk], ps[0:64, :nk]), [ps], [Kh[i]])
            self.dve(lambda i=i: self.V.tensor_copy(Kh[i][64:96, :nk], kr_b[64:96, :nk]), [kr_b], [Kh[i]])
        for kb in range(nk // 128):
            ps = self.nps()
            self.pe(lambda kb=kb, ps=ps: self.T.matmul(ps[:, :], ckv_b[:, kb * 128:(kb + 1) * 128], self.wkv[:, :], start=True, stop=True),
                    [ckv_b, self.wkv], [ps])
            for h8 in range(8):
                pass
            self.dve(lambda kb=kb, ps=ps: self.V.tensor_copy(
                Vaug[:, kb * 520:(kb + 1) * 520].rearrange("p (h e) -> p h e", e=65)[:, :, 0:64],
                ps[:, :].rearrange("p (h e) -> p h e", e=64)), [ps], [Vaug])

    def attn_accum(self, Qh4, q0, nq, Kh4, Vaug, heads, k0, nkb, first, last, Oacc):
        fw = self.fw
        nqs = nq // 128
        its = [(kb, i, h) for kb in range(nkb) for i, h in enumerate(heads)]

        def score(kb, i, h):
            pss = self.psb[4 + (self.sc_rr % 2)]
            self.sc_rr += 1
            self.pe(lambda: self.T.matmul(pss[:, :nq], Kh4[i][:, k0 + kb * 128:k0 + kb * 128 + 128],
                                          Qh4[i][:, q0:q0 + nq], start=True, stop=True), [Kh4[i], Qh4[i]], [pss])
            PT = fw.rot([128, 512], BF16, "PT", n=3)
            self.act(lambda: self.A.activation(PT[:, :nq], pss[:, :nq], AF.Exp, scale=96.0 ** -0.5), [pss], [PT])
            return PT

        def pv(kb, i, h, PT):
            for qs in range(nqs):
                self.pe(lambda qs=qs: self.T.matmul(
                    Oacc[qs][:, i * 65:(i + 1) * 65], PT[:, qs * 128:(qs + 1) * 128],
                    Vaug[:, (k0 // 128 + kb) * 520 + h * 65:(k0 // 128 + kb) * 520 + h * 65 + 65],
                    start=(first and kb == 0 and i == 0), stop=(last and kb == nkb - 1 and i == 3), skip_group_check=True),
                    [PT, Vaug], [Oacc[qs]], signal=(qs == nqs - 1))

        pend = score(*its[0])
        for n in range(len(its)):
            nxt = score(*its[n + 1]) if n + 1 < len(its) else None
            pv(*its[n], pend)
            pend = nxt

    def attn_finish(self, Oacc, nqs, ob, hh):
        fw = self.fw
        for qs in range(nqs):
            rec = fw.rot([128, 4], F32, "rec", n=4)
            self.dve(lambda qs=qs, rec=rec: self.V.reciprocal(rec[:, :], Oacc[qs][:, 64:260:65]), [Oacc[qs]], [rec])
            for i in range(4):
                self.dve(lambda qs=qs, i=i, rec=rec: self.V.tensor_scalar(ob[qs][:, hh * 256 + i * 64:hh * 256 + i * 64 + 64],
                                                                          Oacc[qs][:, i * 65:i * 65 + 64], rec[:, i:i + 1], None, ALU.mult),
                         [Oacc[qs], rec], [ob[qs]])

    def attn_out(self, ob, nqs, GB2, g0, o0):
        for qs in range(nqs):
            for c in range(4):
                pt = self.pst[self.pst_rr % 2]
                self.pst_rr += 1
                self.pe(lambda qs=qs, c=c, pt=pt: self.T.transpose(pt[:, :128], ob[qs][:, c * 128:(c + 1) * 128], self.ident[:, :]),
                        [ob[qs], self.ident], [pt])
                self.dve(lambda qs=qs, c=c, pt=pt: self.V.scalar_tensor_tensor(
                    self.oTs[o0 // 512][1][:, c, o0 % 512 + qs * 128:o0 % 512 + qs * 128 + 128], pt[:, :128], 0.5, GB2[:, c, g0 + qs * 128:g0 + qs * 128 + 128],
                    ALU.mult, ALU.mult), [pt, GB2], [self.oTs[o0 // 512][1]])

    def phaseB_prompt(self, l, half):
        fw = self.fw
        h0 = half * 512
        fw.push()
        self.load_mla_w(l)
        GB2 = fw.sb([128, 4, 512], BF16, "GB2")
        Qh = [fw.sb([96, 512], BF16, f"Qh{h}") for h in range(8)]
        ckv_f = fw.sb([128, 512], F32, "ckvf")
        ckv_b = fw.sb([128, 512], BF16, "ckvb")
        kr_f = fw.sb([96, 512], F32, "krf")
        kr_b = fw.sb([96, 512], BF16, "krb")
        fw.push()
        self.stream_begin(2, depth=2)
        self.mla_front(l, h0, False, GB2, Qh, ckv_f, ckv_b, kr_f, kr_b)
        fw.pop()
        fw.dma("sp", self.d["ckvout"], self.d["ckvout"][l, :, h0:h0 + 512], ckv_f, ckv_f[:, :], sem_owner=self.outsem)
        fw.dma("sp", self.d["krout"], self.d["krout"][l, :, h0:h0 + 512], kr_f, kr_f[64:96, :], sem_owner=self.outsem)
        Kh = [fw.sb([96, 512], BF16, f"Kh{h}") for h in range(8)]
        Vaug = fw.sb([128, 4 * 520], BF16, "Vaug")
        self.pool(lambda: self.G.memset(Vaug[:, :], 1.0), [], [Vaug])
        self.mla_kv_chunk(ckv_b, kr_b, list(range(8)), Kh, Vaug, 512)
        self.sc_rr = 0
        import os
        if "dumpB" in os.environ.get("KDBG", "") and l == 0 and half == 0:
            so = self.d["stout"]
            fw.dma("pool", so, so[0:768, :].rearrange("(p a) b -> p (a b)", p=96), Qh[0], Qh[0][:, :])
            fw.dma("pool", so, so[768:1536, :].rearrange("(p a) b -> p (a b)", p=96), Kh[0], Kh[0][:, :])
            fw.dma("pool", so, so[1536:5632, :].rearrange("(p a) b -> p (a b)", p=128), Vaug, Vaug[:, 0:2048])
        for sq in range(2):
            ob = [fw.rot([128, 512], BF16, "ob", n=4) for _ in range(2)]
            for hh in range(2):
                heads = list(range(hh * 4, hh * 4 + 4))
                Oacc = [self.psb[0 + 2 * (hh % 2)], self.psb[1 + 2 * (hh % 2)]]
                self.attn_accum([Qh[h] for h in heads], sq * 256, 256, [Kh[h] for h in heads], Vaug, heads, sq * 256, 2, True, True, Oacc)
                self.attn_finish(Oacc, 2, ob, hh)
            if "dumpB" in os.environ.get("KDBG", "") and l == 0 and half == 0 and sq == 0:
                so = self.d["stout"]
                fw.dma("pool", so, so[5696:6720, :].rearrange("(p a) b -> p (a b)", p=128), ob[0], ob[0][:, :])
            self.attn_out(ob, 2, GB2, sq * 256, h0 + sq * 256)
        fw.pop()

    def phaseB_contrib(self, l):
        fw = self.fw
        self.load_mla_w(l)
        self.GB2 = fw.sb([128, 4, 512], BF16, "GB2s")
        self.Qh = [fw.sb([96, 512], BF16, f"Qhs{h}") for h in range(8)]
        fw.push()
        self.rope = fw.sb([96, 1024], F32, "rope")
        self.load(self.rope, self.rope[64:96, :], self.d["rope"], self.d["rope"].ap())
        self.stream_begin(2, depth=2)
        ckv_f = fw.sb([128, 512], F32, "ckvf")
        ckv_b = fw.sb([128, 512], BF16, "ckvb")
        kr_f = fw.sb([96, 512], F32, "krf")
        kr_b = fw.sb([96, 512], BF16, "krb")
        self.mla_front(l, 0, True, self.GB2, self.Qh, ckv_f, ckv_b, kr_f, kr_b)
        ag = self.ag1_in[l]["vx"]
        fw.dma("sp", ag, ag[VX_OFF["ckv"]:VX_OFF["ckv"] + 128, :], ckv_b, ckv_b[:, :])
        fw.dma("sp", ag, ag[VX_OFF["kr"]:VX_OFF["kr"] + 32, :], kr_b, kr_b[64:96, :])
        fw.pop()

    def phaseB_consume(self, l):
        fw = self.fw
        ago = self.ag1_out[l]["vx"]
        fw.push()
        self.sc_rr = 0
        self.ps_range = (4, 6)
        ob = [fw.sb([128, 512], BF16, f"obs{i}") for i in range(4)]
        for hh in range(2):
            heads = list(range(hh * 4, hh * 4 + 4))
            Oacc = self.psb[0:4]
            for ch in range(5):
                ckv_b = fw.rot([128, 512], BF16, "ckvg", n=2)
                kr_b = fw.rot([96, 512], BF16, "krg", n=2)
                if ch < 4:
                    fw.dma("sp", ckv_b, ckv_b[:, :], ago, ago[ch * 864 + VX_OFF["ckv"]:ch * 864 + VX_OFF["ckv"] + 128, :])
                    fw.dma("sp", kr_b, kr_b[64:96, :], ago, ago[ch * 864 + VX_OFF["kr"]:ch * 864 + VX_OFF["kr"] + 32, :])
                else:
                    c32 = fw.rot([128, 512], F32, "cc32", n=1)
                    k32 = fw.rot([96, 512], F32, "ck32", n=1)
                    fw.dma("sp", c32, c32[:, :], self.d["cckv"], self.d["cckv"][l, :, :])
                    fw.dma("sp", k32, k32[64:96, :], self.d["ckr"], self.d["ckr"][l, :, :])
                    self.act(lambda c32=c32, ckv_b=ckv_b: self.A.copy(ckv_b[:, :], c32[:, :]), [c32], [ckv_b])
                    self.dve(lambda k32=k32, kr_b=kr_b: self.V.tensor_copy(kr_b[64:96, :], k32[64:96, :]), [k32], [kr_b])
                Kh = [fw.rot([96, 512], BF16, f"Khs{i}", n=2) for i in range(4)]
                Vaug = fw.rot([128, 4 * 520], BF16, "Vaugs", n=2)
                self.pool(lambda Vaug=Vaug: self.G.memset(Vaug[:, :], 1.0), [], [Vaug])
                self.mla_kv_chunk(ckv_b, kr_b, heads, Kh, Vaug, 512)
                self.attn_accum([self.Qh[h] for h in heads], 0, 512, Kh, Vaug, heads, 0, 4, ch == 0, ch == 4, Oacc)
            self.attn_finish(Oacc, 4, ob, hh)
        self.ps_range = (0, 4)
        self.attn_out(ob, 4, self.GB2, 0, 0)
        fw.pop()

    def fnet_stage1(self, fT_buf, fT_ap_fn, dftd, G1):
        for hb in range(2):
            ps = self.psb[4 + hb]
            for gg in range(2):
                g = hb * 2 + gg
                self.pe(lambda g=g, gg=gg, ps=ps: self.T.matmul(ps[:, gg * 256:(gg + 1) * 256], fT_ap_fn(g), dftd[:, :],
                                                                 start=True, stop=True), [fT_buf, dftd], [ps], signal=(gg == 1))
            if hb == 0:
                self.act(lambda ps=ps: self.A.copy(G1[:, 0:512], ps[:, :]), [ps], [G1])
            else:
                self.dve(lambda ps=ps: self.V.tensor_copy(G1[:, 512:1024], ps[:, :]), [ps], [G1])

    def phaseD_prompt(self, l, half):
        fw = self.fw
        base = l * BLK_PER_LAYER
        h0 = half * 512
        fw.push()
        self.dftd_p = fw.sb([128, 256], BF16, "dftd_p")
        self.dftT_p = fw.sb([128, 1024], BF16, "dftT_p")
        for nm in ("dftd_p", "dftT_p"):
            t = getattr(self, nm)
            self.load(t, t[:], self.d[nm], self.d[nm].ap())
        fT = fw.sb([128, 4, 512], BF16, "fT")
        GD2 = fw.sb([128, 4, 512], BF16, "GD2")
        fw.push()
        self.stream_begin(2, depth=2)
        slot = self.next_block(base + WIN_IDX["D0"])
        for c in range(4):
            ps = self.zmm(slot, 512, c * 128, 128, h0, 512)
            if c % 2:
                self.act(lambda c=c, ps=ps: self.A.copy(fT[:, c, :], ps[:, :]), [ps], [fT])
            else:
                self.dve(lambda c=c, ps=ps: self.V.tensor_copy(fT[:, c, :], ps[:, :]), [ps], [fT])
        self.prefetch_next()
        slot = self.next_block(base + WIN_IDX["D1"])
        for c in range(4):
            ps = self.zmm(slot, 512, c * 128, 128, h0, 512)
            self.silu2(ps, 128, 512, GD2[:, c, :], GD2)
        self.prefetch_next()
        fw.pop()
        for sq in range(2):
            t0 = sq * 256
            G1 = [fw.rot([128, 1024], BF16, "G1", n=4) for _ in range(2)]
            for tt in range(2):
                self.fnet_stage1(fT, lambda g, tt=tt: fT[:, g, t0 + tt * 128:t0 + tt * 128 + 128], self.dftd_p, G1[tt])
            for g in range(4):
                ps = self.nps()
                i = 0
                for tt in range(2):
                    for cs in range(2):
                        self.pe(lambda g=g, tt=tt, cs=cs, ps=ps, i=i: self.T.matmul(
                            ps[:, :256], G1[tt][:, g * 256 + cs * 128:g * 256 + cs * 128 + 128],
                            self.dftT_p[:, (tt * 2 + cs) * 256:(tt * 2 + cs) * 256 + 256], start=(i == 0), stop=(i == 3)),
                            [G1[tt], self.dftT_p], [ps], signal=(i == 3))
                        i += 1
                self.dve(lambda g=g, ps=ps: self.V.scalar_tensor_tensor(
                    self.oTs[half][3][:, g, t0:t0 + 256], ps[:, :256], 0.5, GD2[:, g, t0:t0 + 256], ALU.mult, ALU.mult),
                    [ps, GD2], [self.oTs[half][3]])
        fw.pop()

    def contrib_rows(self, l, slot, W, c0, w, part, row0):
        fw = self.fw
        ps = self.zmm(slot, W, c0, w, 0, 512)
        stg = fw.rot([128, 512], BF16, "agstg", n=3)
        if self.cflip % 2:
            self.act(lambda: self.A.copy(stg[:w, :], ps[:w, :]), [ps], [stg])
        else:
            self.dve(lambda: self.V.tensor_copy(stg[:w, :], ps[:w, :]), [ps], [stg])
        self.cflip += 1
        ag = self.ag1_in[l][part]
        fw.dma("sp", ag, ag[row0:row0 + w, :], stg, stg[:w, :])

    def phaseD_contrib(self, l):
        fw = self.fw
        base = l * BLK_PER_LAYER
        self.GD2 = fw.sb([128, 4, 512], BF16, "GD2s")
        fw.push()
        self.stream_begin(2, depth=2)
        slot = self.next_block(base + WIN_IDX["D0"])
        for c in range(4):
            self.contrib_rows(l, slot, 512, c * 128, 128, "f", c * 128)
        self.prefetch_next()
        slot = self.next_block(base + WIN_IDX["D1"])
        for c in range(4):
            ps = self.zmm(slot, 512, c * 128, 128, 0, 512)
            self.silu2(ps, 128, 512, self.GD2[:, c, :], self.GD2)
        self.prefetch_next()
        fw.pop()

    def phaseD_consume(self, l):
        fw = self.fw
        ago = self.ag1_out[l]["f"]
        fw.push()
        self.dftd_s = fw.sb([128, 256], BF16, "dftd_s")
        self.load(self.dftd_s, self.dftd_s[:], self.d["dftd_s"], self.d["dftd_s"].ap())
        acc = self.psb[0:4]
        nt = 0
        for q in range(4):
            fq = fw.rot([128, 4, 512], BF16, "fq", n=2)
            src = ago[q * 512:q * 512 + 512, :].rearrange("(g d) t -> d g t", d=128)
            fw.dma("sp", fq, fq[:], ago, src)
            for s4 in range(4):
                tt = q * 4 + s4
                ct = fw.rot([128, 1024], BF16, "ct", n=3)
                fw.dma("pool", ct, ct[:], self.d["dftT_s"], self.d["dftT_s"][tt, :, :])
                G1 = fw.rot([128, 1024], BF16, "G1s", n=3)
                self.fnet_stage1(fq, lambda g, s4=s4, fq=fq: fq[:, g, s4 * 128:(s4 + 1) * 128], self.dftd_s, G1)
                for g in range(4):
                    for cs in range(2):
                        self.pe(lambda g=g, cs=cs, G1=G1, ct=ct, tt=tt: self.T.matmul(
                            acc[g][:, :], G1[:, g * 256 + cs * 128:g * 256 + cs * 128 + 128], ct[:, cs * 512:(cs + 1) * 512],
                            start=(tt == 0 and cs == 0), stop=(tt == 15 and cs == 1)),
                            [G1, ct], [acc[g]], signal=(g == 3 and cs == 1))
        for g in range(4):
            self.dve(lambda g=g: self.V.scalar_tensor_tensor(self.oTs[0][3][:, g, 0:512], acc[g][:, :], 0.5, self.GD2[:, g, :],
                                                             ALU.mult, ALU.mult), [acc[g], self.GD2], [self.oTs[0][3]])
        fw.pop()

    def allgather(self, dst, src):
        fw = self.fw
        fw.dma("pool", dst, None, src, None, sem_owner=dst, inc=1,
               fn=lambda: self.G.collective_compute("AllGather", ALU.bypass, replica_groups=[[0, 1, 2, 3], [4, 5, 6, 7]],
                                                     ins=[src.h.ap()], outs=[dst.h.ap()]))

    def sample_pass(self, l):
        import os
        fw = self.fw
        dbg = os.environ.get("KDBG", "")
        self.cflip = 0
        br = self.branches
        fw.push()
        if "A" in br:
            self.phaseA_contrib(l)
        fw.push()
        if "B" in br:
            self.phaseB_contrib(l)
        fw.push()
        if "D" in br:
            self.phaseD_contrib(l)
        if "noag" not in dbg:
            if "A" in br:
                self.allgather(self.ag1_out[l]["rk"], self.ag1_in[l]["rk"])
            if "A" in br or "B" in br:
                self.allgather(self.ag1_out[l]["vx"], self.ag1_in[l]["vx"])
            if "D" in br:
                self.allgather(self.ag1_out[l]["f"], self.ag1_in[l]["f"])
        if "C" in br:
            self.phaseC(l, 0, 512, 0)
        if "D" in br and "nocons" not in dbg:
            self.phaseD_consume(l)
        fw.pop()
        if "B" in br:
            self.phaseB_consume(l)
        fw.pop()
        if "A" in br and "noAcons" not in dbg:
            self.phaseA_consume(l)
            self.phaseA_final(l)
        fw.pop()

    def zero_branch(self, n):
        for hf in range(2):
            if self.oTs[hf] is not None:
                t = self.oTs[hf][n]
                self.pool(lambda t=t: self.G.memset(t[:], 0.0), [], [t])

    def win_plan(self, l, grp="p"):
        base = l * BLK_PER_LAYER
        ids = []
        order = (("A", ["A0", "A1", "A2", "A3", "A4"]), ("B", ["B0", "B1"]), ("C", ["C0", "C1", "C2"]), ("D", ["D0", "D1"]))
        if grp == "s":
            order = (order[0], order[1], order[3], order[2])
        for br, names in order:
            if br in self.branches:
                ids += [base + WIN_IDX[n] for n in names]
        return ids

    def tail_plan(self, l):
        base = l * BLK_PER_LAYER
        return [base + 18 + i for i in range(16)] + [base + 34, base + 35]

    def build(self):
        fw = self.fw
        self.outsem = Buf(None, "outsem", "none")
        for l in range(self.depth):
            self.stream_plan([l * BLK_PER_LAYER + b for b in range(6)])
            self.stream_plan(self.win_plan(l) * 2 + self.tail_plan(l))
            self.stream_plan(self.win_plan(l, 's') + self.tail_plan(l))
        for l in range(self.depth):
            fw.push()
            self.load_layer_small(l)
            self.ada(l)
            fw.push()
            self.hTs[1] = fw.sb([128, 8, 512], BF16, "hTb")
            self.oTs[1] = [fw.sb([128, 4, 512], BF16, f"oTb{n}") for n in range(4)]
            for n, br in enumerate("ABCD"):
                if br not in self.branches:
                    self.zero_branch(n)
            for half in range(2):
                self.make_h(0, half * 512, half * 512, 512)
            for half in range(2):
                self.phases(l, "p", half)
            self.merge_out(l, 0, 0, NPT)
            fw.pop()
            self.hTs[1] = None
            self.oTs[1] = None
            self.make_h(1, NPT, 0, 512)
            self.sample_pass(l)
            self.merge_out(l, 1, NPT, NST)
            fw.pop()
        fw.push()
        self.sp = fw.sb([128, NSP], F32, "spf")
        self.load(self.sp, self.sp[:], self.d["sp"], self.d["sp"][0, :, :])
        self.final_out()
        fw.pop()
        for e in ("sp",):
            for ev in list(fw.dma_out.values()):
                fw._wait(e, ev)
        fw.barrier()
        return self.nc

    def phases(self, l, grp, half):
        h0 = half * 512
        if "A" in self.branches:
            self.phaseA_prompt(l, half)
        if "B" in self.branches:
            self.phaseB_prompt(l, half)
        if "C" in self.branches:
            self.phaseC(l, h0, 512, h0)
        if "D" in self.branches:
            self.phaseD_prompt(l, half)


_CFG = {"branches": "ABCD", "depth": DEPTH}


def make_in_maps(inp):
    f32 = lambda a: np.ascontiguousarray(np.asarray(a, dtype=np.float32))
    inp = {k: np.asarray(v) for k, v in inp.items()}
    wst = build_stream(f32(inp["w_ada"]), f32(inp["w_in"]), f32(inp["w_branch"]), f32(inp["w_merge"]), f32(inp["w_out"]))
    sp = build_small(inp)
    cst = build_consts()
    L = DEPTH
    wup = f32(inp["rwkv_w_up"]).transpose(0, 2, 1, 3).reshape(L, 64, 1024)
    aup = f32(inp["rwkv_a_up"]).transpose(0, 2, 1, 3).reshape(L, 64, 1024)
    wqu = f32(inp["mla_w_q_up"]).reshape(L, 2, 128, 8, 96)
    wq = wqu.transpose(0, 2, 1, 3, 4).reshape(L, 128, 2 * 8 * 96)
    wqs = wqu[..., 64 + _SWAP32].transpose(0, 2, 1, 3, 4).reshape(L, 128, 2 * 8 * 32)
    wkvu = f32(inp["mla_w_kv_up"]).reshape(L, 128, 8, 128)
    wkk = np.ascontiguousarray(wkvu[..., :64]).reshape(L, 128, 512)
    wkv = np.ascontiguousarray(wkvu[..., 64:]).reshape(L, 128, 512)
    wsT = f32(inp["gmlp_w_s"]).transpose(0, 3, 1, 2).reshape(L, 128, 512)
    bsb = np.ascontiguousarray(np.broadcast_to(f32(inp["gmlp_b_s"]).reshape(L, 1, 512), (L, 128, 512)))
    maps = []
    xp = f32(inp["x_prompt"])
    xs = f32(inp["x_sample"])
    for c in range(NCORE):
        s, j = c // 4, c % 4
        dftT, rope = build_core_consts(j)
        cond = np.stack([f32(inp["c_ctx"]).reshape(8, 128).T, f32(inp["c"])[s].reshape(8, 128).T], axis=2).reshape(128, 16)
        o, _ = SP_OFF["rw"]
        om, _ = SP_OFF["mu_rkv"]
        spo = np.concatenate([sp[:, :, o:o + 36].reshape(L, 128, 9, 4)[:, :, :, j],
                              sp[:, :, om:om + 12].reshape(L, 128, 3, 4)[:, :, :, j]], axis=2)
        spo = np.ascontiguousarray(spo)
        st0 = np.stack([f32(inp["state_rwkv_fwd"])[s, :, 2 * j:2 * j + 2], f32(inp["state_rwkv_bwd"])[s, :, 2 * j:2 * j + 2]],
                       axis=1)
        st0 = st0.transpose(0, 1, 2, 4, 3).reshape(L, 2, 128, 64)
        p = np.arange(128)
        idx1 = np.zeros((128, 12), np.int32)
        for q in range(4):
            idx1[:, 0 * 4 + q] = q * 1024 + 128 * j + p
            idx1[:, 1 * 4 + q] = q * 1024 + 512 + 128 * j + p
            idx1[:, 2 * 4 + q] = q * 864 + 128 * j + p
        idx2 = np.zeros((128, 4), np.int32)
        for r in range(4):
            idx2[:, r] = (r * 4 + j) * 128 + p
        m = dict(
            xp=np.ascontiguousarray(xp[4 * c:4 * c + 4].reshape(NPT, D).T),
            xs=np.ascontiguousarray(xs[s, 512 * j:512 * j + 512].T),
            wst=wst, sp=sp, cond=np.ascontiguousarray(cond),
            ident=cst["ident"], bones=cst["bones"], mask=cst["mask"],
            dftd_p=cst["dftd_p"], dftd_s=cst["dftd_s"], dftT_p=cst["dftT_p"],
            dftT_s=np.ascontiguousarray(dftT.reshape(16, 128, 1024)), rope=np.ascontiguousarray(rope.reshape(32, 1024)),
            wup=wup, aup=aup,
            wupo=np.ascontiguousarray(wup.reshape(L, 64, 2, 4, 128)[:, :, :, j]).reshape(L, 64, 256),
            aupo=np.ascontiguousarray(aup.reshape(L, 64, 2, 4, 128)[:, :, :, j]).reshape(L, 64, 256),
            spo=spo, wq=wq, wqs=wqs, wkk=wkk, wkv=wkv, wsT=wsT, bsb=bsb,
            st0=np.ascontiguousarray(st0),
            cckv=np.ascontiguousarray(f32(inp["cache_mla_ckv"])[s].transpose(0, 2, 1)),
            ckr=np.ascontiguousarray(f32(inp["cache_mla_krope"])[s].transpose(0, 2, 1)),
            idx1=idx1, idx2=idx2,
        )
        maps.append(m)
    return maps


def assemble(results):
    B = 32
    yp = np.zeros((B, SEQ, D), np.float32)
    ys = np.zeros((2, DSEQ, D), np.float32)
    sf = np.zeros((B, DEPTH, 8, 64, 64), np.float32)
    sbw = np.zeros((B, DEPTH, 8, 64, 64), np.float32)
    ckv = np.zeros((B, DEPTH, SEQ, 128), np.float32)
    kr = np.zeros((B, DEPTH, SEQ, 32), np.float32)
    for c in range(NCORE):
        r = results[c]
        s, j = c // 4, c % 4
        yp[4 * c:4 * c + 4] = np.asarray(r["yp"]).T.reshape(4, SEQ, D)
        ys[s, 512 * j:512 * j + 512] = np.asarray(r["ys"]).T
        st = np.asarray(r["stout"]).reshape(DEPTH, 2, 4, 4, 2, 64, 64)
        st = st.transpose(1, 2, 0, 3, 4, 6, 5).reshape(2, 4, DEPTH, 8, 64, 64)
        sf[4 * c:4 * c + 4] = st[0]
        sbw[4 * c:4 * c + 4] = st[1]
        ck = np.asarray(r["ckvout"]).reshape(DEPTH, 128, 4, SEQ)
        ckv[4 * c:4 * c + 4] = ck.transpose(2, 0, 3, 1)
        k2 = np.asarray(r["krout"]).reshape(DEPTH, 32, 4, SEQ)
        kr[4 * c:4 * c + 4] = k2.transpose(2, 0, 3, 1)
    return yp, ys, sf, sbw, ckv, kr


def kernel(**inputs):
    prog = Prog(dict(_CFG))
    nc = prog.build()
    maps = make_in_maps(inputs)
    res = run_bass_kernel_spmd(nc, maps, core_ids=list(range(NCORE)))
    return assemble(res.results)
```
